# Optimizing a Trainium2 kernel written in Bass

```python
import math
import jax, jax.numpy as jnp
from jax import lax
import numpy as np

D_MODEL = 1024
BATCH = 8
SEQ = 2048
DEPTH = 1

ROPE_THETA = 500000.0
EPS = 1e-6
Q_BLOCK = 128
NEG = -1e30

MLA_HEADS = 8
MLA_NOPE = 64
MLA_ROPE = 32
MLA_QK = MLA_NOPE + MLA_ROPE
MLA_V = 64
MLA_Q_LORA = 768
MLA_KV_LORA = 256

NSA_HEADS = 8
NSA_KV_GROUPS = 2
NSA_REP = NSA_HEADS // NSA_KV_GROUPS
NSA_HEAD = 64
NSA_ROT = NSA_HEAD // 4
CMP_LEN = 32
CMP_STRIDE = 16
CMP_HIDDEN = 256
SEL_LEN = 64
SEL_TOP = 8
SEL_Q_BLOCK = 64
WINDOW = 256
N_NSA_BRANCH = 3
FORCE_BONUS = 1e4
KV_W = NSA_KV_GROUPS * NSA_HEAD

N_BRANCH = 2
D_FF = -(-8 * D_MODEL // (3 * 256)) * 256
N_MOD = 6

IN_SIZES = (MLA_Q_LORA, MLA_KV_LORA, MLA_ROPE, NSA_HEADS * NSA_HEAD,
            KV_W, KV_W, KV_W, KV_W, KV_W, KV_W,
            NSA_HEADS * N_NSA_BRANCH, D_MODEL, D_MODEL)
D_IN = sum(IN_SIZES)

kernel_name = "hybrid_mla_nsa_adaln_block"


def rms_norm(x, gain):
    xf = x.astype(jnp.float32)
    y = xf * lax.rsqrt(jnp.mean(xf * xf, axis=-1, keepdims=True) + EPS)
    return (y * gain.astype(jnp.float32)).astype(x.dtype)


def rope_angles(pos, dim):
    inv = ROPE_THETA ** (-jnp.arange(0, dim, 2, dtype=jnp.float32) / dim)
    ang = pos.astype(jnp.float32)[..., None] * inv
    return jnp.cos(ang), jnp.sin(ang)


def apply_rope(x, cos, sin):
    x1, x2 = jnp.split(x, 2, axis=-1)
    cos = cos.astype(x.dtype)
    sin = sin.astype(x.dtype)
    return jnp.concatenate([x1 * cos - x2 * sin, x2 * cos + x1 * sin], axis=-1)


def partial_rope(x, cos, sin):
    return jnp.concatenate([apply_rope(x[..., :NSA_ROT], cos, sin), x[..., NSA_ROT:]], axis=-1)


def causal_block_attention(q, k, v, scale):
    B, S, H, Dk = q.shape
    nb = S // Q_BLOCK
    qb = q.reshape(B, nb, Q_BLOCK, H, Dk).transpose(1, 0, 2, 3, 4)
    kpos = jnp.arange(S)

    def one(args):
        qi, i = args
        s = jnp.einsum('bqhd,bkhd->bhqk', qi, k).astype(jnp.float32) * scale
        qpos = i * Q_BLOCK + jnp.arange(Q_BLOCK)
        s = jnp.where(kpos[None, :] <= qpos[:, None], s, NEG)
        p = jax.nn.softmax(s, axis=-1).astype(v.dtype)
        return jnp.einsum('bhqk,bkhd->bqhd', p, v)

    o = lax.map(one, (qb, jnp.arange(nb)))
    return o.transpose(1, 0, 2, 3, 4).reshape(B, S, H, v.shape[-1])


def mla_attention(c_q, c_kv, k_pe, cos, sin, q_a_gain, w_q_b, kv_a_gain, w_kv_b, q_gain, k_gain):
    B, S, _ = c_q.shape
    q = (rms_norm(c_q, q_a_gain) @ w_q_b).reshape(B, S, MLA_HEADS, MLA_QK)
    kv = (rms_norm(c_kv, kv_a_gain) @ w_kv_b).reshape(B, S, MLA_HEADS, MLA_NOPE + MLA_V)
    k_nope, v = kv[..., :MLA_NOPE], kv[..., MLA_NOPE:]
    k = jnp.concatenate([k_nope, jnp.broadcast_to(k_pe[:, :, None, :], (B, S, MLA_HEADS, MLA_ROPE))], axis=-1)
    q = rms_norm(q, q_gain)
    k = rms_norm(k, k_gain)
    cos_h, sin_h = cos[:, :, None], sin[:, :, None]
    q = jnp.concatenate([q[..., :MLA_NOPE], apply_rope(q[..., MLA_NOPE:], cos_h, sin_h)], axis=-1)
    k = jnp.concatenate([k[..., :MLA_NOPE], apply_rope(k[..., MLA_NOPE:], cos_h, sin_h)], axis=-1)
    o = causal_block_attention(q, k, v, MLA_QK ** -0.5)
    return o.reshape(B, S, MLA_HEADS * MLA_V)


def nsa_attention(q, k_c, v_c, k_s, v_s, k_w, v_w, g_logit, cos, sin,
                  q_gain, kc_gain, ks_gain, kw_gain,
                  cmp_pos_k, cmp_w1_k, cmp_w2_k, cmp_pos_v, cmp_w1_v, cmp_w2_v):
    B, S, H, Dh = q.shape
    G, R = NSA_KV_GROUPS, NSA_REP
    f32 = jnp.float32
    scale = Dh ** -0.5
    cos_t, sin_t = cos[:, :, None], sin[:, :, None]
    q = partial_rope(rms_norm(q, q_gain), cos_t, sin_t)
    qg = q.reshape(B, S, G, R, Dh)
    t_idx = jnp.arange(S)

    n_cmp = (S - CMP_LEN) // CMP_STRIDE + 1
    starts = jnp.arange(n_cmp) * CMP_STRIDE
    blk_idx = starts[:, None] + jnp.arange(CMP_LEN)[None, :]

    def compress(t, pos_emb, w1, w2):
        blocks = t[:, blk_idx] + pos_emb[None, None, :, None, :]
        flat = blocks.transpose(0, 1, 3, 2, 4).reshape(B, n_cmp, G, CMP_LEN * Dh)
        return jax.nn.silu(flat @ w1) @ w2

    cmp_end = starts + CMP_LEN - 1
    kc = compress(k_c, cmp_pos_k, cmp_w1_k, cmp_w2_k)
    vc = compress(v_c, cmp_pos_v, cmp_w1_v, cmp_w2_v)
    kc = partial_rope(rms_norm(kc, kc_gain), cos[:, cmp_end][:, :, None], sin[:, cmp_end][:, :, None])
    s_c = jnp.einsum('bsgrd,bngd->bgrsn', qg, kc).astype(f32) * scale
    valid_c = cmp_end[None, :] <= t_idx[:, None]
    p_c = jax.nn.softmax(jnp.where(valid_c, s_c, NEG), axis=-1) * valid_c
    o_c = jnp.einsum('bgrsn,bngd->bsgrd', p_c.astype(vc.dtype), vc)

    n_sel = S // SEL_LEN
    sel_start = jnp.arange(n_sel) * SEL_LEN
    overlap = ((starts[:, None] < sel_start[None, :] + SEL_LEN) &
               (starts[:, None] + CMP_LEN > sel_start[None, :])).astype(f32)
    imp = jnp.einsum('bgrsn,nj->bgsj', p_c, overlap)
    cur = t_idx // SEL_LEN
    j = jnp.arange(n_sel)
    forced = (j[None, :] == 0) | (j[None, :] == cur[:, None]) | (j[None, :] == cur[:, None] - 1)
    imp = jnp.where(forced, imp + FORCE_BONUS, imp)
    imp = jnp.where(j[None, :] <= cur[:, None], imp, NEG)
    top = min(SEL_TOP, n_sel)
    _, sel_idx = lax.top_k(imp, top)

    ks = partial_rope(rms_norm(k_s, ks_gain), cos_t, sin_t)
    k_blocks = ks.reshape(B, n_sel, SEL_LEN, G, Dh).transpose(0, 3, 1, 2, 4)
    v_blocks = v_s.reshape(B, n_sel, SEL_LEN, G, Dh).transpose(0, 3, 1, 2, 4)
    nqb = S // SEL_Q_BLOCK
    q_chunks = qg.reshape(B, nqb, SEL_Q_BLOCK, G, R, Dh).transpose(1, 0, 2, 3, 4, 5)
    idx_chunks = sel_idx.reshape(B, G, nqb, SEL_Q_BLOCK, top).transpose(2, 0, 1, 3, 4)
    gather = jax.vmap(jax.vmap(lambda blk, ix: blk[ix]))

    def sel_chunk(args):
        qi, ix, ci = args
        kg = gather(k_blocks, ix)
        vg = gather(v_blocks, ix)
        s = jnp.einsum('bqgrd,bgqnld->bgrqnl', qi, kg).astype(f32) * scale
        tok = ix[..., None] * SEL_LEN + jnp.arange(SEL_LEN)
        qt = ci * SEL_Q_BLOCK + jnp.arange(SEL_Q_BLOCK)
        mask = tok <= qt[None, None, :, None, None]
        s = jnp.where(mask[:, :, None], s, NEG)
        sh = s.shape
        p = jax.nn.softmax(s.reshape(sh[0], sh[1], sh[2], sh[3], -1), axis=-1).reshape(sh)
        return jnp.einsum('bgrqnl,bgqnld->bqgrd', p.astype(vg.dtype), vg)

    o_s = lax.map(sel_chunk, (q_chunks, idx_chunks, jnp.arange(nqb)))
    o_s = o_s.transpose(1, 0, 2, 3, 4, 5).reshape(B, S, G, R, Dh)

    kw = partial_rope(rms_norm(k_w, kw_gain), cos_t, sin_t)
    nqw = S // Q_BLOCK
    span = WINDOW + Q_BLOCK
    kw_pad = jnp.pad(kw, ((0, 0), (WINDOW, 0), (0, 0), (0, 0)))
    vw_pad = jnp.pad(v_w, ((0, 0), (WINDOW, 0), (0, 0), (0, 0)))
    band = jnp.arange(nqw)[:, None] * Q_BLOCK + jnp.arange(span)[None, :]
    kb = kw_pad[:, band]
    vb = vw_pad[:, band]
    qb = qg.reshape(B, nqw, Q_BLOCK, G, R, Dh)
    s_w = jnp.einsum('bcqgrd,bckgd->bcgrqk', qb, kb).astype(f32) * scale
    key_t = band - WINDOW
    q_t = jnp.arange(nqw)[:, None] * Q_BLOCK + jnp.arange(Q_BLOCK)[None, :]
    diff = q_t[:, :, None] - key_t[:, None, :]
    mask_w = (diff >= 0) & (diff < WINDOW) & (key_t[:, None, :] >= 0)
    s_w = jnp.where(mask_w[None, :, None, None], s_w, NEG)
    p_w = jax.nn.softmax(s_w, axis=-1).astype(vb.dtype)
    o_w = jnp.einsum('bcgrqk,bckgd->bcqgrd', p_w, vb).reshape(B, S, G, R, Dh)

    g = jax.nn.sigmoid(g_logit).reshape(B, S, G, R, N_NSA_BRANCH, 1)
    o = g[..., 0, :] * o_c + g[..., 1, :] * o_s + g[..., 2, :] * o_w
    return o.reshape(B, S, H * Dh)


def setup_inputs(seed: int = 0) -> dict:
    key = jax.random.key(seed)

    def nrm(i, shape, scale):
        return jax.random.normal(jax.random.fold_in(key, i), shape, jnp.float32) * scale

    def gain(i, n):
        return 1.0 + nrm(i, (DEPTH, n), 0.02)

    L = DEPTH
    offset = jax.random.randint(jax.random.fold_in(key, 99), (BATCH, 1), 0, 4096)
    positions = (offset + jnp.arange(SEQ)[None, :]).astype(jnp.int32)
    return {
        "x": nrm(0, (BATCH, SEQ, D_MODEL), 1.0),
        "c": nrm(1, (BATCH, D_MODEL), 1.0),
        "positions": positions,
        "ada_w": nrm(2, (L, D_MODEL, N_MOD * D_MODEL), 0.5 * D_MODEL ** -0.5),
        "ada_b": nrm(3, (L, N_MOD * D_MODEL), 0.02),
        "norm1_gain": gain(4, D_MODEL),
        "w_in": nrm(5, (L, D_MODEL, D_IN), D_MODEL ** -0.5),
        "mla_q_a_gain": gain(6, MLA_Q_LORA),
        "mla_w_q_b": nrm(7, (L, MLA_Q_LORA, MLA_HEADS * MLA_QK), MLA_Q_LORA ** -0.5),
        "mla_kv_a_gain": gain(8, MLA_KV_LORA),
        "mla_w_kv_b": nrm(9, (L, MLA_KV_LORA, MLA_HEADS * (MLA_NOPE + MLA_V)), MLA_KV_LORA ** -0.5),
        "mla_q_gain": gain(10, MLA_QK),
        "mla_k_gain": gain(11, MLA_QK),
        "nsa_q_gain": gain(12, NSA_HEAD),
        "nsa_kc_gain": gain(13, NSA_HEAD),
        "nsa_ks_gain": gain(14, NSA_HEAD),
        "nsa_kw_gain": gain(15, NSA_HEAD),
        "cmp_pos_k": nrm(16, (L, CMP_LEN, NSA_HEAD), 0.1),
        "cmp_w1_k": nrm(17, (L, CMP_LEN * NSA_HEAD, CMP_HIDDEN), (CMP_LEN * NSA_HEAD) ** -0.5),
        "cmp_w2_k": nrm(18, (L, CMP_HIDDEN, NSA_HEAD), CMP_HIDDEN ** -0.5),
        "cmp_pos_v": nrm(19, (L, CMP_LEN, NSA_HEAD), 0.1),
        "cmp_w1_v": nrm(20, (L, CMP_LEN * NSA_HEAD, CMP_HIDDEN), (CMP_LEN * NSA_HEAD) ** -0.5),
        "cmp_w2_v": nrm(21, (L, CMP_HIDDEN, NSA_HEAD), CMP_HIDDEN ** -0.5),
        "w_o_mla": nrm(22, (L, MLA_HEADS * MLA_V, D_MODEL), (MLA_HEADS * MLA_V) ** -0.5),
        "w_o_nsa": nrm(23, (L, NSA_HEADS * NSA_HEAD, D_MODEL), (NSA_HEADS * NSA_HEAD) ** -0.5),
        "w_out": nrm(24, (L, D_MODEL, D_MODEL), D_MODEL ** -0.5),
        "norm2_gain": gain(25, D_MODEL),
        "ffn_w_gate": nrm(26, (L, D_MODEL, D_FF), D_MODEL ** -0.5),
        "ffn_w_up": nrm(27, (L, D_MODEL, D_FF), D_MODEL ** -0.5),
        "ffn_w_down": nrm(28, (L, D_FF, D_MODEL), D_FF ** -0.5),
    }


def reference(x, c, positions, ada_w, ada_b, norm1_gain, w_in,
              mla_q_a_gain, mla_w_q_b, mla_kv_a_gain, mla_w_kv_b, mla_q_gain, mla_k_gain,
              nsa_q_gain, nsa_kc_gain, nsa_ks_gain, nsa_kw_gain,
              cmp_pos_k, cmp_w1_k, cmp_w2_k, cmp_pos_v, cmp_w1_v, cmp_w2_v,
              w_o_mla, w_o_nsa, w_out, norm2_gain, ffn_w_gate, ffn_w_up, ffn_w_down):
    B, S, _ = x.shape
    mla_cos, mla_sin = rope_angles(positions, MLA_ROPE)
    nsa_cos, nsa_sin = rope_angles(positions, NSA_ROT)
    split_at = np.cumsum(IN_SIZES)[:-1]
    for l in range(DEPTH):
        mod = jax.nn.silu(c) @ ada_w[l] + ada_b[l]
        sh1, sc1, gt1, sh2, sc2, gt2 = [m[:, None, :] for m in jnp.split(mod, N_MOD, axis=-1)]

        h = rms_norm(x, norm1_gain[l]) * (1.0 + sc1) + sh1
        z = h @ w_in[l]
        (c_q, c_kv, k_pe, q_n, k_c, v_c, k_s, v_s, k_w, v_w,
         g_nsa, g_mla_merge, g_nsa_merge) = jnp.split(z, split_at, axis=-1)
        y_mla = mla_attention(c_q, c_kv, k_pe, mla_cos, mla_sin,
                              mla_q_a_gain[l], mla_w_q_b[l], mla_kv_a_gain[l], mla_w_kv_b[l],
                              mla_q_gain[l], mla_k_gain[l]) @ w_o_mla[l]
        kvs = lambda t: t.reshape(B, S, NSA_KV_GROUPS, NSA_HEAD)
        y_nsa = nsa_attention(q_n.reshape(B, S, NSA_HEADS, NSA_HEAD),
                              kvs(k_c), kvs(v_c), kvs(k_s), kvs(v_s), kvs(k_w), kvs(v_w),
                              g_nsa.reshape(B, S, NSA_HEADS, N_NSA_BRANCH), nsa_cos, nsa_sin,
                              nsa_q_gain[l], nsa_kc_gain[l], nsa_ks_gain[l], nsa_kw_gain[l],
                              cmp_pos_k[l], cmp_w1_k[l], cmp_w2_k[l],
                              cmp_pos_v[l], cmp_w1_v[l], cmp_w2_v[l]) @ w_o_nsa[l]
        merged = jax.nn.sigmoid(g_mla_merge) * y_mla + jax.nn.sigmoid(g_nsa_merge) * y_nsa
        x = x + gt1 * (merged @ w_out[l])

        h2 = rms_norm(x, norm2_gain[l]) * (1.0 + sc2) + sh2
        x = x + gt2 * ((jax.nn.silu(h2 @ ffn_w_gate[l]) * (h2 @ ffn_w_up[l])) @ ffn_w_down[l])
    return x
```

```python
import contextlib
import numpy as np
import ml_dtypes
import concourse.bass as bass
import concourse.mybir as mybir
from concourse.bass_utils import run_bass_kernel_spmd

F32 = mybir.dt.float32
BF16 = mybir.dt.bfloat16
I32 = mybir.dt.int32
ALU = mybir.AluOpType
AF = mybir.ActivationFunctionType
AX = mybir.AxisListType

S_ = 2048
D = 1024
NT = 16
DFF = 2816
NFC = 22
EPS = 1e-6
PH = [0, 4, 1, 5, 2, 6, 3, 7]
TWO_PI = float(2 * np.pi)
PI = float(np.pi)


class Sched:
    N_DMA_SLOTS = {"sp": 24, "pool": 8, "act": 4}

    def __init__(self, nc, stack):
        self.nc = nc
        self.E = {"pe": nc.tensor, "act": nc.scalar, "dve": nc.vector, "pool": nc.gpsimd, "sp": nc.sync}
        self.sem, self.cnt = {}, {}
        for e in ("pe", "act", "dve", "pool"):
            self.sem[e] = stack.enter_context(nc.semaphore("s_" + e))
            self.cnt[e] = 0
        self.dsem, self.dcnt, self.dnext = {}, {}, {}
        for q, n in self.N_DMA_SLOTS.items():
            self.dsem[q] = [stack.enter_context(nc.semaphore(f"d_{q}{i}")) for i in range(n)]
            self.dcnt[q] = [0] * n
            self.dnext[q] = 0
        self.seen = {e: {} for e in self.E}
        self.lastw, self.readers = {}, {}
        self.n_wait = 0
        self.n_inst = 0

    def _sem_of(self, src):
        return self.dsem[src[1]][src[2]] if isinstance(src, tuple) else self.sem[src]

    def _wait(self, e, tok):
        src, val = tok
        if self.seen[e].get(src, 0) >= val:
            return
        self.E[e].wait_ge(self._sem_of(src), val)
        self.seen[e][src] = val
        self.n_wait += 1

    def _deps(self, e, r, w):
        toks = []
        for k in r:
            t = self.lastw.get(k)
            if t is not None:
                toks.append(t)
        for k in w:
            t = self.lastw.get(k)
            if t is not None:
                toks.append(t)
            for t in self.readers.get(k, ()):
                toks.append(t)
        for t in toks:
            if t[0] == e and e == "pe":
                continue
            self._wait(e, t)

    def _commit(self, tok, r, w):
        for k in r:
            lst = self.readers.setdefault(k, [])
            lst[:] = [t for t in lst if t[0] != tok[0]]
            lst.append(tok)
        for k in w:
            self.lastw[k] = tok
            self.readers[k] = []

    def op(self, e, fn, r=(), w=()):
        self._deps(e, r, w)
        ins = fn(self.E[e])
        self.cnt[e] += 1
        ins.then_inc(self.sem[e], 1)
        tok = (e, self.cnt[e])
        self._commit(tok, r, w)
        self.n_inst += 1
        return tok

    def dma(self, q, out, in_, r=(), w=(), **kw):
        slot = self.dnext[q]
        self.dnext[q] = (slot + 1) % len(self.dsem[q])
        src = ("d", q, slot)
        if self.dcnt[q][slot] > 0:
            self._wait(q, (src, self.dcnt[q][slot]))
        self._deps(q, r, w)
        ins = self.E[q].dma_start(out=out, in_=in_, **kw)
        self.dcnt[q][slot] += 16
        ins.then_inc(self.dsem[q][slot], 16)
        tok = (src, self.dcnt[q][slot])
        self._commit(tok, r, w)
        self.n_inst += 1
        return tok

    def barrier(self):
        for e in ("pe", "act", "dve", "pool", "sp"):
            for e2 in ("pe", "act", "dve", "pool"):
                if self.cnt[e2] and not (e2 == e == "pe"):
                    self._wait(e, (e2, self.cnt[e2]))
            for q in self.dsem:
                for i in range(len(self.dsem[q])):
                    if self.dcnt[q][i]:
                        self._wait(e, (("d", q, i), self.dcnt[q][i]))
        self.lastw.clear()
        self.readers.clear()


class Mem:
    def __init__(self, big, nbytes):
        self.big, self.lo, self.hi, self.n = big, 0, nbytes, nbytes

    def _view(self, off, shape, dt):
        nel = int(np.prod(shape))
        esz = 4 if dt in (F32, I32) else 2
        nb = nel * esz
        ap = self.big[:, off // 2:(off + nb) // 2]
        if esz == 4:
            ap = ap.bitcast(dt)
        if len(shape) == 2:
            ap = ap.rearrange("p (a b) -> p a b", a=shape[0])
        elif len(shape) == 3:
            ap = ap.rearrange("p (a b c) -> p a b c", a=shape[0], b=shape[1])
        return ap

    def lo_alloc(self, shape, dt):
        nb = int(np.prod(shape)) * (4 if dt in (F32, I32) else 2)
        nb = (nb + 63) // 64 * 64
        off = self.lo
        self.lo += nb
        assert self.lo <= self.hi, f"SBUF overflow lo={self.lo} hi={self.hi}"
        return self._view(off, shape, dt)

    def hi_alloc(self, shape, dt):
        nb = int(np.prod(shape)) * (4 if dt in (F32, I32) else 2)
        nb = (nb + 63) // 64 * 64
        self.hi -= nb
        assert self.lo <= self.hi, f"SBUF overflow lo={self.lo} hi={self.hi}"
        return self._view(self.hi, shape, dt)


def build(upto=99, dbg=()):
    nc = bass.Bass("TRN2", target_bir_lowering=False)
    I = {}

    def din(name, shape, dt=F32):
        I[name] = nc.dram_tensor(name, list(shape), dt, kind="ExternalInput").ap()
        return I[name]

    x = din("x", [S_, D]); c_pk = din("c_pk", [128, 8]); pos_pk = din("pos_pk", [128, NT], I32)
    posC = din("posC", [127, 1], I32)
    ada_w = din("ada_w", [D, 6 * D]); adabB = din("adabB", [128, 6 * D]); g1B = din("g1B", [128, D]); g2B = din("g2B", [128, D])
    w_cq = din("w_cq", [D, 768]); w_ckv = din("w_ckv", [D, 256]); w_kpe = din("w_kpe", [D, 32])
    w_qn = din("w_qn", [D, 512]); w_kc2 = din("w_kc2", [D, 256]); w_vc2 = din("w_vc2", [D, 256])
    w_kv4 = din("w_kv4", [D, 512]); w_gn = din("w_gn", [D, 24]); w_gm = din("w_gm", [D, D]); w_gnm = din("w_gnm", [D, D])
    qag = din("qag", [128, 6]); kvag = din("kvag", [128, 2])
    w_qb = din("w_qb", [768, 768]); w_kvb = din("w_kvb", [256, 1024])
    qgB = din("qgB", [128, 96]); kgB = din("kgB", [128, 96])
    nqB = din("nqB", [128, 64]); nkcB = din("nkcB", [128, 64]); nksB = din("nksB", [128, 64]); nkwB = din("nkwB", [128, 64])
    posk = din("posk", [128, 16]); w1k = din("w1k", [2048, 256]); w2k = din("w2k", [256, 64])
    posv = din("posv", [128, 16]); w1v = din("w1v", [2048, 256]); w2v = din("w2v", [256, 64])
    wo_mla = din("wo_mla", [512, D]); wo_nsa = din("wo_nsa", [512, D]); w_out = din("w_out", [D, D])
    wg = din("wg", [D, DFF]); wu = din("wu", [D, DFF]); wd = din("wd", [DFF, D])
    ident_d = din("ident", [128, 128], BF16); tri_d = din("tri", [128, 128], BF16); winm_d = din("winm", [128, 384], BF16)
    inv16_d = din("inv16", [128, 16]); inv8_d = din("inv8", [128, 8])
    ovl_d = din("ovl", [127, 32], BF16); vmT_d = din("vmT", [127, S_], BF16); XE_d = din("XE", [32, NT * 128], BF16)
    fb_d = din("fb", [128, NT * 32]); vj_d = din("vj", [128, NT * 32])
    out = nc.dram_tensor("out", [S_, D], F32, kind="ExternalOutput").ap()
    hTs = nc.dram_tensor("hTs", [128, 8 * S_], BF16).ap()
    mods = nc.dram_tensor("mods", [128, 6 * D], F32).ap()
    x1s = nc.dram_tensor("x1s", [S_, D], F32).ap()
    D_ = {}
    for name, shape, dt in dbg:
        D_[name] = nc.dram_tensor("dbg_" + name, list(shape), dt, kind="ExternalOutput").ap()

    st = contextlib.ExitStack()
    with st:
        S = Sched(nc, st)
        NB = 204800
        big = st.enter_context(nc.sbuf_tensor("big", [128, NB // 2], BF16))
        M = Mem(big, NB)
        P = [st.enter_context(nc.psum_tensor(f"ps{i}", [128, 512], F32)) for i in range(8)]
        PK = [f"ps{i}" for i in range(8)]
        bank_state = [0]

        def nb():
            b = bank_state[0]
            bank_state[0] = (b + 1) % 6
            return b

        def mm(ps_ap, lhsT, rhs, start, stop, r, w, **kw):
            S.op("pe", lambda e: e.matmul(ps_ap, lhsT=lhsT, rhs=rhs, start=start, stop=stop, **kw), r=r, w=w)

        def tp(ps_ap, in_, ident_ap, r, w):
            S.op("pe", lambda e: e.transpose(out=ps_ap, in_=in_, identity=ident_ap), r=r, w=w)

        def act(out_, in_, func, r, w, **kw):
            S.op("act", lambda e: e.activation(out=out_, in_=in_, func=func, **kw), r=r, w=w)

        def dbg_out(name, ap, r):
            if name in D_:
                S.dma("sp", D_[name], ap, r=r)

        ident = M.lo_alloc([128], BF16); tri = M.lo_alloc([128], BF16); winm = M.lo_alloc([3, 128], BF16)
        onesb = M.lo_alloc([128], BF16)
        stg = [M.lo_alloc([1024], F32) for _ in range(3)]
        stg_i = [0]
        cosM = M.lo_alloc([NT, 16], F32); sinM = M.lo_alloc([NT, 16], F32)
        cosN = M.lo_alloc([NT, 8], F32); sinN = M.lo_alloc([NT, 8], F32)
        cosC = M.lo_alloc([8], F32); sinC = M.lo_alloc([8], F32)
        lo_pers = M.lo
        omT = M.lo_alloc([4, S_], BF16); onT = M.lo_alloc([4, S_], BF16)
        S.dma("sp", ident, ident_d, w=["ident"])
        S.dma("sp", tri, tri_d, w=["tri"])
        S.dma("sp", winm.rearrange("p a b -> p (a b)"), winm_d, w=["winm"])
        S.op("pool", lambda e: e.memset(onesb, 1.0), w=["onesb"])

        def load_w(dst, W, key, KC, N, ceng="dve"):
            Wv = W.rearrange("(k p) n -> p k n", p=128)
            if N <= 1024:
                g = max(1, min(KC, 1024 // N))
                for k0 in range(0, KC, g):
                    k1 = min(KC, k0 + g)
                    si = stg_i[0]; stg_i[0] = (si + 1) % 3
                    sv = stg[si][:, 0:(k1 - k0) * N].rearrange("p (k n) -> p k n", n=N)
                    S.dma("sp", sv, Wv[:, k0:k1, :], w=[("stg", si)])
                    S.op(ceng, lambda e, sv=sv, k0=k0, k1=k1: e.tensor_copy(out=dst[:, k0:k1, :], in_=sv), r=[("stg", si)], w=[key])
            else:
                for k in range(KC):
                    for c0 in range(0, N, 1024):
                        c1 = min(N, c0 + 1024)
                        si = stg_i[0]; stg_i[0] = (si + 1) % 3
                        sv = stg[si][:, 0:c1 - c0]
                        S.dma("sp", sv, Wv[:, k, c0:c1], w=[("stg", si)])
                        S.op(ceng, lambda e, sv=sv, k=k, c0=c0, c1=c1: e.tensor_copy(out=dst[:, k, c0:c1], in_=sv), r=[("stg", si)], w=[key])

        def load_w_cols(dst, W, key, KC, c0, c1, ceng="dve"):
            Wv = W.rearrange("(k p) n -> p k n", p=128)
            N = c1 - c0
            g = max(1, min(KC, 1024 // N))
            for k0 in range(0, KC, g):
                k1 = min(KC, k0 + g)
                si = stg_i[0]; stg_i[0] = (si + 1) % 3
                sv = stg[si][:, 0:(k1 - k0) * N].rearrange("p (k n) -> p k n", n=N)
                S.dma("sp", sv, Wv[:, k0:k1, c0:c1], w=[("stg", si)])
                S.op(ceng, lambda e, sv=sv, k0=k0, k1=k1: e.tensor_copy(out=dst[:, k0:k1, :], in_=sv), r=[("stg", si)], w=[key])

        def sincos(ang, shape, cos_o, sin_o, np_, tmp_f, tmp_i, tmp_m, key):
            for (shift, dst) in ((0.0, sin_o), (PI / 2, cos_o)):
                S.op("dve", lambda e: e.tensor_scalar(out=tmp_f, in0=ang, scalar1=shift, scalar2=None, op0=ALU.add), r=[key + "ang"], w=[key + "f"])
                S.op("dve", lambda e: e.tensor_scalar(out=tmp_i, in0=tmp_f, scalar1=float(1 / TWO_PI), scalar2=None, op0=ALU.mult), r=[key + "f"], w=[key + "i"])
                S.op("dve", lambda e: e.tensor_copy(out=tmp_m, in_=tmp_i), r=[key + "i"], w=[key + "m"])
                S.op("dve", lambda e: e.scalar_tensor_tensor(out=tmp_f, in0=tmp_m, scalar=-TWO_PI, in1=tmp_f, op0=ALU.mult, op1=ALU.add), r=[key + "m", key + "f"], w=[key + "f"])
                S.op("dve", lambda e: e.tensor_scalar(out=tmp_m, in0=tmp_f, scalar1=PI, scalar2=None, op0=ALU.is_gt), r=[key + "f"], w=[key + "m"])
                S.op("dve", lambda e: e.scalar_tensor_tensor(out=tmp_f, in0=tmp_m, scalar=-TWO_PI, in1=tmp_f, op0=ALU.mult, op1=ALU.add), r=[key + "m", key + "f"], w=[key + "f"])
                S.op("dve", lambda e: e.tensor_scalar(out=tmp_m, in0=tmp_f, scalar1=-PI, scalar2=None, op0=ALU.is_lt), r=[key + "f"], w=[key + "m"])
                S.op("dve", lambda e: e.scalar_tensor_tensor(out=tmp_f, in0=tmp_m, scalar=TWO_PI, in1=tmp_f, op0=ALU.mult, op1=ALU.add), r=[key + "m", key + "f"], w=[key + "f"])
                act(dst, tmp_f, AF.Sin, r=[key + "f"], w=[key + "out"])

        lo0, hi0 = M.lo, M.hi
        if upto <= -2:
            return finish(nc, S, st)
        posi = M.lo_alloc([NT], I32); posf = M.lo_alloc([NT], F32)
        posCi = M.lo_alloc([1], I32); posCf = M.lo_alloc([1], F32)
        inv16 = M.lo_alloc([16], F32); inv8 = M.lo_alloc([8], F32)
        angM = M.lo_alloc([NT, 16], F32); tfM = M.lo_alloc([NT, 16], F32); tiM = M.lo_alloc([NT, 16], I32); tmM = M.lo_alloc([NT, 16], F32)
        S.dma("sp", posi, pos_pk, w=["posi"])
        S.dma("sp", posCi[0:127], posC, w=["posCi"])
        S.dma("sp", inv16, inv16_d, w=["inv16"])
        S.dma("sp", inv8, inv8_d, w=["inv8"])
        S.op("dve", lambda e: e.tensor_copy(out=posf, in_=posi), r=["posi"], w=["posf"])
        S.op("dve", lambda e: e.tensor_copy(out=posCf[0:127], in_=posCi[0:127]), r=["posCi"], w=["posCf"])
        S.op("dve", lambda e: e.tensor_tensor(out=angM, in0=posf.unsqueeze(2).to_broadcast([128, NT, 16]),
                                              in1=inv16.unsqueeze(1).to_broadcast([128, NT, 16]), op=ALU.mult), r=["posf", "inv16"], w=["Mang"])
        sincos(angM, None, cosM, sinM, 128, tfM, tiM, tmM, "M")
        a8 = angM.rearrange("p a b -> p (a b)")[:, 0:NT * 8].rearrange("p (a b) -> p a b", b=8)
        f8 = tfM.rearrange("p a b -> p (a b)")[:, 0:NT * 8].rearrange("p (a b) -> p a b", b=8)
        i8 = tiM.rearrange("p a b -> p (a b)")[:, 0:NT * 8].rearrange("p (a b) -> p a b", b=8)
        m8_ = tmM.rearrange("p a b -> p (a b)")[:, 0:NT * 8].rearrange("p (a b) -> p a b", b=8)
        S.op("dve", lambda e: e.tensor_tensor(out=a8, in0=posf.unsqueeze(2).to_broadcast([128, NT, 8]),
                                              in1=inv8.unsqueeze(1).to_broadcast([128, NT, 8]), op=ALU.mult), r=["posf", "inv8", "Mout", "Mf", "Mm", "Mi"], w=["Nang"])
        sincos(a8, None, cosN, sinN, 128, f8, i8, m8_, "N")
        aC = angM.rearrange("p a b -> p (a b)")[0:127, 0:8]
        fC = tfM.rearrange("p a b -> p (a b)")[0:127, 0:8]
        iC = tiM.rearrange("p a b -> p (a b)")[0:127, 0:8]
        mC = tmM.rearrange("p a b -> p (a b)")[0:127, 0:8]
        S.op("dve", lambda e: e.tensor_scalar(out=aC, in0=inv8[0:127], scalar1=posCf[0:127, 0:1], scalar2=None, op0=ALU.mult),
             r=["posCf", "inv8", "Nout", "Nf", "Nm", "Ni", "Nang"], w=["Cang"])
        sincos(aC, None, cosC[0:127], sinC[0:127], 127, fC, iC, mC, "C")
        dbg_out("cosM", cosM, ["Mout"]); dbg_out("sinM", sinM, ["Mout"])

        if upto <= -1:
            return finish(nc, S, st)
        cpk = M.lo_alloc([8], F32); sc = M.lo_alloc([8], F32)
        sch = M.lo_alloc([8], BF16); scl = M.lo_alloc([8], BF16)
        cBh = M.lo_alloc([8, 128], BF16); cBl = M.lo_alloc([8, 128], BF16)
        modB = M.lo_alloc([6 * D], F32)
        g1t = M.lo_alloc([D], F32); g2t = M.lo_alloc([D], F32)
        awb = [M.hi_alloc([8, 512], F32) for _ in range(2)]
        abb = [M.hi_alloc([512], F32) for _ in range(2)]
        awh = [M.hi_alloc([8, 512], BF16) for _ in range(2)]
        awl = [M.hi_alloc([8, 512], BF16) for _ in range(2)]
        S.dma("sp", cpk, c_pk, w=["cpk"])
        S.dma("sp", g1t, g1B, w=["g1t"]); S.dma("sp", g2t, g2B, w=["g2t"])
        act(sc, cpk, AF.Silu, r=["cpk"], w=["sc"])
        S.op("dve", lambda e: e.tensor_copy(out=sch, in_=sc), r=["sc"], w=["sch"])
        S.op("dve", lambda e: e.tensor_tensor(out=scl, in0=sc, in1=sch, op=ALU.subtract), r=["sc", "sch"], w=["scl"])
        for k in range(8):
            S.op("dve", lambda e, k=k: e.tensor_copy(out=cBh[:, k, :], in_=sch[:, k:k + 1].to_broadcast([128, 128])), r=["sch"], w=["cBh"])
            S.op("dve", lambda e, k=k: e.tensor_copy(out=cBl[:, k, :], in_=scl[:, k:k + 1].to_broadcast([128, 128])), r=["scl"], w=["cBl"])
        awv = ada_w.rearrange("(k p) n -> p k n", p=128)
        for n in range(12):
            q_ = n % 2
            S.dma("sp", awb[q_], awv[:, :, n * 512:(n + 1) * 512], w=[("awb", q_)])
            S.dma("sp", abb[q_], adabB[:, n * 512:(n + 1) * 512], w=[("abb", q_)])
            act(awh[q_], awb[q_], AF.Copy, r=[("awb", q_)], w=[("awh", q_)])
            S.op("dve", lambda e, q_=q_: e.tensor_tensor(out=awl[q_], in0=awb[q_], in1=awh[q_], op=ALU.subtract), r=[("awb", q_), ("awh", q_)], w=[("awl", q_)])
            b = nb()
            passes = [(cBh, "cBh", awh, "awh"), (cBh, "cBh", awl, "awl"), (cBl, "cBl", awh, "awh")]
            for pi_, (cb_, ck, ww, wk) in enumerate(passes):
                for k in range(8):
                    mm(P[b][:, :], cb_[:, k, :], ww[q_][:, k, :], pi_ == 0 and k == 0, pi_ == 2 and k == 7, r=[ck, (wk, q_)], w=[PK[b]])
            S.op("dve", lambda e, n=n, b=b, q_=q_: e.tensor_tensor(out=modB[:, n * 512:(n + 1) * 512], in0=P[b][:, :], in1=abb[q_], op=ALU.add),
                 r=[PK[b], ("abb", q_)], w=["modB"])
        S.op("dve", lambda e: e.scalar_tensor_tensor(out=modB[:, D:2 * D], in0=modB[:, D:2 * D], scalar=1.0, in1=g1t, op0=ALU.add, op1=ALU.mult), r=["modB", "g1t"], w=["modB"])
        S.op("dve", lambda e: e.scalar_tensor_tensor(out=modB[:, 4 * D:5 * D], in0=modB[:, 4 * D:5 * D], scalar=1.0, in1=g2t, op0=ALU.add, op1=ALU.mult), r=["modB", "g2t"], w=["modB"])
        S.dma("sp", mods, modB, r=["modB"], w=["mods"])
        dbg_out("modB", modB, ["modB"])
        B1 = modB[:, 0:D]; A1 = modB[:, D:2 * D]
        if upto <= 0:
            return finish(nc, S, st)

        M.hi = hi0
        hT = M.hi_alloc([8, S_], BF16)
        xb = [M.lo_alloc([D], F32) for _ in range(2)]
        junk = M.lo_alloc([D], F32); tmpA = M.lo_alloc([D], F32)
        hb = [M.lo_alloc([D], BF16) for _ in range(2)]
        ssq = M.lo_alloc([NT], F32); rs = M.lo_alloc([NT], F32)

        def norm_tile(i, xt, xkey, A, B, Akeys, dstT, dkey):
            act(junk, xt, AF.Square, r=[xkey], w=["junk", ("ssq", i)], accum_out=ssq[:, i:i + 1])
            act(rs[:, i:i + 1], ssq[:, i:i + 1], AF.Sqrt, r=[("ssq", i)], w=[("rs", i)], scale=1.0 / D, bias=EPS)
            S.op("dve", lambda e: e.reciprocal(out=rs[:, i:i + 1], in_=rs[:, i:i + 1]), r=[("rs", i)], w=[("rs", i)])
            S.op("dve", lambda e: e.scalar_tensor_tensor(out=tmpA, in0=xt, scalar=rs[:, i:i + 1], in1=A, op0=ALU.mult, op1=ALU.mult),
                 r=[xkey, ("rs", i)] + Akeys, w=["tmpA"])
            S.op("pool", lambda e: e.tensor_tensor(out=hb[i % 2], in0=tmpA, in1=B, op=ALU.add), r=["tmpA"] + Akeys, w=[("hb", i % 2)])
            b = nb()
            pb = P[b][:, :].bitcast(BF16)
            for k in range(8):
                tp(pb[:, k * 128:(k + 1) * 128], hb[i % 2][:, k * 128:(k + 1) * 128], ident, r=[("hb", i % 2), "ident"], w=[PK[b]])
            act(dstT[:, :, i * 128:(i + 1) * 128], pb.rearrange("p (k t) -> p k t", k=8), AF.Copy, r=[PK[b]], w=[(dkey, i)])

        for i in range(NT):
            S.dma("sp", xb[i % 2], x[i * 128:(i + 1) * 128, :], w=[("xb", i % 2)])
            norm_tile(i, xb[i % 2], ("xb", i % 2), A1, B1, ["modB"], hT, "hT")
        hTk = [("hT", i) for i in range(NT)]
        S.dma("sp", hTs, hT.rearrange("p k t -> p (k t)"), r=hTk, w=["hTs"])
        dbg_out("hT", hT.rearrange("p k t -> p (k t)"), hTk)
        if upto <= 1:
            return finish(nc, S, st)
        S.barrier()

        M.lo = lo0
        cqT = M.lo_alloc([6, S_], BF16); ckvT = M.lo_alloc([2, S_], BF16); kpe = M.lo_alloc([NT, 32], F32)
        lo1 = M.lo
        Wcq = M.lo_alloc([8, 768], BF16); Wckv = M.lo_alloc([8, 256], BF16); Wkpe = M.lo_alloc([8, 32], BF16)
        qagt = M.lo_alloc([6], F32); kvagt = M.lo_alloc([2], F32)
        sqb = [M.lo_alloc([512], BF16) for _ in range(2)]
        rb = M.lo_alloc([512], F32)
        S.dma("sp", qagt, qag, w=["qagt"]); S.dma("sp", kvagt, kvag, w=["kvagt"])
        load_w(Wcq, w_cq, "Wcq", 8, 768); load_w(Wckv, w_ckv, "Wckv", 8, 256); load_w(Wkpe, w_kpe, "Wkpe", 8, 32)

        def fm_proj_norm(dstT, dkey, Wt, wkey, nf, gaint, gkey, nfeat):
            for c in range(4):
                hk = [("hT", 4 * c + q) for q in range(4)]
                for j in range(nf):
                    b = nb()
                    for k in range(8):
                        mm(P[b][:, :], Wt[:, k, j * 128:(j + 1) * 128], hT[:, k, c * 512:(c + 1) * 512], k == 0, k == 7, r=[wkey] + hk, w=[PK[b]])
                    act(dstT[:, j, c * 512:(c + 1) * 512], P[b][:, :], AF.Copy, r=[PK[b]], w=[(dkey, c)])
                    act(sqb[j % 2], P[b][:, :], AF.Square, r=[PK[b]], w=[("sqb", j % 2)])
                    mm(P[6][:, :], onesb, sqb[j % 2], j == 0, j == nf - 1, r=["onesb", ("sqb", j % 2)], w=[PK[6]])
                act(rb, P[6][:, :], AF.Sqrt, r=[PK[6]], w=["rb"], scale=1.0 / nfeat, bias=EPS)
                S.op("dve", lambda e: e.reciprocal(out=rb, in_=rb), r=["rb"], w=["rb"])
                for j in range(nf):
                    S.op("dve", lambda e, j=j, c=c: e.scalar_tensor_tensor(out=dstT[:, j, c * 512:(c + 1) * 512], in0=dstT[:, j, c * 512:(c + 1) * 512],
                                                                             scalar=gaint[:, j:j + 1], in1=rb, op0=ALU.mult, op1=ALU.mult),
                         r=[(dkey, c), "rb", gkey], w=[(dkey, c)])

        fm_proj_norm(cqT, "cqT", Wcq, "Wcq", 6, qagt, "qagt", 768)
        fm_proj_norm(ckvT, "ckvT", Wckv, "Wckv", 2, kvagt, "kvagt", 256)
        for i in range(NT):
            b = nb()
            for k in range(8):
                mm(P[b][:, 0:32], hT[:, k, i * 128:(i + 1) * 128], Wkpe[:, k, :], k == 0, k == 7, r=["Wkpe", ("hT", i)], w=[PK[b]])
            S.op("dve", lambda e, i=i, b=b: e.tensor_copy(out=kpe[:, i, :], in_=P[b][:, 0:32]), r=[PK[b]], w=[("kpe", i)])
        cqk = [("cqT", c) for c in range(4)]
        dbg_out("cqT", cqT.rearrange("p k t -> p (k t)"), cqk)
        dbg_out("kpe", kpe.rearrange("p a b -> p (a b)"), [("kpe", i) for i in range(NT)])
        if upto <= 2:
            return finish(nc, S, st)
        S.barrier()

        M.lo = lo1
        M.hi = hi0
        QT = M.hi_alloc([8, S_], BF16); KT = M.hi_alloc([8, S_], BF16); V = M.hi_alloc([NT, 8, 65], BF16)
        hi1 = M.hi
        Wqb = M.lo_alloc([6, 768], BF16); Wkvb = M.lo_alloc([2, 1024], BF16)
        qgt = M.lo_alloc([96], F32); kgt = M.lo_alloc([96], F32)
        raw = M.lo_alloc([768], F32); t1 = M.lo_alloc([768], F32); t2 = M.lo_alloc([768], F32)
        hss = M.lo_alloc([8], F32); hrs = M.lo_alloc([8], F32)
        ra = M.lo_alloc([128], F32); rbb = M.lo_alloc([128], F32)
        drb = [M.lo_alloc([768], BF16) for _ in range(2)]
        S.dma("sp", qgt, qgB, w=["qgt"]); S.dma("sp", kgt, kgB, w=["kgt"])
        load_w(Wqb, w_qb, "Wqb", 6, 768); load_w(Wkvb, w_kvb, "Wkvb", 2, 1024)
        S.op("pool", lambda e: e.memset(V[:, :, :, 64:65], 1.0), w=["Vones"])

        def head_norm_rope(src, skeys, H, Dh, gaint, gkey, ro, hf, cos_, sin_, dst, dkey, np_=128):
            n = H * Dh
            t1v = t1[0:np_, 0:n].rearrange("p (h d) -> p h d", h=H)
            t2v = t2[0:np_, 0:n].rearrange("p (h d) -> p h d", h=H)
            hs = hss[0:np_, 0:H]; hr = hrs[0:np_, 0:H]
            act(t1v, src, AF.Square, r=skeys, w=["t1"])
            S.op("dve", lambda e: e.tensor_reduce(out=hs, in_=t1v, axis=AX.X, op=ALU.add), r=["t1"], w=["hss"])
            act(hr, hs, AF.Sqrt, r=["hss"], w=["hrs"], scale=1.0 / Dh, bias=EPS)
            S.op("dve", lambda e: e.reciprocal(out=hr, in_=hr), r=["hrs"], w=["hrs"])
            S.op("dve", lambda e: e.tensor_tensor(out=t2v, in0=src, in1=hr.unsqueeze(2).to_broadcast([np_, H, Dh]), op=ALU.mult), r=skeys + ["hrs"], w=["t2"])
            S.op("dve", lambda e: e.tensor_tensor(out=t1v, in0=t2v, in1=gaint[0:np_].unsqueeze(1).to_broadcast([np_, H, Dh]), op=ALU.mult), r=["t2", gkey], w=["t1"])
            x1 = t1v[:, :, ro:ro + hf]; x2 = t1v[:, :, ro + hf:ro + 2 * hf]
            cb = cos_.unsqueeze(1).to_broadcast([np_, H, hf]); sb_ = sin_.unsqueeze(1).to_broadcast([np_, H, hf])
            rav = ra[0:np_, 0:H * hf].rearrange("p (h d) -> p h d", h=H)
            rbv = rbb[0:np_, 0:H * hf].rearrange("p (h d) -> p h d", h=H)
            tr = ["Mout", "Nout", "Cout"]
            S.op("dve", lambda e: e.tensor_tensor(out=rav, in0=x1, in1=cb, op=ALU.mult), r=["t1"] + tr, w=["ra"])
            S.op("dve", lambda e: e.tensor_tensor(out=rbv, in0=x2, in1=sb_, op=ALU.mult), r=["t1"] + tr, w=["rbb"])
            S.op("dve", lambda e: e.tensor_tensor(out=dst[:, :, ro:ro + hf], in0=rav, in1=rbv, op=ALU.subtract), r=["ra", "rbb"], w=[dkey])
            S.op("dve", lambda e: e.tensor_tensor(out=rav, in0=x2, in1=cb, op=ALU.mult), r=["t1"] + tr, w=["ra"])
            S.op("dve", lambda e: e.tensor_tensor(out=rbv, in0=x1, in1=sb_, op=ALU.mult), r=["t1"] + tr, w=["rbb"])
            S.op("dve", lambda e: e.tensor_tensor(out=dst[:, :, ro + hf:ro + 2 * hf], in0=rav, in1=rbv, op=ALU.add), r=["ra", "rbb"], w=[dkey])
            if ro > 0:
                S.op("pool", lambda e: e.tensor_copy(out=dst[:, :, 0:ro], in_=t1v[:, :, 0:ro]), r=["t1"], w=[dkey])
            if ro + 2 * hf < Dh:
                S.op("pool", lambda e: e.tensor_copy(out=dst[:, :, ro + 2 * hf:Dh], in_=t1v[:, :, ro + 2 * hf:Dh]), r=["t1"], w=[dkey])

        for i in range(NT):
            ts = slice(i * 128, (i + 1) * 128)
            bA, bB = nb(), nb()
            for k in range(6):
                mm(P[bA][:, :], cqT[:, k, ts], Wqb[:, k, 0:512], k == 0, k == 5, r=["Wqb", ("cqT", i // 4)], w=[PK[bA]])
            for k in range(6):
                mm(P[bB][:, 0:256], cqT[:, k, ts], Wqb[:, k, 512:768], k == 0, k == 5, r=["Wqb", ("cqT", i // 4)], w=[PK[bB]])
            act(raw[:, 0:512], P[bA][:, :], AF.Copy, r=[PK[bA]], w=["raw"])
            act(raw[:, 512:768], P[bB][:, 0:256], AF.Copy, r=[PK[bB]], w=["raw"])
            d = drb[0].rearrange("p (h d) -> p h d", h=8)
            head_norm_rope(raw.rearrange("p (h d) -> p h d", h=8), ["raw"], 8, 96, qgt, "qgt", 64, 16, cosM[:, i, :], sinM[:, i, :], d, "drb0")
            b = nb(); pb = P[b][:, :].bitcast(BF16)
            for h in range(8):
                tp(pb[0:96, h * 128:(h + 1) * 128], d[:, h, :], ident, r=["drb0", "ident"], w=[PK[b]])
            act(QT[0:96, :, ts], pb[0:96, :].rearrange("p (h t) -> p h t", h=8), AF.Copy, r=[PK[b]], w=[("QT", i)])
            bA, bB = nb(), nb()
            for hh, bb in ((0, bA), (1, bB)):
                for k in range(2):
                    mm(P[bb][:, :], ckvT[:, k, ts], Wkvb[:, k, hh * 512:(hh + 1) * 512], k == 0, k == 1, r=["Wkvb", ("ckvT", i // 4)], w=[PK[bb]])
            rv = raw.rearrange("p (h d) -> p h d", h=8)
            for hh, bb in ((0, bA), (1, bB)):
                pv = P[bb][:, :].rearrange("p (h d) -> p h d", h=4)
                act(rv[:, hh * 4:(hh + 1) * 4, 0:64], pv[:, :, 0:64], AF.Copy, r=[PK[bb]], w=["raw"])
                act(V[:, i, hh * 4:(hh + 1) * 4, 0:64], pv[:, :, 64:128], AF.Copy, r=[PK[bb]], w=[("V", i)])
            S.op("pool", lambda e, i=i: e.tensor_copy(out=rv[:, :, 64:96], in_=kpe[:, i, :].unsqueeze(1).to_broadcast([128, 8, 32])), r=[("kpe", i)], w=["raw"])
            d = drb[1].rearrange("p (h d) -> p h d", h=8)
            head_norm_rope(rv, ["raw"], 8, 96, kgt, "kgt", 64, 16, cosM[:, i, :], sinM[:, i, :], d, "drb1")
            b = nb(); pb = P[b][:, :].bitcast(BF16)
            for h in range(8):
                tp(pb[0:96, h * 128:(h + 1) * 128], d[:, h, :], ident, r=["drb1", "ident"], w=[PK[b]])
            act(KT[0:96, :, ts], pb[0:96, :].rearrange("p (h t) -> p h t", h=8), AF.Copy, r=[PK[b]], w=[("KT", i)])
        QTk = [("QT", i) for i in range(NT)]
        dbg_out("QT", QT[0:96].rearrange("p k t -> p (k t)"), QTk)
        dbg_out("KT", KT[0:96].rearrange("p k t -> p (k t)"), [("KT", i) for i in range(NT)])
        dbg_out("V", V.rearrange("p a b c -> p (a b c)"), [("V", i) for i in range(NT)] + ["Vones"])
        if upto <= 3:
            return finish(nc, S, st)
        S.barrier()

        M.lo = lo0
        om = M.lo_alloc([NT, 512], BF16)
        PT = [M.lo_alloc([512], BF16) for _ in range(3)]
        rec4 = [M.lo_alloc([4], F32) for _ in range(2)]
        pt_i = [0]

        gchunk = [0]

        def causal_attn_multi(jobs):
            steps = []
            for ji in range(len(jobs)):
                for c in range(4):
                    for kt in range(4 * c + 4):
                        steps.append((ji, c, kt))

            def emit_qk(step):
                ji, c, kt = step
                J = jobs[ji]
                q0 = max(kt - 4 * c, 0)
                n = 512 - 128 * q0
                b = nb()
                has_extra = J["extra"] is not None
                mm(P[b][:, 0:n], J["KT"][0:J["kn"], kt * 128:(kt + 1) * 128], J["QT"](c * 512 + q0 * 128, (c + 1) * 512),
                   True, not has_extra, r=J["kk"](kt) + J["qk"](c), w=[PK[b]])
                if has_extra:
                    J["extra"](P[b][:, 0:n], kt, c * 512 + q0 * 128, (c + 1) * 512, PK[b])
                return b, n, q0

            pend = emit_qk(steps[0])
            for si, (ji, c, kt) in enumerate(steps):
                J = jobs[ji]
                b, n, q0 = pend
                if si + 1 < len(steps):
                    pend = emit_qk(steps[si + 1])
                if kt == 0:
                    gchunk[0] += 1
                ab = 6 + (gchunk[0] % 2)
                Oacc = P[ab][:, 0:260].rearrange("p (q d) -> p q d", q=4)
                pi = pt_i[0]; pt_i[0] = (pi + 1) % 3
                pt = PT[pi]
                act(pt[:, 0:n], P[b][:, 0:n], AF.Exp, r=[PK[b]], w=[("PT", pi)], scale=J["scale"])
                if kt >= 4 * c:
                    S.op("dve", lambda e, pt=pt: e.tensor_tensor(out=pt[:, 0:128], in0=pt[:, 0:128], in1=tri, op=ALU.mult), r=[("PT", pi), "tri"], w=[("PT", pi)])
                for qi in range(q0, 4):
                    mm(Oacc[:, qi, :], pt[:, (qi - q0) * 128:(qi - q0 + 1) * 128], J["V"](kt), kt == 0 and qi == 0, kt == 4 * c + qi,
                       r=[("PT", pi)] + J["vk"](kt), w=[PK[ab]], skip_group_check=True)
                if kt == 4 * c + 3:
                    J["fin"](c, Oacc, PK[ab])

        jobs = []
        for h in range(8):
            def fin(c, Oacc, pk, h=h):
                rc = rec4[c % 2]
                S.op("dve", lambda e: e.reciprocal(out=rc, in_=Oacc[:, :, 64]), r=[pk], w=[("rec4", c % 2)])
                S.op("dve", lambda e: e.tensor_tensor(out=om[:, 4 * c:4 * c + 4, h * 64:(h + 1) * 64], in0=Oacc[:, :, 0:64],
                                                      in1=rc.unsqueeze(2).to_broadcast([128, 4, 64]), op=ALU.mult),
                     r=[pk, ("rec4", c % 2)], w=[("om", c)])
            jobs.append(dict(KT=KT[:, h, :], kn=96, QT=(lambda a, b_, h=h: QT[0:96, h, a:b_]), V=(lambda kt, h=h: V[:, kt, h, :]), scale=96 ** -0.5,
                             extra=None, fin=fin, qk=(lambda c: [("QT", 4 * c + q) for q in range(4)]), kk=(lambda kt: [("KT", kt)]),
                             vk=(lambda kt: [("V", kt), "Vones"])))
        causal_attn_multi(jobs)
        for i in range(NT):
            b = nb(); pb = P[b][:, :].bitcast(BF16)
            for j in range(4):
                tp(pb[:, j * 128:(j + 1) * 128], om[:, i, j * 128:(j + 1) * 128], ident, r=[("om", i // 4), "ident"], w=[PK[b]])
            act(omT[:, :, i * 128:(i + 1) * 128], pb[:, 0:512].rearrange("p (k t) -> p k t", k=4), AF.Copy, r=[PK[b]], w=[("omT", i)])
        dbg_out("om", om.rearrange("p a b -> p (a b)"), [("om", c) for c in range(4)])
        if upto <= 4:
            return finish(nc, S, st)
        S.barrier()

        M.lo = lo0; M.hi = hi0
        qnT = M.lo_alloc([8, S_], BF16); ksT = M.lo_alloc([2, S_], BF16); kwT = M.lo_alloc([2, S_], BF16)
        vs = M.lo_alloc([NT, 2, 65], BF16); vw = M.lo_alloc([NT, 2, 65], BF16)
        gates = M.lo_alloc([NT, 3, 8], F32)
        kcmpT = M.lo_alloc([2, 128], BF16); VCX = M.lo_alloc([2, 97], BF16)
        PT = [M.lo_alloc([512], BF16) for _ in range(3)]
        rec4 = [M.lo_alloc([4], F32) for _ in range(2)]
        t1 = M.lo_alloc([512], F32); t2 = M.lo_alloc([512], F32)
        hss = M.lo_alloc([8], F32); hrs = M.lo_alloc([8], F32)
        ra = M.lo_alloc([128], F32); rbb = M.lo_alloc([128], F32)
        drb = [M.lo_alloc([512], BF16) for _ in range(2)]
        nqt = M.lo_alloc([64], F32); nkct = M.lo_alloc([64], F32); nkst = M.lo_alloc([64], F32); nkwt = M.lo_alloc([64], F32)
        lo2 = M.lo
        kc2 = M.hi_alloc([2, S_], BF16); vc2 = M.hi_alloc([2, S_], BF16)
        hi_kv = M.hi
        hT = M.hi_alloc([8, S_], BF16)
        Wqn = M.hi_alloc([8, 512], BF16); Wkc2 = M.hi_alloc([8, 256], BF16); Wvc2 = M.hi_alloc([8, 256], BF16)
        Wkv4 = M.hi_alloc([8, 512], BF16); Wgn = M.hi_alloc([8, 24], BF16)
        ge = M.hi_alloc([24], F32)
        S.dma("sp", hT.rearrange("p k t -> p (k t)"), hTs, r=["hTs"], w=["hTall"])
        for t_, d_ in ((nqt, nqB), (nkct, nkcB), (nkst, nksB), (nkwt, nkwB)):
            S.dma("sp", t_, d_, w=["ngain"])
        load_w(Wqn, w_qn, "Wqn", 8, 512); load_w(Wkv4, w_kv4, "Wkv4", 8, 512); load_w(Wgn, w_gn, "Wgn", 8, 24)
        load_w(Wkc2, w_kc2, "Wkc2", 8, 256); load_w(Wvc2, w_vc2, "Wvc2", 8, 256)
        S.op("pool", lambda e: e.memset(vs[:, :, :, 64:65], 1.0), w=["vsones"])
        S.op("pool", lambda e: e.memset(vw[:, :, :, 64:65], 1.0), w=["vwones"])
        S.op("pool", lambda e: e.memset(kc2[64:128, :, S_ - 1:S_], 0.0), w=["kc2pad"])
        S.op("pool", lambda e: e.memset(vc2[64:128, :, S_ - 1:S_], 0.0), w=["vc2pad"])
        for i in range(NT):
            ts = slice(i * 128, (i + 1) * 128)
            b = nb()
            for k in range(8):
                mm(P[b][:, :], hT[:, k, ts], Wqn[:, k, :], k == 0, k == 7, r=["hTall", "Wqn"], w=[PK[b]])
            d = drb[0][:, 0:512].rearrange("p (h d) -> p h d", h=8)
            head_norm_rope(P[b][:, :].rearrange("p (h d) -> p h d", h=8), [PK[b]], 8, 64, nqt, "ngain", 0, 8, cosN[:, i, :], sinN[:, i, :], d, "drb0")
            b = nb(); pb = P[b][:, :].bitcast(BF16)
            for p_ in range(8):
                tp(pb[0:64, p_ * 128:(p_ + 1) * 128], drb[0][:, p_ * 64:(p_ + 1) * 64], ident, r=["drb0", "ident"], w=[PK[b]])
            act(qnT[0:64, :, ts], pb[0:64, :].rearrange("p (k t) -> p k t", k=8), AF.Copy, r=[PK[b]], w=[("qnT", i)])
            b = nb()
            for k in range(8):
                mm(P[b][:, :], hT[:, k, ts], Wkv4[:, k, :], k == 0, k == 7, r=["hTall", "Wkv4"], w=[PK[b]])
            act(vs[:, i, :, 0:64], P[b][:, 256:384].rearrange("p (g d) -> p g d", g=2), AF.Copy, r=[PK[b]], w=[("vs", i)])
            act(vw[:, i, :, 0:64], P[b][:, 384:512].rearrange("p (g d) -> p g d", g=2), AF.Copy, r=[PK[b]], w=[("vw", i)])
            for (c0, gt_, dstT, dk) in ((0, nkst, ksT, "ksT"), (128, nkwt, kwT, "kwT")):
                d = drb[1][:, 0:128].rearrange("p (h d) -> p h d", h=2)
                head_norm_rope(P[b][:, c0:c0 + 128].rearrange("p (h d) -> p h d", h=2), [PK[b]], 2, 64, gt_, "ngain", 0, 8, cosN[:, i, :], sinN[:, i, :], d, "drb1")
                b2 = nb(); pb = P[b2][:, :].bitcast(BF16)
                for g_ in range(2):
                    tp(pb[0:64, g_ * 128:(g_ + 1) * 128], drb[1][:, g_ * 64:(g_ + 1) * 64], ident, r=["drb1", "ident"], w=[PK[b2]])
                act(dstT[0:64, :, ts], pb[0:64, 0:256].rearrange("p (g t) -> p g t", g=2), AF.Copy, r=[PK[b2]], w=[(dk, i)])
            b = nb()
            for k in range(8):
                mm(P[b][:, 0:24], hT[:, k, ts], Wgn[:, k, :], k == 0, k == 7, r=["hTall", "Wgn"], w=[PK[b]])
            act(ge, P[b][:, 0:24], AF.Exp, r=[PK[b]], w=["ge"], scale=-1.0)
            S.op("dve", lambda e: e.tensor_scalar(out=ge, in0=ge, scalar1=1.0, scalar2=None, op0=ALU.add), r=["ge"], w=["ge"])
            S.op("dve", lambda e, i=i: e.reciprocal(out=gates[:, i].rearrange("p a b -> p (a b)"), in_=ge), r=["ge"], w=[("gates", i)])
        for c in range(4):
            for (Wt, wk, dst, dk) in ((Wkc2, "Wkc2", kc2, "kc2"), (Wvc2, "Wvc2", vc2, "vc2")):
                for g in range(2):
                    b = nb()
                    for k in range(8):
                        mm(P[b][:, :], Wt[:, k, g * 128:(g + 1) * 128], hT[:, k, c * 512:(c + 1) * 512], k == 0, k == 7, r=["hTall", wk], w=[PK[b]])
                    act(dst[0:64, g, c * 512:(c + 1) * 512], P[b][0:64, :], AF.Copy, r=[PK[b]], w=[dk])
                    if c == 0:
                        act(dst[64:128, g, 0:511], P[b][64:128, 1:512], AF.Copy, r=[PK[b]], w=[dk])
                    else:
                        act(dst[64:128, g, c * 512 - 1:(c + 1) * 512 - 1], P[b][64:128, :], AF.Copy, r=[PK[b]], w=[dk])
        dbg_out("qnT", qnT[0:64].rearrange("p k t -> p (k t)"), [("qnT", i) for i in range(NT)])
        dbg_out("ksT", ksT[0:64].rearrange("p k t -> p (k t)"), [("ksT", i) for i in range(NT)])
        dbg_out("gates", gates.rearrange("p a b c -> p (a b c)"), [("gates", i) for i in range(NT)])
        dbg_out("kc2", kc2.rearrange("p k t -> p (k t)"), ["kc2", "kc2pad"])
        if upto <= 5:
            return finish(nc, S, st)
        S.barrier()

        M.hi = hi_kv
        hiE = M.hi
        W1k = M.lo_alloc([16, 256], BF16); W1v = M.lo_alloc([16, 256], BF16)
        W2k = M.lo_alloc([2, 64], BF16); W2v = M.lo_alloc([2, 64], BF16)
        pkf = M.lo_alloc([16], F32); pvf = M.lo_alloc([16], F32); pkb = M.lo_alloc([16], BF16); pvb = M.lo_alloc([16], BF16)
        biask = M.lo_alloc([2], F32); biasv = M.lo_alloc([2], F32)
        hid = [M.lo_alloc([128], BF16) for _ in range(2)]
        ovl = M.lo_alloc([32], BF16)
        load_w(W1k, w1k, "W1k", 16, 256); load_w(W1v, w1v, "W1v", 16, 256)
        load_w(W2k, w2k, "W2k", 2, 64); load_w(W2v, w2v, "W2v", 2, 64)
        S.dma("sp", pkf, posk, w=["pkf"]); S.dma("sp", pvf, posv, w=["pvf"]); S.dma("sp", ovl[0:127], ovl_d, w=["ovl"])
        S.op("pool", lambda e: e.memset(VCX[0:127, :, 64:65], 1.0), w=["VCXa"])
        for g in range(2):
            S.op("pool", lambda e, g=g: e.tensor_copy(out=VCX[0:127, g, 65:97], in_=ovl[0:127]), r=["ovl"], w=["VCXb"])
        rt = [M.lo_alloc([128], BF16) for _ in range(3)]
        rt_i = [0]
        for (W1, w1key, W2, w2key, src, skey, posf_, pkey, isk) in ((W1k, "W1k", W2k, "W2k", kc2, ["kc2", "kc2pad"], pkf, "pkf", True),
                                                                    (W1v, "W1v", W2v, "W2v", vc2, ["vc2", "vc2pad"], pvf, "pvf", False)):
            srcv = src.rearrange("p g (n s) -> p g n s", s=16)
            bo = nb()
            for g in range(2):
                bh = []
                for hc in range(2):
                    b = nb()
                    while b == bo or b in bh:
                        b = nb()
                    bh.append(b)
                for lc in range(16):
                    ri = rt_i[0]; rt_i[0] = (ri + 1) % 3
                    rtv = rt[ri][:, 0:127]
                    S.op("pool", lambda e, rtv=rtv, g=g, lc=lc, srcv=srcv, posf_=posf_: e.tensor_scalar(
                        out=rtv, in0=srcv[:, g, (2 * lc) // 16:(2 * lc) // 16 + 127, (2 * lc) % 16], scalar1=posf_[:, lc:lc + 1], scalar2=None, op0=ALU.add),
                        r=skey + [pkey], w=[("rt", ri)])
                    for hc in range(2):
                        mm(P[bh[hc]][:, 0:127], W1[:, lc, hc * 128:(hc + 1) * 128], rtv, lc == 0, lc == 15, r=[w1key, ("rt", ri)], w=[PK[bh[hc]]])
                for hc in range(2):
                    act(hid[hc][:, 0:127], P[bh[hc]][:, 0:127], AF.Silu, r=[PK[bh[hc]]], w=[("hid", hc)])
                for hc in range(2):
                    mm(P[bo][0:127, g * 64:(g + 1) * 64], hid[hc][:, 0:127], W2[:, hc, :], hc == 0, hc == 1, r=[("hid", hc), w2key], w=[PK[bo]])
            if isk:
                d = drb[1][0:127, 0:128].rearrange("p (h d) -> p h d", h=2)
                head_norm_rope(P[bo][0:127, 0:128].rearrange("p (h d) -> p h d", h=2), [PK[bo]], 2, 64, nkct, "ngain", 0, 8, cosC[0:127], sinC[0:127], d, "drb1", np_=127)
                b2 = nb(); pb = P[b2][:, :].bitcast(BF16)
                for g_ in range(2):
                    tp(pb[0:64, g_ * 128:g_ * 128 + 127], drb[1][0:127, g_ * 64:(g_ + 1) * 64], ident[0:127, 0:127], r=["drb1", "ident"], w=[PK[b2]])
                act(kcmpT[0:64, :, 0:127], pb[0:64, 0:256].rearrange("p (g t) -> p g t", g=2)[:, :, 0:127], AF.Copy, r=[PK[b2]], w=["kcmpT"])
            else:
                act(VCX[0:127, :, 0:64], P[bo][0:127, 0:128].rearrange("p (g d) -> p g d", g=2), AF.Copy, r=[PK[bo]], w=["VCXc"])
        dbg_out("kcmpT", kcmpT[0:64].rearrange("p g t -> p (g t)"), ["kcmpT"])
        dbg_out("VCX", VCX[0:127].rearrange("p a b -> p (a b)"), ["VCXa", "VCXb", "VCXc"])
        if upto <= 6:
            return finish(nc, S, st)
        S.barrier()

        M.lo = lo2; M.hi = hi0
        onsa = M.hi_alloc([NT, 512], F32)
        mbT = M.hi_alloc([2, S_], BF16)
        vmT = M.hi_alloc([S_], BF16); XE = M.hi_alloc([NT, 128], BF16)
        fb = M.hi_alloc([NT, 32], F32); vj = M.hi_alloc([NT, 32], F32)
        pc = [M.lo_alloc([4, 128], BF16) for _ in range(2)]
        rsum = M.lo_alloc([8], F32); rec8 = M.lo_alloc([8], F32); gr = M.lo_alloc([8], F32)
        tmp_i = M.lo_alloc([8, 32], F32); imp = M.lo_alloc([2, 32], F32); m8 = M.lo_alloc([2, 8], F32)
        sel = M.lo_alloc([2, 32], F32); mbf = M.lo_alloc([2, 32], BF16)
        tmpo = M.lo_alloc([8, 64], F32)
        pw = [M.lo_alloc([3, 128], BF16) for _ in range(3)]
        onb = [M.lo_alloc([512], BF16) for _ in range(2)]
        S.dma("sp", vmT[0:127], vmT_d, w=["vmT"]); S.dma("sp", XE[0:32].rearrange("p a b -> p (a b)"), XE_d, w=["XE"])
        S.dma("sp", fb.rearrange("p a b -> p (a b)"), fb_d, w=["fb"]); S.dma("sp", vj.rearrange("p a b -> p (a b)"), vj_d, w=["vj"])
        VCXk = ["VCXa", "VCXb", "VCXc"]
        for i in range(NT):
            ts = slice(i * 128, (i + 1) * 128)
            import os
            if int(os.environ.get('KDEV_F', '9')) <= 0:
                continue
            sb_ = [nb(), nb()]
            ob = [nb(), nb()]
            for p in range(8):
                j, g = p // 2, p % 2
                if os.environ.get('KDEV_G0'):
                    g = 0
                mm(P[sb_[p // 4]][0:127, (p % 4) * 128:(p % 4 + 1) * 128], kcmpT[0:64, g, 0:127], qnT[0:64, p, ts], True, True,
                   r=["kcmpT", ("qnT", i)], w=[PK[sb_[p // 4]]])
            for hf_ in range(2):
                pcv = pc[hf_]
                act(pcv[0:127], P[sb_[hf_]][0:127, :].rearrange("p (a b) -> p a b", a=4), AF.Exp, r=[PK[sb_[hf_]]], w=[("pc", hf_)], scale=0.125)
                S.op("dve", lambda e, pcv=pcv: e.tensor_tensor(out=pcv[0:127], in0=pcv[0:127], in1=vmT[0:127, ts].unsqueeze(1).to_broadcast([127, 4, 128]), op=ALU.mult),
                     r=[("pc", hf_), "vmT"], w=[("pc", hf_)])
            import os
            FL = int(os.environ.get('KDEV_F', '9'))
            if FL <= 1:
                continue
            for p in range(8):
                g = p % 2
                mm(P[ob[p // 4]][:, (p % 4) * 97:(p % 4 + 1) * 97], pc[p // 4][0:127, p % 4, :], VCX[0:127, g, :], True, True,
                   r=[("pc", p // 4)] + VCXk, w=[PK[ob[p // 4]]])
            if FL <= 2:
                continue
            OC = [P[ob[h_]][:, 0:388].rearrange("p (a b) -> p a b", a=4) for h_ in range(2)]
            for h_ in range(2):
                S.op("dve", lambda e, h_=h_: e.tensor_scalar(out=rsum[:, h_ * 4:(h_ + 1) * 4], in0=OC[h_][:, :, 64], scalar1=1e-30, scalar2=None, op0=ALU.max), r=[PK[ob[h_]]], w=["rsum"])
            S.op("dve", lambda e: e.reciprocal(out=rec8, in_=rsum), r=["rsum"], w=["rec8"])
            S.op("dve", lambda e, i=i: e.tensor_tensor(out=gr, in0=gates[:, i, 0, :], in1=rec8, op=ALU.mult), r=["rec8", ("gates", i)], w=["gr"])
            for h_ in range(2):
                S.op("dve", lambda e, h_=h_, i=i: e.tensor_tensor(out=onsa[:, i, h_ * 256:(h_ + 1) * 256].rearrange("p (a b) -> p a b", a=4), in0=OC[h_][:, :, 0:64],
                                                                   in1=gr[:, h_ * 4:(h_ + 1) * 4].unsqueeze(2).to_broadcast([128, 4, 64]), op=ALU.mult),
                     r=[PK[ob[h_]], "gr"], w=[("onsa", i)])
                S.op("dve", lambda e, h_=h_: e.tensor_tensor(out=tmp_i[:, h_ * 4:(h_ + 1) * 4, :], in0=OC[h_][:, :, 65:97],
                                                             in1=rec8[:, h_ * 4:(h_ + 1) * 4].unsqueeze(2).to_broadcast([128, 4, 32]), op=ALU.mult),
                     r=[PK[ob[h_]], "rec8"], w=["tmp_i"])
            if FL <= 3:
                continue
            S.op("dve", lambda e: e.tensor_reduce(out=imp, in_=tmp_i.rearrange("t (j g) n -> t g n j", g=2), axis=AX.X, op=ALU.add), r=["tmp_i"], w=["imp"])
            S.op("dve", lambda e, i=i: e.tensor_tensor(out=imp, in0=imp, in1=fb[:, i, :].unsqueeze(1).to_broadcast([128, 2, 32]), op=ALU.add), r=["imp", "fb"], w=["imp"])
            if FL <= 4:
                continue
            for g in range(2):
                S.op("dve", lambda e, g=g: e.max(out=m8[:, g, :], in_=imp[:, g, :]), r=["imp"], w=["m8"])
                S.op("dve", lambda e, g=g: e.tensor_scalar(out=sel[:, g, :], in0=imp[:, g, :], scalar1=m8[:, g, 7:8], scalar2=None, op0=ALU.is_ge), r=["imp", "m8"], w=["sel"])
            S.op("dve", lambda e, i=i: e.tensor_tensor(out=sel, in0=sel, in1=vj[:, i, :].unsqueeze(1).to_broadcast([128, 2, 32]), op=ALU.mult), r=["sel", "vj"], w=["sel"])
            S.op("dve", lambda e: e.tensor_scalar(out=mbf, in0=sel, scalar1=-1.0, scalar2=30000.0, op0=ALU.add, op1=ALU.mult), r=["sel"], w=["mbf"])
            if i == 5:
                dbg_out("sel5", sel.rearrange("p a b -> p (a b)"), ["sel"])
                dbg_out("imp5", imp.rearrange("p a b -> p (a b)"), ["imp"])
            if FL <= 5:
                continue
            b = nb(); pb = P[b][:, :].bitcast(BF16)
            for g in range(2):
                tp(pb[0:32, g * 128:(g + 1) * 128], mbf[:, g, :], ident, r=["mbf", "ident"], w=[PK[b]])
            act(mbT[0:32, :, ts], pb[0:32, 0:256].rearrange("p (g t) -> p g t", g=2), AF.Copy, r=[PK[b]], w=[("mbT", i)])
        dbg_out("onsa_c", onsa.rearrange("p a b -> p (a b)"), [("onsa", i) for i in range(NT)])
        dbg_out("mbT", mbT[0:32].rearrange("p a b -> p (a b)"), [("mbT", i) for i in range(NT)])
        if upto <= 7:
            return finish(nc, S, st)

        jobs = []
        for p in range(8):
            j, g = p // 2, p % 2

            def extra(ps_ap, kt, a, b_, pk, g=g):
                mm(ps_ap, XE[0:32, kt, :], mbT[0:32, g, a:b_], False, True, r=["XE"] + [("mbT", q) for q in range(a // 128, b_ // 128)], w=[pk])

            def fin(c, Oacc, pk, p=p):
                rc = rec4[c % 2]
                S.op("dve", lambda e: e.reciprocal(out=rc, in_=Oacc[:, :, 64]), r=[pk], w=[("rec4", c % 2)])
                S.op("dve", lambda e: e.tensor_tensor(out=rc, in0=rc, in1=gates[:, 4 * c:4 * c + 4, 1, p], op=ALU.mult), r=[("rec4", c % 2)] + [("gates", 4 * c + q) for q in range(4)], w=[("rec4", c % 2)])
                tv = tmpo[:, 0:4, :]
                S.op("dve", lambda e: e.tensor_tensor(out=tv, in0=Oacc[:, :, 0:64], in1=rc.unsqueeze(2).to_broadcast([128, 4, 64]), op=ALU.mult), r=[pk, ("rec4", c % 2)], w=["tmpo"])
                S.op("pool", lambda e: e.tensor_tensor(out=onsa[:, 4 * c:4 * c + 4, p * 64:(p + 1) * 64], in0=onsa[:, 4 * c:4 * c + 4, p * 64:(p + 1) * 64], in1=tv, op=ALU.add),
                     r=["tmpo"] + [("onsa", 4 * c + q) for q in range(4)], w=[("onsa", 4 * c + q) for q in range(4)])
            jobs.append(dict(KT=ksT[:, g, :], kn=64, QT=(lambda a, b_, p=p: qnT[0:64, p, a:b_]), V=(lambda kt, g=g: vs[:, kt, g, :]), scale=0.125,
                             extra=extra, fin=fin, qk=(lambda c: [("qnT", 4 * c + q) for q in range(4)]), kk=(lambda kt: [("ksT", kt)]),
                             vk=(lambda kt: [("vs", kt), "vsones"])))
        causal_attn_multi(jobs)
        dbg_out("onsa_cs", onsa.rearrange("p a b -> p (a b)"), [("onsa", i) for i in range(NT)])
        if upto <= 8:
            return finish(nc, S, st)

        pw_i = [0]
        ob = [6, 7]

        def emit_ws(i, p):
            g = p % 2
            kts = [kt for kt in (i - 2, i - 1, i) if kt >= 0]
            b = nb()
            for kt in kts:
                sl = kt - (i - 2)
                mm(P[b][:, sl * 128:(sl + 1) * 128], kwT[0:64, g, kt * 128:(kt + 1) * 128], qnT[0:64, p, i * 128:(i + 1) * 128], True, True,
                   r=[("kwT", kt), ("qnT", i)], w=[PK[b]])
            return b

        wsteps = [(i, p) for i in range(NT) for p in range(8)]
        wpend = emit_ws(*wsteps[0])
        for wi_, (i, p) in enumerate(wsteps):
            ts = slice(i * 128, (i + 1) * 128)
            g = p % 2
            kts = [kt for kt in (i - 2, i - 1, i) if kt >= 0]
            b = wpend
            if wi_ + 1 < len(wsteps):
                wpend = emit_ws(*wsteps[wi_ + 1])
            s0 = kts[0] - (i - 2)
            wi = pw_i[0]; pw_i[0] = (wi + 1) % 3
            pwv = pw[wi]
            act(pwv[:, s0:3, :], P[b][:, s0 * 128:384].rearrange("p (a b) -> p a b", b=128), AF.Exp, r=[PK[b]], w=[("pw", wi)], scale=0.125)
            S.op("dve", lambda e, pwv=pwv, s0=s0: e.tensor_tensor(out=pwv[:, s0:3, :], in0=pwv[:, s0:3, :], in1=winm[:, s0:3, :], op=ALU.mult), r=[("pw", wi), "winm"], w=[("pw", wi)])
            for kt in kts:
                sl = kt - (i - 2)
                mm(P[ob[p // 4]][:, (p % 4) * 65:(p % 4 + 1) * 65], pwv[:, sl, :], vw[:, kt, g, :], kt == kts[0], kt == kts[-1],
                   r=[("pw", wi), ("vw", kt), "vwones"], w=[PK[ob[p // 4]]])
            if p != 7:
                continue
            OW = [P[ob[h_]][:, 0:260].rearrange("p (a b) -> p a b", a=4) for h_ in range(2)]
            for h_ in range(2):
                S.op("dve", lambda e, h_=h_: e.reciprocal(out=rec8[:, h_ * 4:(h_ + 1) * 4], in_=OW[h_][:, :, 64]), r=[PK[ob[h_]]], w=["rec8"])
            S.op("dve", lambda e, i=i: e.tensor_tensor(out=gr, in0=gates[:, i, 2, :], in1=rec8, op=ALU.mult), r=["rec8", ("gates", i)], w=["gr"])
            for h_ in range(2):
                S.op("dve", lambda e, h_=h_: e.tensor_tensor(out=tmpo[:, h_ * 4:(h_ + 1) * 4, :], in0=OW[h_][:, :, 0:64],
                                                             in1=gr[:, h_ * 4:(h_ + 1) * 4].unsqueeze(2).to_broadcast([128, 4, 64]), op=ALU.mult), r=[PK[ob[h_]], "gr"], w=["tmpo"])
            S.op("pool", lambda e, i=i: e.tensor_tensor(out=onb[i % 2], in0=onsa[:, i, :], in1=tmpo.rearrange("p a b -> p (a b)"), op=ALU.add), r=["tmpo", ("onsa", i)], w=[("onb", i % 2)])
            if "onsa_all" in D_:
                S.dma("sp", D_["onsa_all"][:, i * 512:(i + 1) * 512], onb[i % 2], r=[("onb", i % 2)])
            b = nb(); pb = P[b][:, :].bitcast(BF16)
            for j in range(4):
                tp(pb[:, j * 128:(j + 1) * 128], onb[i % 2][:, j * 128:(j + 1) * 128], ident, r=[("onb", i % 2), "ident"], w=[PK[b]])
            act(onT[:, :, ts], pb[:, 0:512].rearrange("p (k t) -> p k t", k=4), AF.Copy, r=[PK[b]], w=[("onT", i)])
        if upto <= 9:
            return finish(nc, S, st)
        S.barrier()

        M.lo = lo0; M.hi = hi0
        hT = M.hi_alloc([8, S_], BF16)
        mergedT = M.hi_alloc([8, S_], BF16)
        hi2 = M.hi
        Wgm = M.lo_alloc([8, 512], BF16); Wgnm = M.lo_alloc([8, 512], BF16); Wom = M.lo_alloc([4, 512], BF16); Won = M.lo_alloc([4, 512], BF16)
        e3 = M.lo_alloc([512], F32); e4 = M.lo_alloc([512], F32); tA = M.lo_alloc([512], F32); tB = M.lo_alloc([512], F32)
        mgb = [M.lo_alloc([512], BF16) for _ in range(2)]
        S.dma("sp", hT.rearrange("p k t -> p (k t)"), hTs, r=["hTs"], w=["hTall"])
        for cc in range(2):
            load_w_cols(Wgm, w_gm, "Wgm", 8, cc * 512, (cc + 1) * 512); load_w_cols(Wgnm, w_gnm, "Wgnm", 8, cc * 512, (cc + 1) * 512)
            load_w_cols(Wom, wo_mla, "Wom", 4, cc * 512, (cc + 1) * 512); load_w_cols(Won, wo_nsa, "Won", 4, cc * 512, (cc + 1) * 512)
            for i in range(NT):
                ts = slice(i * 128, (i + 1) * 128)
                b1, b2, b3, b4 = nb(), nb(), nb(), nb()
                for k in range(4):
                    mm(P[b1][:, :], omT[:, k, ts], Wom[:, k, :], k == 0, k == 3, r=[("omT", i), "Wom"], w=[PK[b1]])
                for k in range(4):
                    mm(P[b2][:, :], onT[:, k, ts], Won[:, k, :], k == 0, k == 3, r=[("onT", i), "Won"], w=[PK[b2]])
                for k in range(8):
                    mm(P[b3][:, :], hT[:, k, ts], Wgm[:, k, :], k == 0, k == 7, r=["hTall", "Wgm"], w=[PK[b3]])
                for k in range(8):
                    mm(P[b4][:, :], hT[:, k, ts], Wgnm[:, k, :], k == 0, k == 7, r=["hTall", "Wgnm"], w=[PK[b4]])
                act(e3, P[b3][:, :], AF.Sigmoid, r=[PK[b3]], w=["e3"])
                act(e4, P[b4][:, :], AF.Sigmoid, r=[PK[b4]], w=["e4"])
                S.op("dve", lambda e, b1=b1: e.tensor_tensor(out=tA, in0=P[b1][:, :], in1=e3, op=ALU.mult), r=[PK[b1], "e3"], w=["tA"])
                S.op("dve", lambda e, b2=b2: e.tensor_tensor(out=tB, in0=P[b2][:, :], in1=e4, op=ALU.mult), r=[PK[b2], "e4"], w=["tB"])
                S.op("pool", lambda e, i=i: e.tensor_tensor(out=mgb[i % 2], in0=tA, in1=tB, op=ALU.add), r=["tA", "tB"], w=[("mgb", i % 2)])
                if "merged" in D_:
                    S.dma("sp", D_["merged"][i * 128:(i + 1) * 128, cc * 512:(cc + 1) * 512], mgb[i % 2], r=[("mgb", i % 2)])
                b = nb(); pb = P[b][:, :].bitcast(BF16)
                for j in range(4):
                    tp(pb[:, j * 128:(j + 1) * 128], mgb[i % 2][:, j * 128:(j + 1) * 128], ident, r=[("mgb", i % 2), "ident"], w=[PK[b]])
                act(mergedT[:, cc * 4:(cc + 1) * 4, ts], pb[:, 0:512].rearrange("p (k t) -> p k t", k=4), AF.Copy, r=[PK[b]], w=[("mergedT", i)])
        if upto <= 10:
            return finish(nc, S, st)
        S.barrier()

        M.lo = lo0
        h2T = hT
        Wout = M.lo_alloc([8, D], BF16)
        G1 = M.lo_alloc([D], F32); A2 = M.lo_alloc([D], F32); B2 = M.lo_alloc([D], F32)
        xb = [M.lo_alloc([D], F32) for _ in range(2)]
        x1t = [M.lo_alloc([D], F32) for _ in range(2)]
        junk = M.lo_alloc([D], F32); tmpA = M.lo_alloc([D], F32)
        hb = [M.lo_alloc([D], BF16) for _ in range(2)]
        ssq = M.lo_alloc([NT], F32); rs = M.lo_alloc([NT], F32)
        S.dma("sp", G1, mods[:, 2 * D:3 * D], r=["mods"], w=["G1"])
        S.dma("sp", B2, mods[:, 3 * D:4 * D], r=["mods"], w=["AB2"])
        S.dma("sp", A2, mods[:, 4 * D:5 * D], r=["mods"], w=["AB2"])
        load_w(Wout, w_out, "Wout", 8, D, ceng="pool")
        for i in range(NT):
            ts = slice(i * 128, (i + 1) * 128)
            S.dma("sp", xb[i % 2], x[ts, :], w=[("xb", i % 2)])
            for cc in range(2):
                b = nb()
                for k in range(8):
                    mm(P[b][:, :], mergedT[:, k, ts], Wout[:, k, cc * 512:(cc + 1) * 512], k == 0, k == 7, r=[("mergedT", i), "Wout"], w=[PK[b]])
                S.op("dve", lambda e, b=b, cc=cc: e.tensor_tensor(out=tmpA[:, cc * 512:(cc + 1) * 512], in0=P[b][:, :], in1=G1[:, cc * 512:(cc + 1) * 512], op=ALU.mult), r=[PK[b], "G1"], w=["tmpA"])
                S.op("pool", lambda e, i=i, cc=cc: e.tensor_tensor(out=x1t[i % 2][:, cc * 512:(cc + 1) * 512], in0=tmpA[:, cc * 512:(cc + 1) * 512], in1=xb[i % 2][:, cc * 512:(cc + 1) * 512], op=ALU.add),
                     r=["tmpA", ("xb", i % 2)], w=[("x1t", i % 2)])
            S.dma("sp", x1s[ts, :], x1t[i % 2], r=[("x1t", i % 2)], w=[("x1s", i)])
            norm_tile(i, x1t[i % 2], ("x1t", i % 2), A2, B2, ["AB2"], h2T, "h2T")
        if "x1" in D_:
            S.dma("sp", D_["x1"], x1s, r=[("x1s", i) for i in range(NT)])
        if upto <= 11:
            return finish(nc, S, st)
        S.barrier()

        M.lo = lo_pers; M.hi = hi2 + 8 * S_ * 2
        Wd = M.hi_alloc([NFC, D], BF16)
        actT = M.hi_alloc([NFC, 1024], BF16)
        G2 = M.lo_alloc([D], F32)
        Wg2 = [M.lo_alloc([8, 256], BF16) for _ in range(2)]; Wu2 = [M.lo_alloc([8, 256], BF16) for _ in range(2)]
        sg = [M.lo_alloc([512], F32) for _ in range(2)]
        xb = [M.lo_alloc([D], F32) for _ in range(2)]
        ot = [M.lo_alloc([D], F32) for _ in range(2)]
        tmpA = M.lo_alloc([D], F32)
        S.dma("sp", G2, mods[:, 5 * D:6 * D], r=["mods"], w=["G2"])
        load_w(Wd, wd, "Wd", NFC, D)
        h2k = [("h2T", i) for i in range(NT)]
        out_toks = []
        for half in range(2):
            def ld(jg_):
                wb_ = jg_ % 2
                load_w_cols(Wg2[wb_], wg, ("Wg2", wb_), 8, jg_ * 256, (jg_ + 1) * 256)
                load_w_cols(Wu2[wb_], wu, ("Wu2", wb_), 8, jg_ * 256, (jg_ + 1) * 256)
            ld(0)
            for jg in range(NFC // 2):
                wb = jg % 2
                if jg + 1 < NFC // 2:
                    ld(jg + 1)
                for jj in range(2):
                    j = jg * 2 + jj
                    for tc in range(2):
                        t0 = half * 1024 + tc * 512
                        bg, bu = nb(), nb()
                        for k in range(8):
                            mm(P[bg][:, :], Wg2[wb][:, k, jj * 128:(jj + 1) * 128], h2T[:, k, t0:t0 + 512], k == 0, k == 7, r=[("Wg2", wb)] + h2k, w=[PK[bg]])
                        for k in range(8):
                            mm(P[bu][:, :], Wu2[wb][:, k, jj * 128:(jj + 1) * 128], h2T[:, k, t0:t0 + 512], k == 0, k == 7, r=[("Wu2", wb)] + h2k, w=[PK[bu]])
                        act(sg[tc], P[bg][:, :], AF.Silu, r=[PK[bg]], w=[("sg", tc)])
                        S.op("dve", lambda e, j=j, tc=tc, bu=bu: e.tensor_tensor(out=actT[:, j, tc * 512:(tc + 1) * 512], in0=P[bu][:, :], in1=sg[tc], op=ALU.mult),
                             r=[PK[bu], ("sg", tc)], w=[("actT", tc)])
            for il in range(8):
                i = half * 8 + il
                ts = slice(i * 128, (i + 1) * 128)
                S.dma("sp", xb[i % 2], x1s[ts, :], r=[("x1s", i)], w=[("xb", i % 2)])
                for cc in range(2):
                    b = nb()
                    for j in range(NFC):
                        mm(P[b][:, :], actT[:, j, il * 128:(il + 1) * 128], Wd[:, j, cc * 512:(cc + 1) * 512], j == 0, j == NFC - 1, r=[("actT", il // 4), "Wd"], w=[PK[b]])
                    S.op("dve", lambda e, b=b, cc=cc: e.tensor_tensor(out=tmpA[:, cc * 512:(cc + 1) * 512], in0=P[b][:, :], in1=G2[:, cc * 512:(cc + 1) * 512], op=ALU.mult), r=[PK[b], "G2"], w=["tmpA"])
                    S.op("pool", lambda e, i=i, cc=cc: e.tensor_tensor(out=ot[i % 2][:, cc * 512:(cc + 1) * 512], in0=tmpA[:, cc * 512:(cc + 1) * 512], in1=xb[i % 2][:, cc * 512:(cc + 1) * 512], op=ALU.add),
                         r=["tmpA", ("xb", i % 2)], w=[("ot", i % 2)])
                S.dma("sp", out[ts, :], ot[i % 2], r=[("ot", i % 2)], w=[("out", i)])
        return finish(nc, S, st)


def finish(nc, S, st):
    for q in S.dsem:
        for i in range(len(S.dsem[q])):
            if S.dcnt[q][i]:
                S._wait("sp", (("d", q, i), S.dcnt[q][i]))
    for e2 in ("pe", "act", "dve", "pool"):
        if S.cnt[e2]:
            S._wait("sp", (e2, S.cnt[e2]))
    st.close()
    return nc


def _consts():
    bf = ml_dtypes.bfloat16
    c = {}
    c["ident"] = np.eye(128, dtype=np.float32).astype(bf)
    a = np.arange(128)
    c["tri"] = (a[:, None] <= a[None, :]).astype(np.float32).astype(bf)
    w = np.zeros((128, 3, 128), np.float32)
    w[:, 0, :] = (a[:, None] > a[None, :])
    w[:, 1, :] = 1.0
    w[:, 2, :] = (a[:, None] <= a[None, :])
    c["winm"] = w.reshape(128, 384).astype(bf)
    inv16 = (np.float32(500000.0) ** (-np.arange(0, 32, 2, dtype=np.float32) / np.float32(32))).astype(np.float32)
    inv8 = (np.float32(500000.0) ** (-np.arange(0, 16, 2, dtype=np.float32) / np.float32(16))).astype(np.float32)
    c["inv16"] = np.tile(inv16[None], (128, 1)).astype(np.float32)
    c["inv8"] = np.tile(inv8[None], (128, 1)).astype(np.float32)
    n = np.arange(127)
    starts = n * 16
    j = np.arange(32)
    ovl = ((starts[:, None] < j[None, :] * 64 + 64) & (starts[:, None] + 32 > j[None, :] * 64))
    c["ovl"] = ovl.astype(np.float32).astype(bf)
    t = np.arange(S_)
    c["vmT"] = ((starts[:, None] + 31) <= t[None, :]).astype(np.float32).astype(bf)
    XE = np.zeros((32, NT, 128), np.float32)
    for kt in range(NT):
        XE[2 * kt, kt, 0:64] = 1.0
        XE[2 * kt + 1, kt, 64:128] = 1.0
    c["XE"] = XE.reshape(32, NT * 128).astype(bf)
    cur = (t // 64)
    forced = (j[None, :] == 0) | (j[None, :] == cur[:, None]) | (j[None, :] == cur[:, None] - 1)
    valid = j[None, :] <= cur[:, None]
    fb = np.where(valid, np.where(forced, 1e4, 0.0), -1e30).astype(np.float32)
    c["fb"] = fb.reshape(NT, 128, 32).transpose(1, 0, 2).reshape(128, NT * 32).copy()
    c["vj"] = valid.astype(np.float32).reshape(NT, 128, 32).transpose(1, 0, 2).reshape(128, NT * 32).copy()
    return c


def _rep(v, n=128):
    return np.ascontiguousarray(np.broadcast_to(np.asarray(v, np.float32)[None, :], (n, v.shape[0])))


def prep_inputs(inp):
    f = lambda a: np.ascontiguousarray(np.asarray(a, dtype=np.float32))
    w_in = f(inp["w_in"][0])
    o = np.cumsum([0, 768, 256, 32, 512, 128, 128, 128, 128, 128, 128, 24, 1024, 1024])
    seg = lambda i: w_in[:, o[i]:o[i + 1]]
    shared = {}
    shared["ada_w"] = f(inp["ada_w"][0]); shared["adabB"] = _rep(f(inp["ada_b"][0]))
    shared["g1B"] = _rep(f(inp["norm1_gain"][0])); shared["g2B"] = _rep(f(inp["norm2_gain"][0]))
    shared["w_cq"] = f(seg(0)); shared["w_ckv"] = f(seg(1)); shared["w_kpe"] = f(seg(2))
    qn = seg(3).reshape(D, 8, 64)
    shared["w_qn"] = f(qn[:, PH, :].reshape(D, 512))
    kc = seg(4).reshape(D, 2, 64); vc = seg(5).reshape(D, 2, 64)
    shared["w_kc2"] = f(np.stack([kc[:, 0], kc[:, 0], kc[:, 1], kc[:, 1]], 1).reshape(D, 256))
    shared["w_vc2"] = f(np.stack([vc[:, 0], vc[:, 0], vc[:, 1], vc[:, 1]], 1).reshape(D, 256))
    shared["w_kv4"] = f(np.concatenate([seg(6), seg(8), seg(7), seg(9)], 1))
    gn = seg(10).reshape(D, 8, 3)
    shared["w_gn"] = f(gn[:, PH, :].transpose(0, 2, 1).reshape(D, 24))
    shared["w_gm"] = f(seg(11)); shared["w_gnm"] = f(seg(12))
    shared["qag"] = f(f(inp["mla_q_a_gain"][0]).reshape(6, 128).T); shared["kvag"] = f(f(inp["mla_kv_a_gain"][0]).reshape(2, 128).T)
    shared["w_qb"] = f(inp["mla_w_q_b"][0]); shared["w_kvb"] = f(inp["mla_w_kv_b"][0])
    shared["qgB"] = _rep(f(inp["mla_q_gain"][0])); shared["kgB"] = _rep(f(inp["mla_k_gain"][0]))
    shared["nqB"] = _rep(f(inp["nsa_q_gain"][0])); shared["nkcB"] = _rep(f(inp["nsa_kc_gain"][0]))
    shared["nksB"] = _rep(f(inp["nsa_ks_gain"][0])); shared["nkwB"] = _rep(f(inp["nsa_kw_gain"][0]))
    shared["posk"] = f(f(inp["cmp_pos_k"][0]).reshape(16, 128).T); shared["posv"] = f(f(inp["cmp_pos_v"][0]).reshape(16, 128).T)
    shared["w1k"] = f(inp["cmp_w1_k"][0]); shared["w2k"] = f(inp["cmp_w2_k"][0])
    shared["w1v"] = f(inp["cmp_w1_v"][0]); shared["w2v"] = f(inp["cmp_w2_v"][0])
    shared["wo_mla"] = f(inp["w_o_mla"][0])
    shared["wo_nsa"] = f(f(inp["w_o_nsa"][0]).reshape(8, 64, D)[PH].reshape(512, D))
    shared["w_out"] = f(inp["w_out"][0])
    shared["wg"] = f(inp["ffn_w_gate"][0]); shared["wu"] = f(inp["ffn_w_up"][0]); shared["wd"] = f(inp["ffn_w_down"][0])
    shared.update(_consts())
    maps = []
    xs = np.asarray(inp["x"], np.float32); cs = np.asarray(inp["c"], np.float32); ps = np.asarray(inp["positions"]).astype(np.int32)
    for b in range(xs.shape[0]):
        m = dict(shared)
        m["x"] = np.ascontiguousarray(xs[b])
        m["c_pk"] = np.ascontiguousarray(cs[b].reshape(8, 128).T)
        m["pos_pk"] = np.ascontiguousarray(ps[b].reshape(NT, 128).T)
        m["posC"] = np.ascontiguousarray(ps[b][31::16][:127].reshape(127, 1))
        maps.append(m)
    return maps


_NC_CACHE = {}


def kernel(**inputs):
    maps = prep_inputs(inputs)
    if "nc" not in _NC_CACHE:
        _NC_CACHE["nc"] = build()
    nc = _NC_CACHE["nc"]
    res = run_bass_kernel_spmd(nc, maps, core_ids=list(range(len(maps))))
    return np.stack([np.asarray(r["out"], dtype=np.float32) for r in res.results], 0)
```

```python
import contextlib
import numpy as np
import ml_dtypes
import concourse.bass as bass
import concourse.mybir as mybir
from concourse.bass_utils import run_bass_kernel_spmd

F32 = mybir.dt.float32
BF16 = mybir.dt.bfloat16
I32 = mybir.dt.int32
ALU = mybir.AluOpType
AF = mybir.ActivationFunctionType
AX = mybir.AxisListType

S_ = 2048
D = 1024
NT = 16
DFF = 2816
NFC = 22
EPS = 1e-6
PH = [0, 4, 1, 5, 2, 6, 3, 7]
TWO_PI = float(2 * np.pi)
PI = float(np.pi)


class Sched:
    N_DMA_SLOTS = {"sp": 24, "pool": 8, "act": 4}

    def __init__(self, nc, stack):
        self.nc = nc
        self.E = {"pe": nc.tensor, "act": nc.scalar, "dve": nc.vector, "pool": nc.gpsimd, "sp": nc.sync}
        self.sem, self.cnt = {}, {}
        for e in ("pe", "act", "dve", "pool"):
            self.sem[e] = stack.enter_context(nc.semaphore("s_" + e))
            self.cnt[e] = 0
        self.dsem, self.dcnt, self.dnext = {}, {}, {}
        for q, n in self.N_DMA_SLOTS.items():
            self.dsem[q] = [stack.enter_context(nc.semaphore(f"d_{q}{i}")) for i in range(n)]
            self.dcnt[q] = [0] * n
            self.dnext[q] = 0
        self.seen = {e: {} for e in self.E}
        self.lastw, self.readers = {}, {}
        self.n_wait = 0
        self.n_inst = 0

    def _sem_of(self, src):
        return self.dsem[src[1]][src[2]] if isinstance(src, tuple) else self.sem[src]

    def _wait(self, e, tok):
        src, val = tok
        if self.seen[e].get(src, 0) >= val:
            return
        self.E[e].wait_ge(self._sem_of(src), val)
        self.seen[e][src] = val
        self.n_wait += 1

    def _deps(self, e, r, w):
        toks = []
        for k in r:
            t = self.lastw.get(k)
            if t is not None:
                toks.append(t)
        for k in w:
            t = self.lastw.get(k)
            if t is not None:
                toks.append(t)
            for t in self.readers.get(k, ()):
                toks.append(t)
        for t in toks:
            if t[0] == e and e == "pe":
                continue
            self._wait(e, t)

    def _commit(self, tok, r, w):
        for k in r:
            lst = self.readers.setdefault(k, [])
            lst[:] = [t for t in lst if t[0] != tok[0]]
            lst.append(tok)
        for k in w:
            self.lastw[k] = tok
            self.readers[k] = []

    def op(self, e, fn, r=(), w=()):
        self._deps(e, r, w)
        ins = fn(self.E[e])
        self.cnt[e] += 1
        ins.then_inc(self.sem[e], 1)
        tok = (e, self.cnt[e])
        self._commit(tok, r, w)
        self.n_inst += 1
        return tok

    def dma(self, q, out, in_, r=(), w=(), **kw):
        slot = self.dnext[q]
        self.dnext[q] = (slot + 1) % len(self.dsem[q])
        src = ("d", q, slot)
        if self.dcnt[q][slot] > 0:
            self._wait(q, (src, self.dcnt[q][slot]))
        self._deps(q, r, w)
        ins = self.E[q].dma_start(out=out, in_=in_, **kw)
        self.dcnt[q][slot] += 16
        ins.then_inc(self.dsem[q][slot], 16)
        tok = (src, self.dcnt[q][slot])
        self._commit(tok, r, w)
        self.n_inst += 1
        return tok

    def barrier(self):
        for e in ("pe", "act", "dve", "pool", "sp"):
            for e2 in ("pe", "act", "dve", "pool"):
                if self.cnt[e2] and not (e2 == e == "pe"):
                    self._wait(e, (e2, self.cnt[e2]))
            for q in self.dsem:
                for i in range(len(self.dsem[q])):
                    if self.dcnt[q][i]:
                        self._wait(e, (("d", q, i), self.dcnt[q][i]))
        self.lastw.clear()
        self.readers.clear()


class Mem:
    def __init__(self, big, nbytes):
        self.big, self.lo, self.hi, self.n = big, 0, nbytes, nbytes

    def _view(self, off, shape, dt):
        nel = int(np.prod(shape))
        esz = 4 if dt in (F32, I32) else 2
        nb = nel * esz
        ap = self.big[:, off // 2:(off + nb) // 2]
        if esz == 4:
            ap = ap.bitcast(dt)
        if len(shape) == 2:
            ap = ap.rearrange("p (a b) -> p a b", a=shape[0])
        elif len(shape) == 3:
            ap = ap.rearrange("p (a b c) -> p a b c", a=shape[0], b=shape[1])
        return ap

    def lo_alloc(self, shape, dt):
        nb = int(np.prod(shape)) * (4 if dt in (F32, I32) else 2)
        nb = (nb + 63) // 64 * 64
        off = self.lo
        self.lo += nb
        assert self.lo <= self.hi, f"SBUF overflow lo={self.lo} hi={self.hi}"
        return self._view(off, shape, dt)

    def hi_alloc(self, shape, dt):
        nb = int(np.prod(shape)) * (4 if dt in (F32, I32) else 2)
        nb = (nb + 63) // 64 * 64
        self.hi -= nb
        assert self.lo <= self.hi, f"SBUF overflow lo={self.lo} hi={self.hi}"
        return self._view(self.hi, shape, dt)


def build(upto=99, dbg=()):
    nc = bass.Bass("TRN2", target_bir_lowering=False)
    I = {}

    def din(name, shape, dt=F32):
        I[name] = nc.dram_tensor(name, list(shape), dt, kind="ExternalInput").ap()
        return I[name]

    x = din("x", [S_, D]); c_pk = din("c_pk", [128, 8]); pos_pk = din("pos_pk", [128, NT], I32)
    posC = din("posC", [127, 1], I32)
    ada_w = din("ada_w", [D, 6 * D]); adabB = din("adabB", [128, 6 * D]); g1B = din("g1B", [128, D]); g2B = din("g2B", [128, D])
    w_cq = din("w_cq", [D, 768]); w_ckv = din("w_ckv", [D, 256]); w_kpe = din("w_kpe", [D, 32])
    w_qn = din("w_qn", [D, 512]); w_kc2 = din("w_kc2", [D, 256]); w_vc2 = din("w_vc2", [D, 256])
    w_kv4 = din("w_kv4", [D, 512]); w_gn = din("w_gn", [D, 24]); w_gm = din("w_gm", [D, D]); w_gnm = din("w_gnm", [D, D])
    qag = din("qag", [128, 6]); kvag = din("kvag", [128, 2])
    w_qb = din("w_qb", [768, 768]); w_kvb = din("w_kvb", [256, 1024])
    qgB = din("qgB", [128, 96]); kgB = din("kgB", [128, 96])
    nqB = din("nqB", [128, 64]); nkcB = din("nkcB", [128, 64]); nksB = din("nksB", [128, 64]); nkwB = din("nkwB", [128, 64])
    posk = din("posk", [128, 16]); w1k = din("w1k", [2048, 256]); w2k = din("w2k", [256, 64])
    posv = din("posv", [128, 16]); w1v = din("w1v", [2048, 256]); w2v = din("w2v", [256, 64])
    wo_mla = din("wo_mla", [512, D]); wo_nsa = din("wo_nsa", [512, D]); w_out = din("w_out", [D, D])
    wg = din("wg", [D, DFF]); wu = din("wu", [D, DFF]); wd = din("wd", [DFF, D])
    ident_d = din("ident", [128, 128], BF16); tri_d = din("tri", [128, 128], BF16); winm_d = din("winm", [128, 384], BF16)
    inv16_d = din("inv16", [128, 16]); inv8_d = din("inv8", [128, 8])
    ovl_d = din("ovl", [127, 32], BF16); vmT_d = din("vmT", [127, S_], BF16); XE_d = din("XE", [32, NT * 128], BF16)
    fb_d = din("fb", [128, NT * 32]); vj_d = din("vj", [128, NT * 32])
    out = nc.dram_tensor("out", [S_, D], F32, kind="ExternalOutput").ap()
    hTs = nc.dram_tensor("hTs", [128, 8 * S_], BF16).ap()
    mods = nc.dram_tensor("mods", [128, 6 * D], F32).ap()
    x1s = nc.dram_tensor("x1s", [S_, D], F32).ap()
    D_ = {}
    for name, shape, dt in dbg:
        D_[name] = nc.dram_tensor("dbg_" + name, list(shape), dt, kind="ExternalOutput").ap()

    st = contextlib.ExitStack()
    with st:
        S = Sched(nc, st)
        NB = 204800
        big = st.enter_context(nc.sbuf_tensor("big", [128, NB // 2], BF16))
        M = Mem(big, NB)
        P = [st.enter_context(nc.psum_tensor(f"ps{i}", [128, 512], F32)) for i in range(8)]
        PK = [f"ps{i}" for i in range(8)]
        bank_state = [0]

        nbanks = [8]

        def nb():
            b = bank_state[0] % nbanks[0]
            bank_state[0] = (b + 1) % nbanks[0]
            return b

        def mm(ps_ap, lhsT, rhs, start, stop, r, w, **kw):
            S.op("pe", lambda e: e.matmul(ps_ap, lhsT=lhsT, rhs=rhs, start=start, stop=stop, **kw), r=r, w=w)

        def tp(ps_ap, in_, ident_ap, r, w):
            S.op("pe", lambda e: e.transpose(out=ps_ap, in_=in_, identity=ident_ap), r=r, w=w)

        def act(out_, in_, func, r, w, **kw):
            S.op("act", lambda e: e.activation(out=out_, in_=in_, func=func, **kw), r=r, w=w)

        def dbg_out(name, ap, r):
            if name in D_:
                S.dma("sp", D_[name], ap, r=r)

        ident = M.lo_alloc([128], BF16); tri = M.lo_alloc([128], BF16); winm = M.lo_alloc([3, 128], BF16)
        onesb = M.lo_alloc([128], BF16)
        stg = [M.lo_alloc([1024], F32) for _ in range(3)]
        stg_i = [0]
        cosM = M.lo_alloc([NT, 16], F32); sinM = M.lo_alloc([NT, 16], F32)
        cosN = M.lo_alloc([NT, 8], F32); sinN = M.lo_alloc([NT, 8], F32)
        cosC = M.lo_alloc([8], F32); sinC = M.lo_alloc([8], F32)
        lo_pers = M.lo
        omT = M.lo_alloc([4, S_], BF16); onT = M.lo_alloc([4, S_], BF16)
        S.dma("sp", ident, ident_d, w=["ident"])
        S.dma("sp", tri, tri_d, w=["tri"])
        S.dma("sp", winm.rearrange("p a b -> p (a b)"), winm_d, w=["winm"])
        S.op("pool", lambda e: e.memset(onesb, 1.0), w=["onesb"])

        def load_w(dst, W, key, KC, N, ceng="dve"):
            Wv = W.rearrange("(k p) n -> p k n", p=128)
            if N <= 1024:
                g = max(1, min(KC, 1024 // N))
                for k0 in range(0, KC, g):
                    k1 = min(KC, k0 + g)
                    si = stg_i[0]; stg_i[0] = (si + 1) % 3
                    sv = stg[si][:, 0:(k1 - k0) * N].rearrange("p (k n) -> p k n", n=N)
                    S.dma("sp", sv, Wv[:, k0:k1, :], w=[("stg", si)])
                    S.op(ceng, lambda e, sv=sv, k0=k0, k1=k1: e.tensor_copy(out=dst[:, k0:k1, :], in_=sv), r=[("stg", si)], w=[key])
            else:
                for k in range(KC):
                    for c0 in range(0, N, 1024):
                        c1 = min(N, c0 + 1024)
                        si = stg_i[0]; stg_i[0] = (si + 1) % 3
                        sv = stg[si][:, 0:c1 - c0]
                        S.dma("sp", sv, Wv[:, k, c0:c1], w=[("stg", si)])
                        S.op(ceng, lambda e, sv=sv, k=k, c0=c0, c1=c1: e.tensor_copy(out=dst[:, k, c0:c1], in_=sv), r=[("stg", si)], w=[key])

        def load_w_cols(dst, W, key, KC, c0, c1, ceng="dve"):
            Wv = W.rearrange("(k p) n -> p k n", p=128)
            N = c1 - c0
            g = max(1, min(KC, 1024 // N))
            for k0 in range(0, KC, g):
                k1 = min(KC, k0 + g)
                si = stg_i[0]; stg_i[0] = (si + 1) % 3
                sv = stg[si][:, 0:(k1 - k0) * N].rearrange("p (k n) -> p k n", n=N)
                S.dma("sp", sv, Wv[:, k0:k1, c0:c1], w=[("stg", si)])
                S.op(ceng, lambda e, sv=sv, k0=k0, k1=k1: e.tensor_copy(out=dst[:, k0:k1, :], in_=sv), r=[("stg", si)], w=[key])

        def sincos(ang, shape, cos_o, sin_o, np_, tmp_f, tmp_i, tmp_m, key):
            for (shift, dst) in ((0.0, sin_o), (PI / 2, cos_o)):
                S.op("dve", lambda e: e.tensor_scalar(out=tmp_f, in0=ang, scalar1=shift, scalar2=None, op0=ALU.add), r=[key + "ang"], w=[key + "f"])
                S.op("dve", lambda e: e.tensor_scalar(out=tmp_i, in0=tmp_f, scalar1=float(1 / TWO_PI), scalar2=None, op0=ALU.mult), r=[key + "f"], w=[key + "i"])
                S.op("dve", lambda e: e.tensor_copy(out=tmp_m, in_=tmp_i), r=[key + "i"], w=[key + "m"])
                S.op("dve", lambda e: e.scalar_tensor_tensor(out=tmp_f, in0=tmp_m, scalar=-TWO_PI, in1=tmp_f, op0=ALU.mult, op1=ALU.add), r=[key + "m", key + "f"], w=[key + "f"])
                S.op("dve", lambda e: e.tensor_scalar(out=tmp_m, in0=tmp_f, scalar1=PI, scalar2=None, op0=ALU.is_gt), r=[key + "f"], w=[key + "m"])
                S.op("dve", lambda e: e.scalar_tensor_tensor(out=tmp_f, in0=tmp_m, scalar=-TWO_PI, in1=tmp_f, op0=ALU.mult, op1=ALU.add), r=[key + "m", key + "f"], w=[key + "f"])
                S.op("dve", lambda e: e.tensor_scalar(out=tmp_m, in0=tmp_f, scalar1=-PI, scalar2=None, op0=ALU.is_lt), r=[key + "f"], w=[key + "m"])
                S.op("dve", lambda e: e.scalar_tensor_tensor(out=tmp_f, in0=tmp_m, scalar=TWO_PI, in1=tmp_f, op0=ALU.mult, op1=ALU.add), r=[key + "m", key + "f"], w=[key + "f"])
                act(dst, tmp_f, AF.Sin, r=[key + "f"], w=[key + "out"])

        lo0, hi0 = M.lo, M.hi
        if upto <= -2:
            return finish(nc, S, st)
        posi = M.lo_alloc([NT], I32); posf = M.lo_alloc([NT], F32)
        posCi = M.lo_alloc([1], I32); posCf = M.lo_alloc([1], F32)
        inv16 = M.lo_alloc([16], F32); inv8 = M.lo_alloc([8], F32)
        angM = M.lo_alloc([NT, 16], F32); tfM = M.lo_alloc([NT, 16], F32); tiM = M.lo_alloc([NT, 16], I32); tmM = M.lo_alloc([NT, 16], F32)
        S.dma("sp", posi, pos_pk, w=["posi"])
        S.dma("sp", posCi[0:127], posC, w=["posCi"])
        S.dma("sp", inv16, inv16_d, w=["inv16"])
        S.dma("sp", inv8, inv8_d, w=["inv8"])
        S.op("dve", lambda e: e.tensor_copy(out=posf, in_=posi), r=["posi"], w=["posf"])
        S.op("dve", lambda e: e.tensor_copy(out=posCf[0:127], in_=posCi[0:127]), r=["posCi"], w=["posCf"])
        S.op("dve", lambda e: e.tensor_tensor(out=angM, in0=posf.unsqueeze(2).to_broadcast([128, NT, 16]),
                                              in1=inv16.unsqueeze(1).to_broadcast([128, NT, 16]), op=ALU.mult), r=["posf", "inv16"], w=["Mang"])
        sincos(angM, None, cosM, sinM, 128, tfM, tiM, tmM, "M")
        a8 = angM.rearrange("p a b -> p (a b)")[:, 0:NT * 8].rearrange("p (a b) -> p a b", b=8)
        f8 = tfM.rearrange("p a b -> p (a b)")[:, 0:NT * 8].rearrange("p (a b) -> p a b", b=8)
        i8 = tiM.rearrange("p a b -> p (a b)")[:, 0:NT * 8].rearrange("p (a b) -> p a b", b=8)
        m8_ = tmM.rearrange("p a b -> p (a b)")[:, 0:NT * 8].rearrange("p (a b) -> p a b", b=8)
        S.op("dve", lambda e: e.tensor_tensor(out=a8, in0=posf.unsqueeze(2).to_broadcast([128, NT, 8]),
                                              in1=inv8.unsqueeze(1).to_broadcast([128, NT, 8]), op=ALU.mult), r=["posf", "inv8", "Mout", "Mf", "Mm", "Mi"], w=["Nang"])
        sincos(a8, None, cosN, sinN, 128, f8, i8, m8_, "N")
        aC = angM.rearrange("p a b -> p (a b)")[0:127, 0:8]
        fC = tfM.rearrange("p a b -> p (a b)")[0:127, 0:8]
        iC = tiM.rearrange("p a b -> p (a b)")[0:127, 0:8]
        mC = tmM.rearrange("p a b -> p (a b)")[0:127, 0:8]
        S.op("dve", lambda e: e.tensor_scalar(out=aC, in0=inv8[0:127], scalar1=posCf[0:127, 0:1], scalar2=None, op0=ALU.mult),
             r=["posCf", "inv8", "Nout", "Nf", "Nm", "Ni", "Nang"], w=["Cang"])
        sincos(aC, None, cosC[0:127], sinC[0:127], 127, fC, iC, mC, "C")
        dbg_out("cosM", cosM, ["Mout"]); dbg_out("sinM", sinM, ["Mout"])

        if upto <= -1:
            return finish(nc, S, st)
        cpk = M.lo_alloc([8], F32); sc = M.lo_alloc([8], F32)
        sch = M.lo_alloc([8], BF16); scl = M.lo_alloc([8], BF16)
        cBh = M.lo_alloc([8, 128], BF16); cBl = M.lo_alloc([8, 128], BF16)
        modB = M.lo_alloc([6 * D], F32)
        g1t = M.lo_alloc([D], F32); g2t = M.lo_alloc([D], F32)
        awb = [M.hi_alloc([8, 512], F32) for _ in range(2)]
        abb = [M.hi_alloc([512], F32) for _ in range(2)]
        awh = [M.hi_alloc([8, 512], BF16) for _ in range(2)]
        awl = [M.hi_alloc([8, 512], BF16) for _ in range(2)]
        S.dma("sp", cpk, c_pk, w=["cpk"])
        S.dma("sp", g1t, g1B, w=["g1t"]); S.dma("sp", g2t, g2B, w=["g2t"])
        act(sc, cpk, AF.Silu, r=["cpk"], w=["sc"])
        S.op("dve", lambda e: e.tensor_copy(out=sch, in_=sc), r=["sc"], w=["sch"])
        S.op("dve", lambda e: e.tensor_tensor(out=scl, in0=sc, in1=sch, op=ALU.subtract), r=["sc", "sch"], w=["scl"])
        for k in range(8):
            S.op("dve", lambda e, k=k: e.tensor_copy(out=cBh[:, k, :], in_=sch[:, k:k + 1].to_broadcast([128, 128])), r=["sch"], w=["cBh"])
            S.op("dve", lambda e, k=k: e.tensor_copy(out=cBl[:, k, :], in_=scl[:, k:k + 1].to_broadcast([128, 128])), r=["scl"], w=["cBl"])
        awv = ada_w.rearrange("(k p) n -> p k n", p=128)
        for n in range(12):
            q_ = n % 2
            S.dma("sp", awb[q_], awv[:, :, n * 512:(n + 1) * 512], w=[("awb", q_)])
            S.dma("sp", abb[q_], adabB[:, n * 512:(n + 1) * 512], w=[("abb", q_)])
            act(awh[q_], awb[q_], AF.Copy, r=[("awb", q_)], w=[("awh", q_)])
            S.op("dve", lambda e, q_=q_: e.tensor_tensor(out=awl[q_], in0=awb[q_], in1=awh[q_], op=ALU.subtract), r=[("awb", q_), ("awh", q_)], w=[("awl", q_)])
            b = nb()
            passes = [(cBh, "cBh", awh, "awh"), (cBh, "cBh", awl, "awl"), (cBl, "cBl", awh, "awh")]
            for pi_, (cb_, ck, ww, wk) in enumerate(passes):
                for k in range(8):
                    mm(P[b][:, :], cb_[:, k, :], ww[q_][:, k, :], pi_ == 0 and k == 0, pi_ == 2 and k == 7, r=[ck, (wk, q_)], w=[PK[b]])
            S.op("dve", lambda e, n=n, b=b, q_=q_: e.tensor_tensor(out=modB[:, n * 512:(n + 1) * 512], in0=P[b][:, :], in1=abb[q_], op=ALU.add),
                 r=[PK[b], ("abb", q_)], w=["modB"])
        S.op("dve", lambda e: e.scalar_tensor_tensor(out=modB[:, D:2 * D], in0=modB[:, D:2 * D], scalar=1.0, in1=g1t, op0=ALU.add, op1=ALU.mult), r=["modB", "g1t"], w=["modB"])
        S.op("dve", lambda e: e.scalar_tensor_tensor(out=modB[:, 4 * D:5 * D], in0=modB[:, 4 * D:5 * D], scalar=1.0, in1=g2t, op0=ALU.add, op1=ALU.mult), r=["modB", "g2t"], w=["modB"])
        S.dma("sp", mods, modB, r=["modB"], w=["mods"])
        dbg_out("modB", modB, ["modB"])
        B1 = modB[:, 0:D]; A1 = modB[:, D:2 * D]
        if upto <= 0:
            return finish(nc, S, st)

        M.hi = hi0
        hT = M.hi_alloc([8, S_], BF16)
        xb = [M.lo_alloc([D], F32) for _ in range(2)]
        junk = M.lo_alloc([D], F32); tmpA = M.lo_alloc([D], F32)
        hb = [M.lo_alloc([D], BF16) for _ in range(2)]
        ssq = M.lo_alloc([NT], F32); rs = M.lo_alloc([NT], F32)

        def norm_tile(i, xt, xkey, A, B, Akeys, dstT, dkey):
            act(junk, xt, AF.Square, r=[xkey], w=["junk", ("ssq", i)], accum_out=ssq[:, i:i + 1])
            act(rs[:, i:i + 1], ssq[:, i:i + 1], AF.Sqrt, r=[("ssq", i)], w=[("rs", i)], scale=1.0 / D, bias=EPS)
            S.op("dve", lambda e: e.reciprocal(out=rs[:, i:i + 1], in_=rs[:, i:i + 1]), r=[("rs", i)], w=[("rs", i)])
            S.op("dve", lambda e: e.scalar_tensor_tensor(out=tmpA, in0=xt, scalar=rs[:, i:i + 1], in1=A, op0=ALU.mult, op1=ALU.mult),
                 r=[xkey, ("rs", i)] + Akeys, w=["tmpA"])
            S.op("pool", lambda e: e.tensor_tensor(out=hb[i % 2], in0=tmpA, in1=B, op=ALU.add), r=["tmpA"] + Akeys, w=[("hb", i % 2)])
            b = nb()
            pb = P[b][:, :].bitcast(BF16)
            for k in range(8):
                tp(pb[:, k * 128:(k + 1) * 128], hb[i % 2][:, k * 128:(k + 1) * 128], ident, r=[("hb", i % 2), "ident"], w=[PK[b]])
            act(dstT[:, :, i * 128:(i + 1) * 128], pb.rearrange("p (k t) -> p k t", k=8), AF.Copy, r=[PK[b]], w=[(dkey, i)])

        for i in range(NT):
            S.dma("sp", xb[i % 2], x[i * 128:(i + 1) * 128, :], w=[("xb", i % 2)])
            norm_tile(i, xb[i % 2], ("xb", i % 2), A1, B1, ["modB"], hT, "hT")
        hTk = [("hT", i) for i in range(NT)]
        S.dma("sp", hTs, hT.rearrange("p k t -> p (k t)"), r=hTk, w=["hTs"])
        dbg_out("hT", hT.rearrange("p k t -> p (k t)"), hTk)
        if upto <= 1:
            return finish(nc, S, st)
        S.barrier()

        M.lo = lo0
        cqT = M.lo_alloc([6, S_], BF16); ckvT = M.lo_alloc([2, S_], BF16); kpe = M.lo_alloc([NT, 32], F32)
        lo1 = M.lo
        Wcq = M.lo_alloc([8, 768], BF16); Wckv = M.lo_alloc([8, 256], BF16); Wkpe = M.lo_alloc([8, 32], BF16)
        qagt = M.lo_alloc([6], F32); kvagt = M.lo_alloc([2], F32)
        sqb = [M.lo_alloc([512], BF16) for _ in range(2)]
        rb = M.lo_alloc([512], F32)
        S.dma("sp", qagt, qag, w=["qagt"]); S.dma("sp", kvagt, kvag, w=["kvagt"])
        load_w(Wcq, w_cq, "Wcq", 8, 768); load_w(Wckv, w_ckv, "Wckv", 8, 256); load_w(Wkpe, w_kpe, "Wkpe", 8, 32)

        def fm_proj_norm(dstT, dkey, Wt, wkey, nf, gaint, gkey, nfeat):
            for c in range(4):
                hk = [("hT", 4 * c + q) for q in range(4)]
                for j in range(nf):
                    b = nb()
                    for k in range(8):
                        mm(P[b][:, :], Wt[:, k, j * 128:(j + 1) * 128], hT[:, k, c * 512:(c + 1) * 512], k == 0, k == 7, r=[wkey] + hk, w=[PK[b]])
                    act(dstT[:, j, c * 512:(c + 1) * 512], P[b][:, :], AF.Copy, r=[PK[b]], w=[(dkey, c)])
                    act(sqb[j % 2], P[b][:, :], AF.Square, r=[PK[b]], w=[("sqb", j % 2)])
                    mm(P[6][:, :], onesb, sqb[j % 2], j == 0, j == nf - 1, r=["onesb", ("sqb", j % 2)], w=[PK[6]])
                act(rb, P[6][:, :], AF.Sqrt, r=[PK[6]], w=["rb"], scale=1.0 / nfeat, bias=EPS)
                S.op("dve", lambda e: e.reciprocal(out=rb, in_=rb), r=["rb"], w=["rb"])
                for j in range(nf):
                    S.op("dve", lambda e, j=j, c=c: e.scalar_tensor_tensor(out=dstT[:, j, c * 512:(c + 1) * 512], in0=dstT[:, j, c * 512:(c + 1) * 512],
                                                                             scalar=gaint[:, j:j + 1], in1=rb, op0=ALU.mult, op1=ALU.mult),
                         r=[(dkey, c), "rb", gkey], w=[(dkey, c)])

        fm_proj_norm(cqT, "cqT", Wcq, "Wcq", 6, qagt, "qagt", 768)
        fm_proj_norm(ckvT, "ckvT", Wckv, "Wckv", 2, kvagt, "kvagt", 256)
        for i in range(NT):
            b = nb()
            for k in range(8):
                mm(P[b][:, 0:32], hT[:, k, i * 128:(i + 1) * 128], Wkpe[:, k, :], k == 0, k == 7, r=["Wkpe", ("hT", i)], w=[PK[b]])
            S.op("dve", lambda e, i=i, b=b: e.tensor_copy(out=kpe[:, i, :], in_=P[b][:, 0:32]), r=[PK[b]], w=[("kpe", i)])
        cqk = [("cqT", c) for c in range(4)]
        dbg_out("cqT", cqT.rearrange("p k t -> p (k t)"), cqk)
        dbg_out("kpe", kpe.rearrange("p a b -> p (a b)"), [("kpe", i) for i in range(NT)])
        if upto <= 2:
            return finish(nc, S, st)
        S.barrier()

        M.lo = lo1
        M.hi = hi0
        QT = M.hi_alloc([8, S_], BF16); KT = M.hi_alloc([8, S_], BF16); V = M.hi_alloc([NT, 8, 65], BF16)
        hi1 = M.hi
        Wqb = M.lo_alloc([6, 768], BF16); Wkvb = M.lo_alloc([2, 1024], BF16)
        qgt = M.lo_alloc([96], F32); kgt = M.lo_alloc([96], F32)
        drq = [M.lo_alloc([768], BF16) for _ in range(2)]; drk = [M.lo_alloc([768], BF16) for _ in range(2)]
        S.dma("sp", qgt, qgB, w=["qgt"]); S.dma("sp", kgt, kgB, w=["kgt"])
        load_w(Wqb, w_qb, "Wqb", 6, 768); load_w(Wkvb, w_kvb, "Wkvb", 2, 1024)
        S.op("pool", lambda e: e.memset(V[:, :, :, 64:65], 1.0), w=["Vones"])

        def mk_tmps(Mx, n, H, hf):
            return dict(t1=Mx.lo_alloc([n], F32), t2=Mx.lo_alloc([n], F32), hs=Mx.lo_alloc([H], F32), hr=Mx.lo_alloc([H], F32),
                        ra=Mx.lo_alloc([H * hf], F32), rb=Mx.lo_alloc([H * hf], F32), ra2=Mx.lo_alloc([H * hf], F32), rb2=Mx.lo_alloc([H * hf], F32))

        def hnr_stages(tag, T, src, skeys, H, Dh, gaint, gkey, ro, hf, cos_, sin_, dst, dkey, np_=128):
            n = H * Dh
            t1v = T["t1"][0:np_, 0:n].rearrange("p (h d) -> p h d", h=H)
            t2v = T["t2"][0:np_, 0:n].rearrange("p (h d) -> p h d", h=H)
            hs = T["hs"][0:np_, 0:H]; hr = T["hr"][0:np_, 0:H]
            x1 = t1v[:, :, ro:ro + hf]; x2 = t1v[:, :, ro + hf:ro + 2 * hf]
            cb = cos_.unsqueeze(1).to_broadcast([np_, H, hf]); sb_ = sin_.unsqueeze(1).to_broadcast([np_, H, hf])
            rv = {k: T[k][0:np_, 0:H * hf].rearrange("p (h d) -> p h d", h=H) for k in ("ra", "rb", "ra2", "rb2")}
            tr = ["Mout", "Nout", "Cout"]
            k_ = lambda nm: (tag, nm)
            st = []
            st.append(lambda: act(t1v, src, AF.Square, r=skeys, w=[k_("t1")]))
            st.append(lambda: S.op("dve", lambda e: e.tensor_reduce(out=hs, in_=t1v, axis=AX.X, op=ALU.add), r=[k_("t1")], w=[k_("hs")]))
            st.append(lambda: act(hr, hs, AF.Sqrt, r=[k_("hs")], w=[k_("hr")], scale=1.0 / Dh, bias=EPS))
            st.append(lambda: S.op("dve", lambda e: e.reciprocal(out=hr, in_=hr), r=[k_("hr")], w=[k_("hr")]))
            st.append(lambda: S.op("dve", lambda e: e.tensor_tensor(out=t2v, in0=src, in1=hr.unsqueeze(2).to_broadcast([np_, H, Dh]), op=ALU.mult), r=skeys + [k_("hr")], w=[k_("t2")]))
            st.append(lambda: S.op("dve", lambda e: e.tensor_tensor(out=t1v, in0=t2v, in1=gaint[0:np_].unsqueeze(1).to_broadcast([np_, H, Dh]), op=ALU.mult), r=[k_("t2"), gkey], w=[k_("t1")]))
            st.append(lambda: S.op("dve", lambda e: e.tensor_tensor(out=rv["ra"], in0=x1, in1=cb, op=ALU.mult), r=[k_("t1")] + tr, w=[k_("ra")]))
            st.append(lambda: S.op("dve", lambda e: e.tensor_tensor(out=rv["rb"], in0=x2, in1=sb_, op=ALU.mult), r=[k_("t1")] + tr, w=[k_("rb")]))
            st.append(lambda: S.op("dve", lambda e: e.tensor_tensor(out=dst[:, :, ro:ro + hf], in0=rv["ra"], in1=rv["rb"], op=ALU.subtract), r=[k_("ra"), k_("rb")], w=[dkey]))
            st.append(lambda: S.op("dve", lambda e: e.tensor_tensor(out=rv["ra2"], in0=x2, in1=cb, op=ALU.mult), r=[k_("t1")] + tr, w=[k_("ra2")]))
            st.append(lambda: S.op("dve", lambda e: e.tensor_tensor(out=rv["rb2"], in0=x1, in1=sb_, op=ALU.mult), r=[k_("t1")] + tr, w=[k_("rb2")]))
            st.append(lambda: S.op("dve", lambda e: e.tensor_tensor(out=dst[:, :, ro + hf:ro + 2 * hf], in0=rv["ra2"], in1=rv["rb2"], op=ALU.add), r=[k_("ra2"), k_("rb2")], w=[dkey]))

            def copies():
                if ro > 0:
                    S.op("pool", lambda e: e.tensor_copy(out=dst[:, :, 0:ro], in_=t1v[:, :, 0:ro]), r=[k_("t1")], w=[dkey])
                if ro + 2 * hf < Dh:
                    S.op("pool", lambda e: e.tensor_copy(out=dst[:, :, ro + 2 * hf:Dh], in_=t1v[:, :, ro + 2 * hf:Dh]), r=[k_("t1")], w=[dkey])
            st.insert(6, copies)
            return st

        def run_interleaved(chains):
            for s_ in range(max(len(c_) for c_ in chains)):
                for c_ in chains:
                    if s_ < len(c_):
                        c_[s_]()

        def head_norm_rope(src, skeys, H, Dh, gaint, gkey, ro, hf, cos_, sin_, dst, dkey, np_=128):
            n = H * Dh
            t1v = t1[0:np_, 0:n].rearrange("p (h d) -> p h d", h=H)
            t2v = t2[0:np_, 0:n].rearrange("p (h d) -> p h d", h=H)
            hs = hss[0:np_, 0:H]; hr = hrs[0:np_, 0:H]
            act(t1v, src, AF.Square, r=skeys, w=["t1"])
            S.op("dve", lambda e: e.tensor_reduce(out=hs, in_=t1v, axis=AX.X, op=ALU.add), r=["t1"], w=["hss"])
            act(hr, hs, AF.Sqrt, r=["hss"], w=["hrs"], scale=1.0 / Dh, bias=EPS)
            S.op("dve", lambda e: e.reciprocal(out=hr, in_=hr), r=["hrs"], w=["hrs"])
            S.op("dve", lambda e: e.tensor_tensor(out=t2v, in0=src, in1=hr.unsqueeze(2).to_broadcast([np_, H, Dh]), op=ALU.mult), r=skeys + ["hrs"], w=["t2"])
            S.op("dve", lambda e: e.tensor_tensor(out=t1v, in0=t2v, in1=gaint[0:np_].unsqueeze(1).to_broadcast([np_, H, Dh]), op=ALU.mult), r=["t2", gkey], w=["t1"])
            x1 = t1v[:, :, ro:ro + hf]; x2 = t1v[:, :, ro + hf:ro + 2 * hf]
            cb = cos_.unsqueeze(1).to_broadcast([np_, H, hf]); sb_ = sin_.unsqueeze(1).to_broadcast([np_, H, hf])
            rav = ra[0:np_, 0:H * hf].rearrange("p (h d) -> p h d", h=H)
            rbv = rbb[0:np_, 0:H * hf].rearrange("p (h d) -> p h d", h=H)
            tr = ["Mout", "Nout", "Cout"]
            S.op("dve", lambda e: e.tensor_tensor(out=rav, in0=x1, in1=cb, op=ALU.mult), r=["t1"] + tr, w=["ra"])
            S.op("dve", lambda e: e.tensor_tensor(out=rbv, in0=x2, in1=sb_, op=ALU.mult), r=["t1"] + tr, w=["rbb"])
            S.op("dve", lambda e: e.tensor_tensor(out=dst[:, :, ro:ro + hf], in0=rav, in1=rbv, op=ALU.subtract), r=["ra", "rbb"], w=[dkey])
            S.op("dve", lambda e: e.tensor_tensor(out=rav, in0=x2, in1=cb, op=ALU.mult), r=["t1"] + tr, w=["ra"])
            S.op("dve", lambda e: e.tensor_tensor(out=rbv, in0=x1, in1=sb_, op=ALU.mult), r=["t1"] + tr, w=["rbb"])
            S.op("dve", lambda e: e.tensor_tensor(out=dst[:, :, ro + hf:ro + 2 * hf], in0=rav, in1=rbv, op=ALU.add), r=["ra", "rbb"], w=[dkey])
            if ro > 0:
                S.op("pool", lambda e: e.tensor_copy(out=dst[:, :, 0:ro], in_=t1v[:, :, 0:ro]), r=["t1"], w=[dkey])
            if ro + 2 * hf < Dh:
                S.op("pool", lambda e: e.tensor_copy(out=dst[:, :, ro + 2 * hf:Dh], in_=t1v[:, :, ro + 2 * hf:Dh]), r=["t1"], w=[dkey])

        Msub = Mem(big, NB); Msub.lo = lo_pers; Msub.hi = lo0
        rawq = [Msub.lo_alloc([768], F32) for _ in range(2)]; rawk = [Msub.lo_alloc([768], F32) for _ in range(2)]
        Tq = mk_tmps(Msub, 768, 8, 16); Tk = mk_tmps(Msub, 768, 8, 16)

        def b2_front(i):
            ts = slice(i * 128, (i + 1) * 128)
            par = i % 2
            bA, bB = nb(), nb()
            for k in range(6):
                mm(P[bA][:, :], cqT[:, k, ts], Wqb[:, k, 0:512], k == 0, k == 5, r=["Wqb", ("cqT", i // 4)], w=[PK[bA]])
            for k in range(6):
                mm(P[bB][:, 0:256], cqT[:, k, ts], Wqb[:, k, 512:768], k == 0, k == 5, r=["Wqb", ("cqT", i // 4)], w=[PK[bB]])
            act(rawq[par][:, 0:512], P[bA][:, :], AF.Copy, r=[PK[bA]], w=[("rawq", par)])
            act(rawq[par][:, 512:768], P[bB][:, 0:256], AF.Copy, r=[PK[bB]], w=[("rawq", par)])
            bA, bB = nb(), nb()
            for hh, bb in ((0, bA), (1, bB)):
                for k in range(2):
                    mm(P[bb][:, :], ckvT[:, k, ts], Wkvb[:, k, hh * 512:(hh + 1) * 512], k == 0, k == 1, r=["Wkvb", ("ckvT", i // 4)], w=[PK[bb]])
            rv = rawk[par].rearrange("p (h d) -> p h d", h=8)
            for hh, bb in ((0, bA), (1, bB)):
                pv = P[bb][:, :].rearrange("p (h d) -> p h d", h=4)
                act(rv[:, hh * 4:(hh + 1) * 4, 0:64], pv[:, :, 0:64], AF.Copy, r=[PK[bb]], w=[("rawk", par)])
                act(V[:, i, hh * 4:(hh + 1) * 4, 0:64], pv[:, :, 64:128], AF.Copy, r=[PK[bb]], w=[("V", i)])
            S.op("pool", lambda e, i=i: e.tensor_copy(out=rv[:, :, 64:96], in_=kpe[:, i, :].unsqueeze(1).to_broadcast([128, 8, 32])), r=[("kpe", i)], w=[("rawk", par)])

        def b2_back(i):
            ts = slice(i * 128, (i + 1) * 128)
            par = i % 2
            dq = drq[par].rearrange("p (h d) -> p h d", h=8); dk = drk[par].rearrange("p (h d) -> p h d", h=8)
            cq_ = hnr_stages("cq", Tq, rawq[par].rearrange("p (h d) -> p h d", h=8), [("rawq", par)], 8, 96, qgt, "qgt", 64, 16, cosM[:, i, :], sinM[:, i, :], dq, ("drq", par))
            ck_ = hnr_stages("ck", Tk, rawk[par].rearrange("p (h d) -> p h d", h=8), [("rawk", par)], 8, 96, kgt, "kgt", 64, 16, cosM[:, i, :], sinM[:, i, :], dk, ("drk", par))
            run_interleaved([cq_, ck_])
            for (dd, dkey_, dstT, okey) in ((dq, ("drq", par), QT, "QT"), (dk, ("drk", par), KT, "KT")):
                b = nb(); pb = P[b][:, :].bitcast(BF16)
                for h in range(8):
                    tp(pb[0:96, h * 128:(h + 1) * 128], dd[:, h, :], ident, r=[dkey_, "ident"], w=[PK[b]])
                act(dstT[0:96, :, ts], pb[0:96, :].rearrange("p (h t) -> p h t", h=8), AF.Copy, r=[PK[b]], w=[(okey, i)])

        b2_front(0)
        for i in range(NT):
            if i + 1 < NT:
                b2_front(i + 1)
            b2_back(i)
        QTk = [("QT", i) for i in range(NT)]
        dbg_out("QT", QT[0:96].rearrange("p k t -> p (k t)"), QTk)
        dbg_out("KT", KT[0:96].rearrange("p k t -> p (k t)"), [("KT", i) for i in range(NT)])
        dbg_out("V", V.rearrange("p a b c -> p (a b c)"), [("V", i) for i in range(NT)] + ["Vones"])
        if upto <= 3:
            return finish(nc, S, st)
        S.barrier()

        nbanks[0] = 6
        M.lo = lo0
        om = M.lo_alloc([NT, 512], BF16)
        PT = [M.lo_alloc([512], BF16) for _ in range(3)]
        rec4 = [M.lo_alloc([4], F32) for _ in range(2)]
        pt_i = [0]

        gchunk = [0]

        def causal_attn_multi(jobs):
            steps = []
            for ji in range(len(jobs)):
                for c in range(4):
                    for kt in range(4 * c + 4):
                        steps.append((ji, c, kt))

            def emit_qk(step):
                ji, c, kt = step
                J = jobs[ji]
                q0 = max(kt - 4 * c, 0)
                n = 512 - 128 * q0
                b = nb()
                has_extra = J["extra"] is not None
                mm(P[b][:, 0:n], J["KT"][0:J["kn"], kt * 128:(kt + 1) * 128], J["QT"](c * 512 + q0 * 128, (c + 1) * 512),
                   True, not has_extra, r=J["kk"](kt) + J["qk"](c), w=[PK[b]])
                if has_extra:
                    J["extra"](P[b][:, 0:n], kt, c * 512 + q0 * 128, (c + 1) * 512, PK[b])
                return b, n, q0

            pend = emit_qk(steps[0])
            for si, (ji, c, kt) in enumerate(steps):
                J = jobs[ji]
                b, n, q0 = pend
                if si + 1 < len(steps):
                    pend = emit_qk(steps[si + 1])
                if kt == 0:
                    gchunk[0] += 1
                ab = 6 + (gchunk[0] % 2)
                Oacc = P[ab][:, 0:260].rearrange("p (q d) -> p q d", q=4)
                pi = pt_i[0]; pt_i[0] = (pi + 1) % 3
                pt = PT[pi]
                act(pt[:, 0:n], P[b][:, 0:n], AF.Exp, r=[PK[b]], w=[("PT", pi)], scale=J["scale"])
                if kt >= 4 * c:
                    S.op("dve", lambda e, pt=pt: e.tensor_tensor(out=pt[:, 0:128], in0=pt[:, 0:128], in1=tri, op=ALU.mult), r=[("PT", pi), "tri"], w=[("PT", pi)])
                for qi in range(q0, 4):
                    mm(Oacc[:, qi, :], pt[:, (qi - q0) * 128:(qi - q0 + 1) * 128], J["V"](kt), kt == 0 and qi == 0, kt == 4 * c + qi,
                       r=[("PT", pi)] + J["vk"](kt), w=[PK[ab]], skip_group_check=True)
                if kt == 4 * c + 3:
                    J["fin"](c, Oacc, PK[ab])

        jobs = []
        for h in range(8):
            def fin(c, Oacc, pk, h=h):
                rc = rec4[c % 2]
                S.op("dve", lambda e: e.reciprocal(out=rc, in_=Oacc[:, :, 64]), r=[pk], w=[("rec4", c % 2)])
                S.op("dve", lambda e: e.tensor_tensor(out=om[:, 4 * c:4 * c + 4, h * 64:(h + 1) * 64], in0=Oacc[:, :, 0:64],
                                                      in1=rc.unsqueeze(2).to_broadcast([128, 4, 64]), op=ALU.mult),
                     r=[pk, ("rec4", c % 2)], w=[("om", c)])
            jobs.append(dict(KT=KT[:, h, :], kn=96, QT=(lambda a, b_, h=h: QT[0:96, h, a:b_]), V=(lambda kt, h=h: V[:, kt, h, :]), scale=96 ** -0.5,
                             extra=None, fin=fin, qk=(lambda c: [("QT", 4 * c + q) for q in range(4)]), kk=(lambda kt: [("KT", kt)]),
                             vk=(lambda kt: [("V", kt), "Vones"])))
        causal_attn_multi(jobs)
        for i in range(NT):
            b = nb(); pb = P[b][:, :].bitcast(BF16)
            for j in range(4):
                tp(pb[:, j * 128:(j + 1) * 128], om[:, i, j * 128:(j + 1) * 128], ident, r=[("om", i // 4), "ident"], w=[PK[b]])
            act(omT[:, :, i * 128:(i + 1) * 128], pb[:, 0:512].rearrange("p (k t) -> p k t", k=4), AF.Copy, r=[PK[b]], w=[("omT", i)])
        dbg_out("om", om.rearrange("p a b -> p (a b)"), [("om", c) for c in range(4)])
        if upto <= 4:
            return finish(nc, S, st)
        S.barrier()

        nbanks[0] = 4
        M.lo = lo0; M.hi = hi0
        qnT = M.lo_alloc([8, S_], BF16); ksT = M.lo_alloc([2, S_], BF16); kwT = M.lo_alloc([2, S_], BF16)
        vs = M.lo_alloc([NT, 2, 65], BF16); vw = M.lo_alloc([NT, 2, 65], BF16)
        gates = M.lo_alloc([NT, 3, 8], F32)
        kcmpT = M.lo_alloc([2, 128], BF16); VCX = M.lo_alloc([2, 97], BF16)
        PT = [M.lo_alloc([512], BF16) for _ in range(3)]
        rec4 = [M.lo_alloc([4], F32) for _ in range(2)]
        t1 = M.lo_alloc([512], F32); t2 = M.lo_alloc([512], F32)
        hss = M.lo_alloc([8], F32); hrs = M.lo_alloc([8], F32)
        ra = M.lo_alloc([128], F32); rbb = M.lo_alloc([128], F32)
        drb = [M.lo_alloc([512], BF16) for _ in range(2)]
        nqt = M.lo_alloc([64], F32); nkct = M.lo_alloc([64], F32); nkst = M.lo_alloc([64], F32); nkwt = M.lo_alloc([64], F32)
        lo2 = M.lo
        kc2 = M.hi_alloc([2, S_], BF16); vc2 = M.hi_alloc([2, S_], BF16)
        hi_kv = M.hi
        hT = M.hi_alloc([8, S_], BF16)
        Wqn = M.hi_alloc([8, 512], BF16); Wkc2 = M.hi_alloc([8, 256], BF16); Wvc2 = M.hi_alloc([8, 256], BF16)
        Wkv4 = M.hi_alloc([8, 512], BF16); Wgn = M.hi_alloc([8, 24], BF16)
        ge = M.hi_alloc([24], F32)
        S.dma("sp", hT.rearrange("p k t -> p (k t)"), hTs, r=["hTs"], w=["hTall"])
        for t_, d_ in ((nqt, nqB), (nkct, nkcB), (nkst, nksB), (nkwt, nkwB)):
            S.dma("sp", t_, d_, w=["ngain"])
        load_w(Wqn, w_qn, "Wqn", 8, 512); load_w(Wkv4, w_kv4, "Wkv4", 8, 512); load_w(Wgn, w_gn, "Wgn", 8, 24)
        load_w(Wkc2, w_kc2, "Wkc2", 8, 256); load_w(Wvc2, w_vc2, "Wvc2", 8, 256)
        S.op("pool", lambda e: e.memset(vs[:, :, :, 64:65], 1.0), w=["vsones"])
        S.op("pool", lambda e: e.memset(vw[:, :, :, 64:65], 1.0), w=["vwones"])
        S.op("pool", lambda e: e.memset(kc2[64:128, :, S_ - 1:S_], 0.0), w=["kc2pad"])
        S.op("pool", lambda e: e.memset(vc2[64:128, :, S_ - 1:S_], 0.0), w=["vc2pad"])
        MsubD = Mem(big, NB); MsubD.lo = lo_pers + 16384; MsubD.hi = lo0
        TDq = mk_tmps(MsubD, 512, 8, 8); TDs = mk_tmps(MsubD, 128, 2, 8); TDw = mk_tmps(MsubD, 128, 2, 8)
        dnq = [MsubD.lo_alloc([512], BF16) for _ in range(2)]
        dns = [MsubD.lo_alloc([128], BF16) for _ in range(2)]; dnw = [MsubD.lo_alloc([128], BF16) for _ in range(2)]

        def d_front(i):
            ts = slice(i * 128, (i + 1) * 128)
            bq = 4 + 2 * (i % 2)
            for k in range(8):
                mm(P[bq][:, :], hT[:, k, ts], Wqn[:, k, :], k == 0, k == 7, r=["hTall", "Wqn"], w=[PK[bq]])
            bk = 5 + 2 * (i % 2)
            for k in range(8):
                mm(P[bk][:, :], hT[:, k, ts], Wkv4[:, k, :], k == 0, k == 7, r=["hTall", "Wkv4"], w=[PK[bk]])
            bg = nb()
            for k in range(8):
                mm(P[bg][:, 0:24], hT[:, k, ts], Wgn[:, k, :], k == 0, k == 7, r=["hTall", "Wgn"], w=[PK[bg]])
            act(vs[:, i, :, 0:64], P[bk][:, 256:384].rearrange("p (g d) -> p g d", g=2), AF.Copy, r=[PK[bk]], w=[("vs", i)])
            act(vw[:, i, :, 0:64], P[bk][:, 384:512].rearrange("p (g d) -> p g d", g=2), AF.Copy, r=[PK[bk]], w=[("vw", i)])
            act(ge, P[bg][:, 0:24], AF.Exp, r=[PK[bg]], w=["ge"], scale=-1.0)
            S.op("dve", lambda e: e.tensor_scalar(out=ge, in0=ge, scalar1=1.0, scalar2=None, op0=ALU.add), r=["ge"], w=["ge"])
            S.op("dve", lambda e, i=i: e.reciprocal(out=gates[:, i].rearrange("p a b -> p (a b)"), in_=ge), r=["ge"], w=[("gates", i)])
            return bq, bk

        def d_back(i, bq, bk):
            ts = slice(i * 128, (i + 1) * 128)
            par = i % 2
            dq = dnq[par].rearrange("p (h d) -> p h d", h=8)
            ds_ = dns[par].rearrange("p (h d) -> p h d", h=2); dw_ = dnw[par].rearrange("p (h d) -> p h d", h=2)
            c1 = hnr_stages("dq", TDq, P[bq][:, :].rearrange("p (h d) -> p h d", h=8), [PK[bq]], 8, 64, nqt, "ngain", 0, 8, cosN[:, i, :], sinN[:, i, :], dq, ("dnq", par))
            c2 = hnr_stages("ds", TDs, P[bk][:, 0:128].rearrange("p (h d) -> p h d", h=2), [PK[bk]], 2, 64, nkst, "ngain", 0, 8, cosN[:, i, :], sinN[:, i, :], ds_, ("dns", par))
            c3 = hnr_stages("dw", TDw, P[bk][:, 128:256].rearrange("p (h d) -> p h d", h=2), [PK[bk]], 2, 64, nkwt, "ngain", 0, 8, cosN[:, i, :], sinN[:, i, :], dw_, ("dnw", par))
            run_interleaved([c1, c2, c3])
            b = nb(); pb = P[b][:, :].bitcast(BF16)
            for p_ in range(8):
                tp(pb[0:64, p_ * 128:(p_ + 1) * 128], dnq[par][:, p_ * 64:(p_ + 1) * 64], ident, r=[("dnq", par), "ident"], w=[PK[b]])
            act(qnT[0:64, :, ts], pb[0:64, :].rearrange("p (k t) -> p k t", k=8), AF.Copy, r=[PK[b]], w=[("qnT", i)])
            for (dd, dkey_, dstT, dk) in ((dns[par], ("dns", par), ksT, "ksT"), (dnw[par], ("dnw", par), kwT, "kwT")):
                b2 = nb(); pb = P[b2][:, :].bitcast(BF16)
                for g_ in range(2):
                    tp(pb[0:64, g_ * 128:(g_ + 1) * 128], dd[:, g_ * 64:(g_ + 1) * 64], ident, r=[dkey_, "ident"], w=[PK[b2]])
                act(dstT[0:64, :, ts], pb[0:64, 0:256].rearrange("p (g t) -> p g t", g=2), AF.Copy, r=[PK[b2]], w=[(dk, i)])

        fb_ = d_front(0)
        for i in range(NT):
            cur_ = fb_
            if i + 1 < NT:
                fb_ = d_front(i + 1)
            d_back(i, *cur_)
        for c in range(4):
            for (Wt, wk, dst, dk) in ((Wkc2, "Wkc2", kc2, "kc2"), (Wvc2, "Wvc2", vc2, "vc2")):
                for g in range(2):
                    b = nb()
                    for k in range(8):
                        mm(P[b][:, :], Wt[:, k, g * 128:(g + 1) * 128], hT[:, k, c * 512:(c + 1) * 512], k == 0, k == 7, r=["hTall", wk], w=[PK[b]])
                    act(dst[0:64, g, c * 512:(c + 1) * 512], P[b][0:64, :], AF.Copy, r=[PK[b]], w=[dk])
                    if c == 0:
                        act(dst[64:128, g, 0:511], P[b][64:128, 1:512], AF.Copy, r=[PK[b]], w=[dk])
                    else:
                        act(dst[64:128, g, c * 512 - 1:(c + 1) * 512 - 1], P[b][64:128, :], AF.Copy, r=[PK[b]], w=[dk])
        dbg_out("qnT", qnT[0:64].rearrange("p k t -> p (k t)"), [("qnT", i) for i in range(NT)])
        dbg_out("ksT", ksT[0:64].rearrange("p k t -> p (k t)"), [("ksT", i) for i in range(NT)])
        dbg_out("gates", gates.rearrange("p a b c -> p (a b c)"), [("gates", i) for i in range(NT)])
        dbg_out("kc2", kc2.rearrange("p k t -> p (k t)"), ["kc2", "kc2pad"])
        if upto <= 5:
            return finish(nc, S, st)
        S.barrier()

        nbanks[0] = 8
        M.hi = hi_kv
        hiE = M.hi
        W1k = M.lo_alloc([16, 256], BF16); W1v = M.lo_alloc([16, 256], BF16)
        W2k = M.lo_alloc([2, 64], BF16); W2v = M.lo_alloc([2, 64], BF16)
        pkf = M.lo_alloc([16], F32); pvf = M.lo_alloc([16], F32); pkb = M.lo_alloc([16], BF16); pvb = M.lo_alloc([16], BF16)
        biask = M.lo_alloc([2], F32); biasv = M.lo_alloc([2], F32)
        hid = [M.lo_alloc([128], BF16) for _ in range(2)]
        ovl = M.lo_alloc([32], BF16)
        load_w(W1k, w1k, "W1k", 16, 256); load_w(W1v, w1v, "W1v", 16, 256)
        load_w(W2k, w2k, "W2k", 2, 64); load_w(W2v, w2v, "W2v", 2, 64)
        S.dma("sp", pkf, posk, w=["pkf"]); S.dma("sp", pvf, posv, w=["pvf"]); S.dma("sp", ovl[0:127], ovl_d, w=["ovl"])
        S.op("pool", lambda e: e.memset(VCX[0:127, :, 64:65], 1.0), w=["VCXa"])
        for g in range(2):
            S.op("pool", lambda e, g=g: e.tensor_copy(out=VCX[0:127, g, 65:97], in_=ovl[0:127]), r=["ovl"], w=["VCXb"])
        rt = [M.lo_alloc([128], BF16) for _ in range(3)]
        rt_i = [0]
        for (W1, w1key, W2, w2key, src, skey, posf_, pkey, isk) in ((W1k, "W1k", W2k, "W2k", kc2, ["kc2", "kc2pad"], pkf, "pkf", True),
                                                                    (W1v, "W1v", W2v, "W2v", vc2, ["vc2", "vc2pad"], pvf, "pvf", False)):
            srcv = src.rearrange("p g (n s) -> p g n s", s=16)
            bo = nb()
            for g in range(2):
                bh = []
                for hc in range(2):
                    b = nb()
                    while b == bo or b in bh:
                        b = nb()
                    bh.append(b)
                for lc in range(16):
                    ri = rt_i[0]; rt_i[0] = (ri + 1) % 3
                    rtv = rt[ri][:, 0:127]
                    S.op("pool", lambda e, rtv=rtv, g=g, lc=lc, srcv=srcv, posf_=posf_: e.tensor_scalar(
                        out=rtv, in0=srcv[:, g, (2 * lc) // 16:(2 * lc) // 16 + 127, (2 * lc) % 16], scalar1=posf_[:, lc:lc + 1], scalar2=None, op0=ALU.add),
                        r=skey + [pkey], w=[("rt", ri)])
                    for hc in range(2):
                        mm(P[bh[hc]][:, 0:127], W1[:, lc, hc * 128:(hc + 1) * 128], rtv, lc == 0, lc == 15, r=[w1key, ("rt", ri)], w=[PK[bh[hc]]])
                for hc in range(2):
                    act(hid[hc][:, 0:127], P[bh[hc]][:, 0:127], AF.Silu, r=[PK[bh[hc]]], w=[("hid", hc)])
                for hc in range(2):
                    mm(P[bo][0:127, g * 64:(g + 1) * 64], hid[hc][:, 0:127], W2[:, hc, :], hc == 0, hc == 1, r=[("hid", hc), w2key], w=[PK[bo]])
            if isk:
                d = drb[1][0:127, 0:128].rearrange("p (h d) -> p h d", h=2)
                head_norm_rope(P[bo][0:127, 0:128].rearrange("p (h d) -> p h d", h=2), [PK[bo]], 2, 64, nkct, "ngain", 0, 8, cosC[0:127], sinC[0:127], d, "drb1", np_=127)
                b2 = nb(); pb = P[b2][:, :].bitcast(BF16)
                for g_ in range(2):
                    tp(pb[0:64, g_ * 128:g_ * 128 + 127], drb[1][0:127, g_ * 64:(g_ + 1) * 64], ident[0:127, 0:127], r=["drb1", "ident"], w=[PK[b2]])
                act(kcmpT[0:64, :, 0:127], pb[0:64, 0:256].rearrange("p (g t) -> p g t", g=2)[:, :, 0:127], AF.Copy, r=[PK[b2]], w=["kcmpT"])
            else:
                act(VCX[0:127, :, 0:64], P[bo][0:127, 0:128].rearrange("p (g d) -> p g d", g=2), AF.Copy, r=[PK[bo]], w=["VCXc"])
        dbg_out("kcmpT", kcmpT[0:64].rearrange("p g t -> p (g t)"), ["kcmpT"])
        dbg_out("VCX", VCX[0:127].rearrange("p a b -> p (a b)"), ["VCXa", "VCXb", "VCXc"])
        if upto <= 6:
            return finish(nc, S, st)
        S.barrier()

        M.lo = lo2; M.hi = hi0
        onsa = M.hi_alloc([NT, 512], F32)
        mbT = M.hi_alloc([2, S_], BF16)
        vmT = M.hi_alloc([S_], BF16); XE = M.hi_alloc([NT, 128], BF16)
        fb = M.hi_alloc([NT, 32], F32); vj = M.hi_alloc([NT, 32], F32)
        pc = [M.lo_alloc([4, 128], BF16) for _ in range(2)]
        rsum = M.lo_alloc([8], F32); rec8 = M.lo_alloc([8], F32); gr = M.lo_alloc([8], F32)
        tmp_i = M.lo_alloc([8, 32], F32); imp = M.lo_alloc([2, 32], F32); m8 = M.lo_alloc([2, 8], F32)
        sel = M.lo_alloc([2, 32], F32); mbf = M.lo_alloc([2, 32], BF16)
        tmpo = M.lo_alloc([8, 64], F32)
        pw = [M.lo_alloc([3, 128], BF16) for _ in range(3)]
        onb = [M.lo_alloc([512], BF16) for _ in range(2)]
        S.dma("sp", vmT[0:127], vmT_d, w=["vmT"]); S.dma("sp", XE[0:32].rearrange("p a b -> p (a b)"), XE_d, w=["XE"])
        S.dma("sp", fb.rearrange("p a b -> p (a b)"), fb_d, w=["fb"]); S.dma("sp", vj.rearrange("p a b -> p (a b)"), vj_d, w=["vj"])
        VCXk = ["VCXa", "VCXb", "VCXc"]
        for i in range(NT):
            ts = slice(i * 128, (i + 1) * 128)
            import os
            if int(os.environ.get('KDEV_F', '9')) <= 0:
                continue
            sb_ = [nb(), nb()]
            ob = [nb(), nb()]
            for p in range(8):
                j, g = p // 2, p % 2
                if os.environ.get('KDEV_G0'):
                    g = 0
                mm(P[sb_[p // 4]][0:127, (p % 4) * 128:(p % 4 + 1) * 128], kcmpT[0:64, g, 0:127], qnT[0:64, p, ts], True, True,
                   r=["kcmpT", ("qnT", i)], w=[PK[sb_[p // 4]]])
            for hf_ in range(2):
                pcv = pc[hf_]
                act(pcv[0:127], P[sb_[hf_]][0:127, :].rearrange("p (a b) -> p a b", a=4), AF.Exp, r=[PK[sb_[hf_]]], w=[("pc", hf_)], scale=0.125)
                S.op("dve", lambda e, pcv=pcv: e.tensor_tensor(out=pcv[0:127], in0=pcv[0:127], in1=vmT[0:127, ts].unsqueeze(1).to_broadcast([127, 4, 128]), op=ALU.mult),
                     r=[("pc", hf_), "vmT"], w=[("pc", hf_)])
            import os
            FL = int(os.environ.get('KDEV_F', '9'))
            if FL <= 1:
                continue
            for p in range(8):
                g = p % 2
                mm(P[ob[p // 4]][:, (p % 4) * 97:(p % 4 + 1) * 97], pc[p // 4][0:127, p % 4, :], VCX[0:127, g, :], True, True,
                   r=[("pc", p // 4)] + VCXk, w=[PK[ob[p // 4]]])
            if FL <= 2:
                continue
            OC = [P[ob[h_]][:, 0:388].rearrange("p (a b) -> p a b", a=4) for h_ in range(2)]
            for h_ in range(2):
                S.op("dve", lambda e, h_=h_: e.tensor_scalar(out=rsum[:, h_ * 4:(h_ + 1) * 4], in0=OC[h_][:, :, 64], scalar1=1e-30, scalar2=None, op0=ALU.max), r=[PK[ob[h_]]], w=["rsum"])
            S.op("dve", lambda e: e.reciprocal(out=rec8, in_=rsum), r=["rsum"], w=["rec8"])
            S.op("dve", lambda e, i=i: e.tensor_tensor(out=gr, in0=gates[:, i, 0, :], in1=rec8, op=ALU.mult), r=["rec8", ("gates", i)], w=["gr"])
            for h_ in range(2):
                S.op("dve", lambda e, h_=h_, i=i: e.tensor_tensor(out=onsa[:, i, h_ * 256:(h_ + 1) * 256].rearrange("p (a b) -> p a b", a=4), in0=OC[h_][:, :, 0:64],
                                                                   in1=gr[:, h_ * 4:(h_ + 1) * 4].unsqueeze(2).to_broadcast([128, 4, 64]), op=ALU.mult),
                     r=[PK[ob[h_]], "gr"], w=[("onsa", i)])
                S.op("dve", lambda e, h_=h_: e.tensor_tensor(out=tmp_i[:, h_ * 4:(h_ + 1) * 4, :], in0=OC[h_][:, :, 65:97],
                                                             in1=rec8[:, h_ * 4:(h_ + 1) * 4].unsqueeze(2).to_broadcast([128, 4, 32]), op=ALU.mult),
                     r=[PK[ob[h_]], "rec8"], w=["tmp_i"])
            if FL <= 3:
                continue
            S.op("dve", lambda e: e.tensor_reduce(out=imp, in_=tmp_i.rearrange("t (j g) n -> t g n j", g=2), axis=AX.X, op=ALU.add), r=["tmp_i"], w=["imp"])
            S.op("dve", lambda e, i=i: e.tensor_tensor(out=imp, in0=imp, in1=fb[:, i, :].unsqueeze(1).to_broadcast([128, 2, 32]), op=ALU.add), r=["imp", "fb"], w=["imp"])
            if FL <= 4:
                continue
            for g in range(2):
                S.op("dve", lambda e, g=g: e.max(out=m8[:, g, :], in_=imp[:, g, :]), r=["imp"], w=["m8"])
                S.op("dve", lambda e, g=g: e.tensor_scalar(out=sel[:, g, :], in0=imp[:, g, :], scalar1=m8[:, g, 7:8], scalar2=None, op0=ALU.is_ge), r=["imp", "m8"], w=["sel"])
            S.op("dve", lambda e, i=i: e.tensor_tensor(out=sel, in0=sel, in1=vj[:, i, :].unsqueeze(1).to_broadcast([128, 2, 32]), op=ALU.mult), r=["sel", "vj"], w=["sel"])
            S.op("dve", lambda e: e.tensor_scalar(out=mbf, in0=sel, scalar1=-1.0, scalar2=30000.0, op0=ALU.add, op1=ALU.mult), r=["sel"], w=["mbf"])
            if i == 5:
                dbg_out("sel5", sel.rearrange("p a b -> p (a b)"), ["sel"])
                dbg_out("imp5", imp.rearrange("p a b -> p (a b)"), ["imp"])
            if FL <= 5:
                continue
            b = nb(); pb = P[b][:, :].bitcast(BF16)
            for g in range(2):
                tp(pb[0:32, g * 128:(g + 1) * 128], mbf[:, g, :], ident, r=["mbf", "ident"], w=[PK[b]])
            act(mbT[0:32, :, ts], pb[0:32, 0:256].rearrange("p (g t) -> p g t", g=2), AF.Copy, r=[PK[b]], w=[("mbT", i)])
        dbg_out("onsa_c", onsa.rearrange("p a b -> p (a b)"), [("onsa", i) for i in range(NT)])
        dbg_out("mbT", mbT[0:32].rearrange("p a b -> p (a b)"), [("mbT", i) for i in range(NT)])
        if upto <= 7:
            return finish(nc, S, st)

        S.barrier()
        nbanks[0] = 6
        jobs = []
        for p in range(8):
            j, g = p // 2, p % 2

            def extra(ps_ap, kt, a, b_, pk, g=g):
                mm(ps_ap, XE[0:32, kt, :], mbT[0:32, g, a:b_], False, True, r=["XE"] + [("mbT", q) for q in range(a // 128, b_ // 128)], w=[pk])

            def fin(c, Oacc, pk, p=p):
                rc = rec4[c % 2]
                S.op("dve", lambda e: e.reciprocal(out=rc, in_=Oacc[:, :, 64]), r=[pk], w=[("rec4", c % 2)])
                S.op("dve", lambda e: e.tensor_tensor(out=rc, in0=rc, in1=gates[:, 4 * c:4 * c + 4, 1, p], op=ALU.mult), r=[("rec4", c % 2)] + [("gates", 4 * c + q) for q in range(4)], w=[("rec4", c % 2)])
                tv = tmpo[:, 0:4, :]
                S.op("dve", lambda e: e.tensor_tensor(out=tv, in0=Oacc[:, :, 0:64], in1=rc.unsqueeze(2).to_broadcast([128, 4, 64]), op=ALU.mult), r=[pk, ("rec4", c % 2)], w=["tmpo"])
                S.op("pool", lambda e: e.tensor_tensor(out=onsa[:, 4 * c:4 * c + 4, p * 64:(p + 1) * 64], in0=onsa[:, 4 * c:4 * c + 4, p * 64:(p + 1) * 64], in1=tv, op=ALU.add),
                     r=["tmpo"] + [("onsa", 4 * c + q) for q in range(4)], w=[("onsa", 4 * c + q) for q in range(4)])
            jobs.append(dict(KT=ksT[:, g, :], kn=64, QT=(lambda a, b_, p=p: qnT[0:64, p, a:b_]), V=(lambda kt, g=g: vs[:, kt, g, :]), scale=0.125,
                             extra=extra, fin=fin, qk=(lambda c: [("qnT", 4 * c + q) for q in range(4)]), kk=(lambda kt: [("ksT", kt)]),
                             vk=(lambda kt: [("vs", kt), "vsones"])))
        causal_attn_multi(jobs)
        dbg_out("onsa_cs", onsa.rearrange("p a b -> p (a b)"), [("onsa", i) for i in range(NT)])
        if upto <= 8:
            return finish(nc, S, st)

        pw_i = [0]
        ob = [6, 7]

        def emit_ws(i, p):
            g = p % 2
            kts = [kt for kt in (i - 2, i - 1, i) if kt >= 0]
            b = nb()
            for kt in kts:
                sl = kt - (i - 2)
                mm(P[b][:, sl * 128:(sl + 1) * 128], kwT[0:64, g, kt * 128:(kt + 1) * 128], qnT[0:64, p, i * 128:(i + 1) * 128], True, True,
                   r=[("kwT", kt), ("qnT", i)], w=[PK[b]])
            return b

        wsteps = [(i, p) for i in range(NT) for p in range(8)]
        wpend = emit_ws(*wsteps[0])
        for wi_, (i, p) in enumerate(wsteps):
            ts = slice(i * 128, (i + 1) * 128)
            g = p % 2
            kts = [kt for kt in (i - 2, i - 1, i) if kt >= 0]
            b = wpend
            if wi_ + 1 < len(wsteps):
                wpend = emit_ws(*wsteps[wi_ + 1])
            s0 = kts[0] - (i - 2)
            wi = pw_i[0]; pw_i[0] = (wi + 1) % 3
            pwv = pw[wi]
            act(pwv[:, s0:3, :], P[b][:, s0 * 128:384].rearrange("p (a b) -> p a b", b=128), AF.Exp, r=[PK[b]], w=[("pw", wi)], scale=0.125)
            S.op("dve", lambda e, pwv=pwv, s0=s0: e.tensor_tensor(out=pwv[:, s0:3, :], in0=pwv[:, s0:3, :], in1=winm[:, s0:3, :], op=ALU.mult), r=[("pw", wi), "winm"], w=[("pw", wi)])
            for kt in kts:
                sl = kt - (i - 2)
                mm(P[ob[p // 4]][:, (p % 4) * 65:(p % 4 + 1) * 65], pwv[:, sl, :], vw[:, kt, g, :], kt == kts[0], kt == kts[-1],
                   r=[("pw", wi), ("vw", kt), "vwones"], w=[PK[ob[p // 4]]])
            if p != 7:
                continue
            OW = [P[ob[h_]][:, 0:260].rearrange("p (a b) -> p a b", a=4) for h_ in range(2)]
            for h_ in range(2):
                S.op("dve", lambda e, h_=h_: e.reciprocal(out=rec8[:, h_ * 4:(h_ + 1) * 4], in_=OW[h_][:, :, 64]), r=[PK[ob[h_]]], w=["rec8"])
            S.op("dve", lambda e, i=i: e.tensor_tensor(out=gr, in0=gates[:, i, 2, :], in1=rec8, op=ALU.mult), r=["rec8", ("gates", i)], w=["gr"])
            for h_ in range(2):
                S.op("dve", lambda e, h_=h_: e.tensor_tensor(out=tmpo[:, h_ * 4:(h_ + 1) * 4, :], in0=OW[h_][:, :, 0:64],
                                                             in1=gr[:, h_ * 4:(h_ + 1) * 4].unsqueeze(2).to_broadcast([128, 4, 64]), op=ALU.mult), r=[PK[ob[h_]], "gr"], w=["tmpo"])
            S.op("pool", lambda e, i=i: e.tensor_tensor(out=onb[i % 2], in0=onsa[:, i, :], in1=tmpo.rearrange("p a b -> p (a b)"), op=ALU.add), r=["tmpo", ("onsa", i)], w=[("onb", i % 2)])
            if "onsa_all" in D_:
                S.dma("sp", D_["onsa_all"][:, i * 512:(i + 1) * 512], onb[i % 2], r=[("onb", i % 2)])
            b = nb(); pb = P[b][:, :].bitcast(BF16)
            for j in range(4):
                tp(pb[:, j * 128:(j + 1) * 128], onb[i % 2][:, j * 128:(j + 1) * 128], ident, r=[("onb", i % 2), "ident"], w=[PK[b]])
            act(onT[:, :, ts], pb[:, 0:512].rearrange("p (k t) -> p k t", k=4), AF.Copy, r=[PK[b]], w=[("onT", i)])
        if upto <= 9:
            return finish(nc, S, st)
        S.barrier()

        nbanks[0] = 8
        M.lo = lo0; M.hi = hi0
        hT = M.hi_alloc([8, S_], BF16)
        mergedT = M.hi_alloc([8, S_], BF16)
        hi2 = M.hi
        Wgm = M.lo_alloc([8, 512], BF16); Wgnm = M.lo_alloc([8, 512], BF16); Wom = M.lo_alloc([4, 512], BF16); Won = M.lo_alloc([4, 512], BF16)
        e3 = M.lo_alloc([512], F32); e4 = M.lo_alloc([512], F32); tA = M.lo_alloc([512], F32); tB = M.lo_alloc([512], F32)
        mgb = [M.lo_alloc([512], BF16) for _ in range(2)]
        S.dma("sp", hT.rearrange("p k t -> p (k t)"), hTs, r=["hTs"], w=["hTall"])
        for cc in range(2):
            load_w_cols(Wgm, w_gm, "Wgm", 8, cc * 512, (cc + 1) * 512); load_w_cols(Wgnm, w_gnm, "Wgnm", 8, cc * 512, (cc + 1) * 512)
            load_w_cols(Wom, wo_mla, "Wom", 4, cc * 512, (cc + 1) * 512); load_w_cols(Won, wo_nsa, "Won", 4, cc * 512, (cc + 1) * 512)
            for i in range(NT):
                ts = slice(i * 128, (i + 1) * 128)
                b1, b2, b3, b4 = nb(), nb(), nb(), nb()
                for k in range(4):
                    mm(P[b1][:, :], omT[:, k, ts], Wom[:, k, :], k == 0, k == 3, r=[("omT", i), "Wom"], w=[PK[b1]])
                for k in range(4):
                    mm(P[b2][:, :], onT[:, k, ts], Won[:, k, :], k == 0, k == 3, r=[("onT", i), "Won"], w=[PK[b2]])
                for k in range(8):
                    mm(P[b3][:, :], hT[:, k, ts], Wgm[:, k, :], k == 0, k == 7, r=["hTall", "Wgm"], w=[PK[b3]])
                for k in range(8):
                    mm(P[b4][:, :], hT[:, k, ts], Wgnm[:, k, :], k == 0, k == 7, r=["hTall", "Wgnm"], w=[PK[b4]])
                act(e3, P[b3][:, :], AF.Sigmoid, r=[PK[b3]], w=["e3"])
                act(e4, P[b4][:, :], AF.Sigmoid, r=[PK[b4]], w=["e4"])
                S.op("dve", lambda e, b1=b1: e.tensor_tensor(out=tA, in0=P[b1][:, :], in1=e3, op=ALU.mult), r=[PK[b1], "e3"], w=["tA"])
                S.op("dve", lambda e, b2=b2: e.tensor_tensor(out=tB, in0=P[b2][:, :], in1=e4, op=ALU.mult), r=[PK[b2], "e4"], w=["tB"])
                S.op("pool", lambda e, i=i: e.tensor_tensor(out=mgb[i % 2], in0=tA, in1=tB, op=ALU.add), r=["tA", "tB"], w=[("mgb", i % 2)])
                if "merged" in D_:
                    S.dma("sp", D_["merged"][i * 128:(i + 1) * 128, cc * 512:(cc + 1) * 512], mgb[i % 2], r=[("mgb", i % 2)])
                b = nb(); pb = P[b][:, :].bitcast(BF16)
                for j in range(4):
                    tp(pb[:, j * 128:(j + 1) * 128], mgb[i % 2][:, j * 128:(j + 1) * 128], ident, r=[("mgb", i % 2), "ident"], w=[PK[b]])
                act(mergedT[:, cc * 4:(cc + 1) * 4, ts], pb[:, 0:512].rearrange("p (k t) -> p k t", k=4), AF.Copy, r=[PK[b]], w=[("mergedT", i)])
        if upto <= 10:
            return finish(nc, S, st)
        S.barrier()

        M.lo = lo0
        h2T = hT
        Wout = M.lo_alloc([8, D], BF16)
        G1 = M.lo_alloc([D], F32); A2 = M.lo_alloc([D], F32); B2 = M.lo_alloc([D], F32)
        xb = [M.lo_alloc([D], F32) for _ in range(2)]
        x1t = [M.lo_alloc([D], F32) for _ in range(2)]
        junk = M.lo_alloc([D], F32); tmpA = M.lo_alloc([D], F32)
        hb = [M.lo_alloc([D], BF16) for _ in range(2)]
        ssq = M.lo_alloc([NT], F32); rs = M.lo_alloc([NT], F32)
        S.dma("sp", G1, mods[:, 2 * D:3 * D], r=["mods"], w=["G1"])
        S.dma("sp", B2, mods[:, 3 * D:4 * D], r=["mods"], w=["AB2"])
        S.dma("sp", A2, mods[:, 4 * D:5 * D], r=["mods"], w=["AB2"])
        load_w(Wout, w_out, "Wout", 8, D, ceng="pool")
        for i in range(NT):
            ts = slice(i * 128, (i + 1) * 128)
            S.dma("sp", xb[i % 2], x[ts, :], w=[("xb", i % 2)])
            for cc in range(2):
                b = nb()
                for k in range(8):
                    mm(P[b][:, :], mergedT[:, k, ts], Wout[:, k, cc * 512:(cc + 1) * 512], k == 0, k == 7, r=[("mergedT", i), "Wout"], w=[PK[b]])
                S.op("dve", lambda e, b=b, cc=cc: e.tensor_tensor(out=tmpA[:, cc * 512:(cc + 1) * 512], in0=P[b][:, :], in1=G1[:, cc * 512:(cc + 1) * 512], op=ALU.mult), r=[PK[b], "G1"], w=["tmpA"])
                S.op("pool", lambda e, i=i, cc=cc: e.tensor_tensor(out=x1t[i % 2][:, cc * 512:(cc + 1) * 512], in0=tmpA[:, cc * 512:(cc + 1) * 512], in1=xb[i % 2][:, cc * 512:(cc + 1) * 512], op=ALU.add),
                     r=["tmpA", ("xb", i % 2)], w=[("x1t", i % 2)])
            S.dma("sp", x1s[ts, :], x1t[i % 2], r=[("x1t", i % 2)], w=[("x1s", i)])
            norm_tile(i, x1t[i % 2], ("x1t", i % 2), A2, B2, ["AB2"], h2T, "h2T")
        if "x1" in D_:
            S.dma("sp", D_["x1"], x1s, r=[("x1s", i) for i in range(NT)])
        if upto <= 11:
            return finish(nc, S, st)
        S.barrier()

        M.lo = lo_pers; M.hi = hi2 + 8 * S_ * 2
        Wd = M.hi_alloc([NFC, D], BF16)
        actT = M.hi_alloc([NFC, 1024], BF16)
        G2 = M.lo_alloc([D], F32)
        Wg2 = [M.lo_alloc([8, 256], BF16) for _ in range(2)]; Wu2 = [M.lo_alloc([8, 256], BF16) for _ in range(2)]
        sg = [M.lo_alloc([512], F32) for _ in range(2)]
        xb = [M.lo_alloc([D], F32) for _ in range(2)]
        ot = [M.lo_alloc([D], F32) for _ in range(2)]
        tmpA = M.lo_alloc([D], F32)
        S.dma("sp", G2, mods[:, 5 * D:6 * D], r=["mods"], w=["G2"])
        load_w(Wd, wd, "Wd", NFC, D)
        h2k = [("h2T", i) for i in range(NT)]
        out_toks = []
        for half in range(2):
            def ld(jg_):
                wb_ = jg_ % 2
                load_w_cols(Wg2[wb_], wg, ("Wg2", wb_), 8, jg_ * 256, (jg_ + 1) * 256)
                load_w_cols(Wu2[wb_], wu, ("Wu2", wb_), 8, jg_ * 256, (jg_ + 1) * 256)
            ld(0)
            for jg in range(NFC // 2):
                wb = jg % 2
                if jg + 1 < NFC // 2:
                    ld(jg + 1)
                for jj in range(2):
                    j = jg * 2 + jj
                    for tc in range(2):
                        t0 = half * 1024 + tc * 512
                        bg, bu = nb(), nb()
                        for k in range(8):
                            mm(P[bg][:, :], Wg2[wb][:, k, jj * 128:(jj + 1) * 128], h2T[:, k, t0:t0 + 512], k == 0, k == 7, r=[("Wg2", wb)] + h2k, w=[PK[bg]])
                        for k in range(8):
                            mm(P[bu][:, :], Wu2[wb][:, k, jj * 128:(jj + 1) * 128], h2T[:, k, t0:t0 + 512], k == 0, k == 7, r=[("Wu2", wb)] + h2k, w=[PK[bu]])
                        act(sg[tc], P[bg][:, :], AF.Silu, r=[PK[bg]], w=[("sg", tc)])
                        S.op("dve", lambda e, j=j, tc=tc, bu=bu: e.tensor_tensor(out=actT[:, j, tc * 512:(tc + 1) * 512], in0=P[bu][:, :], in1=sg[tc], op=ALU.mult),
                             r=[PK[bu], ("sg", tc)], w=[("actT", tc)])
            for il in range(8):
                i = half * 8 + il
                ts = slice(i * 128, (i + 1) * 128)
                S.dma("sp", xb[i % 2], x1s[ts, :], r=[("x1s", i)], w=[("xb", i % 2)])
                for cc in range(2):
                    b = nb()
                    for j in range(NFC):
                        mm(P[b][:, :], actT[:, j, il * 128:(il + 1) * 128], Wd[:, j, cc * 512:(cc + 1) * 512], j == 0, j == NFC - 1, r=[("actT", il // 4), "Wd"], w=[PK[b]])
                    S.op("dve", lambda e, b=b, cc=cc: e.tensor_tensor(out=tmpA[:, cc * 512:(cc + 1) * 512], in0=P[b][:, :], in1=G2[:, cc * 512:(cc + 1) * 512], op=ALU.mult), r=[PK[b], "G2"], w=["tmpA"])
                    S.op("pool", lambda e, i=i, cc=cc: e.tensor_tensor(out=ot[i % 2][:, cc * 512:(cc + 1) * 512], in0=tmpA[:, cc * 512:(cc + 1) * 512], in1=xb[i % 2][:, cc * 512:(cc + 1) * 512], op=ALU.add),
                         r=["tmpA", ("xb", i % 2)], w=[("ot", i % 2)])
                S.dma("sp", out[ts, :], ot[i % 2], r=[("ot", i % 2)], w=[("out", i)])
        return finish(nc, S, st)


def finish(nc, S, st):
    for q in S.dsem:
        for i in range(len(S.dsem[q])):
            if S.dcnt[q][i]:
                S._wait("sp", (("d", q, i), S.dcnt[q][i]))
    for e2 in ("pe", "act", "dve", "pool"):
        if S.cnt[e2]:
            S._wait("sp", (e2, S.cnt[e2]))
    st.close()
    return nc


def _consts():
    bf = ml_dtypes.bfloat16
    c = {}
    c["ident"] = np.eye(128, dtype=np.float32).astype(bf)
    a = np.arange(128)
    c["tri"] = (a[:, None] <= a[None, :]).astype(np.float32).astype(bf)
    w = np.zeros((128, 3, 128), np.float32)
    w[:, 0, :] = (a[:, None] > a[None, :])
    w[:, 1, :] = 1.0
    w[:, 2, :] = (a[:, None] <= a[None, :])
    c["winm"] = w.reshape(128, 384).astype(bf)
    inv16 = (np.float32(500000.0) ** (-np.arange(0, 32, 2, dtype=np.float32) / np.float32(32))).astype(np.float32)
    inv8 = (np.float32(500000.0) ** (-np.arange(0, 16, 2, dtype=np.float32) / np.float32(16))).astype(np.float32)
    c["inv16"] = np.tile(inv16[None], (128, 1)).astype(np.float32)
    c["inv8"] = np.tile(inv8[None], (128, 1)).astype(np.float32)
    n = np.arange(127)
    starts = n * 16
    j = np.arange(32)
    ovl = ((starts[:, None] < j[None, :] * 64 + 64) & (starts[:, None] + 32 > j[None, :] * 64))
    c["ovl"] = ovl.astype(np.float32).astype(bf)
    t = np.arange(S_)
    c["vmT"] = ((starts[:, None] + 31) <= t[None, :]).astype(np.float32).astype(bf)
    XE = np.zeros((32, NT, 128), np.float32)
    for kt in range(NT):
        XE[2 * kt, kt, 0:64] = 1.0
        XE[2 * kt + 1, kt, 64:128] = 1.0
    c["XE"] = XE.reshape(32, NT * 128).astype(bf)
    cur = (t // 64)
    forced = (j[None, :] == 0) | (j[None, :] == cur[:, None]) | (j[None, :] == cur[:, None] - 1)
    valid = j[None, :] <= cur[:, None]
    fb = np.where(valid, np.where(forced, 1e4, 0.0), -1e30).astype(np.float32)
    c["fb"] = fb.reshape(NT, 128, 32).transpose(1, 0, 2).reshape(128, NT * 32).copy()
    c["vj"] = valid.astype(np.float32).reshape(NT, 128, 32).transpose(1, 0, 2).reshape(128, NT * 32).copy()
    return c


def _rep(v, n=128):
    return np.ascontiguousarray(np.broadcast_to(np.asarray(v, np.float32)[None, :], (n, v.shape[0])))


def prep_inputs(inp):
    f = lambda a: np.ascontiguousarray(np.asarray(a, dtype=np.float32))
    w_in = f(inp["w_in"][0])
    o = np.cumsum([0, 768, 256, 32, 512, 128, 128, 128, 128, 128, 128, 24, 1024, 1024])
    seg = lambda i: w_in[:, o[i]:o[i + 1]]
    shared = {}
    shared["ada_w"] = f(inp["ada_w"][0]); shared["adabB"] = _rep(f(inp["ada_b"][0]))
    shared["g1B"] = _rep(f(inp["norm1_gain"][0])); shared["g2B"] = _rep(f(inp["norm2_gain"][0]))
    shared["w_cq"] = f(seg(0)); shared["w_ckv"] = f(seg(1)); shared["w_kpe"] = f(seg(2))
    qn = seg(3).reshape(D, 8, 64)
    shared["w_qn"] = f(qn[:, PH, :].reshape(D, 512))
    kc = seg(4).reshape(D, 2, 64); vc = seg(5).reshape(D, 2, 64)
    shared["w_kc2"] = f(np.stack([kc[:, 0], kc[:, 0], kc[:, 1], kc[:, 1]], 1).reshape(D, 256))
    shared["w_vc2"] = f(np.stack([vc[:, 0], vc[:, 0], vc[:, 1], vc[:, 1]], 1).reshape(D, 256))
    shared["w_kv4"] = f(np.concatenate([seg(6), seg(8), seg(7), seg(9)], 1))
    gn = seg(10).reshape(D, 8, 3)
    shared["w_gn"] = f(gn[:, PH, :].transpose(0, 2, 1).reshape(D, 24))
    shared["w_gm"] = f(seg(11)); shared["w_gnm"] = f(seg(12))
    shared["qag"] = f(f(inp["mla_q_a_gain"][0]).reshape(6, 128).T); shared["kvag"] = f(f(inp["mla_kv_a_gain"][0]).reshape(2, 128).T)
    shared["w_qb"] = f(inp["mla_w_q_b"][0]); shared["w_kvb"] = f(inp["mla_w_kv_b"][0])
    shared["qgB"] = _rep(f(inp["mla_q_gain"][0])); shared["kgB"] = _rep(f(inp["mla_k_gain"][0]))
    shared["nqB"] = _rep(f(inp["nsa_q_gain"][0])); shared["nkcB"] = _rep(f(inp["nsa_kc_gain"][0]))
    shared["nksB"] = _rep(f(inp["nsa_ks_gain"][0])); shared["nkwB"] = _rep(f(inp["nsa_kw_gain"][0]))
    shared["posk"] = f(f(inp["cmp_pos_k"][0]).reshape(16, 128).T); shared["posv"] = f(f(inp["cmp_pos_v"][0]).reshape(16, 128).T)
    shared["w1k"] = f(inp["cmp_w1_k"][0]); shared["w2k"] = f(inp["cmp_w2_k"][0])
    shared["w1v"] = f(inp["cmp_w1_v"][0]); shared["w2v"] = f(inp["cmp_w2_v"][0])
    shared["wo_mla"] = f(inp["w_o_mla"][0])
    shared["wo_nsa"] = f(f(inp["w_o_nsa"][0]).reshape(8, 64, D)[PH].reshape(512, D))
    shared["w_out"] = f(inp["w_out"][0])
    shared["wg"] = f(inp["ffn_w_gate"][0]); shared["wu"] = f(inp["ffn_w_up"][0]); shared["wd"] = f(inp["ffn_w_down"][0])
    shared.update(_consts())
    maps = []
    xs = np.asarray(inp["x"], np.float32); cs = np.asarray(inp["c"], np.float32); ps = np.asarray(inp["positions"]).astype(np.int32)
    for b in range(xs.shape[0]):
        m = dict(shared)
        m["x"] = np.ascontiguousarray(xs[b])
        m["c_pk"] = np.ascontiguousarray(cs[b].reshape(8, 128).T)
        m["pos_pk"] = np.ascontiguousarray(ps[b].reshape(NT, 128).T)
        m["posC"] = np.ascontiguousarray(ps[b][31::16][:127].reshape(127, 1))
        maps.append(m)
    return maps


_NC_CACHE = {}


def kernel(**inputs):
    maps = prep_inputs(inputs)
    if "nc" not in _NC_CACHE:
        _NC_CACHE["nc"] = build()
    nc = _NC_CACHE["nc"]
    res = run_bass_kernel_spmd(nc, maps, core_ids=list(range(len(maps))))
    return np.stack([np.asarray(r["out"], dtype=np.float32) for r in res.results], 0)
```

```python
import contextlib
import numpy as np
import ml_dtypes
import concourse.bass as bass
import concourse.mybir as mybir
from concourse.bass_utils import run_bass_kernel_spmd

F32 = mybir.dt.float32
BF16 = mybir.dt.bfloat16
I32 = mybir.dt.int32
ALU = mybir.AluOpType
AF = mybir.ActivationFunctionType
AX = mybir.AxisListType

S_ = 2048
D = 1024
NT = 16
DFF = 2816
NFC = 22
EPS = 1e-6
PH = [0, 4, 1, 5, 2, 6, 3, 7]
TWO_PI = float(2 * np.pi)
PI = float(np.pi)


class Sched:
    N_DMA_SLOTS = {"sp": 24, "pool": 8, "act": 4}

    def __init__(self, nc, stack):
        self.nc = nc
        self.E = {"pe": nc.tensor, "act": nc.scalar, "dve": nc.vector, "pool": nc.gpsimd, "sp": nc.sync}
        self.sem, self.cnt = {}, {}
        for e in ("pe", "act", "dve", "pool"):
            self.sem[e] = stack.enter_context(nc.semaphore("s_" + e))
            self.cnt[e] = 0
        self.dsem, self.dcnt, self.dnext = {}, {}, {}
        for q, n in self.N_DMA_SLOTS.items():
            self.dsem[q] = [stack.enter_context(nc.semaphore(f"d_{q}{i}")) for i in range(n)]
            self.dcnt[q] = [0] * n
            self.dnext[q] = 0
        self.seen = {e: {} for e in self.E}
        self.lastw, self.readers = {}, {}
        self.n_wait = 0
        self.n_inst = 0

    def _sem_of(self, src):
        return self.dsem[src[1]][src[2]] if isinstance(src, tuple) else self.sem[src]

    def _wait(self, e, tok):
        src, val = tok
        if self.seen[e].get(src, 0) >= val:
            return
        self.E[e].wait_ge(self._sem_of(src), val)
        self.seen[e][src] = val
        self.n_wait += 1

    def _deps(self, e, r, w):
        toks = []
        for k in r:
            t = self.lastw.get(k)
            if t is not None:
                toks.append(t)
        for k in w:
            t = self.lastw.get(k)
            if t is not None:
                toks.append(t)
            for t in self.readers.get(k, ()):
                toks.append(t)
        for t in toks:
            if t[0] == e and e == "pe":
                continue
            self._wait(e, t)

    def _commit(self, tok, r, w):
        for k in r:
            lst = self.readers.setdefault(k, [])
            lst[:] = [t for t in lst if t[0] != tok[0]]
            lst.append(tok)
        for k in w:
            self.lastw[k] = tok
            self.readers[k] = []

    def op(self, e, fn, r=(), w=()):
        self._deps(e, r, w)
        ins = fn(self.E[e])
        self.cnt[e] += 1
        ins.then_inc(self.sem[e], 1)
        tok = (e, self.cnt[e])
        self._commit(tok, r, w)
        self.n_inst += 1
        return tok

    def dma(self, q, out, in_, r=(), w=(), **kw):
        slot = self.dnext[q]
        self.dnext[q] = (slot + 1) % len(self.dsem[q])
        src = ("d", q, slot)
        if self.dcnt[q][slot] > 0:
            self._wait(q, (src, self.dcnt[q][slot]))
        self._deps(q, r, w)
        ins = self.E[q].dma_start(out=out, in_=in_, **kw)
        self.dcnt[q][slot] += 16
        ins.then_inc(self.dsem[q][slot], 16)
        tok = (src, self.dcnt[q][slot])
        self._commit(tok, r, w)
        self.n_inst += 1
        return tok

    def barrier(self):
        for e in ("pe", "act", "dve", "pool", "sp"):
            for e2 in ("pe", "act", "dve", "pool"):
                if self.cnt[e2] and not (e2 == e == "pe"):
                    self._wait(e, (e2, self.cnt[e2]))
            for q in self.dsem:
                for i in range(len(self.dsem[q])):
                    if self.dcnt[q][i]:
                        self._wait(e, (("d", q, i), self.dcnt[q][i]))
        self.lastw.clear()
        self.readers.clear()


class Mem:
    def __init__(self, big, nbytes):
        self.big, self.lo, self.hi, self.n = big, 0, nbytes, nbytes

    def _view(self, off, shape, dt):
        nel = int(np.prod(shape))
        esz = 4 if dt in (F32, I32) else 2
        nb = nel * esz
        ap = self.big[:, off // 2:(off + nb) // 2]
        if esz == 4:
            ap = ap.bitcast(dt)
        if len(shape) == 2:
            ap = ap.rearrange("p (a b) -> p a b", a=shape[0])
        elif len(shape) == 3:
            ap = ap.rearrange("p (a b c) -> p a b c", a=shape[0], b=shape[1])
        return ap

    def lo_alloc(self, shape, dt):
        nb = int(np.prod(shape)) * (4 if dt in (F32, I32) else 2)
        nb = (nb + 63) // 64 * 64
        off = self.lo
        self.lo += nb
        assert self.lo <= self.hi, f"SBUF overflow lo={self.lo} hi={self.hi}"
        return self._view(off, shape, dt)

    def hi_alloc(self, shape, dt):
        nb = int(np.prod(shape)) * (4 if dt in (F32, I32) else 2)
        nb = (nb + 63) // 64 * 64
        self.hi -= nb
        assert self.lo <= self.hi, f"SBUF overflow lo={self.lo} hi={self.hi}"
        return self._view(self.hi, shape, dt)


def build(upto=99, dbg=()):
    nc = bass.Bass("TRN2", target_bir_lowering=False)
    I = {}

    def din(name, shape, dt=F32):
        I[name] = nc.dram_tensor(name, list(shape), dt, kind="ExternalInput").ap()
        return I[name]

    x = din("x", [S_, D]); c_pk = din("c_pk", [128, 8]); pos_pk = din("pos_pk", [128, NT], I32)
    posC = din("posC", [127, 1], I32)
    ada_w = din("ada_w", [D, 6 * D]); adabB = din("adabB", [128, 6 * D]); g1B = din("g1B", [128, D]); g2B = din("g2B", [128, D])
    w_cq = din("w_cq", [D, 768]); w_ckv = din("w_ckv", [D, 256]); w_kpe = din("w_kpe", [D, 32])
    w_qn = din("w_qn", [D, 512]); w_kc2 = din("w_kc2", [D, 256]); w_vc2 = din("w_vc2", [D, 256])
    w_kv4 = din("w_kv4", [D, 512]); w_gn = din("w_gn", [D, 24]); w_gm = din("w_gm", [D, D]); w_gnm = din("w_gnm", [D, D])
    qag = din("qag", [128, 6]); kvag = din("kvag", [128, 2])
    w_qb = din("w_qb", [768, 768]); w_kvb = din("w_kvb", [256, 1024])
    qgB = din("qgB", [128, 96]); kgB = din("kgB", [128, 96])
    nqB = din("nqB", [128, 64]); nkcB = din("nkcB", [128, 64]); nksB = din("nksB", [128, 64]); nkwB = din("nkwB", [128, 64])
    posk = din("posk", [128, 16]); w1k = din("w1k", [2048, 256]); w2k = din("w2k", [256, 64])
    posv = din("posv", [128, 16]); w1v = din("w1v", [2048, 256]); w2v = din("w2v", [256, 64])
    wo_mla = din("wo_mla", [512, D]); wo_nsa = din("wo_nsa", [512, D]); w_out = din("w_out", [D, D])
    wg = din("wg", [D, DFF]); wu = din("wu", [D, DFF]); wd = din("wd", [DFF, D])
    ident_d = din("ident", [128, 128], BF16); tri_d = din("tri", [128, 128], BF16); winm_d = din("winm", [128, 384], BF16)
    inv16_d = din("inv16", [128, 16]); inv8_d = din("inv8", [128, 8])
    ovl_d = din("ovl", [127, 32], BF16); vmT_d = din("vmT", [127, S_], BF16); XE_d = din("XE", [32, NT * 128], BF16)
    fb_d = din("fb", [128, NT * 32]); vj_d = din("vj", [128, NT * 32])
    out = nc.dram_tensor("out", [S_, D], F32, kind="ExternalOutput").ap()
    hTs = nc.dram_tensor("hTs", [128, 8 * S_], BF16).ap()
    mods = nc.dram_tensor("mods", [128, 6 * D], F32).ap()
    x1s = nc.dram_tensor("x1s", [S_, D], F32).ap()
    D_ = {}
    for name, shape, dt in dbg:
        D_[name] = nc.dram_tensor("dbg_" + name, list(shape), dt, kind="ExternalOutput").ap()

    st = contextlib.ExitStack()
    with st:
        S = Sched(nc, st)
        NB = 204800
        big = st.enter_context(nc.sbuf_tensor("big", [128, NB // 2], BF16))
        M = Mem(big, NB)
        P = [st.enter_context(nc.psum_tensor(f"ps{i}", [128, 512], F32)) for i in range(8)]
        PK = [f"ps{i}" for i in range(8)]
        bank_state = [0]

        nbanks = [8]

        def nb():
            b = bank_state[0] % nbanks[0]
            bank_state[0] = (b + 1) % nbanks[0]
            return b

        def mm(ps_ap, lhsT, rhs, start, stop, r, w, **kw):
            S.op("pe", lambda e: e.matmul(ps_ap, lhsT=lhsT, rhs=rhs, start=start, stop=stop, **kw), r=r, w=w)

        def tp(ps_ap, in_, ident_ap, r, w):
            S.op("pe", lambda e: e.transpose(out=ps_ap, in_=in_, identity=ident_ap), r=r, w=w)

        def act(out_, in_, func, r, w, **kw):
            S.op("act", lambda e: e.activation(out=out_, in_=in_, func=func, **kw), r=r, w=w)

        def dbg_out(name, ap, r):
            if name in D_:
                S.dma("sp", D_[name], ap, r=r)

        ident = M.lo_alloc([128], BF16); tri = M.lo_alloc([128], BF16); winm = M.lo_alloc([3, 128], BF16)
        onesb = M.lo_alloc([128], BF16)
        stg = [M.lo_alloc([1024], F32) for _ in range(3)]
        stg_i = [0]
        cosM = M.lo_alloc([NT, 16], F32); sinM = M.lo_alloc([NT, 16], F32)
        cosN = M.lo_alloc([NT, 8], F32); sinN = M.lo_alloc([NT, 8], F32)
        cosC = M.lo_alloc([8], F32); sinC = M.lo_alloc([8], F32)
        lo_pers = M.lo
        omT = M.lo_alloc([4, S_], BF16); onT = M.lo_alloc([4, S_], BF16)
        S.dma("sp", ident, ident_d, w=["ident"])
        S.dma("sp", tri, tri_d, w=["tri"])
        S.dma("sp", winm.rearrange("p a b -> p (a b)"), winm_d, w=["winm"])
        S.op("pool", lambda e: e.memset(onesb, 1.0), w=["onesb"])

        def load_w(dst, W, key, KC, N, ceng="dve"):
            Wv = W.rearrange("(k p) n -> p k n", p=128)
            if N <= 1024:
                g = max(1, min(KC, 1024 // N))
                for k0 in range(0, KC, g):
                    k1 = min(KC, k0 + g)
                    si = stg_i[0]; stg_i[0] = (si + 1) % 3
                    sv = stg[si][:, 0:(k1 - k0) * N].rearrange("p (k n) -> p k n", n=N)
                    S.dma("sp", sv, Wv[:, k0:k1, :], w=[("stg", si)])
                    S.op(ceng, lambda e, sv=sv, k0=k0, k1=k1: e.tensor_copy(out=dst[:, k0:k1, :], in_=sv), r=[("stg", si)], w=[key])
            else:
                for k in range(KC):
                    for c0 in range(0, N, 1024):
                        c1 = min(N, c0 + 1024)
                        si = stg_i[0]; stg_i[0] = (si + 1) % 3
                        sv = stg[si][:, 0:c1 - c0]
                        S.dma("sp", sv, Wv[:, k, c0:c1], w=[("stg", si)])
                        S.op(ceng, lambda e, sv=sv, k=k, c0=c0, c1=c1: e.tensor_copy(out=dst[:, k, c0:c1], in_=sv), r=[("stg", si)], w=[key])

        def load_w_cols(dst, W, key, KC, c0, c1, ceng="dve"):
            Wv = W.rearrange("(k p) n -> p k n", p=128)
            N = c1 - c0
            g = max(1, min(KC, 1024 // N))
            for k0 in range(0, KC, g):
                k1 = min(KC, k0 + g)
                si = stg_i[0]; stg_i[0] = (si + 1) % 3
                sv = stg[si][:, 0:(k1 - k0) * N].rearrange("p (k n) -> p k n", n=N)
                S.dma("sp", sv, Wv[:, k0:k1, c0:c1], w=[("stg", si)])
                S.op(ceng, lambda e, sv=sv, k0=k0, k1=k1: e.tensor_copy(out=dst[:, k0:k1, :], in_=sv), r=[("stg", si)], w=[key])

        def sincos(ang, shape, cos_o, sin_o, np_, tmp_f, tmp_i, tmp_m, key):
            for (shift, dst) in ((0.0, sin_o), (PI / 2, cos_o)):
                S.op("dve", lambda e: e.tensor_scalar(out=tmp_f, in0=ang, scalar1=shift, scalar2=None, op0=ALU.add), r=[key + "ang"], w=[key + "f"])
                S.op("dve", lambda e: e.tensor_scalar(out=tmp_i, in0=tmp_f, scalar1=float(1 / TWO_PI), scalar2=None, op0=ALU.mult), r=[key + "f"], w=[key + "i"])
                S.op("dve", lambda e: e.tensor_copy(out=tmp_m, in_=tmp_i), r=[key + "i"], w=[key + "m"])
                S.op("dve", lambda e: e.scalar_tensor_tensor(out=tmp_f, in0=tmp_m, scalar=-TWO_PI, in1=tmp_f, op0=ALU.mult, op1=ALU.add), r=[key + "m", key + "f"], w=[key + "f"])
                S.op("dve", lambda e: e.tensor_scalar(out=tmp_m, in0=tmp_f, scalar1=PI, scalar2=None, op0=ALU.is_gt), r=[key + "f"], w=[key + "m"])
                S.op("dve", lambda e: e.scalar_tensor_tensor(out=tmp_f, in0=tmp_m, scalar=-TWO_PI, in1=tmp_f, op0=ALU.mult, op1=ALU.add), r=[key + "m", key + "f"], w=[key + "f"])
                S.op("dve", lambda e: e.tensor_scalar(out=tmp_m, in0=tmp_f, scalar1=-PI, scalar2=None, op0=ALU.is_lt), r=[key + "f"], w=[key + "m"])
                S.op("dve", lambda e: e.scalar_tensor_tensor(out=tmp_f, in0=tmp_m, scalar=TWO_PI, in1=tmp_f, op0=ALU.mult, op1=ALU.add), r=[key + "m", key + "f"], w=[key + "f"])
                act(dst, tmp_f, AF.Sin, r=[key + "f"], w=[key + "out"])

        lo0, hi0 = M.lo, M.hi
        if upto <= -2:
            return finish(nc, S, st)
        posi = M.lo_alloc([NT], I32); posf = M.lo_alloc([NT], F32)
        posCi = M.lo_alloc([1], I32); posCf = M.lo_alloc([1], F32)
        inv16 = M.lo_alloc([16], F32); inv8 = M.lo_alloc([8], F32)
        angM = M.lo_alloc([NT, 16], F32); tfM = M.lo_alloc([NT, 16], F32); tiM = M.lo_alloc([NT, 16], I32); tmM = M.lo_alloc([NT, 16], F32)
        S.dma("sp", posi, pos_pk, w=["posi"])
        S.dma("sp", posCi[0:127], posC, w=["posCi"])
        S.dma("sp", inv16, inv16_d, w=["inv16"])
        S.dma("sp", inv8, inv8_d, w=["inv8"])
        S.op("dve", lambda e: e.tensor_copy(out=posf, in_=posi), r=["posi"], w=["posf"])
        S.op("dve", lambda e: e.tensor_copy(out=posCf[0:127], in_=posCi[0:127]), r=["posCi"], w=["posCf"])
        S.op("dve", lambda e: e.tensor_tensor(out=angM, in0=posf.unsqueeze(2).to_broadcast([128, NT, 16]),
                                              in1=inv16.unsqueeze(1).to_broadcast([128, NT, 16]), op=ALU.mult), r=["posf", "inv16"], w=["Mang"])
        sincos(angM, None, cosM, sinM, 128, tfM, tiM, tmM, "M")
        a8 = angM.rearrange("p a b -> p (a b)")[:, 0:NT * 8].rearrange("p (a b) -> p a b", b=8)
        f8 = tfM.rearrange("p a b -> p (a b)")[:, 0:NT * 8].rearrange("p (a b) -> p a b", b=8)
        i8 = tiM.rearrange("p a b -> p (a b)")[:, 0:NT * 8].rearrange("p (a b) -> p a b", b=8)
        m8_ = tmM.rearrange("p a b -> p (a b)")[:, 0:NT * 8].rearrange("p (a b) -> p a b", b=8)
        S.op("dve", lambda e: e.tensor_tensor(out=a8, in0=posf.unsqueeze(2).to_broadcast([128, NT, 8]),
                                              in1=inv8.unsqueeze(1).to_broadcast([128, NT, 8]), op=ALU.mult), r=["posf", "inv8", "Mout", "Mf", "Mm", "Mi"], w=["Nang"])
        sincos(a8, None, cosN, sinN, 128, f8, i8, m8_, "N")
        aC = angM.rearrange("p a b -> p (a b)")[0:127, 0:8]
        fC = tfM.rearrange("p a b -> p (a b)")[0:127, 0:8]
        iC = tiM.rearrange("p a b -> p (a b)")[0:127, 0:8]
        mC = tmM.rearrange("p a b -> p (a b)")[0:127, 0:8]
        S.op("dve", lambda e: e.tensor_scalar(out=aC, in0=inv8[0:127], scalar1=posCf[0:127, 0:1], scalar2=None, op0=ALU.mult),
             r=["posCf", "inv8", "Nout", "Nf", "Nm", "Ni", "Nang"], w=["Cang"])
        sincos(aC, None, cosC[0:127], sinC[0:127], 127, fC, iC, mC, "C")
        dbg_out("cosM", cosM, ["Mout"]); dbg_out("sinM", sinM, ["Mout"])

        if upto <= -1:
            return finish(nc, S, st)
        cpk = M.lo_alloc([8], F32); sc = M.lo_alloc([8], F32)
        sch = M.lo_alloc([8], BF16); scl = M.lo_alloc([8], BF16)
        cBh = M.lo_alloc([8, 128], BF16); cBl = M.lo_alloc([8, 128], BF16)
        modB = M.lo_alloc([6 * D], F32)
        g1t = M.lo_alloc([D], F32); g2t = M.lo_alloc([D], F32)
        awb = [M.hi_alloc([8, 512], F32) for _ in range(3)]
        abb = [M.hi_alloc([512], F32) for _ in range(3)]
        awh = [M.hi_alloc([8, 512], BF16) for _ in range(3)]
        awl = [M.hi_alloc([8, 512], BF16) for _ in range(3)]
        S.dma("sp", cpk, c_pk, w=["cpk"])
        S.dma("sp", g1t, g1B, w=["g1t"]); S.dma("sp", g2t, g2B, w=["g2t"])
        act(sc, cpk, AF.Silu, r=["cpk"], w=["sc"])
        S.op("dve", lambda e: e.tensor_copy(out=sch, in_=sc), r=["sc"], w=["sch"])
        S.op("dve", lambda e: e.tensor_tensor(out=scl, in0=sc, in1=sch, op=ALU.subtract), r=["sc", "sch"], w=["scl"])
        for k in range(8):
            S.op("dve", lambda e, k=k: e.tensor_copy(out=cBh[:, k, :], in_=sch[:, k:k + 1].to_broadcast([128, 128])), r=["sch"], w=["cBh"])
            S.op("dve", lambda e, k=k: e.tensor_copy(out=cBl[:, k, :], in_=scl[:, k:k + 1].to_broadcast([128, 128])), r=["scl"], w=["cBl"])
        awv = ada_w.rearrange("(k p) n -> p k n", p=128)
        for n in range(12):
            q_ = n % 3
            S.dma("sp", awb[q_], awv[:, :, n * 512:(n + 1) * 512], w=[("awb", q_)])
            S.dma("sp", abb[q_], adabB[:, n * 512:(n + 1) * 512], w=[("abb", q_)])
            act(awh[q_], awb[q_], AF.Copy, r=[("awb", q_)], w=[("awh", q_)])
            S.op("dve", lambda e, q_=q_: e.tensor_tensor(out=awl[q_], in0=awb[q_], in1=awh[q_], op=ALU.subtract), r=[("awb", q_), ("awh", q_)], w=[("awl", q_)])
            b = nb()
            passes = [(cBh, "cBh", awh, "awh"), (cBh, "cBh", awl, "awl"), (cBl, "cBl", awh, "awh")]
            for pi_, (cb_, ck, ww, wk) in enumerate(passes):
                for k in range(8):
                    mm(P[b][:, :], cb_[:, k, :], ww[q_][:, k, :], pi_ == 0 and k == 0, pi_ == 2 and k == 7, r=[ck, (wk, q_)], w=[PK[b]])
            S.op("dve", lambda e, n=n, b=b, q_=q_: e.tensor_tensor(out=modB[:, n * 512:(n + 1) * 512], in0=P[b][:, :], in1=abb[q_], op=ALU.add),
                 r=[PK[b], ("abb", q_)], w=["modB"])
        S.op("dve", lambda e: e.scalar_tensor_tensor(out=modB[:, D:2 * D], in0=modB[:, D:2 * D], scalar=1.0, in1=g1t, op0=ALU.add, op1=ALU.mult), r=["modB", "g1t"], w=["modB"])
        S.op("dve", lambda e: e.scalar_tensor_tensor(out=modB[:, 4 * D:5 * D], in0=modB[:, 4 * D:5 * D], scalar=1.0, in1=g2t, op0=ALU.add, op1=ALU.mult), r=["modB", "g2t"], w=["modB"])
        S.dma("sp", mods, modB, r=["modB"], w=["mods"])
        dbg_out("modB", modB, ["modB"])
        B1 = modB[:, 0:D]; A1 = modB[:, D:2 * D]
        if upto <= 0:
            return finish(nc, S, st)

        M.hi = hi0
        hT = M.hi_alloc([8, S_], BF16)
        xb = [M.lo_alloc([D], F32) for _ in range(2)]
        junk = M.lo_alloc([D], F32); tmpA = M.lo_alloc([D], F32)
        hb = [M.lo_alloc([D], BF16) for _ in range(2)]
        ssq = M.lo_alloc([NT], F32); rs = M.lo_alloc([NT], F32)

        tmpA2 = [tmpA, M.lo_alloc([D], F32)]

        def norm_a(i, xt, xkey):
            act(junk, xt, AF.Square, r=[xkey], w=["junk", ("ssq", i)], accum_out=ssq[:, i:i + 1])
            act(rs[:, i:i + 1], ssq[:, i:i + 1], AF.Sqrt, r=[("ssq", i)], w=[("rs", i)], scale=1.0 / D, bias=EPS)
            S.op("dve", lambda e: e.reciprocal(out=rs[:, i:i + 1], in_=rs[:, i:i + 1]), r=[("rs", i)], w=[("rs", i)])

        def norm_b(i, xt, xkey, A, B, Akeys, dstT, dkey):
            par = i % 2
            S.op("dve", lambda e: e.scalar_tensor_tensor(out=tmpA2[par], in0=xt, scalar=rs[:, i:i + 1], in1=A, op0=ALU.mult, op1=ALU.mult),
                 r=[xkey, ("rs", i)] + Akeys, w=[("tmpA2", par)])
            S.op("pool", lambda e: e.tensor_tensor(out=hb[par], in0=tmpA2[par], in1=B, op=ALU.add), r=[("tmpA2", par)] + Akeys, w=[("hb", par)])
            b = nb()
            pb = P[b][:, :].bitcast(BF16)
            for k in range(8):
                tp(pb[:, k * 128:(k + 1) * 128], hb[par][:, k * 128:(k + 1) * 128], ident, r=[("hb", par), "ident"], w=[PK[b]])
            act(dstT[:, :, i * 128:(i + 1) * 128], pb.rearrange("p (k t) -> p k t", k=8), AF.Copy, r=[PK[b]], w=[(dkey, i)])

        xb = xb + [M.lo_alloc([D], F32)]

        def a_front(i):
            S.dma("sp", xb[i % 3], x[i * 128:(i + 1) * 128, :], w=[("xb", i % 3)])
            norm_a(i, xb[i % 3], ("xb", i % 3))

        a_front(0)
        for i in range(NT):
            if i + 1 < NT:
                a_front(i + 1)
            norm_b(i, xb[i % 3], ("xb", i % 3), A1, B1, ["modB"], hT, "hT")
        hTk = [("hT", i) for i in range(NT)]
        S.dma("sp", hTs, hT.rearrange("p k t -> p (k t)"), r=hTk, w=["hTs"])
        dbg_out("hT", hT.rearrange("p k t -> p (k t)"), hTk)
        if upto <= 1:
            return finish(nc, S, st)
        S.barrier()

        M.lo = lo0
        cqT = M.lo_alloc([6, S_], BF16); ckvT = M.lo_alloc([2, S_], BF16); kpe = M.lo_alloc([NT, 32], F32)
        lo1 = M.lo
        Wcq = M.lo_alloc([8, 768], BF16); Wckv = M.lo_alloc([8, 256], BF16); Wkpe = M.lo_alloc([8, 32], BF16)
        qagt = M.lo_alloc([6], F32); kvagt = M.lo_alloc([2], F32)
        sqb = [M.lo_alloc([512], BF16) for _ in range(2)]
        rb = M.lo_alloc([512], F32)
        S.dma("sp", qagt, qag, w=["qagt"]); S.dma("sp", kvagt, kvag, w=["kvagt"])
        load_w(Wcq, w_cq, "Wcq", 8, 768); load_w(Wckv, w_ckv, "Wckv", 8, 256); load_w(Wkpe, w_kpe, "Wkpe", 8, 32)

        def fm_proj_norm(dstT, dkey, Wt, wkey, nf, gaint, gkey, nfeat):
            for c in range(4):
                hk = [("hT", 4 * c + q) for q in range(4)]
                for j in range(nf):
                    b = nb()
                    for k in range(8):
                        mm(P[b][:, :], Wt[:, k, j * 128:(j + 1) * 128], hT[:, k, c * 512:(c + 1) * 512], k == 0, k == 7, r=[wkey] + hk, w=[PK[b]])
                    act(dstT[:, j, c * 512:(c + 1) * 512], P[b][:, :], AF.Copy, r=[PK[b]], w=[(dkey, c)])
                    act(sqb[j % 2], P[b][:, :], AF.Square, r=[PK[b]], w=[("sqb", j % 2)])
                    mm(P[6][:, :], onesb, sqb[j % 2], j == 0, j == nf - 1, r=["onesb", ("sqb", j % 2)], w=[PK[6]])
                act(rb, P[6][:, :], AF.Sqrt, r=[PK[6]], w=["rb"], scale=1.0 / nfeat, bias=EPS)
                S.op("dve", lambda e: e.reciprocal(out=rb, in_=rb), r=["rb"], w=["rb"])
                for j in range(nf):
                    S.op("dve", lambda e, j=j, c=c: e.scalar_tensor_tensor(out=dstT[:, j, c * 512:(c + 1) * 512], in0=dstT[:, j, c * 512:(c + 1) * 512],
                                                                             scalar=gaint[:, j:j + 1], in1=rb, op0=ALU.mult, op1=ALU.mult),
                         r=[(dkey, c), "rb", gkey], w=[(dkey, c)])

        fm_proj_norm(cqT, "cqT", Wcq, "Wcq", 6, qagt, "qagt", 768)
        fm_proj_norm(ckvT, "ckvT", Wckv, "Wckv", 2, kvagt, "kvagt", 256)
        for i in range(NT):
            b = nb()
            for k in range(8):
                mm(P[b][:, 0:32], hT[:, k, i * 128:(i + 1) * 128], Wkpe[:, k, :], k == 0, k == 7, r=["Wkpe", ("hT", i)], w=[PK[b]])
            S.op("dve", lambda e, i=i, b=b: e.tensor_copy(out=kpe[:, i, :], in_=P[b][:, 0:32]), r=[PK[b]], w=[("kpe", i)])
        cqk = [("cqT", c) for c in range(4)]
        dbg_out("cqT", cqT.rearrange("p k t -> p (k t)"), cqk)
        dbg_out("kpe", kpe.rearrange("p a b -> p (a b)"), [("kpe", i) for i in range(NT)])
        if upto <= 2:
            return finish(nc, S, st)
        S.barrier()

        M.lo = lo1
        M.hi = hi0
        QT = M.hi_alloc([8, S_], BF16); KT = M.hi_alloc([8, S_], BF16); V = M.hi_alloc([NT, 8, 65], BF16)
        hi1 = M.hi
        Wqb = M.lo_alloc([6, 768], BF16); Wkvb = M.lo_alloc([2, 1024], BF16)
        qgt = M.lo_alloc([96], F32); kgt = M.lo_alloc([96], F32)
        drq = [M.lo_alloc([768], BF16) for _ in range(2)]; drk = [M.lo_alloc([768], BF16) for _ in range(2)]
        S.dma("sp", qgt, qgB, w=["qgt"]); S.dma("sp", kgt, kgB, w=["kgt"])
        load_w(Wqb, w_qb, "Wqb", 6, 768); load_w(Wkvb, w_kvb, "Wkvb", 2, 1024)
        S.op("pool", lambda e: e.memset(V[:, :, :, 64:65], 1.0), w=["Vones"])

        def mk_tmps(Mx, n, H, hf):
            return dict(t1=Mx.lo_alloc([n], F32), t2=Mx.lo_alloc([n], F32), hs=Mx.lo_alloc([H], F32), hr=Mx.lo_alloc([H], F32),
                        ra=Mx.lo_alloc([H * hf], F32), rb=Mx.lo_alloc([H * hf], F32), ra2=Mx.lo_alloc([H * hf], F32), rb2=Mx.lo_alloc([H * hf], F32))

        def hnr_stages(tag, T, src, skeys, H, Dh, gaint, gkey, ro, hf, cos_, sin_, dst, dkey, np_=128):
            n = H * Dh
            t1v = T["t1"][0:np_, 0:n].rearrange("p (h d) -> p h d", h=H)
            t2v = T["t2"][0:np_, 0:n].rearrange("p (h d) -> p h d", h=H)
            hs = T["hs"][0:np_, 0:H]; hr = T["hr"][0:np_, 0:H]
            x1 = t1v[:, :, ro:ro + hf]; x2 = t1v[:, :, ro + hf:ro + 2 * hf]
            cb = cos_.unsqueeze(1).to_broadcast([np_, H, hf]); sb_ = sin_.unsqueeze(1).to_broadcast([np_, H, hf])
            rv = {k: T[k][0:np_, 0:H * hf].rearrange("p (h d) -> p h d", h=H) for k in ("ra", "rb", "ra2", "rb2")}
            tr = ["Mout", "Nout", "Cout"]
            k_ = lambda nm: (tag, nm)
            st = []
            st.append(lambda: act(t1v, src, AF.Square, r=skeys, w=[k_("t1")]))
            st.append(lambda: S.op("dve", lambda e: e.tensor_reduce(out=hs, in_=t1v, axis=AX.X, op=ALU.add), r=[k_("t1")], w=[k_("hs")]))
            st.append(lambda: act(hr, hs, AF.Sqrt, r=[k_("hs")], w=[k_("hr")], scale=1.0 / Dh, bias=EPS))
            st.append(lambda: S.op("dve", lambda e: e.reciprocal(out=hr, in_=hr), r=[k_("hr")], w=[k_("hr")]))
            st.append(lambda: S.op("dve", lambda e: e.tensor_tensor(out=t2v, in0=src, in1=hr.unsqueeze(2).to_broadcast([np_, H, Dh]), op=ALU.mult), r=skeys + [k_("hr")], w=[k_("t2")]))
            st.append(lambda: S.op("dve", lambda e: e.tensor_tensor(out=t1v, in0=t2v, in1=gaint[0:np_].unsqueeze(1).to_broadcast([np_, H, Dh]), op=ALU.mult), r=[k_("t2"), gkey], w=[k_("t1")]))
            st.append(lambda: S.op("dve", lambda e: e.tensor_tensor(out=rv["ra"], in0=x1, in1=cb, op=ALU.mult), r=[k_("t1")] + tr, w=[k_("ra")]))
            st.append(lambda: S.op("dve", lambda e: e.tensor_tensor(out=rv["rb"], in0=x2, in1=sb_, op=ALU.mult), r=[k_("t1")] + tr, w=[k_("rb")]))
            st.append(lambda: S.op("dve", lambda e: e.tensor_tensor(out=dst[:, :, ro:ro + hf], in0=rv["ra"], in1=rv["rb"], op=ALU.subtract), r=[k_("ra"), k_("rb")], w=[dkey]))
            st.append(lambda: S.op("dve", lambda e: e.tensor_tensor(out=rv["ra2"], in0=x2, in1=cb, op=ALU.mult), r=[k_("t1")] + tr, w=[k_("ra2")]))
            st.append(lambda: S.op("dve", lambda e: e.tensor_tensor(out=rv["rb2"], in0=x1, in1=sb_, op=ALU.mult), r=[k_("t1")] + tr, w=[k_("rb2")]))
            st.append(lambda: S.op("dve", lambda e: e.tensor_tensor(out=dst[:, :, ro + hf:ro + 2 * hf], in0=rv["ra2"], in1=rv["rb2"], op=ALU.add), r=[k_("ra2"), k_("rb2")], w=[dkey]))

            def copies():
                if ro > 0:
                    S.op("pool", lambda e: e.tensor_copy(out=dst[:, :, 0:ro], in_=t1v[:, :, 0:ro]), r=[k_("t1")], w=[dkey])
                if ro + 2 * hf < Dh:
                    S.op("pool", lambda e: e.tensor_copy(out=dst[:, :, ro + 2 * hf:Dh], in_=t1v[:, :, ro + 2 * hf:Dh]), r=[k_("t1")], w=[dkey])
            st.insert(6, copies)
            return st

        def run_interleaved(chains):
            for s_ in range(max(len(c_) for c_ in chains)):
                for c_ in chains:
                    if s_ < len(c_):
                        c_[s_]()

        def head_norm_rope(src, skeys, H, Dh, gaint, gkey, ro, hf, cos_, sin_, dst, dkey, np_=128):
            n = H * Dh
            t1v = t1[0:np_, 0:n].rearrange("p (h d) -> p h d", h=H)
            t2v = t2[0:np_, 0:n].rearrange("p (h d) -> p h d", h=H)
            hs = hss[0:np_, 0:H]; hr = hrs[0:np_, 0:H]
            act(t1v, src, AF.Square, r=skeys, w=["t1"])
            S.op("dve", lambda e: e.tensor_reduce(out=hs, in_=t1v, axis=AX.X, op=ALU.add), r=["t1"], w=["hss"])
            act(hr, hs, AF.Sqrt, r=["hss"], w=["hrs"], scale=1.0 / Dh, bias=EPS)
            S.op("dve", lambda e: e.reciprocal(out=hr, in_=hr), r=["hrs"], w=["hrs"])
            S.op("dve", lambda e: e.tensor_tensor(out=t2v, in0=src, in1=hr.unsqueeze(2).to_broadcast([np_, H, Dh]), op=ALU.mult), r=skeys + ["hrs"], w=["t2"])
            S.op("dve", lambda e: e.tensor_tensor(out=t1v, in0=t2v, in1=gaint[0:np_].unsqueeze(1).to_broadcast([np_, H, Dh]), op=ALU.mult), r=["t2", gkey], w=["t1"])
            x1 = t1v[:, :, ro:ro + hf]; x2 = t1v[:, :, ro + hf:ro + 2 * hf]
            cb = cos_.unsqueeze(1).to_broadcast([np_, H, hf]); sb_ = sin_.unsqueeze(1).to_broadcast([np_, H, hf])
            rav = ra[0:np_, 0:H * hf].rearrange("p (h d) -> p h d", h=H)
            rbv = rbb[0:np_, 0:H * hf].rearrange("p (h d) -> p h d", h=H)
            tr = ["Mout", "Nout", "Cout"]
            S.op("dve", lambda e: e.tensor_tensor(out=rav, in0=x1, in1=cb, op=ALU.mult), r=["t1"] + tr, w=["ra"])
            S.op("dve", lambda e: e.tensor_tensor(out=rbv, in0=x2, in1=sb_, op=ALU.mult), r=["t1"] + tr, w=["rbb"])
            S.op("dve", lambda e: e.tensor_tensor(out=dst[:, :, ro:ro + hf], in0=rav, in1=rbv, op=ALU.subtract), r=["ra", "rbb"], w=[dkey])
            S.op("dve", lambda e: e.tensor_tensor(out=rav, in0=x2, in1=cb, op=ALU.mult), r=["t1"] + tr, w=["ra"])
            S.op("dve", lambda e: e.tensor_tensor(out=rbv, in0=x1, in1=sb_, op=ALU.mult), r=["t1"] + tr, w=["rbb"])
            S.op("dve", lambda e: e.tensor_tensor(out=dst[:, :, ro + hf:ro + 2 * hf], in0=rav, in1=rbv, op=ALU.add), r=["ra", "rbb"], w=[dkey])
            if ro > 0:
                S.op("pool", lambda e: e.tensor_copy(out=dst[:, :, 0:ro], in_=t1v[:, :, 0:ro]), r=["t1"], w=[dkey])
            if ro + 2 * hf < Dh:
                S.op("pool", lambda e: e.tensor_copy(out=dst[:, :, ro + 2 * hf:Dh], in_=t1v[:, :, ro + 2 * hf:Dh]), r=["t1"], w=[dkey])

        Msub = Mem(big, NB); Msub.lo = lo_pers; Msub.hi = lo0
        rawq = [Msub.lo_alloc([768], F32) for _ in range(2)]; rawk = [Msub.lo_alloc([768], F32) for _ in range(2)]
        Tq = mk_tmps(Msub, 768, 8, 16); Tk = mk_tmps(Msub, 768, 8, 16)

        def b2_front(i):
            ts = slice(i * 128, (i + 1) * 128)
            par = i % 2
            bA, bB = nb(), nb()
            for k in range(6):
                mm(P[bA][:, :], cqT[:, k, ts], Wqb[:, k, 0:512], k == 0, k == 5, r=["Wqb", ("cqT", i // 4)], w=[PK[bA]])
            for k in range(6):
                mm(P[bB][:, 0:256], cqT[:, k, ts], Wqb[:, k, 512:768], k == 0, k == 5, r=["Wqb", ("cqT", i // 4)], w=[PK[bB]])
            act(rawq[par][:, 0:512], P[bA][:, :], AF.Copy, r=[PK[bA]], w=[("rawq", par)])
            act(rawq[par][:, 512:768], P[bB][:, 0:256], AF.Copy, r=[PK[bB]], w=[("rawq", par)])
            bA, bB = nb(), nb()
            for hh, bb in ((0, bA), (1, bB)):
                for k in range(2):
                    mm(P[bb][:, :], ckvT[:, k, ts], Wkvb[:, k, hh * 512:(hh + 1) * 512], k == 0, k == 1, r=["Wkvb", ("ckvT", i // 4)], w=[PK[bb]])
            rv = rawk[par].rearrange("p (h d) -> p h d", h=8)
            for hh, bb in ((0, bA), (1, bB)):
                pv = P[bb][:, :].rearrange("p (h d) -> p h d", h=4)
                act(rv[:, hh * 4:(hh + 1) * 4, 0:64], pv[:, :, 0:64], AF.Copy, r=[PK[bb]], w=[("rawk", par)])
                act(V[:, i, hh * 4:(hh + 1) * 4, 0:64], pv[:, :, 64:128], AF.Copy, r=[PK[bb]], w=[("V", i)])
            S.op("pool", lambda e, i=i: e.tensor_copy(out=rv[:, :, 64:96], in_=kpe[:, i, :].unsqueeze(1).to_broadcast([128, 8, 32])), r=[("kpe", i)], w=[("rawk", par)])

        def b2_back(i):
            ts = slice(i * 128, (i + 1) * 128)
            par = i % 2
            dq = drq[par].rearrange("p (h d) -> p h d", h=8); dk = drk[par].rearrange("p (h d) -> p h d", h=8)
            cq_ = hnr_stages("cq", Tq, rawq[par].rearrange("p (h d) -> p h d", h=8), [("rawq", par)], 8, 96, qgt, "qgt", 64, 16, cosM[:, i, :], sinM[:, i, :], dq, ("drq", par))
            ck_ = hnr_stages("ck", Tk, rawk[par].rearrange("p (h d) -> p h d", h=8), [("rawk", par)], 8, 96, kgt, "kgt", 64, 16, cosM[:, i, :], sinM[:, i, :], dk, ("drk", par))
            run_interleaved([cq_, ck_])
            for (dd, dkey_, dstT, okey) in ((dq, ("drq", par), QT, "QT"), (dk, ("drk", par), KT, "KT")):
                b = nb(); pb = P[b][:, :].bitcast(BF16)
                for h in range(8):
                    tp(pb[0:96, h * 128:(h + 1) * 128], dd[:, h, :], ident, r=[dkey_, "ident"], w=[PK[b]])
                act(dstT[0:96, :, ts], pb[0:96, :].rearrange("p (h t) -> p h t", h=8), AF.Copy, r=[PK[b]], w=[(okey, i)])

        b2_front(0)
        for i in range(NT):
            if i + 1 < NT:
                b2_front(i + 1)
            b2_back(i)
        QTk = [("QT", i) for i in range(NT)]
        dbg_out("QT", QT[0:96].rearrange("p k t -> p (k t)"), QTk)
        dbg_out("KT", KT[0:96].rearrange("p k t -> p (k t)"), [("KT", i) for i in range(NT)])
        dbg_out("V", V.rearrange("p a b c -> p (a b c)"), [("V", i) for i in range(NT)] + ["Vones"])
        if upto <= 3:
            return finish(nc, S, st)
        S.barrier()

        nbanks[0] = 6
        M.lo = lo0
        om = M.lo_alloc([NT, 512], BF16)
        PT = [M.lo_alloc([512], BF16) for _ in range(3)]
        rec4 = [M.lo_alloc([4], F32) for _ in range(2)]
        pt_i = [0]

        gchunk = [0]

        def causal_attn_multi(jobs):
            steps = []
            for ji in range(len(jobs)):
                for c in range(4):
                    for kt in range(4 * c + 4):
                        steps.append((ji, c, kt))

            def emit_qk(step):
                ji, c, kt = step
                J = jobs[ji]
                q0 = max(kt - 4 * c, 0)
                n = 512 - 128 * q0
                b = nb()
                has_extra = J["extra"] is not None
                mm(P[b][:, 0:n], J["KT"][0:J["kn"], kt * 128:(kt + 1) * 128], J["QT"](c * 512 + q0 * 128, (c + 1) * 512),
                   True, not has_extra, r=J["kk"](kt) + J["qk"](c), w=[PK[b]])
                if has_extra:
                    J["extra"](P[b][:, 0:n], kt, c * 512 + q0 * 128, (c + 1) * 512, PK[b])
                return b, n, q0

            pend = emit_qk(steps[0])
            for si, (ji, c, kt) in enumerate(steps):
                J = jobs[ji]
                b, n, q0 = pend
                if si + 1 < len(steps):
                    pend = emit_qk(steps[si + 1])
                if kt == 0:
                    gchunk[0] += 1
                ab = 6 + (gchunk[0] % 2)
                Oacc = P[ab][:, 0:260].rearrange("p (q d) -> p q d", q=4)
                pi = pt_i[0]; pt_i[0] = (pi + 1) % 3
                pt = PT[pi]
                act(pt[:, 0:n], P[b][:, 0:n], AF.Exp, r=[PK[b]], w=[("PT", pi)], scale=J["scale"])
                if kt >= 4 * c:
                    S.op("dve", lambda e, pt=pt: e.tensor_tensor(out=pt[:, 0:128], in0=pt[:, 0:128], in1=tri, op=ALU.mult), r=[("PT", pi), "tri"], w=[("PT", pi)])
                for qi in range(q0, 4):
                    mm(Oacc[:, qi, :], pt[:, (qi - q0) * 128:(qi - q0 + 1) * 128], J["V"](kt), kt == 0 and qi == 0, kt == 4 * c + qi,
                       r=[("PT", pi)] + J["vk"](kt), w=[PK[ab]], skip_group_check=True)
                if kt == 4 * c + 3:
                    J["fin"](c, Oacc, PK[ab])

        jobs = []
        for h in range(8):
            def fin(c, Oacc, pk, h=h):
                rc = rec4[c % 2]
                S.op("dve", lambda e: e.reciprocal(out=rc, in_=Oacc[:, :, 64]), r=[pk], w=[("rec4", c % 2)])
                S.op("dve", lambda e: e.tensor_tensor(out=om[:, 4 * c:4 * c + 4, h * 64:(h + 1) * 64], in0=Oacc[:, :, 0:64],
                                                      in1=rc.unsqueeze(2).to_broadcast([128, 4, 64]), op=ALU.mult),
                     r=[pk, ("rec4", c % 2)], w=[("om", c)])
            jobs.append(dict(KT=KT[:, h, :], kn=96, QT=(lambda a, b_, h=h: QT[0:96, h, a:b_]), V=(lambda kt, h=h: V[:, kt, h, :]), scale=96 ** -0.5,
                             extra=None, fin=fin, qk=(lambda c: [("QT", 4 * c + q) for q in range(4)]), kk=(lambda kt: [("KT", kt)]),
                             vk=(lambda kt: [("V", kt), "Vones"])))
        causal_attn_multi(jobs)
        for i in range(NT):
            b = nb(); pb = P[b][:, :].bitcast(BF16)
            for j in range(4):
                tp(pb[:, j * 128:(j + 1) * 128], om[:, i, j * 128:(j + 1) * 128], ident, r=[("om", i // 4), "ident"], w=[PK[b]])
            act(omT[:, :, i * 128:(i + 1) * 128], pb[:, 0:512].rearrange("p (k t) -> p k t", k=4), AF.Copy, r=[PK[b]], w=[("omT", i)])
        dbg_out("om", om.rearrange("p a b -> p (a b)"), [("om", c) for c in range(4)])
        if upto <= 4:
            return finish(nc, S, st)
        S.barrier()

        nbanks[0] = 4
        M.lo = lo0; M.hi = hi0
        qnT = M.lo_alloc([8, S_], BF16); ksT = M.lo_alloc([2, S_], BF16); kwT = M.lo_alloc([2, S_], BF16)
        vs = M.lo_alloc([NT, 2, 65], BF16); vw = M.lo_alloc([NT, 2, 65], BF16)
        gates = M.lo_alloc([NT, 3, 8], F32)
        kcmpT = M.lo_alloc([2, 128], BF16); VCX = M.lo_alloc([2, 97], BF16)
        PT = [M.lo_alloc([512], BF16) for _ in range(3)]
        rec4 = [M.lo_alloc([4], F32) for _ in range(2)]
        t1 = M.lo_alloc([512], F32); t2 = M.lo_alloc([512], F32)
        hss = M.lo_alloc([8], F32); hrs = M.lo_alloc([8], F32)
        ra = M.lo_alloc([128], F32); rbb = M.lo_alloc([128], F32)
        drb = [M.lo_alloc([512], BF16) for _ in range(2)]
        nqt = M.lo_alloc([64], F32); nkct = M.lo_alloc([64], F32); nkst = M.lo_alloc([64], F32); nkwt = M.lo_alloc([64], F32)
        lo2 = M.lo
        kc2 = M.hi_alloc([2, S_], BF16); vc2 = M.hi_alloc([2, S_], BF16)
        hi_kv = M.hi
        hT = M.hi_alloc([8, S_], BF16)
        Wqn = M.hi_alloc([8, 512], BF16); Wkc2 = M.hi_alloc([8, 256], BF16); Wvc2 = M.hi_alloc([8, 256], BF16)
        Wkv4 = M.hi_alloc([8, 512], BF16); Wgn = M.hi_alloc([8, 24], BF16)
        ge = M.hi_alloc([24], F32)
        S.dma("sp", hT.rearrange("p k t -> p (k t)"), hTs, r=["hTs"], w=["hTall"])
        for t_, d_ in ((nqt, nqB), (nkct, nkcB), (nkst, nksB), (nkwt, nkwB)):
            S.dma("sp", t_, d_, w=["ngain"])
        load_w(Wqn, w_qn, "Wqn", 8, 512); load_w(Wkv4, w_kv4, "Wkv4", 8, 512); load_w(Wgn, w_gn, "Wgn", 8, 24)
        load_w(Wkc2, w_kc2, "Wkc2", 8, 256); load_w(Wvc2, w_vc2, "Wvc2", 8, 256)
        S.op("pool", lambda e: e.memset(vs[:, :, :, 64:65], 1.0), w=["vsones"])
        S.op("pool", lambda e: e.memset(vw[:, :, :, 64:65], 1.0), w=["vwones"])
        S.op("pool", lambda e: e.memset(kc2[64:128, :, S_ - 1:S_], 0.0), w=["kc2pad"])
        S.op("pool", lambda e: e.memset(vc2[64:128, :, S_ - 1:S_], 0.0), w=["vc2pad"])
        MsubD = Mem(big, NB); MsubD.lo = lo_pers + 16384; MsubD.hi = lo0
        TDq = mk_tmps(MsubD, 512, 8, 8); TDs = mk_tmps(MsubD, 128, 2, 8); TDw = mk_tmps(MsubD, 128, 2, 8)
        dnq = [MsubD.lo_alloc([512], BF16) for _ in range(2)]
        dns = [MsubD.lo_alloc([128], BF16) for _ in range(2)]; dnw = [MsubD.lo_alloc([128], BF16) for _ in range(2)]

        def d_front(i):
            ts = slice(i * 128, (i + 1) * 128)
            bq = 4 + 2 * (i % 2)
            for k in range(8):
                mm(P[bq][:, :], hT[:, k, ts], Wqn[:, k, :], k == 0, k == 7, r=["hTall", "Wqn"], w=[PK[bq]])
            bk = 5 + 2 * (i % 2)
            for k in range(8):
                mm(P[bk][:, :], hT[:, k, ts], Wkv4[:, k, :], k == 0, k == 7, r=["hTall", "Wkv4"], w=[PK[bk]])
            bg = nb()
            for k in range(8):
                mm(P[bg][:, 0:24], hT[:, k, ts], Wgn[:, k, :], k == 0, k == 7, r=["hTall", "Wgn"], w=[PK[bg]])
            act(vs[:, i, :, 0:64], P[bk][:, 256:384].rearrange("p (g d) -> p g d", g=2), AF.Copy, r=[PK[bk]], w=[("vs", i)])
            act(vw[:, i, :, 0:64], P[bk][:, 384:512].rearrange("p (g d) -> p g d", g=2), AF.Copy, r=[PK[bk]], w=[("vw", i)])
            act(ge, P[bg][:, 0:24], AF.Exp, r=[PK[bg]], w=["ge"], scale=-1.0)
            S.op("dve", lambda e: e.tensor_scalar(out=ge, in0=ge, scalar1=1.0, scalar2=None, op0=ALU.add), r=["ge"], w=["ge"])
            S.op("dve", lambda e, i=i: e.reciprocal(out=gates[:, i].rearrange("p a b -> p (a b)"), in_=ge), r=["ge"], w=[("gates", i)])
            return bq, bk

        def d_back(i, bq, bk):
            ts = slice(i * 128, (i + 1) * 128)
            par = i % 2
            dq = dnq[par].rearrange("p (h d) -> p h d", h=8)
            ds_ = dns[par].rearrange("p (h d) -> p h d", h=2); dw_ = dnw[par].rearrange("p (h d) -> p h d", h=2)
            c1 = hnr_stages("dq", TDq, P[bq][:, :].rearrange("p (h d) -> p h d", h=8), [PK[bq]], 8, 64, nqt, "ngain", 0, 8, cosN[:, i, :], sinN[:, i, :], dq, ("dnq", par))
            c2 = hnr_stages("ds", TDs, P[bk][:, 0:128].rearrange("p (h d) -> p h d", h=2), [PK[bk]], 2, 64, nkst, "ngain", 0, 8, cosN[:, i, :], sinN[:, i, :], ds_, ("dns", par))
            c3 = hnr_stages("dw", TDw, P[bk][:, 128:256].rearrange("p (h d) -> p h d", h=2), [PK[bk]], 2, 64, nkwt, "ngain", 0, 8, cosN[:, i, :], sinN[:, i, :], dw_, ("dnw", par))
            run_interleaved([c1, c2, c3])
            b = nb(); pb = P[b][:, :].bitcast(BF16)
            for p_ in range(8):
                tp(pb[0:64, p_ * 128:(p_ + 1) * 128], dnq[par][:, p_ * 64:(p_ + 1) * 64], ident, r=[("dnq", par), "ident"], w=[PK[b]])
            act(qnT[0:64, :, ts], pb[0:64, :].rearrange("p (k t) -> p k t", k=8), AF.Copy, r=[PK[b]], w=[("qnT", i)])
            for (dd, dkey_, dstT, dk) in ((dns[par], ("dns", par), ksT, "ksT"), (dnw[par], ("dnw", par), kwT, "kwT")):
                b2 = nb(); pb = P[b2][:, :].bitcast(BF16)
                for g_ in range(2):
                    tp(pb[0:64, g_ * 128:(g_ + 1) * 128], dd[:, g_ * 64:(g_ + 1) * 64], ident, r=[dkey_, "ident"], w=[PK[b2]])
                act(dstT[0:64, :, ts], pb[0:64, 0:256].rearrange("p (g t) -> p g t", g=2), AF.Copy, r=[PK[b2]], w=[(dk, i)])

        fb_ = d_front(0)
        for i in range(NT):
            cur_ = fb_
            if i + 1 < NT:
                fb_ = d_front(i + 1)
            d_back(i, *cur_)
        for c in range(4):
            for (Wt, wk, dst, dk) in ((Wkc2, "Wkc2", kc2, "kc2"), (Wvc2, "Wvc2", vc2, "vc2")):
                for g in range(2):
                    b = nb()
                    for k in range(8):
                        mm(P[b][:, :], Wt[:, k, g * 128:(g + 1) * 128], hT[:, k, c * 512:(c + 1) * 512], k == 0, k == 7, r=["hTall", wk], w=[PK[b]])
                    act(dst[0:64, g, c * 512:(c + 1) * 512], P[b][0:64, :], AF.Copy, r=[PK[b]], w=[dk])
                    if c == 0:
                        act(dst[64:128, g, 0:511], P[b][64:128, 1:512], AF.Copy, r=[PK[b]], w=[dk])
                    else:
                        act(dst[64:128, g, c * 512 - 1:(c + 1) * 512 - 1], P[b][64:128, :], AF.Copy, r=[PK[b]], w=[dk])
        dbg_out("qnT", qnT[0:64].rearrange("p k t -> p (k t)"), [("qnT", i) for i in range(NT)])
        dbg_out("ksT", ksT[0:64].rearrange("p k t -> p (k t)"), [("ksT", i) for i in range(NT)])
        dbg_out("gates", gates.rearrange("p a b c -> p (a b c)"), [("gates", i) for i in range(NT)])
        dbg_out("kc2", kc2.rearrange("p k t -> p (k t)"), ["kc2", "kc2pad"])
        if upto <= 5:
            return finish(nc, S, st)
        S.barrier()

        nbanks[0] = 8
        M.hi = hi_kv
        hiE = M.hi
        W1k = M.lo_alloc([16, 256], BF16); W1v = M.lo_alloc([16, 256], BF16)
        W2k = M.lo_alloc([2, 64], BF16); W2v = M.lo_alloc([2, 64], BF16)
        pkf = M.lo_alloc([16], F32); pvf = M.lo_alloc([16], F32); pkb = M.lo_alloc([16], BF16); pvb = M.lo_alloc([16], BF16)
        biask = M.lo_alloc([2], F32); biasv = M.lo_alloc([2], F32)
        hid = [M.lo_alloc([128], BF16) for _ in range(2)]
        ovl = M.lo_alloc([32], BF16)
        load_w(W1k, w1k, "W1k", 16, 256); load_w(W1v, w1v, "W1v", 16, 256)
        load_w(W2k, w2k, "W2k", 2, 64); load_w(W2v, w2v, "W2v", 2, 64)
        S.dma("sp", pkf, posk, w=["pkf"]); S.dma("sp", pvf, posv, w=["pvf"]); S.dma("sp", ovl[0:127], ovl_d, w=["ovl"])
        S.op("pool", lambda e: e.memset(VCX[0:127, :, 64:65], 1.0), w=["VCXa"])
        for g in range(2):
            S.op("pool", lambda e, g=g: e.tensor_copy(out=VCX[0:127, g, 65:97], in_=ovl[0:127]), r=["ovl"], w=["VCXb"])
        rt = [M.lo_alloc([128], BF16) for _ in range(3)]
        rt_i = [0]
        for (W1, w1key, W2, w2key, src, skey, posf_, pkey, isk) in ((W1k, "W1k", W2k, "W2k", kc2, ["kc2", "kc2pad"], pkf, "pkf", True),
                                                                    (W1v, "W1v", W2v, "W2v", vc2, ["vc2", "vc2pad"], pvf, "pvf", False)):
            srcv = src.rearrange("p g (n s) -> p g n s", s=16)
            bo = nb()
            for g in range(2):
                bh = []
                for hc in range(2):
                    b = nb()
                    while b == bo or b in bh:
                        b = nb()
                    bh.append(b)
                for lc in range(16):
                    ri = rt_i[0]; rt_i[0] = (ri + 1) % 3
                    rtv = rt[ri][:, 0:127]
                    S.op("dve", lambda e, rtv=rtv, g=g, lc=lc, srcv=srcv, posf_=posf_: e.tensor_scalar(
                        out=rtv, in0=srcv[:, g, (2 * lc) // 16:(2 * lc) // 16 + 127, (2 * lc) % 16], scalar1=posf_[:, lc:lc + 1], scalar2=None, op0=ALU.add),
                        r=skey + [pkey], w=[("rt", ri)])
                    for hc in range(2):
                        mm(P[bh[hc]][:, 0:127], W1[:, lc, hc * 128:(hc + 1) * 128], rtv, lc == 0, lc == 15, r=[w1key, ("rt", ri)], w=[PK[bh[hc]]])
                for hc in range(2):
                    act(hid[hc][:, 0:127], P[bh[hc]][:, 0:127], AF.Silu, r=[PK[bh[hc]]], w=[("hid", hc)])
                for hc in range(2):
                    mm(P[bo][0:127, g * 64:(g + 1) * 64], hid[hc][:, 0:127], W2[:, hc, :], hc == 0, hc == 1, r=[("hid", hc), w2key], w=[PK[bo]])
            if isk:
                d = drb[1][0:127, 0:128].rearrange("p (h d) -> p h d", h=2)
                head_norm_rope(P[bo][0:127, 0:128].rearrange("p (h d) -> p h d", h=2), [PK[bo]], 2, 64, nkct, "ngain", 0, 8, cosC[0:127], sinC[0:127], d, "drb1", np_=127)
                b2 = nb(); pb = P[b2][:, :].bitcast(BF16)
                for g_ in range(2):
                    tp(pb[0:64, g_ * 128:g_ * 128 + 127], drb[1][0:127, g_ * 64:(g_ + 1) * 64], ident[0:127, 0:127], r=["drb1", "ident"], w=[PK[b2]])
                act(kcmpT[0:64, :, 0:127], pb[0:64, 0:256].rearrange("p (g t) -> p g t", g=2)[:, :, 0:127], AF.Copy, r=[PK[b2]], w=["kcmpT"])
            else:
                act(VCX[0:127, :, 0:64], P[bo][0:127, 0:128].rearrange("p (g d) -> p g d", g=2), AF.Copy, r=[PK[bo]], w=["VCXc"])
        dbg_out("kcmpT", kcmpT[0:64].rearrange("p g t -> p (g t)"), ["kcmpT"])
        dbg_out("VCX", VCX[0:127].rearrange("p a b -> p (a b)"), ["VCXa", "VCXb", "VCXc"])
        if upto <= 6:
            return finish(nc, S, st)
        S.barrier()

        M.lo = lo2; M.hi = hi0
        onsa = M.hi_alloc([NT, 512], F32)
        mbT = M.hi_alloc([2, S_], BF16)
        vmT = M.hi_alloc([S_], BF16); XE = M.hi_alloc([NT, 128], BF16)
        fb = M.hi_alloc([NT, 32], F32); vj = M.hi_alloc([NT, 32], F32)
        pc = [M.lo_alloc([4, 128], BF16) for _ in range(2)]
        rsum = M.lo_alloc([8], F32); rec8 = M.lo_alloc([8], F32); gr = M.lo_alloc([8], F32)
        tmp_i = M.lo_alloc([8, 32], F32); imp = M.lo_alloc([2, 32], F32); m8 = M.lo_alloc([2, 8], F32)
        sel = M.lo_alloc([2, 32], F32); mbf = M.lo_alloc([2, 32], BF16)
        tmpo = M.lo_alloc([8, 64], F32)
        pw = [M.lo_alloc([3, 128], BF16) for _ in range(3)]
        onb = [M.lo_alloc([512], BF16) for _ in range(2)]
        S.dma("sp", vmT[0:127], vmT_d, w=["vmT"]); S.dma("sp", XE[0:32].rearrange("p a b -> p (a b)"), XE_d, w=["XE"])
        S.dma("sp", fb.rearrange("p a b -> p (a b)"), fb_d, w=["fb"]); S.dma("sp", vj.rearrange("p a b -> p (a b)"), vj_d, w=["vj"])
        VCXk = ["VCXa", "VCXb", "VCXc"]
        for i in range(NT):
            ts = slice(i * 128, (i + 1) * 128)
            import os
            if int(os.environ.get('KDEV_F', '9')) <= 0:
                continue
            sb_ = [nb(), nb()]
            ob = [nb(), nb()]
            for p in range(8):
                j, g = p // 2, p % 2
                if os.environ.get('KDEV_G0'):
                    g = 0
                mm(P[sb_[p // 4]][0:127, (p % 4) * 128:(p % 4 + 1) * 128], kcmpT[0:64, g, 0:127], qnT[0:64, p, ts], True, True,
                   r=["kcmpT", ("qnT", i)], w=[PK[sb_[p // 4]]])
            for hf_ in range(2):
                pcv = pc[hf_]
                act(pcv[0:127], P[sb_[hf_]][0:127, :].rearrange("p (a b) -> p a b", a=4), AF.Exp, r=[PK[sb_[hf_]]], w=[("pc", hf_)], scale=0.125)
                S.op("dve", lambda e, pcv=pcv: e.tensor_tensor(out=pcv[0:127], in0=pcv[0:127], in1=vmT[0:127, ts].unsqueeze(1).to_broadcast([127, 4, 128]), op=ALU.mult),
                     r=[("pc", hf_), "vmT"], w=[("pc", hf_)])
            import os
            FL = int(os.environ.get('KDEV_F', '9'))
            if FL <= 1:
                continue
            for p in range(8):
                g = p % 2
                mm(P[ob[p // 4]][:, (p % 4) * 97:(p % 4 + 1) * 97], pc[p // 4][0:127, p % 4, :], VCX[0:127, g, :], True, True,
                   r=[("pc", p // 4)] + VCXk, w=[PK[ob[p // 4]]])
            if FL <= 2:
                continue
            OC = [P[ob[h_]][:, 0:388].rearrange("p (a b) -> p a b", a=4) for h_ in range(2)]
            for h_ in range(2):
                S.op("dve", lambda e, h_=h_: e.tensor_scalar(out=rsum[:, h_ * 4:(h_ + 1) * 4], in0=OC[h_][:, :, 64], scalar1=1e-30, scalar2=None, op0=ALU.max), r=[PK[ob[h_]]], w=["rsum"])
            S.op("dve", lambda e: e.reciprocal(out=rec8, in_=rsum), r=["rsum"], w=["rec8"])
            S.op("dve", lambda e, i=i: e.tensor_tensor(out=gr, in0=gates[:, i, 0, :], in1=rec8, op=ALU.mult), r=["rec8", ("gates", i)], w=["gr"])
            for h_ in range(2):
                S.op("dve", lambda e, h_=h_, i=i: e.tensor_tensor(out=onsa[:, i, h_ * 256:(h_ + 1) * 256].rearrange("p (a b) -> p a b", a=4), in0=OC[h_][:, :, 0:64],
                                                                   in1=gr[:, h_ * 4:(h_ + 1) * 4].unsqueeze(2).to_broadcast([128, 4, 64]), op=ALU.mult),
                     r=[PK[ob[h_]], "gr"], w=[("onsa", i)])
                S.op("dve", lambda e, h_=h_: e.tensor_tensor(out=tmp_i[:, h_ * 4:(h_ + 1) * 4, :], in0=OC[h_][:, :, 65:97],
                                                             in1=rec8[:, h_ * 4:(h_ + 1) * 4].unsqueeze(2).to_broadcast([128, 4, 32]), op=ALU.mult),
                     r=[PK[ob[h_]], "rec8"], w=["tmp_i"])
            if FL <= 3:
                continue
            S.op("dve", lambda e: e.tensor_reduce(out=imp, in_=tmp_i.rearrange("t (j g) n -> t g n j", g=2), axis=AX.X, op=ALU.add), r=["tmp_i"], w=["imp"])
            S.op("dve", lambda e, i=i: e.tensor_tensor(out=imp, in0=imp, in1=fb[:, i, :].unsqueeze(1).to_broadcast([128, 2, 32]), op=ALU.add), r=["imp", "fb"], w=["imp"])
            if FL <= 4:
                continue
            for g in range(2):
                S.op("dve", lambda e, g=g: e.max(out=m8[:, g, :], in_=imp[:, g, :]), r=["imp"], w=["m8"])
                S.op("dve", lambda e, g=g: e.tensor_scalar(out=sel[:, g, :], in0=imp[:, g, :], scalar1=m8[:, g, 7:8], scalar2=None, op0=ALU.is_ge), r=["imp", "m8"], w=["sel"])
            S.op("dve", lambda e, i=i: e.tensor_tensor(out=sel, in0=sel, in1=vj[:, i, :].unsqueeze(1).to_broadcast([128, 2, 32]), op=ALU.mult), r=["sel", "vj"], w=["sel"])
            S.op("dve", lambda e: e.tensor_scalar(out=mbf, in0=sel, scalar1=-1.0, scalar2=30000.0, op0=ALU.add, op1=ALU.mult), r=["sel"], w=["mbf"])
            if i == 5:
                dbg_out("sel5", sel.rearrange("p a b -> p (a b)"), ["sel"])
                dbg_out("imp5", imp.rearrange("p a b -> p (a b)"), ["imp"])
            if FL <= 5:
                continue
            b = nb(); pb = P[b][:, :].bitcast(BF16)
            for g in range(2):
                tp(pb[0:32, g * 128:(g + 1) * 128], mbf[:, g, :], ident, r=["mbf", "ident"], w=[PK[b]])
            act(mbT[0:32, :, ts], pb[0:32, 0:256].rearrange("p (g t) -> p g t", g=2), AF.Copy, r=[PK[b]], w=[("mbT", i)])
        dbg_out("onsa_c", onsa.rearrange("p a b -> p (a b)"), [("onsa", i) for i in range(NT)])
        dbg_out("mbT", mbT[0:32].rearrange("p a b -> p (a b)"), [("mbT", i) for i in range(NT)])
        if upto <= 7:
            return finish(nc, S, st)

        S.barrier()
        nbanks[0] = 6
        jobs = []
        for p in range(8):
            j, g = p // 2, p % 2

            def extra(ps_ap, kt, a, b_, pk, g=g):
                mm(ps_ap, XE[0:32, kt, :], mbT[0:32, g, a:b_], False, True, r=["XE"] + [("mbT", q) for q in range(a // 128, b_ // 128)], w=[pk])

            def fin(c, Oacc, pk, p=p):
                rc = rec4[c % 2]
                S.op("dve", lambda e: e.reciprocal(out=rc, in_=Oacc[:, :, 64]), r=[pk], w=[("rec4", c % 2)])
                S.op("dve", lambda e: e.tensor_tensor(out=rc, in0=rc, in1=gates[:, 4 * c:4 * c + 4, 1, p], op=ALU.mult), r=[("rec4", c % 2)] + [("gates", 4 * c + q) for q in range(4)], w=[("rec4", c % 2)])
                tv = tmpo[:, 0:4, :]
                S.op("dve", lambda e: e.tensor_tensor(out=tv, in0=Oacc[:, :, 0:64], in1=rc.unsqueeze(2).to_broadcast([128, 4, 64]), op=ALU.mult), r=[pk, ("rec4", c % 2)], w=["tmpo"])
                S.op("pool", lambda e: e.tensor_tensor(out=onsa[:, 4 * c:4 * c + 4, p * 64:(p + 1) * 64], in0=onsa[:, 4 * c:4 * c + 4, p * 64:(p + 1) * 64], in1=tv, op=ALU.add),
                     r=["tmpo"] + [("onsa", 4 * c + q) for q in range(4)], w=[("onsa", 4 * c + q) for q in range(4)])
            jobs.append(dict(KT=ksT[:, g, :], kn=64, QT=(lambda a, b_, p=p: qnT[0:64, p, a:b_]), V=(lambda kt, g=g: vs[:, kt, g, :]), scale=0.125,
                             extra=extra, fin=fin, qk=(lambda c: [("qnT", 4 * c + q) for q in range(4)]), kk=(lambda kt: [("ksT", kt)]),
                             vk=(lambda kt: [("vs", kt), "vsones"])))
        causal_attn_multi(jobs)
        dbg_out("onsa_cs", onsa.rearrange("p a b -> p (a b)"), [("onsa", i) for i in range(NT)])
        if upto <= 8:
            return finish(nc, S, st)

        pw_i = [0]
        ob = [6, 7]

        def emit_ws(i, p):
            g = p % 2
            kts = [kt for kt in (i - 2, i - 1, i) if kt >= 0]
            b = nb()
            for kt in kts:
                sl = kt - (i - 2)
                mm(P[b][:, sl * 128:(sl + 1) * 128], kwT[0:64, g, kt * 128:(kt + 1) * 128], qnT[0:64, p, i * 128:(i + 1) * 128], True, True,
                   r=[("kwT", kt), ("qnT", i)], w=[PK[b]])
            return b

        wsteps = [(i, p) for i in range(NT) for p in range(8)]
        wpend = emit_ws(*wsteps[0])
        for wi_, (i, p) in enumerate(wsteps):
            ts = slice(i * 128, (i + 1) * 128)
            g = p % 2
            kts = [kt for kt in (i - 2, i - 1, i) if kt >= 0]
            b = wpend
            if wi_ + 1 < len(wsteps):
                wpend = emit_ws(*wsteps[wi_ + 1])
            s0 = kts[0] - (i - 2)
            wi = pw_i[0]; pw_i[0] = (wi + 1) % 3
            pwv = pw[wi]
            act(pwv[:, s0:3, :], P[b][:, s0 * 128:384].rearrange("p (a b) -> p a b", b=128), AF.Exp, r=[PK[b]], w=[("pw", wi)], scale=0.125)
            S.op("dve", lambda e, pwv=pwv, s0=s0: e.tensor_tensor(out=pwv[:, s0:3, :], in0=pwv[:, s0:3, :], in1=winm[:, s0:3, :], op=ALU.mult), r=[("pw", wi), "winm"], w=[("pw", wi)])
            for kt in kts:
                sl = kt - (i - 2)
                mm(P[ob[p // 4]][:, (p % 4) * 65:(p % 4 + 1) * 65], pwv[:, sl, :], vw[:, kt, g, :], kt == kts[0], kt == kts[-1],
                   r=[("pw", wi), ("vw", kt), "vwones"], w=[PK[ob[p // 4]]])
            if p != 7:
                continue
            OW = [P[ob[h_]][:, 0:260].rearrange("p (a b) -> p a b", a=4) for h_ in range(2)]
            for h_ in range(2):
                S.op("dve", lambda e, h_=h_: e.reciprocal(out=rec8[:, h_ * 4:(h_ + 1) * 4], in_=OW[h_][:, :, 64]), r=[PK[ob[h_]]], w=["rec8"])
            S.op("dve", lambda e, i=i: e.tensor_tensor(out=gr, in0=gates[:, i, 2, :], in1=rec8, op=ALU.mult), r=["rec8", ("gates", i)], w=["gr"])
            for h_ in range(2):
                S.op("dve", lambda e, h_=h_: e.tensor_tensor(out=tmpo[:, h_ * 4:(h_ + 1) * 4, :], in0=OW[h_][:, :, 0:64],
                                                             in1=gr[:, h_ * 4:(h_ + 1) * 4].unsqueeze(2).to_broadcast([128, 4, 64]), op=ALU.mult), r=[PK[ob[h_]], "gr"], w=["tmpo"])
            S.op("pool", lambda e, i=i: e.tensor_tensor(out=onb[i % 2], in0=onsa[:, i, :], in1=tmpo.rearrange("p a b -> p (a b)"), op=ALU.add), r=["tmpo", ("onsa", i)], w=[("onb", i % 2)])
            if "onsa_all" in D_:
                S.dma("sp", D_["onsa_all"][:, i * 512:(i + 1) * 512], onb[i % 2], r=[("onb", i % 2)])
            b = nb(); pb = P[b][:, :].bitcast(BF16)
            for j in range(4):
                tp(pb[:, j * 128:(j + 1) * 128], onb[i % 2][:, j * 128:(j + 1) * 128], ident, r=[("onb", i % 2), "ident"], w=[PK[b]])
            act(onT[:, :, ts], pb[:, 0:512].rearrange("p (k t) -> p k t", k=4), AF.Copy, r=[PK[b]], w=[("onT", i)])
        if upto <= 9:
            return finish(nc, S, st)
        S.barrier()

        nbanks[0] = 8
        M.lo = lo0; M.hi = hi0
        hT = M.hi_alloc([8, S_], BF16)
        mergedT = M.hi_alloc([8, S_], BF16)
        hi2 = M.hi
        Wgm = M.lo_alloc([8, 512], BF16); Wgnm = M.lo_alloc([8, 512], BF16); Wom = M.lo_alloc([4, 512], BF16); Won = M.lo_alloc([4, 512], BF16)
        e3 = M.lo_alloc([512], F32); e4 = M.lo_alloc([512], F32); tA = M.lo_alloc([512], F32); tB = M.lo_alloc([512], F32)
        mgb = [M.lo_alloc([512], BF16) for _ in range(2)]
        S.dma("sp", hT.rearrange("p k t -> p (k t)"), hTs, r=["hTs"], w=["hTall"])
        e3 = [e3, M.lo_alloc([512], F32)]; e4 = [e4, M.lo_alloc([512], F32)]
        tA = [tA, M.lo_alloc([512], F32)]; tB = [tB, M.lo_alloc([512], F32)]

        def i_front(cc, i, it):
            ts = slice(i * 128, (i + 1) * 128)
            bs = [4 * (it % 2) + q for q in range(4)]
            b1, b2, b3, b4 = bs
            for k in range(8):
                mm(P[b3][:, :], hT[:, k, ts], Wgm[:, k, :], k == 0, k == 7, r=["hTall", "Wgm"], w=[PK[b3]])
            for k in range(8):
                mm(P[b4][:, :], hT[:, k, ts], Wgnm[:, k, :], k == 0, k == 7, r=["hTall", "Wgnm"], w=[PK[b4]])
            for k in range(4):
                mm(P[b1][:, :], omT[:, k, ts], Wom[:, k, :], k == 0, k == 3, r=[("omT", i), "Wom"], w=[PK[b1]])
            for k in range(4):
                mm(P[b2][:, :], onT[:, k, ts], Won[:, k, :], k == 0, k == 3, r=[("onT", i), "Won"], w=[PK[b2]])
            return bs

        def i_back(cc, i, it, bs):
            ts = slice(i * 128, (i + 1) * 128)
            b1, b2, b3, b4 = bs
            par = it % 2
            act(e3[par], P[b3][:, :], AF.Sigmoid, r=[PK[b3]], w=[("e3", par)])
            act(e4[par], P[b4][:, :], AF.Sigmoid, r=[PK[b4]], w=[("e4", par)])
            S.op("dve", lambda e: e.tensor_tensor(out=tA[par], in0=P[b1][:, :], in1=e3[par], op=ALU.mult), r=[PK[b1], ("e3", par)], w=[("tA", par)])
            S.op("dve", lambda e: e.tensor_tensor(out=tB[par], in0=P[b2][:, :], in1=e4[par], op=ALU.mult), r=[PK[b2], ("e4", par)], w=[("tB", par)])
            S.op("dve", lambda e: e.tensor_tensor(out=mgb[par], in0=tA[par], in1=tB[par], op=ALU.add), r=[("tA", par), ("tB", par)], w=[("mgb", par)])
            if "merged" in D_:
                S.dma("sp", D_["merged"][i * 128:(i + 1) * 128, cc * 512:(cc + 1) * 512], mgb[par], r=[("mgb", par)])
            pb = P[b3][:, :].bitcast(BF16)
            for j in range(4):
                tp(pb[:, j * 128:(j + 1) * 128], mgb[par][:, j * 128:(j + 1) * 128], ident, r=[("mgb", par), "ident"], w=[PK[b3]])
            act(mergedT[:, cc * 4:(cc + 1) * 4, ts], pb[:, 0:512].rearrange("p (k t) -> p k t", k=4), AF.Copy, r=[PK[b3]], w=[("mergedT", i)])

        it = 0
        for cc in range(2):
            load_w_cols(Wgm, w_gm, "Wgm", 8, cc * 512, (cc + 1) * 512); load_w_cols(Wgnm, w_gnm, "Wgnm", 8, cc * 512, (cc + 1) * 512)
            load_w_cols(Wom, wo_mla, "Wom", 4, cc * 512, (cc + 1) * 512); load_w_cols(Won, wo_nsa, "Won", 4, cc * 512, (cc + 1) * 512)
            pend_ = i_front(cc, 0, it)
            for i in range(NT):
                cur_ = pend_
                if i + 1 < NT:
                    pend_ = i_front(cc, i + 1, it + 1)
                i_back(cc, i, it, cur_)
                it += 1
        if upto <= 10:
            return finish(nc, S, st)
        S.barrier()

        M.lo = lo0
        h2T = hT
        Wout = M.lo_alloc([8, D], BF16)
        G1 = M.lo_alloc([D], F32); A2 = M.lo_alloc([D], F32); B2 = M.lo_alloc([D], F32)
        xb = [M.lo_alloc([D], F32) for _ in range(2)]
        x1t = [M.lo_alloc([D], F32) for _ in range(2)]
        junk = M.lo_alloc([D], F32); tmpA = M.lo_alloc([D], F32)
        hb = [M.lo_alloc([D], BF16) for _ in range(2)]
        ssq = M.lo_alloc([NT], F32); rs = M.lo_alloc([NT], F32)
        S.dma("sp", G1, mods[:, 2 * D:3 * D], r=["mods"], w=["G1"])
        S.dma("sp", B2, mods[:, 3 * D:4 * D], r=["mods"], w=["AB2"])
        S.dma("sp", A2, mods[:, 4 * D:5 * D], r=["mods"], w=["AB2"])
        load_w(Wout, w_out, "Wout", 8, D, ceng="pool")
        tmpA2 = [tmpA, M.lo_alloc([D], F32)]
        tmpJ = [M.lo_alloc([D], F32) for _ in range(2)]

        def j_front(i):
            ts = slice(i * 128, (i + 1) * 128)
            par = i % 2
            S.dma("sp", xb[par], x[ts, :], w=[("xb", par)])
            for cc in range(2):
                b = 2 * par + cc
                for k in range(8):
                    mm(P[b][:, :], mergedT[:, k, ts], Wout[:, k, cc * 512:(cc + 1) * 512], k == 0, k == 7, r=[("mergedT", i), "Wout"], w=[PK[b]])

        def j_mid(i):
            ts = slice(i * 128, (i + 1) * 128)
            par = i % 2
            for cc in range(2):
                b = 2 * par + cc
                S.op("dve", lambda e, b=b, cc=cc: e.tensor_tensor(out=tmpJ[par][:, cc * 512:(cc + 1) * 512], in0=P[b][:, :], in1=G1[:, cc * 512:(cc + 1) * 512], op=ALU.mult), r=[PK[b], "G1"], w=[("tmpJ", par)])
                S.op("pool", lambda e, cc=cc: e.tensor_tensor(out=x1t[par][:, cc * 512:(cc + 1) * 512], in0=tmpJ[par][:, cc * 512:(cc + 1) * 512], in1=xb[par][:, cc * 512:(cc + 1) * 512], op=ALU.add),
                     r=[("tmpJ", par), ("xb", par)], w=[("x1t", par)])
            S.dma("sp", x1s[ts, :], x1t[par], r=[("x1t", par)], w=[("x1s", i)])
            norm_a(i, x1t[par], ("x1t", par))

        bank_state[0] = 4

        def nbJ():
            b = 4 + (bank_state[0] % 4)
            bank_state[0] = (bank_state[0] + 1) % 4
            return b
        nb_saved = nb
        nb = nbJ
        j_front(0); j_mid(0)
        for i in range(NT):
            if i + 1 < NT:
                j_front(i + 1); j_mid(i + 1)
            norm_b(i, x1t[i % 2], ("x1t", i % 2), A2, B2, ["AB2"], h2T, "h2T")
        nb = nb_saved
        bank_state[0] = 0
        if "x1" in D_:
            S.dma("sp", D_["x1"], x1s, r=[("x1s", i) for i in range(NT)])
        if upto <= 11:
            return finish(nc, S, st)
        S.barrier()

        M.lo = lo_pers; M.hi = hi2 + 8 * S_ * 2
        Wd = M.hi_alloc([NFC, D], BF16)
        actT = M.hi_alloc([NFC, 1024], BF16)
        G2 = M.lo_alloc([D], F32)
        Wg2 = [M.lo_alloc([8, 256], BF16) for _ in range(2)]; Wu2 = [M.lo_alloc([8, 256], BF16) for _ in range(2)]
        sg = [M.lo_alloc([512], F32) for _ in range(2)]
        xb = [M.lo_alloc([D], F32) for _ in range(2)]
        ot = [M.lo_alloc([D], F32) for _ in range(2)]
        tmpA = M.lo_alloc([D], F32)
        S.dma("sp", G2, mods[:, 5 * D:6 * D], r=["mods"], w=["G2"])
        load_w(Wd, wd, "Wd", NFC, D)
        h2k = [("h2T", i) for i in range(NT)]
        out_toks = []
        for half in range(2):
            def ld(jg_):
                wb_ = jg_ % 2
                load_w_cols(Wg2[wb_], wg, ("Wg2", wb_), 8, jg_ * 256, (jg_ + 1) * 256)
                load_w_cols(Wu2[wb_], wu, ("Wu2", wb_), 8, jg_ * 256, (jg_ + 1) * 256)
            ld(0)
            for jg in range(NFC // 2):
                wb = jg % 2
                if jg + 1 < NFC // 2:
                    ld(jg + 1)
                for jj in range(2):
                    j = jg * 2 + jj
                    for tc in range(2):
                        t0 = half * 1024 + tc * 512
                        bg, bu = nb(), nb()
                        for k in range(8):
                            mm(P[bg][:, :], Wg2[wb][:, k, jj * 128:(jj + 1) * 128], h2T[:, k, t0:t0 + 512], k == 0, k == 7, r=[("Wg2", wb)] + h2k, w=[PK[bg]])
                        for k in range(8):
                            mm(P[bu][:, :], Wu2[wb][:, k, jj * 128:(jj + 1) * 128], h2T[:, k, t0:t0 + 512], k == 0, k == 7, r=[("Wu2", wb)] + h2k, w=[PK[bu]])
                        act(sg[tc], P[bg][:, :], AF.Silu, r=[PK[bg]], w=[("sg", tc)])
                        S.op("dve", lambda e, j=j, tc=tc, bu=bu: e.tensor_tensor(out=actT[:, j, tc * 512:(tc + 1) * 512], in0=P[bu][:, :], in1=sg[tc], op=ALU.mult),
                             r=[PK[bu], ("sg", tc)], w=[("actT", tc)])
            for il in range(8):
                i = half * 8 + il
                ts = slice(i * 128, (i + 1) * 128)
                S.dma("sp", xb[i % 2], x1s[ts, :], r=[("x1s", i)], w=[("xb", i % 2)])
                for cc in range(2):
                    b = nb()
                    for j in range(NFC):
                        mm(P[b][:, :], actT[:, j, il * 128:(il + 1) * 128], Wd[:, j, cc * 512:(cc + 1) * 512], j == 0, j == NFC - 1, r=[("actT", il // 4), "Wd"], w=[PK[b]])
                    S.op("dve", lambda e, b=b, cc=cc: e.tensor_tensor(out=tmpA[:, cc * 512:(cc + 1) * 512], in0=P[b][:, :], in1=G2[:, cc * 512:(cc + 1) * 512], op=ALU.mult), r=[PK[b], "G2"], w=["tmpA"])
                    S.op("pool", lambda e, i=i, cc=cc: e.tensor_tensor(out=ot[i % 2][:, cc * 512:(cc + 1) * 512], in0=tmpA[:, cc * 512:(cc + 1) * 512], in1=xb[i % 2][:, cc * 512:(cc + 1) * 512], op=ALU.add),
                         r=["tmpA", ("xb", i % 2)], w=[("ot", i % 2)])
                S.dma("sp", out[ts, :], ot[i % 2], r=[("ot", i % 2)], w=[("out", i)])
        return finish(nc, S, st)


def finish(nc, S, st):
    for q in S.dsem:
        for i in range(len(S.dsem[q])):
            if S.dcnt[q][i]:
                S._wait("sp", (("d", q, i), S.dcnt[q][i]))
    for e2 in ("pe", "act", "dve", "pool"):
        if S.cnt[e2]:
            S._wait("sp", (e2, S.cnt[e2]))
    st.close()
    return nc


def _consts():
    bf = ml_dtypes.bfloat16
    c = {}
    c["ident"] = np.eye(128, dtype=np.float32).astype(bf)
    a = np.arange(128)
    c["tri"] = (a[:, None] <= a[None, :]).astype(np.float32).astype(bf)
    w = np.zeros((128, 3, 128), np.float32)
    w[:, 0, :] = (a[:, None] > a[None, :])
    w[:, 1, :] = 1.0
    w[:, 2, :] = (a[:, None] <= a[None, :])
    c["winm"] = w.reshape(128, 384).astype(bf)
    inv16 = (np.float32(500000.0) ** (-np.arange(0, 32, 2, dtype=np.float32) / np.float32(32))).astype(np.float32)
    inv8 = (np.float32(500000.0) ** (-np.arange(0, 16, 2, dtype=np.float32) / np.float32(16))).astype(np.float32)
    c["inv16"] = np.tile(inv16[None], (128, 1)).astype(np.float32)
    c["inv8"] = np.tile(inv8[None], (128, 1)).astype(np.float32)
    n = np.arange(127)
    starts = n * 16
    j = np.arange(32)
    ovl = ((starts[:, None] < j[None, :] * 64 + 64) & (starts[:, None] + 32 > j[None, :] * 64))
    c["ovl"] = ovl.astype(np.float32).astype(bf)
    t = np.arange(S_)
    c["vmT"] = ((starts[:, None] + 31) <= t[None, :]).astype(np.float32).astype(bf)
    XE = np.zeros((32, NT, 128), np.float32)
    for kt in range(NT):
        XE[2 * kt, kt, 0:64] = 1.0
        XE[2 * kt + 1, kt, 64:128] = 1.0
    c["XE"] = XE.reshape(32, NT * 128).astype(bf)
    cur = (t // 64)
    forced = (j[None, :] == 0) | (j[None, :] == cur[:, None]) | (j[None, :] == cur[:, None] - 1)
    valid = j[None, :] <= cur[:, None]
    fb = np.where(valid, np.where(forced, 1e4, 0.0), -1e30).astype(np.float32)
    c["fb"] = fb.reshape(NT, 128, 32).transpose(1, 0, 2).reshape(128, NT * 32).copy()
    c["vj"] = valid.astype(np.float32).reshape(NT, 128, 32).transpose(1, 0, 2).reshape(128, NT * 32).copy()
    return c


def _rep(v, n=128):
    return np.ascontiguousarray(np.broadcast_to(np.asarray(v, np.float32)[None, :], (n, v.shape[0])))


def prep_inputs(inp):
    f = lambda a: np.ascontiguousarray(np.asarray(a, dtype=np.float32))
    w_in = f(inp["w_in"][0])
    o = np.cumsum([0, 768, 256, 32, 512, 128, 128, 128, 128, 128, 128, 24, 1024, 1024])
    seg = lambda i: w_in[:, o[i]:o[i + 1]]
    shared = {}
    shared["ada_w"] = f(inp["ada_w"][0]); shared["adabB"] = _rep(f(inp["ada_b"][0]))
    shared["g1B"] = _rep(f(inp["norm1_gain"][0])); shared["g2B"] = _rep(f(inp["norm2_gain"][0]))
    shared["w_cq"] = f(seg(0)); shared["w_ckv"] = f(seg(1)); shared["w_kpe"] = f(seg(2))
    qn = seg(3).reshape(D, 8, 64)
    shared["w_qn"] = f(qn[:, PH, :].reshape(D, 512))
    kc = seg(4).reshape(D, 2, 64); vc = seg(5).reshape(D, 2, 64)
    shared["w_kc2"] = f(np.stack([kc[:, 0], kc[:, 0], kc[:, 1], kc[:, 1]], 1).reshape(D, 256))
    shared["w_vc2"] = f(np.stack([vc[:, 0], vc[:, 0], vc[:, 1], vc[:, 1]], 1).reshape(D, 256))
    shared["w_kv4"] = f(np.concatenate([seg(6), seg(8), seg(7), seg(9)], 1))
    gn = seg(10).reshape(D, 8, 3)
    shared["w_gn"] = f(gn[:, PH, :].transpose(0, 2, 1).reshape(D, 24))
    shared["w_gm"] = f(seg(11)); shared["w_gnm"] = f(seg(12))
    shared["qag"] = f(f(inp["mla_q_a_gain"][0]).reshape(6, 128).T); shared["kvag"] = f(f(inp["mla_kv_a_gain"][0]).reshape(2, 128).T)
    shared["w_qb"] = f(inp["mla_w_q_b"][0]); shared["w_kvb"] = f(inp["mla_w_kv_b"][0])
    shared["qgB"] = _rep(f(inp["mla_q_gain"][0])); shared["kgB"] = _rep(f(inp["mla_k_gain"][0]))
    shared["nqB"] = _rep(f(inp["nsa_q_gain"][0])); shared["nkcB"] = _rep(f(inp["nsa_kc_gain"][0]))
    shared["nksB"] = _rep(f(inp["nsa_ks_gain"][0])); shared["nkwB"] = _rep(f(inp["nsa_kw_gain"][0]))
    shared["posk"] = f(f(inp["cmp_pos_k"][0]).reshape(16, 128).T); shared["posv"] = f(f(inp["cmp_pos_v"][0]).reshape(16, 128).T)
    shared["w1k"] = f(inp["cmp_w1_k"][0]); shared["w2k"] = f(inp["cmp_w2_k"][0])
    shared["w1v"] = f(inp["cmp_w1_v"][0]); shared["w2v"] = f(inp["cmp_w2_v"][0])
    shared["wo_mla"] = f(inp["w_o_mla"][0])
    shared["wo_nsa"] = f(f(inp["w_o_nsa"][0]).reshape(8, 64, D)[PH].reshape(512, D))
    shared["w_out"] = f(inp["w_out"][0])
    shared["wg"] = f(inp["ffn_w_gate"][0]); shared["wu"] = f(inp["ffn_w_up"][0]); shared["wd"] = f(inp["ffn_w_down"][0])
    shared.update(_consts())
    maps = []
    xs = np.asarray(inp["x"], np.float32); cs = np.asarray(inp["c"], np.float32); ps = np.asarray(inp["positions"]).astype(np.int32)
    for b in range(xs.shape[0]):
        m = dict(shared)
        m["x"] = np.ascontiguousarray(xs[b])
        m["c_pk"] = np.ascontiguousarray(cs[b].reshape(8, 128).T)
        m["pos_pk"] = np.ascontiguousarray(ps[b].reshape(NT, 128).T)
        m["posC"] = np.ascontiguousarray(ps[b][31::16][:127].reshape(127, 1))
        maps.append(m)
    return maps


_NC_CACHE = {}


def kernel(**inputs):
    maps = prep_inputs(inputs)
    if "nc" not in _NC_CACHE:
        _NC_CACHE["nc"] = build()
    nc = _NC_CACHE["nc"]
    res = run_bass_kernel_spmd(nc, maps, core_ids=list(range(len(maps))))
    return np.stack([np.asarray(r["out"], dtype=np.float32) for r in res.results], 0)
```

```python
import contextlib
import numpy as np
import ml_dtypes
import concourse.bass as bass
import concourse.mybir as mybir
from concourse.bass_utils import run_bass_kernel_spmd

F32 = mybir.dt.float32
BF16 = mybir.dt.bfloat16
I32 = mybir.dt.int32
ALU = mybir.AluOpType
AF = mybir.ActivationFunctionType
AX = mybir.AxisListType

S_ = 2048
D = 1024
NT = 16
DFF = 2816
NFC = 22
EPS = 1e-6
PH = [0, 4, 1, 5, 2, 6, 3, 7]
TWO_PI = float(2 * np.pi)
PI = float(np.pi)


class Sched:
    N_DMA_SLOTS = {"sp": 24, "pool": 8, "act": 4}

    def __init__(self, nc, stack):
        self.nc = nc
        self.E = {"pe": nc.tensor, "act": nc.scalar, "dve": nc.vector, "pool": nc.gpsimd, "sp": nc.sync}
        self.sem, self.cnt = {}, {}
        for e in ("pe", "act", "dve", "pool"):
            self.sem[e] = stack.enter_context(nc.semaphore("s_" + e))
            self.cnt[e] = 0
        self.dsem, self.dcnt, self.dnext = {}, {}, {}
        for q, n in self.N_DMA_SLOTS.items():
            self.dsem[q] = [stack.enter_context(nc.semaphore(f"d_{q}{i}")) for i in range(n)]
            self.dcnt[q] = [0] * n
            self.dnext[q] = 0
        self.seen = {e: {} for e in self.E}
        self.lastw, self.readers = {}, {}
        self.n_wait = 0
        self.n_inst = 0

    def _sem_of(self, src):
        return self.dsem[src[1]][src[2]] if isinstance(src, tuple) else self.sem[src]

    def _wait(self, e, tok):
        src, val = tok
        if self.seen[e].get(src, 0) >= val:
            return
        self.E[e].wait_ge(self._sem_of(src), val)
        self.seen[e][src] = val
        self.n_wait += 1

    def _deps(self, e, r, w):
        toks = []
        for k in r:
            t = self.lastw.get(k)
            if t is not None:
                toks.append(t)
        for k in w:
            t = self.lastw.get(k)
            if t is not None:
                toks.append(t)
            for t in self.readers.get(k, ()):
                toks.append(t)
        for t in toks:
            if t[0] == e and e == "pe":
                continue
            self._wait(e, t)

    def _commit(self, tok, r, w):
        for k in r:
            lst = self.readers.setdefault(k, [])
            lst[:] = [t for t in lst if t[0] != tok[0]]
            lst.append(tok)
        for k in w:
            self.lastw[k] = tok
            self.readers[k] = []

    def op(self, e, fn, r=(), w=()):
        self._deps(e, r, w)
        ins = fn(self.E[e])
        self.cnt[e] += 1
        ins.then_inc(self.sem[e], 1)
        tok = (e, self.cnt[e])
        self._commit(tok, r, w)
        self.n_inst += 1
        return tok

    def dma(self, q, out, in_, r=(), w=(), **kw):
        slot = self.dnext[q]
        self.dnext[q] = (slot + 1) % len(self.dsem[q])
        src = ("d", q, slot)
        if self.dcnt[q][slot] > 0:
            self._wait(q, (src, self.dcnt[q][slot]))
        self._deps(q, r, w)
        ins = self.E[q].dma_start(out=out, in_=in_, **kw)
        self.dcnt[q][slot] += 16
        ins.then_inc(self.dsem[q][slot], 16)
        tok = (src, self.dcnt[q][slot])
        self._commit(tok, r, w)
        self.n_inst += 1
        return tok

    def barrier(self):
        for e in ("pe", "act", "dve", "pool", "sp"):
            for e2 in ("pe", "act", "dve", "pool"):
                if self.cnt[e2] and not (e2 == e == "pe"):
                    self._wait(e, (e2, self.cnt[e2]))
            for q in self.dsem:
                for i in range(len(self.dsem[q])):
                    if self.dcnt[q][i]:
                        self._wait(e, (("d", q, i), self.dcnt[q][i]))
        self.lastw.clear()
        self.readers.clear()


class Mem:
    def __init__(self, big, nbytes):
        self.big, self.lo, self.hi, self.n = big, 0, nbytes, nbytes

    def _view(self, off, shape, dt):
        nel = int(np.prod(shape))
        esz = 4 if dt in (F32, I32) else 2
        nb = nel * esz
        ap = self.big[:, off // 2:(off + nb) // 2]
        if esz == 4:
            ap = ap.bitcast(dt)
        if len(shape) == 2:
            ap = ap.rearrange("p (a b) -> p a b", a=shape[0])
        elif len(shape) == 3:
            ap = ap.rearrange("p (a b c) -> p a b c", a=shape[0], b=shape[1])
        return ap

    def lo_alloc(self, shape, dt):
        nb = int(np.prod(shape)) * (4 if dt in (F32, I32) else 2)
        nb = (nb + 63) // 64 * 64
        off = self.lo
        self.lo += nb
        assert self.lo <= self.hi, f"SBUF overflow lo={self.lo} hi={self.hi}"
        return self._view(off, shape, dt)

    def hi_alloc(self, shape, dt):
        nb = int(np.prod(shape)) * (4 if dt in (F32, I32) else 2)
        nb = (nb + 63) // 64 * 64
        self.hi -= nb
        assert self.lo <= self.hi, f"SBUF overflow lo={self.lo} hi={self.hi}"
        return self._view(self.hi, shape, dt)


def build(upto=99, dbg=()):
    nc = bass.Bass("TRN2", target_bir_lowering=False)
    I = {}

    def din(name, shape, dt=F32):
        I[name] = nc.dram_tensor(name, list(shape), dt, kind="ExternalInput").ap()
        return I[name]

    x = din("x", [S_, D]); c_pk = din("c_pk", [128, 8]); pos_pk = din("pos_pk", [128, NT], I32)
    posC = din("posC", [127, 1], I32)
    ada_w = din("ada_w", [D, 6 * D]); adabB = din("adabB", [128, 6 * D]); g1B = din("g1B", [128, D]); g2B = din("g2B", [128, D])
    w_cq = din("w_cq", [D, 768]); w_ckv = din("w_ckv", [D, 256]); w_kpe = din("w_kpe", [D, 32])
    w_qn = din("w_qn", [D, 512]); w_kc2 = din("w_kc2", [D, 256]); w_vc2 = din("w_vc2", [D, 256])
    w_kv4 = din("w_kv4", [D, 512]); w_gn = din("w_gn", [D, 24]); w_gm = din("w_gm", [D, D]); w_gnm = din("w_gnm", [D, D])
    qag = din("qag", [128, 6]); kvag = din("kvag", [128, 2])
    w_qb = din("w_qb", [768, 768]); w_kvb = din("w_kvb", [256, 1024])
    qgB = din("qgB", [128, 96]); kgB = din("kgB", [128, 96])
    nqB = din("nqB", [128, 64]); nkcB = din("nkcB", [128, 64]); nksB = din("nksB", [128, 64]); nkwB = din("nkwB", [128, 64])
    posk = din("posk", [128, 16]); w1k = din("w1k", [2048, 256]); w2k = din("w2k", [256, 64])
    posv = din("posv", [128, 16]); w1v = din("w1v", [2048, 256]); w2v = din("w2v", [256, 64])
    wo_mla = din("wo_mla", [512, D]); wo_nsa = din("wo_nsa", [512, D]); w_out = din("w_out", [D, D])
    wg = din("wg", [D, DFF]); wu = din("wu", [D, DFF]); wd = din("wd", [DFF, D])
    ident_d = din("ident", [128, 128], BF16); tri_d = din("tri", [128, 128], BF16); winm_d = din("winm", [128, 384], BF16)
    inv16_d = din("inv16", [128, 16]); inv8_d = din("inv8", [128, 8])
    ovl_d = din("ovl", [127, 32], BF16); vmT_d = din("vmT", [127, S_], BF16); XE_d = din("XE", [32, NT * 128], BF16)
    fb_d = din("fb", [128, NT * 32]); vj_d = din("vj", [128, NT * 32])
    out = nc.dram_tensor("out", [S_, D], F32, kind="ExternalOutput").ap()
    hTs = nc.dram_tensor("hTs", [128, 8 * S_], BF16).ap()
    mods = nc.dram_tensor("mods", [128, 6 * D], F32).ap()
    x1s = nc.dram_tensor("x1s", [S_, D], F32).ap()
    D_ = {}
    for name, shape, dt in dbg:
        D_[name] = nc.dram_tensor("dbg_" + name, list(shape), dt, kind="ExternalOutput").ap()

    st = contextlib.ExitStack()
    with st:
        S = Sched(nc, st)
        NB = 204800
        big = st.enter_context(nc.sbuf_tensor("big", [128, NB // 2], BF16))
        M = Mem(big, NB)
        P = [st.enter_context(nc.psum_tensor(f"ps{i}", [128, 512], F32)) for i in range(8)]
        PK = [f"ps{i}" for i in range(8)]
        bank_state = [0]

        nbanks = [8]

        def nb():
            b = bank_state[0] % nbanks[0]
            bank_state[0] = (b + 1) % nbanks[0]
            return b

        def mm(ps_ap, lhsT, rhs, start, stop, r, w, **kw):
            S.op("pe", lambda e: e.matmul(ps_ap, lhsT=lhsT, rhs=rhs, start=start, stop=stop, **kw), r=r, w=w)

        def tp(ps_ap, in_, ident_ap, r, w):
            S.op("pe", lambda e: e.transpose(out=ps_ap, in_=in_, identity=ident_ap), r=r, w=w)

        def act(out_, in_, func, r, w, **kw):
            S.op("act", lambda e: e.activation(out=out_, in_=in_, func=func, **kw), r=r, w=w)

        def dbg_out(name, ap, r):
            if name in D_:
                S.dma("sp", D_[name], ap, r=r)

        ident = M.lo_alloc([128], BF16); tri = M.lo_alloc([128], BF16); winm = M.lo_alloc([3, 128], BF16)
        onesb = M.lo_alloc([128], BF16)
        stg = [M.lo_alloc([1024], F32) for _ in range(3)]
        stg_i = [0]
        cosM = M.lo_alloc([NT, 16], F32); sinM = M.lo_alloc([NT, 16], F32)
        cosN = M.lo_alloc([NT, 8], F32); sinN = M.lo_alloc([NT, 8], F32)
        cosC = M.lo_alloc([8], F32); sinC = M.lo_alloc([8], F32)
        lo_pers = M.lo
        omT = M.lo_alloc([4, S_], BF16); onT = M.lo_alloc([4, S_], BF16)
        S.dma("sp", ident, ident_d, w=["ident"])
        S.dma("sp", tri, tri_d, w=["tri"])
        S.dma("sp", winm.rearrange("p a b -> p (a b)"), winm_d, w=["winm"])
        S.op("pool", lambda e: e.memset(onesb, 1.0), w=["onesb"])

        def load_w(dst, W, key, KC, N, ceng="dve"):
            Wv = W.rearrange("(k p) n -> p k n", p=128)
            if N <= 1024:
                g = max(1, min(KC, 1024 // N))
                for k0 in range(0, KC, g):
                    k1 = min(KC, k0 + g)
                    si = stg_i[0]; stg_i[0] = (si + 1) % 3
                    sv = stg[si][:, 0:(k1 - k0) * N].rearrange("p (k n) -> p k n", n=N)
                    S.dma("sp", sv, Wv[:, k0:k1, :], w=[("stg", si)])
                    S.op(ceng, lambda e, sv=sv, k0=k0, k1=k1: e.tensor_copy(out=dst[:, k0:k1, :], in_=sv), r=[("stg", si)], w=[key])
            else:
                for k in range(KC):
                    for c0 in range(0, N, 1024):
                        c1 = min(N, c0 + 1024)
                        si = stg_i[0]; stg_i[0] = (si + 1) % 3
                        sv = stg[si][:, 0:c1 - c0]
                        S.dma("sp", sv, Wv[:, k, c0:c1], w=[("stg", si)])
                        S.op(ceng, lambda e, sv=sv, k=k, c0=c0, c1=c1: e.tensor_copy(out=dst[:, k, c0:c1], in_=sv), r=[("stg", si)], w=[key])

        def load_w_cols(dst, W, key, KC, c0, c1, ceng="dve"):
            Wv = W.rearrange("(k p) n -> p k n", p=128)
            N = c1 - c0
            g = max(1, min(KC, 1024 // N))
            for k0 in range(0, KC, g):
                k1 = min(KC, k0 + g)
                si = stg_i[0]; stg_i[0] = (si + 1) % 3
                sv = stg[si][:, 0:(k1 - k0) * N].rearrange("p (k n) -> p k n", n=N)
                S.dma("sp", sv, Wv[:, k0:k1, c0:c1], w=[("stg", si)])
                S.op(ceng, lambda e, sv=sv, k0=k0, k1=k1: e.tensor_copy(out=dst[:, k0:k1, :], in_=sv), r=[("stg", si)], w=[key])

        def sincos(ang, shape, cos_o, sin_o, np_, tmp_f, tmp_i, tmp_m, key):
            for (shift, dst) in ((0.0, sin_o), (PI / 2, cos_o)):
                S.op("dve", lambda e: e.tensor_scalar(out=tmp_f, in0=ang, scalar1=shift, scalar2=None, op0=ALU.add), r=[key + "ang"], w=[key + "f"])
                S.op("dve", lambda e: e.tensor_scalar(out=tmp_i, in0=tmp_f, scalar1=float(1 / TWO_PI), scalar2=None, op0=ALU.mult), r=[key + "f"], w=[key + "i"])
                S.op("dve", lambda e: e.tensor_copy(out=tmp_m, in_=tmp_i), r=[key + "i"], w=[key + "m"])
                S.op("dve", lambda e: e.scalar_tensor_tensor(out=tmp_f, in0=tmp_m, scalar=-TWO_PI, in1=tmp_f, op0=ALU.mult, op1=ALU.add), r=[key + "m", key + "f"], w=[key + "f"])
                S.op("dve", lambda e: e.tensor_scalar(out=tmp_m, in0=tmp_f, scalar1=PI, scalar2=None, op0=ALU.is_gt), r=[key + "f"], w=[key + "m"])
                S.op("dve", lambda e: e.scalar_tensor_tensor(out=tmp_f, in0=tmp_m, scalar=-TWO_PI, in1=tmp_f, op0=ALU.mult, op1=ALU.add), r=[key + "m", key + "f"], w=[key + "f"])
                S.op("dve", lambda e: e.tensor_scalar(out=tmp_m, in0=tmp_f, scalar1=-PI, scalar2=None, op0=ALU.is_lt), r=[key + "f"], w=[key + "m"])
                S.op("dve", lambda e: e.scalar_tensor_tensor(out=tmp_f, in0=tmp_m, scalar=TWO_PI, in1=tmp_f, op0=ALU.mult, op1=ALU.add), r=[key + "m", key + "f"], w=[key + "f"])
                act(dst, tmp_f, AF.Sin, r=[key + "f"], w=[key + "out"])

        lo0, hi0 = M.lo, M.hi
        if upto <= -2:
            return finish(nc, S, st)
        posi = M.lo_alloc([NT], I32); posf = M.lo_alloc([NT], F32)
        posCi = M.lo_alloc([1], I32); posCf = M.lo_alloc([1], F32)
        inv16 = M.lo_alloc([16], F32); inv8 = M.lo_alloc([8], F32)
        angM = M.lo_alloc([NT, 16], F32); tfM = M.lo_alloc([NT, 16], F32); tiM = M.lo_alloc([NT, 16], I32); tmM = M.lo_alloc([NT, 16], F32)
        S.dma("sp", posi, pos_pk, w=["posi"])
        S.dma("sp", posCi[0:127], posC, w=["posCi"])
        S.dma("sp", inv16, inv16_d, w=["inv16"])
        S.dma("sp", inv8, inv8_d, w=["inv8"])
        S.op("dve", lambda e: e.tensor_copy(out=posf, in_=posi), r=["posi"], w=["posf"])
        S.op("dve", lambda e: e.tensor_copy(out=posCf[0:127], in_=posCi[0:127]), r=["posCi"], w=["posCf"])
        S.op("dve", lambda e: e.tensor_tensor(out=angM, in0=posf.unsqueeze(2).to_broadcast([128, NT, 16]),
                                              in1=inv16.unsqueeze(1).to_broadcast([128, NT, 16]), op=ALU.mult), r=["posf", "inv16"], w=["Mang"])
        sincos(angM, None, cosM, sinM, 128, tfM, tiM, tmM, "M")
        a8 = angM.rearrange("p a b -> p (a b)")[:, 0:NT * 8].rearrange("p (a b) -> p a b", b=8)
        f8 = tfM.rearrange("p a b -> p (a b)")[:, 0:NT * 8].rearrange("p (a b) -> p a b", b=8)
        i8 = tiM.rearrange("p a b -> p (a b)")[:, 0:NT * 8].rearrange("p (a b) -> p a b", b=8)
        m8_ = tmM.rearrange("p a b -> p (a b)")[:, 0:NT * 8].rearrange("p (a b) -> p a b", b=8)
        S.op("dve", lambda e: e.tensor_tensor(out=a8, in0=posf.unsqueeze(2).to_broadcast([128, NT, 8]),
                                              in1=inv8.unsqueeze(1).to_broadcast([128, NT, 8]), op=ALU.mult), r=["posf", "inv8", "Mout", "Mf", "Mm", "Mi"], w=["Nang"])
        sincos(a8, None, cosN, sinN, 128, f8, i8, m8_, "N")
        aC = angM.rearrange("p a b -> p (a b)")[0:127, 0:8]
        fC = tfM.rearrange("p a b -> p (a b)")[0:127, 0:8]
        iC = tiM.rearrange("p a b -> p (a b)")[0:127, 0:8]
        mC = tmM.rearrange("p a b -> p (a b)")[0:127, 0:8]
        S.op("dve", lambda e: e.tensor_scalar(out=aC, in0=inv8[0:127], scalar1=posCf[0:127, 0:1], scalar2=None, op0=ALU.mult),
             r=["posCf", "inv8", "Nout", "Nf", "Nm", "Ni", "Nang"], w=["Cang"])
        sincos(aC, None, cosC[0:127], sinC[0:127], 127, fC, iC, mC, "C")
        dbg_out("cosM", cosM, ["Mout"]); dbg_out("sinM", sinM, ["Mout"])

        if upto <= -1:
            return finish(nc, S, st)
        cpk = M.lo_alloc([8], F32); sc = M.lo_alloc([8], F32)
        sch = M.lo_alloc([8], BF16); scl = M.lo_alloc([8], BF16)
        cBh = M.lo_alloc([8, 128], BF16); cBl = M.lo_alloc([8, 128], BF16)
        modB = M.lo_alloc([6 * D], F32)
        g1t = M.lo_alloc([D], F32); g2t = M.lo_alloc([D], F32)
        awb = [M.hi_alloc([8, 512], F32) for _ in range(3)]
        abb = [M.hi_alloc([512], F32) for _ in range(3)]
        awh = [M.hi_alloc([8, 512], BF16) for _ in range(3)]
        awl = [M.hi_alloc([8, 512], BF16) for _ in range(3)]
        S.dma("sp", cpk, c_pk, w=["cpk"])
        S.dma("sp", g1t, g1B, w=["g1t"]); S.dma("sp", g2t, g2B, w=["g2t"])
        act(sc, cpk, AF.Silu, r=["cpk"], w=["sc"])
        S.op("dve", lambda e: e.tensor_copy(out=sch, in_=sc), r=["sc"], w=["sch"])
        S.op("dve", lambda e: e.tensor_tensor(out=scl, in0=sc, in1=sch, op=ALU.subtract), r=["sc", "sch"], w=["scl"])
        for k in range(8):
            S.op("dve", lambda e, k=k: e.tensor_copy(out=cBh[:, k, :], in_=sch[:, k:k + 1].to_broadcast([128, 128])), r=["sch"], w=["cBh"])
            S.op("dve", lambda e, k=k: e.tensor_copy(out=cBl[:, k, :], in_=scl[:, k:k + 1].to_broadcast([128, 128])), r=["scl"], w=["cBl"])
        awv = ada_w.rearrange("(k p) n -> p k n", p=128)
        for n in range(12):
            q_ = n % 3
            S.dma("sp", awb[q_], awv[:, :, n * 512:(n + 1) * 512], w=[("awb", q_)])
            S.dma("sp", abb[q_], adabB[:, n * 512:(n + 1) * 512], w=[("abb", q_)])
            act(awh[q_], awb[q_], AF.Copy, r=[("awb", q_)], w=[("awh", q_)])
            S.op("dve", lambda e, q_=q_: e.tensor_tensor(out=awl[q_], in0=awb[q_], in1=awh[q_], op=ALU.subtract), r=[("awb", q_), ("awh", q_)], w=[("awl", q_)])
            b = nb()
            passes = [(cBh, "cBh", awh, "awh"), (cBh, "cBh", awl, "awl"), (cBl, "cBl", awh, "awh")]
            for pi_, (cb_, ck, ww, wk) in enumerate(passes):
                for k in range(8):
                    mm(P[b][:, :], cb_[:, k, :], ww[q_][:, k, :], pi_ == 0 and k == 0, pi_ == 2 and k == 7, r=[ck, (wk, q_)], w=[PK[b]])
            S.op("dve", lambda e, n=n, b=b, q_=q_: e.tensor_tensor(out=modB[:, n * 512:(n + 1) * 512], in0=P[b][:, :], in1=abb[q_], op=ALU.add),
                 r=[PK[b], ("abb", q_)], w=["modB"])
        S.op("dve", lambda e: e.scalar_tensor_tensor(out=modB[:, D:2 * D], in0=modB[:, D:2 * D], scalar=1.0, in1=g1t, op0=ALU.add, op1=ALU.mult), r=["modB", "g1t"], w=["modB"])
        S.op("dve", lambda e: e.scalar_tensor_tensor(out=modB[:, 4 * D:5 * D], in0=modB[:, 4 * D:5 * D], scalar=1.0, in1=g2t, op0=ALU.add, op1=ALU.mult), r=["modB", "g2t"], w=["modB"])
        S.dma("sp", mods, modB, r=["modB"], w=["mods"])
        dbg_out("modB", modB, ["modB"])
        B1 = modB[:, 0:D]; A1 = modB[:, D:2 * D]
        if upto <= 0:
            return finish(nc, S, st)

        M.hi = hi0
        hT = M.hi_alloc([8, S_], BF16)
        xb = [M.lo_alloc([D], F32) for _ in range(2)]
        junk = M.lo_alloc([D], F32); tmpA = M.lo_alloc([D], F32)
        hb = [M.lo_alloc([D], BF16) for _ in range(2)]
        ssq = M.lo_alloc([NT], F32); rs = M.lo_alloc([NT], F32)

        tmpA2 = [tmpA, M.lo_alloc([D], F32)]

        def norm_a(i, xt, xkey):
            act(junk, xt, AF.Square, r=[xkey], w=["junk", ("ssq", i)], accum_out=ssq[:, i:i + 1])
            act(rs[:, i:i + 1], ssq[:, i:i + 1], AF.Sqrt, r=[("ssq", i)], w=[("rs", i)], scale=1.0 / D, bias=EPS)
            S.op("dve", lambda e: e.reciprocal(out=rs[:, i:i + 1], in_=rs[:, i:i + 1]), r=[("rs", i)], w=[("rs", i)])

        def norm_b(i, xt, xkey, A, B, Akeys, dstT, dkey):
            par = i % 2
            S.op("dve", lambda e: e.scalar_tensor_tensor(out=tmpA2[par], in0=xt, scalar=rs[:, i:i + 1], in1=A, op0=ALU.mult, op1=ALU.mult),
                 r=[xkey, ("rs", i)] + Akeys, w=[("tmpA2", par)])
            S.op("pool", lambda e: e.tensor_tensor(out=hb[par], in0=tmpA2[par], in1=B, op=ALU.add), r=[("tmpA2", par)] + Akeys, w=[("hb", par)])
            b = nb()
            pb = P[b][:, :].bitcast(BF16)
            for k in range(8):
                tp(pb[:, k * 128:(k + 1) * 128], hb[par][:, k * 128:(k + 1) * 128], ident, r=[("hb", par), "ident"], w=[PK[b]])
            act(dstT[:, :, i * 128:(i + 1) * 128], pb.rearrange("p (k t) -> p k t", k=8), AF.Copy, r=[PK[b]], w=[(dkey, i)])

        xb = xb + [M.lo_alloc([D], F32), M.lo_alloc([D], F32)]

        def a_front(i):
            S.dma("sp", xb[i % 4], x[i * 128:(i + 1) * 128, :], w=[("xb", i % 4)])
            norm_a(i, xb[i % 4], ("xb", i % 4))

        a_front(0); a_front(1)
        for i in range(NT):
            if i + 2 < NT:
                a_front(i + 2)
            norm_b(i, xb[i % 4], ("xb", i % 4), A1, B1, ["modB"], hT, "hT")
        hTk = [("hT", i) for i in range(NT)]
        S.dma("sp", hTs, hT.rearrange("p k t -> p (k t)"), r=hTk, w=["hTs"])
        dbg_out("hT", hT.rearrange("p k t -> p (k t)"), hTk)
        if upto <= 1:
            return finish(nc, S, st)
        S.barrier()

        M.lo = lo0
        cqT = M.lo_alloc([6, S_], BF16); ckvT = M.lo_alloc([2, S_], BF16); kpe = M.lo_alloc([NT, 32], F32)
        lo1 = M.lo
        Wcq = M.lo_alloc([8, 768], BF16); Wckv = M.lo_alloc([8, 256], BF16); Wkpe = M.lo_alloc([8, 32], BF16)
        qagt = M.lo_alloc([6], F32); kvagt = M.lo_alloc([2], F32)
        sqb = [M.lo_alloc([512], BF16) for _ in range(2)]
        rb = M.lo_alloc([512], F32)
        S.dma("sp", qagt, qag, w=["qagt"]); S.dma("sp", kvagt, kvag, w=["kvagt"])
        load_w(Wcq, w_cq, "Wcq", 8, 768); load_w(Wckv, w_ckv, "Wckv", 8, 256); load_w(Wkpe, w_kpe, "Wkpe", 8, 32)

        def fm_proj_norm(dstT, dkey, Wt, wkey, nf, gaint, gkey, nfeat):
            for c in range(4):
                hk = [("hT", 4 * c + q) for q in range(4)]
                for j in range(nf):
                    b = nb()
                    for k in range(8):
                        mm(P[b][:, :], Wt[:, k, j * 128:(j + 1) * 128], hT[:, k, c * 512:(c + 1) * 512], k == 0, k == 7, r=[wkey] + hk, w=[PK[b]])
                    act(dstT[:, j, c * 512:(c + 1) * 512], P[b][:, :], AF.Copy, r=[PK[b]], w=[(dkey, c)])
                    act(sqb[j % 2], P[b][:, :], AF.Square, r=[PK[b]], w=[("sqb", j % 2)])
                    mm(P[6][:, :], onesb, sqb[j % 2], j == 0, j == nf - 1, r=["onesb", ("sqb", j % 2)], w=[PK[6]])
                act(rb, P[6][:, :], AF.Sqrt, r=[PK[6]], w=["rb"], scale=1.0 / nfeat, bias=EPS)
                S.op("dve", lambda e: e.reciprocal(out=rb, in_=rb), r=["rb"], w=["rb"])
                for j in range(nf):
                    S.op("dve", lambda e, j=j, c=c: e.scalar_tensor_tensor(out=dstT[:, j, c * 512:(c + 1) * 512], in0=dstT[:, j, c * 512:(c + 1) * 512],
                                                                             scalar=gaint[:, j:j + 1], in1=rb, op0=ALU.mult, op1=ALU.mult),
                         r=[(dkey, c), "rb", gkey], w=[(dkey, c)])

        fm_proj_norm(cqT, "cqT", Wcq, "Wcq", 6, qagt, "qagt", 768)
        fm_proj_norm(ckvT, "ckvT", Wckv, "Wckv", 2, kvagt, "kvagt", 256)
        for i in range(NT):
            b = nb()
            for k in range(8):
                mm(P[b][:, 0:32], hT[:, k, i * 128:(i + 1) * 128], Wkpe[:, k, :], k == 0, k == 7, r=["Wkpe", ("hT", i)], w=[PK[b]])
            S.op("dve", lambda e, i=i, b=b: e.tensor_copy(out=kpe[:, i, :], in_=P[b][:, 0:32]), r=[PK[b]], w=[("kpe", i)])
        cqk = [("cqT", c) for c in range(4)]
        dbg_out("cqT", cqT.rearrange("p k t -> p (k t)"), cqk)
        dbg_out("kpe", kpe.rearrange("p a b -> p (a b)"), [("kpe", i) for i in range(NT)])
        if upto <= 2:
            return finish(nc, S, st)
        S.barrier()

        M.lo = lo1
        M.hi = hi0
        QT = M.hi_alloc([8, S_], BF16); KT = M.hi_alloc([8, S_], BF16); V = M.hi_alloc([NT, 8, 65], BF16)
        hi1 = M.hi
        Wqb = M.lo_alloc([6, 768], BF16); Wkvb = M.lo_alloc([2, 1024], BF16)
        qgt = M.lo_alloc([96], F32); kgt = M.lo_alloc([96], F32)
        drq = [M.lo_alloc([768], BF16) for _ in range(2)]; drk = [M.lo_alloc([768], BF16) for _ in range(2)]
        S.dma("sp", qgt, qgB, w=["qgt"]); S.dma("sp", kgt, kgB, w=["kgt"])
        load_w(Wqb, w_qb, "Wqb", 6, 768); load_w(Wkvb, w_kvb, "Wkvb", 2, 1024)
        S.op("pool", lambda e: e.memset(V[:, :, :, 64:65], 1.0), w=["Vones"])

        def mk_tmps(Mx, n, H, hf):
            return dict(t1=Mx.lo_alloc([n], F32), t2=Mx.lo_alloc([n], F32), hs=Mx.lo_alloc([H], F32), hr=Mx.lo_alloc([H], F32),
                        ra=Mx.lo_alloc([H * hf], F32), rb=Mx.lo_alloc([H * hf], F32), ra2=Mx.lo_alloc([H * hf], F32), rb2=Mx.lo_alloc([H * hf], F32))

        def hnr_stages(tag, T, src, skeys, H, Dh, gaint, gkey, ro, hf, cos_, sin_, dst, dkey, np_=128):
            n = H * Dh
            t1v = T["t1"][0:np_, 0:n].rearrange("p (h d) -> p h d", h=H)
            t2v = T["t2"][0:np_, 0:n].rearrange("p (h d) -> p h d", h=H)
            hs = T["hs"][0:np_, 0:H]; hr = T["hr"][0:np_, 0:H]
            x1 = t1v[:, :, ro:ro + hf]; x2 = t1v[:, :, ro + hf:ro + 2 * hf]
            cb = cos_.unsqueeze(1).to_broadcast([np_, H, hf]); sb_ = sin_.unsqueeze(1).to_broadcast([np_, H, hf])
            rv = {k: T[k][0:np_, 0:H * hf].rearrange("p (h d) -> p h d", h=H) for k in ("ra", "rb", "ra2", "rb2")}
            tr = ["Mout", "Nout", "Cout"]
            k_ = lambda nm: (tag, nm)
            st = []
            st.append(lambda: act(t1v, src, AF.Square, r=skeys, w=[k_("t1")]))
            st.append(lambda: S.op("dve", lambda e: e.tensor_reduce(out=hs, in_=t1v, axis=AX.X, op=ALU.add), r=[k_("t1")], w=[k_("hs")]))
            st.append(lambda: act(hr, hs, AF.Sqrt, r=[k_("hs")], w=[k_("hr")], scale=1.0 / Dh, bias=EPS))
            st.append(lambda: S.op("dve", lambda e: e.reciprocal(out=hr, in_=hr), r=[k_("hr")], w=[k_("hr")]))
            st.append(lambda: S.op("dve", lambda e: e.tensor_tensor(out=t2v, in0=src, in1=hr.unsqueeze(2).to_broadcast([np_, H, Dh]), op=ALU.mult), r=skeys + [k_("hr")], w=[k_("t2")]))
            st.append(lambda: S.op("dve", lambda e: e.tensor_tensor(out=t1v, in0=t2v, in1=gaint[0:np_].unsqueeze(1).to_broadcast([np_, H, Dh]), op=ALU.mult), r=[k_("t2"), gkey], w=[k_("t1")]))
            st.append(lambda: S.op("dve", lambda e: e.tensor_tensor(out=rv["ra"], in0=x1, in1=cb, op=ALU.mult), r=[k_("t1")] + tr, w=[k_("ra")]))
            st.append(lambda: S.op("dve", lambda e: e.tensor_tensor(out=rv["rb"], in0=x2, in1=sb_, op=ALU.mult), r=[k_("t1")] + tr, w=[k_("rb")]))
            st.append(lambda: S.op("dve", lambda e: e.tensor_tensor(out=dst[:, :, ro:ro + hf], in0=rv["ra"], in1=rv["rb"], op=ALU.subtract), r=[k_("ra"), k_("rb")], w=[dkey]))
            st.append(lambda: S.op("dve", lambda e: e.tensor_tensor(out=rv["ra2"], in0=x2, in1=cb, op=ALU.mult), r=[k_("t1")] + tr, w=[k_("ra2")]))
            st.append(lambda: S.op("dve", lambda e: e.tensor_tensor(out=rv["rb2"], in0=x1, in1=sb_, op=ALU.mult), r=[k_("t1")] + tr, w=[k_("rb2")]))
            st.append(lambda: S.op("dve", lambda e: e.tensor_tensor(out=dst[:, :, ro + hf:ro + 2 * hf], in0=rv["ra2"], in1=rv["rb2"], op=ALU.add), r=[k_("ra2"), k_("rb2")], w=[dkey]))

            def copies():
                if ro > 0:
                    S.op("pool", lambda e: e.tensor_copy(out=dst[:, :, 0:ro], in_=t1v[:, :, 0:ro]), r=[k_("t1")], w=[dkey])
                if ro + 2 * hf < Dh:
                    S.op("pool", lambda e: e.tensor_copy(out=dst[:, :, ro + 2 * hf:Dh], in_=t1v[:, :, ro + 2 * hf:Dh]), r=[k_("t1")], w=[dkey])
            st.insert(6, copies)
            return st

        def run_interleaved(chains):
            for s_ in range(max(len(c_) for c_ in chains)):
                for c_ in chains:
                    if s_ < len(c_):
                        c_[s_]()

        def head_norm_rope(src, skeys, H, Dh, gaint, gkey, ro, hf, cos_, sin_, dst, dkey, np_=128):
            n = H * Dh
            t1v = t1[0:np_, 0:n].rearrange("p (h d) -> p h d", h=H)
            t2v = t2[0:np_, 0:n].rearrange("p (h d) -> p h d", h=H)
            hs = hss[0:np_, 0:H]; hr = hrs[0:np_, 0:H]
            act(t1v, src, AF.Square, r=skeys, w=["t1"])
            S.op("dve", lambda e: e.tensor_reduce(out=hs, in_=t1v, axis=AX.X, op=ALU.add), r=["t1"], w=["hss"])
            act(hr, hs, AF.Sqrt, r=["hss"], w=["hrs"], scale=1.0 / Dh, bias=EPS)
            S.op("dve", lambda e: e.reciprocal(out=hr, in_=hr), r=["hrs"], w=["hrs"])
            S.op("dve", lambda e: e.tensor_tensor(out=t2v, in0=src, in1=hr.unsqueeze(2).to_broadcast([np_, H, Dh]), op=ALU.mult), r=skeys + ["hrs"], w=["t2"])
            S.op("dve", lambda e: e.tensor_tensor(out=t1v, in0=t2v, in1=gaint[0:np_].unsqueeze(1).to_broadcast([np_, H, Dh]), op=ALU.mult), r=["t2", gkey], w=["t1"])
            x1 = t1v[:, :, ro:ro + hf]; x2 = t1v[:, :, ro + hf:ro + 2 * hf]
            cb = cos_.unsqueeze(1).to_broadcast([np_, H, hf]); sb_ = sin_.unsqueeze(1).to_broadcast([np_, H, hf])
            rav = ra[0:np_, 0:H * hf].rearrange("p (h d) -> p h d", h=H)
            rbv = rbb[0:np_, 0:H * hf].rearrange("p (h d) -> p h d", h=H)
            tr = ["Mout", "Nout", "Cout"]
            S.op("dve", lambda e: e.tensor_tensor(out=rav, in0=x1, in1=cb, op=ALU.mult), r=["t1"] + tr, w=["ra"])
            S.op("dve", lambda e: e.tensor_tensor(out=rbv, in0=x2, in1=sb_, op=ALU.mult), r=["t1"] + tr, w=["rbb"])
            S.op("dve", lambda e: e.tensor_tensor(out=dst[:, :, ro:ro + hf], in0=rav, in1=rbv, op=ALU.subtract), r=["ra", "rbb"], w=[dkey])
            S.op("dve", lambda e: e.tensor_tensor(out=rav, in0=x2, in1=cb, op=ALU.mult), r=["t1"] + tr, w=["ra"])
            S.op("dve", lambda e: e.tensor_tensor(out=rbv, in0=x1, in1=sb_, op=ALU.mult), r=["t1"] + tr, w=["rbb"])
            S.op("dve", lambda e: e.tensor_tensor(out=dst[:, :, ro + hf:ro + 2 * hf], in0=rav, in1=rbv, op=ALU.add), r=["ra", "rbb"], w=[dkey])
            if ro > 0:
                S.op("pool", lambda e: e.tensor_copy(out=dst[:, :, 0:ro], in_=t1v[:, :, 0:ro]), r=["t1"], w=[dkey])
            if ro + 2 * hf < Dh:
                S.op("pool", lambda e: e.tensor_copy(out=dst[:, :, ro + 2 * hf:Dh], in_=t1v[:, :, ro + 2 * hf:Dh]), r=["t1"], w=[dkey])

        Msub = Mem(big, NB); Msub.lo = lo_pers; Msub.hi = lo0
        rawq = [Msub.lo_alloc([768], F32) for _ in range(2)]; rawk = [Msub.lo_alloc([768], F32) for _ in range(2)]
        Tq = mk_tmps(Msub, 768, 8, 16); Tk = mk_tmps(Msub, 768, 8, 16)

        def b2_front(i):
            ts = slice(i * 128, (i + 1) * 128)
            par = i % 2
            bA, bB = nb(), nb()
            for k in range(6):
                mm(P[bA][:, :], cqT[:, k, ts], Wqb[:, k, 0:512], k == 0, k == 5, r=["Wqb", ("cqT", i // 4)], w=[PK[bA]])
            for k in range(6):
                mm(P[bB][:, 0:256], cqT[:, k, ts], Wqb[:, k, 512:768], k == 0, k == 5, r=["Wqb", ("cqT", i // 4)], w=[PK[bB]])
            act(rawq[par][:, 0:512], P[bA][:, :], AF.Copy, r=[PK[bA]], w=[("rawq", par)])
            act(rawq[par][:, 512:768], P[bB][:, 0:256], AF.Copy, r=[PK[bB]], w=[("rawq", par)])
            bA, bB = nb(), nb()
            for hh, bb in ((0, bA), (1, bB)):
                for k in range(2):
                    mm(P[bb][:, :], ckvT[:, k, ts], Wkvb[:, k, hh * 512:(hh + 1) * 512], k == 0, k == 1, r=["Wkvb", ("ckvT", i // 4)], w=[PK[bb]])
            rv = rawk[par].rearrange("p (h d) -> p h d", h=8)
            for hh, bb in ((0, bA), (1, bB)):
                pv = P[bb][:, :].rearrange("p (h d) -> p h d", h=4)
                act(rv[:, hh * 4:(hh + 1) * 4, 0:64], pv[:, :, 0:64], AF.Copy, r=[PK[bb]], w=[("rawk", par)])
                act(V[:, i, hh * 4:(hh + 1) * 4, 0:64], pv[:, :, 64:128], AF.Copy, r=[PK[bb]], w=[("V", i)])
            S.op("pool", lambda e, i=i: e.tensor_copy(out=rv[:, :, 64:96], in_=kpe[:, i, :].unsqueeze(1).to_broadcast([128, 8, 32])), r=[("kpe", i)], w=[("rawk", par)])

        def b2_back(i):
            ts = slice(i * 128, (i + 1) * 128)
            par = i % 2
            dq = drq[par].rearrange("p (h d) -> p h d", h=8); dk = drk[par].rearrange("p (h d) -> p h d", h=8)
            cq_ = hnr_stages("cq", Tq, rawq[par].rearrange("p (h d) -> p h d", h=8), [("rawq", par)], 8, 96, qgt, "qgt", 64, 16, cosM[:, i, :], sinM[:, i, :], dq, ("drq", par))
            ck_ = hnr_stages("ck", Tk, rawk[par].rearrange("p (h d) -> p h d", h=8), [("rawk", par)], 8, 96, kgt, "kgt", 64, 16, cosM[:, i, :], sinM[:, i, :], dk, ("drk", par))
            run_interleaved([cq_, ck_])
            for (dd, dkey_, dstT, okey) in ((dq, ("drq", par), QT, "QT"), (dk, ("drk", par), KT, "KT")):
                b = nb(); pb = P[b][:, :].bitcast(BF16)
                for h in range(8):
                    tp(pb[0:96, h * 128:(h + 1) * 128], dd[:, h, :], ident, r=[dkey_, "ident"], w=[PK[b]])
                act(dstT[0:96, :, ts], pb[0:96, :].rearrange("p (h t) -> p h t", h=8), AF.Copy, r=[PK[b]], w=[(okey, i)])

        b2_front(0)
        for i in range(NT):
            if i + 1 < NT:
                b2_front(i + 1)
            b2_back(i)
        QTk = [("QT", i) for i in range(NT)]
        dbg_out("QT", QT[0:96].rearrange("p k t -> p (k t)"), QTk)
        dbg_out("KT", KT[0:96].rearrange("p k t -> p (k t)"), [("KT", i) for i in range(NT)])
        dbg_out("V", V.rearrange("p a b c -> p (a b c)"), [("V", i) for i in range(NT)] + ["Vones"])
        if upto <= 3:
            return finish(nc, S, st)
        S.barrier()

        nbanks[0] = 6
        M.lo = lo0
        om = M.lo_alloc([NT, 512], BF16)
        PT = [M.lo_alloc([512], BF16) for _ in range(3)]
        rec4 = [M.lo_alloc([4], F32) for _ in range(2)]
        pt_i = [0]

        gchunk = [0]

        def causal_attn_multi(jobs):
            steps = []
            for ji in range(len(jobs)):
                for c in range(4):
                    for kt in range(4 * c + 4):
                        steps.append((ji, c, kt))

            def emit_qk(step):
                ji, c, kt = step
                J = jobs[ji]
                q0 = max(kt - 4 * c, 0)
                n = 512 - 128 * q0
                b = nb()
                has_extra = J["extra"] is not None
                mm(P[b][:, 0:n], J["KT"][0:J["kn"], kt * 128:(kt + 1) * 128], J["QT"](c * 512 + q0 * 128, (c + 1) * 512),
                   True, not has_extra, r=J["kk"](kt) + J["qk"](c), w=[PK[b]])
                if has_extra:
                    J["extra"](P[b][:, 0:n], kt, c * 512 + q0 * 128, (c + 1) * 512, PK[b])
                return b, n, q0

            pend = emit_qk(steps[0])
            for si, (ji, c, kt) in enumerate(steps):
                J = jobs[ji]
                b, n, q0 = pend
                if si + 1 < len(steps):
                    pend = emit_qk(steps[si + 1])
                if kt == 0:
                    gchunk[0] += 1
                ab = 6 + (gchunk[0] % 2)
                Oacc = P[ab][:, 0:260].rearrange("p (q d) -> p q d", q=4)
                pi = pt_i[0]; pt_i[0] = (pi + 1) % 3
                pt = PT[pi]
                act(pt[:, 0:n], P[b][:, 0:n], AF.Exp, r=[PK[b]], w=[("PT", pi)], scale=J["scale"])
                if kt >= 4 * c:
                    S.op("dve", lambda e, pt=pt: e.tensor_tensor(out=pt[:, 0:128], in0=pt[:, 0:128], in1=tri, op=ALU.mult), r=[("PT", pi), "tri"], w=[("PT", pi)])
                for qi in range(q0, 4):
                    mm(Oacc[:, qi, :], pt[:, (qi - q0) * 128:(qi - q0 + 1) * 128], J["V"](kt), kt == 0 and qi == 0, kt == 4 * c + qi,
                       r=[("PT", pi)] + J["vk"](kt), w=[PK[ab]], skip_group_check=True)
                if kt == 4 * c + 3:
                    J["fin"](c, Oacc, PK[ab])

        jobs = []
        for h in range(8):
            def fin(c, Oacc, pk, h=h):
                rc = rec4[c % 2]
                S.op("dve", lambda e: e.reciprocal(out=rc, in_=Oacc[:, :, 64]), r=[pk], w=[("rec4", c % 2)])
                S.op("dve", lambda e: e.tensor_tensor(out=om[:, 4 * c:4 * c + 4, h * 64:(h + 1) * 64], in0=Oacc[:, :, 0:64],
                                                      in1=rc.unsqueeze(2).to_broadcast([128, 4, 64]), op=ALU.mult),
                     r=[pk, ("rec4", c % 2)], w=[("om", c)])
            jobs.append(dict(KT=KT[:, h, :], kn=96, QT=(lambda a, b_, h=h: QT[0:96, h, a:b_]), V=(lambda kt, h=h: V[:, kt, h, :]), scale=96 ** -0.5,
                             extra=None, fin=fin, qk=(lambda c: [("QT", 4 * c + q) for q in range(4)]), kk=(lambda kt: [("KT", kt)]),
                             vk=(lambda kt: [("V", kt), "Vones"])))
        causal_attn_multi(jobs)
        for i in range(NT):
            b = nb(); pb = P[b][:, :].bitcast(BF16)
            for j in range(4):
                tp(pb[:, j * 128:(j + 1) * 128], om[:, i, j * 128:(j + 1) * 128], ident, r=[("om", i // 4), "ident"], w=[PK[b]])
            act(omT[:, :, i * 128:(i + 1) * 128], pb[:, 0:512].rearrange("p (k t) -> p k t", k=4), AF.Copy, r=[PK[b]], w=[("omT", i)])
        dbg_out("om", om.rearrange("p a b -> p (a b)"), [("om", c) for c in range(4)])
        if upto <= 4:
            return finish(nc, S, st)
        S.barrier()

        nbanks[0] = 4
        M.lo = lo0; M.hi = hi0
        qnT = M.lo_alloc([8, S_], BF16); ksT = M.lo_alloc([2, S_], BF16); kwT = M.lo_alloc([2, S_], BF16)
        vs = M.lo_alloc([NT, 2, 65], BF16); vw = M.lo_alloc([NT, 2, 65], BF16)
        gates = M.lo_alloc([NT, 3, 8], F32)
        kcmpT = M.lo_alloc([2, 128], BF16); VCX = M.lo_alloc([2, 97], BF16)
        PT = [M.lo_alloc([512], BF16) for _ in range(3)]
        rec4 = [M.lo_alloc([4], F32) for _ in range(2)]
        t1 = M.lo_alloc([512], F32); t2 = M.lo_alloc([512], F32)
        hss = M.lo_alloc([8], F32); hrs = M.lo_alloc([8], F32)
        ra = M.lo_alloc([128], F32); rbb = M.lo_alloc([128], F32)
        drb = [M.lo_alloc([512], BF16) for _ in range(2)]
        nqt = M.lo_alloc([64], F32); nkct = M.lo_alloc([64], F32); nkst = M.lo_alloc([64], F32); nkwt = M.lo_alloc([64], F32)
        lo2 = M.lo
        kc2 = M.hi_alloc([2, S_], BF16); vc2 = M.hi_alloc([2, S_], BF16)
        hi_kv = M.hi
        hT = M.hi_alloc([8, S_], BF16)
        Wqn = M.hi_alloc([8, 512], BF16); Wkc2 = M.hi_alloc([8, 256], BF16); Wvc2 = M.hi_alloc([8, 256], BF16)
        Wkv4 = M.hi_alloc([8, 512], BF16); Wgn = M.hi_alloc([8, 24], BF16)
        ge = M.hi_alloc([24], F32)
        S.dma("sp", hT.rearrange("p k t -> p (k t)"), hTs, r=["hTs"], w=["hTall"])
        for t_, d_ in ((nqt, nqB), (nkct, nkcB), (nkst, nksB), (nkwt, nkwB)):
            S.dma("sp", t_, d_, w=["ngain"])
        load_w(Wqn, w_qn, "Wqn", 8, 512); load_w(Wkv4, w_kv4, "Wkv4", 8, 512); load_w(Wgn, w_gn, "Wgn", 8, 24)
        load_w(Wkc2, w_kc2, "Wkc2", 8, 256); load_w(Wvc2, w_vc2, "Wvc2", 8, 256)
        S.op("pool", lambda e: e.memset(vs[:, :, :, 64:65], 1.0), w=["vsones"])
        S.op("pool", lambda e: e.memset(vw[:, :, :, 64:65], 1.0), w=["vwones"])
        S.op("pool", lambda e: e.memset(kc2[64:128, :, S_ - 1:S_], 0.0), w=["kc2pad"])
        S.op("pool", lambda e: e.memset(vc2[64:128, :, S_ - 1:S_], 0.0), w=["vc2pad"])
        MsubD = Mem(big, NB); MsubD.lo = lo_pers + 16384; MsubD.hi = lo0
        TDq = mk_tmps(MsubD, 512, 8, 8); TDs = mk_tmps(MsubD, 128, 2, 8); TDw = mk_tmps(MsubD, 128, 2, 8)
        dnq = [MsubD.lo_alloc([512], BF16) for _ in range(2)]
        dns = [MsubD.lo_alloc([128], BF16) for _ in range(2)]; dnw = [MsubD.lo_alloc([128], BF16) for _ in range(2)]

        def d_front(i):
            ts = slice(i * 128, (i + 1) * 128)
            bq = 4 + 2 * (i % 2)
            for k in range(8):
                mm(P[bq][:, :], hT[:, k, ts], Wqn[:, k, :], k == 0, k == 7, r=["hTall", "Wqn"], w=[PK[bq]])
            bk = 5 + 2 * (i % 2)
            for k in range(8):
                mm(P[bk][:, :], hT[:, k, ts], Wkv4[:, k, :], k == 0, k == 7, r=["hTall", "Wkv4"], w=[PK[bk]])
            bg = nb()
            for k in range(8):
                mm(P[bg][:, 0:24], hT[:, k, ts], Wgn[:, k, :], k == 0, k == 7, r=["hTall", "Wgn"], w=[PK[bg]])
            act(vs[:, i, :, 0:64], P[bk][:, 256:384].rearrange("p (g d) -> p g d", g=2), AF.Copy, r=[PK[bk]], w=[("vs", i)])
            act(vw[:, i, :, 0:64], P[bk][:, 384:512].rearrange("p (g d) -> p g d", g=2), AF.Copy, r=[PK[bk]], w=[("vw", i)])
            act(ge, P[bg][:, 0:24], AF.Exp, r=[PK[bg]], w=["ge"], scale=-1.0)
            S.op("dve", lambda e: e.tensor_scalar(out=ge, in0=ge, scalar1=1.0, scalar2=None, op0=ALU.add), r=["ge"], w=["ge"])
            S.op("dve", lambda e, i=i: e.reciprocal(out=gates[:, i].rearrange("p a b -> p (a b)"), in_=ge), r=["ge"], w=[("gates", i)])
            return bq, bk

        def d_back(i, bq, bk):
            ts = slice(i * 128, (i + 1) * 128)
            par = i % 2
            dq = dnq[par].rearrange("p (h d) -> p h d", h=8)
            ds_ = dns[par].rearrange("p (h d) -> p h d", h=2); dw_ = dnw[par].rearrange("p (h d) -> p h d", h=2)
            c1 = hnr_stages("dq", TDq, P[bq][:, :].rearrange("p (h d) -> p h d", h=8), [PK[bq]], 8, 64, nqt, "ngain", 0, 8, cosN[:, i, :], sinN[:, i, :], dq, ("dnq", par))
            c2 = hnr_stages("ds", TDs, P[bk][:, 0:128].rearrange("p (h d) -> p h d", h=2), [PK[bk]], 2, 64, nkst, "ngain", 0, 8, cosN[:, i, :], sinN[:, i, :], ds_, ("dns", par))
            c3 = hnr_stages("dw", TDw, P[bk][:, 128:256].rearrange("p (h d) -> p h d", h=2), [PK[bk]], 2, 64, nkwt, "ngain", 0, 8, cosN[:, i, :], sinN[:, i, :], dw_, ("dnw", par))
            run_interleaved([c1, c2, c3])
            b = nb(); pb = P[b][:, :].bitcast(BF16)
            for p_ in range(8):
                tp(pb[0:64, p_ * 128:(p_ + 1) * 128], dnq[par][:, p_ * 64:(p_ + 1) * 64], ident, r=[("dnq", par), "ident"], w=[PK[b]])
            act(qnT[0:64, :, ts], pb[0:64, :].rearrange("p (k t) -> p k t", k=8), AF.Copy, r=[PK[b]], w=[("qnT", i)])
            for (dd, dkey_, dstT, dk) in ((dns[par], ("dns", par), ksT, "ksT"), (dnw[par], ("dnw", par), kwT, "kwT")):
                b2 = nb(); pb = P[b2][:, :].bitcast(BF16)
                for g_ in range(2):
                    tp(pb[0:64, g_ * 128:(g_ + 1) * 128], dd[:, g_ * 64:(g_ + 1) * 64], ident, r=[dkey_, "ident"], w=[PK[b2]])
                act(dstT[0:64, :, ts], pb[0:64, 0:256].rearrange("p (g t) -> p g t", g=2), AF.Copy, r=[PK[b2]], w=[(dk, i)])

        fb_ = d_front(0)
        for i in range(NT):
            cur_ = fb_
            if i + 1 < NT:
                fb_ = d_front(i + 1)
            d_back(i, *cur_)
        for c in range(4):
            for (Wt, wk, dst, dk) in ((Wkc2, "Wkc2", kc2, "kc2"), (Wvc2, "Wvc2", vc2, "vc2")):
                for g in range(2):
                    b = nb()
                    for k in range(8):
                        mm(P[b][:, :], Wt[:, k, g * 128:(g + 1) * 128], hT[:, k, c * 512:(c + 1) * 512], k == 0, k == 7, r=["hTall", wk], w=[PK[b]])
                    act(dst[0:64, g, c * 512:(c + 1) * 512], P[b][0:64, :], AF.Copy, r=[PK[b]], w=[dk])
                    if c == 0:
                        act(dst[64:128, g, 0:511], P[b][64:128, 1:512], AF.Copy, r=[PK[b]], w=[dk])
                    else:
                        act(dst[64:128, g, c * 512 - 1:(c + 1) * 512 - 1], P[b][64:128, :], AF.Copy, r=[PK[b]], w=[dk])
        dbg_out("qnT", qnT[0:64].rearrange("p k t -> p (k t)"), [("qnT", i) for i in range(NT)])
        dbg_out("ksT", ksT[0:64].rearrange("p k t -> p (k t)"), [("ksT", i) for i in range(NT)])
        dbg_out("gates", gates.rearrange("p a b c -> p (a b c)"), [("gates", i) for i in range(NT)])
        dbg_out("kc2", kc2.rearrange("p k t -> p (k t)"), ["kc2", "kc2pad"])
        if upto <= 5:
            return finish(nc, S, st)
        S.barrier()

        nbanks[0] = 8
        M.hi = hi_kv
        hiE = M.hi
        W1k = M.lo_alloc([16, 256], BF16); W1v = M.lo_alloc([16, 256], BF16)
        W2k = M.lo_alloc([2, 64], BF16); W2v = M.lo_alloc([2, 64], BF16)
        pkf = M.lo_alloc([16], F32); pvf = M.lo_alloc([16], F32); pkb = M.lo_alloc([16], BF16); pvb = M.lo_alloc([16], BF16)
        biask = M.lo_alloc([2], F32); biasv = M.lo_alloc([2], F32)
        hid = [M.lo_alloc([128], BF16) for _ in range(2)]
        ovl = M.lo_alloc([32], BF16)
        load_w(W1k, w1k, "W1k", 16, 256); load_w(W1v, w1v, "W1v", 16, 256)
        load_w(W2k, w2k, "W2k", 2, 64); load_w(W2v, w2v, "W2v", 2, 64)
        S.dma("sp", pkf, posk, w=["pkf"]); S.dma("sp", pvf, posv, w=["pvf"]); S.dma("sp", ovl[0:127], ovl_d, w=["ovl"])
        S.op("pool", lambda e: e.memset(VCX[0:127, :, 64:65], 1.0), w=["VCXa"])
        for g in range(2):
            S.op("pool", lambda e, g=g: e.tensor_copy(out=VCX[0:127, g, 65:97], in_=ovl[0:127]), r=["ovl"], w=["VCXb"])
        rt = [M.lo_alloc([128], BF16) for _ in range(3)]
        rt_i = [0]
        for (W1, w1key, W2, w2key, src, skey, posf_, pkey, isk) in ((W1k, "W1k", W2k, "W2k", kc2, ["kc2", "kc2pad"], pkf, "pkf", True),
                                                                    (W1v, "W1v", W2v, "W2v", vc2, ["vc2", "vc2pad"], pvf, "pvf", False)):
            srcv = src.rearrange("p g (n s) -> p g n s", s=16)
            bo = nb()
            for g in range(2):
                bh = []
                for hc in range(2):
                    b = nb()
                    while b == bo or b in bh:
                        b = nb()
                    bh.append(b)
                for lc in range(16):
                    ri = rt_i[0]; rt_i[0] = (ri + 1) % 3
                    rtv = rt[ri][:, 0:127]
                    S.op("dve", lambda e, rtv=rtv, g=g, lc=lc, srcv=srcv, posf_=posf_: e.tensor_scalar(
                        out=rtv, in0=srcv[:, g, (2 * lc) // 16:(2 * lc) // 16 + 127, (2 * lc) % 16], scalar1=posf_[:, lc:lc + 1], scalar2=None, op0=ALU.add),
                        r=skey + [pkey], w=[("rt", ri)])
                    for hc in range(2):
                        mm(P[bh[hc]][:, 0:127], W1[:, lc, hc * 128:(hc + 1) * 128], rtv, lc == 0, lc == 15, r=[w1key, ("rt", ri)], w=[PK[bh[hc]]])
                for hc in range(2):
                    act(hid[hc][:, 0:127], P[bh[hc]][:, 0:127], AF.Silu, r=[PK[bh[hc]]], w=[("hid", hc)])
                for hc in range(2):
                    mm(P[bo][0:127, g * 64:(g + 1) * 64], hid[hc][:, 0:127], W2[:, hc, :], hc == 0, hc == 1, r=[("hid", hc), w2key], w=[PK[bo]])
            if isk:
                d = drb[1][0:127, 0:128].rearrange("p (h d) -> p h d", h=2)
                head_norm_rope(P[bo][0:127, 0:128].rearrange("p (h d) -> p h d", h=2), [PK[bo]], 2, 64, nkct, "ngain", 0, 8, cosC[0:127], sinC[0:127], d, "drb1", np_=127)
                b2 = nb(); pb = P[b2][:, :].bitcast(BF16)
                for g_ in range(2):
                    tp(pb[0:64, g_ * 128:g_ * 128 + 127], drb[1][0:127, g_ * 64:(g_ + 1) * 64], ident[0:127, 0:127], r=["drb1", "ident"], w=[PK[b2]])
                act(kcmpT[0:64, :, 0:127], pb[0:64, 0:256].rearrange("p (g t) -> p g t", g=2)[:, :, 0:127], AF.Copy, r=[PK[b2]], w=["kcmpT"])
            else:
                act(VCX[0:127, :, 0:64], P[bo][0:127, 0:128].rearrange("p (g d) -> p g d", g=2), AF.Copy, r=[PK[bo]], w=["VCXc"])
        dbg_out("kcmpT", kcmpT[0:64].rearrange("p g t -> p (g t)"), ["kcmpT"])
        dbg_out("VCX", VCX[0:127].rearrange("p a b -> p (a b)"), ["VCXa", "VCXb", "VCXc"])
        if upto <= 6:
            return finish(nc, S, st)
        S.barrier()

        M.lo = lo2; M.hi = hi0
        onsa = M.hi_alloc([NT, 512], F32)
        mbT = M.hi_alloc([2, S_], BF16)
        vmT = M.hi_alloc([S_], BF16); XE = M.hi_alloc([NT, 128], BF16)
        fb = M.hi_alloc([NT, 32], F32); vj = M.hi_alloc([NT, 32], F32)
        pc = [M.lo_alloc([4, 128], BF16) for _ in range(2)]
        rsum = M.lo_alloc([8], F32); rec8 = M.lo_alloc([8], F32); gr = M.lo_alloc([8], F32)
        tmp_i = M.lo_alloc([8, 32], F32); imp = M.lo_alloc([2, 32], F32); m8 = M.lo_alloc([2, 8], F32)
        sel = M.lo_alloc([2, 32], F32); mbf = M.lo_alloc([2, 32], BF16)
        tmpo = M.lo_alloc([8, 64], F32)
        pw = [M.lo_alloc([3, 128], BF16) for _ in range(3)]
        onb = [M.lo_alloc([512], BF16) for _ in range(2)]
        S.dma("sp", vmT[0:127], vmT_d, w=["vmT"]); S.dma("sp", XE[0:32].rearrange("p a b -> p (a b)"), XE_d, w=["XE"])
        S.dma("sp", fb.rearrange("p a b -> p (a b)"), fb_d, w=["fb"]); S.dma("sp", vj.rearrange("p a b -> p (a b)"), vj_d, w=["vj"])
        VCXk = ["VCXa", "VCXb", "VCXc"]
        for i in range(NT):
            ts = slice(i * 128, (i + 1) * 128)
            import os
            if int(os.environ.get('KDEV_F', '9')) <= 0:
                continue
            sb_ = [nb(), nb()]
            ob = [nb(), nb()]
            for p in range(8):
                j, g = p // 2, p % 2
                if os.environ.get('KDEV_G0'):
                    g = 0
                mm(P[sb_[p // 4]][0:127, (p % 4) * 128:(p % 4 + 1) * 128], kcmpT[0:64, g, 0:127], qnT[0:64, p, ts], True, True,
                   r=["kcmpT", ("qnT", i)], w=[PK[sb_[p // 4]]])
            for hf_ in range(2):
                pcv = pc[hf_]
                act(pcv[0:127], P[sb_[hf_]][0:127, :].rearrange("p (a b) -> p a b", a=4), AF.Exp, r=[PK[sb_[hf_]]], w=[("pc", hf_)], scale=0.125)
                S.op("dve", lambda e, pcv=pcv: e.tensor_tensor(out=pcv[0:127], in0=pcv[0:127], in1=vmT[0:127, ts].unsqueeze(1).to_broadcast([127, 4, 128]), op=ALU.mult),
                     r=[("pc", hf_), "vmT"], w=[("pc", hf_)])
            import os
            FL = int(os.environ.get('KDEV_F', '9'))
            if FL <= 1:
                continue
            for p in range(8):
                g = p % 2
                mm(P[ob[p // 4]][:, (p % 4) * 97:(p % 4 + 1) * 97], pc[p // 4][0:127, p % 4, :], VCX[0:127, g, :], True, True,
                   r=[("pc", p // 4)] + VCXk, w=[PK[ob[p // 4]]])
            if FL <= 2:
                continue
            OC = [P[ob[h_]][:, 0:388].rearrange("p (a b) -> p a b", a=4) for h_ in range(2)]
            for h_ in range(2):
                S.op("dve", lambda e, h_=h_: e.tensor_scalar(out=rsum[:, h_ * 4:(h_ + 1) * 4], in0=OC[h_][:, :, 64], scalar1=1e-30, scalar2=None, op0=ALU.max), r=[PK[ob[h_]]], w=["rsum"])
            S.op("dve", lambda e: e.reciprocal(out=rec8, in_=rsum), r=["rsum"], w=["rec8"])
            S.op("dve", lambda e, i=i: e.tensor_tensor(out=gr, in0=gates[:, i, 0, :], in1=rec8, op=ALU.mult), r=["rec8", ("gates", i)], w=["gr"])
            for h_ in range(2):
                S.op("dve", lambda e, h_=h_, i=i: e.tensor_tensor(out=onsa[:, i, h_ * 256:(h_ + 1) * 256].rearrange("p (a b) -> p a b", a=4), in0=OC[h_][:, :, 0:64],
                                                                   in1=gr[:, h_ * 4:(h_ + 1) * 4].unsqueeze(2).to_broadcast([128, 4, 64]), op=ALU.mult),
                     r=[PK[ob[h_]], "gr"], w=[("onsa", i)])
                S.op("dve", lambda e, h_=h_: e.tensor_tensor(out=tmp_i[:, h_ * 4:(h_ + 1) * 4, :], in0=OC[h_][:, :, 65:97],
                                                             in1=rec8[:, h_ * 4:(h_ + 1) * 4].unsqueeze(2).to_broadcast([128, 4, 32]), op=ALU.mult),
                     r=[PK[ob[h_]], "rec8"], w=["tmp_i"])
            if FL <= 3:
                continue
            S.op("dve", lambda e: e.tensor_reduce(out=imp, in_=tmp_i.rearrange("t (j g) n -> t g n j", g=2), axis=AX.X, op=ALU.add), r=["tmp_i"], w=["imp"])
            S.op("dve", lambda e, i=i: e.tensor_tensor(out=imp, in0=imp, in1=fb[:, i, :].unsqueeze(1).to_broadcast([128, 2, 32]), op=ALU.add), r=["imp", "fb"], w=["imp"])
            if FL <= 4:
                continue
            for g in range(2):
                S.op("dve", lambda e, g=g: e.max(out=m8[:, g, :], in_=imp[:, g, :]), r=["imp"], w=["m8"])
                S.op("dve", lambda e, g=g: e.tensor_scalar(out=sel[:, g, :], in0=imp[:, g, :], scalar1=m8[:, g, 7:8], scalar2=None, op0=ALU.is_ge), r=["imp", "m8"], w=["sel"])
            S.op("dve", lambda e, i=i: e.tensor_tensor(out=sel, in0=sel, in1=vj[:, i, :].unsqueeze(1).to_broadcast([128, 2, 32]), op=ALU.mult), r=["sel", "vj"], w=["sel"])
            S.op("dve", lambda e: e.tensor_scalar(out=mbf, in0=sel, scalar1=-1.0, scalar2=30000.0, op0=ALU.add, op1=ALU.mult), r=["sel"], w=["mbf"])
            if i == 5:
                dbg_out("sel5", sel.rearrange("p a b -> p (a b)"), ["sel"])
                dbg_out("imp5", imp.rearrange("p a b -> p (a b)"), ["imp"])
            if FL <= 5:
                continue
            b = nb(); pb = P[b][:, :].bitcast(BF16)
            for g in range(2):
                tp(pb[0:32, g * 128:(g + 1) * 128], mbf[:, g, :], ident, r=["mbf", "ident"], w=[PK[b]])
            act(mbT[0:32, :, ts], pb[0:32, 0:256].rearrange("p (g t) -> p g t", g=2), AF.Copy, r=[PK[b]], w=[("mbT", i)])
        dbg_out("onsa_c", onsa.rearrange("p a b -> p (a b)"), [("onsa", i) for i in range(NT)])
        dbg_out("mbT", mbT[0:32].rearrange("p a b -> p (a b)"), [("mbT", i) for i in range(NT)])
        if upto <= 7:
            return finish(nc, S, st)

        S.barrier()
        nbanks[0] = 6
        jobs = []
        for p in range(8):
            j, g = p // 2, p % 2

            def extra(ps_ap, kt, a, b_, pk, g=g):
                mm(ps_ap, XE[0:32, kt, :], mbT[0:32, g, a:b_], False, True, r=["XE"] + [("mbT", q) for q in range(a // 128, b_ // 128)], w=[pk])

            def fin(c, Oacc, pk, p=p):
                rc = rec4[c % 2]
                S.op("dve", lambda e: e.reciprocal(out=rc, in_=Oacc[:, :, 64]), r=[pk], w=[("rec4", c % 2)])
                S.op("dve", lambda e: e.tensor_tensor(out=rc, in0=rc, in1=gates[:, 4 * c:4 * c + 4, 1, p], op=ALU.mult), r=[("rec4", c % 2)] + [("gates", 4 * c + q) for q in range(4)], w=[("rec4", c % 2)])
                tv = tmpo[:, 0:4, :]
                S.op("dve", lambda e: e.tensor_tensor(out=tv, in0=Oacc[:, :, 0:64], in1=rc.unsqueeze(2).to_broadcast([128, 4, 64]), op=ALU.mult), r=[pk, ("rec4", c % 2)], w=["tmpo"])
                S.op("pool", lambda e: e.tensor_tensor(out=onsa[:, 4 * c:4 * c + 4, p * 64:(p + 1) * 64], in0=onsa[:, 4 * c:4 * c + 4, p * 64:(p + 1) * 64], in1=tv, op=ALU.add),
                     r=["tmpo"] + [("onsa", 4 * c + q) for q in range(4)], w=[("onsa", 4 * c + q) for q in range(4)])
            jobs.append(dict(KT=ksT[:, g, :], kn=64, QT=(lambda a, b_, p=p: qnT[0:64, p, a:b_]), V=(lambda kt, g=g: vs[:, kt, g, :]), scale=0.125,
                             extra=extra, fin=fin, qk=(lambda c: [("qnT", 4 * c + q) for q in range(4)]), kk=(lambda kt: [("ksT", kt)]),
                             vk=(lambda kt: [("vs", kt), "vsones"])))
        causal_attn_multi(jobs)
        dbg_out("onsa_cs", onsa.rearrange("p a b -> p (a b)"), [("onsa", i) for i in range(NT)])
        if upto <= 8:
            return finish(nc, S, st)

        pw_i = [0]
        ob = [6, 7]

        def emit_ws(i, p):
            g = p % 2
            kts = [kt for kt in (i - 2, i - 1, i) if kt >= 0]
            b = nb()
            for kt in kts:
                sl = kt - (i - 2)
                mm(P[b][:, sl * 128:(sl + 1) * 128], kwT[0:64, g, kt * 128:(kt + 1) * 128], qnT[0:64, p, i * 128:(i + 1) * 128], True, True,
                   r=[("kwT", kt), ("qnT", i)], w=[PK[b]])
            return b

        wsteps = [(i, p) for i in range(NT) for p in range(8)]
        wpend = emit_ws(*wsteps[0])
        for wi_, (i, p) in enumerate(wsteps):
            ts = slice(i * 128, (i + 1) * 128)
            g = p % 2
            kts = [kt for kt in (i - 2, i - 1, i) if kt >= 0]
            b = wpend
            if wi_ + 1 < len(wsteps):
                wpend = emit_ws(*wsteps[wi_ + 1])
            s0 = kts[0] - (i - 2)
            wi = pw_i[0]; pw_i[0] = (wi + 1) % 3
            pwv = pw[wi]
            act(pwv[:, s0:3, :], P[b][:, s0 * 128:384].rearrange("p (a b) -> p a b", b=128), AF.Exp, r=[PK[b]], w=[("pw", wi)], scale=0.125)
            S.op("dve", lambda e, pwv=pwv, s0=s0: e.tensor_tensor(out=pwv[:, s0:3, :], in0=pwv[:, s0:3, :], in1=winm[:, s0:3, :], op=ALU.mult), r=[("pw", wi), "winm"], w=[("pw", wi)])
            for kt in kts:
                sl = kt - (i - 2)
                mm(P[ob[p // 4]][:, (p % 4) * 65:(p % 4 + 1) * 65], pwv[:, sl, :], vw[:, kt, g, :], kt == kts[0], kt == kts[-1],
                   r=[("pw", wi), ("vw", kt), "vwones"], w=[PK[ob[p // 4]]])
            if p != 7:
                continue
            OW = [P[ob[h_]][:, 0:260].rearrange("p (a b) -> p a b", a=4) for h_ in range(2)]
            for h_ in range(2):
                S.op("dve", lambda e, h_=h_: e.reciprocal(out=rec8[:, h_ * 4:(h_ + 1) * 4], in_=OW[h_][:, :, 64]), r=[PK[ob[h_]]], w=["rec8"])
            S.op("dve", lambda e, i=i: e.tensor_tensor(out=gr, in0=gates[:, i, 2, :], in1=rec8, op=ALU.mult), r=["rec8", ("gates", i)], w=["gr"])
            for h_ in range(2):
                S.op("dve", lambda e, h_=h_: e.tensor_tensor(out=tmpo[:, h_ * 4:(h_ + 1) * 4, :], in0=OW[h_][:, :, 0:64],
                                                             in1=gr[:, h_ * 4:(h_ + 1) * 4].unsqueeze(2).to_broadcast([128, 4, 64]), op=ALU.mult), r=[PK[ob[h_]], "gr"], w=["tmpo"])
            S.op("pool", lambda e, i=i: e.tensor_tensor(out=onb[i % 2], in0=onsa[:, i, :], in1=tmpo.rearrange("p a b -> p (a b)"), op=ALU.add), r=["tmpo", ("onsa", i)], w=[("onb", i % 2)])
            if "onsa_all" in D_:
                S.dma("sp", D_["onsa_all"][:, i * 512:(i + 1) * 512], onb[i % 2], r=[("onb", i % 2)])
            b = nb(); pb = P[b][:, :].bitcast(BF16)
            for j in range(4):
                tp(pb[:, j * 128:(j + 1) * 128], onb[i % 2][:, j * 128:(j + 1) * 128], ident, r=[("onb", i % 2), "ident"], w=[PK[b]])
            act(onT[:, :, ts], pb[:, 0:512].rearrange("p (k t) -> p k t", k=4), AF.Copy, r=[PK[b]], w=[("onT", i)])
        if upto <= 9:
            return finish(nc, S, st)
        S.barrier()

        nbanks[0] = 8
        M.lo = lo0; M.hi = hi0
        hT = M.hi_alloc([8, S_], BF16)
        mergedT = M.hi_alloc([8, S_], BF16)
        hi2 = M.hi
        Wgm = M.lo_alloc([8, 512], BF16); Wgnm = M.lo_alloc([8, 512], BF16); Wom = M.lo_alloc([4, 512], BF16); Won = M.lo_alloc([4, 512], BF16)
        e3 = M.lo_alloc([512], F32); e4 = M.lo_alloc([512], F32); tA = M.lo_alloc([512], F32); tB = M.lo_alloc([512], F32)
        mgb = [M.lo_alloc([512], BF16) for _ in range(2)]
        S.dma("sp", hT.rearrange("p k t -> p (k t)"), hTs, r=["hTs"], w=["hTall"])
        e3 = [e3, M.lo_alloc([512], F32)]; e4 = [e4, M.lo_alloc([512], F32)]
        tA = [tA, M.lo_alloc([512], F32)]; tB = [tB, M.lo_alloc([512], F32)]

        def i_front(cc, i, it):
            ts = slice(i * 128, (i + 1) * 128)
            bs = [4 * (it % 2) + q for q in range(4)]
            b1, b2, b3, b4 = bs
            for k in range(8):
                mm(P[b3][:, :], hT[:, k, ts], Wgm[:, k, :], k == 0, k == 7, r=["hTall", "Wgm"], w=[PK[b3]])
            for k in range(8):
                mm(P[b4][:, :], hT[:, k, ts], Wgnm[:, k, :], k == 0, k == 7, r=["hTall", "Wgnm"], w=[PK[b4]])
            for k in range(4):
                mm(P[b1][:, :], omT[:, k, ts], Wom[:, k, :], k == 0, k == 3, r=[("omT", i), "Wom"], w=[PK[b1]])
            for k in range(4):
                mm(P[b2][:, :], onT[:, k, ts], Won[:, k, :], k == 0, k == 3, r=[("onT", i), "Won"], w=[PK[b2]])
            return bs

        def i_back(cc, i, it, bs):
            ts = slice(i * 128, (i + 1) * 128)
            b1, b2, b3, b4 = bs
            par = it % 2
            act(e3[par], P[b3][:, :], AF.Sigmoid, r=[PK[b3]], w=[("e3", par)])
            act(e4[par], P[b4][:, :], AF.Sigmoid, r=[PK[b4]], w=[("e4", par)])
            S.op("dve", lambda e: e.tensor_tensor(out=tA[par], in0=P[b1][:, :], in1=e3[par], op=ALU.mult), r=[PK[b1], ("e3", par)], w=[("tA", par)])
            S.op("dve", lambda e: e.tensor_tensor(out=tB[par], in0=P[b2][:, :], in1=e4[par], op=ALU.mult), r=[PK[b2], ("e4", par)], w=[("tB", par)])
            S.op("dve", lambda e: e.tensor_tensor(out=mgb[par], in0=tA[par], in1=tB[par], op=ALU.add), r=[("tA", par), ("tB", par)], w=[("mgb", par)])
            if "merged" in D_:
                S.dma("sp", D_["merged"][i * 128:(i + 1) * 128, cc * 512:(cc + 1) * 512], mgb[par], r=[("mgb", par)])
            pb = P[b3][:, :].bitcast(BF16)
            for j in range(4):
                tp(pb[:, j * 128:(j + 1) * 128], mgb[par][:, j * 128:(j + 1) * 128], ident, r=[("mgb", par), "ident"], w=[PK[b3]])
            act(mergedT[:, cc * 4:(cc + 1) * 4, ts], pb[:, 0:512].rearrange("p (k t) -> p k t", k=4), AF.Copy, r=[PK[b3]], w=[("mergedT", i)])

        it = 0
        for cc in range(2):
            load_w_cols(Wgm, w_gm, "Wgm", 8, cc * 512, (cc + 1) * 512); load_w_cols(Wgnm, w_gnm, "Wgnm", 8, cc * 512, (cc + 1) * 512)
            load_w_cols(Wom, wo_mla, "Wom", 4, cc * 512, (cc + 1) * 512); load_w_cols(Won, wo_nsa, "Won", 4, cc * 512, (cc + 1) * 512)
            pend_ = i_front(cc, 0, it)
            for i in range(NT):
                cur_ = pend_
                if i + 1 < NT:
                    pend_ = i_front(cc, i + 1, it + 1)
                i_back(cc, i, it, cur_)
                it += 1
        if upto <= 10:
            return finish(nc, S, st)
        S.barrier()

        M.lo = lo0
        h2T = hT
        Wout = M.lo_alloc([8, D], BF16)
        G1 = M.lo_alloc([D], F32); A2 = M.lo_alloc([D], F32); B2 = M.lo_alloc([D], F32)
        xb = [M.lo_alloc([D], F32) for _ in range(2)]
        x1t = [M.lo_alloc([D], F32) for _ in range(2)]
        junk = M.lo_alloc([D], F32); tmpA = M.lo_alloc([D], F32)
        hb = [M.lo_alloc([D], BF16) for _ in range(2)]
        ssq = M.lo_alloc([NT], F32); rs = M.lo_alloc([NT], F32)
        S.dma("sp", G1, mods[:, 2 * D:3 * D], r=["mods"], w=["G1"])
        S.dma("sp", B2, mods[:, 3 * D:4 * D], r=["mods"], w=["AB2"])
        S.dma("sp", A2, mods[:, 4 * D:5 * D], r=["mods"], w=["AB2"])
        load_w(Wout, w_out, "Wout", 8, D, ceng="pool")
        tmpA2 = [tmpA, M.lo_alloc([D], F32)]
        tmpJ = [M.lo_alloc([D], F32) for _ in range(3)]
        xb = xb + [M.lo_alloc([D], F32)]
        x1t = x1t + [M.lo_alloc([D], F32)]

        def j_front(i):
            ts = slice(i * 128, (i + 1) * 128)
            par = i % 3
            S.dma("sp", xb[par], x[ts, :], w=[("xb", par)])
            for cc in range(2):
                b = 2 * par + cc
                for k in range(8):
                    mm(P[b][:, :], mergedT[:, k, ts], Wout[:, k, cc * 512:(cc + 1) * 512], k == 0, k == 7, r=[("mergedT", i), "Wout"], w=[PK[b]])

        def j_mid(i):
            ts = slice(i * 128, (i + 1) * 128)
            par = i % 3
            for cc in range(2):
                b = 2 * par + cc
                S.op("dve", lambda e, b=b, cc=cc: e.tensor_tensor(out=tmpJ[par][:, cc * 512:(cc + 1) * 512], in0=P[b][:, :], in1=G1[:, cc * 512:(cc + 1) * 512], op=ALU.mult), r=[PK[b], "G1"], w=[("tmpJ", par)])
                S.op("pool", lambda e, cc=cc: e.tensor_tensor(out=x1t[par][:, cc * 512:(cc + 1) * 512], in0=tmpJ[par][:, cc * 512:(cc + 1) * 512], in1=xb[par][:, cc * 512:(cc + 1) * 512], op=ALU.add),
                     r=[("tmpJ", par), ("xb", par)], w=[("x1t", par)])
            S.dma("sp", x1s[ts, :], x1t[par], r=[("x1t", par)], w=[("x1s", i)])
            norm_a(i, x1t[par], ("x1t", par))

        bank_state[0] = 0

        def nbJ():
            b = 6 + (bank_state[0] % 2)
            bank_state[0] = (bank_state[0] + 1) % 2
            return b
        nb_saved = nb
        nb = nbJ
        j_front(0); j_mid(0); j_front(1); j_mid(1)
        for i in range(NT):
            if i + 2 < NT:
                j_front(i + 2); j_mid(i + 2)
            norm_b(i, x1t[i % 3], ("x1t", i % 3), A2, B2, ["AB2"], h2T, "h2T")
        nb = nb_saved
        bank_state[0] = 0
        if "x1" in D_:
            S.dma("sp", D_["x1"], x1s, r=[("x1s", i) for i in range(NT)])
        if upto <= 11:
            return finish(nc, S, st)
        S.barrier()

        M.lo = lo_pers; M.hi = hi2 + 8 * S_ * 2
        Wd = M.hi_alloc([NFC, D], BF16)
        actT = M.hi_alloc([NFC, 1024], BF16)
        G2 = M.lo_alloc([D], F32)
        Wg2 = [M.lo_alloc([8, 256], BF16) for _ in range(2)]; Wu2 = [M.lo_alloc([8, 256], BF16) for _ in range(2)]
        sg = [M.lo_alloc([512], F32) for _ in range(2)]
        xb = [M.lo_alloc([D], F32) for _ in range(2)]
        ot = [M.lo_alloc([D], F32) for _ in range(2)]
        tmpA = M.lo_alloc([D], F32)
        S.dma("sp", G2, mods[:, 5 * D:6 * D], r=["mods"], w=["G2"])
        load_w(Wd, wd, "Wd", NFC, D)
        h2k = [("h2T", i) for i in range(NT)]
        out_toks = []
        for half in range(2):
            def ld(jg_):
                wb_ = jg_ % 2
                load_w_cols(Wg2[wb_], wg, ("Wg2", wb_), 8, jg_ * 256, (jg_ + 1) * 256)
                load_w_cols(Wu2[wb_], wu, ("Wu2", wb_), 8, jg_ * 256, (jg_ + 1) * 256)
            ld(0)
            for jg in range(NFC // 2):
                wb = jg % 2
                if jg + 1 < NFC // 2:
                    ld(jg + 1)
                for jj in range(2):
                    j = jg * 2 + jj
                    for tc in range(2):
                        t0 = half * 1024 + tc * 512
                        bg, bu = nb(), nb()
                        for k in range(8):
                            mm(P[bg][:, :], Wg2[wb][:, k, jj * 128:(jj + 1) * 128], h2T[:, k, t0:t0 + 512], k == 0, k == 7, r=[("Wg2", wb)] + h2k, w=[PK[bg]])
                        for k in range(8):
                            mm(P[bu][:, :], Wu2[wb][:, k, jj * 128:(jj + 1) * 128], h2T[:, k, t0:t0 + 512], k == 0, k == 7, r=[("Wu2", wb)] + h2k, w=[PK[bu]])
                        act(sg[tc], P[bg][:, :], AF.Silu, r=[PK[bg]], w=[("sg", tc)])
                        S.op("dve", lambda e, j=j, tc=tc, bu=bu: e.tensor_tensor(out=actT[:, j, tc * 512:(tc + 1) * 512], in0=P[bu][:, :], in1=sg[tc], op=ALU.mult),
                             r=[PK[bu], ("sg", tc)], w=[("actT", tc)])
            for il in range(8):
                i = half * 8 + il
                ts = slice(i * 128, (i + 1) * 128)
                S.dma("sp", xb[i % 2], x1s[ts, :], r=[("x1s", i)], w=[("xb", i % 2)])
                for cc in range(2):
                    b = nb()
                    for j in range(NFC):
                        mm(P[b][:, :], actT[:, j, il * 128:(il + 1) * 128], Wd[:, j, cc * 512:(cc + 1) * 512], j == 0, j == NFC - 1, r=[("actT", il // 4), "Wd"], w=[PK[b]])
                    S.op("dve", lambda e, b=b, cc=cc: e.tensor_tensor(out=tmpA[:, cc * 512:(cc + 1) * 512], in0=P[b][:, :], in1=G2[:, cc * 512:(cc + 1) * 512], op=ALU.mult), r=[PK[b], "G2"], w=["tmpA"])
                    S.op("pool", lambda e, i=i, cc=cc: e.tensor_tensor(out=ot[i % 2][:, cc * 512:(cc + 1) * 512], in0=tmpA[:, cc * 512:(cc + 1) * 512], in1=xb[i % 2][:, cc * 512:(cc + 1) * 512], op=ALU.add),
                         r=["tmpA", ("xb", i % 2)], w=[("ot", i % 2)])
                S.dma("sp", out[ts, :], ot[i % 2], r=[("ot", i % 2)], w=[("out", i)])
        return finish(nc, S, st)


def finish(nc, S, st):
    for q in S.dsem:
        for i in range(len(S.dsem[q])):
            if S.dcnt[q][i]:
                S._wait("sp", (("d", q, i), S.dcnt[q][i]))
    for e2 in ("pe", "act", "dve", "pool"):
        if S.cnt[e2]:
            S._wait("sp", (e2, S.cnt[e2]))
    st.close()
    return nc


def _consts():
    bf = ml_dtypes.bfloat16
    c = {}
    c["ident"] = np.eye(128, dtype=np.float32).astype(bf)
    a = np.arange(128)
    c["tri"] = (a[:, None] <= a[None, :]).astype(np.float32).astype(bf)
    w = np.zeros((128, 3, 128), np.float32)
    w[:, 0, :] = (a[:, None] > a[None, :])
    w[:, 1, :] = 1.0
    w[:, 2, :] = (a[:, None] <= a[None, :])
    c["winm"] = w.reshape(128, 384).astype(bf)
    inv16 = (np.float32(500000.0) ** (-np.arange(0, 32, 2, dtype=np.float32) / np.float32(32))).astype(np.float32)
    inv8 = (np.float32(500000.0) ** (-np.arange(0, 16, 2, dtype=np.float32) / np.float32(16))).astype(np.float32)
    c["inv16"] = np.tile(inv16[None], (128, 1)).astype(np.float32)
    c["inv8"] = np.tile(inv8[None], (128, 1)).astype(np.float32)
    n = np.arange(127)
    starts = n * 16
    j = np.arange(32)
    ovl = ((starts[:, None] < j[None, :] * 64 + 64) & (starts[:, None] + 32 > j[None, :] * 64))
    c["ovl"] = ovl.astype(np.float32).astype(bf)
    t = np.arange(S_)
    c["vmT"] = ((starts[:, None] + 31) <= t[None, :]).astype(np.float32).astype(bf)
    XE = np.zeros((32, NT, 128), np.float32)
    for kt in range(NT):
        XE[2 * kt, kt, 0:64] = 1.0
        XE[2 * kt + 1, kt, 64:128] = 1.0
    c["XE"] = XE.reshape(32, NT * 128).astype(bf)
    cur = (t // 64)
    forced = (j[None, :] == 0) | (j[None, :] == cur[:, None]) | (j[None, :] == cur[:, None] - 1)
    valid = j[None, :] <= cur[:, None]
    fb = np.where(valid, np.where(forced, 1e4, 0.0), -1e30).astype(np.float32)
    c["fb"] = fb.reshape(NT, 128, 32).transpose(1, 0, 2).reshape(128, NT * 32).copy()
    c["vj"] = valid.astype(np.float32).reshape(NT, 128, 32).transpose(1, 0, 2).reshape(128, NT * 32).copy()
    return c


def _rep(v, n=128):
    return np.ascontiguousarray(np.broadcast_to(np.asarray(v, np.float32)[None, :], (n, v.shape[0])))


def prep_inputs(inp):
    f = lambda a: np.ascontiguousarray(np.asarray(a, dtype=np.float32))
    w_in = f(inp["w_in"][0])
    o = np.cumsum([0, 768, 256, 32, 512, 128, 128, 128, 128, 128, 128, 24, 1024, 1024])
    seg = lambda i: w_in[:, o[i]:o[i + 1]]
    shared = {}
    shared["ada_w"] = f(inp["ada_w"][0]); shared["adabB"] = _rep(f(inp["ada_b"][0]))
    shared["g1B"] = _rep(f(inp["norm1_gain"][0])); shared["g2B"] = _rep(f(inp["norm2_gain"][0]))
    shared["w_cq"] = f(seg(0)); shared["w_ckv"] = f(seg(1)); shared["w_kpe"] = f(seg(2))
    qn = seg(3).reshape(D, 8, 64)
    shared["w_qn"] = f(qn[:, PH, :].reshape(D, 512))
    kc = seg(4).reshape(D, 2, 64); vc = seg(5).reshape(D, 2, 64)
    shared["w_kc2"] = f(np.stack([kc[:, 0], kc[:, 0], kc[:, 1], kc[:, 1]], 1).reshape(D, 256))
    shared["w_vc2"] = f(np.stack([vc[:, 0], vc[:, 0], vc[:, 1], vc[:, 1]], 1).reshape(D, 256))
    shared["w_kv4"] = f(np.concatenate([seg(6), seg(8), seg(7), seg(9)], 1))
    gn = seg(10).reshape(D, 8, 3)
    shared["w_gn"] = f(gn[:, PH, :].transpose(0, 2, 1).reshape(D, 24))
    shared["w_gm"] = f(seg(11)); shared["w_gnm"] = f(seg(12))
    shared["qag"] = f(f(inp["mla_q_a_gain"][0]).reshape(6, 128).T); shared["kvag"] = f(f(inp["mla_kv_a_gain"][0]).reshape(2, 128).T)
    shared["w_qb"] = f(inp["mla_w_q_b"][0]); shared["w_kvb"] = f(inp["mla_w_kv_b"][0])
    shared["qgB"] = _rep(f(inp["mla_q_gain"][0])); shared["kgB"] = _rep(f(inp["mla_k_gain"][0]))
    shared["nqB"] = _rep(f(inp["nsa_q_gain"][0])); shared["nkcB"] = _rep(f(inp["nsa_kc_gain"][0]))
    shared["nksB"] = _rep(f(inp["nsa_ks_gain"][0])); shared["nkwB"] = _rep(f(inp["nsa_kw_gain"][0]))
    shared["posk"] = f(f(inp["cmp_pos_k"][0]).reshape(16, 128).T); shared["posv"] = f(f(inp["cmp_pos_v"][0]).reshape(16, 128).T)
    shared["w1k"] = f(inp["cmp_w1_k"][0]); shared["w2k"] = f(inp["cmp_w2_k"][0])
    shared["w1v"] = f(inp["cmp_w1_v"][0]); shared["w2v"] = f(inp["cmp_w2_v"][0])
    shared["wo_mla"] = f(inp["w_o_mla"][0])
    shared["wo_nsa"] = f(f(inp["w_o_nsa"][0]).reshape(8, 64, D)[PH].reshape(512, D))
    shared["w_out"] = f(inp["w_out"][0])
    shared["wg"] = f(inp["ffn_w_gate"][0]); shared["wu"] = f(inp["ffn_w_up"][0]); shared["wd"] = f(inp["ffn_w_down"][0])
    shared.update(_consts())
    maps = []
    xs = np.asarray(inp["x"], np.float32); cs = np.asarray(inp["c"], np.float32); ps = np.asarray(inp["positions"]).astype(np.int32)
    for b in range(xs.shape[0]):
        m = dict(shared)
        m["x"] = np.ascontiguousarray(xs[b])
        m["c_pk"] = np.ascontiguousarray(cs[b].reshape(8, 128).T)
        m["pos_pk"] = np.ascontiguousarray(ps[b].reshape(NT, 128).T)
        m["posC"] = np.ascontiguousarray(ps[b][31::16][:127].reshape(127, 1))
        maps.append(m)
    return maps


_NC_CACHE = {}


def kernel(**inputs):
    maps = prep_inputs(inputs)
    if "nc" not in _NC_CACHE:
        _NC_CACHE["nc"] = build()
    nc = _NC_CACHE["nc"]
    res = run_bass_kernel_spmd(nc, maps, core_ids=list(range(len(maps))))
    return np.stack([np.asarray(r["out"], dtype=np.float32) for r in res.results], 0)
```

```python
import contextlib
import numpy as np
import ml_dtypes
import concourse.bass as bass
import concourse.mybir as mybir
from concourse.bass_utils import run_bass_kernel_spmd

F32 = mybir.dt.float32
BF16 = mybir.dt.bfloat16
I32 = mybir.dt.int32
ALU = mybir.AluOpType
AF = mybir.ActivationFunctionType
AX = mybir.AxisListType

S_ = 2048
D = 1024
NT = 16
DFF = 2816
NFC = 22
EPS = 1e-6
PH = [0, 4, 1, 5, 2, 6, 3, 7]
TWO_PI = float(2 * np.pi)
PI = float(np.pi)


class Sched:
    N_DMA_SLOTS = {"sp": 24, "pool": 8, "act": 4}

    def __init__(self, nc, stack):
        self.nc = nc
        self.E = {"pe": nc.tensor, "act": nc.scalar, "dve": nc.vector, "pool": nc.gpsimd, "sp": nc.sync}
        self.sem, self.cnt = {}, {}
        for e in ("pe", "act", "dve", "pool"):
            self.sem[e] = stack.enter_context(nc.semaphore("s_" + e))
            self.cnt[e] = 0
        self.dsem, self.dcnt, self.dnext = {}, {}, {}
        for q, n in self.N_DMA_SLOTS.items():
            self.dsem[q] = [stack.enter_context(nc.semaphore(f"d_{q}{i}")) for i in range(n)]
            self.dcnt[q] = [0] * n
            self.dnext[q] = 0
        self.seen = {e: {} for e in self.E}
        self.lastw, self.readers = {}, {}
        self.n_wait = 0
        self.n_inst = 0

    def _sem_of(self, src):
        return self.dsem[src[1]][src[2]] if isinstance(src, tuple) else self.sem[src]

    def _wait(self, e, tok):
        src, val = tok
        if self.seen[e].get(src, 0) >= val:
            return
        self.E[e].wait_ge(self._sem_of(src), val)
        self.seen[e][src] = val
        self.n_wait += 1

    def _deps(self, e, r, w):
        toks = []
        for k in r:
            t = self.lastw.get(k)
            if t is not None:
                toks.append(t)
        for k in w:
            t = self.lastw.get(k)
            if t is not None:
                toks.append(t)
            for t in self.readers.get(k, ()):
                toks.append(t)
        for t in toks:
            if t[0] == e and e == "pe":
                continue
            self._wait(e, t)

    def _commit(self, tok, r, w):
        for k in r:
            lst = self.readers.setdefault(k, [])
            lst[:] = [t for t in lst if t[0] != tok[0]]
            lst.append(tok)
        for k in w:
            self.lastw[k] = tok
            self.readers[k] = []

    def op(self, e, fn, r=(), w=()):
        self._deps(e, r, w)
        ins = fn(self.E[e])
        self.cnt[e] += 1
        ins.then_inc(self.sem[e], 1)
        tok = (e, self.cnt[e])
        self._commit(tok, r, w)
        self.n_inst += 1
        return tok

    def dma(self, q, out, in_, r=(), w=(), **kw):
        slot = self.dnext[q]
        self.dnext[q] = (slot + 1) % len(self.dsem[q])
        src = ("d", q, slot)
        if self.dcnt[q][slot] > 0:
            self._wait(q, (src, self.dcnt[q][slot]))
        self._deps(q, r, w)
        ins = self.E[q].dma_start(out=out, in_=in_, **kw)
        self.dcnt[q][slot] += 16
        ins.then_inc(self.dsem[q][slot], 16)
        tok = (src, self.dcnt[q][slot])
        self._commit(tok, r, w)
        self.n_inst += 1
        return tok

    def barrier(self):
        for e in ("pe", "act", "dve", "pool", "sp"):
            for e2 in ("pe", "act", "dve", "pool"):
                if self.cnt[e2] and not (e2 == e == "pe"):
                    self._wait(e, (e2, self.cnt[e2]))
            for q in self.dsem:
                for i in range(len(self.dsem[q])):
                    if self.dcnt[q][i]:
                        self._wait(e, (("d", q, i), self.dcnt[q][i]))
        self.lastw.clear()
        self.readers.clear()


class Mem:
    def __init__(self, big, nbytes):
        self.big, self.lo, self.hi, self.n = big, 0, nbytes, nbytes

    def _view(self, off, shape, dt):
        nel = int(np.prod(shape))
        esz = 4 if dt in (F32, I32) else 2
        nb = nel * esz
        ap = self.big[:, off // 2:(off + nb) // 2]
        if esz == 4:
            ap = ap.bitcast(dt)
        if len(shape) == 2:
            ap = ap.rearrange("p (a b) -> p a b", a=shape[0])
        elif len(shape) == 3:
            ap = ap.rearrange("p (a b c) -> p a b c", a=shape[0], b=shape[1])
        return ap

    def lo_alloc(self, shape, dt):
        nb = int(np.prod(shape)) * (4 if dt in (F32, I32) else 2)
        nb = (nb + 63) // 64 * 64
        off = self.lo
        self.lo += nb
        assert self.lo <= self.hi, f"SBUF overflow lo={self.lo} hi={self.hi}"
        return self._view(off, shape, dt)

    def hi_alloc(self, shape, dt):
        nb = int(np.prod(shape)) * (4 if dt in (F32, I32) else 2)
        nb = (nb + 63) // 64 * 64
        self.hi -= nb
        assert self.lo <= self.hi, f"SBUF overflow lo={self.lo} hi={self.hi}"
        return self._view(self.hi, shape, dt)


def build(upto=99, dbg=()):
    nc = bass.Bass("TRN2", target_bir_lowering=False)
    I = {}

    def din(name, shape, dt=F32):
        I[name] = nc.dram_tensor(name, list(shape), dt, kind="ExternalInput").ap()
        return I[name]

    x = din("x", [S_, D]); c_pk = din("c_pk", [128, 8]); pos_pk = din("pos_pk", [128, NT], I32)
    posC = din("posC", [127, 1], I32)
    ada_w = din("ada_w", [D, 6 * D]); adabB = din("adabB", [128, 6 * D]); g1B = din("g1B", [128, D]); g2B = din("g2B", [128, D])
    w_cq = din("w_cq", [D, 768]); w_ckv = din("w_ckv", [D, 256]); w_kpe = din("w_kpe", [D, 32])
    w_qn = din("w_qn", [D, 512]); w_kc2 = din("w_kc2", [D, 256]); w_vc2 = din("w_vc2", [D, 256])
    w_kv4 = din("w_kv4", [D, 512]); w_gn = din("w_gn", [D, 24]); w_gm = din("w_gm", [D, D]); w_gnm = din("w_gnm", [D, D])
    qag = din("qag", [128, 6]); kvag = din("kvag", [128, 2])
    w_qb = din("w_qb", [768, 768]); w_kvb = din("w_kvb", [256, 1024])
    qgB = din("qgB", [128, 96]); kgB = din("kgB", [128, 96])
    nqB = din("nqB", [128, 64]); nkcB = din("nkcB", [128, 64]); nksB = din("nksB", [128, 64]); nkwB = din("nkwB", [128, 64])
    posk = din("posk", [128, 16]); w1k = din("w1k", [2048, 256]); w2k = din("w2k", [256, 64])
    posv = din("posv", [128, 16]); w1v = din("w1v", [2048, 256]); w2v = din("w2v", [256, 64])
    wo_mla = din("wo_mla", [512, D]); wo_nsa = din("wo_nsa", [512, D]); w_out = din("w_out", [D, D])
    wg = din("wg", [D, DFF]); wu = din("wu", [D, DFF]); wd = din("wd", [DFF, D])
    ident_d = din("ident", [128, 128], BF16); tri_d = din("tri", [128, 128], BF16); winm_d = din("winm", [128, 384], BF16)
    inv16_d = din("inv16", [128, 16]); inv8_d = din("inv8", [128, 8])
    ovl_d = din("ovl", [127, 32], BF16); vmT_d = din("vmT", [127, S_], BF16); XE_d = din("XE", [32, NT * 128], BF16)
    fb_d = din("fb", [128, NT * 32]); vj_d = din("vj", [128, NT * 32])
    out = nc.dram_tensor("out", [S_, D], F32, kind="ExternalOutput").ap()
    hTs = nc.dram_tensor("hTs", [128, 8 * S_], BF16).ap()
    mods = nc.dram_tensor("mods", [128, 6 * D], F32).ap()
    x1s = nc.dram_tensor("x1s", [S_, D], F32).ap()
    D_ = {}
    for name, shape, dt in dbg:
        D_[name] = nc.dram_tensor("dbg_" + name, list(shape), dt, kind="ExternalOutput").ap()

    st = contextlib.ExitStack()
    with st:
        S = Sched(nc, st)
        NB = 204800
        big = st.enter_context(nc.sbuf_tensor("big", [128, NB // 2], BF16))
        M = Mem(big, NB)
        P = [st.enter_context(nc.psum_tensor(f"ps{i}", [128, 512], F32)) for i in range(8)]
        PK = [f"ps{i}" for i in range(8)]
        bank_state = [0]

        nbanks = [8]

        def nb():
            b = bank_state[0] % nbanks[0]
            bank_state[0] = (b + 1) % nbanks[0]
            return b

        def mm(ps_ap, lhsT, rhs, start, stop, r, w, **kw):
            S.op("pe", lambda e: e.matmul(ps_ap, lhsT=lhsT, rhs=rhs, start=start, stop=stop, **kw), r=r, w=w)

        def tp(ps_ap, in_, ident_ap, r, w):
            S.op("pe", lambda e: e.transpose(out=ps_ap, in_=in_, identity=ident_ap), r=r, w=w)

        def act(out_, in_, func, r, w, **kw):
            S.op("act", lambda e: e.activation(out=out_, in_=in_, func=func, **kw), r=r, w=w)

        def dbg_out(name, ap, r):
            if name in D_:
                S.dma("sp", D_[name], ap, r=r)

        ident = M.lo_alloc([128], BF16); tri = M.lo_alloc([128], BF16); winm = M.lo_alloc([3, 128], BF16)
        onesb = M.lo_alloc([128], BF16)
        stg = [M.lo_alloc([1024], F32) for _ in range(3)]
        stg_i = [0]
        cosM = M.lo_alloc([NT, 16], F32); sinM = M.lo_alloc([NT, 16], F32)
        cosN = M.lo_alloc([NT, 8], F32); sinN = M.lo_alloc([NT, 8], F32)
        cosC = M.lo_alloc([8], F32); sinC = M.lo_alloc([8], F32)
        lo_pers = M.lo
        omT = M.lo_alloc([4, S_], BF16); onT = M.lo_alloc([4, S_], BF16)
        S.dma("sp", ident, ident_d, w=["ident"])
        S.dma("sp", tri, tri_d, w=["tri"])
        S.dma("sp", winm.rearrange("p a b -> p (a b)"), winm_d, w=["winm"])
        S.op("pool", lambda e: e.memset(onesb, 1.0), w=["onesb"])

        def load_w(dst, W, key, KC, N, ceng="dve"):
            Wv = W.rearrange("(k p) n -> p k n", p=128)
            if N <= 1024:
                g = max(1, min(KC, 1024 // N))
                for k0 in range(0, KC, g):
                    k1 = min(KC, k0 + g)
                    si = stg_i[0]; stg_i[0] = (si + 1) % 3
                    sv = stg[si][:, 0:(k1 - k0) * N].rearrange("p (k n) -> p k n", n=N)
                    S.dma("sp", sv, Wv[:, k0:k1, :], w=[("stg", si)])
                    S.op(ceng, lambda e, sv=sv, k0=k0, k1=k1: e.tensor_copy(out=dst[:, k0:k1, :], in_=sv), r=[("stg", si)], w=[key])
            else:
                for k in range(KC):
                    for c0 in range(0, N, 1024):
                        c1 = min(N, c0 + 1024)
                        si = stg_i[0]; stg_i[0] = (si + 1) % 3
                        sv = stg[si][:, 0:c1 - c0]
                        S.dma("sp", sv, Wv[:, k, c0:c1], w=[("stg", si)])
                        S.op(ceng, lambda e, sv=sv, k=k, c0=c0, c1=c1: e.tensor_copy(out=dst[:, k, c0:c1], in_=sv), r=[("stg", si)], w=[key])

        def load_w_cols(dst, W, key, KC, c0, c1, ceng="dve"):
            Wv = W.rearrange("(k p) n -> p k n", p=128)
            N = c1 - c0
            g = max(1, min(KC, 1024 // N))
            for k0 in range(0, KC, g):
                k1 = min(KC, k0 + g)
                si = stg_i[0]; stg_i[0] = (si + 1) % 3
                sv = stg[si][:, 0:(k1 - k0) * N].rearrange("p (k n) -> p k n", n=N)
                S.dma("sp", sv, Wv[:, k0:k1, c0:c1], w=[("stg", si)])
                S.op(ceng, lambda e, sv=sv, k0=k0, k1=k1: e.tensor_copy(out=dst[:, k0:k1, :], in_=sv), r=[("stg", si)], w=[key])

        def sincos(ang, shape, cos_o, sin_o, np_, tmp_f, tmp_i, tmp_m, key):
            for (shift, dst) in ((0.0, sin_o), (PI / 2, cos_o)):
                S.op("dve", lambda e: e.tensor_scalar(out=tmp_f, in0=ang, scalar1=shift, scalar2=None, op0=ALU.add), r=[key + "ang"], w=[key + "f"])
                S.op("dve", lambda e: e.tensor_scalar(out=tmp_i, in0=tmp_f, scalar1=float(1 / TWO_PI), scalar2=None, op0=ALU.mult), r=[key + "f"], w=[key + "i"])
                S.op("dve", lambda e: e.tensor_copy(out=tmp_m, in_=tmp_i), r=[key + "i"], w=[key + "m"])
                S.op("dve", lambda e: e.scalar_tensor_tensor(out=tmp_f, in0=tmp_m, scalar=-TWO_PI, in1=tmp_f, op0=ALU.mult, op1=ALU.add), r=[key + "m", key + "f"], w=[key + "f"])
                S.op("dve", lambda e: e.tensor_scalar(out=tmp_m, in0=tmp_f, scalar1=PI, scalar2=None, op0=ALU.is_gt), r=[key + "f"], w=[key + "m"])
                S.op("dve", lambda e: e.scalar_tensor_tensor(out=tmp_f, in0=tmp_m, scalar=-TWO_PI, in1=tmp_f, op0=ALU.mult, op1=ALU.add), r=[key + "m", key + "f"], w=[key + "f"])
                S.op("dve", lambda e: e.tensor_scalar(out=tmp_m, in0=tmp_f, scalar1=-PI, scalar2=None, op0=ALU.is_lt), r=[key + "f"], w=[key + "m"])
                S.op("dve", lambda e: e.scalar_tensor_tensor(out=tmp_f, in0=tmp_m, scalar=TWO_PI, in1=tmp_f, op0=ALU.mult, op1=ALU.add), r=[key + "m", key + "f"], w=[key + "f"])
                act(dst, tmp_f, AF.Sin, r=[key + "f"], w=[key + "out"])

        lo0, hi0 = M.lo, M.hi
        if upto <= -2:
            return finish(nc, S, st)
        posi = M.lo_alloc([NT], I32); posf = M.lo_alloc([NT], F32)
        posCi = M.lo_alloc([1], I32); posCf = M.lo_alloc([1], F32)
        inv16 = M.lo_alloc([16], F32); inv8 = M.lo_alloc([8], F32)
        angM = M.lo_alloc([NT, 16], F32); tfM = M.lo_alloc([NT, 16], F32); tiM = M.lo_alloc([NT, 16], I32); tmM = M.lo_alloc([NT, 16], F32)
        S.dma("sp", posi, pos_pk, w=["posi"])
        S.dma("sp", posCi[0:127], posC, w=["posCi"])
        S.dma("sp", inv16, inv16_d, w=["inv16"])
        S.dma("sp", inv8, inv8_d, w=["inv8"])
        S.op("dve", lambda e: e.tensor_copy(out=posf, in_=posi), r=["posi"], w=["posf"])
        S.op("dve", lambda e: e.tensor_copy(out=posCf[0:127], in_=posCi[0:127]), r=["posCi"], w=["posCf"])
        S.op("dve", lambda e: e.tensor_tensor(out=angM, in0=posf.unsqueeze(2).to_broadcast([128, NT, 16]),
                                              in1=inv16.unsqueeze(1).to_broadcast([128, NT, 16]), op=ALU.mult), r=["posf", "inv16"], w=["Mang"])
        sincos(angM, None, cosM, sinM, 128, tfM, tiM, tmM, "M")
        a8 = angM.rearrange("p a b -> p (a b)")[:, 0:NT * 8].rearrange("p (a b) -> p a b", b=8)
        f8 = tfM.rearrange("p a b -> p (a b)")[:, 0:NT * 8].rearrange("p (a b) -> p a b", b=8)
        i8 = tiM.rearrange("p a b -> p (a b)")[:, 0:NT * 8].rearrange("p (a b) -> p a b", b=8)
        m8_ = tmM.rearrange("p a b -> p (a b)")[:, 0:NT * 8].rearrange("p (a b) -> p a b", b=8)
        S.op("dve", lambda e: e.tensor_tensor(out=a8, in0=posf.unsqueeze(2).to_broadcast([128, NT, 8]),
                                              in1=inv8.unsqueeze(1).to_broadcast([128, NT, 8]), op=ALU.mult), r=["posf", "inv8", "Mout", "Mf", "Mm", "Mi"], w=["Nang"])
        sincos(a8, None, cosN, sinN, 128, f8, i8, m8_, "N")
        aC = angM.rearrange("p a b -> p (a b)")[0:127, 0:8]
        fC = tfM.rearrange("p a b -> p (a b)")[0:127, 0:8]
        iC = tiM.rearrange("p a b -> p (a b)")[0:127, 0:8]
        mC = tmM.rearrange("p a b -> p (a b)")[0:127, 0:8]
        S.op("dve", lambda e: e.tensor_scalar(out=aC, in0=inv8[0:127], scalar1=posCf[0:127, 0:1], scalar2=None, op0=ALU.mult),
             r=["posCf", "inv8", "Nout", "Nf", "Nm", "Ni", "Nang"], w=["Cang"])
        sincos(aC, None, cosC[0:127], sinC[0:127], 127, fC, iC, mC, "C")
        dbg_out("cosM", cosM, ["Mout"]); dbg_out("sinM", sinM, ["Mout"])

        if upto <= -1:
            return finish(nc, S, st)
        cpk = M.lo_alloc([8], F32); sc = M.lo_alloc([8], F32)
        sch = M.lo_alloc([8], BF16); scl = M.lo_alloc([8], BF16)
        cBh = M.lo_alloc([8, 128], BF16); cBl = M.lo_alloc([8, 128], BF16)
        modB = M.lo_alloc([6 * D], F32)
        g1t = M.lo_alloc([D], F32); g2t = M.lo_alloc([D], F32)
        awb = [M.hi_alloc([8, 512], F32) for _ in range(3)]
        abb = [M.hi_alloc([512], F32) for _ in range(3)]
        awh = [M.hi_alloc([8, 512], BF16) for _ in range(3)]
        awl = [M.hi_alloc([8, 512], BF16) for _ in range(3)]
        S.dma("sp", cpk, c_pk, w=["cpk"])
        S.dma("sp", g1t, g1B, w=["g1t"]); S.dma("sp", g2t, g2B, w=["g2t"])
        act(sc, cpk, AF.Silu, r=["cpk"], w=["sc"])
        S.op("dve", lambda e: e.tensor_copy(out=sch, in_=sc), r=["sc"], w=["sch"])
        S.op("dve", lambda e: e.tensor_tensor(out=scl, in0=sc, in1=sch, op=ALU.subtract), r=["sc", "sch"], w=["scl"])
        for k in range(8):
            S.op("dve", lambda e, k=k: e.tensor_copy(out=cBh[:, k, :], in_=sch[:, k:k + 1].to_broadcast([128, 128])), r=["sch"], w=["cBh"])
            S.op("dve", lambda e, k=k: e.tensor_copy(out=cBl[:, k, :], in_=scl[:, k:k + 1].to_broadcast([128, 128])), r=["scl"], w=["cBl"])
        awv = ada_w.rearrange("(k p) n -> p k n", p=128)
        for n in range(12):
            q_ = n % 3
            S.dma("sp", awb[q_], awv[:, :, n * 512:(n + 1) * 512], w=[("awb", q_)])
            S.dma("sp", abb[q_], adabB[:, n * 512:(n + 1) * 512], w=[("abb", q_)])
            act(awh[q_], awb[q_], AF.Copy, r=[("awb", q_)], w=[("awh", q_)])
            S.op("dve", lambda e, q_=q_: e.tensor_tensor(out=awl[q_], in0=awb[q_], in1=awh[q_], op=ALU.subtract), r=[("awb", q_), ("awh", q_)], w=[("awl", q_)])
            b = nb()
            passes = [(cBh, "cBh", awh, "awh"), (cBh, "cBh", awl, "awl"), (cBl, "cBl", awh, "awh")]
            for pi_, (cb_, ck, ww, wk) in enumerate(passes):
                for k in range(8):
                    mm(P[b][:, :], cb_[:, k, :], ww[q_][:, k, :], pi_ == 0 and k == 0, pi_ == 2 and k == 7, r=[ck, (wk, q_)], w=[PK[b]])
            S.op("dve", lambda e, n=n, b=b, q_=q_: e.tensor_tensor(out=modB[:, n * 512:(n + 1) * 512], in0=P[b][:, :], in1=abb[q_], op=ALU.add),
                 r=[PK[b], ("abb", q_)], w=["modB"])
        S.op("dve", lambda e: e.scalar_tensor_tensor(out=modB[:, D:2 * D], in0=modB[:, D:2 * D], scalar=1.0, in1=g1t, op0=ALU.add, op1=ALU.mult), r=["modB", "g1t"], w=["modB"])
        S.op("dve", lambda e: e.scalar_tensor_tensor(out=modB[:, 4 * D:5 * D], in0=modB[:, 4 * D:5 * D], scalar=1.0, in1=g2t, op0=ALU.add, op1=ALU.mult), r=["modB", "g2t"], w=["modB"])
        S.dma("sp", mods, modB, r=["modB"], w=["mods"])
        dbg_out("modB", modB, ["modB"])
        B1 = modB[:, 0:D]; A1 = modB[:, D:2 * D]
        if upto <= 0:
            return finish(nc, S, st)

        M.hi = hi0
        hT = M.hi_alloc([8, S_], BF16)
        xb = [M.lo_alloc([D], F32) for _ in range(2)]
        junk = M.lo_alloc([D], F32); tmpA = M.lo_alloc([D], F32)
        hb = [M.lo_alloc([D], BF16) for _ in range(2)]
        ssq = M.lo_alloc([NT], F32); rs = M.lo_alloc([NT], F32)

        tmpA2 = [tmpA, M.lo_alloc([D], F32)]

        def norm_a(i, xt, xkey):
            act(junk, xt, AF.Square, r=[xkey], w=["junk", ("ssq", i)], accum_out=ssq[:, i:i + 1])
            act(rs[:, i:i + 1], ssq[:, i:i + 1], AF.Sqrt, r=[("ssq", i)], w=[("rs", i)], scale=1.0 / D, bias=EPS)
            S.op("dve", lambda e: e.reciprocal(out=rs[:, i:i + 1], in_=rs[:, i:i + 1]), r=[("rs", i)], w=[("rs", i)])

        def norm_b1(i, xt, xkey, A, B, Akeys):
            par = i % 2
            S.op("dve", lambda e: e.scalar_tensor_tensor(out=tmpA2[par], in0=xt, scalar=rs[:, i:i + 1], in1=A, op0=ALU.mult, op1=ALU.mult),
                 r=[xkey, ("rs", i)] + Akeys, w=[("tmpA2", par)])
            S.op("pool", lambda e: e.tensor_tensor(out=hb[par], in0=tmpA2[par], in1=B, op=ALU.add), r=[("tmpA2", par)] + Akeys, w=[("hb", par)])
            b = nb()
            pb = P[b][:, :].bitcast(BF16)
            for k in range(8):
                tp(pb[:, k * 128:(k + 1) * 128], hb[par][:, k * 128:(k + 1) * 128], ident, r=[("hb", par), "ident"], w=[PK[b]])
            return b

        def norm_b2(i, b, dstT, dkey):
            pb = P[b][:, :].bitcast(BF16)
            act(dstT[:, :, i * 128:(i + 1) * 128], pb.rearrange("p (k t) -> p k t", k=8), AF.Copy, r=[PK[b]], w=[(dkey, i)])

        xb = xb + [M.lo_alloc([D], F32), M.lo_alloc([D], F32)]

        def a_front(i):
            S.dma("sp", xb[i % 4], x[i * 128:(i + 1) * 128, :], w=[("xb", i % 4)])
            norm_a(i, xb[i % 4], ("xb", i % 4))

        a_front(0); a_front(1)
        for i in range(NT):
            b_ = norm_b1(i, xb[i % 4], ("xb", i % 4), A1, B1, ["modB"])
            if i + 2 < NT:
                a_front(i + 2)
            norm_b2(i, b_, hT, "hT")
        hTk = [("hT", i) for i in range(NT)]
        S.dma("sp", hTs, hT.rearrange("p k t -> p (k t)"), r=hTk, w=["hTs"])
        dbg_out("hT", hT.rearrange("p k t -> p (k t)"), hTk)
        if upto <= 1:
            return finish(nc, S, st)
        S.barrier()

        M.lo = lo0
        cqT = M.lo_alloc([6, S_], BF16); ckvT = M.lo_alloc([2, S_], BF16); kpe = M.lo_alloc([NT, 32], F32)
        lo1 = M.lo
        Wcq = M.lo_alloc([8, 768], BF16); Wckv = M.lo_alloc([8, 256], BF16); Wkpe = M.lo_alloc([8, 32], BF16)
        qagt = M.lo_alloc([6], F32); kvagt = M.lo_alloc([2], F32)
        sqb = [M.lo_alloc([512], BF16) for _ in range(2)]
        rb = M.lo_alloc([512], F32)
        S.dma("sp", qagt, qag, w=["qagt"]); S.dma("sp", kvagt, kvag, w=["kvagt"])
        load_w(Wcq, w_cq, "Wcq", 8, 768); load_w(Wckv, w_ckv, "Wckv", 8, 256); load_w(Wkpe, w_kpe, "Wkpe", 8, 32)

        def fm_proj_norm(dstT, dkey, Wt, wkey, nf, gaint, gkey, nfeat):
            for c in range(4):
                hk = [("hT", 4 * c + q) for q in range(4)]
                for j in range(nf):
                    b = nb()
                    for k in range(8):
                        mm(P[b][:, :], Wt[:, k, j * 128:(j + 1) * 128], hT[:, k, c * 512:(c + 1) * 512], k == 0, k == 7, r=[wkey] + hk, w=[PK[b]])
                    act(dstT[:, j, c * 512:(c + 1) * 512], P[b][:, :], AF.Copy, r=[PK[b]], w=[(dkey, c)])
                    act(sqb[j % 2], P[b][:, :], AF.Square, r=[PK[b]], w=[("sqb", j % 2)])
                    mm(P[6][:, :], onesb, sqb[j % 2], j == 0, j == nf - 1, r=["onesb", ("sqb", j % 2)], w=[PK[6]])
                act(rb, P[6][:, :], AF.Sqrt, r=[PK[6]], w=["rb"], scale=1.0 / nfeat, bias=EPS)
                S.op("dve", lambda e: e.reciprocal(out=rb, in_=rb), r=["rb"], w=["rb"])
                for j in range(nf):
                    S.op("dve", lambda e, j=j, c=c: e.scalar_tensor_tensor(out=dstT[:, j, c * 512:(c + 1) * 512], in0=dstT[:, j, c * 512:(c + 1) * 512],
                                                                             scalar=gaint[:, j:j + 1], in1=rb, op0=ALU.mult, op1=ALU.mult),
                         r=[(dkey, c), "rb", gkey], w=[(dkey, c)])

        fm_proj_norm(cqT, "cqT", Wcq, "Wcq", 6, qagt, "qagt", 768)
        fm_proj_norm(ckvT, "ckvT", Wckv, "Wckv", 2, kvagt, "kvagt", 256)
        for i in range(NT):
            b = nb()
            for k in range(8):
                mm(P[b][:, 0:32], hT[:, k, i * 128:(i + 1) * 128], Wkpe[:, k, :], k == 0, k == 7, r=["Wkpe", ("hT", i)], w=[PK[b]])
            S.op("dve", lambda e, i=i, b=b: e.tensor_copy(out=kpe[:, i, :], in_=P[b][:, 0:32]), r=[PK[b]], w=[("kpe", i)])
        cqk = [("cqT", c) for c in range(4)]
        dbg_out("cqT", cqT.rearrange("p k t -> p (k t)"), cqk)
        dbg_out("kpe", kpe.rearrange("p a b -> p (a b)"), [("kpe", i) for i in range(NT)])
        if upto <= 2:
            return finish(nc, S, st)
        S.barrier()

        M.lo = lo1
        M.hi = hi0
        QT = M.hi_alloc([8, S_], BF16); KT = M.hi_alloc([8, S_], BF16); V = M.hi_alloc([NT, 8, 65], BF16)
        hi1 = M.hi
        Wqb = M.lo_alloc([6, 768], BF16); Wkvb = M.lo_alloc([2, 1024], BF16)
        qgt = M.lo_alloc([96], F32); kgt = M.lo_alloc([96], F32)
        drq = [M.lo_alloc([768], BF16) for _ in range(2)]; drk = [M.lo_alloc([768], BF16) for _ in range(2)]
        S.dma("sp", qgt, qgB, w=["qgt"]); S.dma("sp", kgt, kgB, w=["kgt"])
        load_w(Wqb, w_qb, "Wqb", 6, 768); load_w(Wkvb, w_kvb, "Wkvb", 2, 1024)
        S.op("pool", lambda e: e.memset(V[:, :, :, 64:65], 1.0), w=["Vones"])

        def mk_tmps(Mx, n, H, hf):
            return dict(t1=Mx.lo_alloc([n], F32), t2=Mx.lo_alloc([n], F32), hs=Mx.lo_alloc([H], F32), hr=Mx.lo_alloc([H], F32),
                        ra=Mx.lo_alloc([H * hf], F32), rb=Mx.lo_alloc([H * hf], F32), ra2=Mx.lo_alloc([H * hf], F32), rb2=Mx.lo_alloc([H * hf], F32))

        def hnr_stages(tag, T, src, skeys, H, Dh, gaint, gkey, ro, hf, cos_, sin_, dst, dkey, np_=128):
            n = H * Dh
            t1v = T["t1"][0:np_, 0:n].rearrange("p (h d) -> p h d", h=H)
            t2v = T["t2"][0:np_, 0:n].rearrange("p (h d) -> p h d", h=H)
            hs = T["hs"][0:np_, 0:H]; hr = T["hr"][0:np_, 0:H]
            x1 = t1v[:, :, ro:ro + hf]; x2 = t1v[:, :, ro + hf:ro + 2 * hf]
            cb = cos_.unsqueeze(1).to_broadcast([np_, H, hf]); sb_ = sin_.unsqueeze(1).to_broadcast([np_, H, hf])
            rv = {k: T[k][0:np_, 0:H * hf].rearrange("p (h d) -> p h d", h=H) for k in ("ra", "rb", "ra2", "rb2")}
            tr = ["Mout", "Nout", "Cout"]
            k_ = lambda nm: (tag, nm)
            st = []
            st.append(lambda: act(t1v, src, AF.Square, r=skeys, w=[k_("t1")]))
            st.append(lambda: S.op("dve", lambda e: e.tensor_reduce(out=hs, in_=t1v, axis=AX.X, op=ALU.add), r=[k_("t1")], w=[k_("hs")]))
            st.append(lambda: act(hr, hs, AF.Sqrt, r=[k_("hs")], w=[k_("hr")], scale=1.0 / Dh, bias=EPS))
            st.append(lambda: S.op("dve", lambda e: e.reciprocal(out=hr, in_=hr), r=[k_("hr")], w=[k_("hr")]))
            st.append(lambda: S.op("dve", lambda e: e.tensor_tensor(out=t2v, in0=src, in1=hr.unsqueeze(2).to_broadcast([np_, H, Dh]), op=ALU.mult), r=skeys + [k_("hr")], w=[k_("t2")]))
            st.append(lambda: S.op("dve", lambda e: e.tensor_tensor(out=t1v, in0=t2v, in1=gaint[0:np_].unsqueeze(1).to_broadcast([np_, H, Dh]), op=ALU.mult), r=[k_("t2"), gkey], w=[k_("t1")]))
            st.append(lambda: S.op("dve", lambda e: e.tensor_tensor(out=rv["ra"], in0=x1, in1=cb, op=ALU.mult), r=[k_("t1")] + tr, w=[k_("ra")]))
            st.append(lambda: S.op("dve", lambda e: e.tensor_tensor(out=rv["rb"], in0=x2, in1=sb_, op=ALU.mult), r=[k_("t1")] + tr, w=[k_("rb")]))
            st.append(lambda: S.op("dve", lambda e: e.tensor_tensor(out=dst[:, :, ro:ro + hf], in0=rv["ra"], in1=rv["rb"], op=ALU.subtract), r=[k_("ra"), k_("rb")], w=[dkey]))
            st.append(lambda: S.op("dve", lambda e: e.tensor_tensor(out=rv["ra2"], in0=x2, in1=cb, op=ALU.mult), r=[k_("t1")] + tr, w=[k_("ra2")]))
            st.append(lambda: S.op("dve", lambda e: e.tensor_tensor(out=rv["rb2"], in0=x1, in1=sb_, op=ALU.mult), r=[k_("t1")] + tr, w=[k_("rb2")]))
            st.append(lambda: S.op("dve", lambda e: e.tensor_tensor(out=dst[:, :, ro + hf:ro + 2 * hf], in0=rv["ra2"], in1=rv["rb2"], op=ALU.add), r=[k_("ra2"), k_("rb2")], w=[dkey]))

            def copies():
                if ro > 0:
                    S.op("pool", lambda e: e.tensor_copy(out=dst[:, :, 0:ro], in_=t1v[:, :, 0:ro]), r=[k_("t1")], w=[dkey])
                if ro + 2 * hf < Dh:
                    S.op("pool", lambda e: e.tensor_copy(out=dst[:, :, ro + 2 * hf:Dh], in_=t1v[:, :, ro + 2 * hf:Dh]), r=[k_("t1")], w=[dkey])
            st.insert(6, copies)
            return st

        def run_interleaved(chains):
            for s_ in range(max(len(c_) for c_ in chains)):
                for c_ in chains:
                    if s_ < len(c_):
                        c_[s_]()

        def head_norm_rope(src, skeys, H, Dh, gaint, gkey, ro, hf, cos_, sin_, dst, dkey, np_=128):
            n = H * Dh
            t1v = t1[0:np_, 0:n].rearrange("p (h d) -> p h d", h=H)
            t2v = t2[0:np_, 0:n].rearrange("p (h d) -> p h d", h=H)
            hs = hss[0:np_, 0:H]; hr = hrs[0:np_, 0:H]
            act(t1v, src, AF.Square, r=skeys, w=["t1"])
            S.op("dve", lambda e: e.tensor_reduce(out=hs, in_=t1v, axis=AX.X, op=ALU.add), r=["t1"], w=["hss"])
            act(hr, hs, AF.Sqrt, r=["hss"], w=["hrs"], scale=1.0 / Dh, bias=EPS)
            S.op("dve", lambda e: e.reciprocal(out=hr, in_=hr), r=["hrs"], w=["hrs"])
            S.op("dve", lambda e: e.tensor_tensor(out=t2v, in0=src, in1=hr.unsqueeze(2).to_broadcast([np_, H, Dh]), op=ALU.mult), r=skeys + ["hrs"], w=["t2"])
            S.op("dve", lambda e: e.tensor_tensor(out=t1v, in0=t2v, in1=gaint[0:np_].unsqueeze(1).to_broadcast([np_, H, Dh]), op=ALU.mult), r=["t2", gkey], w=["t1"])
            x1 = t1v[:, :, ro:ro + hf]; x2 = t1v[:, :, ro + hf:ro + 2 * hf]
            cb = cos_.unsqueeze(1).to_broadcast([np_, H, hf]); sb_ = sin_.unsqueeze(1).to_broadcast([np_, H, hf])
            rav = ra[0:np_, 0:H * hf].rearrange("p (h d) -> p h d", h=H)
            rbv = rbb[0:np_, 0:H * hf].rearrange("p (h d) -> p h d", h=H)
            tr = ["Mout", "Nout", "Cout"]
            S.op("dve", lambda e: e.tensor_tensor(out=rav, in0=x1, in1=cb, op=ALU.mult), r=["t1"] + tr, w=["ra"])
            S.op("dve", lambda e: e.tensor_tensor(out=rbv, in0=x2, in1=sb_, op=ALU.mult), r=["t1"] + tr, w=["rbb"])
            S.op("dve", lambda e: e.tensor_tensor(out=dst[:, :, ro:ro + hf], in0=rav, in1=rbv, op=ALU.subtract), r=["ra", "rbb"], w=[dkey])
            S.op("dve", lambda e: e.tensor_tensor(out=rav, in0=x2, in1=cb, op=ALU.mult), r=["t1"] + tr, w=["ra"])
            S.op("dve", lambda e: e.tensor_tensor(out=rbv, in0=x1, in1=sb_, op=ALU.mult), r=["t1"] + tr, w=["rbb"])
            S.op("dve", lambda e: e.tensor_tensor(out=dst[:, :, ro + hf:ro + 2 * hf], in0=rav, in1=rbv, op=ALU.add), r=["ra", "rbb"], w=[dkey])
            if ro > 0:
                S.op("pool", lambda e: e.tensor_copy(out=dst[:, :, 0:ro], in_=t1v[:, :, 0:ro]), r=["t1"], w=[dkey])
            if ro + 2 * hf < Dh:
                S.op("pool", lambda e: e.tensor_copy(out=dst[:, :, ro + 2 * hf:Dh], in_=t1v[:, :, ro + 2 * hf:Dh]), r=["t1"], w=[dkey])

        Msub = Mem(big, NB); Msub.lo = lo_pers; Msub.hi = lo0
        rawq = [Msub.lo_alloc([768], F32) for _ in range(2)]; rawk = [Msub.lo_alloc([768], F32) for _ in range(2)]
        Tq = mk_tmps(Msub, 768, 8, 16); Tk = mk_tmps(Msub, 768, 8, 16)

        def b2_front(i):
            ts = slice(i * 128, (i + 1) * 128)
            par = i % 2
            bA, bB = nb(), nb()
            for k in range(6):
                mm(P[bA][:, :], cqT[:, k, ts], Wqb[:, k, 0:512], k == 0, k == 5, r=["Wqb", ("cqT", i // 4)], w=[PK[bA]])
            for k in range(6):
                mm(P[bB][:, 0:256], cqT[:, k, ts], Wqb[:, k, 512:768], k == 0, k == 5, r=["Wqb", ("cqT", i // 4)], w=[PK[bB]])
            act(rawq[par][:, 0:512], P[bA][:, :], AF.Copy, r=[PK[bA]], w=[("rawq", par)])
            act(rawq[par][:, 512:768], P[bB][:, 0:256], AF.Copy, r=[PK[bB]], w=[("rawq", par)])
            bA, bB = nb(), nb()
            for hh, bb in ((0, bA), (1, bB)):
                for k in range(2):
                    mm(P[bb][:, :], ckvT[:, k, ts], Wkvb[:, k, hh * 512:(hh + 1) * 512], k == 0, k == 1, r=["Wkvb", ("ckvT", i // 4)], w=[PK[bb]])
            rv = rawk[par].rearrange("p (h d) -> p h d", h=8)
            for hh, bb in ((0, bA), (1, bB)):
                pv = P[bb][:, :].rearrange("p (h d) -> p h d", h=4)
                act(rv[:, hh * 4:(hh + 1) * 4, 0:64], pv[:, :, 0:64], AF.Copy, r=[PK[bb]], w=[("rawk", par)])
                act(V[:, i, hh * 4:(hh + 1) * 4, 0:64], pv[:, :, 64:128], AF.Copy, r=[PK[bb]], w=[("V", i)])
            S.op("pool", lambda e, i=i: e.tensor_copy(out=rv[:, :, 64:96], in_=kpe[:, i, :].unsqueeze(1).to_broadcast([128, 8, 32])), r=[("kpe", i)], w=[("rawk", par)])

        def b2_back(i):
            ts = slice(i * 128, (i + 1) * 128)
            par = i % 2
            dq = drq[par].rearrange("p (h d) -> p h d", h=8); dk = drk[par].rearrange("p (h d) -> p h d", h=8)
            cq_ = hnr_stages("cq", Tq, rawq[par].rearrange("p (h d) -> p h d", h=8), [("rawq", par)], 8, 96, qgt, "qgt", 64, 16, cosM[:, i, :], sinM[:, i, :], dq, ("drq", par))
            ck_ = hnr_stages("ck", Tk, rawk[par].rearrange("p (h d) -> p h d", h=8), [("rawk", par)], 8, 96, kgt, "kgt", 64, 16, cosM[:, i, :], sinM[:, i, :], dk, ("drk", par))
            run_interleaved([cq_, ck_])
            for (dd, dkey_, dstT, okey) in ((dq, ("drq", par), QT, "QT"), (dk, ("drk", par), KT, "KT")):
                b = nb(); pb = P[b][:, :].bitcast(BF16)
                for h in range(8):
                    tp(pb[0:96, h * 128:(h + 1) * 128], dd[:, h, :], ident, r=[dkey_, "ident"], w=[PK[b]])
                act(dstT[0:96, :, ts], pb[0:96, :].rearrange("p (h t) -> p h t", h=8), AF.Copy, r=[PK[b]], w=[(okey, i)])

        b2_front(0)
        for i in range(NT):
            if i + 1 < NT:
                b2_front(i + 1)
            b2_back(i)
        QTk = [("QT", i) for i in range(NT)]
        dbg_out("QT", QT[0:96].rearrange("p k t -> p (k t)"), QTk)
        dbg_out("KT", KT[0:96].rearrange("p k t -> p (k t)"), [("KT", i) for i in range(NT)])
        dbg_out("V", V.rearrange("p a b c -> p (a b c)"), [("V", i) for i in range(NT)] + ["Vones"])
        if upto <= 3:
            return finish(nc, S, st)
        S.barrier()

        nbanks[0] = 6
        M.lo = lo0
        om = M.lo_alloc([NT, 512], BF16)
        PT = [M.lo_alloc([512], BF16) for _ in range(3)]
        rec4 = [M.lo_alloc([4], F32) for _ in range(2)]
        pt_i = [0]

        gchunk = [0]

        def causal_attn_multi(jobs):
            steps = []
            for ji in range(len(jobs)):
                for c in range(4):
                    for kt in range(4 * c + 4):
                        steps.append((ji, c, kt))

            def emit_qk(step):
                ji, c, kt = step
                J = jobs[ji]
                q0 = max(kt - 4 * c, 0)
                n = 512 - 128 * q0
                b = nb()
                has_extra = J["extra"] is not None
                mm(P[b][:, 0:n], J["KT"][0:J["kn"], kt * 128:(kt + 1) * 128], J["QT"](c * 512 + q0 * 128, (c + 1) * 512),
                   True, not has_extra, r=J["kk"](kt) + J["qk"](c), w=[PK[b]])
                if has_extra:
                    J["extra"](P[b][:, 0:n], kt, c * 512 + q0 * 128, (c + 1) * 512, PK[b])
                return b, n, q0

            pend = emit_qk(steps[0])
            for si, (ji, c, kt) in enumerate(steps):
                J = jobs[ji]
                b, n, q0 = pend
                if si + 1 < len(steps):
                    pend = emit_qk(steps[si + 1])
                if kt == 0:
                    gchunk[0] += 1
                ab = 6 + (gchunk[0] % 2)
                Oacc = P[ab][:, 0:260].rearrange("p (q d) -> p q d", q=4)
                pi = pt_i[0]; pt_i[0] = (pi + 1) % 3
                pt = PT[pi]
                act(pt[:, 0:n], P[b][:, 0:n], AF.Exp, r=[PK[b]], w=[("PT", pi)], scale=J["scale"])
                if kt >= 4 * c:
                    S.op("dve", lambda e, pt=pt: e.tensor_tensor(out=pt[:, 0:128], in0=pt[:, 0:128], in1=tri, op=ALU.mult), r=[("PT", pi), "tri"], w=[("PT", pi)])
                for qi in range(q0, 4):
                    mm(Oacc[:, qi, :], pt[:, (qi - q0) * 128:(qi - q0 + 1) * 128], J["V"](kt), kt == 0 and qi == 0, kt == 4 * c + qi,
                       r=[("PT", pi)] + J["vk"](kt), w=[PK[ab]], skip_group_check=True)
                if kt == 4 * c + 3:
                    J["fin"](c, Oacc, PK[ab])

        jobs = []
        for h in range(8):
            def fin(c, Oacc, pk, h=h):
                rc = rec4[c % 2]
                S.op("dve", lambda e: e.reciprocal(out=rc, in_=Oacc[:, :, 64]), r=[pk], w=[("rec4", c % 2)])
                S.op("dve", lambda e: e.tensor_tensor(out=om[:, 4 * c:4 * c + 4, h * 64:(h + 1) * 64], in0=Oacc[:, :, 0:64],
                                                      in1=rc.unsqueeze(2).to_broadcast([128, 4, 64]), op=ALU.mult),
                     r=[pk, ("rec4", c % 2)], w=[("om", c)])
            jobs.append(dict(KT=KT[:, h, :], kn=96, QT=(lambda a, b_, h=h: QT[0:96, h, a:b_]), V=(lambda kt, h=h: V[:, kt, h, :]), scale=96 ** -0.5,
                             extra=None, fin=fin, qk=(lambda c: [("QT", 4 * c + q) for q in range(4)]), kk=(lambda kt: [("KT", kt)]),
                             vk=(lambda kt: [("V", kt), "Vones"])))
        causal_attn_multi(jobs)
        for i in range(NT):
            b = nb(); pb = P[b][:, :].bitcast(BF16)
            for j in range(4):
                tp(pb[:, j * 128:(j + 1) * 128], om[:, i, j * 128:(j + 1) * 128], ident, r=[("om", i // 4), "ident"], w=[PK[b]])
            act(omT[:, :, i * 128:(i + 1) * 128], pb[:, 0:512].rearrange("p (k t) -> p k t", k=4), AF.Copy, r=[PK[b]], w=[("omT", i)])
        dbg_out("om", om.rearrange("p a b -> p (a b)"), [("om", c) for c in range(4)])
        if upto <= 4:
            return finish(nc, S, st)
        S.barrier()

        nbanks[0] = 4
        M.lo = lo0; M.hi = hi0
        qnT = M.lo_alloc([8, S_], BF16); ksT = M.lo_alloc([2, S_], BF16); kwT = M.lo_alloc([2, S_], BF16)
        vs = M.lo_alloc([NT, 2, 65], BF16); vw = M.lo_alloc([NT, 2, 65], BF16)
        gates = M.lo_alloc([NT, 3, 8], F32)
        kcmpT = M.lo_alloc([2, 128], BF16); VCX = M.lo_alloc([2, 97], BF16)
        PT = [M.lo_alloc([512], BF16) for _ in range(3)]
        rec4 = [M.lo_alloc([4], F32) for _ in range(2)]
        t1 = M.lo_alloc([512], F32); t2 = M.lo_alloc([512], F32)
        hss = M.lo_alloc([8], F32); hrs = M.lo_alloc([8], F32)
        ra = M.lo_alloc([128], F32); rbb = M.lo_alloc([128], F32)
        drb = [M.lo_alloc([512], BF16) for _ in range(2)]
        nqt = M.lo_alloc([64], F32); nkct = M.lo_alloc([64], F32); nkst = M.lo_alloc([64], F32); nkwt = M.lo_alloc([64], F32)
        lo2 = M.lo
        kc2 = M.hi_alloc([2, S_], BF16); vc2 = M.hi_alloc([2, S_], BF16)
        hi_kv = M.hi
        hT = M.hi_alloc([8, S_], BF16)
        Wqn = M.hi_alloc([8, 512], BF16); Wkc2 = M.hi_alloc([8, 256], BF16); Wvc2 = M.hi_alloc([8, 256], BF16)
        Wkv4 = M.hi_alloc([8, 512], BF16); Wgn = M.hi_alloc([8, 24], BF16)
        ge = M.hi_alloc([24], F32)
        S.dma("sp", hT.rearrange("p k t -> p (k t)"), hTs, r=["hTs"], w=["hTall"])
        for t_, d_ in ((nqt, nqB), (nkct, nkcB), (nkst, nksB), (nkwt, nkwB)):
            S.dma("sp", t_, d_, w=["ngain"])
        load_w(Wqn, w_qn, "Wqn", 8, 512); load_w(Wkv4, w_kv4, "Wkv4", 8, 512); load_w(Wgn, w_gn, "Wgn", 8, 24)
        load_w(Wkc2, w_kc2, "Wkc2", 8, 256); load_w(Wvc2, w_vc2, "Wvc2", 8, 256)
        S.op("pool", lambda e: e.memset(vs[:, :, :, 64:65], 1.0), w=["vsones"])
        S.op("pool", lambda e: e.memset(vw[:, :, :, 64:65], 1.0), w=["vwones"])
        S.op("pool", lambda e: e.memset(kc2[64:128, :, S_ - 1:S_], 0.0), w=["kc2pad"])
        S.op("pool", lambda e: e.memset(vc2[64:128, :, S_ - 1:S_], 0.0), w=["vc2pad"])
        MsubD = Mem(big, NB); MsubD.lo = lo_pers + 16384; MsubD.hi = lo0
        TDq = mk_tmps(MsubD, 512, 8, 8); TDs = mk_tmps(MsubD, 128, 2, 8); TDw = mk_tmps(MsubD, 128, 2, 8)
        dnq = [MsubD.lo_alloc([512], BF16) for _ in range(2)]
        dns = [MsubD.lo_alloc([128], BF16) for _ in range(2)]; dnw = [MsubD.lo_alloc([128], BF16) for _ in range(2)]

        def d_front(i):
            ts = slice(i * 128, (i + 1) * 128)
            bq = 4 + 2 * (i % 2)
            for k in range(8):
                mm(P[bq][:, :], hT[:, k, ts], Wqn[:, k, :], k == 0, k == 7, r=["hTall", "Wqn"], w=[PK[bq]])
            bk = 5 + 2 * (i % 2)
            for k in range(8):
                mm(P[bk][:, :], hT[:, k, ts], Wkv4[:, k, :], k == 0, k == 7, r=["hTall", "Wkv4"], w=[PK[bk]])
            bg = nb()
            for k in range(8):
                mm(P[bg][:, 0:24], hT[:, k, ts], Wgn[:, k, :], k == 0, k == 7, r=["hTall", "Wgn"], w=[PK[bg]])
            act(vs[:, i, :, 0:64], P[bk][:, 256:384].rearrange("p (g d) -> p g d", g=2), AF.Copy, r=[PK[bk]], w=[("vs", i)])
            act(vw[:, i, :, 0:64], P[bk][:, 384:512].rearrange("p (g d) -> p g d", g=2), AF.Copy, r=[PK[bk]], w=[("vw", i)])
            act(ge, P[bg][:, 0:24], AF.Exp, r=[PK[bg]], w=["ge"], scale=-1.0)
            S.op("dve", lambda e: e.tensor_scalar(out=ge, in0=ge, scalar1=1.0, scalar2=None, op0=ALU.add), r=["ge"], w=["ge"])
            S.op("dve", lambda e, i=i: e.reciprocal(out=gates[:, i].rearrange("p a b -> p (a b)"), in_=ge), r=["ge"], w=[("gates", i)])
            return bq, bk

        def d_back(i, bq, bk):
            ts = slice(i * 128, (i + 1) * 128)
            par = i % 2
            dq = dnq[par].rearrange("p (h d) -> p h d", h=8)
            ds_ = dns[par].rearrange("p (h d) -> p h d", h=2); dw_ = dnw[par].rearrange("p (h d) -> p h d", h=2)
            c1 = hnr_stages("dq", TDq, P[bq][:, :].rearrange("p (h d) -> p h d", h=8), [PK[bq]], 8, 64, nqt, "ngain", 0, 8, cosN[:, i, :], sinN[:, i, :], dq, ("dnq", par))
            c2 = hnr_stages("ds", TDs, P[bk][:, 0:128].rearrange("p (h d) -> p h d", h=2), [PK[bk]], 2, 64, nkst, "ngain", 0, 8, cosN[:, i, :], sinN[:, i, :], ds_, ("dns", par))
            c3 = hnr_stages("dw", TDw, P[bk][:, 128:256].rearrange("p (h d) -> p h d", h=2), [PK[bk]], 2, 64, nkwt, "ngain", 0, 8, cosN[:, i, :], sinN[:, i, :], dw_, ("dnw", par))
            run_interleaved([c1, c2, c3])
            b = nb(); pb = P[b][:, :].bitcast(BF16)
            for p_ in range(8):
                tp(pb[0:64, p_ * 128:(p_ + 1) * 128], dnq[par][:, p_ * 64:(p_ + 1) * 64], ident, r=[("dnq", par), "ident"], w=[PK[b]])
            act(qnT[0:64, :, ts], pb[0:64, :].rearrange("p (k t) -> p k t", k=8), AF.Copy, r=[PK[b]], w=[("qnT", i)])
            for (dd, dkey_, dstT, dk) in ((dns[par], ("dns", par), ksT, "ksT"), (dnw[par], ("dnw", par), kwT, "kwT")):
                b2 = nb(); pb = P[b2][:, :].bitcast(BF16)
                for g_ in range(2):
                    tp(pb[0:64, g_ * 128:(g_ + 1) * 128], dd[:, g_ * 64:(g_ + 1) * 64], ident, r=[dkey_, "ident"], w=[PK[b2]])
                act(dstT[0:64, :, ts], pb[0:64, 0:256].rearrange("p (g t) -> p g t", g=2), AF.Copy, r=[PK[b2]], w=[(dk, i)])

        fb_ = d_front(0)
        for i in range(NT):
            cur_ = fb_
            if i + 1 < NT:
                fb_ = d_front(i + 1)
            d_back(i, *cur_)
        for c in range(4):
            for (Wt, wk, dst, dk) in ((Wkc2, "Wkc2", kc2, "kc2"), (Wvc2, "Wvc2", vc2, "vc2")):
                for g in range(2):
                    b = nb()
                    for k in range(8):
                        mm(P[b][:, :], Wt[:, k, g * 128:(g + 1) * 128], hT[:, k, c * 512:(c + 1) * 512], k == 0, k == 7, r=["hTall", wk], w=[PK[b]])
                    act(dst[0:64, g, c * 512:(c + 1) * 512], P[b][0:64, :], AF.Copy, r=[PK[b]], w=[dk])
                    if c == 0:
                        act(dst[64:128, g, 0:511], P[b][64:128, 1:512], AF.Copy, r=[PK[b]], w=[dk])
                    else:
                        act(dst[64:128, g, c * 512 - 1:(c + 1) * 512 - 1], P[b][64:128, :], AF.Copy, r=[PK[b]], w=[dk])
        dbg_out("qnT", qnT[0:64].rearrange("p k t -> p (k t)"), [("qnT", i) for i in range(NT)])
        dbg_out("ksT", ksT[0:64].rearrange("p k t -> p (k t)"), [("ksT", i) for i in range(NT)])
        dbg_out("gates", gates.rearrange("p a b c -> p (a b c)"), [("gates", i) for i in range(NT)])
        dbg_out("kc2", kc2.rearrange("p k t -> p (k t)"), ["kc2", "kc2pad"])
        if upto <= 5:
            return finish(nc, S, st)
        S.barrier()

        nbanks[0] = 8
        M.hi = hi_kv
        hiE = M.hi
        W1k = M.lo_alloc([16, 256], BF16); W1v = M.lo_alloc([16, 256], BF16)
        W2k = M.lo_alloc([2, 64], BF16); W2v = M.lo_alloc([2, 64], BF16)
        pkf = M.lo_alloc([16], F32); pvf = M.lo_alloc([16], F32); pkb = M.lo_alloc([16], BF16); pvb = M.lo_alloc([16], BF16)
        biask = M.lo_alloc([2], F32); biasv = M.lo_alloc([2], F32)
        hid = [M.lo_alloc([128], BF16) for _ in range(2)]
        ovl = M.lo_alloc([32], BF16)
        load_w(W1k, w1k, "W1k", 16, 256); load_w(W1v, w1v, "W1v", 16, 256)
        load_w(W2k, w2k, "W2k", 2, 64); load_w(W2v, w2v, "W2v", 2, 64)
        S.dma("sp", pkf, posk, w=["pkf"]); S.dma("sp", pvf, posv, w=["pvf"]); S.dma("sp", ovl[0:127], ovl_d, w=["ovl"])
        S.op("pool", lambda e: e.memset(VCX[0:127, :, 64:65], 1.0), w=["VCXa"])
        for g in range(2):
            S.op("pool", lambda e, g=g: e.tensor_copy(out=VCX[0:127, g, 65:97], in_=ovl[0:127]), r=["ovl"], w=["VCXb"])
        rt = [M.lo_alloc([128], BF16) for _ in range(3)]
        rt_i = [0]
        for (W1, w1key, W2, w2key, src, skey, posf_, pkey, isk) in ((W1k, "W1k", W2k, "W2k", kc2, ["kc2", "kc2pad"], pkf, "pkf", True),
                                                                    (W1v, "W1v", W2v, "W2v", vc2, ["vc2", "vc2pad"], pvf, "pvf", False)):
            srcv = src.rearrange("p g (n s) -> p g n s", s=16)
            bo = nb()
            for g in range(2):
                bh = []
                for hc in range(2):
                    b = nb()
                    while b == bo or b in bh:
                        b = nb()
                    bh.append(b)
                for lc in range(16):
                    ri = rt_i[0]; rt_i[0] = (ri + 1) % 3
                    rtv = rt[ri][:, 0:127]
                    S.op("dve", lambda e, rtv=rtv, g=g, lc=lc, srcv=srcv, posf_=posf_: e.tensor_scalar(
                        out=rtv, in0=srcv[:, g, (2 * lc) // 16:(2 * lc) // 16 + 127, (2 * lc) % 16], scalar1=posf_[:, lc:lc + 1], scalar2=None, op0=ALU.add),
                        r=skey + [pkey], w=[("rt", ri)])
                    for hc in range(2):
                        mm(P[bh[hc]][:, 0:127], W1[:, lc, hc * 128:(hc + 1) * 128], rtv, lc == 0, lc == 15, r=[w1key, ("rt", ri)], w=[PK[bh[hc]]])
                for hc in range(2):
                    act(hid[hc][:, 0:127], P[bh[hc]][:, 0:127], AF.Silu, r=[PK[bh[hc]]], w=[("hid", hc)])
                for hc in range(2):
                    mm(P[bo][0:127, g * 64:(g + 1) * 64], hid[hc][:, 0:127], W2[:, hc, :], hc == 0, hc == 1, r=[("hid", hc), w2key], w=[PK[bo]])
            if isk:
                d = drb[1][0:127, 0:128].rearrange("p (h d) -> p h d", h=2)
                head_norm_rope(P[bo][0:127, 0:128].rearrange("p (h d) -> p h d", h=2), [PK[bo]], 2, 64, nkct, "ngain", 0, 8, cosC[0:127], sinC[0:127], d, "drb1", np_=127)
                b2 = nb(); pb = P[b2][:, :].bitcast(BF16)
                for g_ in range(2):
                    tp(pb[0:64, g_ * 128:g_ * 128 + 127], drb[1][0:127, g_ * 64:(g_ + 1) * 64], ident[0:127, 0:127], r=["drb1", "ident"], w=[PK[b2]])
                act(kcmpT[0:64, :, 0:127], pb[0:64, 0:256].rearrange("p (g t) -> p g t", g=2)[:, :, 0:127], AF.Copy, r=[PK[b2]], w=["kcmpT"])
            else:
                act(VCX[0:127, :, 0:64], P[bo][0:127, 0:128].rearrange("p (g d) -> p g d", g=2), AF.Copy, r=[PK[bo]], w=["VCXc"])
        dbg_out("kcmpT", kcmpT[0:64].rearrange("p g t -> p (g t)"), ["kcmpT"])
        dbg_out("VCX", VCX[0:127].rearrange("p a b -> p (a b)"), ["VCXa", "VCXb", "VCXc"])
        if upto <= 6:
            return finish(nc, S, st)
        S.barrier()

        M.lo = lo2; M.hi = hi0
        onsa = M.hi_alloc([NT, 512], F32)
        mbT = M.hi_alloc([2, S_], BF16)
        vmT = M.hi_alloc([S_], BF16); XE = M.hi_alloc([NT, 128], BF16)
        fb = M.hi_alloc([NT, 32], F32); vj = M.hi_alloc([NT, 32], F32)
        pc = [M.lo_alloc([4, 128], BF16) for _ in range(2)]
        rsum = M.lo_alloc([8], F32); rec8 = M.lo_alloc([8], F32); gr = M.lo_alloc([8], F32)
        tmp_i = M.lo_alloc([8, 32], F32); imp = M.lo_alloc([2, 32], F32); m8 = M.lo_alloc([2, 8], F32)
        sel = M.lo_alloc([2, 32], F32); mbf = M.lo_alloc([2, 32], BF16)
        tmpo = M.lo_alloc([8, 64], F32)
        pw = [M.lo_alloc([3, 128], BF16) for _ in range(3)]
        onb = [M.lo_alloc([512], BF16) for _ in range(2)]
        S.dma("sp", vmT[0:127], vmT_d, w=["vmT"]); S.dma("sp", XE[0:32].rearrange("p a b -> p (a b)"), XE_d, w=["XE"])
        S.dma("sp", fb.rearrange("p a b -> p (a b)"), fb_d, w=["fb"]); S.dma("sp", vj.rearrange("p a b -> p (a b)"), vj_d, w=["vj"])
        VCXk = ["VCXa", "VCXb", "VCXc"]
        for i in range(NT):
            ts = slice(i * 128, (i + 1) * 128)
            import os
            if int(os.environ.get('KDEV_F', '9')) <= 0:
                continue
            sb_ = [nb(), nb()]
            ob = [nb(), nb()]
            for p in range(8):
                j, g = p // 2, p % 2
                if os.environ.get('KDEV_G0'):
                    g = 0
                mm(P[sb_[p // 4]][0:127, (p % 4) * 128:(p % 4 + 1) * 128], kcmpT[0:64, g, 0:127], qnT[0:64, p, ts], True, True,
                   r=["kcmpT", ("qnT", i)], w=[PK[sb_[p // 4]]])
            for hf_ in range(2):
                pcv = pc[hf_]
                act(pcv[0:127], P[sb_[hf_]][0:127, :].rearrange("p (a b) -> p a b", a=4), AF.Exp, r=[PK[sb_[hf_]]], w=[("pc", hf_)], scale=0.125)
                S.op("dve", lambda e, pcv=pcv: e.tensor_tensor(out=pcv[0:127], in0=pcv[0:127], in1=vmT[0:127, ts].unsqueeze(1).to_broadcast([127, 4, 128]), op=ALU.mult),
                     r=[("pc", hf_), "vmT"], w=[("pc", hf_)])
            import os
            FL = int(os.environ.get('KDEV_F', '9'))
            if FL <= 1:
                continue
            for p in range(8):
                g = p % 2
                mm(P[ob[p // 4]][:, (p % 4) * 97:(p % 4 + 1) * 97], pc[p // 4][0:127, p % 4, :], VCX[0:127, g, :], True, True,
                   r=[("pc", p // 4)] + VCXk, w=[PK[ob[p // 4]]])
            if FL <= 2:
                continue
            OC = [P[ob[h_]][:, 0:388].rearrange("p (a b) -> p a b", a=4) for h_ in range(2)]
            for h_ in range(2):
                S.op("dve", lambda e, h_=h_: e.tensor_scalar(out=rsum[:, h_ * 4:(h_ + 1) * 4], in0=OC[h_][:, :, 64], scalar1=1e-30, scalar2=None, op0=ALU.max), r=[PK[ob[h_]]], w=["rsum"])
            S.op("dve", lambda e: e.reciprocal(out=rec8, in_=rsum), r=["rsum"], w=["rec8"])
            S.op("dve", lambda e, i=i: e.tensor_tensor(out=gr, in0=gates[:, i, 0, :], in1=rec8, op=ALU.mult), r=["rec8", ("gates", i)], w=["gr"])
            for h_ in range(2):
                S.op("dve", lambda e, h_=h_, i=i: e.tensor_tensor(out=onsa[:, i, h_ * 256:(h_ + 1) * 256].rearrange("p (a b) -> p a b", a=4), in0=OC[h_][:, :, 0:64],
                                                                   in1=gr[:, h_ * 4:(h_ + 1) * 4].unsqueeze(2).to_broadcast([128, 4, 64]), op=ALU.mult),
                     r=[PK[ob[h_]], "gr"], w=[("onsa", i)])
                S.op("dve", lambda e, h_=h_: e.tensor_tensor(out=tmp_i[:, h_ * 4:(h_ + 1) * 4, :], in0=OC[h_][:, :, 65:97],
                                                             in1=rec8[:, h_ * 4:(h_ + 1) * 4].unsqueeze(2).to_broadcast([128, 4, 32]), op=ALU.mult),
                     r=[PK[ob[h_]], "rec8"], w=["tmp_i"])
            if FL <= 3:
                continue
            S.op("dve", lambda e: e.tensor_reduce(out=imp, in_=tmp_i.rearrange("t (j g) n -> t g n j", g=2), axis=AX.X, op=ALU.add), r=["tmp_i"], w=["imp"])
            S.op("dve", lambda e, i=i: e.tensor_tensor(out=imp, in0=imp, in1=fb[:, i, :].unsqueeze(1).to_broadcast([128, 2, 32]), op=ALU.add), r=["imp", "fb"], w=["imp"])
            if FL <= 4:
                continue
            for g in range(2):
                S.op("dve", lambda e, g=g: e.max(out=m8[:, g, :], in_=imp[:, g, :]), r=["imp"], w=["m8"])
                S.op("dve", lambda e, g=g: e.tensor_scalar(out=sel[:, g, :], in0=imp[:, g, :], scalar1=m8[:, g, 7:8], scalar2=None, op0=ALU.is_ge), r=["imp", "m8"], w=["sel"])
            S.op("dve", lambda e, i=i: e.tensor_tensor(out=sel, in0=sel, in1=vj[:, i, :].unsqueeze(1).to_broadcast([128, 2, 32]), op=ALU.mult), r=["sel", "vj"], w=["sel"])
            S.op("dve", lambda e: e.tensor_scalar(out=mbf, in0=sel, scalar1=-1.0, scalar2=30000.0, op0=ALU.add, op1=ALU.mult), r=["sel"], w=["mbf"])
            if i == 5:
                dbg_out("sel5", sel.rearrange("p a b -> p (a b)"), ["sel"])
                dbg_out("imp5", imp.rearrange("p a b -> p (a b)"), ["imp"])
            if FL <= 5:
                continue
            b = nb(); pb = P[b][:, :].bitcast(BF16)
            for g in range(2):
                tp(pb[0:32, g * 128:(g + 1) * 128], mbf[:, g, :], ident, r=["mbf", "ident"], w=[PK[b]])
            act(mbT[0:32, :, ts], pb[0:32, 0:256].rearrange("p (g t) -> p g t", g=2), AF.Copy, r=[PK[b]], w=[("mbT", i)])
        dbg_out("onsa_c", onsa.rearrange("p a b -> p (a b)"), [("onsa", i) for i in range(NT)])
        dbg_out("mbT", mbT[0:32].rearrange("p a b -> p (a b)"), [("mbT", i) for i in range(NT)])
        if upto <= 7:
            return finish(nc, S, st)

        S.barrier()
        nbanks[0] = 6
        jobs = []
        for p in range(8):
            j, g = p // 2, p % 2

            def extra(ps_ap, kt, a, b_, pk, g=g):
                mm(ps_ap, XE[0:32, kt, :], mbT[0:32, g, a:b_], False, True, r=["XE"] + [("mbT", q) for q in range(a // 128, b_ // 128)], w=[pk])

            def fin(c, Oacc, pk, p=p):
                rc = rec4[c % 2]
                S.op("dve", lambda e: e.reciprocal(out=rc, in_=Oacc[:, :, 64]), r=[pk], w=[("rec4", c % 2)])
                S.op("dve", lambda e: e.tensor_tensor(out=rc, in0=rc, in1=gates[:, 4 * c:4 * c + 4, 1, p], op=ALU.mult), r=[("rec4", c % 2)] + [("gates", 4 * c + q) for q in range(4)], w=[("rec4", c % 2)])
                tv = tmpo[:, 0:4, :]
                S.op("dve", lambda e: e.tensor_tensor(out=tv, in0=Oacc[:, :, 0:64], in1=rc.unsqueeze(2).to_broadcast([128, 4, 64]), op=ALU.mult), r=[pk, ("rec4", c % 2)], w=["tmpo"])
                S.op("pool", lambda e: e.tensor_tensor(out=onsa[:, 4 * c:4 * c + 4, p * 64:(p + 1) * 64], in0=onsa[:, 4 * c:4 * c + 4, p * 64:(p + 1) * 64], in1=tv, op=ALU.add),
                     r=["tmpo"] + [("onsa", 4 * c + q) for q in range(4)], w=[("onsa", 4 * c + q) for q in range(4)])
            jobs.append(dict(KT=ksT[:, g, :], kn=64, QT=(lambda a, b_, p=p: qnT[0:64, p, a:b_]), V=(lambda kt, g=g: vs[:, kt, g, :]), scale=0.125,
                             extra=extra, fin=fin, qk=(lambda c: [("qnT", 4 * c + q) for q in range(4)]), kk=(lambda kt: [("ksT", kt)]),
                             vk=(lambda kt: [("vs", kt), "vsones"])))
        causal_attn_multi(jobs)
        dbg_out("onsa_cs", onsa.rearrange("p a b -> p (a b)"), [("onsa", i) for i in range(NT)])
        if upto <= 8:
            return finish(nc, S, st)

        pw_i = [0]
        ob = [6, 7]

        def emit_ws(i, p):
            g = p % 2
            kts = [kt for kt in (i - 2, i - 1, i) if kt >= 0]
            b = nb()
            for kt in kts:
                sl = kt - (i - 2)
                mm(P[b][:, sl * 128:(sl + 1) * 128], kwT[0:64, g, kt * 128:(kt + 1) * 128], qnT[0:64, p, i * 128:(i + 1) * 128], True, True,
                   r=[("kwT", kt), ("qnT", i)], w=[PK[b]])
            return b

        wsteps = [(i, p) for i in range(NT) for p in range(8)]
        wpend = emit_ws(*wsteps[0])
        for wi_, (i, p) in enumerate(wsteps):
            ts = slice(i * 128, (i + 1) * 128)
            g = p % 2
            kts = [kt for kt in (i - 2, i - 1, i) if kt >= 0]
            b = wpend
            if wi_ + 1 < len(wsteps):
                wpend = emit_ws(*wsteps[wi_ + 1])
            s0 = kts[0] - (i - 2)
            wi = pw_i[0]; pw_i[0] = (wi + 1) % 3
            pwv = pw[wi]
            act(pwv[:, s0:3, :], P[b][:, s0 * 128:384].rearrange("p (a b) -> p a b", b=128), AF.Exp, r=[PK[b]], w=[("pw", wi)], scale=0.125)
            S.op("dve", lambda e, pwv=pwv, s0=s0: e.tensor_tensor(out=pwv[:, s0:3, :], in0=pwv[:, s0:3, :], in1=winm[:, s0:3, :], op=ALU.mult), r=[("pw", wi), "winm"], w=[("pw", wi)])
            for kt in kts:
                sl = kt - (i - 2)
                mm(P[ob[p // 4]][:, (p % 4) * 65:(p % 4 + 1) * 65], pwv[:, sl, :], vw[:, kt, g, :], kt == kts[0], kt == kts[-1],
                   r=[("pw", wi), ("vw", kt), "vwones"], w=[PK[ob[p // 4]]])
            if p != 7:
                continue
            OW = [P[ob[h_]][:, 0:260].rearrange("p (a b) -> p a b", a=4) for h_ in range(2)]
            for h_ in range(2):
                S.op("dve", lambda e, h_=h_: e.reciprocal(out=rec8[:, h_ * 4:(h_ + 1) * 4], in_=OW[h_][:, :, 64]), r=[PK[ob[h_]]], w=["rec8"])
            S.op("dve", lambda e, i=i: e.tensor_tensor(out=gr, in0=gates[:, i, 2, :], in1=rec8, op=ALU.mult), r=["rec8", ("gates", i)], w=["gr"])
            for h_ in range(2):
                S.op("dve", lambda e, h_=h_: e.tensor_tensor(out=tmpo[:, h_ * 4:(h_ + 1) * 4, :], in0=OW[h_][:, :, 0:64],
                                                             in1=gr[:, h_ * 4:(h_ + 1) * 4].unsqueeze(2).to_broadcast([128, 4, 64]), op=ALU.mult), r=[PK[ob[h_]], "gr"], w=["tmpo"])
            S.op("pool", lambda e, i=i: e.tensor_tensor(out=onb[i % 2], in0=onsa[:, i, :], in1=tmpo.rearrange("p a b -> p (a b)"), op=ALU.add), r=["tmpo", ("onsa", i)], w=[("onb", i % 2)])
            if "onsa_all" in D_:
                S.dma("sp", D_["onsa_all"][:, i * 512:(i + 1) * 512], onb[i % 2], r=[("onb", i % 2)])
            b = nb(); pb = P[b][:, :].bitcast(BF16)
            for j in range(4):
                tp(pb[:, j * 128:(j + 1) * 128], onb[i % 2][:, j * 128:(j + 1) * 128], ident, r=[("onb", i % 2), "ident"], w=[PK[b]])
            act(onT[:, :, ts], pb[:, 0:512].rearrange("p (k t) -> p k t", k=4), AF.Copy, r=[PK[b]], w=[("onT", i)])
        if upto <= 9:
            return finish(nc, S, st)
        S.barrier()

        nbanks[0] = 8
        M.lo = lo0; M.hi = hi0
        hT = M.hi_alloc([8, S_], BF16)
        mergedT = M.hi_alloc([8, S_], BF16)
        hi2 = M.hi
        Wgm = M.lo_alloc([8, 512], BF16); Wgnm = M.lo_alloc([8, 512], BF16); Wom = M.lo_alloc([4, 512], BF16); Won = M.lo_alloc([4, 512], BF16)
        e3 = M.lo_alloc([512], F32); e4 = M.lo_alloc([512], F32); tA = M.lo_alloc([512], F32); tB = M.lo_alloc([512], F32)
        mgb = [M.lo_alloc([512], BF16) for _ in range(2)]
        S.dma("sp", hT.rearrange("p k t -> p (k t)"), hTs, r=["hTs"], w=["hTall"])
        e3 = [e3, M.lo_alloc([512], F32)]; e4 = [e4, M.lo_alloc([512], F32)]
        tA = [tA, M.lo_alloc([512], F32)]; tB = [tB, M.lo_alloc([512], F32)]

        def i_front(cc, i, it):
            ts = slice(i * 128, (i + 1) * 128)
            bs = [4 * (it % 2) + q for q in range(4)]
            b1, b2, b3, b4 = bs
            for k in range(8):
                mm(P[b3][:, :], hT[:, k, ts], Wgm[:, k, :], k == 0, k == 7, r=["hTall", "Wgm"], w=[PK[b3]])
            for k in range(8):
                mm(P[b4][:, :], hT[:, k, ts], Wgnm[:, k, :], k == 0, k == 7, r=["hTall", "Wgnm"], w=[PK[b4]])
            for k in range(4):
                mm(P[b1][:, :], omT[:, k, ts], Wom[:, k, :], k == 0, k == 3, r=[("omT", i), "Wom"], w=[PK[b1]])
            for k in range(4):
                mm(P[b2][:, :], onT[:, k, ts], Won[:, k, :], k == 0, k == 3, r=[("onT", i), "Won"], w=[PK[b2]])
            return bs

        def i_back(cc, i, it, bs):
            ts = slice(i * 128, (i + 1) * 128)
            b1, b2, b3, b4 = bs
            par = it % 2
            act(e3[par], P[b3][:, :], AF.Sigmoid, r=[PK[b3]], w=[("e3", par)])
            act(e4[par], P[b4][:, :], AF.Sigmoid, r=[PK[b4]], w=[("e4", par)])
            S.op("dve", lambda e: e.tensor_tensor(out=tA[par], in0=P[b1][:, :], in1=e3[par], op=ALU.mult), r=[PK[b1], ("e3", par)], w=[("tA", par)])
            S.op("dve", lambda e: e.tensor_tensor(out=tB[par], in0=P[b2][:, :], in1=e4[par], op=ALU.mult), r=[PK[b2], ("e4", par)], w=[("tB", par)])
            S.op("dve", lambda e: e.tensor_tensor(out=mgb[par], in0=tA[par], in1=tB[par], op=ALU.add), r=[("tA", par), ("tB", par)], w=[("mgb", par)])
            if "merged" in D_:
                S.dma("sp", D_["merged"][i * 128:(i + 1) * 128, cc * 512:(cc + 1) * 512], mgb[par], r=[("mgb", par)])
            pb = P[b3][:, :].bitcast(BF16)
            for j in range(4):
                tp(pb[:, j * 128:(j + 1) * 128], mgb[par][:, j * 128:(j + 1) * 128], ident, r=[("mgb", par), "ident"], w=[PK[b3]])
            act(mergedT[:, cc * 4:(cc + 1) * 4, ts], pb[:, 0:512].rearrange("p (k t) -> p k t", k=4), AF.Copy, r=[PK[b3]], w=[("mergedT", i)])

        it = 0
        for cc in range(2):
            load_w_cols(Wgm, w_gm, "Wgm", 8, cc * 512, (cc + 1) * 512); load_w_cols(Wgnm, w_gnm, "Wgnm", 8, cc * 512, (cc + 1) * 512)
            load_w_cols(Wom, wo_mla, "Wom", 4, cc * 512, (cc + 1) * 512); load_w_cols(Won, wo_nsa, "Won", 4, cc * 512, (cc + 1) * 512)
            pend_ = i_front(cc, 0, it)
            for i in range(NT):
                cur_ = pend_
                if i + 1 < NT:
                    pend_ = i_front(cc, i + 1, it + 1)
                i_back(cc, i, it, cur_)
                it += 1
        if upto <= 10:
            return finish(nc, S, st)
        S.barrier()

        M.lo = lo0
        h2T = hT
        Wout = M.lo_alloc([8, D], BF16)
        G1 = M.lo_alloc([D], F32); A2 = M.lo_alloc([D], F32); B2 = M.lo_alloc([D], F32)
        xb = [M.lo_alloc([D], F32) for _ in range(2)]
        x1t = [M.lo_alloc([D], F32) for _ in range(2)]
        junk = M.lo_alloc([D], F32); tmpA = M.lo_alloc([D], F32)
        hb = [M.lo_alloc([D], BF16) for _ in range(2)]
        ssq = M.lo_alloc([NT], F32); rs = M.lo_alloc([NT], F32)
        S.dma("sp", G1, mods[:, 2 * D:3 * D], r=["mods"], w=["G1"])
        S.dma("sp", B2, mods[:, 3 * D:4 * D], r=["mods"], w=["AB2"])
        S.dma("sp", A2, mods[:, 4 * D:5 * D], r=["mods"], w=["AB2"])
        load_w(Wout, w_out, "Wout", 8, D, ceng="pool")
        tmpA2 = [tmpA, M.lo_alloc([D], F32)]
        tmpJ = [M.lo_alloc([D], F32) for _ in range(3)]
        xb = xb + [M.lo_alloc([D], F32)]
        x1t = x1t + [M.lo_alloc([D], F32)]

        def j_front(i):
            ts = slice(i * 128, (i + 1) * 128)
            par = i % 3
            S.dma("sp", xb[par], x[ts, :], w=[("xb", par)])
            for cc in range(2):
                b = 2 * par + cc
                for k in range(8):
                    mm(P[b][:, :], mergedT[:, k, ts], Wout[:, k, cc * 512:(cc + 1) * 512], k == 0, k == 7, r=[("mergedT", i), "Wout"], w=[PK[b]])

        def j_mid(i):
            ts = slice(i * 128, (i + 1) * 128)
            par = i % 3
            for cc in range(2):
                b = 2 * par + cc
                S.op("dve", lambda e, b=b, cc=cc: e.tensor_tensor(out=tmpJ[par][:, cc * 512:(cc + 1) * 512], in0=P[b][:, :], in1=G1[:, cc * 512:(cc + 1) * 512], op=ALU.mult), r=[PK[b], "G1"], w=[("tmpJ", par)])
                S.op("pool", lambda e, cc=cc: e.tensor_tensor(out=x1t[par][:, cc * 512:(cc + 1) * 512], in0=tmpJ[par][:, cc * 512:(cc + 1) * 512], in1=xb[par][:, cc * 512:(cc + 1) * 512], op=ALU.add),
                     r=[("tmpJ", par), ("xb", par)], w=[("x1t", par)])
            S.dma("sp", x1s[ts, :], x1t[par], r=[("x1t", par)], w=[("x1s", i)])
            norm_a(i, x1t[par], ("x1t", par))

        bank_state[0] = 0

        def nbJ():
            b = 6 + (bank_state[0] % 2)
            bank_state[0] = (bank_state[0] + 1) % 2
            return b
        nb_saved = nb
        nb = nbJ
        j_front(0); j_mid(0); j_front(1); j_mid(1)
        for i in range(NT):
            if i + 2 < NT:
                j_front(i + 2)
            b_ = norm_b1(i, x1t[i % 3], ("x1t", i % 3), A2, B2, ["AB2"])
            if i + 2 < NT:
                j_mid(i + 2)
            norm_b2(i, b_, h2T, "h2T")
        nb = nb_saved
        bank_state[0] = 0
        if "x1" in D_:
            S.dma("sp", D_["x1"], x1s, r=[("x1s", i) for i in range(NT)])
        if upto <= 11:
            return finish(nc, S, st)
        S.barrier()

        M.lo = lo_pers; M.hi = hi2 + 8 * S_ * 2
        Wd = M.hi_alloc([NFC, D], BF16)
        actT = M.hi_alloc([NFC, 1024], BF16)
        G2 = M.lo_alloc([D], F32)
        Wg2 = [M.lo_alloc([8, 256], BF16) for _ in range(2)]; Wu2 = [M.lo_alloc([8, 256], BF16) for _ in range(2)]
        sg = [M.lo_alloc([512], F32) for _ in range(2)]
        xb = [M.lo_alloc([D], F32) for _ in range(2)]
        ot = [M.lo_alloc([D], F32) for _ in range(2)]
        tmpA = M.lo_alloc([D], F32)
        S.dma("sp", G2, mods[:, 5 * D:6 * D], r=["mods"], w=["G2"])
        load_w(Wd, wd, "Wd", NFC, D)
        h2k = [("h2T", i) for i in range(NT)]
        out_toks = []
        for half in range(2):
            def ld(jg_):
                wb_ = jg_ % 2
                load_w_cols(Wg2[wb_], wg, ("Wg2", wb_), 8, jg_ * 256, (jg_ + 1) * 256)
                load_w_cols(Wu2[wb_], wu, ("Wu2", wb_), 8, jg_ * 256, (jg_ + 1) * 256)
            ld(0)
            for jg in range(NFC // 2):
                wb = jg % 2
                if jg + 1 < NFC // 2:
                    ld(jg + 1)
                for jj in range(2):
                    j = jg * 2 + jj
                    for tc in range(2):
                        t0 = half * 1024 + tc * 512
                        bg, bu = nb(), nb()
                        for k in range(8):
                            mm(P[bg][:, :], Wg2[wb][:, k, jj * 128:(jj + 1) * 128], h2T[:, k, t0:t0 + 512], k == 0, k == 7, r=[("Wg2", wb)] + h2k, w=[PK[bg]])
                        for k in range(8):
                            mm(P[bu][:, :], Wu2[wb][:, k, jj * 128:(jj + 1) * 128], h2T[:, k, t0:t0 + 512], k == 0, k == 7, r=[("Wu2", wb)] + h2k, w=[PK[bu]])
                        act(sg[tc], P[bg][:, :], AF.Silu, r=[PK[bg]], w=[("sg", tc)])
                        S.op("dve", lambda e, j=j, tc=tc, bu=bu: e.tensor_tensor(out=actT[:, j, tc * 512:(tc + 1) * 512], in0=P[bu][:, :], in1=sg[tc], op=ALU.mult),
                             r=[PK[bu], ("sg", tc)], w=[("actT", tc)])
            for il in range(8):
                i = half * 8 + il
                ts = slice(i * 128, (i + 1) * 128)
                S.dma("sp", xb[i % 2], x1s[ts, :], r=[("x1s", i)], w=[("xb", i % 2)])
                for cc in range(2):
                    b = nb()
                    for j in range(NFC):
                        mm(P[b][:, :], actT[:, j, il * 128:(il + 1) * 128], Wd[:, j, cc * 512:(cc + 1) * 512], j == 0, j == NFC - 1, r=[("actT", il // 4), "Wd"], w=[PK[b]])
                    S.op("dve", lambda e, b=b, cc=cc: e.tensor_tensor(out=tmpA[:, cc * 512:(cc + 1) * 512], in0=P[b][:, :], in1=G2[:, cc * 512:(cc + 1) * 512], op=ALU.mult), r=[PK[b], "G2"], w=["tmpA"])
                    S.op("pool", lambda e, i=i, cc=cc: e.tensor_tensor(out=ot[i % 2][:, cc * 512:(cc + 1) * 512], in0=tmpA[:, cc * 512:(cc + 1) * 512], in1=xb[i % 2][:, cc * 512:(cc + 1) * 512], op=ALU.add),
                         r=["tmpA", ("xb", i % 2)], w=[("ot", i % 2)])
                S.dma("sp", out[ts, :], ot[i % 2], r=[("ot", i % 2)], w=[("out", i)])
        return finish(nc, S, st)


def finish(nc, S, st):
    for q in S.dsem:
        for i in range(len(S.dsem[q])):
            if S.dcnt[q][i]:
                S._wait("sp", (("d", q, i), S.dcnt[q][i]))
    for e2 in ("pe", "act", "dve", "pool"):
        if S.cnt[e2]:
            S._wait("sp", (e2, S.cnt[e2]))
    st.close()
    return nc


def _consts():
    bf = ml_dtypes.bfloat16
    c = {}
    c["ident"] = np.eye(128, dtype=np.float32).astype(bf)
    a = np.arange(128)
    c["tri"] = (a[:, None] <= a[None, :]).astype(np.float32).astype(bf)
    w = np.zeros((128, 3, 128), np.float32)
    w[:, 0, :] = (a[:, None] > a[None, :])
    w[:, 1, :] = 1.0
    w[:, 2, :] = (a[:, None] <= a[None, :])
    c["winm"] = w.reshape(128, 384).astype(bf)
    inv16 = (np.float32(500000.0) ** (-np.arange(0, 32, 2, dtype=np.float32) / np.float32(32))).astype(np.float32)
    inv8 = (np.float32(500000.0) ** (-np.arange(0, 16, 2, dtype=np.float32) / np.float32(16))).astype(np.float32)
    c["inv16"] = np.tile(inv16[None], (128, 1)).astype(np.float32)
    c["inv8"] = np.tile(inv8[None], (128, 1)).astype(np.float32)
    n = np.arange(127)
    starts = n * 16
    j = np.arange(32)
    ovl = ((starts[:, None] < j[None, :] * 64 + 64) & (starts[:, None] + 32 > j[None, :] * 64))
    c["ovl"] = ovl.astype(np.float32).astype(bf)
    t = np.arange(S_)
    c["vmT"] = ((starts[:, None] + 31) <= t[None, :]).astype(np.float32).astype(bf)
    XE = np.zeros((32, NT, 128), np.float32)
    for kt in range(NT):
        XE[2 * kt, kt, 0:64] = 1.0
        XE[2 * kt + 1, kt, 64:128] = 1.0
    c["XE"] = XE.reshape(32, NT * 128).astype(bf)
    cur = (t // 64)
    forced = (j[None, :] == 0) | (j[None, :] == cur[:, None]) | (j[None, :] == cur[:, None] - 1)
    valid = j[None, :] <= cur[:, None]
    fb = np.where(valid, np.where(forced, 1e4, 0.0), -1e30).astype(np.float32)
    c["fb"] = fb.reshape(NT, 128, 32).transpose(1, 0, 2).reshape(128, NT * 32).copy()
    c["vj"] = valid.astype(np.float32).reshape(NT, 128, 32).transpose(1, 0, 2).reshape(128, NT * 32).copy()
    return c


def _rep(v, n=128):
    return np.ascontiguousarray(np.broadcast_to(np.asarray(v, np.float32)[None, :], (n, v.shape[0])))


def prep_inputs(inp):
    f = lambda a: np.ascontiguousarray(np.asarray(a, dtype=np.float32))
    w_in = f(inp["w_in"][0])
    o = np.cumsum([0, 768, 256, 32, 512, 128, 128, 128, 128, 128, 128, 24, 1024, 1024])
    seg = lambda i: w_in[:, o[i]:o[i + 1]]
    shared = {}
    shared["ada_w"] = f(inp["ada_w"][0]); shared["adabB"] = _rep(f(inp["ada_b"][0]))
    shared["g1B"] = _rep(f(inp["norm1_gain"][0])); shared["g2B"] = _rep(f(inp["norm2_gain"][0]))
    shared["w_cq"] = f(seg(0)); shared["w_ckv"] = f(seg(1)); shared["w_kpe"] = f(seg(2))
    qn = seg(3).reshape(D, 8, 64)
    shared["w_qn"] = f(qn[:, PH, :].reshape(D, 512))
    kc = seg(4).reshape(D, 2, 64); vc = seg(5).reshape(D, 2, 64)
    shared["w_kc2"] = f(np.stack([kc[:, 0], kc[:, 0], kc[:, 1], kc[:, 1]], 1).reshape(D, 256))
    shared["w_vc2"] = f(np.stack([vc[:, 0], vc[:, 0], vc[:, 1], vc[:, 1]], 1).reshape(D, 256))
    shared["w_kv4"] = f(np.concatenate([seg(6), seg(8), seg(7), seg(9)], 1))
    gn = seg(10).reshape(D, 8, 3)
    shared["w_gn"] = f(gn[:, PH, :].transpose(0, 2, 1).reshape(D, 24))
    shared["w_gm"] = f(seg(11)); shared["w_gnm"] = f(seg(12))
    shared["qag"] = f(f(inp["mla_q_a_gain"][0]).reshape(6, 128).T); shared["kvag"] = f(f(inp["mla_kv_a_gain"][0]).reshape(2, 128).T)
    shared["w_qb"] = f(inp["mla_w_q_b"][0]); shared["w_kvb"] = f(inp["mla_w_kv_b"][0])
    shared["qgB"] = _rep(f(inp["mla_q_gain"][0])); shared["kgB"] = _rep(f(inp["mla_k_gain"][0]))
    shared["nqB"] = _rep(f(inp["nsa_q_gain"][0])); shared["nkcB"] = _rep(f(inp["nsa_kc_gain"][0]))
    shared["nksB"] = _rep(f(inp["nsa_ks_gain"][0])); shared["nkwB"] = _rep(f(inp["nsa_kw_gain"][0]))
    shared["posk"] = f(f(inp["cmp_pos_k"][0]).reshape(16, 128).T); shared["posv"] = f(f(inp["cmp_pos_v"][0]).reshape(16, 128).T)
    shared["w1k"] = f(inp["cmp_w1_k"][0]); shared["w2k"] = f(inp["cmp_w2_k"][0])
    shared["w1v"] = f(inp["cmp_w1_v"][0]); shared["w2v"] = f(inp["cmp_w2_v"][0])
    shared["wo_mla"] = f(inp["w_o_mla"][0])
    shared["wo_nsa"] = f(f(inp["w_o_nsa"][0]).reshape(8, 64, D)[PH].reshape(512, D))
    shared["w_out"] = f(inp["w_out"][0])
    shared["wg"] = f(inp["ffn_w_gate"][0]); shared["wu"] = f(inp["ffn_w_up"][0]); shared["wd"] = f(inp["ffn_w_down"][0])
    shared.update(_consts())
    maps = []
    xs = np.asarray(inp["x"], np.float32); cs = np.asarray(inp["c"], np.float32); ps = np.asarray(inp["positions"]).astype(np.int32)
    for b in range(xs.shape[0]):
        m = dict(shared)
        m["x"] = np.ascontiguousarray(xs[b])
        m["c_pk"] = np.ascontiguousarray(cs[b].reshape(8, 128).T)
        m["pos_pk"] = np.ascontiguousarray(ps[b].reshape(NT, 128).T)
        m["posC"] = np.ascontiguousarray(ps[b][31::16][:127].reshape(127, 1))
        maps.append(m)
    return maps


_NC_CACHE = {}


def kernel(**inputs):
    maps = prep_inputs(inputs)
    if "nc" not in _NC_CACHE:
        _NC_CACHE["nc"] = build()
    nc = _NC_CACHE["nc"]
    res = run_bass_kernel_spmd(nc, maps, core_ids=list(range(len(maps))))
    return np.stack([np.asarray(r["out"], dtype=np.float32) for r in res.results], 0)
```

```python
import contextlib
import numpy as np
import ml_dtypes
import concourse.bass as bass
import concourse.mybir as mybir
from concourse.bass_utils import run_bass_kernel_spmd

F32 = mybir.dt.float32
BF16 = mybir.dt.bfloat16
I32 = mybir.dt.int32
ALU = mybir.AluOpType
AF = mybir.ActivationFunctionType
AX = mybir.AxisListType

S_ = 2048
D = 1024
NT = 16
DFF = 2816
NFC = 22
EPS = 1e-6
PH = [0, 4, 1, 5, 2, 6, 3, 7]
TWO_PI = float(2 * np.pi)
PI = float(np.pi)


class Sched:
    N_DMA_SLOTS = {"sp": 24, "pool": 8, "act": 4}

    def __init__(self, nc, stack):
        self.nc = nc
        self.E = {"pe": nc.tensor, "act": nc.scalar, "dve": nc.vector, "pool": nc.gpsimd, "sp": nc.sync}
        self.sem, self.cnt = {}, {}
        for e in ("pe", "act", "dve", "pool"):
            self.sem[e] = stack.enter_context(nc.semaphore("s_" + e))
            self.cnt[e] = 0
        self.dsem, self.dcnt, self.dnext = {}, {}, {}
        for q, n in self.N_DMA_SLOTS.items():
            self.dsem[q] = [stack.enter_context(nc.semaphore(f"d_{q}{i}")) for i in range(n)]
            self.dcnt[q] = [0] * n
            self.dnext[q] = 0
        self.seen = {e: {} for e in self.E}
        self.lastw, self.readers = {}, {}
        self.n_wait = 0
        self.n_inst = 0
        self.prog = {}

    def check_no_deadlock(self):
        val = {}
        ptr = {e: 0 for e in self.prog}
        progress = True
        while progress:
            progress = False
            for e, lst in self.prog.items():
                while ptr[e] < len(lst):
                    it = lst[ptr[e]]
                    if it[0] == "w":
                        if val.get(it[1], 0) < it[2]:
                            break
                    else:
                        val[it[1]] = val.get(it[1], 0) + it[2]
                    ptr[e] += 1
                    progress = True
        stuck = {e: (ptr[e], len(l), l[ptr[e]]) for e, l in self.prog.items() if ptr[e] < len(l)}
        assert not stuck, f"DEADLOCK in emitted program: {stuck}"

    def _sem_of(self, src):
        return self.dsem[src[1]][src[2]] if isinstance(src, tuple) else self.sem[src]

    def _wait(self, e, tok):
        src, val = tok
        if self.seen[e].get(src, 0) >= val:
            return
        self.E[e].wait_ge(self._sem_of(src), val)
        self.seen[e][src] = val
        self.n_wait += 1
        self.prog.setdefault(e, []).append(("w", src, val))

    def _deps(self, e, r, w):
        toks = []
        for k in r:
            t = self.lastw.get(k)
            if t is not None:
                toks.append(t)
        for k in w:
            t = self.lastw.get(k)
            if t is not None:
                toks.append(t)
            for t in self.readers.get(k, ()):
                toks.append(t)
        for t in toks:
            if t[0] == e and e == "pe":
                continue
            self._wait(e, t)

    def _commit(self, tok, r, w):
        for k in r:
            lst = self.readers.setdefault(k, [])
            lst[:] = [t for t in lst if t[0] != tok[0]]
            lst.append(tok)
        for k in w:
            self.lastw[k] = tok
            self.readers[k] = []

    def op(self, e, fn, r=(), w=(), signal=True):
        self._deps(e, r, w)
        ins = fn(self.E[e])
        if signal:
            self.cnt[e] += 1
            ins.then_inc(self.sem[e], 1)
            tok = (e, self.cnt[e])
            self.prog.setdefault(e, []).append(("i", e, 1))
        else:
            tok = (e, self.cnt[e] + 1)
        self._commit(tok, r, w)
        self.n_inst += 1
        return tok

    def dma(self, q, out, in_, r=(), w=(), **kw):
        slot = self.dnext[q]
        self.dnext[q] = (slot + 1) % len(self.dsem[q])
        src = ("d", q, slot)
        if self.dcnt[q][slot] > 0:
            self._wait(q, (src, self.dcnt[q][slot]))
        self._deps(q, r, w)
        ins = self.E[q].dma_start(out=out, in_=in_, **kw)
        self.dcnt[q][slot] += 16
        ins.then_inc(self.dsem[q][slot], 16)
        self.prog.setdefault(q, []).append(("i", src, 16))
        tok = (src, self.dcnt[q][slot])
        self._commit(tok, r, w)
        self.n_inst += 1
        return tok

    def barrier(self):
        for e in ("pe", "act", "dve", "pool", "sp"):
            for e2 in ("pe", "act", "dve", "pool"):
                if self.cnt[e2] and not (e2 == e == "pe"):
                    self._wait(e, (e2, self.cnt[e2]))
            for q in self.dsem:
                for i in range(len(self.dsem[q])):
                    if self.dcnt[q][i]:
                        self._wait(e, (("d", q, i), self.dcnt[q][i]))
        self.lastw.clear()
        self.readers.clear()


class Mem:
    def __init__(self, big, nbytes):
        self.big, self.lo, self.hi, self.n = big, 0, nbytes, nbytes

    def _view(self, off, shape, dt):
        nel = int(np.prod(shape))
        esz = 4 if dt in (F32, I32) else 2
        nb = nel * esz
        ap = self.big[:, off // 2:(off + nb) // 2]
        if esz == 4:
            ap = ap.bitcast(dt)
        if len(shape) == 2:
            ap = ap.rearrange("p (a b) -> p a b", a=shape[0])
        elif len(shape) == 3:
            ap = ap.rearrange("p (a b c) -> p a b c", a=shape[0], b=shape[1])
        return ap

    def lo_alloc(self, shape, dt):
        nb = int(np.prod(shape)) * (4 if dt in (F32, I32) else 2)
        nb = (nb + 63) // 64 * 64
        off = self.lo
        self.lo += nb
        assert self.lo <= self.hi, f"SBUF overflow lo={self.lo} hi={self.hi}"
        return self._view(off, shape, dt)

    def hi_alloc(self, shape, dt):
        nb = int(np.prod(shape)) * (4 if dt in (F32, I32) else 2)
        nb = (nb + 63) // 64 * 64
        self.hi -= nb
        assert self.lo <= self.hi, f"SBUF overflow lo={self.lo} hi={self.hi}"
        return self._view(self.hi, shape, dt)


def build(upto=99, dbg=()):
    nc = bass.Bass("TRN2", target_bir_lowering=False)
    I = {}

    def din(name, shape, dt=F32):
        I[name] = nc.dram_tensor(name, list(shape), dt, kind="ExternalInput").ap()
        return I[name]

    x = din("x", [S_, D]); c_pk = din("c_pk", [128, 8]); pos_pk = din("pos_pk", [128, NT], I32)
    posC = din("posC", [127, 1], I32)
    ada_w = din("ada_w", [D, 6 * D]); adabB = din("adabB", [128, 6 * D]); g1B = din("g1B", [128, D]); g2B = din("g2B", [128, D])
    w_cq = din("w_cq", [D, 768]); w_ckv = din("w_ckv", [D, 256]); w_kpe = din("w_kpe", [D, 32])
    w_qn = din("w_qn", [D, 512]); w_kc2 = din("w_kc2", [D, 256]); w_vc2 = din("w_vc2", [D, 256])
    w_kv4 = din("w_kv4", [D, 512]); w_gn = din("w_gn", [D, 24]); w_gm = din("w_gm", [D, D]); w_gnm = din("w_gnm", [D, D])
    qag = din("qag", [128, 6]); kvag = din("kvag", [128, 2])
    w_qb = din("w_qb", [768, 768]); w_kvb = din("w_kvb", [256, 1024])
    qgB = din("qgB", [128, 96]); kgB = din("kgB", [128, 96])
    nqB = din("nqB", [128, 64]); nkcB = din("nkcB", [128, 64]); nksB = din("nksB", [128, 64]); nkwB = din("nkwB", [128, 64])
    posk = din("posk", [128, 16]); w1k = din("w1k", [2048, 256]); w2k = din("w2k", [256, 64])
    posv = din("posv", [128, 16]); w1v = din("w1v", [2048, 256]); w2v = din("w2v", [256, 64])
    wo_mla = din("wo_mla", [512, D]); wo_nsa = din("wo_nsa", [512, D]); w_out = din("w_out", [D, D])
    wg = din("wg", [D, DFF]); wu = din("wu", [D, DFF]); wd = din("wd", [DFF, D])
    ident_d = din("ident", [128, 128], BF16); tri_d = din("tri", [128, 128], BF16); winm_d = din("winm", [128, 384], BF16)
    inv16_d = din("inv16", [128, 16]); inv8_d = din("inv8", [128, 8])
    ovl_d = din("ovl", [127, 32], BF16); vmT_d = din("vmT", [127, S_], BF16); XE_d = din("XE", [32, NT * 128], BF16)
    fb_d = din("fb", [128, NT * 32]); vj_d = din("vj", [128, NT * 32])
    out = nc.dram_tensor("out", [S_, D], F32, kind="ExternalOutput").ap()
    hTs = nc.dram_tensor("hTs", [128, 8 * S_], BF16).ap()
    mods = nc.dram_tensor("mods", [128, 6 * D], F32).ap()
    x1s = nc.dram_tensor("x1s", [S_, D], F32).ap()
    D_ = {}
    for name, shape, dt in dbg:
        D_[name] = nc.dram_tensor("dbg_" + name, list(shape), dt, kind="ExternalOutput").ap()

    st = contextlib.ExitStack()
    with st:
        S = Sched(nc, st)
        NB = 204800
        big = st.enter_context(nc.sbuf_tensor("big", [128, NB // 2], BF16))
        M = Mem(big, NB)
        P = [st.enter_context(nc.psum_tensor(f"ps{i}", [128, 512], F32)) for i in range(8)]
        PK = [f"ps{i}" for i in range(8)]
        bank_state = [0]

        nbanks = [8]

        def nb():
            b = bank_state[0] % nbanks[0]
            bank_state[0] = (b + 1) % nbanks[0]
            return b

        def mm(ps_ap, lhsT, rhs, start, stop, r, w, sig=None, **kw):
            S.op("pe", lambda e: e.matmul(ps_ap, lhsT=lhsT, rhs=rhs, start=start, stop=stop, **kw), r=r, w=w, signal=bool(stop) if sig is None else sig)

        def tp(ps_ap, in_, ident_ap, r, w):
            S.op("pe", lambda e: e.transpose(out=ps_ap, in_=in_, identity=ident_ap), r=r, w=w)

        def act(out_, in_, func, r, w, **kw):
            S.op("act", lambda e: e.activation(out=out_, in_=in_, func=func, **kw), r=r, w=w)

        def dbg_out(name, ap, r):
            if name in D_:
                S.dma("sp", D_[name], ap, r=r)

        ident = M.lo_alloc([128], BF16); tri = M.lo_alloc([128], BF16); winm = M.lo_alloc([3, 128], BF16)
        onesb = M.lo_alloc([128], BF16)
        stg = [M.lo_alloc([1024], F32) for _ in range(3)]
        stg_i = [0]
        cosM = M.lo_alloc([NT, 16], F32); sinM = M.lo_alloc([NT, 16], F32)
        cosN = M.lo_alloc([NT, 8], F32); sinN = M.lo_alloc([NT, 8], F32)
        cosC = M.lo_alloc([8], F32); sinC = M.lo_alloc([8], F32)
        lo_pers = M.lo
        omT = M.lo_alloc([4, S_], BF16); onT = M.lo_alloc([4, S_], BF16)
        S.dma("sp", ident, ident_d, w=["ident"])
        S.dma("sp", tri, tri_d, w=["tri"])
        S.dma("sp", winm.rearrange("p a b -> p (a b)"), winm_d, w=["winm"])
        S.op("pool", lambda e: e.memset(onesb, 1.0), w=["onesb"])

        def load_w(dst, W, key, KC, N, ceng="dve"):
            Wv = W.rearrange("(k p) n -> p k n", p=128)
            if N <= 1024:
                g = max(1, min(KC, 1024 // N))
                for k0 in range(0, KC, g):
                    k1 = min(KC, k0 + g)
                    si = stg_i[0]; stg_i[0] = (si + 1) % 3
                    sv = stg[si][:, 0:(k1 - k0) * N].rearrange("p (k n) -> p k n", n=N)
                    S.dma("sp", sv, Wv[:, k0:k1, :], w=[("stg", si)])
                    S.op(ceng, lambda e, sv=sv, k0=k0, k1=k1: e.tensor_copy(out=dst[:, k0:k1, :], in_=sv), r=[("stg", si)], w=[key])
            else:
                for k in range(KC):
                    for c0 in range(0, N, 1024):
                        c1 = min(N, c0 + 1024)
                        si = stg_i[0]; stg_i[0] = (si + 1) % 3
                        sv = stg[si][:, 0:c1 - c0]
                        S.dma("sp", sv, Wv[:, k, c0:c1], w=[("stg", si)])
                        S.op(ceng, lambda e, sv=sv, k=k, c0=c0, c1=c1: e.tensor_copy(out=dst[:, k, c0:c1], in_=sv), r=[("stg", si)], w=[key])

        def load_w_cols(dst, W, key, KC, c0, c1, ceng="dve"):
            Wv = W.rearrange("(k p) n -> p k n", p=128)
            N = c1 - c0
            g = max(1, min(KC, 1024 // N))
            for k0 in range(0, KC, g):
                k1 = min(KC, k0 + g)
                si = stg_i[0]; stg_i[0] = (si + 1) % 3
                sv = stg[si][:, 0:(k1 - k0) * N].rearrange("p (k n) -> p k n", n=N)
                S.dma("sp", sv, Wv[:, k0:k1, c0:c1], w=[("stg", si)])
                S.op(ceng, lambda e, sv=sv, k0=k0, k1=k1: e.tensor_copy(out=dst[:, k0:k1, :], in_=sv), r=[("stg", si)], w=[key])

        def sincos(ang, shape, cos_o, sin_o, np_, tmp_f, tmp_i, tmp_m, key):
            for (shift, dst) in ((0.0, sin_o), (PI / 2, cos_o)):
                S.op("dve", lambda e: e.tensor_scalar(out=tmp_f, in0=ang, scalar1=shift, scalar2=None, op0=ALU.add), r=[key + "ang"], w=[key + "f"])
                S.op("dve", lambda e: e.tensor_scalar(out=tmp_i, in0=tmp_f, scalar1=float(1 / TWO_PI), scalar2=None, op0=ALU.mult), r=[key + "f"], w=[key + "i"])
                S.op("dve", lambda e: e.tensor_copy(out=tmp_m, in_=tmp_i), r=[key + "i"], w=[key + "m"])
                S.op("dve", lambda e: e.scalar_tensor_tensor(out=tmp_f, in0=tmp_m, scalar=-TWO_PI, in1=tmp_f, op0=ALU.mult, op1=ALU.add), r=[key + "m", key + "f"], w=[key + "f"])
                S.op("dve", lambda e: e.tensor_scalar(out=tmp_m, in0=tmp_f, scalar1=PI, scalar2=None, op0=ALU.is_gt), r=[key + "f"], w=[key + "m"])
                S.op("dve", lambda e: e.scalar_tensor_tensor(out=tmp_f, in0=tmp_m, scalar=-TWO_PI, in1=tmp_f, op0=ALU.mult, op1=ALU.add), r=[key + "m", key + "f"], w=[key + "f"])
                S.op("dve", lambda e: e.tensor_scalar(out=tmp_m, in0=tmp_f, scalar1=-PI, scalar2=None, op0=ALU.is_lt), r=[key + "f"], w=[key + "m"])
                S.op("dve", lambda e: e.scalar_tensor_tensor(out=tmp_f, in0=tmp_m, scalar=TWO_PI, in1=tmp_f, op0=ALU.mult, op1=ALU.add), r=[key + "m", key + "f"], w=[key + "f"])
                act(dst, tmp_f, AF.Sin, r=[key + "f"], w=[key + "out"])

        lo0, hi0 = M.lo, M.hi
        if upto <= -2:
            return finish(nc, S, st)
        posi = M.lo_alloc([NT], I32); posf = M.lo_alloc([NT], F32)
        posCi = M.lo_alloc([1], I32); posCf = M.lo_alloc([1], F32)
        inv16 = M.lo_alloc([16], F32); inv8 = M.lo_alloc([8], F32)
        angM = M.lo_alloc([NT, 16], F32); tfM = M.lo_alloc([NT, 16], F32); tiM = M.lo_alloc([NT, 16], I32); tmM = M.lo_alloc([NT, 16], F32)
        S.dma("sp", posi, pos_pk, w=["posi"])
        S.dma("sp", posCi[0:127], posC, w=["posCi"])
        S.dma("sp", inv16, inv16_d, w=["inv16"])
        S.dma("sp", inv8, inv8_d, w=["inv8"])
        S.op("dve", lambda e: e.tensor_copy(out=posf, in_=posi), r=["posi"], w=["posf"])
        S.op("dve", lambda e: e.tensor_copy(out=posCf[0:127], in_=posCi[0:127]), r=["posCi"], w=["posCf"])
        S.op("dve", lambda e: e.tensor_tensor(out=angM, in0=posf.unsqueeze(2).to_broadcast([128, NT, 16]),
                                              in1=inv16.unsqueeze(1).to_broadcast([128, NT, 16]), op=ALU.mult), r=["posf", "inv16"], w=["Mang"])
        sincos(angM, None, cosM, sinM, 128, tfM, tiM, tmM, "M")
        a8 = angM.rearrange("p a b -> p (a b)")[:, 0:NT * 8].rearrange("p (a b) -> p a b", b=8)
        f8 = tfM.rearrange("p a b -> p (a b)")[:, 0:NT * 8].rearrange("p (a b) -> p a b", b=8)
        i8 = tiM.rearrange("p a b -> p (a b)")[:, 0:NT * 8].rearrange("p (a b) -> p a b", b=8)
        m8_ = tmM.rearrange("p a b -> p (a b)")[:, 0:NT * 8].rearrange("p (a b) -> p a b", b=8)
        S.op("dve", lambda e: e.tensor_tensor(out=a8, in0=posf.unsqueeze(2).to_broadcast([128, NT, 8]),
                                              in1=inv8.unsqueeze(1).to_broadcast([128, NT, 8]), op=ALU.mult), r=["posf", "inv8", "Mout", "Mf", "Mm", "Mi"], w=["Nang"])
        sincos(a8, None, cosN, sinN, 128, f8, i8, m8_, "N")
        aC = angM.rearrange("p a b -> p (a b)")[0:127, 0:8]
        fC = tfM.rearrange("p a b -> p (a b)")[0:127, 0:8]
        iC = tiM.rearrange("p a b -> p (a b)")[0:127, 0:8]
        mC = tmM.rearrange("p a b -> p (a b)")[0:127, 0:8]
        S.op("dve", lambda e: e.tensor_scalar(out=aC, in0=inv8[0:127], scalar1=posCf[0:127, 0:1], scalar2=None, op0=ALU.mult),
             r=["posCf", "inv8", "Nout", "Nf", "Nm", "Ni", "Nang"], w=["Cang"])
        sincos(aC, None, cosC[0:127], sinC[0:127], 127, fC, iC, mC, "C")
        dbg_out("cosM", cosM, ["Mout"]); dbg_out("sinM", sinM, ["Mout"])

        if upto <= -1:
            return finish(nc, S, st)
        cpk = M.lo_alloc([8], F32); sc = M.lo_alloc([8], F32)
        sch = M.lo_alloc([8], BF16); scl = M.lo_alloc([8], BF16)
        cBh = M.lo_alloc([8, 128], BF16); cBl = M.lo_alloc([8, 128], BF16)
        modB = M.lo_alloc([6 * D], F32)
        g1t = M.lo_alloc([D], F32); g2t = M.lo_alloc([D], F32)
        awb = [M.hi_alloc([8, 512], F32) for _ in range(3)]
        abb = [M.hi_alloc([512], F32) for _ in range(3)]
        awh = [M.hi_alloc([8, 512], BF16) for _ in range(3)]
        awl = [M.hi_alloc([8, 512], BF16) for _ in range(3)]
        S.dma("sp", cpk, c_pk, w=["cpk"])
        S.dma("sp", g1t, g1B, w=["g1t"]); S.dma("sp", g2t, g2B, w=["g2t"])
        act(sc, cpk, AF.Silu, r=["cpk"], w=["sc"])
        S.op("dve", lambda e: e.tensor_copy(out=sch, in_=sc), r=["sc"], w=["sch"])
        S.op("dve", lambda e: e.tensor_tensor(out=scl, in0=sc, in1=sch, op=ALU.subtract), r=["sc", "sch"], w=["scl"])
        for k in range(8):
            S.op("dve", lambda e, k=k: e.tensor_copy(out=cBh[:, k, :], in_=sch[:, k:k + 1].to_broadcast([128, 128])), r=["sch"], w=["cBh"])
            S.op("dve", lambda e, k=k: e.tensor_copy(out=cBl[:, k, :], in_=scl[:, k:k + 1].to_broadcast([128, 128])), r=["scl"], w=["cBl"])
        awv = ada_w.rearrange("(k p) n -> p k n", p=128)
        for n in range(12):
            q_ = n % 3
            S.dma("sp", awb[q_], awv[:, :, n * 512:(n + 1) * 512], w=[("awb", q_)])
            S.dma("sp", abb[q_], adabB[:, n * 512:(n + 1) * 512], w=[("abb", q_)])
            act(awh[q_], awb[q_], AF.Copy, r=[("awb", q_)], w=[("awh", q_)])
            S.op("dve", lambda e, q_=q_: e.tensor_tensor(out=awl[q_], in0=awb[q_], in1=awh[q_], op=ALU.subtract), r=[("awb", q_), ("awh", q_)], w=[("awl", q_)])
            b = nb()
            passes = [(cBh, "cBh", awh, "awh"), (cBh, "cBh", awl, "awl"), (cBl, "cBl", awh, "awh")]
            for pi_, (cb_, ck, ww, wk) in enumerate(passes):
                for k in range(8):
                    mm(P[b][:, :], cb_[:, k, :], ww[q_][:, k, :], pi_ == 0 and k == 0, pi_ == 2 and k == 7, r=[ck, (wk, q_)], w=[PK[b]])
            S.op("dve", lambda e, n=n, b=b, q_=q_: e.tensor_tensor(out=modB[:, n * 512:(n + 1) * 512], in0=P[b][:, :], in1=abb[q_], op=ALU.add),
                 r=[PK[b], ("abb", q_)], w=["modB"])
        S.op("dve", lambda e: e.scalar_tensor_tensor(out=modB[:, D:2 * D], in0=modB[:, D:2 * D], scalar=1.0, in1=g1t, op0=ALU.add, op1=ALU.mult), r=["modB", "g1t"], w=["modB"])
        S.op("dve", lambda e: e.scalar_tensor_tensor(out=modB[:, 4 * D:5 * D], in0=modB[:, 4 * D:5 * D], scalar=1.0, in1=g2t, op0=ALU.add, op1=ALU.mult), r=["modB", "g2t"], w=["modB"])
        S.dma("sp", mods, modB, r=["modB"], w=["mods"])
        dbg_out("modB", modB, ["modB"])
        B1 = modB[:, 0:D]; A1 = modB[:, D:2 * D]
        if upto <= 0:
            return finish(nc, S, st)

        M.hi = hi0
        hT = M.hi_alloc([8, S_], BF16)
        xb = [M.lo_alloc([D], F32) for _ in range(2)]
        junk = M.lo_alloc([D], F32); tmpA = M.lo_alloc([D], F32)
        hb = [M.lo_alloc([D], BF16) for _ in range(2)]
        ssq = M.lo_alloc([NT], F32); rs = M.lo_alloc([NT], F32)

        tmpA2 = [tmpA, M.lo_alloc([D], F32)]

        def norm_a(i, xt, xkey):
            act(junk, xt, AF.Square, r=[xkey], w=["junk", ("ssq", i)], accum_out=ssq[:, i:i + 1])
            act(rs[:, i:i + 1], ssq[:, i:i + 1], AF.Sqrt, r=[("ssq", i)], w=[("rs", i)], scale=1.0 / D, bias=EPS)
            S.op("dve", lambda e: e.reciprocal(out=rs[:, i:i + 1], in_=rs[:, i:i + 1]), r=[("rs", i)], w=[("rs", i)])

        def norm_b1(i, xt, xkey, A, B, Akeys):
            par = i % 2
            S.op("dve", lambda e: e.scalar_tensor_tensor(out=tmpA2[par], in0=xt, scalar=rs[:, i:i + 1], in1=A, op0=ALU.mult, op1=ALU.mult),
                 r=[xkey, ("rs", i)] + Akeys, w=[("tmpA2", par)])
            S.op("pool", lambda e: e.tensor_tensor(out=hb[par], in0=tmpA2[par], in1=B, op=ALU.add), r=[("tmpA2", par)] + Akeys, w=[("hb", par)])
            b = nb()
            pb = P[b][:, :].bitcast(BF16)
            for k in range(8):
                tp(pb[:, k * 128:(k + 1) * 128], hb[par][:, k * 128:(k + 1) * 128], ident, r=[("hb", par), "ident"], w=[PK[b]])
            return b

        def norm_b2(i, b, dstT, dkey):
            pb = P[b][:, :].bitcast(BF16)
            act(dstT[:, :, i * 128:(i + 1) * 128], pb.rearrange("p (k t) -> p k t", k=8), AF.Copy, r=[PK[b]], w=[(dkey, i)])

        xb = xb + [M.lo_alloc([D], F32), M.lo_alloc([D], F32)]

        def a_front(i):
            S.dma("sp", xb[i % 4], x[i * 128:(i + 1) * 128, :], w=[("xb", i % 4)])
            norm_a(i, xb[i % 4], ("xb", i % 4))

        a_front(0); a_front(1)
        for i in range(NT):
            b_ = norm_b1(i, xb[i % 4], ("xb", i % 4), A1, B1, ["modB"])
            if i + 2 < NT:
                a_front(i + 2)
            norm_b2(i, b_, hT, "hT")
        hTk = [("hT", i) for i in range(NT)]
        S.dma("sp", hTs, hT.rearrange("p k t -> p (k t)"), r=hTk, w=["hTs"])
        dbg_out("hT", hT.rearrange("p k t -> p (k t)"), hTk)
        if upto <= 1:
            return finish(nc, S, st)
        S.barrier()

        M.lo = lo0
        cqT = M.lo_alloc([6, S_], BF16); ckvT = M.lo_alloc([2, S_], BF16); kpe = M.lo_alloc([NT, 32], F32)
        lo1 = M.lo
        Wcq = M.lo_alloc([8, 768], BF16); Wckv = M.lo_alloc([8, 256], BF16); Wkpe = M.lo_alloc([8, 32], BF16)
        qagt = M.lo_alloc([6], F32); kvagt = M.lo_alloc([2], F32)
        sqb = [M.lo_alloc([512], BF16) for _ in range(2)]
        rb = M.lo_alloc([512], F32)
        S.dma("sp", qagt, qag, w=["qagt"]); S.dma("sp", kvagt, kvag, w=["kvagt"])
        load_w(Wcq, w_cq, "Wcq", 8, 768); load_w(Wckv, w_ckv, "Wckv", 8, 256); load_w(Wkpe, w_kpe, "Wkpe", 8, 32)

        def fm_proj_norm(dstT, dkey, Wt, wkey, nf, gaint, gkey, nfeat):
            for c in range(4):
                hk = [("hT", 4 * c + q) for q in range(4)]
                for j in range(nf):
                    b = nb()
                    for k in range(8):
                        mm(P[b][:, :], Wt[:, k, j * 128:(j + 1) * 128], hT[:, k, c * 512:(c + 1) * 512], k == 0, k == 7, r=[wkey] + hk, w=[PK[b]])
                    act(dstT[:, j, c * 512:(c + 1) * 512], P[b][:, :], AF.Copy, r=[PK[b]], w=[(dkey, c)])
                    act(sqb[j % 2], P[b][:, :], AF.Square, r=[PK[b]], w=[("sqb", j % 2)])
                    mm(P[6][:, :], onesb, sqb[j % 2], j == 0, j == nf - 1, r=["onesb", ("sqb", j % 2)], w=[PK[6]])
                act(rb, P[6][:, :], AF.Sqrt, r=[PK[6]], w=["rb"], scale=1.0 / nfeat, bias=EPS)
                S.op("dve", lambda e: e.reciprocal(out=rb, in_=rb), r=["rb"], w=["rb"])
                for j in range(nf):
                    S.op("dve", lambda e, j=j, c=c: e.scalar_tensor_tensor(out=dstT[:, j, c * 512:(c + 1) * 512], in0=dstT[:, j, c * 512:(c + 1) * 512],
                                                                             scalar=gaint[:, j:j + 1], in1=rb, op0=ALU.mult, op1=ALU.mult),
                         r=[(dkey, c), "rb", gkey], w=[(dkey, c)])

        fm_proj_norm(cqT, "cqT", Wcq, "Wcq", 6, qagt, "qagt", 768)
        fm_proj_norm(ckvT, "ckvT", Wckv, "Wckv", 2, kvagt, "kvagt", 256)
        for i in range(NT):
            b = nb()
            for k in range(8):
                mm(P[b][:, 0:32], hT[:, k, i * 128:(i + 1) * 128], Wkpe[:, k, :], k == 0, k == 7, r=["Wkpe", ("hT", i)], w=[PK[b]])
            S.op("dve", lambda e, i=i, b=b: e.tensor_copy(out=kpe[:, i, :], in_=P[b][:, 0:32]), r=[PK[b]], w=[("kpe", i)])
        cqk = [("cqT", c) for c in range(4)]
        dbg_out("cqT", cqT.rearrange("p k t -> p (k t)"), cqk)
        dbg_out("kpe", kpe.rearrange("p a b -> p (a b)"), [("kpe", i) for i in range(NT)])
        if upto <= 2:
            return finish(nc, S, st)
        S.barrier()

        M.lo = lo1
        M.hi = hi0
        QT = M.hi_alloc([8, S_], BF16); KT = M.hi_alloc([8, S_], BF16); V = M.hi_alloc([NT, 8, 65], BF16)
        hi1 = M.hi
        Wqb = M.lo_alloc([6, 768], BF16); Wkvb = M.lo_alloc([2, 1024], BF16)
        qgt = M.lo_alloc([96], F32); kgt = M.lo_alloc([96], F32)
        drq = [M.lo_alloc([768], BF16) for _ in range(2)]; drk = [M.lo_alloc([768], BF16) for _ in range(2)]
        S.dma("sp", qgt, qgB, w=["qgt"]); S.dma("sp", kgt, kgB, w=["kgt"])
        load_w(Wqb, w_qb, "Wqb", 6, 768); load_w(Wkvb, w_kvb, "Wkvb", 2, 1024)
        S.op("pool", lambda e: e.memset(V[:, :, :, 64:65], 1.0), w=["Vones"])

        def mk_tmps(Mx, n, H, hf):
            return dict(t1=Mx.lo_alloc([n], F32), hs=Mx.lo_alloc([H], F32), hr=Mx.lo_alloc([H], F32),
                        ra=Mx.lo_alloc([H * hf], F32), rb=Mx.lo_alloc([H * hf], F32), ra2=Mx.lo_alloc([H * hf], F32), rb2=Mx.lo_alloc([H * hf], F32))

        def hnr_stages(tag, T, src, skeys, H, Dh, gaint, gkey, ro, hf, cos_, sin_, dst, dkey, np_=128):
            n = H * Dh
            t1v = T["t1"][0:np_, 0:n].rearrange("p (h d) -> p h d", h=H)
            hs = T["hs"][0:np_, 0:H]; hr = T["hr"][0:np_, 0:H]
            x1 = t1v[:, :, ro:ro + hf]; x2 = t1v[:, :, ro + hf:ro + 2 * hf]
            cb = cos_.unsqueeze(1).to_broadcast([np_, H, hf]); sb_ = sin_.unsqueeze(1).to_broadcast([np_, H, hf])
            rv = {k: T[k][0:np_, 0:H * hf].rearrange("p (h d) -> p h d", h=H) for k in ("ra", "rb", "ra2", "rb2")}
            tr = ["Mout", "Nout", "Cout"]
            k_ = lambda nm: (tag, nm)
            st = []
            st.append(lambda: act(t1v, src, AF.Square, r=skeys, w=[k_("t1")]))
            st.append(lambda: S.op("dve", lambda e: e.tensor_reduce(out=hs, in_=t1v, axis=AX.X, op=ALU.add), r=[k_("t1")], w=[k_("hs")]))
            st.append(lambda: act(hr, hs, AF.Sqrt, r=[k_("hs")], w=[k_("hr")], scale=1.0 / Dh, bias=EPS))
            st.append(lambda: S.op("dve", lambda e: e.reciprocal(out=hr, in_=hr), r=[k_("hr")], w=[k_("hr")]))
            st.append(lambda: S.op("dve", lambda e: e.tensor_tensor(out=t1v, in0=src, in1=hr.unsqueeze(2).to_broadcast([np_, H, Dh]), op=ALU.mult), r=skeys + [k_("hr"), k_("hs")], w=[k_("t1")]))
            st.append(lambda: S.op("dve", lambda e: e.tensor_tensor(out=t1v, in0=t1v, in1=gaint[0:np_].unsqueeze(1).to_broadcast([np_, H, Dh]), op=ALU.mult), r=[k_("t1"), gkey], w=[k_("t1")]))
            st.append(lambda: S.op("dve", lambda e: e.tensor_tensor(out=rv["ra"], in0=x1, in1=cb, op=ALU.mult), r=[k_("t1")] + tr, w=[k_("ra")]))
            st.append(lambda: S.op("dve", lambda e: e.tensor_tensor(out=rv["rb"], in0=x2, in1=sb_, op=ALU.mult), r=[k_("t1")] + tr, w=[k_("rb")]))
            st.append(lambda: S.op("dve", lambda e: e.tensor_tensor(out=dst[:, :, ro:ro + hf], in0=rv["ra"], in1=rv["rb"], op=ALU.subtract), r=[k_("ra"), k_("rb")], w=[dkey]))
            st.append(lambda: S.op("dve", lambda e: e.tensor_tensor(out=rv["ra2"], in0=x2, in1=cb, op=ALU.mult), r=[k_("t1")] + tr, w=[k_("ra2")]))
            st.append(lambda: S.op("dve", lambda e: e.tensor_tensor(out=rv["rb2"], in0=x1, in1=sb_, op=ALU.mult), r=[k_("t1")] + tr, w=[k_("rb2")]))
            st.append(lambda: S.op("dve", lambda e: e.tensor_tensor(out=dst[:, :, ro + hf:ro + 2 * hf], in0=rv["ra2"], in1=rv["rb2"], op=ALU.add), r=[k_("ra2"), k_("rb2")], w=[dkey]))

            def copies():
                if ro > 0:
                    S.op("pool", lambda e: e.tensor_copy(out=dst[:, :, 0:ro], in_=t1v[:, :, 0:ro]), r=[k_("t1")], w=[dkey])
                if ro + 2 * hf < Dh:
                    S.op("pool", lambda e: e.tensor_copy(out=dst[:, :, ro + 2 * hf:Dh], in_=t1v[:, :, ro + 2 * hf:Dh]), r=[k_("t1")], w=[dkey])
            st.insert(6, copies)
            return st

        def run_interleaved(chains):
            for s_ in range(max(len(c_) for c_ in chains)):
                for c_ in chains:
                    if s_ < len(c_):
                        c_[s_]()

        def head_norm_rope(src, skeys, H, Dh, gaint, gkey, ro, hf, cos_, sin_, dst, dkey, np_=128):
            n = H * Dh
            t1v = t1[0:np_, 0:n].rearrange("p (h d) -> p h d", h=H)
            t2v = t2[0:np_, 0:n].rearrange("p (h d) -> p h d", h=H)
            hs = hss[0:np_, 0:H]; hr = hrs[0:np_, 0:H]
            act(t1v, src, AF.Square, r=skeys, w=["t1"])
            S.op("dve", lambda e: e.tensor_reduce(out=hs, in_=t1v, axis=AX.X, op=ALU.add), r=["t1"], w=["hss"])
            act(hr, hs, AF.Sqrt, r=["hss"], w=["hrs"], scale=1.0 / Dh, bias=EPS)
            S.op("dve", lambda e: e.reciprocal(out=hr, in_=hr), r=["hrs"], w=["hrs"])
            S.op("dve", lambda e: e.tensor_tensor(out=t2v, in0=src, in1=hr.unsqueeze(2).to_broadcast([np_, H, Dh]), op=ALU.mult), r=skeys + ["hrs"], w=["t2"])
            S.op("dve", lambda e: e.tensor_tensor(out=t1v, in0=t2v, in1=gaint[0:np_].unsqueeze(1).to_broadcast([np_, H, Dh]), op=ALU.mult), r=["t2", gkey], w=["t1"])
            x1 = t1v[:, :, ro:ro + hf]; x2 = t1v[:, :, ro + hf:ro + 2 * hf]
            cb = cos_.unsqueeze(1).to_broadcast([np_, H, hf]); sb_ = sin_.unsqueeze(1).to_broadcast([np_, H, hf])
            rav = ra[0:np_, 0:H * hf].rearrange("p (h d) -> p h d", h=H)
            rbv = rbb[0:np_, 0:H * hf].rearrange("p (h d) -> p h d", h=H)
            tr = ["Mout", "Nout", "Cout"]
            S.op("dve", lambda e: e.tensor_tensor(out=rav, in0=x1, in1=cb, op=ALU.mult), r=["t1"] + tr, w=["ra"])
            S.op("dve", lambda e: e.tensor_tensor(out=rbv, in0=x2, in1=sb_, op=ALU.mult), r=["t1"] + tr, w=["rbb"])
            S.op("dve", lambda e: e.tensor_tensor(out=dst[:, :, ro:ro + hf], in0=rav, in1=rbv, op=ALU.subtract), r=["ra", "rbb"], w=[dkey])
            S.op("dve", lambda e: e.tensor_tensor(out=rav, in0=x2, in1=cb, op=ALU.mult), r=["t1"] + tr, w=["ra"])
            S.op("dve", lambda e: e.tensor_tensor(out=rbv, in0=x1, in1=sb_, op=ALU.mult), r=["t1"] + tr, w=["rbb"])
            S.op("dve", lambda e: e.tensor_tensor(out=dst[:, :, ro + hf:ro + 2 * hf], in0=rav, in1=rbv, op=ALU.add), r=["ra", "rbb"], w=[dkey])
            if ro > 0:
                S.op("pool", lambda e: e.tensor_copy(out=dst[:, :, 0:ro], in_=t1v[:, :, 0:ro]), r=["t1"], w=[dkey])
            if ro + 2 * hf < Dh:
                S.op("pool", lambda e: e.tensor_copy(out=dst[:, :, ro + 2 * hf:Dh], in_=t1v[:, :, ro + 2 * hf:Dh]), r=["t1"], w=[dkey])

        Msub = Mem(big, NB); Msub.lo = lo_pers; Msub.hi = lo0
        rawq = [Msub.lo_alloc([768], F32) for _ in range(2)]; rawk = [Msub.lo_alloc([768], F32), M.lo_alloc([768], F32)]
        Tq = [mk_tmps(Msub, 768, 8, 16) for _ in range(2)]; Tk = [mk_tmps(Msub, 768, 8, 16) for _ in range(2)]

        def b2_front(i):
            ts = slice(i * 128, (i + 1) * 128)
            par = i % 2
            bA, bB = nb(), nb()
            for k in range(6):
                mm(P[bA][:, :], cqT[:, k, ts], Wqb[:, k, 0:512], k == 0, k == 5, r=["Wqb", ("cqT", i // 4)], w=[PK[bA]])
            for k in range(6):
                mm(P[bB][:, 0:256], cqT[:, k, ts], Wqb[:, k, 512:768], k == 0, k == 5, r=["Wqb", ("cqT", i // 4)], w=[PK[bB]])
            act(rawq[par][:, 0:512], P[bA][:, :], AF.Copy, r=[PK[bA]], w=[("rawq", par)])
            act(rawq[par][:, 512:768], P[bB][:, 0:256], AF.Copy, r=[PK[bB]], w=[("rawq", par)])
            bA, bB = nb(), nb()
            for hh, bb in ((0, bA), (1, bB)):
                for k in range(2):
                    mm(P[bb][:, :], ckvT[:, k, ts], Wkvb[:, k, hh * 512:(hh + 1) * 512], k == 0, k == 1, r=["Wkvb", ("ckvT", i // 4)], w=[PK[bb]])
            rv = rawk[par].rearrange("p (h d) -> p h d", h=8)
            for hh, bb in ((0, bA), (1, bB)):
                pv = P[bb][:, :].rearrange("p (h d) -> p h d", h=4)
                act(rv[:, hh * 4:(hh + 1) * 4, 0:64], pv[:, :, 0:64], AF.Copy, r=[PK[bb]], w=[("rawk", par)])
                act(V[:, i, hh * 4:(hh + 1) * 4, 0:64], pv[:, :, 64:128], AF.Copy, r=[PK[bb]], w=[("V", i)])
            S.op("pool", lambda e, i=i: e.tensor_copy(out=rv[:, :, 64:96], in_=kpe[:, i, :].unsqueeze(1).to_broadcast([128, 8, 32])), r=[("kpe", i)], w=[("rawk", par)])

        def b2_chains(i):
            par = i % 2
            dq = drq[par].rearrange("p (h d) -> p h d", h=8); dk = drk[par].rearrange("p (h d) -> p h d", h=8)
            cq_ = hnr_stages(("cq", par), Tq[par], rawq[par].rearrange("p (h d) -> p h d", h=8), [("rawq", par)], 8, 96, qgt, "qgt", 64, 16, cosM[:, i, :], sinM[:, i, :], dq, ("drq", par))
            ck_ = hnr_stages(("ck", par), Tk[par], rawk[par].rearrange("p (h d) -> p h d", h=8), [("rawk", par)], 8, 96, kgt, "kgt", 64, 16, cosM[:, i, :], sinM[:, i, :], dk, ("drk", par))
            return [cq_[:6], ck_[:6]], [cq_[6:], ck_[6:]]

        def b2_out(i):
            ts = slice(i * 128, (i + 1) * 128)
            par = i % 2
            dq = drq[par].rearrange("p (h d) -> p h d", h=8); dk = drk[par].rearrange("p (h d) -> p h d", h=8)
            for (dd, dkey_, dstT, okey) in ((dq, ("drq", par), QT, "QT"), (dk, ("drk", par), KT, "KT")):
                b = nb(); pb = P[b][:, :].bitcast(BF16)
                for h in range(8):
                    tp(pb[0:96, h * 128:(h + 1) * 128], dd[:, h, :], ident, r=[dkey_, "ident"], w=[PK[b]])
                act(dstT[0:96, :, ts], pb[0:96, :].rearrange("p (h t) -> p h t", h=8), AF.Copy, r=[PK[b]], w=[(okey, i)])

        b2_front(0); b2_front(1)
        h1_, h2_ = b2_chains(0)
        run_interleaved(h1_)
        for i in range(NT):
            if i + 2 < NT:
                b2_front(i + 2)
            nxt = b2_chains(i + 1) if i + 1 < NT else ([], [])
            run_interleaved(nxt[0] + h2_)
            h2_ = nxt[1]
            if i >= 1:
                b2_out(i - 1)
        b2_out(NT - 1)
        QTk = [("QT", i) for i in range(NT)]
        dbg_out("QT", QT[0:96].rearrange("p k t -> p (k t)"), QTk)
        dbg_out("KT", KT[0:96].rearrange("p k t -> p (k t)"), [("KT", i) for i in range(NT)])
        dbg_out("V", V.rearrange("p a b c -> p (a b c)"), [("V", i) for i in range(NT)] + ["Vones"])
        if upto <= 3:
            return finish(nc, S, st)
        S.barrier()

        nbanks[0] = 6
        M.lo = lo0
        om = M.lo_alloc([NT, 512], BF16)
        PT = [M.lo_alloc([512], BF16) for _ in range(3)]
        rec4 = [M.lo_alloc([4], F32) for _ in range(2)]
        pt_i = [0]

        gchunk = [0]

        def causal_attn_multi(jobs):
            steps = []
            for ji in range(len(jobs)):
                for c in range(4):
                    for kt in range(4 * c + 4):
                        steps.append((ji, c, kt))

            def emit_qk(step):
                ji, c, kt = step
                J = jobs[ji]
                q0 = max(kt - 4 * c, 0)
                n = 512 - 128 * q0
                b = nb()
                has_extra = J["extra"] is not None
                mm(P[b][:, 0:n], J["KT"][0:J["kn"], kt * 128:(kt + 1) * 128], J["QT"](c * 512 + q0 * 128, (c + 1) * 512),
                   True, not has_extra, r=J["kk"](kt) + J["qk"](c), w=[PK[b]])
                if has_extra:
                    J["extra"](P[b][:, 0:n], kt, c * 512 + q0 * 128, (c + 1) * 512, PK[b])
                return b, n, q0

            pend = emit_qk(steps[0])
            for si, (ji, c, kt) in enumerate(steps):
                J = jobs[ji]
                b, n, q0 = pend
                if si + 1 < len(steps):
                    pend = emit_qk(steps[si + 1])
                if kt == 0:
                    gchunk[0] += 1
                ab = 6 + (gchunk[0] % 2)
                Oacc = P[ab][:, 0:260].rearrange("p (q d) -> p q d", q=4)
                pi = pt_i[0]; pt_i[0] = (pi + 1) % 3
                pt = PT[pi]
                act(pt[:, 0:n], P[b][:, 0:n], AF.Exp, r=[PK[b]], w=[("PT", pi)], scale=J["scale"])
                if kt >= 4 * c:
                    S.op("dve", lambda e, pt=pt: e.tensor_tensor(out=pt[:, 0:128], in0=pt[:, 0:128], in1=tri, op=ALU.mult), r=[("PT", pi), "tri"], w=[("PT", pi)])
                for qi in range(q0, 4):
                    mm(Oacc[:, qi, :], pt[:, (qi - q0) * 128:(qi - q0 + 1) * 128], J["V"](kt), kt == 0 and qi == 0, kt == 4 * c + qi,
                       r=[("PT", pi)] + J["vk"](kt), w=[PK[ab]], skip_group_check=True)
                if kt == 4 * c + 3:
                    J["fin"](c, Oacc, PK[ab])

        jobs = []
        for h in range(8):
            def fin(c, Oacc, pk, h=h):
                rc = rec4[c % 2]
                S.op("dve", lambda e: e.reciprocal(out=rc, in_=Oacc[:, :, 64]), r=[pk], w=[("rec4", c % 2)])
                S.op("dve", lambda e: e.tensor_tensor(out=om[:, 4 * c:4 * c + 4, h * 64:(h + 1) * 64], in0=Oacc[:, :, 0:64],
                                                      in1=rc.unsqueeze(2).to_broadcast([128, 4, 64]), op=ALU.mult),
                     r=[pk, ("rec4", c % 2)], w=[("om", c)])
            jobs.append(dict(KT=KT[:, h, :], kn=96, QT=(lambda a, b_, h=h: QT[0:96, h, a:b_]), V=(lambda kt, h=h: V[:, kt, h, :]), scale=96 ** -0.5,
                             extra=None, fin=fin, qk=(lambda c: [("QT", 4 * c + q) for q in range(4)]), kk=(lambda kt: [("KT", kt)]),
                             vk=(lambda kt: [("V", kt), "Vones"])))
        causal_attn_multi(jobs)
        for i in range(NT):
            b = nb(); pb = P[b][:, :].bitcast(BF16)
            for j in range(4):
                tp(pb[:, j * 128:(j + 1) * 128], om[:, i, j * 128:(j + 1) * 128], ident, r=[("om", i // 4), "ident"], w=[PK[b]])
            act(omT[:, :, i * 128:(i + 1) * 128], pb[:, 0:512].rearrange("p (k t) -> p k t", k=4), AF.Copy, r=[PK[b]], w=[("omT", i)])
        dbg_out("om", om.rearrange("p a b -> p (a b)"), [("om", c) for c in range(4)])
        if upto <= 4:
            return finish(nc, S, st)
        S.barrier()

        nbanks[0] = 4
        M.lo = lo0; M.hi = hi0
        qnT = M.lo_alloc([8, S_], BF16); ksT = M.lo_alloc([2, S_], BF16); kwT = M.lo_alloc([2, S_], BF16)
        vs = M.lo_alloc([NT, 2, 65], BF16); vw = M.lo_alloc([NT, 2, 65], BF16)
        gates = M.lo_alloc([NT, 3, 8], F32)
        kcmpT = M.lo_alloc([2, 128], BF16); VCX = M.lo_alloc([2, 97], BF16)
        PT = [M.lo_alloc([512], BF16) for _ in range(3)]
        rec4 = [M.lo_alloc([4], F32) for _ in range(2)]
        t1 = M.lo_alloc([512], F32); t2 = M.lo_alloc([512], F32)
        hss = M.lo_alloc([8], F32); hrs = M.lo_alloc([8], F32)
        ra = M.lo_alloc([128], F32); rbb = M.lo_alloc([128], F32)
        drb = [M.lo_alloc([512], BF16) for _ in range(2)]
        nqt = M.lo_alloc([64], F32); nkct = M.lo_alloc([64], F32); nkst = M.lo_alloc([64], F32); nkwt = M.lo_alloc([64], F32)
        lo2 = M.lo
        kc2 = M.hi_alloc([2, S_], BF16); vc2 = M.hi_alloc([2, S_], BF16)
        hi_kv = M.hi
        hT = M.hi_alloc([8, S_], BF16)
        Wqn = M.hi_alloc([8, 512], BF16); Wkc2 = M.hi_alloc([8, 256], BF16); Wvc2 = M.hi_alloc([8, 256], BF16)
        Wkv4 = M.hi_alloc([8, 512], BF16); Wgn = M.hi_alloc([8, 24], BF16)
        ge = M.hi_alloc([24], F32)
        S.dma("sp", hT.rearrange("p k t -> p (k t)"), hTs, r=["hTs"], w=["hTall"])
        for t_, d_ in ((nqt, nqB), (nkct, nkcB), (nkst, nksB), (nkwt, nkwB)):
            S.dma("sp", t_, d_, w=["ngain"])
        load_w(Wqn, w_qn, "Wqn", 8, 512); load_w(Wkv4, w_kv4, "Wkv4", 8, 512); load_w(Wgn, w_gn, "Wgn", 8, 24)
        load_w(Wkc2, w_kc2, "Wkc2", 8, 256); load_w(Wvc2, w_vc2, "Wvc2", 8, 256)
        S.op("pool", lambda e: e.memset(vs[:, :, :, 64:65], 1.0), w=["vsones"])
        S.op("pool", lambda e: e.memset(vw[:, :, :, 64:65], 1.0), w=["vwones"])
        S.op("pool", lambda e: e.memset(kc2[64:128, :, S_ - 1:S_], 0.0), w=["kc2pad"])
        S.op("pool", lambda e: e.memset(vc2[64:128, :, S_ - 1:S_], 0.0), w=["vc2pad"])
        MsubD = Mem(big, NB); MsubD.lo = lo_pers + 16384; MsubD.hi = lo0
        TDq = [mk_tmps(MsubD, 512, 8, 8) for _ in range(2)]; TDs = [mk_tmps(MsubD, 128, 2, 8) for _ in range(2)]; TDw = [mk_tmps(MsubD, 128, 2, 8) for _ in range(2)]
        dnq = [MsubD.lo_alloc([512], BF16) for _ in range(2)]
        dns = [MsubD.lo_alloc([128], BF16) for _ in range(2)]; dnw = [MsubD.lo_alloc([128], BF16) for _ in range(2)]

        def d_front(i):
            ts = slice(i * 128, (i + 1) * 128)
            bq = 4 + 2 * (i % 2)
            for k in range(8):
                mm(P[bq][:, :], hT[:, k, ts], Wqn[:, k, :], k == 0, k == 7, r=["hTall", "Wqn"], w=[PK[bq]])
            bk = 5 + 2 * (i % 2)
            for k in range(8):
                mm(P[bk][:, :], hT[:, k, ts], Wkv4[:, k, :], k == 0, k == 7, r=["hTall", "Wkv4"], w=[PK[bk]])
            bg = nb()
            for k in range(8):
                mm(P[bg][:, 0:24], hT[:, k, ts], Wgn[:, k, :], k == 0, k == 7, r=["hTall", "Wgn"], w=[PK[bg]])
            act(vs[:, i, :, 0:64], P[bk][:, 256:384].rearrange("p (g d) -> p g d", g=2), AF.Copy, r=[PK[bk]], w=[("vs", i)])
            act(vw[:, i, :, 0:64], P[bk][:, 384:512].rearrange("p (g d) -> p g d", g=2), AF.Copy, r=[PK[bk]], w=[("vw", i)])
            act(ge, P[bg][:, 0:24], AF.Exp, r=[PK[bg]], w=["ge"], scale=-1.0)
            S.op("dve", lambda e: e.tensor_scalar(out=ge, in0=ge, scalar1=1.0, scalar2=None, op0=ALU.add), r=["ge"], w=["ge"])
            S.op("dve", lambda e, i=i: e.reciprocal(out=gates[:, i].rearrange("p a b -> p (a b)"), in_=ge), r=["ge"], w=[("gates", i)])
            return bq, bk

        def d_chains(i, bq, bk):
            par = i % 2
            dq = dnq[par].rearrange("p (h d) -> p h d", h=8)
            ds_ = dns[par].rearrange("p (h d) -> p h d", h=2); dw_ = dnw[par].rearrange("p (h d) -> p h d", h=2)
            c1 = hnr_stages(("dq", par), TDq[par], P[bq][:, :].rearrange("p (h d) -> p h d", h=8), [PK[bq]], 8, 64, nqt, "ngain", 0, 8, cosN[:, i, :], sinN[:, i, :], dq, ("dnq", par))
            c2 = hnr_stages(("ds", par), TDs[par], P[bk][:, 0:128].rearrange("p (h d) -> p h d", h=2), [PK[bk]], 2, 64, nkst, "ngain", 0, 8, cosN[:, i, :], sinN[:, i, :], ds_, ("dns", par))
            c3 = hnr_stages(("dw", par), TDw[par], P[bk][:, 128:256].rearrange("p (h d) -> p h d", h=2), [PK[bk]], 2, 64, nkwt, "ngain", 0, 8, cosN[:, i, :], sinN[:, i, :], dw_, ("dnw", par))
            return [c1[:6], c2[:6], c3[:6]], [c1[6:], c2[6:], c3[6:]]

        def d_out(i):
            ts = slice(i * 128, (i + 1) * 128)
            par = i % 2
            b = nb(); pb = P[b][:, :].bitcast(BF16)
            for p_ in range(8):
                tp(pb[0:64, p_ * 128:(p_ + 1) * 128], dnq[par][:, p_ * 64:(p_ + 1) * 64], ident, r=[("dnq", par), "ident"], w=[PK[b]])
            act(qnT[0:64, :, ts], pb[0:64, :].rearrange("p (k t) -> p k t", k=8), AF.Copy, r=[PK[b]], w=[("qnT", i)])
            for (dd, dkey_, dstT, dk) in ((dns[par], ("dns", par), ksT, "ksT"), (dnw[par], ("dnw", par), kwT, "kwT")):
                b2 = nb(); pb = P[b2][:, :].bitcast(BF16)
                for g_ in range(2):
                    tp(pb[0:64, g_ * 128:(g_ + 1) * 128], dd[:, g_ * 64:(g_ + 1) * 64], ident, r=[dkey_, "ident"], w=[PK[b2]])
                act(dstT[0:64, :, ts], pb[0:64, 0:256].rearrange("p (g t) -> p g t", g=2), AF.Copy, r=[PK[b2]], w=[(dk, i)])

        fr_ = {0: d_front(0), 1: d_front(1)}
        h1_, h2_ = d_chains(0, *fr_[0])
        run_interleaved(h1_)
        for i in range(NT):
            if i + 2 < NT:
                fr_[i + 2] = d_front(i + 2)
            nxt = d_chains(i + 1, *fr_[i + 1]) if i + 1 < NT else ([], [])
            run_interleaved(nxt[0] + h2_)
            h2_ = nxt[1]
            if i >= 1:
                d_out(i - 1)
        d_out(NT - 1)
        for c in range(4):
            for (Wt, wk, dst, dk) in ((Wkc2, "Wkc2", kc2, "kc2"), (Wvc2, "Wvc2", vc2, "vc2")):
                for g in range(2):
                    b = nb()
                    for k in range(8):
                        mm(P[b][:, :], Wt[:, k, g * 128:(g + 1) * 128], hT[:, k, c * 512:(c + 1) * 512], k == 0, k == 7, r=["hTall", wk], w=[PK[b]])
                    act(dst[0:64, g, c * 512:(c + 1) * 512], P[b][0:64, :], AF.Copy, r=[PK[b]], w=[dk])
                    if c == 0:
                        act(dst[64:128, g, 0:511], P[b][64:128, 1:512], AF.Copy, r=[PK[b]], w=[dk])
                    else:
                        act(dst[64:128, g, c * 512 - 1:(c + 1) * 512 - 1], P[b][64:128, :], AF.Copy, r=[PK[b]], w=[dk])
        dbg_out("qnT", qnT[0:64].rearrange("p k t -> p (k t)"), [("qnT", i) for i in range(NT)])
        dbg_out("ksT", ksT[0:64].rearrange("p k t -> p (k t)"), [("ksT", i) for i in range(NT)])
        dbg_out("gates", gates.rearrange("p a b c -> p (a b c)"), [("gates", i) for i in range(NT)])
        dbg_out("kc2", kc2.rearrange("p k t -> p (k t)"), ["kc2", "kc2pad"])
        if upto <= 5:
            return finish(nc, S, st)
        S.barrier()

        nbanks[0] = 8
        M.hi = hi_kv
        hiE = M.hi
        W1k = M.lo_alloc([16, 256], BF16); W1v = M.lo_alloc([16, 256], BF16)
        W2k = M.lo_alloc([2, 64], BF16); W2v = M.lo_alloc([2, 64], BF16)
        pkf = M.lo_alloc([16], F32); pvf = M.lo_alloc([16], F32); pkb = M.lo_alloc([16], BF16); pvb = M.lo_alloc([16], BF16)
        biask = M.lo_alloc([2], F32); biasv = M.lo_alloc([2], F32)
        hid = [M.lo_alloc([128], BF16) for _ in range(2)]
        ovl = M.lo_alloc([32], BF16)
        load_w(W1k, w1k, "W1k", 16, 256); load_w(W1v, w1v, "W1v", 16, 256)
        load_w(W2k, w2k, "W2k", 2, 64); load_w(W2v, w2v, "W2v", 2, 64)
        S.dma("sp", pkf, posk, w=["pkf"]); S.dma("sp", pvf, posv, w=["pvf"]); S.dma("sp", ovl[0:127], ovl_d, w=["ovl"])
        S.op("pool", lambda e: e.memset(VCX[0:127, :, 64:65], 1.0), w=["VCXa"])
        for g in range(2):
            S.op("pool", lambda e, g=g: e.tensor_copy(out=VCX[0:127, g, 65:97], in_=ovl[0:127]), r=["ovl"], w=["VCXb"])
        rt = [M.lo_alloc([128], BF16) for _ in range(3)]
        rt_i = [0]
        for (W1, w1key, W2, w2key, src, skey, posf_, pkey, isk) in ((W1k, "W1k", W2k, "W2k", kc2, ["kc2", "kc2pad"], pkf, "pkf", True),
                                                                    (W1v, "W1v", W2v, "W2v", vc2, ["vc2", "vc2pad"], pvf, "pvf", False)):
            srcv = src.rearrange("p g (n s) -> p g n s", s=16)
            bo = nb()
            for g in range(2):
                bh = []
                for hc in range(2):
                    b = nb()
                    while b == bo or b in bh:
                        b = nb()
                    bh.append(b)
                for lc in range(16):
                    ri = rt_i[0]; rt_i[0] = (ri + 1) % 3
                    rtv = rt[ri][:, 0:127]
                    S.op("dve", lambda e, rtv=rtv, g=g, lc=lc, srcv=srcv, posf_=posf_: e.tensor_scalar(
                        out=rtv, in0=srcv[:, g, (2 * lc) // 16:(2 * lc) // 16 + 127, (2 * lc) % 16], scalar1=posf_[:, lc:lc + 1], scalar2=None, op0=ALU.add),
                        r=skey + [pkey], w=[("rt", ri)])
                    for hc in range(2):
                        mm(P[bh[hc]][:, 0:127], W1[:, lc, hc * 128:(hc + 1) * 128], rtv, lc == 0, lc == 15, r=[w1key, ("rt", ri)], w=[PK[bh[hc]]], sig=(hc == 1 or lc == 15))
                for hc in range(2):
                    act(hid[hc][:, 0:127], P[bh[hc]][:, 0:127], AF.Silu, r=[PK[bh[hc]]], w=[("hid", hc)])
                for hc in range(2):
                    mm(P[bo][0:127, g * 64:(g + 1) * 64], hid[hc][:, 0:127], W2[:, hc, :], hc == 0, hc == 1, r=[("hid", hc), w2key], w=[PK[bo]])
            if isk:
                d = drb[1][0:127, 0:128].rearrange("p (h d) -> p h d", h=2)
                head_norm_rope(P[bo][0:127, 0:128].rearrange("p (h d) -> p h d", h=2), [PK[bo]], 2, 64, nkct, "ngain", 0, 8, cosC[0:127], sinC[0:127], d, "drb1", np_=127)
                b2 = nb(); pb = P[b2][:, :].bitcast(BF16)
                for g_ in range(2):
                    tp(pb[0:64, g_ * 128:g_ * 128 + 127], drb[1][0:127, g_ * 64:(g_ + 1) * 64], ident[0:127, 0:127], r=["drb1", "ident"], w=[PK[b2]])
                act(kcmpT[0:64, :, 0:127], pb[0:64, 0:256].rearrange("p (g t) -> p g t", g=2)[:, :, 0:127], AF.Copy, r=[PK[b2]], w=["kcmpT"])
            else:
                act(VCX[0:127, :, 0:64], P[bo][0:127, 0:128].rearrange("p (g d) -> p g d", g=2), AF.Copy, r=[PK[bo]], w=["VCXc"])
        dbg_out("kcmpT", kcmpT[0:64].rearrange("p g t -> p (g t)"), ["kcmpT"])
        dbg_out("VCX", VCX[0:127].rearrange("p a b -> p (a b)"), ["VCXa", "VCXb", "VCXc"])
        if upto <= 6:
            return finish(nc, S, st)
        S.barrier()

        M.lo = lo2; M.hi = hi0
        onsa = M.hi_alloc([NT, 512], F32)
        mbT = M.hi_alloc([2, S_], BF16)
        vmT = M.hi_alloc([S_], BF16); XE = M.hi_alloc([NT, 128], BF16)
        fb = M.hi_alloc([NT, 32], F32); vj = M.hi_alloc([NT, 32], F32)
        pc = [M.lo_alloc([4, 128], BF16) for _ in range(2)]
        rsum = M.lo_alloc([8], F32); rec8 = M.lo_alloc([8], F32); gr = M.lo_alloc([8], F32)
        tmp_i = M.lo_alloc([8, 32], F32); imp = M.lo_alloc([2, 32], F32); m8 = M.lo_alloc([2, 8], F32)
        sel = M.lo_alloc([2, 32], F32); mbf = M.lo_alloc([2, 32], BF16)
        tmpo = M.lo_alloc([8, 64], F32)
        pw = [M.lo_alloc([3, 128], BF16) for _ in range(3)]
        onb = [M.lo_alloc([512], BF16) for _ in range(2)]
        S.dma("sp", vmT[0:127], vmT_d, w=["vmT"]); S.dma("sp", XE[0:32].rearrange("p a b -> p (a b)"), XE_d, w=["XE"])
        S.dma("sp", fb.rearrange("p a b -> p (a b)"), fb_d, w=["fb"]); S.dma("sp", vj.rearrange("p a b -> p (a b)"), vj_d, w=["vj"])
        VCXk = ["VCXa", "VCXb", "VCXc"]
        for i in range(NT):
            ts = slice(i * 128, (i + 1) * 128)
            import os
            if int(os.environ.get('KDEV_F', '9')) <= 0:
                continue
            sb_ = [nb(), nb()]
            ob = [nb(), nb()]
            for p in range(8):
                j, g = p // 2, p % 2
                if os.environ.get('KDEV_G0'):
                    g = 0
                mm(P[sb_[p // 4]][0:127, (p % 4) * 128:(p % 4 + 1) * 128], kcmpT[0:64, g, 0:127], qnT[0:64, p, ts], True, True,
                   r=["kcmpT", ("qnT", i)], w=[PK[sb_[p // 4]]])
            for hf_ in range(2):
                pcv = pc[hf_]
                act(pcv[0:127], P[sb_[hf_]][0:127, :].rearrange("p (a b) -> p a b", a=4), AF.Exp, r=[PK[sb_[hf_]]], w=[("pc", hf_)], scale=0.125)
                S.op("dve", lambda e, pcv=pcv: e.tensor_tensor(out=pcv[0:127], in0=pcv[0:127], in1=vmT[0:127, ts].unsqueeze(1).to_broadcast([127, 4, 128]), op=ALU.mult),
                     r=[("pc", hf_), "vmT"], w=[("pc", hf_)])
            import os
            FL = int(os.environ.get('KDEV_F', '9'))
            if FL <= 1:
                continue
            for p in range(8):
                g = p % 2
                mm(P[ob[p // 4]][:, (p % 4) * 97:(p % 4 + 1) * 97], pc[p // 4][0:127, p % 4, :], VCX[0:127, g, :], True, True,
                   r=[("pc", p // 4)] + VCXk, w=[PK[ob[p // 4]]])
            if FL <= 2:
                continue
            OC = [P[ob[h_]][:, 0:388].rearrange("p (a b) -> p a b", a=4) for h_ in range(2)]
            for h_ in range(2):
                S.op("dve", lambda e, h_=h_: e.tensor_scalar(out=rsum[:, h_ * 4:(h_ + 1) * 4], in0=OC[h_][:, :, 64], scalar1=1e-30, scalar2=None, op0=ALU.max), r=[PK[ob[h_]]], w=["rsum"])
            S.op("dve", lambda e: e.reciprocal(out=rec8, in_=rsum), r=["rsum"], w=["rec8"])
            S.op("dve", lambda e, i=i: e.tensor_tensor(out=gr, in0=gates[:, i, 0, :], in1=rec8, op=ALU.mult), r=["rec8", ("gates", i)], w=["gr"])
            for h_ in range(2):
                S.op("dve", lambda e, h_=h_, i=i: e.tensor_tensor(out=onsa[:, i, h_ * 256:(h_ + 1) * 256].rearrange("p (a b) -> p a b", a=4), in0=OC[h_][:, :, 0:64],
                                                                   in1=gr[:, h_ * 4:(h_ + 1) * 4].unsqueeze(2).to_broadcast([128, 4, 64]), op=ALU.mult),
                     r=[PK[ob[h_]], "gr"], w=[("onsa", i)])
                S.op("dve", lambda e, h_=h_: e.tensor_tensor(out=tmp_i[:, h_ * 4:(h_ + 1) * 4, :], in0=OC[h_][:, :, 65:97],
                                                             in1=rec8[:, h_ * 4:(h_ + 1) * 4].unsqueeze(2).to_broadcast([128, 4, 32]), op=ALU.mult),
                     r=[PK[ob[h_]], "rec8"], w=["tmp_i"])
            if FL <= 3:
                continue
            S.op("dve", lambda e: e.tensor_reduce(out=imp, in_=tmp_i.rearrange("t (j g) n -> t g n j", g=2), axis=AX.X, op=ALU.add), r=["tmp_i"], w=["imp"])
            S.op("dve", lambda e, i=i: e.tensor_tensor(out=imp, in0=imp, in1=fb[:, i, :].unsqueeze(1).to_broadcast([128, 2, 32]), op=ALU.add), r=["imp", "fb"], w=["imp"])
            if FL <= 4:
                continue
            for g in range(2):
                S.op("dve", lambda e, g=g: e.max(out=m8[:, g, :], in_=imp[:, g, :]), r=["imp"], w=["m8"])
                S.op("dve", lambda e, g=g: e.tensor_scalar(out=sel[:, g, :], in0=imp[:, g, :], scalar1=m8[:, g, 7:8], scalar2=None, op0=ALU.is_ge), r=["imp", "m8"], w=["sel"])
            S.op("dve", lambda e, i=i: e.tensor_tensor(out=sel, in0=sel, in1=vj[:, i, :].unsqueeze(1).to_broadcast([128, 2, 32]), op=ALU.mult), r=["sel", "vj"], w=["sel"])
            S.op("dve", lambda e: e.tensor_scalar(out=mbf, in0=sel, scalar1=-1.0, scalar2=30000.0, op0=ALU.add, op1=ALU.mult), r=["sel"], w=["mbf"])
            if i == 5:
                dbg_out("sel5", sel.rearrange("p a b -> p (a b)"), ["sel"])
                dbg_out("imp5", imp.rearrange("p a b -> p (a b)"), ["imp"])
            if FL <= 5:
                continue
            b = nb(); pb = P[b][:, :].bitcast(BF16)
            for g in range(2):
                tp(pb[0:32, g * 128:(g + 1) * 128], mbf[:, g, :], ident, r=["mbf", "ident"], w=[PK[b]])
            act(mbT[0:32, :, ts], pb[0:32, 0:256].rearrange("p (g t) -> p g t", g=2), AF.Copy, r=[PK[b]], w=[("mbT", i)])
        dbg_out("onsa_c", onsa.rearrange("p a b -> p (a b)"), [("onsa", i) for i in range(NT)])
        dbg_out("mbT", mbT[0:32].rearrange("p a b -> p (a b)"), [("mbT", i) for i in range(NT)])
        if upto <= 7:
            return finish(nc, S, st)

        S.barrier()
        nbanks[0] = 6
        jobs = []
        for p in range(8):
            j, g = p // 2, p % 2

            def extra(ps_ap, kt, a, b_, pk, g=g):
                mm(ps_ap, XE[0:32, kt, :], mbT[0:32, g, a:b_], False, True, r=["XE"] + [("mbT", q) for q in range(a // 128, b_ // 128)], w=[pk])

            def fin(c, Oacc, pk, p=p):
                rc = rec4[c % 2]
                S.op("dve", lambda e: e.reciprocal(out=rc, in_=Oacc[:, :, 64]), r=[pk], w=[("rec4", c % 2)])
                S.op("dve", lambda e: e.tensor_tensor(out=rc, in0=rc, in1=gates[:, 4 * c:4 * c + 4, 1, p], op=ALU.mult), r=[("rec4", c % 2)] + [("gates", 4 * c + q) for q in range(4)], w=[("rec4", c % 2)])
                tv = tmpo[:, 0:4, :]
                S.op("dve", lambda e: e.tensor_tensor(out=tv, in0=Oacc[:, :, 0:64], in1=rc.unsqueeze(2).to_broadcast([128, 4, 64]), op=ALU.mult), r=[pk, ("rec4", c % 2)], w=["tmpo"])
                S.op("pool", lambda e: e.tensor_tensor(out=onsa[:, 4 * c:4 * c + 4, p * 64:(p + 1) * 64], in0=onsa[:, 4 * c:4 * c + 4, p * 64:(p + 1) * 64], in1=tv, op=ALU.add),
                     r=["tmpo"] + [("onsa", 4 * c + q) for q in range(4)], w=[("onsa", 4 * c + q) for q in range(4)])
            jobs.append(dict(KT=ksT[:, g, :], kn=64, QT=(lambda a, b_, p=p: qnT[0:64, p, a:b_]), V=(lambda kt, g=g: vs[:, kt, g, :]), scale=0.125,
                             extra=extra, fin=fin, qk=(lambda c: [("qnT", 4 * c + q) for q in range(4)]), kk=(lambda kt: [("ksT", kt)]),
                             vk=(lambda kt: [("vs", kt), "vsones"])))
        causal_attn_multi(jobs)
        dbg_out("onsa_cs", onsa.rearrange("p a b -> p (a b)"), [("onsa", i) for i in range(NT)])
        if upto <= 8:
            return finish(nc, S, st)

        pw_i = [0]
        ob = [6, 7]

        def emit_ws(i, p):
            g = p % 2
            kts = [kt for kt in (i - 2, i - 1, i) if kt >= 0]
            b = nb()
            for kt in kts:
                sl = kt - (i - 2)
                mm(P[b][:, sl * 128:(sl + 1) * 128], kwT[0:64, g, kt * 128:(kt + 1) * 128], qnT[0:64, p, i * 128:(i + 1) * 128], True, True,
                   r=[("kwT", kt), ("qnT", i)], w=[PK[b]])
            return b

        wsteps = [(i, p) for i in range(NT) for p in range(8)]
        wpend = emit_ws(*wsteps[0])
        for wi_, (i, p) in enumerate(wsteps):
            ts = slice(i * 128, (i + 1) * 128)
            g = p % 2
            kts = [kt for kt in (i - 2, i - 1, i) if kt >= 0]
            b = wpend
            if wi_ + 1 < len(wsteps):
                wpend = emit_ws(*wsteps[wi_ + 1])
            s0 = kts[0] - (i - 2)
            wi = pw_i[0]; pw_i[0] = (wi + 1) % 3
            pwv = pw[wi]
            act(pwv[:, s0:3, :], P[b][:, s0 * 128:384].rearrange("p (a b) -> p a b", b=128), AF.Exp, r=[PK[b]], w=[("pw", wi)], scale=0.125)
            S.op("dve", lambda e, pwv=pwv, s0=s0: e.tensor_tensor(out=pwv[:, s0:3, :], in0=pwv[:, s0:3, :], in1=winm[:, s0:3, :], op=ALU.mult), r=[("pw", wi), "winm"], w=[("pw", wi)])
            for kt in kts:
                sl = kt - (i - 2)
                mm(P[ob[p // 4]][:, (p % 4) * 65:(p % 4 + 1) * 65], pwv[:, sl, :], vw[:, kt, g, :], kt == kts[0], kt == kts[-1],
                   r=[("pw", wi), ("vw", kt), "vwones"], w=[PK[ob[p // 4]]])
            if p != 7:
                continue
            OW = [P[ob[h_]][:, 0:260].rearrange("p (a b) -> p a b", a=4) for h_ in range(2)]
            for h_ in range(2):
                S.op("dve", lambda e, h_=h_: e.reciprocal(out=rec8[:, h_ * 4:(h_ + 1) * 4], in_=OW[h_][:, :, 64]), r=[PK[ob[h_]]], w=["rec8"])
            S.op("dve", lambda e, i=i: e.tensor_tensor(out=gr, in0=gates[:, i, 2, :], in1=rec8, op=ALU.mult), r=["rec8", ("gates", i)], w=["gr"])
            for h_ in range(2):
                S.op("dve", lambda e, h_=h_: e.tensor_tensor(out=tmpo[:, h_ * 4:(h_ + 1) * 4, :], in0=OW[h_][:, :, 0:64],
                                                             in1=gr[:, h_ * 4:(h_ + 1) * 4].unsqueeze(2).to_broadcast([128, 4, 64]), op=ALU.mult), r=[PK[ob[h_]], "gr"], w=["tmpo"])
            S.op("pool", lambda e, i=i: e.tensor_tensor(out=onb[i % 2], in0=onsa[:, i, :], in1=tmpo.rearrange("p a b -> p (a b)"), op=ALU.add), r=["tmpo", ("onsa", i)], w=[("onb", i % 2)])
            if "onsa_all" in D_:
                S.dma("sp", D_["onsa_all"][:, i * 512:(i + 1) * 512], onb[i % 2], r=[("onb", i % 2)])
            b = nb(); pb = P[b][:, :].bitcast(BF16)
            for j in range(4):
                tp(pb[:, j * 128:(j + 1) * 128], onb[i % 2][:, j * 128:(j + 1) * 128], ident, r=[("onb", i % 2), "ident"], w=[PK[b]])
            act(onT[:, :, ts], pb[:, 0:512].rearrange("p (k t) -> p k t", k=4), AF.Copy, r=[PK[b]], w=[("onT", i)])
        if upto <= 9:
            return finish(nc, S, st)
        S.barrier()

        nbanks[0] = 8
        M.lo = lo0; M.hi = hi0
        hT = M.hi_alloc([8, S_], BF16)
        mergedT = M.hi_alloc([8, S_], BF16)
        hi2 = M.hi
        Wgm = M.lo_alloc([8, 512], BF16); Wgnm = M.lo_alloc([8, 512], BF16); Wom = M.lo_alloc([4, 512], BF16); Won = M.lo_alloc([4, 512], BF16)
        e3 = M.lo_alloc([512], F32); e4 = M.lo_alloc([512], F32); tA = M.lo_alloc([512], F32); tB = M.lo_alloc([512], F32)
        mgb = [M.lo_alloc([512], BF16) for _ in range(2)]
        S.dma("sp", hT.rearrange("p k t -> p (k t)"), hTs, r=["hTs"], w=["hTall"])
        e3 = [e3, M.lo_alloc([512], F32)]; e4 = [e4, M.lo_alloc([512], F32)]
        tA = [tA, M.lo_alloc([512], F32)]; tB = [tB, M.lo_alloc([512], F32)]

        def i_front(cc, i, it):
            ts = slice(i * 128, (i + 1) * 128)
            bs = [4 * (it % 2) + q for q in range(4)]
            b1, b2, b3, b4 = bs
            for k in range(8):
                mm(P[b3][:, :], hT[:, k, ts], Wgm[:, k, :], k == 0, k == 7, r=["hTall", "Wgm"], w=[PK[b3]])
            for k in range(8):
                mm(P[b4][:, :], hT[:, k, ts], Wgnm[:, k, :], k == 0, k == 7, r=["hTall", "Wgnm"], w=[PK[b4]])
            for k in range(4):
                mm(P[b1][:, :], omT[:, k, ts], Wom[:, k, :], k == 0, k == 3, r=[("omT", i), "Wom"], w=[PK[b1]])
            for k in range(4):
                mm(P[b2][:, :], onT[:, k, ts], Won[:, k, :], k == 0, k == 3, r=[("onT", i), "Won"], w=[PK[b2]])
            return bs

        def i_back(cc, i, it, bs):
            ts = slice(i * 128, (i + 1) * 128)
            b1, b2, b3, b4 = bs
            par = it % 2
            act(e3[par], P[b3][:, :], AF.Sigmoid, r=[PK[b3]], w=[("e3", par)])
            act(e4[par], P[b4][:, :], AF.Sigmoid, r=[PK[b4]], w=[("e4", par)])
            S.op("dve", lambda e: e.tensor_tensor(out=tA[par], in0=P[b1][:, :], in1=e3[par], op=ALU.mult), r=[PK[b1], ("e3", par)], w=[("tA", par)])
            S.op("dve", lambda e: e.tensor_tensor(out=tB[par], in0=P[b2][:, :], in1=e4[par], op=ALU.mult), r=[PK[b2], ("e4", par)], w=[("tB", par)])
            S.op("dve", lambda e: e.tensor_tensor(out=mgb[par], in0=tA[par], in1=tB[par], op=ALU.add), r=[("tA", par), ("tB", par)], w=[("mgb", par)])
            if "merged" in D_:
                S.dma("sp", D_["merged"][i * 128:(i + 1) * 128, cc * 512:(cc + 1) * 512], mgb[par], r=[("mgb", par)])
            pb = P[b3][:, :].bitcast(BF16)
            for j in range(4):
                tp(pb[:, j * 128:(j + 1) * 128], mgb[par][:, j * 128:(j + 1) * 128], ident, r=[("mgb", par), "ident"], w=[PK[b3]])
            act(mergedT[:, cc * 4:(cc + 1) * 4, ts], pb[:, 0:512].rearrange("p (k t) -> p k t", k=4), AF.Copy, r=[PK[b3]], w=[("mergedT", i)])

        it = 0
        for cc in range(2):
            load_w_cols(Wgm, w_gm, "Wgm", 8, cc * 512, (cc + 1) * 512); load_w_cols(Wgnm, w_gnm, "Wgnm", 8, cc * 512, (cc + 1) * 512)
            load_w_cols(Wom, wo_mla, "Wom", 4, cc * 512, (cc + 1) * 512); load_w_cols(Won, wo_nsa, "Won", 4, cc * 512, (cc + 1) * 512)
            pend_ = i_front(cc, 0, it)
            for i in range(NT):
                cur_ = pend_
                if i + 1 < NT:
                    pend_ = i_front(cc, i + 1, it + 1)
                i_back(cc, i, it, cur_)
                it += 1
        if upto <= 10:
            return finish(nc, S, st)
        S.barrier()

        M.lo = lo0
        h2T = hT
        Wout = M.lo_alloc([8, D], BF16)
        G1 = M.lo_alloc([D], F32); A2 = M.lo_alloc([D], F32); B2 = M.lo_alloc([D], F32)
        xb = [M.lo_alloc([D], F32) for _ in range(2)]
        x1t = [M.lo_alloc([D], F32) for _ in range(2)]
        junk = M.lo_alloc([D], F32); tmpA = M.lo_alloc([D], F32)
        hb = [M.lo_alloc([D], BF16) for _ in range(2)]
        ssq = M.lo_alloc([NT], F32); rs = M.lo_alloc([NT], F32)
        S.dma("sp", G1, mods[:, 2 * D:3 * D], r=["mods"], w=["G1"])
        S.dma("sp", B2, mods[:, 3 * D:4 * D], r=["mods"], w=["AB2"])
        S.dma("sp", A2, mods[:, 4 * D:5 * D], r=["mods"], w=["AB2"])
        load_w(Wout, w_out, "Wout", 8, D, ceng="pool")
        tmpA2 = [tmpA, M.lo_alloc([D], F32)]
        tmpJ = [M.lo_alloc([D], F32) for _ in range(3)]
        xb = xb + [M.lo_alloc([D], F32)]
        x1t = x1t + [M.lo_alloc([D], F32)]

        def j_front(i):
            ts = slice(i * 128, (i + 1) * 128)
            par = i % 3
            S.dma("sp", xb[par], x[ts, :], w=[("xb", par)])
            for cc in range(2):
                b = 2 * par + cc
                for k in range(8):
                    mm(P[b][:, :], mergedT[:, k, ts], Wout[:, k, cc * 512:(cc + 1) * 512], k == 0, k == 7, r=[("mergedT", i), "Wout"], w=[PK[b]])

        def j_mid(i):
            ts = slice(i * 128, (i + 1) * 128)
            par = i % 3
            for cc in range(2):
                b = 2 * par + cc
                S.op("dve", lambda e, b=b, cc=cc: e.tensor_tensor(out=tmpJ[par][:, cc * 512:(cc + 1) * 512], in0=P[b][:, :], in1=G1[:, cc * 512:(cc + 1) * 512], op=ALU.mult), r=[PK[b], "G1"], w=[("tmpJ", par)])
                S.op("pool", lambda e, cc=cc: e.tensor_tensor(out=x1t[par][:, cc * 512:(cc + 1) * 512], in0=tmpJ[par][:, cc * 512:(cc + 1) * 512], in1=xb[par][:, cc * 512:(cc + 1) * 512], op=ALU.add),
                     r=[("tmpJ", par), ("xb", par)], w=[("x1t", par)])
            S.dma("sp", x1s[ts, :], x1t[par], r=[("x1t", par)], w=[("x1s", i)])
            norm_a(i, x1t[par], ("x1t", par))

        bank_state[0] = 0

        def nbJ():
            b = 6 + (bank_state[0] % 2)
            bank_state[0] = (bank_state[0] + 1) % 2
            return b
        nb_saved = nb
        nb = nbJ
        j_front(0); j_mid(0); j_front(1); j_mid(1)
        for i in range(NT):
            if i + 2 < NT:
                j_front(i + 2)
            b_ = norm_b1(i, x1t[i % 3], ("x1t", i % 3), A2, B2, ["AB2"])
            if i + 2 < NT:
                j_mid(i + 2)
            norm_b2(i, b_, h2T, "h2T")
        nb = nb_saved
        bank_state[0] = 0
        if "x1" in D_:
            S.dma("sp", D_["x1"], x1s, r=[("x1s", i) for i in range(NT)])
        if upto <= 11:
            return finish(nc, S, st)
        S.barrier()

        M.lo = lo_pers; M.hi = hi2 + 8 * S_ * 2
        Wd = M.hi_alloc([NFC, D], BF16)
        actT = M.hi_alloc([NFC, 1024], BF16)
        G2 = M.lo_alloc([D], F32)
        Wg2 = [M.lo_alloc([8, 256], BF16) for _ in range(2)]; Wu2 = [M.lo_alloc([8, 256], BF16) for _ in range(2)]
        sg = [M.lo_alloc([512], F32) for _ in range(2)]
        xb = [M.lo_alloc([D], F32) for _ in range(2)]
        ot = [M.lo_alloc([D], F32) for _ in range(2)]
        tmpA = M.lo_alloc([D], F32)
        S.dma("sp", G2, mods[:, 5 * D:6 * D], r=["mods"], w=["G2"])
        load_w(Wd, wd, "Wd", NFC, D)
        h2k = [("h2T", i) for i in range(NT)]
        out_toks = []
        for half in range(2):
            def ld(jg_):
                wb_ = jg_ % 2
                load_w_cols(Wg2[wb_], wg, ("Wg2", wb_), 8, jg_ * 256, (jg_ + 1) * 256)
                load_w_cols(Wu2[wb_], wu, ("Wu2", wb_), 8, jg_ * 256, (jg_ + 1) * 256)
            ld(0)
            for jg in range(NFC // 2):
                wb = jg % 2
                if jg + 1 < NFC // 2:
                    ld(jg + 1)
                for jj in range(2):
                    j = jg * 2 + jj
                    for tc in range(2):
                        t0 = half * 1024 + tc * 512
                        bg, bu = nb(), nb()
                        for k in range(8):
                            mm(P[bg][:, :], Wg2[wb][:, k, jj * 128:(jj + 1) * 128], h2T[:, k, t0:t0 + 512], k == 0, k == 7, r=[("Wg2", wb)] + h2k, w=[PK[bg]])
                        for k in range(8):
                            mm(P[bu][:, :], Wu2[wb][:, k, jj * 128:(jj + 1) * 128], h2T[:, k, t0:t0 + 512], k == 0, k == 7, r=[("Wu2", wb)] + h2k, w=[PK[bu]])
                        act(sg[tc], P[bg][:, :], AF.Silu, r=[PK[bg]], w=[("sg", tc)])
                        S.op("dve", lambda e, j=j, tc=tc, bu=bu: e.tensor_tensor(out=actT[:, j, tc * 512:(tc + 1) * 512], in0=P[bu][:, :], in1=sg[tc], op=ALU.mult),
                             r=[PK[bu], ("sg", tc)], w=[("actT", tc)])
            for il in range(8):
                i = half * 8 + il
                ts = slice(i * 128, (i + 1) * 128)
                S.dma("sp", xb[i % 2], x1s[ts, :], r=[("x1s", i)], w=[("xb", i % 2)])
                for cc in range(2):
                    b = nb()
                    for j in range(NFC):
                        mm(P[b][:, :], actT[:, j, il * 128:(il + 1) * 128], Wd[:, j, cc * 512:(cc + 1) * 512], j == 0, j == NFC - 1, r=[("actT", il // 4), "Wd"], w=[PK[b]])
                    S.op("dve", lambda e, b=b, cc=cc: e.tensor_tensor(out=tmpA[:, cc * 512:(cc + 1) * 512], in0=P[b][:, :], in1=G2[:, cc * 512:(cc + 1) * 512], op=ALU.mult), r=[PK[b], "G2"], w=["tmpA"])
                    S.op("pool", lambda e, i=i, cc=cc: e.tensor_tensor(out=ot[i % 2][:, cc * 512:(cc + 1) * 512], in0=tmpA[:, cc * 512:(cc + 1) * 512], in1=xb[i % 2][:, cc * 512:(cc + 1) * 512], op=ALU.add),
                         r=["tmpA", ("xb", i % 2)], w=[("ot", i % 2)])
                S.dma("sp", out[ts, :], ot[i % 2], r=[("ot", i % 2)], w=[("out", i)])
        return finish(nc, S, st)


def finish(nc, S, st):
    for q in S.dsem:
        for i in range(len(S.dsem[q])):
            if S.dcnt[q][i]:
                S._wait("sp", (("d", q, i), S.dcnt[q][i]))
    for e2 in ("pe", "act", "dve", "pool"):
        if S.cnt[e2]:
            S._wait("sp", (e2, S.cnt[e2]))
    S.check_no_deadlock()
    st.close()
    return nc


def _consts():
    bf = ml_dtypes.bfloat16
    c = {}
    c["ident"] = np.eye(128, dtype=np.float32).astype(bf)
    a = np.arange(128)
    c["tri"] = (a[:, None] <= a[None, :]).astype(np.float32).astype(bf)
    w = np.zeros((128, 3, 128), np.float32)
    w[:, 0, :] = (a[:, None] > a[None, :])
    w[:, 1, :] = 1.0
    w[:, 2, :] = (a[:, None] <= a[None, :])
    c["winm"] = w.reshape(128, 384).astype(bf)
    inv16 = (np.float32(500000.0) ** (-np.arange(0, 32, 2, dtype=np.float32) / np.float32(32))).astype(np.float32)
    inv8 = (np.float32(500000.0) ** (-np.arange(0, 16, 2, dtype=np.float32) / np.float32(16))).astype(np.float32)
    c["inv16"] = np.tile(inv16[None], (128, 1)).astype(np.float32)
    c["inv8"] = np.tile(inv8[None], (128, 1)).astype(np.float32)
    n = np.arange(127)
    starts = n * 16
    j = np.arange(32)
    ovl = ((starts[:, None] < j[None, :] * 64 + 64) & (starts[:, None] + 32 > j[None, :] * 64))
    c["ovl"] = ovl.astype(np.float32).astype(bf)
    t = np.arange(S_)
    c["vmT"] = ((starts[:, None] + 31) <= t[None, :]).astype(np.float32).astype(bf)
    XE = np.zeros((32, NT, 128), np.float32)
    for kt in range(NT):
        XE[2 * kt, kt, 0:64] = 1.0
        XE[2 * kt + 1, kt, 64:128] = 1.0
    c["XE"] = XE.reshape(32, NT * 128).astype(bf)
    cur = (t // 64)
    forced = (j[None, :] == 0) | (j[None, :] == cur[:, None]) | (j[None, :] == cur[:, None] - 1)
    valid = j[None, :] <= cur[:, None]
    fb = np.where(valid, np.where(forced, 1e4, 0.0), -1e30).astype(np.float32)
    c["fb"] = fb.reshape(NT, 128, 32).transpose(1, 0, 2).reshape(128, NT * 32).copy()
    c["vj"] = valid.astype(np.float32).reshape(NT, 128, 32).transpose(1, 0, 2).reshape(128, NT * 32).copy()
    return c


def _rep(v, n=128):
    return np.ascontiguousarray(np.broadcast_to(np.asarray(v, np.float32)[None, :], (n, v.shape[0])))


def prep_inputs(inp):
    f = lambda a: np.ascontiguousarray(np.asarray(a, dtype=np.float32))
    w_in = f(inp["w_in"][0])
    o = np.cumsum([0, 768, 256, 32, 512, 128, 128, 128, 128, 128, 128, 24, 1024, 1024])
    seg = lambda i: w_in[:, o[i]:o[i + 1]]
    shared = {}
    shared["ada_w"] = f(inp["ada_w"][0]); shared["adabB"] = _rep(f(inp["ada_b"][0]))
    shared["g1B"] = _rep(f(inp["norm1_gain"][0])); shared["g2B"] = _rep(f(inp["norm2_gain"][0]))
    shared["w_cq"] = f(seg(0)); shared["w_ckv"] = f(seg(1)); shared["w_kpe"] = f(seg(2))
    qn = seg(3).reshape(D, 8, 64)
    shared["w_qn"] = f(qn[:, PH, :].reshape(D, 512))
    kc = seg(4).reshape(D, 2, 64); vc = seg(5).reshape(D, 2, 64)
    shared["w_kc2"] = f(np.stack([kc[:, 0], kc[:, 0], kc[:, 1], kc[:, 1]], 1).reshape(D, 256))
    shared["w_vc2"] = f(np.stack([vc[:, 0], vc[:, 0], vc[:, 1], vc[:, 1]], 1).reshape(D, 256))
    shared["w_kv4"] = f(np.concatenate([seg(6), seg(8), seg(7), seg(9)], 1))
    gn = seg(10).reshape(D, 8, 3)
    shared["w_gn"] = f(gn[:, PH, :].transpose(0, 2, 1).reshape(D, 24))
    shared["w_gm"] = f(seg(11)); shared["w_gnm"] = f(seg(12))
    shared["qag"] = f(f(inp["mla_q_a_gain"][0]).reshape(6, 128).T); shared["kvag"] = f(f(inp["mla_kv_a_gain"][0]).reshape(2, 128).T)
    shared["w_qb"] = f(inp["mla_w_q_b"][0]); shared["w_kvb"] = f(inp["mla_w_kv_b"][0])
    shared["qgB"] = _rep(f(inp["mla_q_gain"][0])); shared["kgB"] = _rep(f(inp["mla_k_gain"][0]))
    shared["nqB"] = _rep(f(inp["nsa_q_gain"][0])); shared["nkcB"] = _rep(f(inp["nsa_kc_gain"][0]))
    shared["nksB"] = _rep(f(inp["nsa_ks_gain"][0])); shared["nkwB"] = _rep(f(inp["nsa_kw_gain"][0]))
    shared["posk"] = f(f(inp["cmp_pos_k"][0]).reshape(16, 128).T); shared["posv"] = f(f(inp["cmp_pos_v"][0]).reshape(16, 128).T)
    shared["w1k"] = f(inp["cmp_w1_k"][0]); shared["w2k"] = f(inp["cmp_w2_k"][0])
    shared["w1v"] = f(inp["cmp_w1_v"][0]); shared["w2v"] = f(inp["cmp_w2_v"][0])
    shared["wo_mla"] = f(inp["w_o_mla"][0])
    shared["wo_nsa"] = f(f(inp["w_o_nsa"][0]).reshape(8, 64, D)[PH].reshape(512, D))
    shared["w_out"] = f(inp["w_out"][0])
    shared["wg"] = f(inp["ffn_w_gate"][0]); shared["wu"] = f(inp["ffn_w_up"][0]); shared["wd"] = f(inp["ffn_w_down"][0])
    shared.update(_consts())
    maps = []
    xs = np.asarray(inp["x"], np.float32); cs = np.asarray(inp["c"], np.float32); ps = np.asarray(inp["positions"]).astype(np.int32)
    for b in range(xs.shape[0]):
        m = dict(shared)
        m["x"] = np.ascontiguousarray(xs[b])
        m["c_pk"] = np.ascontiguousarray(cs[b].reshape(8, 128).T)
        m["pos_pk"] = np.ascontiguousarray(ps[b].reshape(NT, 128).T)
        m["posC"] = np.ascontiguousarray(ps[b][31::16][:127].reshape(127, 1))
        maps.append(m)
    return maps


_NC_CACHE = {}


def kernel(**inputs):
    maps = prep_inputs(inputs)
    if "nc" not in _NC_CACHE:
        _NC_CACHE["nc"] = build()
    nc = _NC_CACHE["nc"]
    res = run_bass_kernel_spmd(nc, maps, core_ids=list(range(len(maps))))
    return np.stack([np.asarray(r["out"], dtype=np.float32) for r in res.results], 0)
```

```python
import contextlib
import numpy as np
import ml_dtypes
import concourse.bass as bass
import concourse.mybir as mybir
from concourse.bass_utils import run_bass_kernel_spmd

F32 = mybir.dt.float32
BF16 = mybir.dt.bfloat16
I32 = mybir.dt.int32
ALU = mybir.AluOpType
AF = mybir.ActivationFunctionType
AX = mybir.AxisListType

S_ = 2048
D = 1024
NT = 16
DFF = 2816
NFC = 22
EPS = 1e-6
PH = [0, 4, 1, 5, 2, 6, 3, 7]
TWO_PI = float(2 * np.pi)
PI = float(np.pi)


class Sched:
    N_DMA_SLOTS = {"sp": 24, "pool": 8, "act": 4}

    def __init__(self, nc, stack):
        self.nc = nc
        self.E = {"pe": nc.tensor, "act": nc.scalar, "dve": nc.vector, "pool": nc.gpsimd, "sp": nc.sync}
        self.sem, self.cnt = {}, {}
        for e in ("pe", "act", "dve", "pool"):
            self.sem[e] = stack.enter_context(nc.semaphore("s_" + e))
            self.cnt[e] = 0
        self.dsem, self.dcnt, self.dnext = {}, {}, {}
        for q, n in self.N_DMA_SLOTS.items():
            self.dsem[q] = [stack.enter_context(nc.semaphore(f"d_{q}{i}")) for i in range(n)]
            self.dcnt[q] = [0] * n
            self.dnext[q] = 0
        self.seen = {e: {} for e in self.E}
        self.lastw, self.readers = {}, {}
        self.n_wait = 0
        self.n_inst = 0
        self.prog = {}
        self.ninst = {}

    def check_no_deadlock(self):
        val = {}
        ptr = {e: 0 for e in self.prog}
        progress = True
        while progress:
            progress = False
            for e, lst in self.prog.items():
                while ptr[e] < len(lst):
                    it = lst[ptr[e]]
                    if it[0] == "w":
                        if val.get(it[1], 0) < it[2]:
                            break
                    else:
                        val[it[1]] = val.get(it[1], 0) + it[2]
                    ptr[e] += 1
                    progress = True
        stuck = {e: (ptr[e], len(l), l[ptr[e]]) for e, l in self.prog.items() if ptr[e] < len(l)}
        assert not stuck, f"DEADLOCK in emitted program: {stuck}"

    def _sem_of(self, src):
        return self.dsem[src[1]][src[2]] if isinstance(src, tuple) else self.sem[src]

    def _wait(self, e, tok):
        src, val = tok
        if self.seen[e].get(src, 0) >= val:
            return
        self.E[e].wait_ge(self._sem_of(src), val)
        self.seen[e][src] = val
        self.n_wait += 1
        self.prog.setdefault(e, []).append(("w", src, val))

    def _deps(self, e, r, w):
        toks = []
        for k in r:
            t = self.lastw.get(k)
            if t is not None:
                toks.append(t)
        for k in w:
            t = self.lastw.get(k)
            if t is not None:
                toks.append(t)
            for t in self.readers.get(k, ()):
                toks.append(t)
        for t in toks:
            if t[0] == e and e == "pe":
                continue
            self._wait(e, t)

    def _commit(self, tok, r, w):
        for k in r:
            lst = self.readers.setdefault(k, [])
            lst[:] = [t for t in lst if t[0] != tok[0]]
            lst.append(tok)
        for k in w:
            self.lastw[k] = tok
            self.readers[k] = []

    def op(self, e, fn, r=(), w=(), signal=True):
        self._deps(e, r, w)
        ins = fn(self.E[e])
        if signal:
            self.cnt[e] += 1
            ins.then_inc(self.sem[e], 1)
            tok = (e, self.cnt[e])
            self.prog.setdefault(e, []).append(("i", e, 1))
        else:
            tok = (e, self.cnt[e] + 1)
        self._commit(tok, r, w)
        self.n_inst += 1
        self.ninst[e] = self.ninst.get(e, 0) + 1
        return tok

    def dma(self, q, out, in_, r=(), w=(), **kw):
        slot = self.dnext[q]
        self.dnext[q] = (slot + 1) % len(self.dsem[q])
        src = ("d", q, slot)
        if self.dcnt[q][slot] > 0:
            self._wait(q, (src, self.dcnt[q][slot]))
        self._deps(q, r, w)
        ins = self.E[q].dma_start(out=out, in_=in_, **kw)
        self.dcnt[q][slot] += 16
        ins.then_inc(self.dsem[q][slot], 16)
        self.prog.setdefault(q, []).append(("i", src, 16))
        tok = (src, self.dcnt[q][slot])
        self._commit(tok, r, w)
        self.n_inst += 1
        return tok

    def barrier(self):
        for e in ("pe", "act", "dve", "pool", "sp"):
            for e2 in ("pe", "act", "dve", "pool"):
                if self.cnt[e2] and not (e2 == e == "pe"):
                    self._wait(e, (e2, self.cnt[e2]))
            for q in self.dsem:
                for i in range(len(self.dsem[q])):
                    if self.dcnt[q][i]:
                        self._wait(e, (("d", q, i), self.dcnt[q][i]))
        self.lastw.clear()
        self.readers.clear()


class Mem:
    def __init__(self, big, nbytes):
        self.big, self.lo, self.hi, self.n = big, 0, nbytes, nbytes

    def _view(self, off, shape, dt):
        nel = int(np.prod(shape))
        esz = 4 if dt in (F32, I32) else 2
        nb = nel * esz
        ap = self.big[:, off // 2:(off + nb) // 2]
        if esz == 4:
            ap = ap.bitcast(dt)
        if len(shape) == 2:
            ap = ap.rearrange("p (a b) -> p a b", a=shape[0])
        elif len(shape) == 3:
            ap = ap.rearrange("p (a b c) -> p a b c", a=shape[0], b=shape[1])
        return ap

    def lo_alloc(self, shape, dt):
        nb = int(np.prod(shape)) * (4 if dt in (F32, I32) else 2)
        nb = (nb + 63) // 64 * 64
        off = self.lo
        self.lo += nb
        assert self.lo <= self.hi, f"SBUF overflow lo={self.lo} hi={self.hi}"
        return self._view(off, shape, dt)

    def hi_alloc(self, shape, dt):
        nb = int(np.prod(shape)) * (4 if dt in (F32, I32) else 2)
        nb = (nb + 63) // 64 * 64
        self.hi -= nb
        assert self.lo <= self.hi, f"SBUF overflow lo={self.lo} hi={self.hi}"
        return self._view(self.hi, shape, dt)


def build(upto=99, dbg=()):
    nc = bass.Bass("TRN2", target_bir_lowering=False)
    I = {}

    def din(name, shape, dt=F32):
        I[name] = nc.dram_tensor(name, list(shape), dt, kind="ExternalInput").ap()
        return I[name]

    x = din("x", [S_, D]); c_pk = din("c_pk", [128, 8]); pos_pk = din("pos_pk", [128, NT], I32)
    posC = din("posC", [127, 1], I32)
    ada_w = din("ada_w", [D, 6 * D]); adabB = din("adabB", [128, 6 * D]); g1B = din("g1B", [128, D]); g2B = din("g2B", [128, D])
    w_cq = din("w_cq", [D, 768]); w_ckv = din("w_ckv", [D, 256]); w_kpe = din("w_kpe", [D, 32])
    w_qn = din("w_qn", [D, 512]); w_kc2 = din("w_kc2", [D, 256]); w_vc2 = din("w_vc2", [D, 256])
    w_kv4 = din("w_kv4", [D, 512]); w_gn = din("w_gn", [D, 24]); w_gm = din("w_gm", [D, D]); w_gnm = din("w_gnm", [D, D])
    qag = din("qag", [128, 6]); kvag = din("kvag", [128, 2])
    w_qb = din("w_qb", [768, 768]); w_kvb = din("w_kvb", [256, 1024])
    qgB = din("qgB", [128, 96]); kgB = din("kgB", [128, 96])
    nqB = din("nqB", [128, 64]); nkcB = din("nkcB", [128, 64]); nksB = din("nksB", [128, 64]); nkwB = din("nkwB", [128, 64])
    posk = din("posk", [128, 16]); w1k = din("w1k", [2048, 256]); w2k = din("w2k", [256, 64])
    posv = din("posv", [128, 16]); w1v = din("w1v", [2048, 256]); w2v = din("w2v", [256, 64])
    wo_mla = din("wo_mla", [512, D]); wo_nsa = din("wo_nsa", [512, D]); w_out = din("w_out", [D, D])
    wg = din("wg", [D, DFF]); wu = din("wu", [D, DFF]); wd = din("wd", [DFF, D])
    ident_d = din("ident", [128, 128], BF16); tri_d = din("tri", [128, 128], BF16); winm_d = din("winm", [128, 384], BF16)
    inv16_d = din("inv16", [128, 16]); inv8_d = din("inv8", [128, 8])
    ovl_d = din("ovl", [127, 32], BF16); vmT_d = din("vmT", [127, S_], BF16); XE_d = din("XE", [32, NT * 128], BF16)
    fb_d = din("fb", [128, NT * 32]); vj_d = din("vj", [128, NT * 32])
    out = nc.dram_tensor("out", [S_, D], F32, kind="ExternalOutput").ap()
    hTs = nc.dram_tensor("hTs", [128, 8 * S_], BF16).ap()
    mods = nc.dram_tensor("mods", [128, 6 * D], F32).ap()
    x1s = nc.dram_tensor("x1s", [S_, D], F32).ap()
    D_ = {}
    for name, shape, dt in dbg:
        D_[name] = nc.dram_tensor("dbg_" + name, list(shape), dt, kind="ExternalOutput").ap()

    st = contextlib.ExitStack()
    with st:
        S = Sched(nc, st)
        NB = 204800
        big = st.enter_context(nc.sbuf_tensor("big", [128, NB // 2], BF16))
        M = Mem(big, NB)
        P = [st.enter_context(nc.psum_tensor(f"ps{i}", [128, 512], F32)) for i in range(8)]
        PK = [f"ps{i}" for i in range(8)]
        bank_state = [0]

        nbanks = [8]

        def nb():
            b = bank_state[0] % nbanks[0]
            bank_state[0] = (b + 1) % nbanks[0]
            return b

        def mm(ps_ap, lhsT, rhs, start, stop, r, w, sig=None, **kw):
            S.op("pe", lambda e: e.matmul(ps_ap, lhsT=lhsT, rhs=rhs, start=start, stop=stop, **kw), r=r, w=w, signal=bool(stop) if sig is None else sig)

        def tp(ps_ap, in_, ident_ap, r, w):
            S.op("pe", lambda e: e.transpose(out=ps_ap, in_=in_, identity=ident_ap), r=r, w=w)

        def act(out_, in_, func, r, w, **kw):
            S.op("act", lambda e: e.activation(out=out_, in_=in_, func=func, **kw), r=r, w=w)

        def dbg_out(name, ap, r):
            if name in D_:
                S.dma("sp", D_[name], ap, r=r)

        ident = M.lo_alloc([128], BF16); tri = M.lo_alloc([128], BF16); winm = M.lo_alloc([3, 128], BF16)
        onesb = M.lo_alloc([128], BF16)
        stg = [M.lo_alloc([1024], F32) for _ in range(3)]
        stg_i = [0]
        cosM = M.lo_alloc([NT, 16], F32); sinM = M.lo_alloc([NT, 16], F32)
        cosN = M.lo_alloc([NT, 8], F32); sinN = M.lo_alloc([NT, 8], F32)
        cosC = M.lo_alloc([8], F32); sinC = M.lo_alloc([8], F32)
        lo_pers = M.lo
        omT = M.lo_alloc([4, S_], BF16); onT = M.lo_alloc([4, S_], BF16)
        S.dma("sp", ident, ident_d, w=["ident"])
        S.dma("sp", tri, tri_d, w=["tri"])
        S.dma("sp", winm.rearrange("p a b -> p (a b)"), winm_d, w=["winm"])
        S.op("pool", lambda e: e.memset(onesb, 1.0), w=["onesb"])

        def load_w(dst, W, key, KC, N, ceng="dve"):
            Wv = W.rearrange("(k p) n -> p k n", p=128)
            if N <= 1024:
                g = max(1, min(KC, 1024 // N))
                for k0 in range(0, KC, g):
                    k1 = min(KC, k0 + g)
                    si = stg_i[0]; stg_i[0] = (si + 1) % 3
                    sv = stg[si][:, 0:(k1 - k0) * N].rearrange("p (k n) -> p k n", n=N)
                    S.dma("sp", sv, Wv[:, k0:k1, :], w=[("stg", si)])
                    S.op(ceng, lambda e, sv=sv, k0=k0, k1=k1: e.tensor_copy(out=dst[:, k0:k1, :], in_=sv), r=[("stg", si)], w=[key])
            else:
                for k in range(KC):
                    for c0 in range(0, N, 1024):
                        c1 = min(N, c0 + 1024)
                        si = stg_i[0]; stg_i[0] = (si + 1) % 3
                        sv = stg[si][:, 0:c1 - c0]
                        S.dma("sp", sv, Wv[:, k, c0:c1], w=[("stg", si)])
                        S.op(ceng, lambda e, sv=sv, k=k, c0=c0, c1=c1: e.tensor_copy(out=dst[:, k, c0:c1], in_=sv), r=[("stg", si)], w=[key])

        def load_w_cols(dst, W, key, KC, c0, c1, ceng="dve"):
            Wv = W.rearrange("(k p) n -> p k n", p=128)
            N = c1 - c0
            g = max(1, min(KC, 1024 // N))
            for k0 in range(0, KC, g):
                k1 = min(KC, k0 + g)
                si = stg_i[0]; stg_i[0] = (si + 1) % 3
                sv = stg[si][:, 0:(k1 - k0) * N].rearrange("p (k n) -> p k n", n=N)
                S.dma("sp", sv, Wv[:, k0:k1, c0:c1], w=[("stg", si)])
                S.op(ceng, lambda e, sv=sv, k0=k0, k1=k1: e.tensor_copy(out=dst[:, k0:k1, :], in_=sv), r=[("stg", si)], w=[key])

        def sincos(ang, shape, cos_o, sin_o, np_, tmp_f, tmp_i, tmp_m, key):
            for (shift, dst) in ((0.0, sin_o), (PI / 2, cos_o)):
                S.op("dve", lambda e: e.tensor_scalar(out=tmp_f, in0=ang, scalar1=shift, scalar2=None, op0=ALU.add), r=[key + "ang"], w=[key + "f"])
                S.op("dve", lambda e: e.tensor_scalar(out=tmp_i, in0=tmp_f, scalar1=float(1 / TWO_PI), scalar2=None, op0=ALU.mult), r=[key + "f"], w=[key + "i"])
                S.op("dve", lambda e: e.tensor_copy(out=tmp_m, in_=tmp_i), r=[key + "i"], w=[key + "m"])
                S.op("dve", lambda e: e.scalar_tensor_tensor(out=tmp_f, in0=tmp_m, scalar=-TWO_PI, in1=tmp_f, op0=ALU.mult, op1=ALU.add), r=[key + "m", key + "f"], w=[key + "f"])
                S.op("dve", lambda e: e.tensor_scalar(out=tmp_m, in0=tmp_f, scalar1=PI, scalar2=None, op0=ALU.is_gt), r=[key + "f"], w=[key + "m"])
                S.op("dve", lambda e: e.scalar_tensor_tensor(out=tmp_f, in0=tmp_m, scalar=-TWO_PI, in1=tmp_f, op0=ALU.mult, op1=ALU.add), r=[key + "m", key + "f"], w=[key + "f"])
                S.op("dve", lambda e: e.tensor_scalar(out=tmp_m, in0=tmp_f, scalar1=-PI, scalar2=None, op0=ALU.is_lt), r=[key + "f"], w=[key + "m"])
                S.op("dve", lambda e: e.scalar_tensor_tensor(out=tmp_f, in0=tmp_m, scalar=TWO_PI, in1=tmp_f, op0=ALU.mult, op1=ALU.add), r=[key + "m", key + "f"], w=[key + "f"])
                act(dst, tmp_f, AF.Sin, r=[key + "f"], w=[key + "out"])

        lo0, hi0 = M.lo, M.hi
        if upto <= -2:
            return finish(nc, S, st)
        posi = M.lo_alloc([NT], I32); posf = M.lo_alloc([NT], F32)
        posCi = M.lo_alloc([1], I32); posCf = M.lo_alloc([1], F32)
        inv16 = M.lo_alloc([16], F32); inv8 = M.lo_alloc([8], F32)
        angM = M.lo_alloc([NT, 16], F32); tfM = M.lo_alloc([NT, 16], F32); tiM = M.lo_alloc([NT, 16], I32); tmM = M.lo_alloc([NT, 16], F32)
        S.dma("sp", posi, pos_pk, w=["posi"])
        S.dma("sp", posCi[0:127], posC, w=["posCi"])
        S.dma("sp", inv16, inv16_d, w=["inv16"])
        S.dma("sp", inv8, inv8_d, w=["inv8"])
        S.op("dve", lambda e: e.tensor_copy(out=posf, in_=posi), r=["posi"], w=["posf"])
        S.op("dve", lambda e: e.tensor_copy(out=posCf[0:127], in_=posCi[0:127]), r=["posCi"], w=["posCf"])
        S.op("dve", lambda e: e.tensor_tensor(out=angM, in0=posf.unsqueeze(2).to_broadcast([128, NT, 16]),
                                              in1=inv16.unsqueeze(1).to_broadcast([128, NT, 16]), op=ALU.mult), r=["posf", "inv16"], w=["Mang"])
        sincos(angM, None, cosM, sinM, 128, tfM, tiM, tmM, "M")
        a8 = angM.rearrange("p a b -> p (a b)")[:, 0:NT * 8].rearrange("p (a b) -> p a b", b=8)
        f8 = tfM.rearrange("p a b -> p (a b)")[:, 0:NT * 8].rearrange("p (a b) -> p a b", b=8)
        i8 = tiM.rearrange("p a b -> p (a b)")[:, 0:NT * 8].rearrange("p (a b) -> p a b", b=8)
        m8_ = tmM.rearrange("p a b -> p (a b)")[:, 0:NT * 8].rearrange("p (a b) -> p a b", b=8)
        S.op("dve", lambda e: e.tensor_tensor(out=a8, in0=posf.unsqueeze(2).to_broadcast([128, NT, 8]),
                                              in1=inv8.unsqueeze(1).to_broadcast([128, NT, 8]), op=ALU.mult), r=["posf", "inv8", "Mout", "Mf", "Mm", "Mi"], w=["Nang"])
        sincos(a8, None, cosN, sinN, 128, f8, i8, m8_, "N")
        aC = angM.rearrange("p a b -> p (a b)")[0:127, 0:8]
        fC = tfM.rearrange("p a b -> p (a b)")[0:127, 0:8]
        iC = tiM.rearrange("p a b -> p (a b)")[0:127, 0:8]
        mC = tmM.rearrange("p a b -> p (a b)")[0:127, 0:8]
        S.op("dve", lambda e: e.tensor_scalar(out=aC, in0=inv8[0:127], scalar1=posCf[0:127, 0:1], scalar2=None, op0=ALU.mult),
             r=["posCf", "inv8", "Nout", "Nf", "Nm", "Ni", "Nang"], w=["Cang"])
        sincos(aC, None, cosC[0:127], sinC[0:127], 127, fC, iC, mC, "C")
        dbg_out("cosM", cosM, ["Mout"]); dbg_out("sinM", sinM, ["Mout"])

        if upto <= -1:
            return finish(nc, S, st)
        cpk = M.lo_alloc([8], F32); sc = M.lo_alloc([8], F32)
        sch = M.lo_alloc([8], BF16); scl = M.lo_alloc([8], BF16)
        cBh = M.lo_alloc([8, 128], BF16); cBl = M.lo_alloc([8, 128], BF16)
        modB = M.lo_alloc([6 * D], F32)
        g1t = M.lo_alloc([D], F32); g2t = M.lo_alloc([D], F32)
        awb = [M.hi_alloc([8, 512], F32) for _ in range(3)]
        abb = [M.hi_alloc([512], F32) for _ in range(3)]
        awh = [M.hi_alloc([8, 512], BF16) for _ in range(3)]
        awl = [M.hi_alloc([8, 512], BF16) for _ in range(3)]
        S.dma("sp", cpk, c_pk, w=["cpk"])
        S.dma("sp", g1t, g1B, w=["g1t"]); S.dma("sp", g2t, g2B, w=["g2t"])
        act(sc, cpk, AF.Silu, r=["cpk"], w=["sc"])
        S.op("dve", lambda e: e.tensor_copy(out=sch, in_=sc), r=["sc"], w=["sch"])
        S.op("dve", lambda e: e.tensor_tensor(out=scl, in0=sc, in1=sch, op=ALU.subtract), r=["sc", "sch"], w=["scl"])
        for k in range(8):
            S.op("dve", lambda e, k=k: e.tensor_copy(out=cBh[:, k, :], in_=sch[:, k:k + 1].to_broadcast([128, 128])), r=["sch"], w=["cBh"])
            S.op("dve", lambda e, k=k: e.tensor_copy(out=cBl[:, k, :], in_=scl[:, k:k + 1].to_broadcast([128, 128])), r=["scl"], w=["cBl"])
        awv = ada_w.rearrange("(k p) n -> p k n", p=128)
        for n in range(12):
            q_ = n % 3
            S.dma("sp", awb[q_], awv[:, :, n * 512:(n + 1) * 512], w=[("awb", q_)])
            S.dma("sp", abb[q_], adabB[:, n * 512:(n + 1) * 512], w=[("abb", q_)])
            act(awh[q_], awb[q_], AF.Copy, r=[("awb", q_)], w=[("awh", q_)])
            S.op("dve", lambda e, q_=q_: e.tensor_tensor(out=awl[q_], in0=awb[q_], in1=awh[q_], op=ALU.subtract), r=[("awb", q_), ("awh", q_)], w=[("awl", q_)])
            b = nb()
            passes = [(cBh, "cBh", awh, "awh"), (cBh, "cBh", awl, "awl"), (cBl, "cBl", awh, "awh")]
            for pi_, (cb_, ck, ww, wk) in enumerate(passes):
                for k in range(8):
                    mm(P[b][:, :], cb_[:, k, :], ww[q_][:, k, :], pi_ == 0 and k == 0, pi_ == 2 and k == 7, r=[ck, (wk, q_)], w=[PK[b]])
            S.op("dve", lambda e, n=n, b=b, q_=q_: e.tensor_tensor(out=modB[:, n * 512:(n + 1) * 512], in0=P[b][:, :], in1=abb[q_], op=ALU.add),
                 r=[PK[b], ("abb", q_)], w=["modB"])
        S.op("dve", lambda e: e.scalar_tensor_tensor(out=modB[:, D:2 * D], in0=modB[:, D:2 * D], scalar=1.0, in1=g1t, op0=ALU.add, op1=ALU.mult), r=["modB", "g1t"], w=["modB"])
        S.op("dve", lambda e: e.scalar_tensor_tensor(out=modB[:, 4 * D:5 * D], in0=modB[:, 4 * D:5 * D], scalar=1.0, in1=g2t, op0=ALU.add, op1=ALU.mult), r=["modB", "g2t"], w=["modB"])
        S.dma("sp", mods, modB, r=["modB"], w=["mods"])
        dbg_out("modB", modB, ["modB"])
        B1 = modB[:, 0:D]; A1 = modB[:, D:2 * D]
        if upto <= 0:
            return finish(nc, S, st)

        M.hi = hi0
        hT = M.hi_alloc([8, S_], BF16)
        xb = [M.lo_alloc([D], F32) for _ in range(2)]
        junk = M.lo_alloc([D], F32); tmpA = M.lo_alloc([D], F32)
        hb = [M.lo_alloc([D], BF16) for _ in range(2)]
        ssq = M.lo_alloc([NT], F32); rs = M.lo_alloc([NT], F32)

        tmpA2 = [tmpA, M.lo_alloc([D], F32)]

        def norm_a(i, xt, xkey):
            act(junk, xt, AF.Square, r=[xkey], w=["junk", ("ssq", i)], accum_out=ssq[:, i:i + 1])
            act(rs[:, i:i + 1], ssq[:, i:i + 1], AF.Sqrt, r=[("ssq", i)], w=[("rs", i)], scale=1.0 / D, bias=EPS)
            S.op("dve", lambda e: e.reciprocal(out=rs[:, i:i + 1], in_=rs[:, i:i + 1]), r=[("rs", i)], w=[("rs", i)])

        def norm_b1(i, xt, xkey, A, B, Akeys):
            par = i % 2
            S.op("dve", lambda e: e.scalar_tensor_tensor(out=tmpA2[par], in0=xt, scalar=rs[:, i:i + 1], in1=A, op0=ALU.mult, op1=ALU.mult),
                 r=[xkey, ("rs", i)] + Akeys, w=[("tmpA2", par)])
            S.op("pool", lambda e: e.tensor_tensor(out=hb[par], in0=tmpA2[par], in1=B, op=ALU.add), r=[("tmpA2", par)] + Akeys, w=[("hb", par)])
            b = nb()
            pb = P[b][:, :].bitcast(BF16)
            for k in range(8):
                tp(pb[:, k * 128:(k + 1) * 128], hb[par][:, k * 128:(k + 1) * 128], ident, r=[("hb", par), "ident"], w=[PK[b]])
            return b

        def norm_b2(i, b, dstT, dkey):
            pb = P[b][:, :].bitcast(BF16)
            act(dstT[:, :, i * 128:(i + 1) * 128], pb.rearrange("p (k t) -> p k t", k=8), AF.Copy, r=[PK[b]], w=[(dkey, i)])

        xb = xb + [M.lo_alloc([D], F32), M.lo_alloc([D], F32)]

        def a_front(i):
            S.dma("sp", xb[i % 4], x[i * 128:(i + 1) * 128, :], w=[("xb", i % 4)])
            norm_a(i, xb[i % 4], ("xb", i % 4))

        a_front(0); a_front(1)
        for i in range(NT):
            b_ = norm_b1(i, xb[i % 4], ("xb", i % 4), A1, B1, ["modB"])
            if i + 2 < NT:
                a_front(i + 2)
            norm_b2(i, b_, hT, "hT")
        hTk = [("hT", i) for i in range(NT)]
        S.dma("sp", hTs, hT.rearrange("p k t -> p (k t)"), r=hTk, w=["hTs"])
        dbg_out("hT", hT.rearrange("p k t -> p (k t)"), hTk)
        if upto <= 1:
            return finish(nc, S, st)
        S.barrier()

        nbanks[0] = 6
        M.lo = lo0
        cqT = M.lo_alloc([6, S_], BF16); ckvT = M.lo_alloc([2, S_], BF16); kpe = M.lo_alloc([NT, 32], F32)
        lo1 = M.lo
        Wcq = M.lo_alloc([8, 768], BF16); Wckv = M.lo_alloc([8, 256], BF16); Wkpe = M.lo_alloc([8, 32], BF16)
        qagt = M.lo_alloc([6], F32); kvagt = M.lo_alloc([2], F32)
        sqb = [M.lo_alloc([512], BF16) for _ in range(2)]
        rb = M.lo_alloc([512], F32)
        S.dma("sp", qagt, qag, w=["qagt"]); S.dma("sp", kvagt, kvag, w=["kvagt"])
        load_w(Wcq, w_cq, "Wcq", 8, 768); load_w(Wckv, w_ckv, "Wckv", 8, 256); load_w(Wkpe, w_kpe, "Wkpe", 8, 32)

        def fm_proj_norm(dstT, dkey, Wt, wkey, nf, gaint, gkey, nfeat):
            for c in range(4):
                hk = [("hT", 4 * c + q) for q in range(4)]
                for j in range(nf + 1):
                    if j < nf:
                        b = nb()
                        for k in range(8):
                            mm(P[b][:, :], Wt[:, k, j * 128:(j + 1) * 128], hT[:, k, c * 512:(c + 1) * 512], k == 0, k == 7, r=[wkey] + hk, w=[PK[b]])
                    if j >= 1:
                        jj = j - 1
                        mm(P[6][:, :], onesb, sqb[jj % 2], jj == 0, jj == nf - 1, r=["onesb", ("sqb", jj % 2)], w=[PK[6]], sig=True)
                    if j < nf:
                        act(sqb[j % 2], P[b][:, :], AF.Square, r=[PK[b]], w=[("sqb", j % 2)])
                        act(dstT[:, j, c * 512:(c + 1) * 512], P[b][:, :], AF.Copy, r=[PK[b]], w=[(dkey, c)])
                act(rb, P[6][:, :], AF.Sqrt, r=[PK[6]], w=["rb"], scale=1.0 / nfeat, bias=EPS)
                S.op("dve", lambda e: e.reciprocal(out=rb, in_=rb), r=["rb"], w=["rb"])
                for j in range(nf):
                    S.op("dve", lambda e, j=j, c=c: e.scalar_tensor_tensor(out=dstT[:, j, c * 512:(c + 1) * 512], in0=dstT[:, j, c * 512:(c + 1) * 512],
                                                                             scalar=gaint[:, j:j + 1], in1=rb, op0=ALU.mult, op1=ALU.mult),
                         r=[(dkey, c), "rb", gkey], w=[(dkey, c)])

        fm_proj_norm(cqT, "cqT", Wcq, "Wcq", 6, qagt, "qagt", 768)
        fm_proj_norm(ckvT, "ckvT", Wckv, "Wckv", 2, kvagt, "kvagt", 256)
        for i in range(NT):
            b = nb()
            for k in range(8):
                mm(P[b][:, 0:32], hT[:, k, i * 128:(i + 1) * 128], Wkpe[:, k, :], k == 0, k == 7, r=["Wkpe", ("hT", i)], w=[PK[b]])
            S.op("dve", lambda e, i=i, b=b: e.tensor_copy(out=kpe[:, i, :], in_=P[b][:, 0:32]), r=[PK[b]], w=[("kpe", i)])
        cqk = [("cqT", c) for c in range(4)]
        dbg_out("cqT", cqT.rearrange("p k t -> p (k t)"), cqk)
        dbg_out("kpe", kpe.rearrange("p a b -> p (a b)"), [("kpe", i) for i in range(NT)])
        if upto <= 2:
            return finish(nc, S, st)
        S.barrier()

        nbanks[0] = 8
        M.lo = lo1
        M.hi = hi0
        QT = M.hi_alloc([8, S_], BF16); KT = M.hi_alloc([8, S_], BF16); V = M.hi_alloc([NT, 8, 65], BF16)
        hi1 = M.hi
        Wqb = M.lo_alloc([6, 768], BF16); Wkvb = M.lo_alloc([2, 1024], BF16)
        qgt = M.lo_alloc([96], F32); kgt = M.lo_alloc([96], F32)
        drq = [M.lo_alloc([768], BF16) for _ in range(2)]; drk = [M.lo_alloc([768], BF16) for _ in range(2)]
        S.dma("sp", qgt, qgB, w=["qgt"]); S.dma("sp", kgt, kgB, w=["kgt"])
        load_w(Wqb, w_qb, "Wqb", 6, 768); load_w(Wkvb, w_kvb, "Wkvb", 2, 1024)
        S.op("pool", lambda e: e.memset(V[:, :, :, 64:65], 1.0), w=["Vones"])

        def mk_tmps(Mx, n, H, hf):
            return dict(t1=Mx.lo_alloc([n], F32), hs=Mx.lo_alloc([H], F32), hr=Mx.lo_alloc([H], F32),
                        ra=Mx.lo_alloc([H * hf], F32), rb=Mx.lo_alloc([H * hf], F32), ra2=Mx.lo_alloc([H * hf], F32), rb2=Mx.lo_alloc([H * hf], F32))

        def hnr_stages(tag, T, src, skeys, H, Dh, gaint, gkey, ro, hf, cos_, sin_, dst, dkey, np_=128):
            n = H * Dh
            t1v = T["t1"][0:np_, 0:n].rearrange("p (h d) -> p h d", h=H)
            hs = T["hs"][0:np_, 0:H]; hr = T["hr"][0:np_, 0:H]
            x1 = t1v[:, :, ro:ro + hf]; x2 = t1v[:, :, ro + hf:ro + 2 * hf]
            cb = cos_.unsqueeze(1).to_broadcast([np_, H, hf]); sb_ = sin_.unsqueeze(1).to_broadcast([np_, H, hf])
            rv = {k: T[k][0:np_, 0:H * hf].rearrange("p (h d) -> p h d", h=H) for k in ("ra", "rb", "ra2", "rb2")}
            tr = ["Mout", "Nout", "Cout"]
            k_ = lambda nm: (tag, nm)
            st = []
            st.append(lambda: act(t1v, src, AF.Square, r=skeys, w=[k_("t1")]))
            st.append(lambda: S.op("dve", lambda e: e.tensor_reduce(out=hs, in_=t1v, axis=AX.X, op=ALU.add), r=[k_("t1")], w=[k_("hs")]))
            st.append(lambda: act(hr, hs, AF.Sqrt, r=[k_("hs")], w=[k_("hr")], scale=1.0 / Dh, bias=EPS))
            st.append(lambda: S.op("dve", lambda e: e.reciprocal(out=hr, in_=hr), r=[k_("hr")], w=[k_("hr")]))
            st.append(lambda: S.op("dve", lambda e: e.tensor_tensor(out=t1v, in0=src, in1=hr.unsqueeze(2).to_broadcast([np_, H, Dh]), op=ALU.mult), r=skeys + [k_("hr"), k_("hs")], w=[k_("t1")]))
            st.append(lambda: S.op("dve", lambda e: e.tensor_tensor(out=t1v, in0=t1v, in1=gaint[0:np_].unsqueeze(1).to_broadcast([np_, H, Dh]), op=ALU.mult), r=[k_("t1"), gkey], w=[k_("t1")]))
            st.append(lambda: S.op("dve", lambda e: e.tensor_tensor(out=rv["ra"], in0=x1, in1=cb, op=ALU.mult), r=[k_("t1")] + tr, w=[k_("ra")]))
            st.append(lambda: S.op("dve", lambda e: e.tensor_tensor(out=rv["rb"], in0=x2, in1=sb_, op=ALU.mult), r=[k_("t1")] + tr, w=[k_("rb")]))
            st.append(lambda: S.op("dve", lambda e: e.tensor_tensor(out=dst[:, :, ro:ro + hf], in0=rv["ra"], in1=rv["rb"], op=ALU.subtract), r=[k_("ra"), k_("rb")], w=[dkey]))
            st.append(lambda: S.op("dve", lambda e: e.tensor_tensor(out=rv["ra2"], in0=x2, in1=cb, op=ALU.mult), r=[k_("t1")] + tr, w=[k_("ra2")]))
            st.append(lambda: S.op("dve", lambda e: e.tensor_tensor(out=rv["rb2"], in0=x1, in1=sb_, op=ALU.mult), r=[k_("t1")] + tr, w=[k_("rb2")]))
            st.append(lambda: S.op("dve", lambda e: e.tensor_tensor(out=dst[:, :, ro + hf:ro + 2 * hf], in0=rv["ra2"], in1=rv["rb2"], op=ALU.add), r=[k_("ra2"), k_("rb2")], w=[dkey]))

            def copies():
                if ro > 0:
                    S.op("pool", lambda e: e.tensor_copy(out=dst[:, :, 0:ro], in_=t1v[:, :, 0:ro]), r=[k_("t1")], w=[dkey])
                if ro + 2 * hf < Dh:
                    S.op("pool", lambda e: e.tensor_copy(out=dst[:, :, ro + 2 * hf:Dh], in_=t1v[:, :, ro + 2 * hf:Dh]), r=[k_("t1")], w=[dkey])
            st.insert(6, copies)
            return st

        def run_interleaved(chains):
            for s_ in range(max(len(c_) for c_ in chains)):
                for c_ in chains:
                    if s_ < len(c_):
                        c_[s_]()

        def head_norm_rope(src, skeys, H, Dh, gaint, gkey, ro, hf, cos_, sin_, dst, dkey, np_=128):
            n = H * Dh
            t1v = t1[0:np_, 0:n].rearrange("p (h d) -> p h d", h=H)
            t2v = t2[0:np_, 0:n].rearrange("p (h d) -> p h d", h=H)
            hs = hss[0:np_, 0:H]; hr = hrs[0:np_, 0:H]
            act(t1v, src, AF.Square, r=skeys, w=["t1"])
            S.op("dve", lambda e: e.tensor_reduce(out=hs, in_=t1v, axis=AX.X, op=ALU.add), r=["t1"], w=["hss"])
            act(hr, hs, AF.Sqrt, r=["hss"], w=["hrs"], scale=1.0 / Dh, bias=EPS)
            S.op("dve", lambda e: e.reciprocal(out=hr, in_=hr), r=["hrs"], w=["hrs"])
            S.op("dve", lambda e: e.tensor_tensor(out=t2v, in0=src, in1=hr.unsqueeze(2).to_broadcast([np_, H, Dh]), op=ALU.mult), r=skeys + ["hrs"], w=["t2"])
            S.op("dve", lambda e: e.tensor_tensor(out=t1v, in0=t2v, in1=gaint[0:np_].unsqueeze(1).to_broadcast([np_, H, Dh]), op=ALU.mult), r=["t2", gkey], w=["t1"])
            x1 = t1v[:, :, ro:ro + hf]; x2 = t1v[:, :, ro + hf:ro + 2 * hf]
            cb = cos_.unsqueeze(1).to_broadcast([np_, H, hf]); sb_ = sin_.unsqueeze(1).to_broadcast([np_, H, hf])
            rav = ra[0:np_, 0:H * hf].rearrange("p (h d) -> p h d", h=H)
            rbv = rbb[0:np_, 0:H * hf].rearrange("p (h d) -> p h d", h=H)
            tr = ["Mout", "Nout", "Cout"]
            S.op("dve", lambda e: e.tensor_tensor(out=rav, in0=x1, in1=cb, op=ALU.mult), r=["t1"] + tr, w=["ra"])
            S.op("dve", lambda e: e.tensor_tensor(out=rbv, in0=x2, in1=sb_, op=ALU.mult), r=["t1"] + tr, w=["rbb"])
            S.op("dve", lambda e: e.tensor_tensor(out=dst[:, :, ro:ro + hf], in0=rav, in1=rbv, op=ALU.subtract), r=["ra", "rbb"], w=[dkey])
            S.op("dve", lambda e: e.tensor_tensor(out=rav, in0=x2, in1=cb, op=ALU.mult), r=["t1"] + tr, w=["ra"])
            S.op("dve", lambda e: e.tensor_tensor(out=rbv, in0=x1, in1=sb_, op=ALU.mult), r=["t1"] + tr, w=["rbb"])
            S.op("dve", lambda e: e.tensor_tensor(out=dst[:, :, ro + hf:ro + 2 * hf], in0=rav, in1=rbv, op=ALU.add), r=["ra", "rbb"], w=[dkey])
            if ro > 0:
                S.op("pool", lambda e: e.tensor_copy(out=dst[:, :, 0:ro], in_=t1v[:, :, 0:ro]), r=["t1"], w=[dkey])
            if ro + 2 * hf < Dh:
                S.op("pool", lambda e: e.tensor_copy(out=dst[:, :, ro + 2 * hf:Dh], in_=t1v[:, :, ro + 2 * hf:Dh]), r=["t1"], w=[dkey])

        Msub = Mem(big, NB); Msub.lo = lo_pers; Msub.hi = lo0
        rawq = [Msub.lo_alloc([768], F32) for _ in range(2)]; rawk = [Msub.lo_alloc([768], F32), M.lo_alloc([768], F32)]
        Tq = [mk_tmps(Msub, 768, 8, 16) for _ in range(2)]; Tk = [mk_tmps(Msub, 768, 8, 16) for _ in range(2)]

        def b2_front(i):
            ts = slice(i * 128, (i + 1) * 128)
            par = i % 2
            bA, bB = nb(), nb()
            for k in range(6):
                mm(P[bA][:, :], cqT[:, k, ts], Wqb[:, k, 0:512], k == 0, k == 5, r=["Wqb", ("cqT", i // 4)], w=[PK[bA]])
            for k in range(6):
                mm(P[bB][:, 0:256], cqT[:, k, ts], Wqb[:, k, 512:768], k == 0, k == 5, r=["Wqb", ("cqT", i // 4)], w=[PK[bB]])
            act(rawq[par][:, 0:512], P[bA][:, :], AF.Copy, r=[PK[bA]], w=[("rawq", par)])
            act(rawq[par][:, 512:768], P[bB][:, 0:256], AF.Copy, r=[PK[bB]], w=[("rawq", par)])
            bA, bB = nb(), nb()
            for hh, bb in ((0, bA), (1, bB)):
                for k in range(2):
                    mm(P[bb][:, :], ckvT[:, k, ts], Wkvb[:, k, hh * 512:(hh + 1) * 512], k == 0, k == 1, r=["Wkvb", ("ckvT", i // 4)], w=[PK[bb]])
            rv = rawk[par].rearrange("p (h d) -> p h d", h=8)
            for hh, bb in ((0, bA), (1, bB)):
                pv = P[bb][:, :].rearrange("p (h d) -> p h d", h=4)
                act(rv[:, hh * 4:(hh + 1) * 4, 0:64], pv[:, :, 0:64], AF.Copy, r=[PK[bb]], w=[("rawk", par)])
                act(V[:, i, hh * 4:(hh + 1) * 4, 0:64], pv[:, :, 64:128], AF.Copy, r=[PK[bb]], w=[("V", i)])
            S.op("pool", lambda e, i=i: e.tensor_copy(out=rv[:, :, 64:96], in_=kpe[:, i, :].unsqueeze(1).to_broadcast([128, 8, 32])), r=[("kpe", i)], w=[("rawk", par)])

        def b2_chains(i):
            par = i % 2
            dq = drq[par].rearrange("p (h d) -> p h d", h=8); dk = drk[par].rearrange("p (h d) -> p h d", h=8)
            cq_ = hnr_stages(("cq", par), Tq[par], rawq[par].rearrange("p (h d) -> p h d", h=8), [("rawq", par)], 8, 96, qgt, "qgt", 64, 16, cosM[:, i, :], sinM[:, i, :], dq, ("drq", par))
            ck_ = hnr_stages(("ck", par), Tk[par], rawk[par].rearrange("p (h d) -> p h d", h=8), [("rawk", par)], 8, 96, kgt, "kgt", 64, 16, cosM[:, i, :], sinM[:, i, :], dk, ("drk", par))
            return [cq_[:6], ck_[:6]], [cq_[6:], ck_[6:]]

        def b2_out(i):
            ts = slice(i * 128, (i + 1) * 128)
            par = i % 2
            dq = drq[par].rearrange("p (h d) -> p h d", h=8); dk = drk[par].rearrange("p (h d) -> p h d", h=8)
            for (dd, dkey_, dstT, okey) in ((dq, ("drq", par), QT, "QT"), (dk, ("drk", par), KT, "KT")):
                b = nb(); pb = P[b][:, :].bitcast(BF16)
                for h in range(8):
                    tp(pb[0:96, h * 128:(h + 1) * 128], dd[:, h, :], ident, r=[dkey_, "ident"], w=[PK[b]])
                act(dstT[0:96, :, ts], pb[0:96, :].rearrange("p (h t) -> p h t", h=8), AF.Copy, r=[PK[b]], w=[(okey, i)])

        b2_front(0); b2_front(1)
        h1_, h2_ = b2_chains(0)
        run_interleaved(h1_)
        for i in range(NT):
            if i + 2 < NT:
                b2_front(i + 2)
            nxt = b2_chains(i + 1) if i + 1 < NT else ([], [])
            run_interleaved(nxt[0] + h2_)
            h2_ = nxt[1]
            if i >= 1:
                b2_out(i - 1)
        b2_out(NT - 1)
        QTk = [("QT", i) for i in range(NT)]
        dbg_out("QT", QT[0:96].rearrange("p k t -> p (k t)"), QTk)
        dbg_out("KT", KT[0:96].rearrange("p k t -> p (k t)"), [("KT", i) for i in range(NT)])
        dbg_out("V", V.rearrange("p a b c -> p (a b c)"), [("V", i) for i in range(NT)] + ["Vones"])
        if upto <= 3:
            return finish(nc, S, st)
        S.barrier()

        nbanks[0] = 6
        M.lo = lo0
        om = M.lo_alloc([NT, 512], BF16)
        PT = [M.lo_alloc([512], BF16) for _ in range(4)]
        rec4 = [M.lo_alloc([4], F32) for _ in range(2)]
        pt_i = [0]

        gchunk = [0]

        def causal_attn_multi(jobs):
            steps = []
            for ji in range(len(jobs)):
                for c in range(4):
                    for kt in range(4 * c + 4):
                        steps.append((ji, c, kt))

            def emit_qk(step):
                ji, c, kt = step
                J = jobs[ji]
                q0 = max(kt - 4 * c, 0)
                n = 512 - 128 * q0
                b = nb()
                has_extra = J["extra"] is not None
                mm(P[b][:, 0:n], J["KT"][0:J["kn"], kt * 128:(kt + 1) * 128], J["QT"](c * 512 + q0 * 128, (c + 1) * 512),
                   True, not has_extra, r=J["kk"](kt) + J["qk"](c), w=[PK[b]])
                if has_extra:
                    J["extra"](P[b][:, 0:n], kt, c * 512 + q0 * 128, (c + 1) * 512, PK[b])
                return b, n, q0

            LA = 2
            pend = [emit_qk(steps[q]) for q in range(min(LA, len(steps)))]
            for si, (ji, c, kt) in enumerate(steps):
                J = jobs[ji]
                b, n, q0 = pend.pop(0)
                if si + LA < len(steps):
                    pend.append(emit_qk(steps[si + LA]))
                if kt == 0:
                    gchunk[0] += 1
                ab = 6 + (gchunk[0] % 2)
                Oacc = P[ab][:, 0:260].rearrange("p (q d) -> p q d", q=4)
                pi = pt_i[0]; pt_i[0] = (pi + 1) % len(PT)
                pt = PT[pi]
                act(pt[:, 0:n], P[b][:, 0:n], AF.Exp, r=[PK[b]], w=[("PT", pi)], scale=J["scale"])
                if kt >= 4 * c:
                    S.op("dve", lambda e, pt=pt: e.tensor_tensor(out=pt[:, 0:128], in0=pt[:, 0:128], in1=tri, op=ALU.mult), r=[("PT", pi), "tri"], w=[("PT", pi)])
                for qi in range(q0, 4):
                    mm(Oacc[:, qi, :], pt[:, (qi - q0) * 128:(qi - q0 + 1) * 128], J["V"](kt), kt == 0 and qi == 0, kt == 4 * c + qi,
                       r=[("PT", pi)] + J["vk"](kt), w=[PK[ab]], skip_group_check=True)
                if kt == 4 * c + 3:
                    J["fin"](c, Oacc, PK[ab])

        jobs = []
        for h in range(8):
            def fin(c, Oacc, pk, h=h):
                rc = rec4[c % 2]
                S.op("dve", lambda e: e.reciprocal(out=rc, in_=Oacc[:, :, 64]), r=[pk], w=[("rec4", c % 2)])
                S.op("dve", lambda e: e.tensor_tensor(out=om[:, 4 * c:4 * c + 4, h * 64:(h + 1) * 64], in0=Oacc[:, :, 0:64],
                                                      in1=rc.unsqueeze(2).to_broadcast([128, 4, 64]), op=ALU.mult),
                     r=[pk, ("rec4", c % 2)], w=[("om", c)])
            jobs.append(dict(KT=KT[:, h, :], kn=96, QT=(lambda a, b_, h=h: QT[0:96, h, a:b_]), V=(lambda kt, h=h: V[:, kt, h, :]), scale=96 ** -0.5,
                             extra=None, fin=fin, qk=(lambda c: [("QT", 4 * c + q) for q in range(4)]), kk=(lambda kt: [("KT", kt)]),
                             vk=(lambda kt: [("V", kt), "Vones"])))
        causal_attn_multi(jobs)
        for i in range(NT):
            b = nb(); pb = P[b][:, :].bitcast(BF16)
            for j in range(4):
                tp(pb[:, j * 128:(j + 1) * 128], om[:, i, j * 128:(j + 1) * 128], ident, r=[("om", i // 4), "ident"], w=[PK[b]])
            act(omT[:, :, i * 128:(i + 1) * 128], pb[:, 0:512].rearrange("p (k t) -> p k t", k=4), AF.Copy, r=[PK[b]], w=[("omT", i)])
        dbg_out("om", om.rearrange("p a b -> p (a b)"), [("om", c) for c in range(4)])
        if upto <= 4:
            return finish(nc, S, st)
        S.barrier()

        nbanks[0] = 4
        M.lo = lo0; M.hi = hi0
        qnT = M.lo_alloc([8, S_], BF16); ksT = M.lo_alloc([2, S_], BF16); kwT = M.lo_alloc([2, S_], BF16)
        vs = M.lo_alloc([NT, 2, 65], BF16); vw = M.lo_alloc([NT, 2, 65], BF16)
        gates = M.lo_alloc([NT, 3, 8], F32)
        kcmpT = M.lo_alloc([2, 128], BF16); VCX = M.lo_alloc([2, 97], BF16)
        PT = [M.lo_alloc([512], BF16) for _ in range(4)]
        rec4 = [M.lo_alloc([4], F32) for _ in range(2)]
        t1 = M.lo_alloc([512], F32); t2 = M.lo_alloc([512], F32)
        hss = M.lo_alloc([8], F32); hrs = M.lo_alloc([8], F32)
        ra = M.lo_alloc([128], F32); rbb = M.lo_alloc([128], F32)
        drb = [M.lo_alloc([512], BF16) for _ in range(2)]
        nqt = M.lo_alloc([64], F32); nkct = M.lo_alloc([64], F32); nkst = M.lo_alloc([64], F32); nkwt = M.lo_alloc([64], F32)
        lo2 = M.lo
        kc2 = M.hi_alloc([2, S_], BF16); vc2 = M.hi_alloc([2, S_], BF16)
        hi_kv = M.hi
        hT = M.hi_alloc([8, S_], BF16)
        Wqn = M.hi_alloc([8, 512], BF16); Wkc2 = M.hi_alloc([8, 256], BF16); Wvc2 = M.hi_alloc([8, 256], BF16)
        Wkv4 = M.hi_alloc([8, 512], BF16); Wgn = M.hi_alloc([8, 24], BF16)
        ge = M.hi_alloc([24], F32)
        S.dma("sp", hT.rearrange("p k t -> p (k t)"), hTs, r=["hTs"], w=["hTall"])
        for t_, d_ in ((nqt, nqB), (nkct, nkcB), (nkst, nksB), (nkwt, nkwB)):
            S.dma("sp", t_, d_, w=["ngain"])
        load_w(Wqn, w_qn, "Wqn", 8, 512); load_w(Wkv4, w_kv4, "Wkv4", 8, 512); load_w(Wgn, w_gn, "Wgn", 8, 24)
        load_w(Wkc2, w_kc2, "Wkc2", 8, 256); load_w(Wvc2, w_vc2, "Wvc2", 8, 256)
        S.op("pool", lambda e: e.memset(vs[:, :, :, 64:65], 1.0), w=["vsones"])
        S.op("pool", lambda e: e.memset(vw[:, :, :, 64:65], 1.0), w=["vwones"])
        S.op("pool", lambda e: e.memset(kc2[64:128, :, S_ - 1:S_], 0.0), w=["kc2pad"])
        S.op("pool", lambda e: e.memset(vc2[64:128, :, S_ - 1:S_], 0.0), w=["vc2pad"])
        MsubD = Mem(big, NB); MsubD.lo = lo_pers + 16384; MsubD.hi = lo0
        TDq = [mk_tmps(MsubD, 512, 8, 8) for _ in range(2)]; TDs = [mk_tmps(MsubD, 128, 2, 8) for _ in range(2)]; TDw = [mk_tmps(MsubD, 128, 2, 8) for _ in range(2)]
        dnq = [MsubD.lo_alloc([512], BF16) for _ in range(2)]
        dns = [MsubD.lo_alloc([128], BF16) for _ in range(2)]; dnw = [MsubD.lo_alloc([128], BF16) for _ in range(2)]

        def d_front(i):
            ts = slice(i * 128, (i + 1) * 128)
            bq = 4 + 2 * (i % 2)
            for k in range(8):
                mm(P[bq][:, :], hT[:, k, ts], Wqn[:, k, :], k == 0, k == 7, r=["hTall", "Wqn"], w=[PK[bq]])
            bk = 5 + 2 * (i % 2)
            for k in range(8):
                mm(P[bk][:, :], hT[:, k, ts], Wkv4[:, k, :], k == 0, k == 7, r=["hTall", "Wkv4"], w=[PK[bk]])
            bg = nb()
            for k in range(8):
                mm(P[bg][:, 0:24], hT[:, k, ts], Wgn[:, k, :], k == 0, k == 7, r=["hTall", "Wgn"], w=[PK[bg]])
            act(vs[:, i, :, 0:64], P[bk][:, 256:384].rearrange("p (g d) -> p g d", g=2), AF.Copy, r=[PK[bk]], w=[("vs", i)])
            act(vw[:, i, :, 0:64], P[bk][:, 384:512].rearrange("p (g d) -> p g d", g=2), AF.Copy, r=[PK[bk]], w=[("vw", i)])
            act(ge, P[bg][:, 0:24], AF.Exp, r=[PK[bg]], w=["ge"], scale=-1.0)
            S.op("dve", lambda e: e.tensor_scalar(out=ge, in0=ge, scalar1=1.0, scalar2=None, op0=ALU.add), r=["ge"], w=["ge"])
            S.op("dve", lambda e, i=i: e.reciprocal(out=gates[:, i].rearrange("p a b -> p (a b)"), in_=ge), r=["ge"], w=[("gates", i)])
            return bq, bk

        def d_chains(i, bq, bk):
            par = i % 2
            dq = dnq[par].rearrange("p (h d) -> p h d", h=8)
            ds_ = dns[par].rearrange("p (h d) -> p h d", h=2); dw_ = dnw[par].rearrange("p (h d) -> p h d", h=2)
            c1 = hnr_stages(("dq", par), TDq[par], P[bq][:, :].rearrange("p (h d) -> p h d", h=8), [PK[bq]], 8, 64, nqt, "ngain", 0, 8, cosN[:, i, :], sinN[:, i, :], dq, ("dnq", par))
            c2 = hnr_stages(("ds", par), TDs[par], P[bk][:, 0:128].rearrange("p (h d) -> p h d", h=2), [PK[bk]], 2, 64, nkst, "ngain", 0, 8, cosN[:, i, :], sinN[:, i, :], ds_, ("dns", par))
            c3 = hnr_stages(("dw", par), TDw[par], P[bk][:, 128:256].rearrange("p (h d) -> p h d", h=2), [PK[bk]], 2, 64, nkwt, "ngain", 0, 8, cosN[:, i, :], sinN[:, i, :], dw_, ("dnw", par))
            return [c1[:6], c2[:6], c3[:6]], [c1[6:], c2[6:], c3[6:]]

        def d_out(i):
            ts = slice(i * 128, (i + 1) * 128)
            par = i % 2
            b = nb(); pb = P[b][:, :].bitcast(BF16)
            for p_ in range(8):
                tp(pb[0:64, p_ * 128:(p_ + 1) * 128], dnq[par][:, p_ * 64:(p_ + 1) * 64], ident, r=[("dnq", par), "ident"], w=[PK[b]])
            act(qnT[0:64, :, ts], pb[0:64, :].rearrange("p (k t) -> p k t", k=8), AF.Copy, r=[PK[b]], w=[("qnT", i)])
            for (dd, dkey_, dstT, dk) in ((dns[par], ("dns", par), ksT, "ksT"), (dnw[par], ("dnw", par), kwT, "kwT")):
                b2 = nb(); pb = P[b2][:, :].bitcast(BF16)
                for g_ in range(2):
                    tp(pb[0:64, g_ * 128:(g_ + 1) * 128], dd[:, g_ * 64:(g_ + 1) * 64], ident, r=[dkey_, "ident"], w=[PK[b2]])
                act(dstT[0:64, :, ts], pb[0:64, 0:256].rearrange("p (g t) -> p g t", g=2), AF.Copy, r=[PK[b2]], w=[(dk, i)])

        fr_ = {0: d_front(0), 1: d_front(1)}
        h1_, h2_ = d_chains(0, *fr_[0])
        run_interleaved(h1_)
        for i in range(NT):
            if i + 2 < NT:
                fr_[i + 2] = d_front(i + 2)
            nxt = d_chains(i + 1, *fr_[i + 1]) if i + 1 < NT else ([], [])
            run_interleaved(nxt[0] + h2_)
            h2_ = nxt[1]
            if i >= 1:
                d_out(i - 1)
        d_out(NT - 1)
        for c in range(4):
            for (Wt, wk, dst, dk) in ((Wkc2, "Wkc2", kc2, "kc2"), (Wvc2, "Wvc2", vc2, "vc2")):
                for g in range(2):
                    b = nb()
                    for k in range(8):
                        mm(P[b][:, :], Wt[:, k, g * 128:(g + 1) * 128], hT[:, k, c * 512:(c + 1) * 512], k == 0, k == 7, r=["hTall", wk], w=[PK[b]])
                    act(dst[0:64, g, c * 512:(c + 1) * 512], P[b][0:64, :], AF.Copy, r=[PK[b]], w=[dk])
                    if c == 0:
                        act(dst[64:128, g, 0:511], P[b][64:128, 1:512], AF.Copy, r=[PK[b]], w=[dk])
                    else:
                        act(dst[64:128, g, c * 512 - 1:(c + 1) * 512 - 1], P[b][64:128, :], AF.Copy, r=[PK[b]], w=[dk])
        dbg_out("qnT", qnT[0:64].rearrange("p k t -> p (k t)"), [("qnT", i) for i in range(NT)])
        dbg_out("ksT", ksT[0:64].rearrange("p k t -> p (k t)"), [("ksT", i) for i in range(NT)])
        dbg_out("gates", gates.rearrange("p a b c -> p (a b c)"), [("gates", i) for i in range(NT)])
        dbg_out("kc2", kc2.rearrange("p k t -> p (k t)"), ["kc2", "kc2pad"])
        if upto <= 5:
            return finish(nc, S, st)
        S.barrier()

        nbanks[0] = 8
        M.hi = hi_kv
        hiE = M.hi
        W1k = M.lo_alloc([16, 256], BF16); W1v = M.lo_alloc([16, 256], BF16)
        W2k = M.lo_alloc([2, 64], BF16); W2v = M.lo_alloc([2, 64], BF16)
        pkf = M.lo_alloc([16], F32); pvf = M.lo_alloc([16], F32); pkb = M.lo_alloc([16], BF16); pvb = M.lo_alloc([16], BF16)
        biask = M.lo_alloc([2], F32); biasv = M.lo_alloc([2], F32)
        hid = [M.lo_alloc([128], BF16) for _ in range(2)]
        ovl = M.lo_alloc([32], BF16)
        load_w(W1k, w1k, "W1k", 16, 256); load_w(W1v, w1v, "W1v", 16, 256)
        load_w(W2k, w2k, "W2k", 2, 64); load_w(W2v, w2v, "W2v", 2, 64)
        S.dma("sp", pkf, posk, w=["pkf"]); S.dma("sp", pvf, posv, w=["pvf"]); S.dma("sp", ovl[0:127], ovl_d, w=["ovl"])
        S.op("pool", lambda e: e.memset(VCX[0:127, :, 64:65], 1.0), w=["VCXa"])
        for g in range(2):
            S.op("pool", lambda e, g=g: e.tensor_copy(out=VCX[0:127, g, 65:97], in_=ovl[0:127]), r=["ovl"], w=["VCXb"])
        rt = [M.lo_alloc([128], BF16) for _ in range(3)]
        rt_i = [0]
        for (W1, w1key, W2, w2key, src, skey, posf_, pkey, isk) in ((W1k, "W1k", W2k, "W2k", kc2, ["kc2", "kc2pad"], pkf, "pkf", True),
                                                                    (W1v, "W1v", W2v, "W2v", vc2, ["vc2", "vc2pad"], pvf, "pvf", False)):
            srcv = src.rearrange("p g (n s) -> p g n s", s=16)
            bo = nb()
            for g in range(2):
                bh = []
                for hc in range(2):
                    b = nb()
                    while b == bo or b in bh:
                        b = nb()
                    bh.append(b)
                for lc in range(16):
                    ri = rt_i[0]; rt_i[0] = (ri + 1) % 3
                    rtv = rt[ri][:, 0:127]
                    S.op("dve", lambda e, rtv=rtv, g=g, lc=lc, srcv=srcv, posf_=posf_: e.tensor_scalar(
                        out=rtv, in0=srcv[:, g, (2 * lc) // 16:(2 * lc) // 16 + 127, (2 * lc) % 16], scalar1=posf_[:, lc:lc + 1], scalar2=None, op0=ALU.add),
                        r=skey + [pkey], w=[("rt", ri)])
                    for hc in range(2):
                        mm(P[bh[hc]][:, 0:127], W1[:, lc, hc * 128:(hc + 1) * 128], rtv, lc == 0, lc == 15, r=[w1key, ("rt", ri)], w=[PK[bh[hc]]], sig=(hc == 1 or lc == 15))
                for hc in range(2):
                    act(hid[hc][:, 0:127], P[bh[hc]][:, 0:127], AF.Silu, r=[PK[bh[hc]]], w=[("hid", hc)])
                for hc in range(2):
                    mm(P[bo][0:127, g * 64:(g + 1) * 64], hid[hc][:, 0:127], W2[:, hc, :], hc == 0, hc == 1, r=[("hid", hc), w2key], w=[PK[bo]])
            if isk:
                d = drb[1][0:127, 0:128].rearrange("p (h d) -> p h d", h=2)
                head_norm_rope(P[bo][0:127, 0:128].rearrange("p (h d) -> p h d", h=2), [PK[bo]], 2, 64, nkct, "ngain", 0, 8, cosC[0:127], sinC[0:127], d, "drb1", np_=127)
                b2 = nb(); pb = P[b2][:, :].bitcast(BF16)
                for g_ in range(2):
                    tp(pb[0:64, g_ * 128:g_ * 128 + 127], drb[1][0:127, g_ * 64:(g_ + 1) * 64], ident[0:127, 0:127], r=["drb1", "ident"], w=[PK[b2]])
                act(kcmpT[0:64, :, 0:127], pb[0:64, 0:256].rearrange("p (g t) -> p g t", g=2)[:, :, 0:127], AF.Copy, r=[PK[b2]], w=["kcmpT"])
            else:
                act(VCX[0:127, :, 0:64], P[bo][0:127, 0:128].rearrange("p (g d) -> p g d", g=2), AF.Copy, r=[PK[bo]], w=["VCXc"])
        dbg_out("kcmpT", kcmpT[0:64].rearrange("p g t -> p (g t)"), ["kcmpT"])
        dbg_out("VCX", VCX[0:127].rearrange("p a b -> p (a b)"), ["VCXa", "VCXb", "VCXc"])
        if upto <= 6:
            return finish(nc, S, st)
        S.barrier()

        M.lo = lo2; M.hi = hi0
        onsa = M.hi_alloc([NT, 512], F32)
        mbT = M.hi_alloc([2, S_], BF16)
        vmT = M.hi_alloc([S_], BF16); XE = M.hi_alloc([NT, 128], BF16)
        fb = M.hi_alloc([NT, 32], F32); vj = M.hi_alloc([NT, 32], F32)
        pc = [M.lo_alloc([4, 128], BF16) for _ in range(2)]
        rsum = M.lo_alloc([8], F32); rec8 = M.lo_alloc([8], F32); gr = M.lo_alloc([8], F32)
        tmp_i = M.lo_alloc([8, 32], F32); imp = M.lo_alloc([2, 32], F32); m8 = M.lo_alloc([2, 8], F32)
        sel = M.lo_alloc([2, 32], F32); mbf = M.lo_alloc([2, 32], BF16)
        tmpo = M.lo_alloc([8, 64], F32)
        pw = [M.lo_alloc([3, 128], BF16) for _ in range(4)]
        onb = [M.lo_alloc([512], BF16) for _ in range(2)]
        S.dma("sp", vmT[0:127], vmT_d, w=["vmT"]); S.dma("sp", XE[0:32].rearrange("p a b -> p (a b)"), XE_d, w=["XE"])
        S.dma("sp", fb.rearrange("p a b -> p (a b)"), fb_d, w=["fb"]); S.dma("sp", vj.rearrange("p a b -> p (a b)"), vj_d, w=["vj"])
        VCXk = ["VCXa", "VCXb", "VCXc"]
        for i in range(NT):
            ts = slice(i * 128, (i + 1) * 128)
            import os
            if int(os.environ.get('KDEV_F', '9')) <= 0:
                continue
            sb_ = [nb(), nb()]
            ob = [nb(), nb()]
            for p in range(8):
                j, g = p // 2, p % 2
                if os.environ.get('KDEV_G0'):
                    g = 0
                mm(P[sb_[p // 4]][0:127, (p % 4) * 128:(p % 4 + 1) * 128], kcmpT[0:64, g, 0:127], qnT[0:64, p, ts], True, True,
                   r=["kcmpT", ("qnT", i)], w=[PK[sb_[p // 4]]])
            for hf_ in range(2):
                pcv = pc[hf_]
                act(pcv[0:127], P[sb_[hf_]][0:127, :].rearrange("p (a b) -> p a b", a=4), AF.Exp, r=[PK[sb_[hf_]]], w=[("pc", hf_)], scale=0.125)
                S.op("dve", lambda e, pcv=pcv: e.tensor_tensor(out=pcv[0:127], in0=pcv[0:127], in1=vmT[0:127, ts].unsqueeze(1).to_broadcast([127, 4, 128]), op=ALU.mult),
                     r=[("pc", hf_), "vmT"], w=[("pc", hf_)])
            import os
            FL = int(os.environ.get('KDEV_F', '9'))
            if FL <= 1:
                continue
            for p in range(8):
                g = p % 2
                mm(P[ob[p // 4]][:, (p % 4) * 97:(p % 4 + 1) * 97], pc[p // 4][0:127, p % 4, :], VCX[0:127, g, :], True, True,
                   r=[("pc", p // 4)] + VCXk, w=[PK[ob[p // 4]]])
            if FL <= 2:
                continue
            OC = [P[ob[h_]][:, 0:388].rearrange("p (a b) -> p a b", a=4) for h_ in range(2)]
            for h_ in range(2):
                S.op("dve", lambda e, h_=h_: e.tensor_scalar(out=rsum[:, h_ * 4:(h_ + 1) * 4], in0=OC[h_][:, :, 64], scalar1=1e-30, scalar2=None, op0=ALU.max), r=[PK[ob[h_]]], w=["rsum"])
            S.op("dve", lambda e: e.reciprocal(out=rec8, in_=rsum), r=["rsum"], w=["rec8"])
            S.op("dve", lambda e, i=i: e.tensor_tensor(out=gr, in0=gates[:, i, 0, :], in1=rec8, op=ALU.mult), r=["rec8", ("gates", i)], w=["gr"])
            for h_ in range(2):
                S.op("dve", lambda e, h_=h_, i=i: e.tensor_tensor(out=onsa[:, i, h_ * 256:(h_ + 1) * 256].rearrange("p (a b) -> p a b", a=4), in0=OC[h_][:, :, 0:64],
                                                                   in1=gr[:, h_ * 4:(h_ + 1) * 4].unsqueeze(2).to_broadcast([128, 4, 64]), op=ALU.mult),
                     r=[PK[ob[h_]], "gr"], w=[("onsa", i)])
                S.op("dve", lambda e, h_=h_: e.tensor_tensor(out=tmp_i[:, h_ * 4:(h_ + 1) * 4, :], in0=OC[h_][:, :, 65:97],
                                                             in1=rec8[:, h_ * 4:(h_ + 1) * 4].unsqueeze(2).to_broadcast([128, 4, 32]), op=ALU.mult),
                     r=[PK[ob[h_]], "rec8"], w=["tmp_i"])
            if FL <= 3:
                continue
            S.op("dve", lambda e: e.tensor_reduce(out=imp, in_=tmp_i.rearrange("t (j g) n -> t g n j", g=2), axis=AX.X, op=ALU.add), r=["tmp_i"], w=["imp"])
            S.op("dve", lambda e, i=i: e.tensor_tensor(out=imp, in0=imp, in1=fb[:, i, :].unsqueeze(1).to_broadcast([128, 2, 32]), op=ALU.add), r=["imp", "fb"], w=["imp"])
            if FL <= 4:
                continue
            for g in range(2):
                S.op("dve", lambda e, g=g: e.max(out=m8[:, g, :], in_=imp[:, g, :]), r=["imp"], w=["m8"])
                S.op("dve", lambda e, g=g: e.tensor_scalar(out=sel[:, g, :], in0=imp[:, g, :], scalar1=m8[:, g, 7:8], scalar2=None, op0=ALU.is_ge), r=["imp", "m8"], w=["sel"])
            S.op("dve", lambda e, i=i: e.tensor_tensor(out=sel, in0=sel, in1=vj[:, i, :].unsqueeze(1).to_broadcast([128, 2, 32]), op=ALU.mult), r=["sel", "vj"], w=["sel"])
            S.op("dve", lambda e: e.tensor_scalar(out=mbf, in0=sel, scalar1=-1.0, scalar2=30000.0, op0=ALU.add, op1=ALU.mult), r=["sel"], w=["mbf"])
            if i == 5:
                dbg_out("sel5", sel.rearrange("p a b -> p (a b)"), ["sel"])
                dbg_out("imp5", imp.rearrange("p a b -> p (a b)"), ["imp"])
            if FL <= 5:
                continue
            b = nb(); pb = P[b][:, :].bitcast(BF16)
            for g in range(2):
                tp(pb[0:32, g * 128:(g + 1) * 128], mbf[:, g, :], ident, r=["mbf", "ident"], w=[PK[b]])
            act(mbT[0:32, :, ts], pb[0:32, 0:256].rearrange("p (g t) -> p g t", g=2), AF.Copy, r=[PK[b]], w=[("mbT", i)])
        dbg_out("onsa_c", onsa.rearrange("p a b -> p (a b)"), [("onsa", i) for i in range(NT)])
        dbg_out("mbT", mbT[0:32].rearrange("p a b -> p (a b)"), [("mbT", i) for i in range(NT)])
        if upto <= 7:
            return finish(nc, S, st)

        S.barrier()
        nbanks[0] = 6
        jobs = []
        for p in range(8):
            j, g = p // 2, p % 2

            def extra(ps_ap, kt, a, b_, pk, g=g):
                mm(ps_ap, XE[0:32, kt, :], mbT[0:32, g, a:b_], False, True, r=["XE"] + [("mbT", q) for q in range(a // 128, b_ // 128)], w=[pk])

            def fin(c, Oacc, pk, p=p):
                rc = rec4[c % 2]
                S.op("dve", lambda e: e.reciprocal(out=rc, in_=Oacc[:, :, 64]), r=[pk], w=[("rec4", c % 2)])
                S.op("dve", lambda e: e.tensor_tensor(out=rc, in0=rc, in1=gates[:, 4 * c:4 * c + 4, 1, p], op=ALU.mult), r=[("rec4", c % 2)] + [("gates", 4 * c + q) for q in range(4)], w=[("rec4", c % 2)])
                tv = tmpo[:, 0:4, :]
                S.op("dve", lambda e: e.tensor_tensor(out=tv, in0=Oacc[:, :, 0:64], in1=rc.unsqueeze(2).to_broadcast([128, 4, 64]), op=ALU.mult), r=[pk, ("rec4", c % 2)], w=["tmpo"])
                S.op("pool", lambda e: e.tensor_tensor(out=onsa[:, 4 * c:4 * c + 4, p * 64:(p + 1) * 64], in0=onsa[:, 4 * c:4 * c + 4, p * 64:(p + 1) * 64], in1=tv, op=ALU.add),
                     r=["tmpo"] + [("onsa", 4 * c + q) for q in range(4)], w=[("onsa", 4 * c + q) for q in range(4)])
            jobs.append(dict(KT=ksT[:, g, :], kn=64, QT=(lambda a, b_, p=p: qnT[0:64, p, a:b_]), V=(lambda kt, g=g: vs[:, kt, g, :]), scale=0.125,
                             extra=extra, fin=fin, qk=(lambda c: [("qnT", 4 * c + q) for q in range(4)]), kk=(lambda kt: [("ksT", kt)]),
                             vk=(lambda kt: [("vs", kt), "vsones"])))
        causal_attn_multi(jobs)
        dbg_out("onsa_cs", onsa.rearrange("p a b -> p (a b)"), [("onsa", i) for i in range(NT)])
        if upto <= 8:
            return finish(nc, S, st)

        pw_i = [0]
        ob = [6, 7]

        def emit_ws(i, p):
            g = p % 2
            kts = [kt for kt in (i - 2, i - 1, i) if kt >= 0]
            b = nb()
            for kt in kts:
                sl = kt - (i - 2)
                mm(P[b][:, sl * 128:(sl + 1) * 128], kwT[0:64, g, kt * 128:(kt + 1) * 128], qnT[0:64, p, i * 128:(i + 1) * 128], True, True,
                   r=[("kwT", kt), ("qnT", i)], w=[PK[b]])
            return b

        wsteps = [(i, p) for i in range(NT) for p in range(8)]
        WLA = 3
        wpend = [emit_ws(*wsteps[q]) for q in range(WLA)]
        for wi_, (i, p) in enumerate(wsteps):
            ts = slice(i * 128, (i + 1) * 128)
            g = p % 2
            kts = [kt for kt in (i - 2, i - 1, i) if kt >= 0]
            b = wpend.pop(0)
            if wi_ + WLA < len(wsteps):
                wpend.append(emit_ws(*wsteps[wi_ + WLA]))
            s0 = kts[0] - (i - 2)
            wi = pw_i[0]; pw_i[0] = (wi + 1) % len(pw)
            pwv = pw[wi]
            act(pwv[:, s0:3, :], P[b][:, s0 * 128:384].rearrange("p (a b) -> p a b", b=128), AF.Exp, r=[PK[b]], w=[("pw", wi)], scale=0.125)
            S.op("dve", lambda e, pwv=pwv, s0=s0: e.tensor_tensor(out=pwv[:, s0:3, :], in0=pwv[:, s0:3, :], in1=winm[:, s0:3, :], op=ALU.mult), r=[("pw", wi), "winm"], w=[("pw", wi)])
            for kt in kts:
                sl = kt - (i - 2)
                mm(P[ob[p // 4]][:, (p % 4) * 65:(p % 4 + 1) * 65], pwv[:, sl, :], vw[:, kt, g, :], kt == kts[0], kt == kts[-1],
                   r=[("pw", wi), ("vw", kt), "vwones"], w=[PK[ob[p // 4]]])
            if p != 7:
                continue
            OW = [P[ob[h_]][:, 0:260].rearrange("p (a b) -> p a b", a=4) for h_ in range(2)]
            for h_ in range(2):
                S.op("dve", lambda e, h_=h_: e.reciprocal(out=rec8[:, h_ * 4:(h_ + 1) * 4], in_=OW[h_][:, :, 64]), r=[PK[ob[h_]]], w=["rec8"])
            S.op("dve", lambda e, i=i: e.tensor_tensor(out=gr, in0=gates[:, i, 2, :], in1=rec8, op=ALU.mult), r=["rec8", ("gates", i)], w=["gr"])
            for h_ in range(2):
                S.op("dve", lambda e, h_=h_: e.tensor_tensor(out=tmpo[:, h_ * 4:(h_ + 1) * 4, :], in0=OW[h_][:, :, 0:64],
                                                             in1=gr[:, h_ * 4:(h_ + 1) * 4].unsqueeze(2).to_broadcast([128, 4, 64]), op=ALU.mult), r=[PK[ob[h_]], "gr"], w=["tmpo"])
            S.op("pool", lambda e, i=i: e.tensor_tensor(out=onb[i % 2], in0=onsa[:, i, :], in1=tmpo.rearrange("p a b -> p (a b)"), op=ALU.add), r=["tmpo", ("onsa", i)], w=[("onb", i % 2)])
            if "onsa_all" in D_:
                S.dma("sp", D_["onsa_all"][:, i * 512:(i + 1) * 512], onb[i % 2], r=[("onb", i % 2)])
            b = nb(); pb = P[b][:, :].bitcast(BF16)
            for j in range(4):
                tp(pb[:, j * 128:(j + 1) * 128], onb[i % 2][:, j * 128:(j + 1) * 128], ident, r=[("onb", i % 2), "ident"], w=[PK[b]])
            act(onT[:, :, ts], pb[:, 0:512].rearrange("p (k t) -> p k t", k=4), AF.Copy, r=[PK[b]], w=[("onT", i)])
        if upto <= 9:
            return finish(nc, S, st)
        S.barrier()

        nbanks[0] = 8
        M.lo = lo0; M.hi = hi0
        hT = M.hi_alloc([8, S_], BF16)
        mergedT = M.hi_alloc([8, S_], BF16)
        hi2 = M.hi
        Wgm = M.lo_alloc([8, 512], BF16); Wgnm = M.lo_alloc([8, 512], BF16); Wom = M.lo_alloc([4, 512], BF16); Won = M.lo_alloc([4, 512], BF16)
        e3 = M.lo_alloc([512], F32); e4 = M.lo_alloc([512], F32); tA = M.lo_alloc([512], F32); tB = M.lo_alloc([512], F32)
        mgb = [M.lo_alloc([512], BF16) for _ in range(2)]
        S.dma("sp", hT.rearrange("p k t -> p (k t)"), hTs, r=["hTs"], w=["hTall"])
        e3 = [e3, M.lo_alloc([512], F32)]; e4 = [e4, M.lo_alloc([512], F32)]
        tA = [tA, M.lo_alloc([512], F32)]; tB = [tB, M.lo_alloc([512], F32)]

        def i_front(cc, i, it):
            ts = slice(i * 128, (i + 1) * 128)
            bs = [4 * (it % 2) + q for q in range(4)]
            b1, b2, b3, b4 = bs
            for k in range(8):
                mm(P[b3][:, :], hT[:, k, ts], Wgm[:, k, :], k == 0, k == 7, r=["hTall", "Wgm"], w=[PK[b3]])
            for k in range(8):
                mm(P[b4][:, :], hT[:, k, ts], Wgnm[:, k, :], k == 0, k == 7, r=["hTall", "Wgnm"], w=[PK[b4]])
            for k in range(4):
                mm(P[b1][:, :], omT[:, k, ts], Wom[:, k, :], k == 0, k == 3, r=[("omT", i), "Wom"], w=[PK[b1]])
            for k in range(4):
                mm(P[b2][:, :], onT[:, k, ts], Won[:, k, :], k == 0, k == 3, r=[("onT", i), "Won"], w=[PK[b2]])
            return bs

        def i_back(cc, i, it, bs):
            ts = slice(i * 128, (i + 1) * 128)
            b1, b2, b3, b4 = bs
            par = it % 2
            act(e3[par], P[b3][:, :], AF.Sigmoid, r=[PK[b3]], w=[("e3", par)])
            act(e4[par], P[b4][:, :], AF.Sigmoid, r=[PK[b4]], w=[("e4", par)])
            S.op("dve", lambda e: e.tensor_tensor(out=tA[par], in0=P[b1][:, :], in1=e3[par], op=ALU.mult), r=[PK[b1], ("e3", par)], w=[("tA", par)])
            S.op("dve", lambda e: e.tensor_tensor(out=tB[par], in0=P[b2][:, :], in1=e4[par], op=ALU.mult), r=[PK[b2], ("e4", par)], w=[("tB", par)])
            S.op("dve", lambda e: e.tensor_tensor(out=mgb[par], in0=tA[par], in1=tB[par], op=ALU.add), r=[("tA", par), ("tB", par)], w=[("mgb", par)])
            if "merged" in D_:
                S.dma("sp", D_["merged"][i * 128:(i + 1) * 128, cc * 512:(cc + 1) * 512], mgb[par], r=[("mgb", par)])
            pb = P[b3][:, :].bitcast(BF16)
            for j in range(4):
                tp(pb[:, j * 128:(j + 1) * 128], mgb[par][:, j * 128:(j + 1) * 128], ident, r=[("mgb", par), "ident"], w=[PK[b3]])
            act(mergedT[:, cc * 4:(cc + 1) * 4, ts], pb[:, 0:512].rearrange("p (k t) -> p k t", k=4), AF.Copy, r=[PK[b3]], w=[("mergedT", i)])

        it = 0
        for cc in range(2):
            load_w_cols(Wgm, w_gm, "Wgm", 8, cc * 512, (cc + 1) * 512); load_w_cols(Wgnm, w_gnm, "Wgnm", 8, cc * 512, (cc + 1) * 512)
            load_w_cols(Wom, wo_mla, "Wom", 4, cc * 512, (cc + 1) * 512); load_w_cols(Won, wo_nsa, "Won", 4, cc * 512, (cc + 1) * 512)
            pend_ = i_front(cc, 0, it)
            for i in range(NT):
                cur_ = pend_
                if i + 1 < NT:
                    pend_ = i_front(cc, i + 1, it + 1)
                i_back(cc, i, it, cur_)
                it += 1
        if upto <= 10:
            return finish(nc, S, st)
        S.barrier()

        M.lo = lo0
        h2T = hT
        Wout = M.lo_alloc([8, D], BF16)
        G1 = M.lo_alloc([D], F32); A2 = M.lo_alloc([D], F32); B2 = M.lo_alloc([D], F32)
        xb = [M.lo_alloc([D], F32) for _ in range(2)]
        x1t = [M.lo_alloc([D], F32) for _ in range(2)]
        junk = M.lo_alloc([D], F32); tmpA = M.lo_alloc([D], F32)
        hb = [M.lo_alloc([D], BF16) for _ in range(2)]
        ssq = M.lo_alloc([NT], F32); rs = M.lo_alloc([NT], F32)
        S.dma("sp", G1, mods[:, 2 * D:3 * D], r=["mods"], w=["G1"])
        S.dma("sp", B2, mods[:, 3 * D:4 * D], r=["mods"], w=["AB2"])
        S.dma("sp", A2, mods[:, 4 * D:5 * D], r=["mods"], w=["AB2"])
        load_w(Wout, w_out, "Wout", 8, D, ceng="pool")
        tmpA2 = [tmpA, M.lo_alloc([D], F32)]
        tmpJ = [M.lo_alloc([D], F32) for _ in range(3)]
        xb = xb + [M.lo_alloc([D], F32)]
        x1t = x1t + [M.lo_alloc([D], F32)]

        def j_front(i):
            ts = slice(i * 128, (i + 1) * 128)
            par = i % 3
            S.dma("sp", xb[par], x[ts, :], w=[("xb", par)])
            for cc in range(2):
                b = 2 * par + cc
                for k in range(8):
                    mm(P[b][:, :], mergedT[:, k, ts], Wout[:, k, cc * 512:(cc + 1) * 512], k == 0, k == 7, r=[("mergedT", i), "Wout"], w=[PK[b]])

        def j_mid(i):
            ts = slice(i * 128, (i + 1) * 128)
            par = i % 3
            for cc in range(2):
                b = 2 * par + cc
                S.op("dve", lambda e, b=b, cc=cc: e.tensor_tensor(out=tmpJ[par][:, cc * 512:(cc + 1) * 512], in0=P[b][:, :], in1=G1[:, cc * 512:(cc + 1) * 512], op=ALU.mult), r=[PK[b], "G1"], w=[("tmpJ", par)])
                S.op("pool", lambda e, cc=cc: e.tensor_tensor(out=x1t[par][:, cc * 512:(cc + 1) * 512], in0=tmpJ[par][:, cc * 512:(cc + 1) * 512], in1=xb[par][:, cc * 512:(cc + 1) * 512], op=ALU.add),
                     r=[("tmpJ", par), ("xb", par)], w=[("x1t", par)])
            S.dma("sp", x1s[ts, :], x1t[par], r=[("x1t", par)], w=[("x1s", i)])
            norm_a(i, x1t[par], ("x1t", par))

        bank_state[0] = 0

        def nbJ():
            b = 6 + (bank_state[0] % 2)
            bank_state[0] = (bank_state[0] + 1) % 2
            return b
        nb_saved = nb
        nb = nbJ
        j_front(0); j_mid(0); j_front(1); j_mid(1)
        for i in range(NT):
            if i + 2 < NT:
                j_front(i + 2)
            b_ = norm_b1(i, x1t[i % 3], ("x1t", i % 3), A2, B2, ["AB2"])
            if i + 2 < NT:
                j_mid(i + 2)
            norm_b2(i, b_, h2T, "h2T")
        nb = nb_saved
        bank_state[0] = 0
        if "x1" in D_:
            S.dma("sp", D_["x1"], x1s, r=[("x1s", i) for i in range(NT)])
        if upto <= 11:
            return finish(nc, S, st)
        S.barrier()

        M.lo = lo_pers; M.hi = hi2 + 8 * S_ * 2
        Wd = M.hi_alloc([NFC, D], BF16)
        actT = M.hi_alloc([NFC, 1024], BF16)
        G2 = M.lo_alloc([D], F32)
        Wg2 = [M.lo_alloc([8, 256], BF16) for _ in range(2)]; Wu2 = [M.lo_alloc([8, 256], BF16) for _ in range(2)]
        sg = [M.lo_alloc([512], F32) for _ in range(2)]
        xb = [M.lo_alloc([D], F32) for _ in range(2)]
        ot = [M.lo_alloc([D], F32) for _ in range(2)]
        tmpA = M.lo_alloc([D], F32)
        S.dma("sp", G2, mods[:, 5 * D:6 * D], r=["mods"], w=["G2"])
        load_w(Wd, wd, "Wd", NFC, D)
        h2k = [("h2T", i) for i in range(NT)]
        out_toks = []
        for half in range(2):
            def ld(jg_):
                wb_ = jg_ % 2
                load_w_cols(Wg2[wb_], wg, ("Wg2", wb_), 8, jg_ * 256, (jg_ + 1) * 256)
                load_w_cols(Wu2[wb_], wu, ("Wu2", wb_), 8, jg_ * 256, (jg_ + 1) * 256)
            ld(0)
            for jg in range(NFC // 2):
                wb = jg % 2
                if jg + 1 < NFC // 2:
                    ld(jg + 1)
                for jj in range(2):
                    j = jg * 2 + jj
                    for tc in range(2):
                        t0 = half * 1024 + tc * 512
                        bg, bu = nb(), nb()
                        for k in range(8):
                            mm(P[bg][:, :], Wg2[wb][:, k, jj * 128:(jj + 1) * 128], h2T[:, k, t0:t0 + 512], k == 0, k == 7, r=[("Wg2", wb)] + h2k, w=[PK[bg]])
                        for k in range(8):
                            mm(P[bu][:, :], Wu2[wb][:, k, jj * 128:(jj + 1) * 128], h2T[:, k, t0:t0 + 512], k == 0, k == 7, r=[("Wu2", wb)] + h2k, w=[PK[bu]])
                        act(sg[tc], P[bg][:, :], AF.Silu, r=[PK[bg]], w=[("sg", tc)])
                        S.op("dve", lambda e, j=j, tc=tc, bu=bu: e.tensor_tensor(out=actT[:, j, tc * 512:(tc + 1) * 512], in0=P[bu][:, :], in1=sg[tc], op=ALU.mult),
                             r=[PK[bu], ("sg", tc)], w=[("actT", tc)])
            for il in range(8):
                i = half * 8 + il
                ts = slice(i * 128, (i + 1) * 128)
                S.dma("sp", xb[i % 2], x1s[ts, :], r=[("x1s", i)], w=[("xb", i % 2)])
                for cc in range(2):
                    b = nb()
                    for j in range(NFC):
                        mm(P[b][:, :], actT[:, j, il * 128:(il + 1) * 128], Wd[:, j, cc * 512:(cc + 1) * 512], j == 0, j == NFC - 1, r=[("actT", il // 4), "Wd"], w=[PK[b]])
                    S.op("dve", lambda e, b=b, cc=cc: e.tensor_tensor(out=tmpA[:, cc * 512:(cc + 1) * 512], in0=P[b][:, :], in1=G2[:, cc * 512:(cc + 1) * 512], op=ALU.mult), r=[PK[b], "G2"], w=["tmpA"])
                    S.op("pool", lambda e, i=i, cc=cc: e.tensor_tensor(out=ot[i % 2][:, cc * 512:(cc + 1) * 512], in0=tmpA[:, cc * 512:(cc + 1) * 512], in1=xb[i % 2][:, cc * 512:(cc + 1) * 512], op=ALU.add),
                         r=["tmpA", ("xb", i % 2)], w=[("ot", i % 2)])
                S.dma("sp", out[ts, :], ot[i % 2], r=[("ot", i % 2)], w=[("out", i)])
        return finish(nc, S, st)


def finish(nc, S, st):
    for q in S.dsem:
        for i in range(len(S.dsem[q])):
            if S.dcnt[q][i]:
                S._wait("sp", (("d", q, i), S.dcnt[q][i]))
    for e2 in ("pe", "act", "dve", "pool"):
        if S.cnt[e2]:
            S._wait("sp", (e2, S.cnt[e2]))
    S.check_no_deadlock()
    st.close()
    return nc


def _consts():
    bf = ml_dtypes.bfloat16
    c = {}
    c["ident"] = np.eye(128, dtype=np.float32).astype(bf)
    a = np.arange(128)
    c["tri"] = (a[:, None] <= a[None, :]).astype(np.float32).astype(bf)
    w = np.zeros((128, 3, 128), np.float32)
    w[:, 0, :] = (a[:, None] > a[None, :])
    w[:, 1, :] = 1.0
    w[:, 2, :] = (a[:, None] <= a[None, :])
    c["winm"] = w.reshape(128, 384).astype(bf)
    inv16 = (np.float32(500000.0) ** (-np.arange(0, 32, 2, dtype=np.float32) / np.float32(32))).astype(np.float32)
    inv8 = (np.float32(500000.0) ** (-np.arange(0, 16, 2, dtype=np.float32) / np.float32(16))).astype(np.float32)
    c["inv16"] = np.tile(inv16[None], (128, 1)).astype(np.float32)
    c["inv8"] = np.tile(inv8[None], (128, 1)).astype(np.float32)
    n = np.arange(127)
    starts = n * 16
    j = np.arange(32)
    ovl = ((starts[:, None] < j[None, :] * 64 + 64) & (starts[:, None] + 32 > j[None, :] * 64))
    c["ovl"] = ovl.astype(np.float32).astype(bf)
    t = np.arange(S_)
    c["vmT"] = ((starts[:, None] + 31) <= t[None, :]).astype(np.float32).astype(bf)
    XE = np.zeros((32, NT, 128), np.float32)
    for kt in range(NT):
        XE[2 * kt, kt, 0:64] = 1.0
        XE[2 * kt + 1, kt, 64:128] = 1.0
    c["XE"] = XE.reshape(32, NT * 128).astype(bf)
    cur = (t // 64)
    forced = (j[None, :] == 0) | (j[None, :] == cur[:, None]) | (j[None, :] == cur[:, None] - 1)
    valid = j[None, :] <= cur[:, None]
    fb = np.where(valid, np.where(forced, 1e4, 0.0), -1e30).astype(np.float32)
    c["fb"] = fb.reshape(NT, 128, 32).transpose(1, 0, 2).reshape(128, NT * 32).copy()
    c["vj"] = valid.astype(np.float32).reshape(NT, 128, 32).transpose(1, 0, 2).reshape(128, NT * 32).copy()
    return c


def _rep(v, n=128):
    return np.ascontiguousarray(np.broadcast_to(np.asarray(v, np.float32)[None, :], (n, v.shape[0])))


def prep_inputs(inp):
    f = lambda a: np.ascontiguousarray(np.asarray(a, dtype=np.float32))
    w_in = f(inp["w_in"][0])
    o = np.cumsum([0, 768, 256, 32, 512, 128, 128, 128, 128, 128, 128, 24, 1024, 1024])
    seg = lambda i: w_in[:, o[i]:o[i + 1]]
    shared = {}
    shared["ada_w"] = f(inp["ada_w"][0]); shared["adabB"] = _rep(f(inp["ada_b"][0]))
    shared["g1B"] = _rep(f(inp["norm1_gain"][0])); shared["g2B"] = _rep(f(inp["norm2_gain"][0]))
    shared["w_cq"] = f(seg(0)); shared["w_ckv"] = f(seg(1)); shared["w_kpe"] = f(seg(2))
    qn = seg(3).reshape(D, 8, 64)
    shared["w_qn"] = f(qn[:, PH, :].reshape(D, 512))
    kc = seg(4).reshape(D, 2, 64); vc = seg(5).reshape(D, 2, 64)
    shared["w_kc2"] = f(np.stack([kc[:, 0], kc[:, 0], kc[:, 1], kc[:, 1]], 1).reshape(D, 256))
    shared["w_vc2"] = f(np.stack([vc[:, 0], vc[:, 0], vc[:, 1], vc[:, 1]], 1).reshape(D, 256))
    shared["w_kv4"] = f(np.concatenate([seg(6), seg(8), seg(7), seg(9)], 1))
    gn = seg(10).reshape(D, 8, 3)
    shared["w_gn"] = f(gn[:, PH, :].transpose(0, 2, 1).reshape(D, 24))
    shared["w_gm"] = f(seg(11)); shared["w_gnm"] = f(seg(12))
    shared["qag"] = f(f(inp["mla_q_a_gain"][0]).reshape(6, 128).T); shared["kvag"] = f(f(inp["mla_kv_a_gain"][0]).reshape(2, 128).T)
    shared["w_qb"] = f(inp["mla_w_q_b"][0]); shared["w_kvb"] = f(inp["mla_w_kv_b"][0])
    shared["qgB"] = _rep(f(inp["mla_q_gain"][0])); shared["kgB"] = _rep(f(inp["mla_k_gain"][0]))
    shared["nqB"] = _rep(f(inp["nsa_q_gain"][0])); shared["nkcB"] = _rep(f(inp["nsa_kc_gain"][0]))
    shared["nksB"] = _rep(f(inp["nsa_ks_gain"][0])); shared["nkwB"] = _rep(f(inp["nsa_kw_gain"][0]))
    shared["posk"] = f(f(inp["cmp_pos_k"][0]).reshape(16, 128).T); shared["posv"] = f(f(inp["cmp_pos_v"][0]).reshape(16, 128).T)
    shared["w1k"] = f(inp["cmp_w1_k"][0]); shared["w2k"] = f(inp["cmp_w2_k"][0])
    shared["w1v"] = f(inp["cmp_w1_v"][0]); shared["w2v"] = f(inp["cmp_w2_v"][0])
    shared["wo_mla"] = f(inp["w_o_mla"][0])
    shared["wo_nsa"] = f(f(inp["w_o_nsa"][0]).reshape(8, 64, D)[PH].reshape(512, D))
    shared["w_out"] = f(inp["w_out"][0])
    shared["wg"] = f(inp["ffn_w_gate"][0]); shared["wu"] = f(inp["ffn_w_up"][0]); shared["wd"] = f(inp["ffn_w_down"][0])
    shared.update(_consts())
    maps = []
    xs = np.asarray(inp["x"], np.float32); cs = np.asarray(inp["c"], np.float32); ps = np.asarray(inp["positions"]).astype(np.int32)
    for b in range(xs.shape[0]):
        m = dict(shared)
        m["x"] = np.ascontiguousarray(xs[b])
        m["c_pk"] = np.ascontiguousarray(cs[b].reshape(8, 128).T)
        m["pos_pk"] = np.ascontiguousarray(ps[b].reshape(NT, 128).T)
        m["posC"] = np.ascontiguousarray(ps[b][31::16][:127].reshape(127, 1))
        maps.append(m)
    return maps


_NC_CACHE = {}


def kernel(**inputs):
    maps = prep_inputs(inputs)
    if "nc" not in _NC_CACHE:
        _NC_CACHE["nc"] = build()
    nc = _NC_CACHE["nc"]
    res = run_bass_kernel_spmd(nc, maps, core_ids=list(range(len(maps))))
    return np.stack([np.asarray(r["out"], dtype=np.float32) for r in res.results], 0)
```

```python
import contextlib
import numpy as np
import ml_dtypes
import concourse.bass as bass
import concourse.mybir as mybir
from concourse.bass_utils import run_bass_kernel_spmd

F32 = mybir.dt.float32
BF16 = mybir.dt.bfloat16
I32 = mybir.dt.int32
ALU = mybir.AluOpType
AF = mybir.ActivationFunctionType
AX = mybir.AxisListType

S_ = 2048
D = 1024
NT = 16
DFF = 2816
NFC = 22
EPS = 1e-6
PH = [0, 4, 1, 5, 2, 6, 3, 7]
TWO_PI = float(2 * np.pi)
PI = float(np.pi)


class Sched:
    N_DMA_SLOTS = {"sp": 24, "pool": 8, "act": 4}

    def __init__(self, nc, stack):
        self.nc = nc
        self.E = {"pe": nc.tensor, "act": nc.scalar, "dve": nc.vector, "pool": nc.gpsimd, "sp": nc.sync}
        self.sem, self.cnt = {}, {}
        for e in ("pe", "act", "dve", "pool"):
            self.sem[e] = stack.enter_context(nc.semaphore("s_" + e))
            self.cnt[e] = 0
        self.dsem, self.dcnt, self.dnext = {}, {}, {}
        for q, n in self.N_DMA_SLOTS.items():
            self.dsem[q] = [stack.enter_context(nc.semaphore(f"d_{q}{i}")) for i in range(n)]
            self.dcnt[q] = [0] * n
            self.dnext[q] = 0
        self.seen = {e: {} for e in self.E}
        self.lastw, self.readers = {}, {}
        self.n_wait = 0
        self.n_inst = 0
        self.prog = {}
        self.ninst = {}

    def check_no_deadlock(self):
        val = {}
        ptr = {e: 0 for e in self.prog}
        progress = True
        while progress:
            progress = False
            for e, lst in self.prog.items():
                while ptr[e] < len(lst):
                    it = lst[ptr[e]]
                    if it[0] == "w":
                        if val.get(it[1], 0) < it[2]:
                            break
                    else:
                        val[it[1]] = val.get(it[1], 0) + it[2]
                    ptr[e] += 1
                    progress = True
        stuck = {e: (ptr[e], len(l), l[ptr[e]]) for e, l in self.prog.items() if ptr[e] < len(l)}
        assert not stuck, f"DEADLOCK in emitted program: {stuck}"

    def _sem_of(self, src):
        return self.dsem[src[1]][src[2]] if isinstance(src, tuple) else self.sem[src]

    def _wait(self, e, tok):
        src, val = tok
        if self.seen[e].get(src, 0) >= val:
            return
        self.E[e].wait_ge(self._sem_of(src), val)
        self.seen[e][src] = val
        self.n_wait += 1
        self.prog.setdefault(e, []).append(("w", src, val))

    def _deps(self, e, r, w):
        toks = []
        for k in r:
            t = self.lastw.get(k)
            if t is not None:
                toks.append(t)
        for k in w:
            t = self.lastw.get(k)
            if t is not None:
                toks.append(t)
            for t in self.readers.get(k, ()):
                toks.append(t)
        for t in toks:
            if t[0] == e and e == "pe":
                continue
            self._wait(e, t)

    def _commit(self, tok, r, w):
        for k in r:
            lst = self.readers.setdefault(k, [])
            lst[:] = [t for t in lst if t[0] != tok[0]]
            lst.append(tok)
        for k in w:
            self.lastw[k] = tok
            self.readers[k] = []

    def op(self, e, fn, r=(), w=(), signal=True):
        self._deps(e, r, w)
        ins = fn(self.E[e])
        if signal:
            self.cnt[e] += 1
            ins.then_inc(self.sem[e], 1)
            tok = (e, self.cnt[e])
            self.prog.setdefault(e, []).append(("i", e, 1))
        else:
            tok = (e, self.cnt[e] + 1)
        self._commit(tok, r, w)
        self.n_inst += 1
        self.ninst[e] = self.ninst.get(e, 0) + 1
        return tok

    def dma(self, q, out, in_, r=(), w=(), **kw):
        slot = self.dnext[q]
        self.dnext[q] = (slot + 1) % len(self.dsem[q])
        src = ("d", q, slot)
        if self.dcnt[q][slot] > 0:
            self._wait(q, (src, self.dcnt[q][slot]))
        self._deps(q, r, w)
        ins = self.E[q].dma_start(out=out, in_=in_, **kw)
        self.dcnt[q][slot] += 16
        ins.then_inc(self.dsem[q][slot], 16)
        self.prog.setdefault(q, []).append(("i", src, 16))
        tok = (src, self.dcnt[q][slot])
        self._commit(tok, r, w)
        self.n_inst += 1
        return tok

    def barrier(self):
        for e in ("pe", "act", "dve", "pool", "sp"):
            for e2 in ("pe", "act", "dve", "pool"):
                if self.cnt[e2] and not (e2 == e == "pe"):
                    self._wait(e, (e2, self.cnt[e2]))
            for q in self.dsem:
                for i in range(len(self.dsem[q])):
                    if self.dcnt[q][i]:
                        self._wait(e, (("d", q, i), self.dcnt[q][i]))
        self.lastw.clear()
        self.readers.clear()


class Mem:
    def __init__(self, big, nbytes):
        self.big, self.lo, self.hi, self.n = big, 0, nbytes, nbytes

    def _view(self, off, shape, dt):
        nel = int(np.prod(shape))
        esz = 4 if dt in (F32, I32) else 2
        nb = nel * esz
        ap = self.big[:, off // 2:(off + nb) // 2]
        if esz == 4:
            ap = ap.bitcast(dt)
        if len(shape) == 2:
            ap = ap.rearrange("p (a b) -> p a b", a=shape[0])
        elif len(shape) == 3:
            ap = ap.rearrange("p (a b c) -> p a b c", a=shape[0], b=shape[1])
        return ap

    def lo_alloc(self, shape, dt):
        nb = int(np.prod(shape)) * (4 if dt in (F32, I32) else 2)
        nb = (nb + 63) // 64 * 64
        off = self.lo
        self.lo += nb
        assert self.lo <= self.hi, f"SBUF overflow lo={self.lo} hi={self.hi}"
        return self._view(off, shape, dt)

    def hi_alloc(self, shape, dt):
        nb = int(np.prod(shape)) * (4 if dt in (F32, I32) else 2)
        nb = (nb + 63) // 64 * 64
        self.hi -= nb
        assert self.lo <= self.hi, f"SBUF overflow lo={self.lo} hi={self.hi}"
        return self._view(self.hi, shape, dt)


def build(upto=99, dbg=()):
    nc = bass.Bass("TRN2", target_bir_lowering=False)
    I = {}

    def din(name, shape, dt=F32):
        I[name] = nc.dram_tensor(name, list(shape), dt, kind="ExternalInput").ap()
        return I[name]

    x = din("x", [S_, D]); c_pk = din("c_pk", [128, 8]); pos_pk = din("pos_pk", [128, NT], I32)
    posC = din("posC", [127, 1], I32)
    ada_w = din("ada_w", [D, 6 * D]); adabB = din("adabB", [128, 6 * D]); g1B = din("g1B", [128, D]); g2B = din("g2B", [128, D])
    w_cq = din("w_cq", [D, 768]); w_ckv = din("w_ckv", [D, 256]); w_kpe = din("w_kpe", [D, 32])
    w_qn = din("w_qn", [D, 512]); w_kc2 = din("w_kc2", [D, 256]); w_vc2 = din("w_vc2", [D, 256])
    w_kv4 = din("w_kv4", [D, 512]); w_gn = din("w_gn", [D, 24]); w_gm = din("w_gm", [D, D]); w_gnm = din("w_gnm", [D, D])
    qag = din("qag", [128, 6]); kvag = din("kvag", [128, 2])
    w_qb = din("w_qb", [768, 768]); w_kvb = din("w_kvb", [256, 1024])
    qgB = din("qgB", [128, 96]); kgB = din("kgB", [128, 96])
    nqB = din("nqB", [128, 64]); nkcB = din("nkcB", [128, 64]); nksB = din("nksB", [128, 64]); nkwB = din("nkwB", [128, 64])
    posk = din("posk", [128, 16]); w1k = din("w1k", [2048, 256]); w2k = din("w2k", [256, 64])
    posv = din("posv", [128, 16]); w1v = din("w1v", [2048, 256]); w2v = din("w2v", [256, 64])
    wo_mla = din("wo_mla", [512, D]); wo_nsa = din("wo_nsa", [512, D]); w_out = din("w_out", [D, D])
    wg = din("wg", [D, DFF]); wu = din("wu", [D, DFF]); wd = din("wd", [DFF, D])
    ident_d = din("ident", [128, 128], BF16); tri_d = din("tri", [128, 128], BF16); winm_d = din("winm", [128, 384], BF16)
    inv16_d = din("inv16", [128, 16]); inv8_d = din("inv8", [128, 8])
    XEall_d = din("XEall", [32, S_], BF16); ovl_d = din("ovl", [127, 32], BF16); vmT_d = din("vmT", [127, S_], BF16); XE_d = din("XE", [32, NT * 128], BF16)
    fb_d = din("fb", [128, NT * 32]); vj_d = din("vj", [128, NT * 32])
    out = nc.dram_tensor("out", [S_, D], F32, kind="ExternalOutput").ap()
    hTs = nc.dram_tensor("hTs", [128, 8 * S_], BF16).ap()
    mods = nc.dram_tensor("mods", [128, 6 * D], F32).ap()
    x1s = nc.dram_tensor("x1s", [S_, D], F32).ap()
    D_ = {}
    for name, shape, dt in dbg:
        D_[name] = nc.dram_tensor("dbg_" + name, list(shape), dt, kind="ExternalOutput").ap()

    st = contextlib.ExitStack()
    with st:
        S = Sched(nc, st)
        NB = 204800
        big = st.enter_context(nc.sbuf_tensor("big", [128, NB // 2], BF16))
        M = Mem(big, NB)
        P = [st.enter_context(nc.psum_tensor(f"ps{i}", [128, 512], F32)) for i in range(8)]
        PK = [f"ps{i}" for i in range(8)]
        bank_state = [0]

        nbanks = [8]

        def nb():
            b = bank_state[0] % nbanks[0]
            bank_state[0] = (b + 1) % nbanks[0]
            return b

        def mm(ps_ap, lhsT, rhs, start, stop, r, w, sig=None, **kw):
            S.op("pe", lambda e: e.matmul(ps_ap, lhsT=lhsT, rhs=rhs, start=start, stop=stop, **kw), r=r, w=w, signal=bool(stop) if sig is None else sig)

        def tp(ps_ap, in_, ident_ap, r, w):
            S.op("pe", lambda e: e.transpose(out=ps_ap, in_=in_, identity=ident_ap), r=r, w=w)

        def act(out_, in_, func, r, w, **kw):
            S.op("act", lambda e: e.activation(out=out_, in_=in_, func=func, **kw), r=r, w=w)

        def dbg_out(name, ap, r):
            if name in D_:
                S.dma("sp", D_[name], ap, r=r)

        ident = M.lo_alloc([128], BF16); tri = M.lo_alloc([128], BF16); winm = M.lo_alloc([3, 128], BF16)
        onesb = M.lo_alloc([128], BF16)
        stg = [M.lo_alloc([1024], F32) for _ in range(3)]
        stg_i = [0]
        cosM = M.lo_alloc([NT, 16], F32); sinM = M.lo_alloc([NT, 16], F32)
        cosN = M.lo_alloc([NT, 8], F32); sinN = M.lo_alloc([NT, 8], F32)
        cosC = M.lo_alloc([8], F32); sinC = M.lo_alloc([8], F32)
        lo_pers = M.lo
        omT = M.lo_alloc([4, S_], BF16); onT = M.lo_alloc([4, S_], BF16)
        S.dma("sp", ident, ident_d, w=["ident"])
        S.dma("sp", tri, tri_d, w=["tri"])
        S.dma("sp", winm.rearrange("p a b -> p (a b)"), winm_d, w=["winm"])
        S.op("pool", lambda e: e.memset(onesb, 1.0), w=["onesb"])

        def load_w(dst, W, key, KC, N, ceng="dve"):
            Wv = W.rearrange("(k p) n -> p k n", p=128)
            if N <= 1024:
                g = max(1, min(KC, 1024 // N))
                for k0 in range(0, KC, g):
                    k1 = min(KC, k0 + g)
                    si = stg_i[0]; stg_i[0] = (si + 1) % 3
                    sv = stg[si][:, 0:(k1 - k0) * N].rearrange("p (k n) -> p k n", n=N)
                    S.dma("sp", sv, Wv[:, k0:k1, :], w=[("stg", si)])
                    S.op(ceng, lambda e, sv=sv, k0=k0, k1=k1: e.tensor_copy(out=dst[:, k0:k1, :], in_=sv), r=[("stg", si)], w=[key])
            else:
                for k in range(KC):
                    for c0 in range(0, N, 1024):
                        c1 = min(N, c0 + 1024)
                        si = stg_i[0]; stg_i[0] = (si + 1) % 3
                        sv = stg[si][:, 0:c1 - c0]
                        S.dma("sp", sv, Wv[:, k, c0:c1], w=[("stg", si)])
                        S.op(ceng, lambda e, sv=sv, k=k, c0=c0, c1=c1: e.tensor_copy(out=dst[:, k, c0:c1], in_=sv), r=[("stg", si)], w=[key])

        def load_w_cols(dst, W, key, KC, c0, c1, ceng="dve"):
            Wv = W.rearrange("(k p) n -> p k n", p=128)
            N = c1 - c0
            g = max(1, min(KC, 1024 // N))
            for k0 in range(0, KC, g):
                k1 = min(KC, k0 + g)
                si = stg_i[0]; stg_i[0] = (si + 1) % 3
                sv = stg[si][:, 0:(k1 - k0) * N].rearrange("p (k n) -> p k n", n=N)
                S.dma("sp", sv, Wv[:, k0:k1, c0:c1], w=[("stg", si)])
                S.op(ceng, lambda e, sv=sv, k0=k0, k1=k1: e.tensor_copy(out=dst[:, k0:k1, :], in_=sv), r=[("stg", si)], w=[key])

        def sincos(ang, shape, cos_o, sin_o, np_, tmp_f, tmp_i, tmp_m, key):
            for (shift, dst) in ((0.0, sin_o), (PI / 2, cos_o)):
                S.op("dve", lambda e: e.tensor_scalar(out=tmp_f, in0=ang, scalar1=shift, scalar2=None, op0=ALU.add), r=[key + "ang"], w=[key + "f"])
                S.op("dve", lambda e: e.tensor_scalar(out=tmp_i, in0=tmp_f, scalar1=float(1 / TWO_PI), scalar2=None, op0=ALU.mult), r=[key + "f"], w=[key + "i"])
                S.op("dve", lambda e: e.tensor_copy(out=tmp_m, in_=tmp_i), r=[key + "i"], w=[key + "m"])
                S.op("dve", lambda e: e.scalar_tensor_tensor(out=tmp_f, in0=tmp_m, scalar=-TWO_PI, in1=tmp_f, op0=ALU.mult, op1=ALU.add), r=[key + "m", key + "f"], w=[key + "f"])
                S.op("dve", lambda e: e.tensor_scalar(out=tmp_m, in0=tmp_f, scalar1=PI, scalar2=None, op0=ALU.is_gt), r=[key + "f"], w=[key + "m"])
                S.op("dve", lambda e: e.scalar_tensor_tensor(out=tmp_f, in0=tmp_m, scalar=-TWO_PI, in1=tmp_f, op0=ALU.mult, op1=ALU.add), r=[key + "m", key + "f"], w=[key + "f"])
                S.op("dve", lambda e: e.tensor_scalar(out=tmp_m, in0=tmp_f, scalar1=-PI, scalar2=None, op0=ALU.is_lt), r=[key + "f"], w=[key + "m"])
                S.op("dve", lambda e: e.scalar_tensor_tensor(out=tmp_f, in0=tmp_m, scalar=TWO_PI, in1=tmp_f, op0=ALU.mult, op1=ALU.add), r=[key + "m", key + "f"], w=[key + "f"])
                act(dst, tmp_f, AF.Sin, r=[key + "f"], w=[key + "out"])

        lo0, hi0 = M.lo, M.hi
        if upto <= -2:
            return finish(nc, S, st)
        posi = M.lo_alloc([NT], I32); posf = M.lo_alloc([NT], F32)
        posCi = M.lo_alloc([1], I32); posCf = M.lo_alloc([1], F32)
        inv16 = M.lo_alloc([16], F32); inv8 = M.lo_alloc([8], F32)
        angM = M.lo_alloc([NT, 16], F32); tfM = M.lo_alloc([NT, 16], F32); tiM = M.lo_alloc([NT, 16], I32); tmM = M.lo_alloc([NT, 16], F32)
        S.dma("sp", posi, pos_pk, w=["posi"])
        S.dma("sp", posCi[0:127], posC, w=["posCi"])
        S.dma("sp", inv16, inv16_d, w=["inv16"])
        S.dma("sp", inv8, inv8_d, w=["inv8"])
        S.op("dve", lambda e: e.tensor_copy(out=posf, in_=posi), r=["posi"], w=["posf"])
        S.op("dve", lambda e: e.tensor_copy(out=posCf[0:127], in_=posCi[0:127]), r=["posCi"], w=["posCf"])
        S.op("dve", lambda e: e.tensor_tensor(out=angM, in0=posf.unsqueeze(2).to_broadcast([128, NT, 16]),
                                              in1=inv16.unsqueeze(1).to_broadcast([128, NT, 16]), op=ALU.mult), r=["posf", "inv16"], w=["Mang"])
        sincos(angM, None, cosM, sinM, 128, tfM, tiM, tmM, "M")
        a8 = angM.rearrange("p a b -> p (a b)")[:, 0:NT * 8].rearrange("p (a b) -> p a b", b=8)
        f8 = tfM.rearrange("p a b -> p (a b)")[:, 0:NT * 8].rearrange("p (a b) -> p a b", b=8)
        i8 = tiM.rearrange("p a b -> p (a b)")[:, 0:NT * 8].rearrange("p (a b) -> p a b", b=8)
        m8_ = tmM.rearrange("p a b -> p (a b)")[:, 0:NT * 8].rearrange("p (a b) -> p a b", b=8)
        S.op("dve", lambda e: e.tensor_tensor(out=a8, in0=posf.unsqueeze(2).to_broadcast([128, NT, 8]),
                                              in1=inv8.unsqueeze(1).to_broadcast([128, NT, 8]), op=ALU.mult), r=["posf", "inv8", "Mout", "Mf", "Mm", "Mi"], w=["Nang"])
        sincos(a8, None, cosN, sinN, 128, f8, i8, m8_, "N")
        aC = angM.rearrange("p a b -> p (a b)")[0:127, 0:8]
        fC = tfM.rearrange("p a b -> p (a b)")[0:127, 0:8]
        iC = tiM.rearrange("p a b -> p (a b)")[0:127, 0:8]
        mC = tmM.rearrange("p a b -> p (a b)")[0:127, 0:8]
        S.op("dve", lambda e: e.tensor_scalar(out=aC, in0=inv8[0:127], scalar1=posCf[0:127, 0:1], scalar2=None, op0=ALU.mult),
             r=["posCf", "inv8", "Nout", "Nf", "Nm", "Ni", "Nang"], w=["Cang"])
        sincos(aC, None, cosC[0:127], sinC[0:127], 127, fC, iC, mC, "C")
        dbg_out("cosM", cosM, ["Mout"]); dbg_out("sinM", sinM, ["Mout"])

        if upto <= -1:
            return finish(nc, S, st)
        cpk = M.lo_alloc([8], F32); sc = M.lo_alloc([8], F32)
        sch = M.lo_alloc([8], BF16); scl = M.lo_alloc([8], BF16)
        cBh = M.lo_alloc([8, 128], BF16); cBl = M.lo_alloc([8, 128], BF16)
        modB = M.lo_alloc([6 * D], F32)
        g1t = M.lo_alloc([D], F32); g2t = M.lo_alloc([D], F32)
        awb = [M.hi_alloc([8, 512], F32) for _ in range(3)]
        abb = [M.hi_alloc([512], F32) for _ in range(3)]
        awh = [M.hi_alloc([8, 512], BF16) for _ in range(3)]
        awl = [M.hi_alloc([8, 512], BF16) for _ in range(3)]
        S.dma("sp", cpk, c_pk, w=["cpk"])
        S.dma("sp", g1t, g1B, w=["g1t"]); S.dma("sp", g2t, g2B, w=["g2t"])
        act(sc, cpk, AF.Silu, r=["cpk"], w=["sc"])
        S.op("dve", lambda e: e.tensor_copy(out=sch, in_=sc), r=["sc"], w=["sch"])
        S.op("dve", lambda e: e.tensor_tensor(out=scl, in0=sc, in1=sch, op=ALU.subtract), r=["sc", "sch"], w=["scl"])
        for k in range(8):
            S.op("dve", lambda e, k=k: e.tensor_copy(out=cBh[:, k, :], in_=sch[:, k:k + 1].to_broadcast([128, 128])), r=["sch"], w=["cBh"])
            S.op("dve", lambda e, k=k: e.tensor_copy(out=cBl[:, k, :], in_=scl[:, k:k + 1].to_broadcast([128, 128])), r=["scl"], w=["cBl"])
        awv = ada_w.rearrange("(k p) n -> p k n", p=128)
        for n in range(12):
            q_ = n % 3
            S.dma("sp", awb[q_], awv[:, :, n * 512:(n + 1) * 512], w=[("awb", q_)])
            S.dma("sp", abb[q_], adabB[:, n * 512:(n + 1) * 512], w=[("abb", q_)])
            act(awh[q_], awb[q_], AF.Copy, r=[("awb", q_)], w=[("awh", q_)])
            S.op("dve", lambda e, q_=q_: e.tensor_tensor(out=awl[q_], in0=awb[q_], in1=awh[q_], op=ALU.subtract), r=[("awb", q_), ("awh", q_)], w=[("awl", q_)])
            b = nb()
            passes = [(cBh, "cBh", awh, "awh"), (cBh, "cBh", awl, "awl"), (cBl, "cBl", awh, "awh")]
            for pi_, (cb_, ck, ww, wk) in enumerate(passes):
                for k in range(8):
                    mm(P[b][:, :], cb_[:, k, :], ww[q_][:, k, :], pi_ == 0 and k == 0, pi_ == 2 and k == 7, r=[ck, (wk, q_)], w=[PK[b]])
            S.op("dve", lambda e, n=n, b=b, q_=q_: e.tensor_tensor(out=modB[:, n * 512:(n + 1) * 512], in0=P[b][:, :], in1=abb[q_], op=ALU.add),
                 r=[PK[b], ("abb", q_)], w=["modB"])
        S.op("dve", lambda e: e.scalar_tensor_tensor(out=modB[:, D:2 * D], in0=modB[:, D:2 * D], scalar=1.0, in1=g1t, op0=ALU.add, op1=ALU.mult), r=["modB", "g1t"], w=["modB"])
        S.op("dve", lambda e: e.scalar_tensor_tensor(out=modB[:, 4 * D:5 * D], in0=modB[:, 4 * D:5 * D], scalar=1.0, in1=g2t, op0=ALU.add, op1=ALU.mult), r=["modB", "g2t"], w=["modB"])
        S.dma("sp", mods, modB, r=["modB"], w=["mods"])
        dbg_out("modB", modB, ["modB"])
        B1 = modB[:, 0:D]; A1 = modB[:, D:2 * D]
        if upto <= 0:
            return finish(nc, S, st)

        M.hi = hi0
        hT = M.hi_alloc([8, S_], BF16)
        xb = [M.lo_alloc([D], F32) for _ in range(2)]
        junk = M.lo_alloc([D], F32); tmpA = M.lo_alloc([D], F32)
        hb = [M.lo_alloc([D], BF16) for _ in range(2)]
        ssq = M.lo_alloc([NT], F32); rs = M.lo_alloc([NT], F32)

        tmpA2 = [tmpA, M.lo_alloc([D], F32)]

        def norm_a(i, xt, xkey):
            act(junk, xt, AF.Square, r=[xkey], w=["junk", ("ssq", i)], accum_out=ssq[:, i:i + 1])
            act(rs[:, i:i + 1], ssq[:, i:i + 1], AF.Sqrt, r=[("ssq", i)], w=[("rs", i)], scale=1.0 / D, bias=EPS)
            S.op("dve", lambda e: e.reciprocal(out=rs[:, i:i + 1], in_=rs[:, i:i + 1]), r=[("rs", i)], w=[("rs", i)])

        def norm_b1(i, xt, xkey, A, B, Akeys):
            par = i % 2
            S.op("dve", lambda e: e.scalar_tensor_tensor(out=tmpA2[par], in0=xt, scalar=rs[:, i:i + 1], in1=A, op0=ALU.mult, op1=ALU.mult),
                 r=[xkey, ("rs", i)] + Akeys, w=[("tmpA2", par)])
            S.op("pool", lambda e: e.tensor_tensor(out=hb[par], in0=tmpA2[par], in1=B, op=ALU.add), r=[("tmpA2", par)] + Akeys, w=[("hb", par)])
            b = nb()
            pb = P[b][:, :].bitcast(BF16)
            for k in range(8):
                tp(pb[:, k * 128:(k + 1) * 128], hb[par][:, k * 128:(k + 1) * 128], ident, r=[("hb", par), "ident"], w=[PK[b]])
            return b

        def norm_b2(i, b, dstT, dkey):
            pb = P[b][:, :].bitcast(BF16)
            act(dstT[:, :, i * 128:(i + 1) * 128], pb.rearrange("p (k t) -> p k t", k=8), AF.Copy, r=[PK[b]], w=[(dkey, i)])

        xb = xb + [M.lo_alloc([D], F32), M.lo_alloc([D], F32)]

        def a_front(i):
            S.dma("sp", xb[i % 4], x[i * 128:(i + 1) * 128, :], w=[("xb", i % 4)])
            norm_a(i, xb[i % 4], ("xb", i % 4))

        a_front(0); a_front(1)
        for i in range(NT):
            b_ = norm_b1(i, xb[i % 4], ("xb", i % 4), A1, B1, ["modB"])
            if i + 2 < NT:
                a_front(i + 2)
            norm_b2(i, b_, hT, "hT")
        hTk = [("hT", i) for i in range(NT)]
        S.dma("sp", hTs, hT.rearrange("p k t -> p (k t)"), r=hTk, w=["hTs"])
        dbg_out("hT", hT.rearrange("p k t -> p (k t)"), hTk)
        if upto <= 1:
            return finish(nc, S, st)
        S.barrier()

        nbanks[0] = 6
        M.lo = lo0
        cqT = M.lo_alloc([6, S_], BF16); ckvT = M.lo_alloc([2, S_], BF16); kpe = M.lo_alloc([NT, 32], F32)
        lo1 = M.lo
        Wcq = M.lo_alloc([8, 768], BF16); Wckv = M.lo_alloc([8, 256], BF16); Wkpe = M.lo_alloc([8, 32], BF16)
        qagt = M.lo_alloc([6], F32); kvagt = M.lo_alloc([2], F32)
        sqb = [M.lo_alloc([512], BF16) for _ in range(2)]
        rb = M.lo_alloc([512], F32)
        S.dma("sp", qagt, qag, w=["qagt"]); S.dma("sp", kvagt, kvag, w=["kvagt"])
        load_w(Wcq, w_cq, "Wcq", 8, 768); load_w(Wckv, w_ckv, "Wckv", 8, 256); load_w(Wkpe, w_kpe, "Wkpe", 8, 32)

        def fm_proj_norm(dstT, dkey, Wt, wkey, nf, gaint, gkey, nfeat):
            for c in range(4):
                hk = [("hT", 4 * c + q) for q in range(4)]
                for j in range(nf + 1):
                    if j < nf:
                        b = nb()
                        for k in range(8):
                            mm(P[b][:, :], Wt[:, k, j * 128:(j + 1) * 128], hT[:, k, c * 512:(c + 1) * 512], k == 0, k == 7, r=[wkey] + hk, w=[PK[b]])
                    if j >= 1:
                        jj = j - 1
                        mm(P[6][:, :], onesb, sqb[jj % 2], jj == 0, jj == nf - 1, r=["onesb", ("sqb", jj % 2)], w=[PK[6]], sig=True)
                    if j < nf:
                        act(sqb[j % 2], P[b][:, :], AF.Square, r=[PK[b]], w=[("sqb", j % 2)])
                        act(dstT[:, j, c * 512:(c + 1) * 512], P[b][:, :], AF.Copy, r=[PK[b]], w=[(dkey, c)])
                act(rb, P[6][:, :], AF.Sqrt, r=[PK[6]], w=["rb"], scale=1.0 / nfeat, bias=EPS)
                S.op("dve", lambda e: e.reciprocal(out=rb, in_=rb), r=["rb"], w=["rb"])
                for j in range(nf):
                    S.op("dve", lambda e, j=j, c=c: e.scalar_tensor_tensor(out=dstT[:, j, c * 512:(c + 1) * 512], in0=dstT[:, j, c * 512:(c + 1) * 512],
                                                                             scalar=gaint[:, j:j + 1], in1=rb, op0=ALU.mult, op1=ALU.mult),
                         r=[(dkey, c), "rb", gkey], w=[(dkey, c)])

        fm_proj_norm(cqT, "cqT", Wcq, "Wcq", 6, qagt, "qagt", 768)
        fm_proj_norm(ckvT, "ckvT", Wckv, "Wckv", 2, kvagt, "kvagt", 256)
        for i in range(NT):
            b = nb()
            for k in range(8):
                mm(P[b][:, 0:32], hT[:, k, i * 128:(i + 1) * 128], Wkpe[:, k, :], k == 0, k == 7, r=["Wkpe", ("hT", i)], w=[PK[b]])
            S.op("dve", lambda e, i=i, b=b: e.tensor_copy(out=kpe[:, i, :], in_=P[b][:, 0:32]), r=[PK[b]], w=[("kpe", i)])
        cqk = [("cqT", c) for c in range(4)]
        dbg_out("cqT", cqT.rearrange("p k t -> p (k t)"), cqk)
        dbg_out("kpe", kpe.rearrange("p a b -> p (a b)"), [("kpe", i) for i in range(NT)])
        if upto <= 2:
            return finish(nc, S, st)
        S.barrier()

        nbanks[0] = 8
        M.lo = lo1
        M.hi = hi0
        QT = M.hi_alloc([8, S_], BF16); KT = M.hi_alloc([8, S_], BF16); V = M.hi_alloc([NT, 8, 65], BF16)
        hi1 = M.hi
        Wqb = M.lo_alloc([6, 768], BF16); Wkvb = M.lo_alloc([2, 1024], BF16)
        qgt = M.lo_alloc([96], F32); kgt = M.lo_alloc([96], F32)
        drq = [M.lo_alloc([768], BF16) for _ in range(2)]; drk = [M.lo_alloc([768], BF16) for _ in range(2)]
        S.dma("sp", qgt, qgB, w=["qgt"]); S.dma("sp", kgt, kgB, w=["kgt"])
        load_w(Wqb, w_qb, "Wqb", 6, 768); load_w(Wkvb, w_kvb, "Wkvb", 2, 1024)
        S.op("pool", lambda e: e.memset(V[:, :, :, 64:65], 1.0), w=["Vones"])

        def mk_tmps(Mx, n, H, hf):
            return dict(t1=Mx.lo_alloc([n], F32), hs=Mx.lo_alloc([H], F32), hr=Mx.lo_alloc([H], F32),
                        ra=Mx.lo_alloc([H * hf], F32), rb=Mx.lo_alloc([H * hf], F32), ra2=Mx.lo_alloc([H * hf], F32), rb2=Mx.lo_alloc([H * hf], F32))

        def hnr_stages(tag, T, src, skeys, H, Dh, gaint, gkey, ro, hf, cos_, sin_, dst, dkey, np_=128):
            n = H * Dh
            t1v = T["t1"][0:np_, 0:n].rearrange("p (h d) -> p h d", h=H)
            hs = T["hs"][0:np_, 0:H]; hr = T["hr"][0:np_, 0:H]
            x1 = t1v[:, :, ro:ro + hf]; x2 = t1v[:, :, ro + hf:ro + 2 * hf]
            cb = cos_.unsqueeze(1).to_broadcast([np_, H, hf]); sb_ = sin_.unsqueeze(1).to_broadcast([np_, H, hf])
            rv = {k: T[k][0:np_, 0:H * hf].rearrange("p (h d) -> p h d", h=H) for k in ("ra", "rb", "ra2", "rb2")}
            tr = ["Mout", "Nout", "Cout"]
            k_ = lambda nm: (tag, nm)
            st = []
            st.append(lambda: act(t1v, src, AF.Square, r=skeys, w=[k_("t1")]))
            st.append(lambda: S.op("dve", lambda e: e.tensor_reduce(out=hs, in_=t1v, axis=AX.X, op=ALU.add), r=[k_("t1")], w=[k_("hs")]))
            st.append(lambda: act(hr, hs, AF.Sqrt, r=[k_("hs")], w=[k_("hr")], scale=1.0 / Dh, bias=EPS))
            st.append(lambda: S.op("dve", lambda e: e.reciprocal(out=hr, in_=hr), r=[k_("hr")], w=[k_("hr")]))
            st.append(lambda: S.op("dve", lambda e: e.tensor_tensor(out=t1v, in0=src, in1=hr.unsqueeze(2).to_broadcast([np_, H, Dh]), op=ALU.mult), r=skeys + [k_("hr"), k_("hs")], w=[k_("t1")]))
            st.append(lambda: S.op("dve", lambda e: e.tensor_tensor(out=t1v, in0=t1v, in1=gaint[0:np_].unsqueeze(1).to_broadcast([np_, H, Dh]), op=ALU.mult), r=[k_("t1"), gkey], w=[k_("t1")]))
            st.append(lambda: S.op("dve", lambda e: e.tensor_tensor(out=rv["ra"], in0=x1, in1=cb, op=ALU.mult), r=[k_("t1")] + tr, w=[k_("ra")]))
            st.append(lambda: S.op("dve", lambda e: e.tensor_tensor(out=rv["rb"], in0=x2, in1=sb_, op=ALU.mult), r=[k_("t1")] + tr, w=[k_("rb")]))
            st.append(lambda: S.op("dve", lambda e: e.tensor_tensor(out=dst[:, :, ro:ro + hf], in0=rv["ra"], in1=rv["rb"], op=ALU.subtract), r=[k_("ra"), k_("rb")], w=[dkey]))
            st.append(lambda: S.op("dve", lambda e: e.tensor_tensor(out=rv["ra2"], in0=x2, in1=cb, op=ALU.mult), r=[k_("t1")] + tr, w=[k_("ra2")]))
            st.append(lambda: S.op("dve", lambda e: e.tensor_tensor(out=rv["rb2"], in0=x1, in1=sb_, op=ALU.mult), r=[k_("t1")] + tr, w=[k_("rb2")]))
            st.append(lambda: S.op("dve", lambda e: e.tensor_tensor(out=dst[:, :, ro + hf:ro + 2 * hf], in0=rv["ra2"], in1=rv["rb2"], op=ALU.add), r=[k_("ra2"), k_("rb2")], w=[dkey]))

            def copies():
                if ro > 0:
                    S.op("pool", lambda e: e.tensor_copy(out=dst[:, :, 0:ro], in_=t1v[:, :, 0:ro]), r=[k_("t1")], w=[dkey])
                if ro + 2 * hf < Dh:
                    S.op("pool", lambda e: e.tensor_copy(out=dst[:, :, ro + 2 * hf:Dh], in_=t1v[:, :, ro + 2 * hf:Dh]), r=[k_("t1")], w=[dkey])
            st.insert(6, copies)
            return st

        def run_interleaved(chains):
            for s_ in range(max(len(c_) for c_ in chains)):
                for c_ in chains:
                    if s_ < len(c_):
                        c_[s_]()

        def head_norm_rope(src, skeys, H, Dh, gaint, gkey, ro, hf, cos_, sin_, dst, dkey, np_=128):
            n = H * Dh
            t1v = t1[0:np_, 0:n].rearrange("p (h d) -> p h d", h=H)
            t2v = t2[0:np_, 0:n].rearrange("p (h d) -> p h d", h=H)
            hs = hss[0:np_, 0:H]; hr = hrs[0:np_, 0:H]
            act(t1v, src, AF.Square, r=skeys, w=["t1"])
            S.op("dve", lambda e: e.tensor_reduce(out=hs, in_=t1v, axis=AX.X, op=ALU.add), r=["t1"], w=["hss"])
            act(hr, hs, AF.Sqrt, r=["hss"], w=["hrs"], scale=1.0 / Dh, bias=EPS)
            S.op("dve", lambda e: e.reciprocal(out=hr, in_=hr), r=["hrs"], w=["hrs"])
            S.op("dve", lambda e: e.tensor_tensor(out=t2v, in0=src, in1=hr.unsqueeze(2).to_broadcast([np_, H, Dh]), op=ALU.mult), r=skeys + ["hrs"], w=["t2"])
            S.op("dve", lambda e: e.tensor_tensor(out=t1v, in0=t2v, in1=gaint[0:np_].unsqueeze(1).to_broadcast([np_, H, Dh]), op=ALU.mult), r=["t2", gkey], w=["t1"])
            x1 = t1v[:, :, ro:ro + hf]; x2 = t1v[:, :, ro + hf:ro + 2 * hf]
            cb = cos_.unsqueeze(1).to_broadcast([np_, H, hf]); sb_ = sin_.unsqueeze(1).to_broadcast([np_, H, hf])
            rav = ra[0:np_, 0:H * hf].rearrange("p (h d) -> p h d", h=H)
            rbv = rbb[0:np_, 0:H * hf].rearrange("p (h d) -> p h d", h=H)
            tr = ["Mout", "Nout", "Cout"]
            S.op("dve", lambda e: e.tensor_tensor(out=rav, in0=x1, in1=cb, op=ALU.mult), r=["t1"] + tr, w=["ra"])
            S.op("dve", lambda e: e.tensor_tensor(out=rbv, in0=x2, in1=sb_, op=ALU.mult), r=["t1"] + tr, w=["rbb"])
            S.op("dve", lambda e: e.tensor_tensor(out=dst[:, :, ro:ro + hf], in0=rav, in1=rbv, op=ALU.subtract), r=["ra", "rbb"], w=[dkey])
            S.op("dve", lambda e: e.tensor_tensor(out=rav, in0=x2, in1=cb, op=ALU.mult), r=["t1"] + tr, w=["ra"])
            S.op("dve", lambda e: e.tensor_tensor(out=rbv, in0=x1, in1=sb_, op=ALU.mult), r=["t1"] + tr, w=["rbb"])
            S.op("dve", lambda e: e.tensor_tensor(out=dst[:, :, ro + hf:ro + 2 * hf], in0=rav, in1=rbv, op=ALU.add), r=["ra", "rbb"], w=[dkey])
            if ro > 0:
                S.op("pool", lambda e: e.tensor_copy(out=dst[:, :, 0:ro], in_=t1v[:, :, 0:ro]), r=["t1"], w=[dkey])
            if ro + 2 * hf < Dh:
                S.op("pool", lambda e: e.tensor_copy(out=dst[:, :, ro + 2 * hf:Dh], in_=t1v[:, :, ro + 2 * hf:Dh]), r=["t1"], w=[dkey])

        Msub = Mem(big, NB); Msub.lo = lo_pers; Msub.hi = lo0
        rawq = [Msub.lo_alloc([768], F32) for _ in range(2)]; rawk = [Msub.lo_alloc([768], F32), M.lo_alloc([768], F32)]
        Tq = [mk_tmps(Msub, 768, 8, 16) for _ in range(2)]; Tk = [mk_tmps(Msub, 768, 8, 16) for _ in range(2)]

        def b2_front(i):
            ts = slice(i * 128, (i + 1) * 128)
            par = i % 2
            bA, bB = nb(), nb()
            for k in range(6):
                mm(P[bA][:, :], cqT[:, k, ts], Wqb[:, k, 0:512], k == 0, k == 5, r=["Wqb", ("cqT", i // 4)], w=[PK[bA]])
            for k in range(6):
                mm(P[bB][:, 0:256], cqT[:, k, ts], Wqb[:, k, 512:768], k == 0, k == 5, r=["Wqb", ("cqT", i // 4)], w=[PK[bB]])
            act(rawq[par][:, 0:512], P[bA][:, :], AF.Copy, r=[PK[bA]], w=[("rawq", par)])
            act(rawq[par][:, 512:768], P[bB][:, 0:256], AF.Copy, r=[PK[bB]], w=[("rawq", par)])
            bA, bB = nb(), nb()
            for hh, bb in ((0, bA), (1, bB)):
                for k in range(2):
                    mm(P[bb][:, :], ckvT[:, k, ts], Wkvb[:, k, hh * 512:(hh + 1) * 512], k == 0, k == 1, r=["Wkvb", ("ckvT", i // 4)], w=[PK[bb]])
            rv = rawk[par].rearrange("p (h d) -> p h d", h=8)
            for hh, bb in ((0, bA), (1, bB)):
                pv = P[bb][:, :].rearrange("p (h d) -> p h d", h=4)
                act(rv[:, hh * 4:(hh + 1) * 4, 0:64], pv[:, :, 0:64], AF.Copy, r=[PK[bb]], w=[("rawk", par)])
                act(V[:, i, hh * 4:(hh + 1) * 4, 0:64], pv[:, :, 64:128], AF.Copy, r=[PK[bb]], w=[("V", i)])
            S.op("pool", lambda e, i=i: e.tensor_copy(out=rv[:, :, 64:96], in_=kpe[:, i, :].unsqueeze(1).to_broadcast([128, 8, 32])), r=[("kpe", i)], w=[("rawk", par)])

        def b2_chains(i):
            par = i % 2
            dq = drq[par].rearrange("p (h d) -> p h d", h=8); dk = drk[par].rearrange("p (h d) -> p h d", h=8)
            cq_ = hnr_stages(("cq", par), Tq[par], rawq[par].rearrange("p (h d) -> p h d", h=8), [("rawq", par)], 8, 96, qgt, "qgt", 64, 16, cosM[:, i, :], sinM[:, i, :], dq, ("drq", par))
            ck_ = hnr_stages(("ck", par), Tk[par], rawk[par].rearrange("p (h d) -> p h d", h=8), [("rawk", par)], 8, 96, kgt, "kgt", 64, 16, cosM[:, i, :], sinM[:, i, :], dk, ("drk", par))
            return [cq_[:6], ck_[:6]], [cq_[6:], ck_[6:]]

        def b2_out(i):
            ts = slice(i * 128, (i + 1) * 128)
            par = i % 2
            dq = drq[par].rearrange("p (h d) -> p h d", h=8); dk = drk[par].rearrange("p (h d) -> p h d", h=8)
            for (dd, dkey_, dstT, okey) in ((dq, ("drq", par), QT, "QT"), (dk, ("drk", par), KT, "KT")):
                b = nb(); pb = P[b][:, :].bitcast(BF16)
                for h in range(8):
                    tp(pb[0:96, h * 128:(h + 1) * 128], dd[:, h, :], ident, r=[dkey_, "ident"], w=[PK[b]])
                act(dstT[0:96, :, ts], pb[0:96, :].rearrange("p (h t) -> p h t", h=8), AF.Copy, r=[PK[b]], w=[(okey, i)])

        b2_front(0); b2_front(1)
        h1_, h2_ = b2_chains(0)
        run_interleaved(h1_)
        for i in range(NT):
            if i + 2 < NT:
                b2_front(i + 2)
            nxt = b2_chains(i + 1) if i + 1 < NT else ([], [])
            run_interleaved(nxt[0] + h2_)
            h2_ = nxt[1]
            if i >= 1:
                b2_out(i - 1)
        b2_out(NT - 1)
        QTk = [("QT", i) for i in range(NT)]
        dbg_out("QT", QT[0:96].rearrange("p k t -> p (k t)"), QTk)
        dbg_out("KT", KT[0:96].rearrange("p k t -> p (k t)"), [("KT", i) for i in range(NT)])
        dbg_out("V", V.rearrange("p a b c -> p (a b c)"), [("V", i) for i in range(NT)] + ["Vones"])
        if upto <= 3:
            return finish(nc, S, st)
        S.barrier()

        nbanks[0] = 6
        M.lo = lo0
        om = M.lo_alloc([NT, 512], BF16)
        PT = [M.lo_alloc([512], BF16) for _ in range(4)]
        rec4 = [M.lo_alloc([4], F32) for _ in range(2)]
        pt_i = [0]

        gchunk = [0]

        def causal_attn_multi(jobs):
            steps = []
            for ji in range(len(jobs)):
                for c in range(4):
                    for kt in range(4 * c + 4):
                        steps.append((ji, c, kt))

            def emit_qk(step):
                ji, c, kt = step
                J = jobs[ji]
                q0 = max(kt - 4 * c, 0)
                n = 512 - 128 * q0
                b = nb()
                has_extra = J["extra"] is not None
                mm(P[b][:, 0:n], J["KT"][0:J["kn"], kt * 128:(kt + 1) * 128], J["QT"](c * 512 + q0 * 128, (c + 1) * 512),
                   True, not has_extra, r=J["kk"](kt) + J["qk"](c), w=[PK[b]])
                if has_extra:
                    J["extra"](P[b][:, 0:n], kt, c * 512 + q0 * 128, (c + 1) * 512, PK[b])
                return b, n, q0

            LA = 2
            pend = [emit_qk(steps[q]) for q in range(min(LA, len(steps)))]
            for si, (ji, c, kt) in enumerate(steps):
                J = jobs[ji]
                b, n, q0 = pend.pop(0)
                if si + LA < len(steps):
                    pend.append(emit_qk(steps[si + LA]))
                if kt == 0:
                    gchunk[0] += 1
                ab = 6 + (gchunk[0] % 2)
                Oacc = P[ab][:, 0:260].rearrange("p (q d) -> p q d", q=4)
                pi = pt_i[0]; pt_i[0] = (pi + 1) % len(PT)
                pt = PT[pi]
                act(pt[:, 0:n], P[b][:, 0:n], AF.Exp, r=[PK[b]], w=[("PT", pi)], scale=J["scale"])
                if kt >= 4 * c:
                    S.op("dve", lambda e, pt=pt: e.tensor_tensor(out=pt[:, 0:128], in0=pt[:, 0:128], in1=tri, op=ALU.mult), r=[("PT", pi), "tri"], w=[("PT", pi)])
                for qi in range(q0, 4):
                    mm(Oacc[:, qi, :], pt[:, (qi - q0) * 128:(qi - q0 + 1) * 128], J["V"](kt), kt == 0 and qi == 0, kt == 4 * c + qi,
                       r=[("PT", pi)] + J["vk"](kt), w=[PK[ab]], skip_group_check=True)
                if kt == 4 * c + 3:
                    J["fin"](c, Oacc, PK[ab])

        jobs = []
        for h in range(8):
            def fin(c, Oacc, pk, h=h):
                rc = rec4[c % 2]
                S.op("dve", lambda e: e.reciprocal(out=rc, in_=Oacc[:, :, 64]), r=[pk], w=[("rec4", c % 2)])
                S.op("dve", lambda e: e.tensor_tensor(out=om[:, 4 * c:4 * c + 4, h * 64:(h + 1) * 64], in0=Oacc[:, :, 0:64],
                                                      in1=rc.unsqueeze(2).to_broadcast([128, 4, 64]), op=ALU.mult),
                     r=[pk, ("rec4", c % 2)], w=[("om", c)])
            jobs.append(dict(KT=KT[:, h, :], kn=96, QT=(lambda a, b_, h=h: QT[0:96, h, a:b_]), V=(lambda kt, h=h: V[:, kt, h, :]), scale=96 ** -0.5,
                             extra=None, fin=fin, qk=(lambda c: [("QT", 4 * c + q) for q in range(4)]), kk=(lambda kt: [("KT", kt)]),
                             vk=(lambda kt: [("V", kt), "Vones"])))
        causal_attn_multi(jobs)
        for i in range(NT):
            b = nb(); pb = P[b][:, :].bitcast(BF16)
            for j in range(4):
                tp(pb[:, j * 128:(j + 1) * 128], om[:, i, j * 128:(j + 1) * 128], ident, r=[("om", i // 4), "ident"], w=[PK[b]])
            act(omT[:, :, i * 128:(i + 1) * 128], pb[:, 0:512].rearrange("p (k t) -> p k t", k=4), AF.Copy, r=[PK[b]], w=[("omT", i)])
        dbg_out("om", om.rearrange("p a b -> p (a b)"), [("om", c) for c in range(4)])
        if upto <= 4:
            return finish(nc, S, st)
        S.barrier()

        nbanks[0] = 4
        M.lo = lo0; M.hi = hi0
        qnT = M.lo_alloc([8, S_], BF16); ksT = M.lo_alloc([2, S_], BF16); kwT = M.lo_alloc([2, S_], BF16)
        vs = M.lo_alloc([NT, 2, 65], BF16); vw = M.lo_alloc([NT, 2, 65], BF16)
        gates = M.lo_alloc([NT, 3, 8], F32)
        kcmpT = M.lo_alloc([2, 128], BF16); VCX = M.lo_alloc([2, 97], BF16)
        PT = [M.lo_alloc([512], BF16) for _ in range(4)]
        rec4 = [M.lo_alloc([4], F32) for _ in range(2)]
        t1 = M.lo_alloc([512], F32); t2 = M.lo_alloc([512], F32)
        hss = M.lo_alloc([8], F32); hrs = M.lo_alloc([8], F32)
        ra = M.lo_alloc([128], F32); rbb = M.lo_alloc([128], F32)
        drb = [M.lo_alloc([512], BF16) for _ in range(2)]
        nqt = M.lo_alloc([64], F32); nkct = M.lo_alloc([64], F32); nkst = M.lo_alloc([64], F32); nkwt = M.lo_alloc([64], F32)
        lo2 = M.lo
        kc2 = M.hi_alloc([2, S_], BF16); vc2 = M.hi_alloc([2, S_], BF16)
        hi_kv = M.hi
        hT = M.hi_alloc([8, S_], BF16)
        Wqn = M.hi_alloc([8, 512], BF16); Wkc2 = M.hi_alloc([8, 256], BF16); Wvc2 = M.hi_alloc([8, 256], BF16)
        Wkv4 = M.hi_alloc([8, 512], BF16); Wgn = M.hi_alloc([8, 24], BF16)
        ge = M.hi_alloc([24], F32)
        S.dma("sp", hT.rearrange("p k t -> p (k t)"), hTs, r=["hTs"], w=["hTall"])
        for t_, d_ in ((nqt, nqB), (nkct, nkcB), (nkst, nksB), (nkwt, nkwB)):
            S.dma("sp", t_, d_, w=["ngain"])
        load_w(Wqn, w_qn, "Wqn", 8, 512); load_w(Wkv4, w_kv4, "Wkv4", 8, 512); load_w(Wgn, w_gn, "Wgn", 8, 24)
        load_w(Wkc2, w_kc2, "Wkc2", 8, 256); load_w(Wvc2, w_vc2, "Wvc2", 8, 256)
        for g_ in range(2):
            S.dma("sp", ksT[64:96, g_, :], XEall_d, w=["ksTx"])
        S.op("pool", lambda e: e.memset(vs[:, :, :, 64:65], 1.0), w=["vsones"])
        S.op("pool", lambda e: e.memset(vw[:, :, :, 64:65], 1.0), w=["vwones"])
        S.op("pool", lambda e: e.memset(kc2[64:128, :, S_ - 1:S_], 0.0), w=["kc2pad"])
        S.op("pool", lambda e: e.memset(vc2[64:128, :, S_ - 1:S_], 0.0), w=["vc2pad"])
        MsubD = Mem(big, NB); MsubD.lo = lo_pers + 16384; MsubD.hi = lo0
        TDq = [mk_tmps(MsubD, 512, 8, 8) for _ in range(2)]; TDs = [mk_tmps(MsubD, 128, 2, 8) for _ in range(2)]; TDw = [mk_tmps(MsubD, 128, 2, 8) for _ in range(2)]
        dnq = [MsubD.lo_alloc([512], BF16) for _ in range(2)]
        dns = [MsubD.lo_alloc([128], BF16) for _ in range(2)]; dnw = [MsubD.lo_alloc([128], BF16) for _ in range(2)]

        def d_front(i):
            ts = slice(i * 128, (i + 1) * 128)
            bq = 4 + 2 * (i % 2)
            for k in range(8):
                mm(P[bq][:, :], hT[:, k, ts], Wqn[:, k, :], k == 0, k == 7, r=["hTall", "Wqn"], w=[PK[bq]])
            bk = 5 + 2 * (i % 2)
            for k in range(8):
                mm(P[bk][:, :], hT[:, k, ts], Wkv4[:, k, :], k == 0, k == 7, r=["hTall", "Wkv4"], w=[PK[bk]])
            bg = nb()
            for k in range(8):
                mm(P[bg][:, 0:24], hT[:, k, ts], Wgn[:, k, :], k == 0, k == 7, r=["hTall", "Wgn"], w=[PK[bg]])
            act(vs[:, i, :, 0:64], P[bk][:, 256:384].rearrange("p (g d) -> p g d", g=2), AF.Copy, r=[PK[bk]], w=[("vs", i)])
            act(vw[:, i, :, 0:64], P[bk][:, 384:512].rearrange("p (g d) -> p g d", g=2), AF.Copy, r=[PK[bk]], w=[("vw", i)])
            act(ge, P[bg][:, 0:24], AF.Exp, r=[PK[bg]], w=["ge"], scale=-1.0)
            S.op("dve", lambda e: e.tensor_scalar(out=ge, in0=ge, scalar1=1.0, scalar2=None, op0=ALU.add), r=["ge"], w=["ge"])
            S.op("dve", lambda e, i=i: e.reciprocal(out=gates[:, i].rearrange("p a b -> p (a b)"), in_=ge), r=["ge"], w=[("gates", i)])
            return bq, bk

        def d_chains(i, bq, bk):
            par = i % 2
            dq = dnq[par].rearrange("p (h d) -> p h d", h=8)
            ds_ = dns[par].rearrange("p (h d) -> p h d", h=2); dw_ = dnw[par].rearrange("p (h d) -> p h d", h=2)
            c1 = hnr_stages(("dq", par), TDq[par], P[bq][:, :].rearrange("p (h d) -> p h d", h=8), [PK[bq]], 8, 64, nqt, "ngain", 0, 8, cosN[:, i, :], sinN[:, i, :], dq, ("dnq", par))
            c2 = hnr_stages(("ds", par), TDs[par], P[bk][:, 0:128].rearrange("p (h d) -> p h d", h=2), [PK[bk]], 2, 64, nkst, "ngain", 0, 8, cosN[:, i, :], sinN[:, i, :], ds_, ("dns", par))
            c3 = hnr_stages(("dw", par), TDw[par], P[bk][:, 128:256].rearrange("p (h d) -> p h d", h=2), [PK[bk]], 2, 64, nkwt, "ngain", 0, 8, cosN[:, i, :], sinN[:, i, :], dw_, ("dnw", par))
            return [c1[:6], c2[:6], c3[:6]], [c1[6:], c2[6:], c3[6:]]

        def d_out(i):
            ts = slice(i * 128, (i + 1) * 128)
            par = i % 2
            b = nb(); pb = P[b][:, :].bitcast(BF16)
            for p_ in range(8):
                tp(pb[0:64, p_ * 128:(p_ + 1) * 128], dnq[par][:, p_ * 64:(p_ + 1) * 64], ident, r=[("dnq", par), "ident"], w=[PK[b]])
            act(qnT[0:64, :, ts], pb[0:64, :].rearrange("p (k t) -> p k t", k=8), AF.Copy, r=[PK[b]], w=[("qnT", i)])
            for (dd, dkey_, dstT, dk) in ((dns[par], ("dns", par), ksT, "ksT"), (dnw[par], ("dnw", par), kwT, "kwT")):
                b2 = nb(); pb = P[b2][:, :].bitcast(BF16)
                for g_ in range(2):
                    tp(pb[0:64, g_ * 128:(g_ + 1) * 128], dd[:, g_ * 64:(g_ + 1) * 64], ident, r=[dkey_, "ident"], w=[PK[b2]])
                act(dstT[0:64, :, ts], pb[0:64, 0:256].rearrange("p (g t) -> p g t", g=2), AF.Copy, r=[PK[b2]], w=[(dk, i)])

        fr_ = {0: d_front(0), 1: d_front(1)}
        h1_, h2_ = d_chains(0, *fr_[0])
        run_interleaved(h1_)
        for i in range(NT):
            if i + 2 < NT:
                fr_[i + 2] = d_front(i + 2)
            nxt = d_chains(i + 1, *fr_[i + 1]) if i + 1 < NT else ([], [])
            run_interleaved(nxt[0] + h2_)
            h2_ = nxt[1]
            if i >= 1:
                d_out(i - 1)
        d_out(NT - 1)
        for c in range(4):
            for (Wt, wk, dst, dk) in ((Wkc2, "Wkc2", kc2, "kc2"), (Wvc2, "Wvc2", vc2, "vc2")):
                for g in range(2):
                    b = nb()
                    for k in range(8):
                        mm(P[b][:, :], Wt[:, k, g * 128:(g + 1) * 128], hT[:, k, c * 512:(c + 1) * 512], k == 0, k == 7, r=["hTall", wk], w=[PK[b]])
                    act(dst[0:64, g, c * 512:(c + 1) * 512], P[b][0:64, :], AF.Copy, r=[PK[b]], w=[dk])
                    if c == 0:
                        act(dst[64:128, g, 0:511], P[b][64:128, 1:512], AF.Copy, r=[PK[b]], w=[dk])
                    else:
                        act(dst[64:128, g, c * 512 - 1:(c + 1) * 512 - 1], P[b][64:128, :], AF.Copy, r=[PK[b]], w=[dk])
        dbg_out("qnT", qnT[0:64].rearrange("p k t -> p (k t)"), [("qnT", i) for i in range(NT)])
        dbg_out("ksT", ksT[0:64].rearrange("p k t -> p (k t)"), [("ksT", i) for i in range(NT)])
        dbg_out("gates", gates.rearrange("p a b c -> p (a b c)"), [("gates", i) for i in range(NT)])
        dbg_out("kc2", kc2.rearrange("p k t -> p (k t)"), ["kc2", "kc2pad"])
        if upto <= 5:
            return finish(nc, S, st)
        S.barrier()

        nbanks[0] = 8
        M.hi = hi_kv
        hiE = M.hi
        W1k = M.lo_alloc([16, 256], BF16); W1v = M.lo_alloc([16, 256], BF16)
        W2k = M.lo_alloc([2, 64], BF16); W2v = M.lo_alloc([2, 64], BF16)
        pkf = M.lo_alloc([16], F32); pvf = M.lo_alloc([16], F32); pkb = M.lo_alloc([16], BF16); pvb = M.lo_alloc([16], BF16)
        biask = M.lo_alloc([2], F32); biasv = M.lo_alloc([2], F32)
        hid = [M.lo_alloc([128], BF16) for _ in range(2)]
        ovl = M.lo_alloc([32], BF16)
        load_w(W1k, w1k, "W1k", 16, 256); load_w(W1v, w1v, "W1v", 16, 256)
        load_w(W2k, w2k, "W2k", 2, 64); load_w(W2v, w2v, "W2v", 2, 64)
        S.dma("sp", pkf, posk, w=["pkf"]); S.dma("sp", pvf, posv, w=["pvf"]); S.dma("sp", ovl[0:127], ovl_d, w=["ovl"])
        S.op("pool", lambda e: e.memset(VCX[0:127, :, 64:65], 1.0), w=["VCXa"])
        for g in range(2):
            S.op("pool", lambda e, g=g: e.tensor_copy(out=VCX[0:127, g, 65:97], in_=ovl[0:127]), r=["ovl"], w=["VCXb"])
        rt = [M.lo_alloc([128], BF16) for _ in range(3)]
        rt_i = [0]
        for (W1, w1key, W2, w2key, src, skey, posf_, pkey, isk) in ((W1k, "W1k", W2k, "W2k", kc2, ["kc2", "kc2pad"], pkf, "pkf", True),
                                                                    (W1v, "W1v", W2v, "W2v", vc2, ["vc2", "vc2pad"], pvf, "pvf", False)):
            srcv = src.rearrange("p g (n s) -> p g n s", s=16)
            bo = nb()
            for g in range(2):
                bh = []
                for hc in range(2):
                    b = nb()
                    while b == bo or b in bh:
                        b = nb()
                    bh.append(b)
                for lc in range(16):
                    ri = rt_i[0]; rt_i[0] = (ri + 1) % 3
                    rtv = rt[ri][:, 0:127]
                    S.op("dve", lambda e, rtv=rtv, g=g, lc=lc, srcv=srcv, posf_=posf_: e.tensor_scalar(
                        out=rtv, in0=srcv[:, g, (2 * lc) // 16:(2 * lc) // 16 + 127, (2 * lc) % 16], scalar1=posf_[:, lc:lc + 1], scalar2=None, op0=ALU.add),
                        r=skey + [pkey], w=[("rt", ri)])
                    for hc in range(2):
                        mm(P[bh[hc]][:, 0:127], W1[:, lc, hc * 128:(hc + 1) * 128], rtv, lc == 0, lc == 15, r=[w1key, ("rt", ri)], w=[PK[bh[hc]]], sig=(hc == 1 or lc == 15))
                for hc in range(2):
                    act(hid[hc][:, 0:127], P[bh[hc]][:, 0:127], AF.Silu, r=[PK[bh[hc]]], w=[("hid", hc)])
                for hc in range(2):
                    mm(P[bo][0:127, g * 64:(g + 1) * 64], hid[hc][:, 0:127], W2[:, hc, :], hc == 0, hc == 1, r=[("hid", hc), w2key], w=[PK[bo]])
            if isk:
                d = drb[1][0:127, 0:128].rearrange("p (h d) -> p h d", h=2)
                head_norm_rope(P[bo][0:127, 0:128].rearrange("p (h d) -> p h d", h=2), [PK[bo]], 2, 64, nkct, "ngain", 0, 8, cosC[0:127], sinC[0:127], d, "drb1", np_=127)
                b2 = nb(); pb = P[b2][:, :].bitcast(BF16)
                for g_ in range(2):
                    tp(pb[0:64, g_ * 128:g_ * 128 + 127], drb[1][0:127, g_ * 64:(g_ + 1) * 64], ident[0:127, 0:127], r=["drb1", "ident"], w=[PK[b2]])
                act(kcmpT[0:64, :, 0:127], pb[0:64, 0:256].rearrange("p (g t) -> p g t", g=2)[:, :, 0:127], AF.Copy, r=[PK[b2]], w=["kcmpT"])
            else:
                act(VCX[0:127, :, 0:64], P[bo][0:127, 0:128].rearrange("p (g d) -> p g d", g=2), AF.Copy, r=[PK[bo]], w=["VCXc"])
        dbg_out("kcmpT", kcmpT[0:64].rearrange("p g t -> p (g t)"), ["kcmpT"])
        dbg_out("VCX", VCX[0:127].rearrange("p a b -> p (a b)"), ["VCXa", "VCXb", "VCXc"])
        if upto <= 6:
            return finish(nc, S, st)
        S.barrier()

        M.lo = lo2; M.hi = hi0
        onsa = M.hi_alloc([NT, 512], F32)
        mbT = M.hi_alloc([2, S_], BF16)
        vmT = M.hi_alloc([S_], BF16); XE = M.hi_alloc([NT, 128], BF16)
        fb = M.hi_alloc([NT, 32], F32); vj = M.hi_alloc([NT, 32], F32)
        pc = [M.lo_alloc([4, 128], BF16) for _ in range(2)]
        rsum = M.lo_alloc([8], F32); rec8 = M.lo_alloc([8], F32); gr = M.lo_alloc([8], F32)
        tmp_i = M.lo_alloc([8, 32], F32); imp = M.lo_alloc([2, 32], F32); m8 = M.lo_alloc([2, 8], F32)
        sel = M.lo_alloc([2, 32], F32); mbf = M.lo_alloc([2, 96], BF16)
        S.op("pool", lambda e: e.memset(mbf, 0.0), w=["mbf"])
        tmpo = M.lo_alloc([8, 64], F32)
        pw = [M.lo_alloc([3, 128], BF16) for _ in range(4)]
        onb = [M.lo_alloc([512], BF16) for _ in range(2)]
        S.dma("sp", vmT[0:127], vmT_d, w=["vmT"]); S.dma("sp", XE[0:32].rearrange("p a b -> p (a b)"), XE_d, w=["XE"])
        S.dma("sp", fb.rearrange("p a b -> p (a b)"), fb_d, w=["fb"]); S.dma("sp", vj.rearrange("p a b -> p (a b)"), vj_d, w=["vj"])
        VCXk = ["VCXa", "VCXb", "VCXc"]
        for i in range(NT):
            ts = slice(i * 128, (i + 1) * 128)
            import os
            if int(os.environ.get('KDEV_F', '9')) <= 0:
                continue
            sb_ = [nb(), nb()]
            ob = [nb(), nb()]
            for p in range(8):
                j, g = p // 2, p % 2
                if os.environ.get('KDEV_G0'):
                    g = 0
                mm(P[sb_[p // 4]][0:127, (p % 4) * 128:(p % 4 + 1) * 128], kcmpT[0:64, g, 0:127], qnT[0:64, p, ts], True, True,
                   r=["kcmpT", ("qnT", i)], w=[PK[sb_[p // 4]]])
            for hf_ in range(2):
                pcv = pc[hf_]
                act(pcv[0:127], P[sb_[hf_]][0:127, :].rearrange("p (a b) -> p a b", a=4), AF.Exp, r=[PK[sb_[hf_]]], w=[("pc", hf_)], scale=0.125)
                S.op("dve", lambda e, pcv=pcv: e.tensor_tensor(out=pcv[0:127], in0=pcv[0:127], in1=vmT[0:127, ts].unsqueeze(1).to_broadcast([127, 4, 128]), op=ALU.mult),
                     r=[("pc", hf_), "vmT"], w=[("pc", hf_)])
            import os
            FL = int(os.environ.get('KDEV_F', '9'))
            if FL <= 1:
                continue
            for p in range(8):
                g = p % 2
                mm(P[ob[p // 4]][:, (p % 4) * 97:(p % 4 + 1) * 97], pc[p // 4][0:127, p % 4, :], VCX[0:127, g, :], True, True,
                   r=[("pc", p // 4)] + VCXk, w=[PK[ob[p // 4]]])
            if FL <= 2:
                continue
            OC = [P[ob[h_]][:, 0:388].rearrange("p (a b) -> p a b", a=4) for h_ in range(2)]
            for h_ in range(2):
                S.op("dve", lambda e, h_=h_: e.tensor_scalar(out=rsum[:, h_ * 4:(h_ + 1) * 4], in0=OC[h_][:, :, 64], scalar1=1e-30, scalar2=None, op0=ALU.max), r=[PK[ob[h_]]], w=["rsum"])
            S.op("dve", lambda e: e.reciprocal(out=rec8, in_=rsum), r=["rsum"], w=["rec8"])
            S.op("dve", lambda e, i=i: e.tensor_tensor(out=gr, in0=gates[:, i, 0, :], in1=rec8, op=ALU.mult), r=["rec8", ("gates", i)], w=["gr"])
            for h_ in range(2):
                S.op("dve", lambda e, h_=h_, i=i: e.tensor_tensor(out=onsa[:, i, h_ * 256:(h_ + 1) * 256].rearrange("p (a b) -> p a b", a=4), in0=OC[h_][:, :, 0:64],
                                                                   in1=gr[:, h_ * 4:(h_ + 1) * 4].unsqueeze(2).to_broadcast([128, 4, 64]), op=ALU.mult),
                     r=[PK[ob[h_]], "gr"], w=[("onsa", i)])
                S.op("dve", lambda e, h_=h_: e.tensor_tensor(out=tmp_i[:, h_ * 4:(h_ + 1) * 4, :], in0=OC[h_][:, :, 65:97],
                                                             in1=rec8[:, h_ * 4:(h_ + 1) * 4].unsqueeze(2).to_broadcast([128, 4, 32]), op=ALU.mult),
                     r=[PK[ob[h_]], "rec8"], w=["tmp_i"])
            if FL <= 3:
                continue
            S.op("dve", lambda e: e.tensor_reduce(out=imp, in_=tmp_i.rearrange("t (j g) n -> t g n j", g=2), axis=AX.X, op=ALU.add), r=["tmp_i"], w=["imp"])
            S.op("dve", lambda e, i=i: e.tensor_tensor(out=imp, in0=imp, in1=fb[:, i, :].unsqueeze(1).to_broadcast([128, 2, 32]), op=ALU.add), r=["imp", "fb"], w=["imp"])
            if FL <= 4:
                continue
            for g in range(2):
                S.op("dve", lambda e, g=g: e.max(out=m8[:, g, :], in_=imp[:, g, :]), r=["imp"], w=["m8"])
                S.op("dve", lambda e, g=g: e.tensor_scalar(out=sel[:, g, :], in0=imp[:, g, :], scalar1=m8[:, g, 7:8], scalar2=None, op0=ALU.is_ge), r=["imp", "m8"], w=["sel"])
            S.op("dve", lambda e, i=i: e.tensor_tensor(out=sel, in0=sel, in1=vj[:, i, :].unsqueeze(1).to_broadcast([128, 2, 32]), op=ALU.mult), r=["sel", "vj"], w=["sel"])
            S.op("dve", lambda e: e.tensor_scalar(out=mbf[:, :, 64:96], in0=sel, scalar1=-1.0, scalar2=30000.0, op0=ALU.add, op1=ALU.mult), r=["sel"], w=["mbf"])
            if i == 5:
                dbg_out("sel5", sel.rearrange("p a b -> p (a b)"), ["sel"])
                dbg_out("imp5", imp.rearrange("p a b -> p (a b)"), ["imp"])
            if FL <= 5:
                continue
            b = nb(); pb = P[b][:, :].bitcast(BF16)
            for g in range(2):
                tp(pb[0:96, g * 128:(g + 1) * 128], mbf[:, g, :], ident, r=["mbf", "ident"], w=[PK[b]])
            qm = qnT[64:96].rearrange("p (j g) t -> p j g t", g=2)
            for g in range(2):
                act(qm[:, :, g, ts], pb[64:96, g * 128:(g + 1) * 128].unsqueeze(1).to_broadcast([32, 4, 128]), AF.Copy, r=[PK[b]], w=[("mbq", i)])
        dbg_out("onsa_c", onsa.rearrange("p a b -> p (a b)"), [("onsa", i) for i in range(NT)])
        dbg_out("mbT", qnT[64:96, 0:2, :].rearrange("p a b -> p (a b)"), [("mbq", i) for i in range(NT)])
        if upto <= 7:
            return finish(nc, S, st)

        S.barrier()
        nbanks[0] = 6
        jobs = []
        for p in range(8):
            j, g = p // 2, p % 2

            def extra(ps_ap, kt, a, b_, pk, g=g):
                mm(ps_ap, XE[0:32, kt, :], mbT[0:32, g, a:b_], False, True, r=["XE"] + [("mbT", q) for q in range(a // 128, b_ // 128)], w=[pk])

            def fin(c, Oacc, pk, p=p):
                rc = rec4[c % 2]
                S.op("dve", lambda e: e.reciprocal(out=rc, in_=Oacc[:, :, 64]), r=[pk], w=[("rec4", c % 2)])
                S.op("dve", lambda e: e.tensor_tensor(out=rc, in0=rc, in1=gates[:, 4 * c:4 * c + 4, 1, p], op=ALU.mult), r=[("rec4", c % 2)] + [("gates", 4 * c + q) for q in range(4)], w=[("rec4", c % 2)])
                tv = tmpo[:, 0:4, :]
                S.op("dve", lambda e: e.tensor_tensor(out=tv, in0=Oacc[:, :, 0:64], in1=rc.unsqueeze(2).to_broadcast([128, 4, 64]), op=ALU.mult), r=[pk, ("rec4", c % 2)], w=["tmpo"])
                S.op("pool", lambda e: e.tensor_tensor(out=onsa[:, 4 * c:4 * c + 4, p * 64:(p + 1) * 64], in0=onsa[:, 4 * c:4 * c + 4, p * 64:(p + 1) * 64], in1=tv, op=ALU.add),
                     r=["tmpo"] + [("onsa", 4 * c + q) for q in range(4)], w=[("onsa", 4 * c + q) for q in range(4)])
            jobs.append(dict(KT=ksT[:, g, :], kn=96, QT=(lambda a, b_, p=p: qnT[0:96, p, a:b_]), V=(lambda kt, g=g: vs[:, kt, g, :]), scale=0.125,
                             extra=None, fin=fin, qk=(lambda c: [("qnT", 4 * c + q) for q in range(4)] + [("mbq", 4 * c + q) for q in range(4)]), kk=(lambda kt: [("ksT", kt), "ksTx"]),
                             vk=(lambda kt: [("vs", kt), "vsones"])))
        causal_attn_multi(jobs)
        dbg_out("onsa_cs", onsa.rearrange("p a b -> p (a b)"), [("onsa", i) for i in range(NT)])
        if upto <= 8:
            return finish(nc, S, st)

        pw_i = [0]
        ob = [6, 7]

        def emit_ws(i, p):
            g = p % 2
            kts = [kt for kt in (i - 2, i - 1, i) if kt >= 0]
            b = nb()
            for kt in kts:
                sl = kt - (i - 2)
                mm(P[b][:, sl * 128:(sl + 1) * 128], kwT[0:64, g, kt * 128:(kt + 1) * 128], qnT[0:64, p, i * 128:(i + 1) * 128], True, True,
                   r=[("kwT", kt), ("qnT", i)], w=[PK[b]])
            return b

        wsteps = [(i, p) for i in range(NT) for p in range(8)]
        WLA = 3
        wpend = [emit_ws(*wsteps[q]) for q in range(WLA)]
        for wi_, (i, p) in enumerate(wsteps):
            ts = slice(i * 128, (i + 1) * 128)
            g = p % 2
            kts = [kt for kt in (i - 2, i - 1, i) if kt >= 0]
            b = wpend.pop(0)
            if wi_ + WLA < len(wsteps):
                wpend.append(emit_ws(*wsteps[wi_ + WLA]))
            s0 = kts[0] - (i - 2)
            wi = pw_i[0]; pw_i[0] = (wi + 1) % len(pw)
            pwv = pw[wi]
            act(pwv[:, s0:3, :], P[b][:, s0 * 128:384].rearrange("p (a b) -> p a b", b=128), AF.Exp, r=[PK[b]], w=[("pw", wi)], scale=0.125)
            S.op("dve", lambda e, pwv=pwv, s0=s0: e.tensor_tensor(out=pwv[:, s0:3, :], in0=pwv[:, s0:3, :], in1=winm[:, s0:3, :], op=ALU.mult), r=[("pw", wi), "winm"], w=[("pw", wi)])
            for kt in kts:
                sl = kt - (i - 2)
                mm(P[ob[p // 4]][:, (p % 4) * 65:(p % 4 + 1) * 65], pwv[:, sl, :], vw[:, kt, g, :], kt == kts[0], kt == kts[-1],
                   r=[("pw", wi), ("vw", kt), "vwones"], w=[PK[ob[p // 4]]])
            if p != 7:
                continue
            OW = [P[ob[h_]][:, 0:260].rearrange("p (a b) -> p a b", a=4) for h_ in range(2)]
            for h_ in range(2):
                S.op("dve", lambda e, h_=h_: e.reciprocal(out=rec8[:, h_ * 4:(h_ + 1) * 4], in_=OW[h_][:, :, 64]), r=[PK[ob[h_]]], w=["rec8"])
            S.op("dve", lambda e, i=i: e.tensor_tensor(out=gr, in0=gates[:, i, 2, :], in1=rec8, op=ALU.mult), r=["rec8", ("gates", i)], w=["gr"])
            for h_ in range(2):
                S.op("dve", lambda e, h_=h_: e.tensor_tensor(out=tmpo[:, h_ * 4:(h_ + 1) * 4, :], in0=OW[h_][:, :, 0:64],
                                                             in1=gr[:, h_ * 4:(h_ + 1) * 4].unsqueeze(2).to_broadcast([128, 4, 64]), op=ALU.mult), r=[PK[ob[h_]], "gr"], w=["tmpo"])
            S.op("pool", lambda e, i=i: e.tensor_tensor(out=onb[i % 2], in0=onsa[:, i, :], in1=tmpo.rearrange("p a b -> p (a b)"), op=ALU.add), r=["tmpo", ("onsa", i)], w=[("onb", i % 2)])
            if "onsa_all" in D_:
                S.dma("sp", D_["onsa_all"][:, i * 512:(i + 1) * 512], onb[i % 2], r=[("onb", i % 2)])
            b = nb(); pb = P[b][:, :].bitcast(BF16)
            for j in range(4):
                tp(pb[:, j * 128:(j + 1) * 128], onb[i % 2][:, j * 128:(j + 1) * 128], ident, r=[("onb", i % 2), "ident"], w=[PK[b]])
            act(onT[:, :, ts], pb[:, 0:512].rearrange("p (k t) -> p k t", k=4), AF.Copy, r=[PK[b]], w=[("onT", i)])
        if upto <= 9:
            return finish(nc, S, st)
        S.barrier()

        nbanks[0] = 8
        M.lo = lo0; M.hi = hi0
        hT = M.hi_alloc([8, S_], BF16)
        mergedT = M.hi_alloc([8, S_], BF16)
        hi2 = M.hi
        Wgm = M.lo_alloc([8, 512], BF16); Wgnm = M.lo_alloc([8, 512], BF16); Wom = M.lo_alloc([4, 512], BF16); Won = M.lo_alloc([4, 512], BF16)
        e3 = M.lo_alloc([512], F32); e4 = M.lo_alloc([512], F32); tA = M.lo_alloc([512], F32); tB = M.lo_alloc([512], F32)
        mgb = [M.lo_alloc([512], BF16) for _ in range(2)]
        S.dma("sp", hT.rearrange("p k t -> p (k t)"), hTs, r=["hTs"], w=["hTall"])
        e3 = [e3, M.lo_alloc([512], F32)]; e4 = [e4, M.lo_alloc([512], F32)]
        tA = [tA, M.lo_alloc([512], F32)]; tB = [tB, M.lo_alloc([512], F32)]

        def i_front(cc, i, it):
            ts = slice(i * 128, (i + 1) * 128)
            bs = [4 * (it % 2) + q for q in range(4)]
            b1, b2, b3, b4 = bs
            for k in range(8):
                mm(P[b3][:, :], hT[:, k, ts], Wgm[:, k, :], k == 0, k == 7, r=["hTall", "Wgm"], w=[PK[b3]])
            for k in range(8):
                mm(P[b4][:, :], hT[:, k, ts], Wgnm[:, k, :], k == 0, k == 7, r=["hTall", "Wgnm"], w=[PK[b4]])
            for k in range(4):
                mm(P[b1][:, :], omT[:, k, ts], Wom[:, k, :], k == 0, k == 3, r=[("omT", i), "Wom"], w=[PK[b1]])
            for k in range(4):
                mm(P[b2][:, :], onT[:, k, ts], Won[:, k, :], k == 0, k == 3, r=[("onT", i), "Won"], w=[PK[b2]])
            return bs

        def i_back(cc, i, it, bs):
            ts = slice(i * 128, (i + 1) * 128)
            b1, b2, b3, b4 = bs
            par = it % 2
            act(e3[par], P[b3][:, :], AF.Sigmoid, r=[PK[b3]], w=[("e3", par)])
            act(e4[par], P[b4][:, :], AF.Sigmoid, r=[PK[b4]], w=[("e4", par)])
            S.op("dve", lambda e: e.tensor_tensor(out=tA[par], in0=P[b1][:, :], in1=e3[par], op=ALU.mult), r=[PK[b1], ("e3", par)], w=[("tA", par)])
            S.op("dve", lambda e: e.tensor_tensor(out=tB[par], in0=P[b2][:, :], in1=e4[par], op=ALU.mult), r=[PK[b2], ("e4", par)], w=[("tB", par)])
            S.op("dve", lambda e: e.tensor_tensor(out=mgb[par], in0=tA[par], in1=tB[par], op=ALU.add), r=[("tA", par), ("tB", par)], w=[("mgb", par)])
            if "merged" in D_:
                S.dma("sp", D_["merged"][i * 128:(i + 1) * 128, cc * 512:(cc + 1) * 512], mgb[par], r=[("mgb", par)])
            pb = P[b3][:, :].bitcast(BF16)
            for j in range(4):
                tp(pb[:, j * 128:(j + 1) * 128], mgb[par][:, j * 128:(j + 1) * 128], ident, r=[("mgb", par), "ident"], w=[PK[b3]])
            act(mergedT[:, cc * 4:(cc + 1) * 4, ts], pb[:, 0:512].rearrange("p (k t) -> p k t", k=4), AF.Copy, r=[PK[b3]], w=[("mergedT", i)])

        it = 0
        for cc in range(2):
            load_w_cols(Wgm, w_gm, "Wgm", 8, cc * 512, (cc + 1) * 512); load_w_cols(Wgnm, w_gnm, "Wgnm", 8, cc * 512, (cc + 1) * 512)
            load_w_cols(Wom, wo_mla, "Wom", 4, cc * 512, (cc + 1) * 512); load_w_cols(Won, wo_nsa, "Won", 4, cc * 512, (cc + 1) * 512)
            pend_ = i_front(cc, 0, it)
            for i in range(NT):
                cur_ = pend_
                if i + 1 < NT:
                    pend_ = i_front(cc, i + 1, it + 1)
                i_back(cc, i, it, cur_)
                it += 1
        if upto <= 10:
            return finish(nc, S, st)
        S.barrier()

        M.lo = lo0
        h2T = hT
        Wout = M.lo_alloc([8, D], BF16)
        G1 = M.lo_alloc([D], F32); A2 = M.lo_alloc([D], F32); B2 = M.lo_alloc([D], F32)
        xb = [M.lo_alloc([D], F32) for _ in range(2)]
        x1t = [M.lo_alloc([D], F32) for _ in range(2)]
        junk = M.lo_alloc([D], F32); tmpA = M.lo_alloc([D], F32)
        hb = [M.lo_alloc([D], BF16) for _ in range(2)]
        ssq = M.lo_alloc([NT], F32); rs = M.lo_alloc([NT], F32)
        S.dma("sp", G1, mods[:, 2 * D:3 * D], r=["mods"], w=["G1"])
        S.dma("sp", B2, mods[:, 3 * D:4 * D], r=["mods"], w=["AB2"])
        S.dma("sp", A2, mods[:, 4 * D:5 * D], r=["mods"], w=["AB2"])
        load_w(Wout, w_out, "Wout", 8, D, ceng="pool")
        tmpA2 = [tmpA, M.lo_alloc([D], F32)]
        tmpJ = [M.lo_alloc([D], F32) for _ in range(3)]
        xb = xb + [M.lo_alloc([D], F32)]
        x1t = x1t + [M.lo_alloc([D], F32)]

        def j_front(i):
            ts = slice(i * 128, (i + 1) * 128)
            par = i % 3
            S.dma("sp", xb[par], x[ts, :], w=[("xb", par)])
            for cc in range(2):
                b = 2 * par + cc
                for k in range(8):
                    mm(P[b][:, :], mergedT[:, k, ts], Wout[:, k, cc * 512:(cc + 1) * 512], k == 0, k == 7, r=[("mergedT", i), "Wout"], w=[PK[b]])

        def j_mid(i):
            ts = slice(i * 128, (i + 1) * 128)
            par = i % 3
            for cc in range(2):
                b = 2 * par + cc
                S.op("dve", lambda e, b=b, cc=cc: e.tensor_tensor(out=tmpJ[par][:, cc * 512:(cc + 1) * 512], in0=P[b][:, :], in1=G1[:, cc * 512:(cc + 1) * 512], op=ALU.mult), r=[PK[b], "G1"], w=[("tmpJ", par)])
                S.op("pool", lambda e, cc=cc: e.tensor_tensor(out=x1t[par][:, cc * 512:(cc + 1) * 512], in0=tmpJ[par][:, cc * 512:(cc + 1) * 512], in1=xb[par][:, cc * 512:(cc + 1) * 512], op=ALU.add),
                     r=[("tmpJ", par), ("xb", par)], w=[("x1t", par)])
            S.dma("sp", x1s[ts, :], x1t[par], r=[("x1t", par)], w=[("x1s", i)])
            norm_a(i, x1t[par], ("x1t", par))

        bank_state[0] = 0

        def nbJ():
            b = 6 + (bank_state[0] % 2)
            bank_state[0] = (bank_state[0] + 1) % 2
            return b
        nb_saved = nb
        nb = nbJ
        j_front(0); j_mid(0); j_front(1); j_mid(1)
        for i in range(NT):
            if i + 2 < NT:
                j_front(i + 2)
            b_ = norm_b1(i, x1t[i % 3], ("x1t", i % 3), A2, B2, ["AB2"])
            if i + 2 < NT:
                j_mid(i + 2)
            norm_b2(i, b_, h2T, "h2T")
        nb = nb_saved
        bank_state[0] = 0
        if "x1" in D_:
            S.dma("sp", D_["x1"], x1s, r=[("x1s", i) for i in range(NT)])
        if upto <= 11:
            return finish(nc, S, st)
        S.barrier()

        M.lo = lo_pers; M.hi = hi2 + 8 * S_ * 2
        Wd = M.hi_alloc([NFC, D], BF16)
        actT = M.hi_alloc([NFC, 1024], BF16)
        G2 = M.lo_alloc([D], F32)
        Wg2 = [M.lo_alloc([8, 256], BF16) for _ in range(2)]; Wu2 = [M.lo_alloc([8, 256], BF16) for _ in range(2)]
        sg = [M.lo_alloc([512], F32) for _ in range(2)]
        xb = [M.lo_alloc([D], F32) for _ in range(2)]
        ot = [M.lo_alloc([D], F32) for _ in range(2)]
        tmpA = M.lo_alloc([D], F32)
        S.dma("sp", G2, mods[:, 5 * D:6 * D], r=["mods"], w=["G2"])
        load_w(Wd, wd, "Wd", NFC, D)
        h2k = [("h2T", i) for i in range(NT)]
        out_toks = []
        for half in range(2):
            def ld(jg_):
                wb_ = jg_ % 2
                load_w_cols(Wg2[wb_], wg, ("Wg2", wb_), 8, jg_ * 256, (jg_ + 1) * 256)
                load_w_cols(Wu2[wb_], wu, ("Wu2", wb_), 8, jg_ * 256, (jg_ + 1) * 256)
            ld(0)
            for jg in range(NFC // 2):
                wb = jg % 2
                if jg + 1 < NFC // 2:
                    ld(jg + 1)
                for jj in range(2):
                    j = jg * 2 + jj
                    for tc in range(2):
                        t0 = half * 1024 + tc * 512
                        bg, bu = nb(), nb()
                        for k in range(8):
                            mm(P[bg][:, :], Wg2[wb][:, k, jj * 128:(jj + 1) * 128], h2T[:, k, t0:t0 + 512], k == 0, k == 7, r=[("Wg2", wb)] + h2k, w=[PK[bg]])
                        for k in range(8):
                            mm(P[bu][:, :], Wu2[wb][:, k, jj * 128:(jj + 1) * 128], h2T[:, k, t0:t0 + 512], k == 0, k == 7, r=[("Wu2", wb)] + h2k, w=[PK[bu]])
                        act(sg[tc], P[bg][:, :], AF.Silu, r=[PK[bg]], w=[("sg", tc)])
                        S.op("dve", lambda e, j=j, tc=tc, bu=bu: e.tensor_tensor(out=actT[:, j, tc * 512:(tc + 1) * 512], in0=P[bu][:, :], in1=sg[tc], op=ALU.mult),
                             r=[PK[bu], ("sg", tc)], w=[("actT", tc)])
            for il in range(8):
                i = half * 8 + il
                ts = slice(i * 128, (i + 1) * 128)
                S.dma("sp", xb[i % 2], x1s[ts, :], r=[("x1s", i)], w=[("xb", i % 2)])
                for cc in range(2):
                    b = nb()
                    for j in range(NFC):
                        mm(P[b][:, :], actT[:, j, il * 128:(il + 1) * 128], Wd[:, j, cc * 512:(cc + 1) * 512], j == 0, j == NFC - 1, r=[("actT", il // 4), "Wd"], w=[PK[b]])
                    S.op("dve", lambda e, b=b, cc=cc: e.tensor_tensor(out=tmpA[:, cc * 512:(cc + 1) * 512], in0=P[b][:, :], in1=G2[:, cc * 512:(cc + 1) * 512], op=ALU.mult), r=[PK[b], "G2"], w=["tmpA"])
                    S.op("pool", lambda e, i=i, cc=cc: e.tensor_tensor(out=ot[i % 2][:, cc * 512:(cc + 1) * 512], in0=tmpA[:, cc * 512:(cc + 1) * 512], in1=xb[i % 2][:, cc * 512:(cc + 1) * 512], op=ALU.add),
                         r=["tmpA", ("xb", i % 2)], w=[("ot", i % 2)])
                S.dma("sp", out[ts, :], ot[i % 2], r=[("ot", i % 2)], w=[("out", i)])
        return finish(nc, S, st)


def finish(nc, S, st):
    for q in S.dsem:
        for i in range(len(S.dsem[q])):
            if S.dcnt[q][i]:
                S._wait("sp", (("d", q, i), S.dcnt[q][i]))
    for e2 in ("pe", "act", "dve", "pool"):
        if S.cnt[e2]:
            S._wait("sp", (e2, S.cnt[e2]))
    S.check_no_deadlock()
    st.close()
    return nc


def _consts():
    bf = ml_dtypes.bfloat16
    c = {}
    c["ident"] = np.eye(128, dtype=np.float32).astype(bf)
    a = np.arange(128)
    c["tri"] = (a[:, None] <= a[None, :]).astype(np.float32).astype(bf)
    w = np.zeros((128, 3, 128), np.float32)
    w[:, 0, :] = (a[:, None] > a[None, :])
    w[:, 1, :] = 1.0
    w[:, 2, :] = (a[:, None] <= a[None, :])
    c["winm"] = w.reshape(128, 384).astype(bf)
    inv16 = (np.float32(500000.0) ** (-np.arange(0, 32, 2, dtype=np.float32) / np.float32(32))).astype(np.float32)
    inv8 = (np.float32(500000.0) ** (-np.arange(0, 16, 2, dtype=np.float32) / np.float32(16))).astype(np.float32)
    c["inv16"] = np.tile(inv16[None], (128, 1)).astype(np.float32)
    c["inv8"] = np.tile(inv8[None], (128, 1)).astype(np.float32)
    n = np.arange(127)
    starts = n * 16
    j = np.arange(32)
    ovl = ((starts[:, None] < j[None, :] * 64 + 64) & (starts[:, None] + 32 > j[None, :] * 64))
    c["ovl"] = ovl.astype(np.float32).astype(bf)
    t = np.arange(S_)
    c["vmT"] = ((starts[:, None] + 31) <= t[None, :]).astype(np.float32).astype(bf)
    XE = np.zeros((32, NT, 128), np.float32)
    for kt in range(NT):
        XE[2 * kt, kt, 0:64] = 1.0
        XE[2 * kt + 1, kt, 64:128] = 1.0
    c["XE"] = XE.reshape(32, NT * 128).astype(bf)
    c["XEall"] = (np.arange(32)[:, None] == (np.arange(S_)[None, :] // 64)).astype(np.float32).astype(bf)
    cur = (t // 64)
    forced = (j[None, :] == 0) | (j[None, :] == cur[:, None]) | (j[None, :] == cur[:, None] - 1)
    valid = j[None, :] <= cur[:, None]
    fb = np.where(valid, np.where(forced, 1e4, 0.0), -1e30).astype(np.float32)
    c["fb"] = fb.reshape(NT, 128, 32).transpose(1, 0, 2).reshape(128, NT * 32).copy()
    c["vj"] = valid.astype(np.float32).reshape(NT, 128, 32).transpose(1, 0, 2).reshape(128, NT * 32).copy()
    return c


def _rep(v, n=128):
    return np.ascontiguousarray(np.broadcast_to(np.asarray(v, np.float32)[None, :], (n, v.shape[0])))


def prep_inputs(inp):
    f = lambda a: np.ascontiguousarray(np.asarray(a, dtype=np.float32))
    w_in = f(inp["w_in"][0])
    o = np.cumsum([0, 768, 256, 32, 512, 128, 128, 128, 128, 128, 128, 24, 1024, 1024])
    seg = lambda i: w_in[:, o[i]:o[i + 1]]
    shared = {}
    shared["ada_w"] = f(inp["ada_w"][0]); shared["adabB"] = _rep(f(inp["ada_b"][0]))
    shared["g1B"] = _rep(f(inp["norm1_gain"][0])); shared["g2B"] = _rep(f(inp["norm2_gain"][0]))
    shared["w_cq"] = f(seg(0)); shared["w_ckv"] = f(seg(1)); shared["w_kpe"] = f(seg(2))
    qn = seg(3).reshape(D, 8, 64)
    shared["w_qn"] = f(qn[:, PH, :].reshape(D, 512))
    kc = seg(4).reshape(D, 2, 64); vc = seg(5).reshape(D, 2, 64)
    shared["w_kc2"] = f(np.stack([kc[:, 0], kc[:, 0], kc[:, 1], kc[:, 1]], 1).reshape(D, 256))
    shared["w_vc2"] = f(np.stack([vc[:, 0], vc[:, 0], vc[:, 1], vc[:, 1]], 1).reshape(D, 256))
    shared["w_kv4"] = f(np.concatenate([seg(6), seg(8), seg(7), seg(9)], 1))
    gn = seg(10).reshape(D, 8, 3)
    shared["w_gn"] = f(gn[:, PH, :].transpose(0, 2, 1).reshape(D, 24))
    shared["w_gm"] = f(seg(11)); shared["w_gnm"] = f(seg(12))
    shared["qag"] = f(f(inp["mla_q_a_gain"][0]).reshape(6, 128).T); shared["kvag"] = f(f(inp["mla_kv_a_gain"][0]).reshape(2, 128).T)
    shared["w_qb"] = f(inp["mla_w_q_b"][0]); shared["w_kvb"] = f(inp["mla_w_kv_b"][0])
    shared["qgB"] = _rep(f(inp["mla_q_gain"][0])); shared["kgB"] = _rep(f(inp["mla_k_gain"][0]))
    shared["nqB"] = _rep(f(inp["nsa_q_gain"][0])); shared["nkcB"] = _rep(f(inp["nsa_kc_gain"][0]))
    shared["nksB"] = _rep(f(inp["nsa_ks_gain"][0])); shared["nkwB"] = _rep(f(inp["nsa_kw_gain"][0]))
    shared["posk"] = f(f(inp["cmp_pos_k"][0]).reshape(16, 128).T); shared["posv"] = f(f(inp["cmp_pos_v"][0]).reshape(16, 128).T)
    shared["w1k"] = f(inp["cmp_w1_k"][0]); shared["w2k"] = f(inp["cmp_w2_k"][0])
    shared["w1v"] = f(inp["cmp_w1_v"][0]); shared["w2v"] = f(inp["cmp_w2_v"][0])
    shared["wo_mla"] = f(inp["w_o_mla"][0])
    shared["wo_nsa"] = f(f(inp["w_o_nsa"][0]).reshape(8, 64, D)[PH].reshape(512, D))
    shared["w_out"] = f(inp["w_out"][0])
    shared["wg"] = f(inp["ffn_w_gate"][0]); shared["wu"] = f(inp["ffn_w_up"][0]); shared["wd"] = f(inp["ffn_w_down"][0])
    shared.update(_consts())
    maps = []
    xs = np.asarray(inp["x"], np.float32); cs = np.asarray(inp["c"], np.float32); ps = np.asarray(inp["positions"]).astype(np.int32)
    for b in range(xs.shape[0]):
        m = dict(shared)
        m["x"] = np.ascontiguousarray(xs[b])
        m["c_pk"] = np.ascontiguousarray(cs[b].reshape(8, 128).T)
        m["pos_pk"] = np.ascontiguousarray(ps[b].reshape(NT, 128).T)
        m["posC"] = np.ascontiguousarray(ps[b][31::16][:127].reshape(127, 1))
        maps.append(m)
    return maps


_NC_CACHE = {}


def kernel(**inputs):
    maps = prep_inputs(inputs)
    if "nc" not in _NC_CACHE:
        _NC_CACHE["nc"] = build()
    nc = _NC_CACHE["nc"]
    res = run_bass_kernel_spmd(nc, maps, core_ids=list(range(len(maps))))
    return np.stack([np.asarray(r["out"], dtype=np.float32) for r in res.results], 0)
```

```python
import contextlib
import numpy as np
import ml_dtypes
import concourse.bass as bass
import concourse.mybir as mybir
from concourse.bass_utils import run_bass_kernel_spmd

F32 = mybir.dt.float32
BF16 = mybir.dt.bfloat16
I32 = mybir.dt.int32
ALU = mybir.AluOpType
AF = mybir.ActivationFunctionType
AX = mybir.AxisListType

S_ = 2048
D = 1024
NT = 16
DFF = 2816
NFC = 22
EPS = 1e-6
PH = [0, 4, 1, 5, 2, 6, 3, 7]
TWO_PI = float(2 * np.pi)
PI = float(np.pi)


class Sched:
    N_DMA_SLOTS = {"sp": 24, "pool": 8, "act": 4}

    def __init__(self, nc, stack):
        self.nc = nc
        self.E = {"pe": nc.tensor, "act": nc.scalar, "dve": nc.vector, "pool": nc.gpsimd, "sp": nc.sync}
        self.sem, self.cnt = {}, {}
        for e in ("pe", "act", "dve", "pool"):
            self.sem[e] = stack.enter_context(nc.semaphore("s_" + e))
            self.cnt[e] = 0
        self.dsem, self.dcnt, self.dnext = {}, {}, {}
        for q, n in self.N_DMA_SLOTS.items():
            self.dsem[q] = [stack.enter_context(nc.semaphore(f"d_{q}{i}")) for i in range(n)]
            self.dcnt[q] = [0] * n
            self.dnext[q] = 0
        self.seen = {e: {} for e in self.E}
        self.lastw, self.readers = {}, {}
        self.n_wait = 0
        self.n_inst = 0
        self.prog = {}
        self.ninst = {}

    def check_no_deadlock(self):
        val = {}
        ptr = {e: 0 for e in self.prog}
        progress = True
        while progress:
            progress = False
            for e, lst in self.prog.items():
                while ptr[e] < len(lst):
                    it = lst[ptr[e]]
                    if it[0] == "w":
                        if val.get(it[1], 0) < it[2]:
                            break
                    else:
                        val[it[1]] = val.get(it[1], 0) + it[2]
                    ptr[e] += 1
                    progress = True
        stuck = {e: (ptr[e], len(l), l[ptr[e]]) for e, l in self.prog.items() if ptr[e] < len(l)}
        assert not stuck, f"DEADLOCK in emitted program: {stuck}"

    def _sem_of(self, src):
        return self.dsem[src[1]][src[2]] if isinstance(src, tuple) else self.sem[src]

    def _wait(self, e, tok):
        src, val = tok
        if self.seen[e].get(src, 0) >= val:
            return
        self.E[e].wait_ge(self._sem_of(src), val)
        self.seen[e][src] = val
        self.n_wait += 1
        self.prog.setdefault(e, []).append(("w", src, val))

    def _deps(self, e, r, w):
        toks = []
        for k in r:
            t = self.lastw.get(k)
            if t is not None:
                toks.append(t)
        for k in w:
            t = self.lastw.get(k)
            if t is not None:
                toks.append(t)
            for t in self.readers.get(k, ()):
                toks.append(t)
        for t in toks:
            if t[0] == e and e == "pe":
                continue
            self._wait(e, t)

    def _commit(self, tok, r, w):
        for k in r:
            lst = self.readers.setdefault(k, [])
            lst[:] = [t for t in lst if t[0] != tok[0]]
            lst.append(tok)
        for k in w:
            self.lastw[k] = tok
            self.readers[k] = []

    def op(self, e, fn, r=(), w=(), signal=True):
        self._deps(e, r, w)
        ins = fn(self.E[e])
        if signal:
            self.cnt[e] += 1
            ins.then_inc(self.sem[e], 1)
            tok = (e, self.cnt[e])
            self.prog.setdefault(e, []).append(("i", e, 1))
        else:
            tok = (e, self.cnt[e] + 1)
        self._commit(tok, r, w)
        self.n_inst += 1
        self.ninst[e] = self.ninst.get(e, 0) + 1
        return tok

    def dma(self, q, out, in_, r=(), w=(), **kw):
        slot = self.dnext[q]
        self.dnext[q] = (slot + 1) % len(self.dsem[q])
        src = ("d", q, slot)
        if self.dcnt[q][slot] > 0:
            self._wait(q, (src, self.dcnt[q][slot]))
        self._deps(q, r, w)
        ins = self.E[q].dma_start(out=out, in_=in_, **kw)
        self.dcnt[q][slot] += 16
        ins.then_inc(self.dsem[q][slot], 16)
        self.prog.setdefault(q, []).append(("i", src, 16))
        tok = (src, self.dcnt[q][slot])
        self._commit(tok, r, w)
        self.n_inst += 1
        return tok

    def barrier(self):
        for e in ("pe", "act", "dve", "pool", "sp"):
            for e2 in ("pe", "act", "dve", "pool"):
                if self.cnt[e2] and not (e2 == e == "pe"):
                    self._wait(e, (e2, self.cnt[e2]))
            for q in self.dsem:
                for i in range(len(self.dsem[q])):
                    if self.dcnt[q][i]:
                        self._wait(e, (("d", q, i), self.dcnt[q][i]))
        self.lastw.clear()
        self.readers.clear()


class Mem:
    def __init__(self, big, nbytes):
        self.big, self.lo, self.hi, self.n = big, 0, nbytes, nbytes

    def _view(self, off, shape, dt):
        nel = int(np.prod(shape))
        esz = 4 if dt in (F32, I32) else 2
        nb = nel * esz
        ap = self.big[:, off // 2:(off + nb) // 2]
        if esz == 4:
            ap = ap.bitcast(dt)
        if len(shape) == 2:
            ap = ap.rearrange("p (a b) -> p a b", a=shape[0])
        elif len(shape) == 3:
            ap = ap.rearrange("p (a b c) -> p a b c", a=shape[0], b=shape[1])
        return ap

    def lo_alloc(self, shape, dt):
        nb = int(np.prod(shape)) * (4 if dt in (F32, I32) else 2)
        nb = (nb + 63) // 64 * 64
        off = self.lo
        self.lo += nb
        assert self.lo <= self.hi, f"SBUF overflow lo={self.lo} hi={self.hi}"
        return self._view(off, shape, dt)

    def hi_alloc(self, shape, dt):
        nb = int(np.prod(shape)) * (4 if dt in (F32, I32) else 2)
        nb = (nb + 63) // 64 * 64
        self.hi -= nb
        assert self.lo <= self.hi, f"SBUF overflow lo={self.lo} hi={self.hi}"
        return self._view(self.hi, shape, dt)


def build(upto=99, dbg=()):
    nc = bass.Bass("TRN2", target_bir_lowering=False)
    I = {}

    def din(name, shape, dt=F32):
        I[name] = nc.dram_tensor(name, list(shape), dt, kind="ExternalInput").ap()
        return I[name]

    x = din("x", [S_, D]); c_pk = din("c_pk", [128, 8]); pos_pk = din("pos_pk", [128, NT], I32)
    posC = din("posC", [127, 1], I32)
    ada_w = din("ada_w", [D, 6 * D]); adabB = din("adabB", [128, 6 * D]); g1B = din("g1B", [128, D]); g2B = din("g2B", [128, D])
    w_cq = din("w_cq", [D, 768]); w_ckv = din("w_ckv", [D, 256]); w_kpe = din("w_kpe", [D, 32])
    w_qn = din("w_qn", [D, 512]); w_kc2 = din("w_kc2", [D, 256]); w_vc2 = din("w_vc2", [D, 256])
    w_kv4 = din("w_kv4", [D, 512]); w_gn = din("w_gn", [D, 24]); w_gm = din("w_gm", [D, D]); w_gnm = din("w_gnm", [D, D])
    qag = din("qag", [128, 6]); kvag = din("kvag", [128, 2])
    w_qb = din("w_qb", [768, 768]); w_kvb = din("w_kvb", [256, 1024])
    qgB = din("qgB", [128, 96]); kgB = din("kgB", [128, 96])
    nqB = din("nqB", [128, 64]); nkcB = din("nkcB", [128, 64]); nksB = din("nksB", [128, 64]); nkwB = din("nkwB", [128, 64])
    posk = din("posk", [128, 16]); w1k = din("w1k", [2048, 256]); w2k = din("w2k", [256, 64])
    posv = din("posv", [128, 16]); w1v = din("w1v", [2048, 256]); w2v = din("w2v", [256, 64])
    wo_mla = din("wo_mla", [512, D]); wo_nsa = din("wo_nsa", [512, D]); w_out = din("w_out", [D, D])
    wg = din("wg", [D, DFF]); wu = din("wu", [D, DFF]); wd = din("wd", [DFF, D])
    ident_d = din("ident", [128, 128], BF16); tri_d = din("tri", [128, 128], BF16); winm_d = din("winm", [128, 384], BF16)
    inv16_d = din("inv16", [128, 16]); inv8_d = din("inv8", [128, 8])
    XEall_d = din("XEall", [32, S_], BF16); ovl_d = din("ovl", [127, 32], BF16); vmT_d = din("vmT", [127, S_], BF16); XE_d = din("XE", [32, NT * 128], BF16)
    fb_d = din("fb", [128, NT * 32]); vj_d = din("vj", [128, NT * 32])
    out = nc.dram_tensor("out", [S_, D], F32, kind="ExternalOutput").ap()
    hTs = nc.dram_tensor("hTs", [128, 8 * S_], BF16).ap()
    mods = nc.dram_tensor("mods", [128, 6 * D], F32).ap()
    x1s = nc.dram_tensor("x1s", [S_, D], F32).ap()
    D_ = {}
    for name, shape, dt in dbg:
        D_[name] = nc.dram_tensor("dbg_" + name, list(shape), dt, kind="ExternalOutput").ap()

    st = contextlib.ExitStack()
    with st:
        S = Sched(nc, st)
        NB = 204800
        big = st.enter_context(nc.sbuf_tensor("big", [128, NB // 2], BF16))
        M = Mem(big, NB)
        P = [st.enter_context(nc.psum_tensor(f"ps{i}", [128, 512], F32)) for i in range(8)]
        PK = [f"ps{i}" for i in range(8)]
        bank_state = [0]

        nbanks = [8]

        def nb():
            b = bank_state[0] % nbanks[0]
            bank_state[0] = (b + 1) % nbanks[0]
            return b

        def mm(ps_ap, lhsT, rhs, start, stop, r, w, sig=None, **kw):
            S.op("pe", lambda e: e.matmul(ps_ap, lhsT=lhsT, rhs=rhs, start=start, stop=stop, **kw), r=r, w=w, signal=bool(stop) if sig is None else sig)

        def tp(ps_ap, in_, ident_ap, r, w):
            S.op("pe", lambda e: e.transpose(out=ps_ap, in_=in_, identity=ident_ap), r=r, w=w)

        def act(out_, in_, func, r, w, **kw):
            S.op("act", lambda e: e.activation(out=out_, in_=in_, func=func, **kw), r=r, w=w)

        def dbg_out(name, ap, r):
            if name in D_:
                S.dma("sp", D_[name], ap, r=r)

        ident = M.lo_alloc([128], BF16); tri = M.lo_alloc([128], BF16); winm = M.lo_alloc([3, 128], BF16)
        onesb = M.lo_alloc([128], BF16)
        stg = [M.lo_alloc([1024], F32) for _ in range(3)]
        stg_i = [0]
        cosM = M.lo_alloc([NT, 16], F32); sinM = M.lo_alloc([NT, 16], F32)
        cosN = M.lo_alloc([NT, 8], F32); sinN = M.lo_alloc([NT, 8], F32)
        cosC = M.lo_alloc([8], F32); sinC = M.lo_alloc([8], F32)
        lo_pers = M.lo
        omT = M.lo_alloc([4, S_], BF16); onT = M.lo_alloc([4, S_], BF16)
        S.dma("sp", ident, ident_d, w=["ident"])
        S.dma("sp", tri, tri_d, w=["tri"])
        S.dma("sp", winm.rearrange("p a b -> p (a b)"), winm_d, w=["winm"])
        S.op("pool", lambda e: e.memset(onesb, 1.0), w=["onesb"])

        def load_w(dst, W, key, KC, N, ceng="dve"):
            Wv = W.rearrange("(k p) n -> p k n", p=128)
            if N <= 1024:
                g = max(1, min(KC, 1024 // N))
                for k0 in range(0, KC, g):
                    k1 = min(KC, k0 + g)
                    si = stg_i[0]; stg_i[0] = (si + 1) % 3
                    sv = stg[si][:, 0:(k1 - k0) * N].rearrange("p (k n) -> p k n", n=N)
                    S.dma("sp", sv, Wv[:, k0:k1, :], w=[("stg", si)])
                    S.op(ceng, lambda e, sv=sv, k0=k0, k1=k1: e.tensor_copy(out=dst[:, k0:k1, :], in_=sv), r=[("stg", si)], w=[key])
            else:
                for k in range(KC):
                    for c0 in range(0, N, 1024):
                        c1 = min(N, c0 + 1024)
                        si = stg_i[0]; stg_i[0] = (si + 1) % 3
                        sv = stg[si][:, 0:c1 - c0]
                        S.dma("sp", sv, Wv[:, k, c0:c1], w=[("stg", si)])
                        S.op(ceng, lambda e, sv=sv, k=k, c0=c0, c1=c1: e.tensor_copy(out=dst[:, k, c0:c1], in_=sv), r=[("stg", si)], w=[key])

        def load_w_cols(dst, W, key, KC, c0, c1, ceng="dve"):
            Wv = W.rearrange("(k p) n -> p k n", p=128)
            N = c1 - c0
            g = max(1, min(KC, 1024 // N))
            for k0 in range(0, KC, g):
                k1 = min(KC, k0 + g)
                si = stg_i[0]; stg_i[0] = (si + 1) % 3
                sv = stg[si][:, 0:(k1 - k0) * N].rearrange("p (k n) -> p k n", n=N)
                S.dma("sp", sv, Wv[:, k0:k1, c0:c1], w=[("stg", si)])
                S.op(ceng, lambda e, sv=sv, k0=k0, k1=k1: e.tensor_copy(out=dst[:, k0:k1, :], in_=sv), r=[("stg", si)], w=[key])

        def sincos(ang, shape, cos_o, sin_o, np_, tmp_f, tmp_i, tmp_m, key):
            for (shift, dst) in ((0.0, sin_o), (PI / 2, cos_o)):
                S.op("dve", lambda e: e.tensor_scalar(out=tmp_f, in0=ang, scalar1=shift, scalar2=None, op0=ALU.add), r=[key + "ang"], w=[key + "f"])
                S.op("dve", lambda e: e.tensor_scalar(out=tmp_i, in0=tmp_f, scalar1=float(1 / TWO_PI), scalar2=None, op0=ALU.mult), r=[key + "f"], w=[key + "i"])
                S.op("dve", lambda e: e.tensor_copy(out=tmp_m, in_=tmp_i), r=[key + "i"], w=[key + "m"])
                S.op("dve", lambda e: e.scalar_tensor_tensor(out=tmp_f, in0=tmp_m, scalar=-TWO_PI, in1=tmp_f, op0=ALU.mult, op1=ALU.add), r=[key + "m", key + "f"], w=[key + "f"])
                S.op("dve", lambda e: e.tensor_scalar(out=tmp_m, in0=tmp_f, scalar1=PI, scalar2=None, op0=ALU.is_gt), r=[key + "f"], w=[key + "m"])
                S.op("dve", lambda e: e.scalar_tensor_tensor(out=tmp_f, in0=tmp_m, scalar=-TWO_PI, in1=tmp_f, op0=ALU.mult, op1=ALU.add), r=[key + "m", key + "f"], w=[key + "f"])
                S.op("dve", lambda e: e.tensor_scalar(out=tmp_m, in0=tmp_f, scalar1=-PI, scalar2=None, op0=ALU.is_lt), r=[key + "f"], w=[key + "m"])
                S.op("dve", lambda e: e.scalar_tensor_tensor(out=tmp_f, in0=tmp_m, scalar=TWO_PI, in1=tmp_f, op0=ALU.mult, op1=ALU.add), r=[key + "m", key + "f"], w=[key + "f"])
                act(dst, tmp_f, AF.Sin, r=[key + "f"], w=[key + "out"])

        lo0, hi0 = M.lo, M.hi
        if upto <= -2:
            return finish(nc, S, st)
        posi = M.lo_alloc([NT], I32); posf = M.lo_alloc([NT], F32)
        posCi = M.lo_alloc([1], I32); posCf = M.lo_alloc([1], F32)
        inv16 = M.lo_alloc([16], F32); inv8 = M.lo_alloc([8], F32)
        angM = M.lo_alloc([NT, 16], F32); tfM = M.lo_alloc([NT, 16], F32); tiM = M.lo_alloc([NT, 16], I32); tmM = M.lo_alloc([NT, 16], F32)
        S.dma("sp", posi, pos_pk, w=["posi"])
        S.dma("sp", posCi[0:127], posC, w=["posCi"])
        S.dma("sp", inv16, inv16_d, w=["inv16"])
        S.dma("sp", inv8, inv8_d, w=["inv8"])
        S.op("dve", lambda e: e.tensor_copy(out=posf, in_=posi), r=["posi"], w=["posf"])
        S.op("dve", lambda e: e.tensor_copy(out=posCf[0:127], in_=posCi[0:127]), r=["posCi"], w=["posCf"])
        S.op("dve", lambda e: e.tensor_tensor(out=angM, in0=posf.unsqueeze(2).to_broadcast([128, NT, 16]),
                                              in1=inv16.unsqueeze(1).to_broadcast([128, NT, 16]), op=ALU.mult), r=["posf", "inv16"], w=["Mang"])
        sincos(angM, None, cosM, sinM, 128, tfM, tiM, tmM, "M")
        a8 = angM.rearrange("p a b -> p (a b)")[:, 0:NT * 8].rearrange("p (a b) -> p a b", b=8)
        f8 = tfM.rearrange("p a b -> p (a b)")[:, 0:NT * 8].rearrange("p (a b) -> p a b", b=8)
        i8 = tiM.rearrange("p a b -> p (a b)")[:, 0:NT * 8].rearrange("p (a b) -> p a b", b=8)
        m8_ = tmM.rearrange("p a b -> p (a b)")[:, 0:NT * 8].rearrange("p (a b) -> p a b", b=8)
        S.op("dve", lambda e: e.tensor_tensor(out=a8, in0=posf.unsqueeze(2).to_broadcast([128, NT, 8]),
                                              in1=inv8.unsqueeze(1).to_broadcast([128, NT, 8]), op=ALU.mult), r=["posf", "inv8", "Mout", "Mf", "Mm", "Mi"], w=["Nang"])
        sincos(a8, None, cosN, sinN, 128, f8, i8, m8_, "N")
        aC = angM.rearrange("p a b -> p (a b)")[0:127, 0:8]
        fC = tfM.rearrange("p a b -> p (a b)")[0:127, 0:8]
        iC = tiM.rearrange("p a b -> p (a b)")[0:127, 0:8]
        mC = tmM.rearrange("p a b -> p (a b)")[0:127, 0:8]
        S.op("dve", lambda e: e.tensor_scalar(out=aC, in0=inv8[0:127], scalar1=posCf[0:127, 0:1], scalar2=None, op0=ALU.mult),
             r=["posCf", "inv8", "Nout", "Nf", "Nm", "Ni", "Nang"], w=["Cang"])
        sincos(aC, None, cosC[0:127], sinC[0:127], 127, fC, iC, mC, "C")
        dbg_out("cosM", cosM, ["Mout"]); dbg_out("sinM", sinM, ["Mout"])

        if upto <= -1:
            return finish(nc, S, st)
        cpk = M.lo_alloc([8], F32); sc = M.lo_alloc([8], F32)
        sch = M.lo_alloc([8], BF16); scl = M.lo_alloc([8], BF16)
        cBh = M.lo_alloc([8, 128], BF16); cBl = M.lo_alloc([8, 128], BF16)
        modB = M.lo_alloc([6 * D], F32)
        g1t = M.lo_alloc([D], F32); g2t = M.lo_alloc([D], F32)
        awb = [M.hi_alloc([8, 512], F32) for _ in range(3)]
        abb = [M.hi_alloc([512], F32) for _ in range(3)]
        awh = [M.hi_alloc([8, 512], BF16) for _ in range(3)]
        awl = [M.hi_alloc([8, 512], BF16) for _ in range(3)]
        S.dma("sp", cpk, c_pk, w=["cpk"])
        S.dma("sp", g1t, g1B, w=["g1t"]); S.dma("sp", g2t, g2B, w=["g2t"])
        act(sc, cpk, AF.Silu, r=["cpk"], w=["sc"])
        S.op("dve", lambda e: e.tensor_copy(out=sch, in_=sc), r=["sc"], w=["sch"])
        S.op("dve", lambda e: e.tensor_tensor(out=scl, in0=sc, in1=sch, op=ALU.subtract), r=["sc", "sch"], w=["scl"])
        for k in range(8):
            S.op("dve", lambda e, k=k: e.tensor_copy(out=cBh[:, k, :], in_=sch[:, k:k + 1].to_broadcast([128, 128])), r=["sch"], w=["cBh"])
            S.op("dve", lambda e, k=k: e.tensor_copy(out=cBl[:, k, :], in_=scl[:, k:k + 1].to_broadcast([128, 128])), r=["scl"], w=["cBl"])
        awv = ada_w.rearrange("(k p) n -> p k n", p=128)
        for n in range(12):
            q_ = n % 3
            S.dma("sp", awb[q_], awv[:, :, n * 512:(n + 1) * 512], w=[("awb", q_)])
            S.dma("sp", abb[q_], adabB[:, n * 512:(n + 1) * 512], w=[("abb", q_)])
            act(awh[q_], awb[q_], AF.Copy, r=[("awb", q_)], w=[("awh", q_)])
            S.op("dve", lambda e, q_=q_: e.tensor_tensor(out=awl[q_], in0=awb[q_], in1=awh[q_], op=ALU.subtract), r=[("awb", q_), ("awh", q_)], w=[("awl", q_)])
            b = nb()
            passes = [(cBh, "cBh", awh, "awh"), (cBh, "cBh", awl, "awl"), (cBl, "cBl", awh, "awh")]
            for pi_, (cb_, ck, ww, wk) in enumerate(passes):
                for k in range(8):
                    mm(P[b][:, :], cb_[:, k, :], ww[q_][:, k, :], pi_ == 0 and k == 0, pi_ == 2 and k == 7, r=[ck, (wk, q_)], w=[PK[b]])
            S.op("dve", lambda e, n=n, b=b, q_=q_: e.tensor_tensor(out=modB[:, n * 512:(n + 1) * 512], in0=P[b][:, :], in1=abb[q_], op=ALU.add),
                 r=[PK[b], ("abb", q_)], w=["modB"])
        S.op("dve", lambda e: e.scalar_tensor_tensor(out=modB[:, D:2 * D], in0=modB[:, D:2 * D], scalar=1.0, in1=g1t, op0=ALU.add, op1=ALU.mult), r=["modB", "g1t"], w=["modB"])
        S.op("dve", lambda e: e.scalar_tensor_tensor(out=modB[:, 4 * D:5 * D], in0=modB[:, 4 * D:5 * D], scalar=1.0, in1=g2t, op0=ALU.add, op1=ALU.mult), r=["modB", "g2t"], w=["modB"])
        S.dma("sp", mods, modB, r=["modB"], w=["mods"])
        dbg_out("modB", modB, ["modB"])
        B1 = modB[:, 0:D]; A1 = modB[:, D:2 * D]
        if upto <= 0:
            return finish(nc, S, st)

        M.hi = hi0
        hT = M.hi_alloc([8, S_], BF16)
        xb = [M.lo_alloc([D], F32) for _ in range(2)]
        junk = M.lo_alloc([D], F32); tmpA = M.lo_alloc([D], F32)
        hb = [M.lo_alloc([D], BF16) for _ in range(2)]
        ssq = M.lo_alloc([NT], F32); rs = M.lo_alloc([NT], F32)

        tmpA2 = [tmpA, M.lo_alloc([D], F32)]

        def norm_a(i, xt, xkey):
            act(junk, xt, AF.Square, r=[xkey], w=["junk", ("ssq", i)], accum_out=ssq[:, i:i + 1])
            act(rs[:, i:i + 1], ssq[:, i:i + 1], AF.Sqrt, r=[("ssq", i)], w=[("rs", i)], scale=1.0 / D, bias=EPS)
            S.op("dve", lambda e: e.reciprocal(out=rs[:, i:i + 1], in_=rs[:, i:i + 1]), r=[("rs", i)], w=[("rs", i)])

        def norm_b1(i, xt, xkey, A, B, Akeys):
            par = i % 2
            S.op("dve", lambda e: e.scalar_tensor_tensor(out=tmpA2[par], in0=xt, scalar=rs[:, i:i + 1], in1=A, op0=ALU.mult, op1=ALU.mult),
                 r=[xkey, ("rs", i)] + Akeys, w=[("tmpA2", par)])
            S.op("pool", lambda e: e.tensor_tensor(out=hb[par], in0=tmpA2[par], in1=B, op=ALU.add), r=[("tmpA2", par)] + Akeys, w=[("hb", par)])
            b = nb()
            pb = P[b][:, :].bitcast(BF16)
            for k in range(8):
                tp(pb[:, k * 128:(k + 1) * 128], hb[par][:, k * 128:(k + 1) * 128], ident, r=[("hb", par), "ident"], w=[PK[b]])
            return b

        def norm_b2(i, b, dstT, dkey):
            pb = P[b][:, :].bitcast(BF16)
            act(dstT[:, :, i * 128:(i + 1) * 128], pb.rearrange("p (k t) -> p k t", k=8), AF.Copy, r=[PK[b]], w=[(dkey, i)])

        xb = xb + [M.lo_alloc([D], F32), M.lo_alloc([D], F32)]

        def a_front(i):
            S.dma("sp", xb[i % 4], x[i * 128:(i + 1) * 128, :], w=[("xb", i % 4)])
            norm_a(i, xb[i % 4], ("xb", i % 4))

        a_front(0); a_front(1)
        for i in range(NT):
            b_ = norm_b1(i, xb[i % 4], ("xb", i % 4), A1, B1, ["modB"])
            if i + 2 < NT:
                a_front(i + 2)
            norm_b2(i, b_, hT, "hT")
        hTk = [("hT", i) for i in range(NT)]
        S.dma("sp", hTs, hT.rearrange("p k t -> p (k t)"), r=hTk, w=["hTs"])
        dbg_out("hT", hT.rearrange("p k t -> p (k t)"), hTk)
        if upto <= 1:
            return finish(nc, S, st)
        S.barrier()

        nbanks[0] = 6
        M.lo = lo0
        cqT = M.lo_alloc([6, S_], BF16); ckvT = M.lo_alloc([2, S_], BF16); kpe = M.lo_alloc([NT, 32], F32)
        lo1 = M.lo
        Wcq = M.lo_alloc([8, 768], BF16); Wckv = M.lo_alloc([8, 256], BF16); Wkpe = M.lo_alloc([8, 32], BF16)
        qagt = M.lo_alloc([6], F32); kvagt = M.lo_alloc([2], F32)
        sqb = [M.lo_alloc([512], BF16) for _ in range(2)]
        rb = M.lo_alloc([512], F32)
        S.dma("sp", qagt, qag, w=["qagt"]); S.dma("sp", kvagt, kvag, w=["kvagt"])
        load_w(Wcq, w_cq, "Wcq", 8, 768); load_w(Wckv, w_ckv, "Wckv", 8, 256); load_w(Wkpe, w_kpe, "Wkpe", 8, 32)

        def fm_proj_norm(dstT, dkey, Wt, wkey, nf, gaint, gkey, nfeat):
            for c in range(4):
                hk = [("hT", 4 * c + q) for q in range(4)]
                for j in range(nf + 1):
                    if j < nf:
                        b = nb()
                        for k in range(8):
                            mm(P[b][:, :], Wt[:, k, j * 128:(j + 1) * 128], hT[:, k, c * 512:(c + 1) * 512], k == 0, k == 7, r=[wkey] + hk, w=[PK[b]])
                    if j >= 1:
                        jj = j - 1
                        mm(P[6][:, :], onesb, sqb[jj % 2], jj == 0, jj == nf - 1, r=["onesb", ("sqb", jj % 2)], w=[PK[6]], sig=True)
                    if j < nf:
                        act(sqb[j % 2], P[b][:, :], AF.Square, r=[PK[b]], w=[("sqb", j % 2)])
                        act(dstT[:, j, c * 512:(c + 1) * 512], P[b][:, :], AF.Copy, r=[PK[b]], w=[(dkey, c)])
                act(rb, P[6][:, :], AF.Sqrt, r=[PK[6]], w=["rb"], scale=1.0 / nfeat, bias=EPS)
                S.op("dve", lambda e: e.reciprocal(out=rb, in_=rb), r=["rb"], w=["rb"])
                for j in range(nf):
                    S.op("dve", lambda e, j=j, c=c: e.scalar_tensor_tensor(out=dstT[:, j, c * 512:(c + 1) * 512], in0=dstT[:, j, c * 512:(c + 1) * 512],
                                                                             scalar=gaint[:, j:j + 1], in1=rb, op0=ALU.mult, op1=ALU.mult),
                         r=[(dkey, c), "rb", gkey], w=[(dkey, c)])

        fm_proj_norm(cqT, "cqT", Wcq, "Wcq", 6, qagt, "qagt", 768)
        fm_proj_norm(ckvT, "ckvT", Wckv, "Wckv", 2, kvagt, "kvagt", 256)
        for i in range(NT):
            b = nb()
            for k in range(8):
                mm(P[b][:, 0:32], hT[:, k, i * 128:(i + 1) * 128], Wkpe[:, k, :], k == 0, k == 7, r=["Wkpe", ("hT", i)], w=[PK[b]])
            S.op("dve", lambda e, i=i, b=b: e.tensor_copy(out=kpe[:, i, :], in_=P[b][:, 0:32]), r=[PK[b]], w=[("kpe", i)])
        cqk = [("cqT", c) for c in range(4)]
        dbg_out("cqT", cqT.rearrange("p k t -> p (k t)"), cqk)
        dbg_out("kpe", kpe.rearrange("p a b -> p (a b)"), [("kpe", i) for i in range(NT)])
        if upto <= 2:
            return finish(nc, S, st)
        S.barrier()

        nbanks[0] = 8
        M.lo = lo1
        M.hi = hi0
        QT = M.hi_alloc([8, S_], BF16); KT = M.hi_alloc([8, S_], BF16); V = M.hi_alloc([NT, 8, 65], BF16)
        hi1 = M.hi
        Wqb = M.lo_alloc([6, 768], BF16); Wkvb = M.lo_alloc([2, 1024], BF16)
        qgt = M.lo_alloc([96], F32); kgt = M.lo_alloc([96], F32)
        drq = [M.lo_alloc([768], BF16) for _ in range(2)]; drk = [M.lo_alloc([768], BF16) for _ in range(2)]
        S.dma("sp", qgt, qgB, w=["qgt"]); S.dma("sp", kgt, kgB, w=["kgt"])
        load_w(Wqb, w_qb, "Wqb", 6, 768); load_w(Wkvb, w_kvb, "Wkvb", 2, 1024)
        S.op("pool", lambda e: e.memset(V[:, :, :, 64:65], 1.0), w=["Vones"])

        def mk_tmps(Mx, n, H, hf):
            return dict(t1=Mx.lo_alloc([n], F32), hs=Mx.lo_alloc([H], F32), hr=Mx.lo_alloc([H], F32),
                        ra=Mx.lo_alloc([H * hf], F32), rb=Mx.lo_alloc([H * hf], F32), ra2=Mx.lo_alloc([H * hf], F32), rb2=Mx.lo_alloc([H * hf], F32))

        def hnr_stages(tag, T, src, skeys, H, Dh, gaint, gkey, ro, hf, cos_, sin_, dst, dkey, np_=128):
            n = H * Dh
            t1v = T["t1"][0:np_, 0:n].rearrange("p (h d) -> p h d", h=H)
            hs = T["hs"][0:np_, 0:H]; hr = T["hr"][0:np_, 0:H]
            x1 = t1v[:, :, ro:ro + hf]; x2 = t1v[:, :, ro + hf:ro + 2 * hf]
            cb = cos_.unsqueeze(1).to_broadcast([np_, H, hf]); sb_ = sin_.unsqueeze(1).to_broadcast([np_, H, hf])
            rv = {k: T[k][0:np_, 0:H * hf].rearrange("p (h d) -> p h d", h=H) for k in ("ra", "rb", "ra2", "rb2")}
            tr = ["Mout", "Nout", "Cout"]
            k_ = lambda nm: (tag, nm)
            st = []
            st.append(lambda: act(t1v, src, AF.Square, r=skeys, w=[k_("t1")]))
            st.append(lambda: S.op("dve", lambda e: e.tensor_reduce(out=hs, in_=t1v, axis=AX.X, op=ALU.add), r=[k_("t1")], w=[k_("hs")]))
            st.append(lambda: act(hr, hs, AF.Sqrt, r=[k_("hs")], w=[k_("hr")], scale=1.0 / Dh, bias=EPS))
            st.append(lambda: S.op("dve", lambda e: e.reciprocal(out=hr, in_=hr), r=[k_("hr")], w=[k_("hr")]))
            st.append(lambda: S.op("dve", lambda e: e.tensor_tensor(out=t1v, in0=src, in1=hr.unsqueeze(2).to_broadcast([np_, H, Dh]), op=ALU.mult), r=skeys + [k_("hr"), k_("hs")], w=[k_("t1")]))
            st.append(lambda: S.op("dve", lambda e: e.tensor_tensor(out=t1v, in0=t1v, in1=gaint[0:np_].unsqueeze(1).to_broadcast([np_, H, Dh]), op=ALU.mult), r=[k_("t1"), gkey], w=[k_("t1")]))
            st.append(lambda: S.op("dve", lambda e: e.tensor_tensor(out=rv["ra"], in0=x1, in1=cb, op=ALU.mult), r=[k_("t1")] + tr, w=[k_("ra")]))
            st.append(lambda: S.op("dve", lambda e: e.tensor_tensor(out=rv["rb"], in0=x2, in1=sb_, op=ALU.mult), r=[k_("t1")] + tr, w=[k_("rb")]))
            st.append(lambda: S.op("dve", lambda e: e.tensor_tensor(out=dst[:, :, ro:ro + hf], in0=rv["ra"], in1=rv["rb"], op=ALU.subtract), r=[k_("ra"), k_("rb")], w=[dkey]))
            st.append(lambda: S.op("dve", lambda e: e.tensor_tensor(out=rv["ra2"], in0=x2, in1=cb, op=ALU.mult), r=[k_("t1")] + tr, w=[k_("ra2")]))
            st.append(lambda: S.op("dve", lambda e: e.tensor_tensor(out=rv["rb2"], in0=x1, in1=sb_, op=ALU.mult), r=[k_("t1")] + tr, w=[k_("rb2")]))
            st.append(lambda: S.op("dve", lambda e: e.tensor_tensor(out=dst[:, :, ro + hf:ro + 2 * hf], in0=rv["ra2"], in1=rv["rb2"], op=ALU.add), r=[k_("ra2"), k_("rb2")], w=[dkey]))

            def copies():
                if ro > 0:
                    S.op("pool", lambda e: e.tensor_copy(out=dst[:, :, 0:ro], in_=t1v[:, :, 0:ro]), r=[k_("t1")], w=[dkey])
                if ro + 2 * hf < Dh:
                    S.op("pool", lambda e: e.tensor_copy(out=dst[:, :, ro + 2 * hf:Dh], in_=t1v[:, :, ro + 2 * hf:Dh]), r=[k_("t1")], w=[dkey])
            st.insert(6, copies)
            return st

        def run_interleaved(chains):
            for s_ in range(max(len(c_) for c_ in chains)):
                for c_ in chains:
                    if s_ < len(c_):
                        c_[s_]()

        def head_norm_rope(src, skeys, H, Dh, gaint, gkey, ro, hf, cos_, sin_, dst, dkey, np_=128):
            n = H * Dh
            t1v = t1[0:np_, 0:n].rearrange("p (h d) -> p h d", h=H)
            t2v = t2[0:np_, 0:n].rearrange("p (h d) -> p h d", h=H)
            hs = hss[0:np_, 0:H]; hr = hrs[0:np_, 0:H]
            act(t1v, src, AF.Square, r=skeys, w=["t1"])
            S.op("dve", lambda e: e.tensor_reduce(out=hs, in_=t1v, axis=AX.X, op=ALU.add), r=["t1"], w=["hss"])
            act(hr, hs, AF.Sqrt, r=["hss"], w=["hrs"], scale=1.0 / Dh, bias=EPS)
            S.op("dve", lambda e: e.reciprocal(out=hr, in_=hr), r=["hrs"], w=["hrs"])
            S.op("dve", lambda e: e.tensor_tensor(out=t2v, in0=src, in1=hr.unsqueeze(2).to_broadcast([np_, H, Dh]), op=ALU.mult), r=skeys + ["hrs"], w=["t2"])
            S.op("dve", lambda e: e.tensor_tensor(out=t1v, in0=t2v, in1=gaint[0:np_].unsqueeze(1).to_broadcast([np_, H, Dh]), op=ALU.mult), r=["t2", gkey], w=["t1"])
            x1 = t1v[:, :, ro:ro + hf]; x2 = t1v[:, :, ro + hf:ro + 2 * hf]
            cb = cos_.unsqueeze(1).to_broadcast([np_, H, hf]); sb_ = sin_.unsqueeze(1).to_broadcast([np_, H, hf])
            rav = ra[0:np_, 0:H * hf].rearrange("p (h d) -> p h d", h=H)
            rbv = rbb[0:np_, 0:H * hf].rearrange("p (h d) -> p h d", h=H)
            tr = ["Mout", "Nout", "Cout"]
            S.op("dve", lambda e: e.tensor_tensor(out=rav, in0=x1, in1=cb, op=ALU.mult), r=["t1"] + tr, w=["ra"])
            S.op("dve", lambda e: e.tensor_tensor(out=rbv, in0=x2, in1=sb_, op=ALU.mult), r=["t1"] + tr, w=["rbb"])
            S.op("dve", lambda e: e.tensor_tensor(out=dst[:, :, ro:ro + hf], in0=rav, in1=rbv, op=ALU.subtract), r=["ra", "rbb"], w=[dkey])
            S.op("dve", lambda e: e.tensor_tensor(out=rav, in0=x2, in1=cb, op=ALU.mult), r=["t1"] + tr, w=["ra"])
            S.op("dve", lambda e: e.tensor_tensor(out=rbv, in0=x1, in1=sb_, op=ALU.mult), r=["t1"] + tr, w=["rbb"])
            S.op("dve", lambda e: e.tensor_tensor(out=dst[:, :, ro + hf:ro + 2 * hf], in0=rav, in1=rbv, op=ALU.add), r=["ra", "rbb"], w=[dkey])
            if ro > 0:
                S.op("pool", lambda e: e.tensor_copy(out=dst[:, :, 0:ro], in_=t1v[:, :, 0:ro]), r=["t1"], w=[dkey])
            if ro + 2 * hf < Dh:
                S.op("pool", lambda e: e.tensor_copy(out=dst[:, :, ro + 2 * hf:Dh], in_=t1v[:, :, ro + 2 * hf:Dh]), r=["t1"], w=[dkey])

        Msub = Mem(big, NB); Msub.lo = lo_pers; Msub.hi = lo0
        rawq = [Msub.lo_alloc([768], F32) for _ in range(2)]; rawk = [Msub.lo_alloc([768], F32), M.lo_alloc([768], F32)]
        Tq = [mk_tmps(Msub, 768, 8, 16) for _ in range(2)]; Tk = [mk_tmps(Msub, 768, 8, 16) for _ in range(2)]

        def b2_front(i):
            ts = slice(i * 128, (i + 1) * 128)
            par = i % 2
            bA, bB = nb(), nb()
            for k in range(6):
                mm(P[bA][:, :], cqT[:, k, ts], Wqb[:, k, 0:512], k == 0, k == 5, r=["Wqb", ("cqT", i // 4)], w=[PK[bA]])
            for k in range(6):
                mm(P[bB][:, 0:256], cqT[:, k, ts], Wqb[:, k, 512:768], k == 0, k == 5, r=["Wqb", ("cqT", i // 4)], w=[PK[bB]])
            act(rawq[par][:, 0:512], P[bA][:, :], AF.Copy, r=[PK[bA]], w=[("rawq", par)])
            act(rawq[par][:, 512:768], P[bB][:, 0:256], AF.Copy, r=[PK[bB]], w=[("rawq", par)])
            bA, bB = nb(), nb()
            for hh, bb in ((0, bA), (1, bB)):
                for k in range(2):
                    mm(P[bb][:, :], ckvT[:, k, ts], Wkvb[:, k, hh * 512:(hh + 1) * 512], k == 0, k == 1, r=["Wkvb", ("ckvT", i // 4)], w=[PK[bb]])
            rv = rawk[par].rearrange("p (h d) -> p h d", h=8)
            for hh, bb in ((0, bA), (1, bB)):
                pv = P[bb][:, :].rearrange("p (h d) -> p h d", h=4)
                act(rv[:, hh * 4:(hh + 1) * 4, 0:64], pv[:, :, 0:64], AF.Copy, r=[PK[bb]], w=[("rawk", par)])
                act(V[:, i, hh * 4:(hh + 1) * 4, 0:64], pv[:, :, 64:128], AF.Copy, r=[PK[bb]], w=[("V", i)])
            S.op("pool", lambda e, i=i: e.tensor_copy(out=rv[:, :, 64:96], in_=kpe[:, i, :].unsqueeze(1).to_broadcast([128, 8, 32])), r=[("kpe", i)], w=[("rawk", par)])

        def b2_chains(i):
            par = i % 2
            dq = drq[par].rearrange("p (h d) -> p h d", h=8); dk = drk[par].rearrange("p (h d) -> p h d", h=8)
            cq_ = hnr_stages(("cq", par), Tq[par], rawq[par].rearrange("p (h d) -> p h d", h=8), [("rawq", par)], 8, 96, qgt, "qgt", 64, 16, cosM[:, i, :], sinM[:, i, :], dq, ("drq", par))
            ck_ = hnr_stages(("ck", par), Tk[par], rawk[par].rearrange("p (h d) -> p h d", h=8), [("rawk", par)], 8, 96, kgt, "kgt", 64, 16, cosM[:, i, :], sinM[:, i, :], dk, ("drk", par))
            return [cq_[:6], ck_[:6]], [cq_[6:], ck_[6:]]

        def b2_out(i):
            ts = slice(i * 128, (i + 1) * 128)
            par = i % 2
            dq = drq[par].rearrange("p (h d) -> p h d", h=8); dk = drk[par].rearrange("p (h d) -> p h d", h=8)
            for (dd, dkey_, dstT, okey) in ((dq, ("drq", par), QT, "QT"), (dk, ("drk", par), KT, "KT")):
                b = nb(); pb = P[b][:, :].bitcast(BF16)
                for h in range(8):
                    tp(pb[0:96, h * 128:(h + 1) * 128], dd[:, h, :], ident, r=[dkey_, "ident"], w=[PK[b]])
                act(dstT[0:96, :, ts], pb[0:96, :].rearrange("p (h t) -> p h t", h=8), AF.Copy, r=[PK[b]], w=[(okey, i)])

        b2_front(0); b2_front(1)
        h1_, h2_ = b2_chains(0)
        run_interleaved(h1_)
        for i in range(NT):
            if i + 2 < NT:
                b2_front(i + 2)
            nxt = b2_chains(i + 1) if i + 1 < NT else ([], [])
            run_interleaved(nxt[0] + h2_)
            h2_ = nxt[1]
            if i >= 1:
                b2_out(i - 1)
        b2_out(NT - 1)
        QTk = [("QT", i) for i in range(NT)]
        dbg_out("QT", QT[0:96].rearrange("p k t -> p (k t)"), QTk)
        dbg_out("KT", KT[0:96].rearrange("p k t -> p (k t)"), [("KT", i) for i in range(NT)])
        dbg_out("V", V.rearrange("p a b c -> p (a b c)"), [("V", i) for i in range(NT)] + ["Vones"])
        if upto <= 3:
            return finish(nc, S, st)
        S.barrier()

        nbanks[0] = 6
        M.lo = lo0
        om = M.lo_alloc([NT, 512], BF16)
        PT = [M.lo_alloc([512], BF16) for _ in range(4)]
        rec4 = [M.lo_alloc([4], F32) for _ in range(2)]
        pt_i = [0]

        gchunk = [0]

        def causal_attn_multi(jobs):
            steps = []
            for ji in range(len(jobs)):
                for c in range(4):
                    for kt in range(4 * c + 4):
                        steps.append((ji, c, kt))

            def emit_qk(step):
                ji, c, kt = step
                J = jobs[ji]
                q0 = max(kt - 4 * c, 0)
                n = 512 - 128 * q0
                b = nb()
                has_extra = J["extra"] is not None
                mm(P[b][:, 0:n], J["KT"][0:J["kn"], kt * 128:(kt + 1) * 128], J["QT"](c * 512 + q0 * 128, (c + 1) * 512),
                   True, not has_extra, r=J["kk"](kt) + J["qk"](c), w=[PK[b]])
                if has_extra:
                    J["extra"](P[b][:, 0:n], kt, c * 512 + q0 * 128, (c + 1) * 512, PK[b])
                return b, n, q0

            LA = 2
            pend = [emit_qk(steps[q]) for q in range(min(LA, len(steps)))]
            for si, (ji, c, kt) in enumerate(steps):
                J = jobs[ji]
                b, n, q0 = pend.pop(0)
                if si + LA < len(steps):
                    pend.append(emit_qk(steps[si + LA]))
                if kt == 0:
                    gchunk[0] += 1
                ab = 6 + (gchunk[0] % 2)
                Oacc = P[ab][:, 0:260].rearrange("p (q d) -> p q d", q=4)
                pi = pt_i[0]; pt_i[0] = (pi + 1) % len(PT)
                pt = PT[pi]
                act(pt[:, 0:n], P[b][:, 0:n], AF.Exp, r=[PK[b]], w=[("PT", pi)], scale=J["scale"])
                if kt >= 4 * c:
                    S.op("dve", lambda e, pt=pt: e.tensor_tensor(out=pt[:, 0:128], in0=pt[:, 0:128], in1=tri, op=ALU.mult), r=[("PT", pi), "tri"], w=[("PT", pi)])
                for qi in range(q0, 4):
                    mm(Oacc[:, qi, :], pt[:, (qi - q0) * 128:(qi - q0 + 1) * 128], J["V"](kt), kt == 0 and qi == 0, kt == 4 * c + qi,
                       r=[("PT", pi)] + J["vk"](kt), w=[PK[ab]], skip_group_check=True)
                if kt == 4 * c + 3:
                    J["fin"](c, Oacc, PK[ab])

        jobs = []
        for h in range(8):
            def fin(c, Oacc, pk, h=h):
                rc = rec4[c % 2]
                S.op("dve", lambda e: e.reciprocal(out=rc, in_=Oacc[:, :, 64]), r=[pk], w=[("rec4", c % 2)])
                S.op("dve", lambda e: e.tensor_tensor(out=om[:, 4 * c:4 * c + 4, h * 64:(h + 1) * 64], in0=Oacc[:, :, 0:64],
                                                      in1=rc.unsqueeze(2).to_broadcast([128, 4, 64]), op=ALU.mult),
                     r=[pk, ("rec4", c % 2)], w=[("om", c)])
            jobs.append(dict(KT=KT[:, h, :], kn=96, QT=(lambda a, b_, h=h: QT[0:96, h, a:b_]), V=(lambda kt, h=h: V[:, kt, h, :]), scale=96 ** -0.5,
                             extra=None, fin=fin, qk=(lambda c: [("QT", 4 * c + q) for q in range(4)]), kk=(lambda kt: [("KT", kt)]),
                             vk=(lambda kt: [("V", kt), "Vones"])))
        causal_attn_multi(jobs)
        for i in range(NT):
            b = nb(); pb = P[b][:, :].bitcast(BF16)
            for j in range(4):
                tp(pb[:, j * 128:(j + 1) * 128], om[:, i, j * 128:(j + 1) * 128], ident, r=[("om", i // 4), "ident"], w=[PK[b]])
            act(omT[:, :, i * 128:(i + 1) * 128], pb[:, 0:512].rearrange("p (k t) -> p k t", k=4), AF.Copy, r=[PK[b]], w=[("omT", i)])
        dbg_out("om", om.rearrange("p a b -> p (a b)"), [("om", c) for c in range(4)])
        if upto <= 4:
            return finish(nc, S, st)
        S.barrier()

        nbanks[0] = 4
        M.lo = lo0; M.hi = hi0
        qnT = M.lo_alloc([8, S_], BF16); ksT = M.lo_alloc([2, S_], BF16); kwT = M.lo_alloc([2, S_], BF16)
        vs = M.lo_alloc([NT, 2, 65], BF16); vw = M.lo_alloc([NT, 2, 65], BF16)
        gates = M.lo_alloc([NT, 3, 8], F32)
        kcmpT = M.lo_alloc([2, 128], BF16); VCX = M.lo_alloc([2, 97], BF16)
        PT = [M.lo_alloc([512], BF16) for _ in range(4)]
        rec4 = [M.lo_alloc([4], F32) for _ in range(2)]
        t1 = M.lo_alloc([512], F32); t2 = M.lo_alloc([512], F32)
        hss = M.lo_alloc([8], F32); hrs = M.lo_alloc([8], F32)
        ra = M.lo_alloc([128], F32); rbb = M.lo_alloc([128], F32)
        drb = [M.lo_alloc([512], BF16) for _ in range(2)]
        nqt = M.lo_alloc([64], F32); nkct = M.lo_alloc([64], F32); nkst = M.lo_alloc([64], F32); nkwt = M.lo_alloc([64], F32)
        lo2 = M.lo
        kc2 = M.hi_alloc([2, S_], BF16); vc2 = M.hi_alloc([2, S_], BF16)
        hi_kv = M.hi
        hT = M.hi_alloc([8, S_], BF16)
        Wqn = M.hi_alloc([8, 512], BF16); Wkc2 = M.hi_alloc([8, 256], BF16); Wvc2 = M.hi_alloc([8, 256], BF16)
        Wkv4 = M.hi_alloc([8, 512], BF16); Wgn = M.hi_alloc([8, 24], BF16)
        ge = M.hi_alloc([24], F32)
        S.dma("sp", hT.rearrange("p k t -> p (k t)"), hTs, r=["hTs"], w=["hTall"])
        for t_, d_ in ((nqt, nqB), (nkct, nkcB), (nkst, nksB), (nkwt, nkwB)):
            S.dma("sp", t_, d_, w=["ngain"])
        load_w(Wqn, w_qn, "Wqn", 8, 512); load_w(Wkv4, w_kv4, "Wkv4", 8, 512); load_w(Wgn, w_gn, "Wgn", 8, 24)
        load_w(Wkc2, w_kc2, "Wkc2", 8, 256); load_w(Wvc2, w_vc2, "Wvc2", 8, 256)
        for g_ in range(2):
            S.dma("sp", ksT[64:96, g_, :], XEall_d, w=["ksTx"])
        S.op("pool", lambda e: e.memset(vs[:, :, :, 64:65], 1.0), w=["vsones"])
        S.op("pool", lambda e: e.memset(vw[:, :, :, 64:65], 1.0), w=["vwones"])
        S.op("pool", lambda e: e.memset(kc2[64:128, :, S_ - 1:S_], 0.0), w=["kc2pad"])
        S.op("pool", lambda e: e.memset(vc2[64:128, :, S_ - 1:S_], 0.0), w=["vc2pad"])
        MsubD = Mem(big, NB); MsubD.lo = lo_pers + 16384; MsubD.hi = lo0
        TDq = [mk_tmps(MsubD, 512, 8, 8) for _ in range(2)]; TDs = [mk_tmps(MsubD, 128, 2, 8) for _ in range(2)]; TDw = [mk_tmps(MsubD, 128, 2, 8) for _ in range(2)]
        dnq = [MsubD.lo_alloc([512], BF16) for _ in range(2)]
        dns = [MsubD.lo_alloc([128], BF16) for _ in range(2)]; dnw = [MsubD.lo_alloc([128], BF16) for _ in range(2)]

        def d_front(i):
            ts = slice(i * 128, (i + 1) * 128)
            bq = 4 + 2 * (i % 2)
            for k in range(8):
                mm(P[bq][:, :], hT[:, k, ts], Wqn[:, k, :], k == 0, k == 7, r=["hTall", "Wqn"], w=[PK[bq]])
            bk = 5 + 2 * (i % 2)
            for k in range(8):
                mm(P[bk][:, :], hT[:, k, ts], Wkv4[:, k, :], k == 0, k == 7, r=["hTall", "Wkv4"], w=[PK[bk]])
            bg = nb()
            for k in range(8):
                mm(P[bg][:, 0:24], hT[:, k, ts], Wgn[:, k, :], k == 0, k == 7, r=["hTall", "Wgn"], w=[PK[bg]])
            act(vs[:, i, :, 0:64], P[bk][:, 256:384].rearrange("p (g d) -> p g d", g=2), AF.Copy, r=[PK[bk]], w=[("vs", i)])
            act(vw[:, i, :, 0:64], P[bk][:, 384:512].rearrange("p (g d) -> p g d", g=2), AF.Copy, r=[PK[bk]], w=[("vw", i)])
            act(ge, P[bg][:, 0:24], AF.Exp, r=[PK[bg]], w=["ge"], scale=-1.0)
            S.op("dve", lambda e: e.tensor_scalar(out=ge, in0=ge, scalar1=1.0, scalar2=None, op0=ALU.add), r=["ge"], w=["ge"])
            S.op("dve", lambda e, i=i: e.reciprocal(out=gates[:, i].rearrange("p a b -> p (a b)"), in_=ge), r=["ge"], w=[("gates", i)])
            return bq, bk

        def d_chains(i, bq, bk):
            par = i % 2
            dq = dnq[par].rearrange("p (h d) -> p h d", h=8)
            ds_ = dns[par].rearrange("p (h d) -> p h d", h=2); dw_ = dnw[par].rearrange("p (h d) -> p h d", h=2)
            c1 = hnr_stages(("dq", par), TDq[par], P[bq][:, :].rearrange("p (h d) -> p h d", h=8), [PK[bq]], 8, 64, nqt, "ngain", 0, 8, cosN[:, i, :], sinN[:, i, :], dq, ("dnq", par))
            c2 = hnr_stages(("ds", par), TDs[par], P[bk][:, 0:128].rearrange("p (h d) -> p h d", h=2), [PK[bk]], 2, 64, nkst, "ngain", 0, 8, cosN[:, i, :], sinN[:, i, :], ds_, ("dns", par))
            c3 = hnr_stages(("dw", par), TDw[par], P[bk][:, 128:256].rearrange("p (h d) -> p h d", h=2), [PK[bk]], 2, 64, nkwt, "ngain", 0, 8, cosN[:, i, :], sinN[:, i, :], dw_, ("dnw", par))
            return [c1[:6], c2[:6], c3[:6]], [c1[6:], c2[6:], c3[6:]]

        def d_out(i):
            ts = slice(i * 128, (i + 1) * 128)
            par = i % 2
            b = nb(); pb = P[b][:, :].bitcast(BF16)
            for p_ in range(8):
                tp(pb[0:64, p_ * 128:(p_ + 1) * 128], dnq[par][:, p_ * 64:(p_ + 1) * 64], ident, r=[("dnq", par), "ident"], w=[PK[b]])
            act(qnT[0:64, :, ts], pb[0:64, :].rearrange("p (k t) -> p k t", k=8), AF.Copy, r=[PK[b]], w=[("qnT", i)])
            for (dd, dkey_, dstT, dk) in ((dns[par], ("dns", par), ksT, "ksT"), (dnw[par], ("dnw", par), kwT, "kwT")):
                b2 = nb(); pb = P[b2][:, :].bitcast(BF16)
                for g_ in range(2):
                    tp(pb[0:64, g_ * 128:(g_ + 1) * 128], dd[:, g_ * 64:(g_ + 1) * 64], ident, r=[dkey_, "ident"], w=[PK[b2]])
                act(dstT[0:64, :, ts], pb[0:64, 0:256].rearrange("p (g t) -> p g t", g=2), AF.Copy, r=[PK[b2]], w=[(dk, i)])

        def fm_cmp_proj(idx):
            c, rem = idx // 4, idx % 4
            (Wt, wk, dst, dk) = ((Wkc2, "Wkc2", kc2, "kc2"), (Wvc2, "Wvc2", vc2, "vc2"))[rem // 2]
            g = rem % 2
            b = nb()
            for k in range(8):
                mm(P[b][:, :], Wt[:, k, g * 128:(g + 1) * 128], hT[:, k, c * 512:(c + 1) * 512], k == 0, k == 7, r=["hTall", wk], w=[PK[b]])
            act(dst[0:64, g, c * 512:(c + 1) * 512], P[b][0:64, :], AF.Copy, r=[PK[b]], w=[dk])
            if c == 0:
                act(dst[64:128, g, 0:511], P[b][64:128, 1:512], AF.Copy, r=[PK[b]], w=[dk])
            else:
                act(dst[64:128, g, c * 512 - 1:(c + 1) * 512 - 1], P[b][64:128, :], AF.Copy, r=[PK[b]], w=[dk])

        fr_ = {0: d_front(0), 1: d_front(1)}
        h1_, h2_ = d_chains(0, *fr_[0])
        run_interleaved(h1_)
        for i in range(NT):
            if i + 2 < NT:
                fr_[i + 2] = d_front(i + 2)
            nxt = d_chains(i + 1, *fr_[i + 1]) if i + 1 < NT else ([], [])
            run_interleaved(nxt[0] + h2_)
            h2_ = nxt[1]
            fm_cmp_proj(i)
            if i >= 1:
                d_out(i - 1)
        d_out(NT - 1)
        dbg_out("qnT", qnT[0:64].rearrange("p k t -> p (k t)"), [("qnT", i) for i in range(NT)])
        dbg_out("ksT", ksT[0:64].rearrange("p k t -> p (k t)"), [("ksT", i) for i in range(NT)])
        dbg_out("gates", gates.rearrange("p a b c -> p (a b c)"), [("gates", i) for i in range(NT)])
        dbg_out("kc2", kc2.rearrange("p k t -> p (k t)"), ["kc2", "kc2pad"])
        if upto <= 5:
            return finish(nc, S, st)
        S.barrier()

        nbanks[0] = 8
        M.hi = hi_kv
        hiE = M.hi
        W1k = M.lo_alloc([16, 256], BF16); W1v = M.lo_alloc([16, 256], BF16)
        W2k = M.lo_alloc([2, 64], BF16); W2v = M.lo_alloc([2, 64], BF16)
        pkf = M.lo_alloc([16], F32); pvf = M.lo_alloc([16], F32); pkb = M.lo_alloc([16], BF16); pvb = M.lo_alloc([16], BF16)
        biask = M.lo_alloc([2], F32); biasv = M.lo_alloc([2], F32)
        hid = [M.lo_alloc([128], BF16) for _ in range(2)]
        ovl = M.lo_alloc([32], BF16)
        load_w(W1k, w1k, "W1k", 16, 256); load_w(W1v, w1v, "W1v", 16, 256)
        load_w(W2k, w2k, "W2k", 2, 64); load_w(W2v, w2v, "W2v", 2, 64)
        S.dma("sp", pkf, posk, w=["pkf"]); S.dma("sp", pvf, posv, w=["pvf"]); S.dma("sp", ovl[0:127], ovl_d, w=["ovl"])
        S.op("pool", lambda e: e.memset(VCX[0:127, :, 64:65], 1.0), w=["VCXa"])
        for g in range(2):
            S.op("pool", lambda e, g=g: e.tensor_copy(out=VCX[0:127, g, 65:97], in_=ovl[0:127]), r=["ovl"], w=["VCXb"])
        rt = [M.lo_alloc([128], BF16) for _ in range(3)]
        rt_i = [0]
        for (W1, w1key, W2, w2key, src, skey, posf_, pkey, isk) in ((W1k, "W1k", W2k, "W2k", kc2, ["kc2", "kc2pad"], pkf, "pkf", True),
                                                                    (W1v, "W1v", W2v, "W2v", vc2, ["vc2", "vc2pad"], pvf, "pvf", False)):
            srcv = src.rearrange("p g (n s) -> p g n s", s=16)
            bo = nb()
            for g in range(2):
                bh = []
                for hc in range(2):
                    b = nb()
                    while b == bo or b in bh:
                        b = nb()
                    bh.append(b)
                for lc in range(16):
                    ri = rt_i[0]; rt_i[0] = (ri + 1) % 3
                    rtv = rt[ri][:, 0:127]
                    S.op("dve", lambda e, rtv=rtv, g=g, lc=lc, srcv=srcv, posf_=posf_: e.tensor_scalar(
                        out=rtv, in0=srcv[:, g, (2 * lc) // 16:(2 * lc) // 16 + 127, (2 * lc) % 16], scalar1=posf_[:, lc:lc + 1], scalar2=None, op0=ALU.add),
                        r=skey + [pkey], w=[("rt", ri)])
                    for hc in range(2):
                        mm(P[bh[hc]][:, 0:127], W1[:, lc, hc * 128:(hc + 1) * 128], rtv, lc == 0, lc == 15, r=[w1key, ("rt", ri)], w=[PK[bh[hc]]], sig=(hc == 1 or lc == 15))
                for hc in range(2):
                    act(hid[hc][:, 0:127], P[bh[hc]][:, 0:127], AF.Silu, r=[PK[bh[hc]]], w=[("hid", hc)])
                for hc in range(2):
                    mm(P[bo][0:127, g * 64:(g + 1) * 64], hid[hc][:, 0:127], W2[:, hc, :], hc == 0, hc == 1, r=[("hid", hc), w2key], w=[PK[bo]])
            if isk:
                d = drb[1][0:127, 0:128].rearrange("p (h d) -> p h d", h=2)
                head_norm_rope(P[bo][0:127, 0:128].rearrange("p (h d) -> p h d", h=2), [PK[bo]], 2, 64, nkct, "ngain", 0, 8, cosC[0:127], sinC[0:127], d, "drb1", np_=127)
                b2 = nb(); pb = P[b2][:, :].bitcast(BF16)
                for g_ in range(2):
                    tp(pb[0:64, g_ * 128:g_ * 128 + 127], drb[1][0:127, g_ * 64:(g_ + 1) * 64], ident[0:127, 0:127], r=["drb1", "ident"], w=[PK[b2]])
                act(kcmpT[0:64, :, 0:127], pb[0:64, 0:256].rearrange("p (g t) -> p g t", g=2)[:, :, 0:127], AF.Copy, r=[PK[b2]], w=["kcmpT"])
            else:
                act(VCX[0:127, :, 0:64], P[bo][0:127, 0:128].rearrange("p (g d) -> p g d", g=2), AF.Copy, r=[PK[bo]], w=["VCXc"])
        dbg_out("kcmpT", kcmpT[0:64].rearrange("p g t -> p (g t)"), ["kcmpT"])
        dbg_out("VCX", VCX[0:127].rearrange("p a b -> p (a b)"), ["VCXa", "VCXb", "VCXc"])
        if upto <= 6:
            return finish(nc, S, st)
        S.barrier()

        M.lo = lo2; M.hi = hi0
        onsa = M.hi_alloc([NT, 512], F32)
        mbT = M.hi_alloc([2, S_], BF16)
        vmT = M.hi_alloc([S_], BF16); XE = M.hi_alloc([NT, 128], BF16)
        fb = M.hi_alloc([NT, 32], F32); vj = M.hi_alloc([NT, 32], F32)
        pc = [M.lo_alloc([4, 128], BF16) for _ in range(2)]
        rsum = M.lo_alloc([8], F32); rec8 = M.lo_alloc([8], F32); gr = M.lo_alloc([8], F32)
        tmp_i = M.lo_alloc([8, 32], F32); imp = M.lo_alloc([2, 32], F32); m8 = M.lo_alloc([2, 8], F32)
        sel = M.lo_alloc([2, 32], F32); mbf = M.lo_alloc([2, 96], BF16)
        S.op("pool", lambda e: e.memset(mbf, 0.0), w=["mbf"])
        tmpo = M.lo_alloc([8, 64], F32)
        pw = [M.lo_alloc([3, 128], BF16) for _ in range(4)]
        onb = [M.lo_alloc([512], BF16) for _ in range(2)]
        S.dma("sp", vmT[0:127], vmT_d, w=["vmT"]); S.dma("sp", XE[0:32].rearrange("p a b -> p (a b)"), XE_d, w=["XE"])
        S.dma("sp", fb.rearrange("p a b -> p (a b)"), fb_d, w=["fb"]); S.dma("sp", vj.rearrange("p a b -> p (a b)"), vj_d, w=["vj"])
        VCXk = ["VCXa", "VCXb", "VCXc"]
        for i in range(NT):
            ts = slice(i * 128, (i + 1) * 128)
            import os
            if int(os.environ.get('KDEV_F', '9')) <= 0:
                continue
            sb_ = [nb(), nb()]
            ob = [nb(), nb()]
            for p in range(8):
                j, g = p // 2, p % 2
                if os.environ.get('KDEV_G0'):
                    g = 0
                mm(P[sb_[p // 4]][0:127, (p % 4) * 128:(p % 4 + 1) * 128], kcmpT[0:64, g, 0:127], qnT[0:64, p, ts], True, True,
                   r=["kcmpT", ("qnT", i)], w=[PK[sb_[p // 4]]])
            for hf_ in range(2):
                pcv = pc[hf_]
                act(pcv[0:127], P[sb_[hf_]][0:127, :].rearrange("p (a b) -> p a b", a=4), AF.Exp, r=[PK[sb_[hf_]]], w=[("pc", hf_)], scale=0.125)
                S.op("dve", lambda e, pcv=pcv: e.tensor_tensor(out=pcv[0:127], in0=pcv[0:127], in1=vmT[0:127, ts].unsqueeze(1).to_broadcast([127, 4, 128]), op=ALU.mult),
                     r=[("pc", hf_), "vmT"], w=[("pc", hf_)])
            import os
            FL = int(os.environ.get('KDEV_F', '9'))
            if FL <= 1:
                continue
            for p in range(8):
                g = p % 2
                mm(P[ob[p // 4]][:, (p % 4) * 97:(p % 4 + 1) * 97], pc[p // 4][0:127, p % 4, :], VCX[0:127, g, :], True, True,
                   r=[("pc", p // 4)] + VCXk, w=[PK[ob[p // 4]]])
            if FL <= 2:
                continue
            OC = [P[ob[h_]][:, 0:388].rearrange("p (a b) -> p a b", a=4) for h_ in range(2)]
            for h_ in range(2):
                S.op("dve", lambda e, h_=h_: e.tensor_scalar(out=rsum[:, h_ * 4:(h_ + 1) * 4], in0=OC[h_][:, :, 64], scalar1=1e-30, scalar2=None, op0=ALU.max), r=[PK[ob[h_]]], w=["rsum"])
            S.op("dve", lambda e: e.reciprocal(out=rec8, in_=rsum), r=["rsum"], w=["rec8"])
            S.op("dve", lambda e, i=i: e.tensor_tensor(out=gr, in0=gates[:, i, 0, :], in1=rec8, op=ALU.mult), r=["rec8", ("gates", i)], w=["gr"])
            for h_ in range(2):
                S.op("dve", lambda e, h_=h_, i=i: e.tensor_tensor(out=onsa[:, i, h_ * 256:(h_ + 1) * 256].rearrange("p (a b) -> p a b", a=4), in0=OC[h_][:, :, 0:64],
                                                                   in1=gr[:, h_ * 4:(h_ + 1) * 4].unsqueeze(2).to_broadcast([128, 4, 64]), op=ALU.mult),
                     r=[PK[ob[h_]], "gr"], w=[("onsa", i)])
                S.op("dve", lambda e, h_=h_: e.tensor_tensor(out=tmp_i[:, h_ * 4:(h_ + 1) * 4, :], in0=OC[h_][:, :, 65:97],
                                                             in1=rec8[:, h_ * 4:(h_ + 1) * 4].unsqueeze(2).to_broadcast([128, 4, 32]), op=ALU.mult),
                     r=[PK[ob[h_]], "rec8"], w=["tmp_i"])
            if FL <= 3:
                continue
            S.op("dve", lambda e: e.tensor_reduce(out=imp, in_=tmp_i.rearrange("t (j g) n -> t g n j", g=2), axis=AX.X, op=ALU.add), r=["tmp_i"], w=["imp"])
            S.op("dve", lambda e, i=i: e.tensor_tensor(out=imp, in0=imp, in1=fb[:, i, :].unsqueeze(1).to_broadcast([128, 2, 32]), op=ALU.add), r=["imp", "fb"], w=["imp"])
            if FL <= 4:
                continue
            for g in range(2):
                S.op("dve", lambda e, g=g: e.max(out=m8[:, g, :], in_=imp[:, g, :]), r=["imp"], w=["m8"])
                S.op("dve", lambda e, g=g: e.tensor_scalar(out=sel[:, g, :], in0=imp[:, g, :], scalar1=m8[:, g, 7:8], scalar2=None, op0=ALU.is_ge), r=["imp", "m8"], w=["sel"])
            S.op("dve", lambda e, i=i: e.tensor_tensor(out=sel, in0=sel, in1=vj[:, i, :].unsqueeze(1).to_broadcast([128, 2, 32]), op=ALU.mult), r=["sel", "vj"], w=["sel"])
            S.op("dve", lambda e: e.tensor_scalar(out=mbf[:, :, 64:96], in0=sel, scalar1=-1.0, scalar2=30000.0, op0=ALU.add, op1=ALU.mult), r=["sel"], w=["mbf"])
            if i == 5:
                dbg_out("sel5", sel.rearrange("p a b -> p (a b)"), ["sel"])
                dbg_out("imp5", imp.rearrange("p a b -> p (a b)"), ["imp"])
            if FL <= 5:
                continue
            b = nb(); pb = P[b][:, :].bitcast(BF16)
            for g in range(2):
                tp(pb[0:96, g * 128:(g + 1) * 128], mbf[:, g, :], ident, r=["mbf", "ident"], w=[PK[b]])
            qm = qnT[64:96].rearrange("p (j g) t -> p j g t", g=2)
            for g in range(2):
                act(qm[:, :, g, ts], pb[64:96, g * 128:(g + 1) * 128].unsqueeze(1).to_broadcast([32, 4, 128]), AF.Copy, r=[PK[b]], w=[("mbq", i)])
        dbg_out("onsa_c", onsa.rearrange("p a b -> p (a b)"), [("onsa", i) for i in range(NT)])
        dbg_out("mbT", qnT[64:96, 0:2, :].rearrange("p a b -> p (a b)"), [("mbq", i) for i in range(NT)])
        if upto <= 7:
            return finish(nc, S, st)

        S.barrier()
        nbanks[0] = 6
        jobs = []
        for p in range(8):
            j, g = p // 2, p % 2

            def extra(ps_ap, kt, a, b_, pk, g=g):
                mm(ps_ap, XE[0:32, kt, :], mbT[0:32, g, a:b_], False, True, r=["XE"] + [("mbT", q) for q in range(a // 128, b_ // 128)], w=[pk])

            def fin(c, Oacc, pk, p=p):
                rc = rec4[c % 2]
                S.op("dve", lambda e: e.reciprocal(out=rc, in_=Oacc[:, :, 64]), r=[pk], w=[("rec4", c % 2)])
                S.op("dve", lambda e: e.tensor_tensor(out=rc, in0=rc, in1=gates[:, 4 * c:4 * c + 4, 1, p], op=ALU.mult), r=[("rec4", c % 2)] + [("gates", 4 * c + q) for q in range(4)], w=[("rec4", c % 2)])
                tv = tmpo[:, 0:4, :]
                S.op("dve", lambda e: e.tensor_tensor(out=tv, in0=Oacc[:, :, 0:64], in1=rc.unsqueeze(2).to_broadcast([128, 4, 64]), op=ALU.mult), r=[pk, ("rec4", c % 2)], w=["tmpo"])
                S.op("pool", lambda e: e.tensor_tensor(out=onsa[:, 4 * c:4 * c + 4, p * 64:(p + 1) * 64], in0=onsa[:, 4 * c:4 * c + 4, p * 64:(p + 1) * 64], in1=tv, op=ALU.add),
                     r=["tmpo"] + [("onsa", 4 * c + q) for q in range(4)], w=[("onsa", 4 * c + q) for q in range(4)])
            jobs.append(dict(KT=ksT[:, g, :], kn=96, QT=(lambda a, b_, p=p: qnT[0:96, p, a:b_]), V=(lambda kt, g=g: vs[:, kt, g, :]), scale=0.125,
                             extra=None, fin=fin, qk=(lambda c: [("qnT", 4 * c + q) for q in range(4)] + [("mbq", 4 * c + q) for q in range(4)]), kk=(lambda kt: [("ksT", kt), "ksTx"]),
                             vk=(lambda kt: [("vs", kt), "vsones"])))
        causal_attn_multi(jobs)
        dbg_out("onsa_cs", onsa.rearrange("p a b -> p (a b)"), [("onsa", i) for i in range(NT)])
        if upto <= 8:
            return finish(nc, S, st)

        pw_i = [0]
        ob = [6, 7]

        def emit_ws(i, p):
            g = p % 2
            kts = [kt for kt in (i - 2, i - 1, i) if kt >= 0]
            b = nb()
            for kt in kts:
                sl = kt - (i - 2)
                mm(P[b][:, sl * 128:(sl + 1) * 128], kwT[0:64, g, kt * 128:(kt + 1) * 128], qnT[0:64, p, i * 128:(i + 1) * 128], True, True,
                   r=[("kwT", kt), ("qnT", i)], w=[PK[b]])
            return b

        wsteps = [(i, p) for i in range(NT) for p in range(8)]
        WLA = 3
        wpend = [emit_ws(*wsteps[q]) for q in range(WLA)]
        for wi_, (i, p) in enumerate(wsteps):
            ts = slice(i * 128, (i + 1) * 128)
            g = p % 2
            kts = [kt for kt in (i - 2, i - 1, i) if kt >= 0]
            b = wpend.pop(0)
            if wi_ + WLA < len(wsteps):
                wpend.append(emit_ws(*wsteps[wi_ + WLA]))
            s0 = kts[0] - (i - 2)
            wi = pw_i[0]; pw_i[0] = (wi + 1) % len(pw)
            pwv = pw[wi]
            act(pwv[:, s0:3, :], P[b][:, s0 * 128:384].rearrange("p (a b) -> p a b", b=128), AF.Exp, r=[PK[b]], w=[("pw", wi)], scale=0.125)
            S.op("dve", lambda e, pwv=pwv, s0=s0: e.tensor_tensor(out=pwv[:, s0:3, :], in0=pwv[:, s0:3, :], in1=winm[:, s0:3, :], op=ALU.mult), r=[("pw", wi), "winm"], w=[("pw", wi)])
            for kt in kts:
                sl = kt - (i - 2)
                mm(P[ob[p // 4]][:, (p % 4) * 65:(p % 4 + 1) * 65], pwv[:, sl, :], vw[:, kt, g, :], kt == kts[0], kt == kts[-1],
                   r=[("pw", wi), ("vw", kt), "vwones"], w=[PK[ob[p // 4]]])
            if p != 7:
                continue
            OW = [P[ob[h_]][:, 0:260].rearrange("p (a b) -> p a b", a=4) for h_ in range(2)]
            for h_ in range(2):
                S.op("dve", lambda e, h_=h_: e.reciprocal(out=rec8[:, h_ * 4:(h_ + 1) * 4], in_=OW[h_][:, :, 64]), r=[PK[ob[h_]]], w=["rec8"])
            S.op("dve", lambda e, i=i: e.tensor_tensor(out=gr, in0=gates[:, i, 2, :], in1=rec8, op=ALU.mult), r=["rec8", ("gates", i)], w=["gr"])
            for h_ in range(2):
                S.op("dve", lambda e, h_=h_: e.tensor_tensor(out=tmpo[:, h_ * 4:(h_ + 1) * 4, :], in0=OW[h_][:, :, 0:64],
                                                             in1=gr[:, h_ * 4:(h_ + 1) * 4].unsqueeze(2).to_broadcast([128, 4, 64]), op=ALU.mult), r=[PK[ob[h_]], "gr"], w=["tmpo"])
            S.op("pool", lambda e, i=i: e.tensor_tensor(out=onb[i % 2], in0=onsa[:, i, :], in1=tmpo.rearrange("p a b -> p (a b)"), op=ALU.add), r=["tmpo", ("onsa", i)], w=[("onb", i % 2)])
            if "onsa_all" in D_:
                S.dma("sp", D_["onsa_all"][:, i * 512:(i + 1) * 512], onb[i % 2], r=[("onb", i % 2)])
            b = nb(); pb = P[b][:, :].bitcast(BF16)
            for j in range(4):
                tp(pb[:, j * 128:(j + 1) * 128], onb[i % 2][:, j * 128:(j + 1) * 128], ident, r=[("onb", i % 2), "ident"], w=[PK[b]])
            act(onT[:, :, ts], pb[:, 0:512].rearrange("p (k t) -> p k t", k=4), AF.Copy, r=[PK[b]], w=[("onT", i)])
        if upto <= 9:
            return finish(nc, S, st)
        S.barrier()

        nbanks[0] = 8
        M.lo = lo0; M.hi = hi0
        hT = M.hi_alloc([8, S_], BF16)
        mergedT = M.hi_alloc([8, S_], BF16)
        hi2 = M.hi
        Wgm = M.lo_alloc([8, 512], BF16); Wgnm = M.lo_alloc([8, 512], BF16); Wom = M.lo_alloc([4, 512], BF16); Won = M.lo_alloc([4, 512], BF16)
        e3 = M.lo_alloc([512], F32); e4 = M.lo_alloc([512], F32); tA = M.lo_alloc([512], F32); tB = M.lo_alloc([512], F32)
        mgb = [M.lo_alloc([512], BF16) for _ in range(2)]
        S.dma("sp", hT.rearrange("p k t -> p (k t)"), hTs, r=["hTs"], w=["hTall"])
        e3 = [e3, M.lo_alloc([512], F32)]; e4 = [e4, M.lo_alloc([512], F32)]
        tA = [tA, M.lo_alloc([512], F32)]; tB = [tB, M.lo_alloc([512], F32)]

        def i_front(cc, i, it):
            ts = slice(i * 128, (i + 1) * 128)
            bs = [4 * (it % 2) + q for q in range(4)]
            b1, b2, b3, b4 = bs
            for k in range(8):
                mm(P[b3][:, :], hT[:, k, ts], Wgm[:, k, :], k == 0, k == 7, r=["hTall", "Wgm"], w=[PK[b3]])
            for k in range(8):
                mm(P[b4][:, :], hT[:, k, ts], Wgnm[:, k, :], k == 0, k == 7, r=["hTall", "Wgnm"], w=[PK[b4]])
            for k in range(4):
                mm(P[b1][:, :], omT[:, k, ts], Wom[:, k, :], k == 0, k == 3, r=[("omT", i), "Wom"], w=[PK[b1]])
            for k in range(4):
                mm(P[b2][:, :], onT[:, k, ts], Won[:, k, :], k == 0, k == 3, r=[("onT", i), "Won"], w=[PK[b2]])
            return bs

        def i_back(cc, i, it, bs):
            ts = slice(i * 128, (i + 1) * 128)
            b1, b2, b3, b4 = bs
            par = it % 2
            act(e3[par], P[b3][:, :], AF.Sigmoid, r=[PK[b3]], w=[("e3", par)])
            act(e4[par], P[b4][:, :], AF.Sigmoid, r=[PK[b4]], w=[("e4", par)])
            S.op("dve", lambda e: e.tensor_tensor(out=tA[par], in0=P[b1][:, :], in1=e3[par], op=ALU.mult), r=[PK[b1], ("e3", par)], w=[("tA", par)])
            S.op("dve", lambda e: e.tensor_tensor(out=tB[par], in0=P[b2][:, :], in1=e4[par], op=ALU.mult), r=[PK[b2], ("e4", par)], w=[("tB", par)])
            S.op("dve", lambda e: e.tensor_tensor(out=mgb[par], in0=tA[par], in1=tB[par], op=ALU.add), r=[("tA", par), ("tB", par)], w=[("mgb", par)])
            if "merged" in D_:
                S.dma("sp", D_["merged"][i * 128:(i + 1) * 128, cc * 512:(cc + 1) * 512], mgb[par], r=[("mgb", par)])
            pb = P[b3][:, :].bitcast(BF16)
            for j in range(4):
                tp(pb[:, j * 128:(j + 1) * 128], mgb[par][:, j * 128:(j + 1) * 128], ident, r=[("mgb", par), "ident"], w=[PK[b3]])
            act(mergedT[:, cc * 4:(cc + 1) * 4, ts], pb[:, 0:512].rearrange("p (k t) -> p k t", k=4), AF.Copy, r=[PK[b3]], w=[("mergedT", i)])

        it = 0
        for cc in range(2):
            load_w_cols(Wgm, w_gm, "Wgm", 8, cc * 512, (cc + 1) * 512); load_w_cols(Wgnm, w_gnm, "Wgnm", 8, cc * 512, (cc + 1) * 512)
            load_w_cols(Wom, wo_mla, "Wom", 4, cc * 512, (cc + 1) * 512); load_w_cols(Won, wo_nsa, "Won", 4, cc * 512, (cc + 1) * 512)
            pend_ = i_front(cc, 0, it)
            for i in range(NT):
                cur_ = pend_
                if i + 1 < NT:
                    pend_ = i_front(cc, i + 1, it + 1)
                i_back(cc, i, it, cur_)
                it += 1
        if upto <= 10:
            return finish(nc, S, st)
        S.barrier()

        M.lo = lo0
        h2T = hT
        Wout = M.lo_alloc([8, D], BF16)
        G1 = M.lo_alloc([D], F32); A2 = M.lo_alloc([D], F32); B2 = M.lo_alloc([D], F32)
        xb = [M.lo_alloc([D], F32) for _ in range(2)]
        x1t = [M.lo_alloc([D], F32) for _ in range(2)]
        junk = M.lo_alloc([D], F32); tmpA = M.lo_alloc([D], F32)
        hb = [M.lo_alloc([D], BF16) for _ in range(2)]
        ssq = M.lo_alloc([NT], F32); rs = M.lo_alloc([NT], F32)
        S.dma("sp", G1, mods[:, 2 * D:3 * D], r=["mods"], w=["G1"])
        S.dma("sp", B2, mods[:, 3 * D:4 * D], r=["mods"], w=["AB2"])
        S.dma("sp", A2, mods[:, 4 * D:5 * D], r=["mods"], w=["AB2"])
        load_w(Wout, w_out, "Wout", 8, D, ceng="pool")
        tmpA2 = [tmpA, M.lo_alloc([D], F32)]
        tmpJ = [M.lo_alloc([D], F32) for _ in range(3)]
        xb = xb + [M.lo_alloc([D], F32)]
        x1t = x1t + [M.lo_alloc([D], F32)]

        def j_front(i):
            ts = slice(i * 128, (i + 1) * 128)
            par = i % 3
            S.dma("sp", xb[par], x[ts, :], w=[("xb", par)])
            for cc in range(2):
                b = 2 * par + cc
                for k in range(8):
                    mm(P[b][:, :], mergedT[:, k, ts], Wout[:, k, cc * 512:(cc + 1) * 512], k == 0, k == 7, r=[("mergedT", i), "Wout"], w=[PK[b]])

        def j_mid(i):
            ts = slice(i * 128, (i + 1) * 128)
            par = i % 3
            for cc in range(2):
                b = 2 * par + cc
                S.op("dve", lambda e, b=b, cc=cc: e.tensor_tensor(out=tmpJ[par][:, cc * 512:(cc + 1) * 512], in0=P[b][:, :], in1=G1[:, cc * 512:(cc + 1) * 512], op=ALU.mult), r=[PK[b], "G1"], w=[("tmpJ", par)])
                S.op("pool", lambda e, cc=cc: e.tensor_tensor(out=x1t[par][:, cc * 512:(cc + 1) * 512], in0=tmpJ[par][:, cc * 512:(cc + 1) * 512], in1=xb[par][:, cc * 512:(cc + 1) * 512], op=ALU.add),
                     r=[("tmpJ", par), ("xb", par)], w=[("x1t", par)])
            S.dma("sp", x1s[ts, :], x1t[par], r=[("x1t", par)], w=[("x1s", i)])
            norm_a(i, x1t[par], ("x1t", par))

        bank_state[0] = 0

        def nbJ():
            b = 6 + (bank_state[0] % 2)
            bank_state[0] = (bank_state[0] + 1) % 2
            return b
        nb_saved = nb
        nb = nbJ
        j_front(0); j_mid(0); j_front(1); j_mid(1)
        for i in range(NT):
            if i + 2 < NT:
                j_front(i + 2)
            b_ = norm_b1(i, x1t[i % 3], ("x1t", i % 3), A2, B2, ["AB2"])
            if i + 2 < NT:
                j_mid(i + 2)
            norm_b2(i, b_, h2T, "h2T")
        nb = nb_saved
        bank_state[0] = 0
        if "x1" in D_:
            S.dma("sp", D_["x1"], x1s, r=[("x1s", i) for i in range(NT)])
        if upto <= 11:
            return finish(nc, S, st)
        S.barrier()

        M.lo = lo_pers; M.hi = hi2 + 8 * S_ * 2
        Wd = M.hi_alloc([NFC, D], BF16)
        actT = M.hi_alloc([NFC, 1024], BF16)
        G2 = M.lo_alloc([D], F32)
        Wg2 = [M.lo_alloc([8, 256], BF16) for _ in range(2)]; Wu2 = [M.lo_alloc([8, 256], BF16) for _ in range(2)]
        sg = [M.lo_alloc([512], F32) for _ in range(2)]
        xb = [M.lo_alloc([D], F32) for _ in range(2)]
        ot = [M.lo_alloc([D], F32) for _ in range(2)]
        tmpA = M.lo_alloc([D], F32)
        S.dma("sp", G2, mods[:, 5 * D:6 * D], r=["mods"], w=["G2"])
        load_w(Wd, wd, "Wd", NFC, D)
        h2k = [("h2T", i) for i in range(NT)]
        out_toks = []
        for half in range(2):
            def ld(jg_):
                wb_ = jg_ % 2
                load_w_cols(Wg2[wb_], wg, ("Wg2", wb_), 8, jg_ * 256, (jg_ + 1) * 256)
                load_w_cols(Wu2[wb_], wu, ("Wu2", wb_), 8, jg_ * 256, (jg_ + 1) * 256)
            ld(0)
            for jg in range(NFC // 2):
                wb = jg % 2
                if jg + 1 < NFC // 2:
                    ld(jg + 1)
                for jj in range(2):
                    j = jg * 2 + jj
                    for tc in range(2):
                        t0 = half * 1024 + tc * 512
                        bg, bu = nb(), nb()
                        for k in range(8):
                            mm(P[bg][:, :], Wg2[wb][:, k, jj * 128:(jj + 1) * 128], h2T[:, k, t0:t0 + 512], k == 0, k == 7, r=[("Wg2", wb)] + h2k, w=[PK[bg]])
                        for k in range(8):
                            mm(P[bu][:, :], Wu2[wb][:, k, jj * 128:(jj + 1) * 128], h2T[:, k, t0:t0 + 512], k == 0, k == 7, r=[("Wu2", wb)] + h2k, w=[PK[bu]])
                        act(sg[tc], P[bg][:, :], AF.Silu, r=[PK[bg]], w=[("sg", tc)])
                        S.op("dve", lambda e, j=j, tc=tc, bu=bu: e.tensor_tensor(out=actT[:, j, tc * 512:(tc + 1) * 512], in0=P[bu][:, :], in1=sg[tc], op=ALU.mult),
                             r=[PK[bu], ("sg", tc)], w=[("actT", tc)])
            for il in range(8):
                i = half * 8 + il
                ts = slice(i * 128, (i + 1) * 128)
                S.dma("sp", xb[i % 2], x1s[ts, :], r=[("x1s", i)], w=[("xb", i % 2)])
                for cc in range(2):
                    b = nb()
                    for j in range(NFC):
                        mm(P[b][:, :], actT[:, j, il * 128:(il + 1) * 128], Wd[:, j, cc * 512:(cc + 1) * 512], j == 0, j == NFC - 1, r=[("actT", il // 4), "Wd"], w=[PK[b]])
                    S.op("dve", lambda e, b=b, cc=cc: e.tensor_tensor(out=tmpA[:, cc * 512:(cc + 1) * 512], in0=P[b][:, :], in1=G2[:, cc * 512:(cc + 1) * 512], op=ALU.mult), r=[PK[b], "G2"], w=["tmpA"])
                    S.op("pool", lambda e, i=i, cc=cc: e.tensor_tensor(out=ot[i % 2][:, cc * 512:(cc + 1) * 512], in0=tmpA[:, cc * 512:(cc + 1) * 512], in1=xb[i % 2][:, cc * 512:(cc + 1) * 512], op=ALU.add),
                         r=["tmpA", ("xb", i % 2)], w=[("ot", i % 2)])
                S.dma("sp", out[ts, :], ot[i % 2], r=[("ot", i % 2)], w=[("out", i)])
        return finish(nc, S, st)


def finish(nc, S, st):
    for q in S.dsem:
        for i in range(len(S.dsem[q])):
            if S.dcnt[q][i]:
                S._wait("sp", (("d", q, i), S.dcnt[q][i]))
    for e2 in ("pe", "act", "dve", "pool"):
        if S.cnt[e2]:
            S._wait("sp", (e2, S.cnt[e2]))
    S.check_no_deadlock()
    st.close()
    return nc


def _consts():
    bf = ml_dtypes.bfloat16
    c = {}
    c["ident"] = np.eye(128, dtype=np.float32).astype(bf)
    a = np.arange(128)
    c["tri"] = (a[:, None] <= a[None, :]).astype(np.float32).astype(bf)
    w = np.zeros((128, 3, 128), np.float32)
    w[:, 0, :] = (a[:, None] > a[None, :])
    w[:, 1, :] = 1.0
    w[:, 2, :] = (a[:, None] <= a[None, :])
    c["winm"] = w.reshape(128, 384).astype(bf)
    inv16 = (np.float32(500000.0) ** (-np.arange(0, 32, 2, dtype=np.float32) / np.float32(32))).astype(np.float32)
    inv8 = (np.float32(500000.0) ** (-np.arange(0, 16, 2, dtype=np.float32) / np.float32(16))).astype(np.float32)
    c["inv16"] = np.tile(inv16[None], (128, 1)).astype(np.float32)
    c["inv8"] = np.tile(inv8[None], (128, 1)).astype(np.float32)
    n = np.arange(127)
    starts = n * 16
    j = np.arange(32)
    ovl = ((starts[:, None] < j[None, :] * 64 + 64) & (starts[:, None] + 32 > j[None, :] * 64))
    c["ovl"] = ovl.astype(np.float32).astype(bf)
    t = np.arange(S_)
    c["vmT"] = ((starts[:, None] + 31) <= t[None, :]).astype(np.float32).astype(bf)
    XE = np.zeros((32, NT, 128), np.float32)
    for kt in range(NT):
        XE[2 * kt, kt, 0:64] = 1.0
        XE[2 * kt + 1, kt, 64:128] = 1.0
    c["XE"] = XE.reshape(32, NT * 128).astype(bf)
    c["XEall"] = (np.arange(32)[:, None] == (np.arange(S_)[None, :] // 64)).astype(np.float32).astype(bf)
    cur = (t // 64)
    forced = (j[None, :] == 0) | (j[None, :] == cur[:, None]) | (j[None, :] == cur[:, None] - 1)
    valid = j[None, :] <= cur[:, None]
    fb = np.where(valid, np.where(forced, 1e4, 0.0), -1e30).astype(np.float32)
    c["fb"] = fb.reshape(NT, 128, 32).transpose(1, 0, 2).reshape(128, NT * 32).copy()
    c["vj"] = valid.astype(np.float32).reshape(NT, 128, 32).transpose(1, 0, 2).reshape(128, NT * 32).copy()
    return c


def _rep(v, n=128):
    return np.ascontiguousarray(np.broadcast_to(np.asarray(v, np.float32)[None, :], (n, v.shape[0])))


def prep_inputs(inp):
    f = lambda a: np.ascontiguousarray(np.asarray(a, dtype=np.float32))
    w_in = f(inp["w_in"][0])
    o = np.cumsum([0, 768, 256, 32, 512, 128, 128, 128, 128, 128, 128, 24, 1024, 1024])
    seg = lambda i: w_in[:, o[i]:o[i + 1]]
    shared = {}
    shared["ada_w"] = f(inp["ada_w"][0]); shared["adabB"] = _rep(f(inp["ada_b"][0]))
    shared["g1B"] = _rep(f(inp["norm1_gain"][0])); shared["g2B"] = _rep(f(inp["norm2_gain"][0]))
    shared["w_cq"] = f(seg(0)); shared["w_ckv"] = f(seg(1)); shared["w_kpe"] = f(seg(2))
    qn = seg(3).reshape(D, 8, 64)
    shared["w_qn"] = f(qn[:, PH, :].reshape(D, 512))
    kc = seg(4).reshape(D, 2, 64); vc = seg(5).reshape(D, 2, 64)
    shared["w_kc2"] = f(np.stack([kc[:, 0], kc[:, 0], kc[:, 1], kc[:, 1]], 1).reshape(D, 256))
    shared["w_vc2"] = f(np.stack([vc[:, 0], vc[:, 0], vc[:, 1], vc[:, 1]], 1).reshape(D, 256))
    shared["w_kv4"] = f(np.concatenate([seg(6), seg(8), seg(7), seg(9)], 1))
    gn = seg(10).reshape(D, 8, 3)
    shared["w_gn"] = f(gn[:, PH, :].transpose(0, 2, 1).reshape(D, 24))
    shared["w_gm"] = f(seg(11)); shared["w_gnm"] = f(seg(12))
    shared["qag"] = f(f(inp["mla_q_a_gain"][0]).reshape(6, 128).T); shared["kvag"] = f(f(inp["mla_kv_a_gain"][0]).reshape(2, 128).T)
    shared["w_qb"] = f(inp["mla_w_q_b"][0]); shared["w_kvb"] = f(inp["mla_w_kv_b"][0])
    shared["qgB"] = _rep(f(inp["mla_q_gain"][0])); shared["kgB"] = _rep(f(inp["mla_k_gain"][0]))
    shared["nqB"] = _rep(f(inp["nsa_q_gain"][0])); shared["nkcB"] = _rep(f(inp["nsa_kc_gain"][0]))
    shared["nksB"] = _rep(f(inp["nsa_ks_gain"][0])); shared["nkwB"] = _rep(f(inp["nsa_kw_gain"][0]))
    shared["posk"] = f(f(inp["cmp_pos_k"][0]).reshape(16, 128).T); shared["posv"] = f(f(inp["cmp_pos_v"][0]).reshape(16, 128).T)
    shared["w1k"] = f(inp["cmp_w1_k"][0]); shared["w2k"] = f(inp["cmp_w2_k"][0])
    shared["w1v"] = f(inp["cmp_w1_v"][0]); shared["w2v"] = f(inp["cmp_w2_v"][0])
    shared["wo_mla"] = f(inp["w_o_mla"][0])
    shared["wo_nsa"] = f(f(inp["w_o_nsa"][0]).reshape(8, 64, D)[PH].reshape(512, D))
    shared["w_out"] = f(inp["w_out"][0])
    shared["wg"] = f(inp["ffn_w_gate"][0]); shared["wu"] = f(inp["ffn_w_up"][0]); shared["wd"] = f(inp["ffn_w_down"][0])
    shared.update(_consts())
    maps = []
    xs = np.asarray(inp["x"], np.float32); cs = np.asarray(inp["c"], np.float32); ps = np.asarray(inp["positions"]).astype(np.int32)
    for b in range(xs.shape[0]):
        m = dict(shared)
        m["x"] = np.ascontiguousarray(xs[b])
        m["c_pk"] = np.ascontiguousarray(cs[b].reshape(8, 128).T)
        m["pos_pk"] = np.ascontiguousarray(ps[b].reshape(NT, 128).T)
        m["posC"] = np.ascontiguousarray(ps[b][31::16][:127].reshape(127, 1))
        maps.append(m)
    return maps


_NC_CACHE = {}


def kernel(**inputs):
    maps = prep_inputs(inputs)
    if "nc" not in _NC_CACHE:
        _NC_CACHE["nc"] = build()
    nc = _NC_CACHE["nc"]
    res = run_bass_kernel_spmd(nc, maps, core_ids=list(range(len(maps))))
    return np.stack([np.asarray(r["out"], dtype=np.float32) for r in res.results], 0)
```

```python
import contextlib
import numpy as np
import ml_dtypes
import concourse.bass as bass
import concourse.mybir as mybir
from concourse.bass_utils import run_bass_kernel_spmd

F32 = mybir.dt.float32
BF16 = mybir.dt.bfloat16
I32 = mybir.dt.int32
ALU = mybir.AluOpType
AF = mybir.ActivationFunctionType
AX = mybir.AxisListType

S_ = 2048
D = 1024
NT = 16
DFF = 2816
NFC = 22
EPS = 1e-6
PH = [0, 4, 1, 5, 2, 6, 3, 7]
TWO_PI = float(2 * np.pi)
PI = float(np.pi)


class Sched:
    N_DMA_SLOTS = {"sp": 24, "pool": 8, "act": 4}

    def __init__(self, nc, stack):
        self.nc = nc
        self.E = {"pe": nc.tensor, "act": nc.scalar, "dve": nc.vector, "pool": nc.gpsimd, "sp": nc.sync}
        self.sem, self.cnt = {}, {}
        for e in ("pe", "act", "dve", "pool"):
            self.sem[e] = stack.enter_context(nc.semaphore("s_" + e))
            self.cnt[e] = 0
        self.dsem, self.dcnt, self.dnext = {}, {}, {}
        for q, n in self.N_DMA_SLOTS.items():
            self.dsem[q] = [stack.enter_context(nc.semaphore(f"d_{q}{i}")) for i in range(n)]
            self.dcnt[q] = [0] * n
            self.dnext[q] = 0
        self.seen = {e: {} for e in self.E}
        self.lastw, self.readers = {}, {}
        self.n_wait = 0
        self.n_inst = 0
        self.prog = {}
        self.ninst = {}

    def check_no_deadlock(self):
        val = {}
        ptr = {e: 0 for e in self.prog}
        progress = True
        while progress:
            progress = False
            for e, lst in self.prog.items():
                while ptr[e] < len(lst):
                    it = lst[ptr[e]]
                    if it[0] == "w":
                        if val.get(it[1], 0) < it[2]:
                            break
                    else:
                        val[it[1]] = val.get(it[1], 0) + it[2]
                    ptr[e] += 1
                    progress = True
        stuck = {e: (ptr[e], len(l), l[ptr[e]]) for e, l in self.prog.items() if ptr[e] < len(l)}
        assert not stuck, f"DEADLOCK in emitted program: {stuck}"

    def _sem_of(self, src):
        return self.dsem[src[1]][src[2]] if isinstance(src, tuple) else self.sem[src]

    def _wait(self, e, tok):
        src, val = tok
        if self.seen[e].get(src, 0) >= val:
            return
        self.E[e].wait_ge(self._sem_of(src), val)
        self.seen[e][src] = val
        self.n_wait += 1
        self.prog.setdefault(e, []).append(("w", src, val))

    def _deps(self, e, r, w):
        toks = []
        for k in r:
            t = self.lastw.get(k)
            if t is not None:
                toks.append(t)
        for k in w:
            t = self.lastw.get(k)
            if t is not None:
                toks.append(t)
            for t in self.readers.get(k, ()):
                toks.append(t)
        for t in toks:
            if t[0] == e and e == "pe":
                continue
            self._wait(e, t)

    def _commit(self, tok, r, w):
        for k in r:
            lst = self.readers.setdefault(k, [])
            lst[:] = [t for t in lst if t[0] != tok[0]]
            lst.append(tok)
        for k in w:
            self.lastw[k] = tok
            self.readers[k] = []

    def op(self, e, fn, r=(), w=(), signal=True):
        self._deps(e, r, w)
        ins = fn(self.E[e])
        if signal:
            self.cnt[e] += 1
            ins.then_inc(self.sem[e], 1)
            tok = (e, self.cnt[e])
            self.prog.setdefault(e, []).append(("i", e, 1))
        else:
            tok = (e, self.cnt[e] + 1)
        self._commit(tok, r, w)
        self.n_inst += 1
        self.ninst[e] = self.ninst.get(e, 0) + 1
        return tok

    def dma(self, q, out, in_, r=(), w=(), **kw):
        slot = self.dnext[q]
        self.dnext[q] = (slot + 1) % len(self.dsem[q])
        src = ("d", q, slot)
        if self.dcnt[q][slot] > 0:
            self._wait(q, (src, self.dcnt[q][slot]))
        self._deps(q, r, w)
        ins = self.E[q].dma_start(out=out, in_=in_, **kw)
        self.dcnt[q][slot] += 16
        ins.then_inc(self.dsem[q][slot], 16)
        self.prog.setdefault(q, []).append(("i", src, 16))
        tok = (src, self.dcnt[q][slot])
        self._commit(tok, r, w)
        self.n_inst += 1
        return tok

    def barrier(self):
        for e in ("pe", "act", "dve", "pool", "sp"):
            for e2 in ("pe", "act", "dve", "pool"):
                if self.cnt[e2] and not (e2 == e == "pe"):
                    self._wait(e, (e2, self.cnt[e2]))
            for q in self.dsem:
                for i in range(len(self.dsem[q])):
                    if self.dcnt[q][i]:
                        self._wait(e, (("d", q, i), self.dcnt[q][i]))
        self.lastw.clear()
        self.readers.clear()


class Mem:
    def __init__(self, big, nbytes):
        self.big, self.lo, self.hi, self.n = big, 0, nbytes, nbytes

    def _view(self, off, shape, dt):
        nel = int(np.prod(shape))
        esz = 4 if dt in (F32, I32) else 2
        nb = nel * esz
        ap = self.big[:, off // 2:(off + nb) // 2]
        if esz == 4:
            ap = ap.bitcast(dt)
        if len(shape) == 2:
            ap = ap.rearrange("p (a b) -> p a b", a=shape[0])
        elif len(shape) == 3:
            ap = ap.rearrange("p (a b c) -> p a b c", a=shape[0], b=shape[1])
        return ap

    def lo_alloc(self, shape, dt):
        nb = int(np.prod(shape)) * (4 if dt in (F32, I32) else 2)
        nb = (nb + 63) // 64 * 64
        off = self.lo
        self.lo += nb
        assert self.lo <= self.hi, f"SBUF overflow lo={self.lo} hi={self.hi}"
        return self._view(off, shape, dt)

    def hi_alloc(self, shape, dt):
        nb = int(np.prod(shape)) * (4 if dt in (F32, I32) else 2)
        nb = (nb + 63) // 64 * 64
        self.hi -= nb
        assert self.lo <= self.hi, f"SBUF overflow lo={self.lo} hi={self.hi}"
        return self._view(self.hi, shape, dt)


def build(upto=99, dbg=()):
    nc = bass.Bass("TRN2", target_bir_lowering=False)
    I = {}

    def din(name, shape, dt=F32):
        I[name] = nc.dram_tensor(name, list(shape), dt, kind="ExternalInput").ap()
        return I[name]

    x = din("x", [S_, D]); c_pk = din("c_pk", [128, 8]); pos_pk = din("pos_pk", [128, NT], I32)
    posC = din("posC", [127, 1], I32)
    ada_w = din("ada_w", [D, 6 * D]); adabB = din("adabB", [128, 6 * D]); g1B = din("g1B", [128, D]); g2B = din("g2B", [128, D])
    w_cq = din("w_cq", [D, 768]); w_ckv = din("w_ckv", [D, 256]); w_kpe = din("w_kpe", [D, 32])
    w_qn = din("w_qn", [D, 512]); w_kc2 = din("w_kc2", [D, 256]); w_vc2 = din("w_vc2", [D, 256])
    w_kv4 = din("w_kv4", [D, 512]); w_gn = din("w_gn", [D, 24]); w_gm = din("w_gm", [D, D]); w_gnm = din("w_gnm", [D, D])
    qag = din("qag", [128, 6]); kvag = din("kvag", [128, 2])
    w_qb = din("w_qb", [768, 768]); w_kvb = din("w_kvb", [256, 1024])
    qgB = din("qgB", [128, 96]); kgB = din("kgB", [128, 96])
    nqB = din("nqB", [128, 64]); nkcB = din("nkcB", [128, 64]); nksB = din("nksB", [128, 64]); nkwB = din("nkwB", [128, 64])
    posk = din("posk", [128, 16]); w1k = din("w1k", [2048, 256]); w2k = din("w2k", [256, 64])
    posv = din("posv", [128, 16]); w1v = din("w1v", [2048, 256]); w2v = din("w2v", [256, 64])
    wo_mla = din("wo_mla", [512, D]); wo_nsa = din("wo_nsa", [512, D]); w_out = din("w_out", [D, D])
    wg = din("wg", [D, DFF]); wu = din("wu", [D, DFF]); wd = din("wd", [DFF, D])
    ident_d = din("ident", [128, 128], BF16); tri_d = din("tri", [128, 128], BF16); winm_d = din("winm", [128, 384], BF16)
    inv16_d = din("inv16", [128, 16]); inv8_d = din("inv8", [128, 8])
    XEall_d = din("XEall", [32, S_], BF16); ovl_d = din("ovl", [127, 32], BF16); vmT_d = din("vmT", [127, S_], BF16); XE_d = din("XE", [32, NT * 128], BF16)
    fb_d = din("fb", [128, NT * 32]); vj_d = din("vj", [128, NT * 32])
    out = nc.dram_tensor("out", [S_, D], F32, kind="ExternalOutput").ap()
    hTs = nc.dram_tensor("hTs", [128, 8 * S_], BF16).ap()
    mods = nc.dram_tensor("mods", [128, 6 * D], F32).ap()
    x1s = nc.dram_tensor("x1s", [S_, D], F32).ap()
    D_ = {}
    for name, shape, dt in dbg:
        D_[name] = nc.dram_tensor("dbg_" + name, list(shape), dt, kind="ExternalOutput").ap()

    st = contextlib.ExitStack()
    with st:
        S = Sched(nc, st)
        NB = 204800
        big = st.enter_context(nc.sbuf_tensor("big", [128, NB // 2], BF16))
        M = Mem(big, NB)
        P = [st.enter_context(nc.psum_tensor(f"ps{i}", [128, 512], F32)) for i in range(8)]
        PK = [f"ps{i}" for i in range(8)]
        bank_state = [0]

        nbanks = [8]

        def nb():
            b = bank_state[0] % nbanks[0]
            bank_state[0] = (b + 1) % nbanks[0]
            return b

        def mm(ps_ap, lhsT, rhs, start, stop, r, w, sig=None, **kw):
            S.op("pe", lambda e: e.matmul(ps_ap, lhsT=lhsT, rhs=rhs, start=start, stop=stop, **kw), r=r, w=w, signal=bool(stop) if sig is None else sig)

        def tp(ps_ap, in_, ident_ap, r, w):
            S.op("pe", lambda e: e.transpose(out=ps_ap, in_=in_, identity=ident_ap), r=r, w=w)

        def act(out_, in_, func, r, w, **kw):
            S.op("act", lambda e: e.activation(out=out_, in_=in_, func=func, **kw), r=r, w=w)

        def dbg_out(name, ap, r):
            if name in D_:
                S.dma("sp", D_[name], ap, r=r)

        ident = M.lo_alloc([128], BF16); tri = M.lo_alloc([128], BF16); winm = M.lo_alloc([3, 128], BF16)
        onesb = M.lo_alloc([128], BF16)
        stg = [M.lo_alloc([1024], F32) for _ in range(3)]
        stg_i = [0]
        cosM = M.lo_alloc([NT, 16], F32); sinM = M.lo_alloc([NT, 16], F32)
        cosN = M.lo_alloc([NT, 8], F32); sinN = M.lo_alloc([NT, 8], F32)
        cosC = M.lo_alloc([8], F32); sinC = M.lo_alloc([8], F32)
        lo_pers = M.lo
        omT = M.lo_alloc([4, S_], BF16); onT = M.lo_alloc([4, S_], BF16)
        S.dma("sp", ident, ident_d, w=["ident"])
        S.dma("sp", tri, tri_d, w=["tri"])
        S.dma("sp", winm.rearrange("p a b -> p (a b)"), winm_d, w=["winm"])
        S.op("pool", lambda e: e.memset(onesb, 1.0), w=["onesb"])

        def load_w(dst, W, key, KC, N, ceng="dve"):
            Wv = W.rearrange("(k p) n -> p k n", p=128)
            if N <= 1024:
                g = max(1, min(KC, 1024 // N))
                for k0 in range(0, KC, g):
                    k1 = min(KC, k0 + g)
                    si = stg_i[0]; stg_i[0] = (si + 1) % 3
                    sv = stg[si][:, 0:(k1 - k0) * N].rearrange("p (k n) -> p k n", n=N)
                    S.dma("sp", sv, Wv[:, k0:k1, :], w=[("stg", si)])
                    S.op(ceng, lambda e, sv=sv, k0=k0, k1=k1: e.tensor_copy(out=dst[:, k0:k1, :], in_=sv), r=[("stg", si)], w=[key])
            else:
                for k in range(KC):
                    for c0 in range(0, N, 1024):
                        c1 = min(N, c0 + 1024)
                        si = stg_i[0]; stg_i[0] = (si + 1) % 3
                        sv = stg[si][:, 0:c1 - c0]
                        S.dma("sp", sv, Wv[:, k, c0:c1], w=[("stg", si)])
                        S.op(ceng, lambda e, sv=sv, k=k, c0=c0, c1=c1: e.tensor_copy(out=dst[:, k, c0:c1], in_=sv), r=[("stg", si)], w=[key])

        def load_w_cols(dst, W, key, KC, c0, c1, ceng="dve"):
            Wv = W.rearrange("(k p) n -> p k n", p=128)
            N = c1 - c0
            g = max(1, min(KC, 1024 // N))
            for k0 in range(0, KC, g):
                k1 = min(KC, k0 + g)
                si = stg_i[0]; stg_i[0] = (si + 1) % 3
                sv = stg[si][:, 0:(k1 - k0) * N].rearrange("p (k n) -> p k n", n=N)
                S.dma("sp", sv, Wv[:, k0:k1, c0:c1], w=[("stg", si)])
                S.op(ceng, lambda e, sv=sv, k0=k0, k1=k1: e.tensor_copy(out=dst[:, k0:k1, :], in_=sv), r=[("stg", si)], w=[key])

        def sincos(ang, shape, cos_o, sin_o, np_, tmp_f, tmp_i, tmp_m, key):
            for (shift, dst) in ((0.0, sin_o), (PI / 2, cos_o)):
                S.op("dve", lambda e: e.tensor_scalar(out=tmp_f, in0=ang, scalar1=shift, scalar2=None, op0=ALU.add), r=[key + "ang"], w=[key + "f"])
                S.op("dve", lambda e: e.tensor_scalar(out=tmp_i, in0=tmp_f, scalar1=float(1 / TWO_PI), scalar2=None, op0=ALU.mult), r=[key + "f"], w=[key + "i"])
                S.op("dve", lambda e: e.tensor_copy(out=tmp_m, in_=tmp_i), r=[key + "i"], w=[key + "m"])
                S.op("dve", lambda e: e.scalar_tensor_tensor(out=tmp_f, in0=tmp_m, scalar=-TWO_PI, in1=tmp_f, op0=ALU.mult, op1=ALU.add), r=[key + "m", key + "f"], w=[key + "f"])
                S.op("dve", lambda e: e.tensor_scalar(out=tmp_m, in0=tmp_f, scalar1=PI, scalar2=None, op0=ALU.is_gt), r=[key + "f"], w=[key + "m"])
                S.op("dve", lambda e: e.scalar_tensor_tensor(out=tmp_f, in0=tmp_m, scalar=-TWO_PI, in1=tmp_f, op0=ALU.mult, op1=ALU.add), r=[key + "m", key + "f"], w=[key + "f"])
                S.op("dve", lambda e: e.tensor_scalar(out=tmp_m, in0=tmp_f, scalar1=-PI, scalar2=None, op0=ALU.is_lt), r=[key + "f"], w=[key + "m"])
                S.op("dve", lambda e: e.scalar_tensor_tensor(out=tmp_f, in0=tmp_m, scalar=TWO_PI, in1=tmp_f, op0=ALU.mult, op1=ALU.add), r=[key + "m", key + "f"], w=[key + "f"])
                act(dst, tmp_f, AF.Sin, r=[key + "f"], w=[key + "out"])

        lo0, hi0 = M.lo, M.hi
        if upto <= -2:
            return finish(nc, S, st)
        posi = M.lo_alloc([NT], I32); posf = M.lo_alloc([NT], F32)
        posCi = M.lo_alloc([1], I32); posCf = M.lo_alloc([1], F32)
        inv16 = M.lo_alloc([16], F32); inv8 = M.lo_alloc([8], F32)
        angM = M.lo_alloc([NT, 16], F32); tfM = M.lo_alloc([NT, 16], F32); tiM = M.lo_alloc([NT, 16], I32); tmM = M.lo_alloc([NT, 16], F32)
        S.dma("sp", posi, pos_pk, w=["posi"])
        S.dma("sp", posCi[0:127], posC, w=["posCi"])
        S.dma("sp", inv16, inv16_d, w=["inv16"])
        S.dma("sp", inv8, inv8_d, w=["inv8"])
        S.op("dve", lambda e: e.tensor_copy(out=posf, in_=posi), r=["posi"], w=["posf"])
        S.op("dve", lambda e: e.tensor_copy(out=posCf[0:127], in_=posCi[0:127]), r=["posCi"], w=["posCf"])
        S.op("dve", lambda e: e.tensor_tensor(out=angM, in0=posf.unsqueeze(2).to_broadcast([128, NT, 16]),
                                              in1=inv16.unsqueeze(1).to_broadcast([128, NT, 16]), op=ALU.mult), r=["posf", "inv16"], w=["Mang"])
        sincos(angM, None, cosM, sinM, 128, tfM, tiM, tmM, "M")
        a8 = angM.rearrange("p a b -> p (a b)")[:, 0:NT * 8].rearrange("p (a b) -> p a b", b=8)
        f8 = tfM.rearrange("p a b -> p (a b)")[:, 0:NT * 8].rearrange("p (a b) -> p a b", b=8)
        i8 = tiM.rearrange("p a b -> p (a b)")[:, 0:NT * 8].rearrange("p (a b) -> p a b", b=8)
        m8_ = tmM.rearrange("p a b -> p (a b)")[:, 0:NT * 8].rearrange("p (a b) -> p a b", b=8)
        S.op("dve", lambda e: e.tensor_tensor(out=a8, in0=posf.unsqueeze(2).to_broadcast([128, NT, 8]),
                                              in1=inv8.unsqueeze(1).to_broadcast([128, NT, 8]), op=ALU.mult), r=["posf", "inv8", "Mout", "Mf", "Mm", "Mi"], w=["Nang"])
        sincos(a8, None, cosN, sinN, 128, f8, i8, m8_, "N")
        aC = angM.rearrange("p a b -> p (a b)")[0:127, 0:8]
        fC = tfM.rearrange("p a b -> p (a b)")[0:127, 0:8]
        iC = tiM.rearrange("p a b -> p (a b)")[0:127, 0:8]
        mC = tmM.rearrange("p a b -> p (a b)")[0:127, 0:8]
        S.op("dve", lambda e: e.tensor_scalar(out=aC, in0=inv8[0:127], scalar1=posCf[0:127, 0:1], scalar2=None, op0=ALU.mult),
             r=["posCf", "inv8", "Nout", "Nf", "Nm", "Ni", "Nang"], w=["Cang"])
        sincos(aC, None, cosC[0:127], sinC[0:127], 127, fC, iC, mC, "C")
        dbg_out("cosM", cosM, ["Mout"]); dbg_out("sinM", sinM, ["Mout"])

        if upto <= -1:
            return finish(nc, S, st)
        cpk = M.lo_alloc([8], F32); sc = M.lo_alloc([8], F32)
        sch = M.lo_alloc([8], BF16); scl = M.lo_alloc([8], BF16)
        cBh = M.lo_alloc([8, 128], BF16); cBl = M.lo_alloc([8, 128], BF16)
        modB = M.lo_alloc([6 * D], F32)
        g1t = M.lo_alloc([D], F32); g2t = M.lo_alloc([D], F32)
        awb = [M.hi_alloc([8, 512], F32) for _ in range(3)]
        abb = [M.hi_alloc([512], F32) for _ in range(3)]
        awh = [M.hi_alloc([8, 512], BF16) for _ in range(3)]
        awl = [M.hi_alloc([8, 512], BF16) for _ in range(3)]
        S.dma("sp", cpk, c_pk, w=["cpk"])
        S.dma("sp", g1t, g1B, w=["g1t"]); S.dma("sp", g2t, g2B, w=["g2t"])
        act(sc, cpk, AF.Silu, r=["cpk"], w=["sc"])
        S.op("dve", lambda e: e.tensor_copy(out=sch, in_=sc), r=["sc"], w=["sch"])
        S.op("dve", lambda e: e.tensor_tensor(out=scl, in0=sc, in1=sch, op=ALU.subtract), r=["sc", "sch"], w=["scl"])
        for k in range(8):
            S.op("dve", lambda e, k=k: e.tensor_copy(out=cBh[:, k, :], in_=sch[:, k:k + 1].to_broadcast([128, 128])), r=["sch"], w=["cBh"])
            S.op("dve", lambda e, k=k: e.tensor_copy(out=cBl[:, k, :], in_=scl[:, k:k + 1].to_broadcast([128, 128])), r=["scl"], w=["cBl"])
        awv = ada_w.rearrange("(k p) n -> p k n", p=128)
        for n in range(12):
            q_ = n % 3
            S.dma("sp", awb[q_], awv[:, :, n * 512:(n + 1) * 512], w=[("awb", q_)])
            S.dma("sp", abb[q_], adabB[:, n * 512:(n + 1) * 512], w=[("abb", q_)])
            act(awh[q_], awb[q_], AF.Copy, r=[("awb", q_)], w=[("awh", q_)])
            S.op("dve", lambda e, q_=q_: e.tensor_tensor(out=awl[q_], in0=awb[q_], in1=awh[q_], op=ALU.subtract), r=[("awb", q_), ("awh", q_)], w=[("awl", q_)])
            b = nb()
            passes = [(cBh, "cBh", awh, "awh"), (cBh, "cBh", awl, "awl"), (cBl, "cBl", awh, "awh")]
            for pi_, (cb_, ck, ww, wk) in enumerate(passes):
                for k in range(8):
                    mm(P[b][:, :], cb_[:, k, :], ww[q_][:, k, :], pi_ == 0 and k == 0, pi_ == 2 and k == 7, r=[ck, (wk, q_)], w=[PK[b]])
            S.op("dve", lambda e, n=n, b=b, q_=q_: e.tensor_tensor(out=modB[:, n * 512:(n + 1) * 512], in0=P[b][:, :], in1=abb[q_], op=ALU.add),
                 r=[PK[b], ("abb", q_)], w=["modB"])
        S.op("dve", lambda e: e.scalar_tensor_tensor(out=modB[:, D:2 * D], in0=modB[:, D:2 * D], scalar=1.0, in1=g1t, op0=ALU.add, op1=ALU.mult), r=["modB", "g1t"], w=["modB"])
        S.op("dve", lambda e: e.scalar_tensor_tensor(out=modB[:, 4 * D:5 * D], in0=modB[:, 4 * D:5 * D], scalar=1.0, in1=g2t, op0=ALU.add, op1=ALU.mult), r=["modB", "g2t"], w=["modB"])
        S.dma("sp", mods, modB, r=["modB"], w=["mods"])
        dbg_out("modB", modB, ["modB"])
        B1 = modB[:, 0:D]; A1 = modB[:, D:2 * D]
        if upto <= 0:
            return finish(nc, S, st)

        M.hi = hi0
        hT = M.hi_alloc([8, S_], BF16)
        xb = [M.lo_alloc([D], F32) for _ in range(2)]
        junk = M.lo_alloc([D], F32); tmpA = M.lo_alloc([D], F32)
        hb = [M.lo_alloc([D], BF16) for _ in range(2)]
        ssq = M.lo_alloc([NT], F32); rs = M.lo_alloc([NT], F32)

        tmpA2 = [tmpA, M.lo_alloc([D], F32)]

        def norm_a(i, xt, xkey):
            act(junk, xt, AF.Square, r=[xkey], w=["junk", ("ssq", i)], accum_out=ssq[:, i:i + 1])
            act(rs[:, i:i + 1], ssq[:, i:i + 1], AF.Sqrt, r=[("ssq", i)], w=[("rs", i)], scale=1.0 / D, bias=EPS)
            S.op("dve", lambda e: e.reciprocal(out=rs[:, i:i + 1], in_=rs[:, i:i + 1]), r=[("rs", i)], w=[("rs", i)])

        def norm_b1(i, xt, xkey, A, B, Akeys):
            par = i % 2
            S.op("dve", lambda e: e.scalar_tensor_tensor(out=tmpA2[par], in0=xt, scalar=rs[:, i:i + 1], in1=A, op0=ALU.mult, op1=ALU.mult),
                 r=[xkey, ("rs", i)] + Akeys, w=[("tmpA2", par)])
            S.op("pool", lambda e: e.tensor_tensor(out=hb[par], in0=tmpA2[par], in1=B, op=ALU.add), r=[("tmpA2", par)] + Akeys, w=[("hb", par)])
            b = nb()
            pb = P[b][:, :].bitcast(BF16)
            for k in range(8):
                tp(pb[:, k * 128:(k + 1) * 128], hb[par][:, k * 128:(k + 1) * 128], ident, r=[("hb", par), "ident"], w=[PK[b]])
            return b

        def norm_b2(i, b, dstT, dkey):
            pb = P[b][:, :].bitcast(BF16)
            act(dstT[:, :, i * 128:(i + 1) * 128], pb.rearrange("p (k t) -> p k t", k=8), AF.Copy, r=[PK[b]], w=[(dkey, i)])

        xb = xb + [M.lo_alloc([D], F32), M.lo_alloc([D], F32)]

        def a_front(i):
            S.dma("sp", xb[i % 4], x[i * 128:(i + 1) * 128, :], w=[("xb", i % 4)])
            norm_a(i, xb[i % 4], ("xb", i % 4))

        a_front(0); a_front(1)
        for i in range(NT):
            b_ = norm_b1(i, xb[i % 4], ("xb", i % 4), A1, B1, ["modB"])
            if i + 2 < NT:
                a_front(i + 2)
            norm_b2(i, b_, hT, "hT")
        hTk = [("hT", i) for i in range(NT)]
        S.dma("sp", hTs, hT.rearrange("p k t -> p (k t)"), r=hTk, w=["hTs"])
        dbg_out("hT", hT.rearrange("p k t -> p (k t)"), hTk)
        if upto <= 1:
            return finish(nc, S, st)
        S.barrier()

        nbanks[0] = 6
        M.lo = lo0
        cqT = M.lo_alloc([6, S_], BF16); ckvT = M.lo_alloc([2, S_], BF16); kpe = M.lo_alloc([NT, 32], F32)
        lo1 = M.lo
        Wcq = M.lo_alloc([8, 768], BF16); Wckv = M.lo_alloc([8, 256], BF16); Wkpe = M.lo_alloc([8, 32], BF16)
        qagt = M.lo_alloc([6], F32); kvagt = M.lo_alloc([2], F32)
        sqb = [M.lo_alloc([512], BF16) for _ in range(2)]
        rb = M.lo_alloc([512], F32)
        S.dma("sp", qagt, qag, w=["qagt"]); S.dma("sp", kvagt, kvag, w=["kvagt"])
        load_w(Wcq, w_cq, "Wcq", 8, 768); load_w(Wckv, w_ckv, "Wckv", 8, 256); load_w(Wkpe, w_kpe, "Wkpe", 8, 32)

        def fm_proj_norm(dstT, dkey, Wt, wkey, nf, gaint, gkey, nfeat):
            for c in range(4):
                hk = [("hT", 4 * c + q) for q in range(4)]
                for j in range(nf + 1):
                    if j < nf:
                        b = nb()
                        for k in range(8):
                            mm(P[b][:, :], Wt[:, k, j * 128:(j + 1) * 128], hT[:, k, c * 512:(c + 1) * 512], k == 0, k == 7, r=[wkey] + hk, w=[PK[b]])
                    if j >= 1:
                        jj = j - 1
                        mm(P[6][:, :], onesb, sqb[jj % 2], jj == 0, jj == nf - 1, r=["onesb", ("sqb", jj % 2)], w=[PK[6]], sig=True)
                    if j < nf:
                        act(sqb[j % 2], P[b][:, :], AF.Square, r=[PK[b]], w=[("sqb", j % 2)])
                        act(dstT[:, j, c * 512:(c + 1) * 512], P[b][:, :], AF.Copy, r=[PK[b]], w=[(dkey, c)])
                act(rb, P[6][:, :], AF.Sqrt, r=[PK[6]], w=["rb"], scale=1.0 / nfeat, bias=EPS)
                S.op("dve", lambda e: e.reciprocal(out=rb, in_=rb), r=["rb"], w=["rb"])
                for j in range(nf):
                    S.op("dve", lambda e, j=j, c=c: e.scalar_tensor_tensor(out=dstT[:, j, c * 512:(c + 1) * 512], in0=dstT[:, j, c * 512:(c + 1) * 512],
                                                                             scalar=gaint[:, j:j + 1], in1=rb, op0=ALU.mult, op1=ALU.mult),
                         r=[(dkey, c), "rb", gkey], w=[(dkey, c)])

        fm_proj_norm(cqT, "cqT", Wcq, "Wcq", 6, qagt, "qagt", 768)
        fm_proj_norm(ckvT, "ckvT", Wckv, "Wckv", 2, kvagt, "kvagt", 256)
        for i in range(NT):
            b = nb()
            for k in range(8):
                mm(P[b][:, 0:32], hT[:, k, i * 128:(i + 1) * 128], Wkpe[:, k, :], k == 0, k == 7, r=["Wkpe", ("hT", i)], w=[PK[b]])
            S.op("dve", lambda e, i=i, b=b: e.tensor_copy(out=kpe[:, i, :], in_=P[b][:, 0:32]), r=[PK[b]], w=[("kpe", i)])
        cqk = [("cqT", c) for c in range(4)]
        dbg_out("cqT", cqT.rearrange("p k t -> p (k t)"), cqk)
        dbg_out("kpe", kpe.rearrange("p a b -> p (a b)"), [("kpe", i) for i in range(NT)])
        if upto <= 2:
            return finish(nc, S, st)
        S.barrier()

        nbanks[0] = 8
        M.lo = lo1
        M.hi = hi0
        QT = M.hi_alloc([8, S_], BF16); KT = M.hi_alloc([8, S_], BF16); V = M.hi_alloc([NT, 8, 65], BF16)
        hi1 = M.hi
        Wqb = M.lo_alloc([6, 768], BF16); Wkvb = M.lo_alloc([2, 1024], BF16)
        qgt = M.lo_alloc([96], F32); kgt = M.lo_alloc([96], F32)
        drq = [M.lo_alloc([768], BF16) for _ in range(2)]; drk = [M.lo_alloc([768], BF16) for _ in range(2)]
        S.dma("sp", qgt, qgB, w=["qgt"]); S.dma("sp", kgt, kgB, w=["kgt"])
        load_w(Wqb, w_qb, "Wqb", 6, 768); load_w(Wkvb, w_kvb, "Wkvb", 2, 1024)
        S.op("pool", lambda e: e.memset(V[:, :, :, 64:65], 1.0), w=["Vones"])

        def mk_tmps(Mx, n, H, hf):
            return dict(t1=Mx.lo_alloc([n], F32), hs=Mx.lo_alloc([H], F32), hr=Mx.lo_alloc([H], F32),
                        ra=Mx.lo_alloc([H * hf], F32), rb=Mx.lo_alloc([H * hf], F32), ra2=Mx.lo_alloc([H * hf], F32), rb2=Mx.lo_alloc([H * hf], F32))

        def hnr_stages(tag, T, src, skeys, H, Dh, gaint, gkey, ro, hf, cos_, sin_, dst, dkey, np_=128):
            n = H * Dh
            t1v = T["t1"][0:np_, 0:n].rearrange("p (h d) -> p h d", h=H)
            hs = T["hs"][0:np_, 0:H]; hr = T["hr"][0:np_, 0:H]
            x1 = t1v[:, :, ro:ro + hf]; x2 = t1v[:, :, ro + hf:ro + 2 * hf]
            cb = cos_.unsqueeze(1).to_broadcast([np_, H, hf]); sb_ = sin_.unsqueeze(1).to_broadcast([np_, H, hf])
            rv = {k: T[k][0:np_, 0:H * hf].rearrange("p (h d) -> p h d", h=H) for k in ("ra", "rb", "ra2", "rb2")}
            tr = ["Mout", "Nout", "Cout"]
            k_ = lambda nm: (tag, nm)
            st = []
            st.append(lambda: act(t1v, src, AF.Square, r=skeys, w=[k_("t1")]))
            st.append(lambda: S.op("dve", lambda e: e.tensor_reduce(out=hs, in_=t1v, axis=AX.X, op=ALU.add), r=[k_("t1")], w=[k_("hs")]))
            st.append(lambda: act(hr, hs, AF.Sqrt, r=[k_("hs")], w=[k_("hr")], scale=1.0 / Dh, bias=EPS))
            st.append(lambda: S.op("dve", lambda e: e.reciprocal(out=hr, in_=hr), r=[k_("hr")], w=[k_("hr")]))
            st.append(lambda: S.op("dve", lambda e: e.tensor_tensor(out=t1v, in0=src, in1=hr.unsqueeze(2).to_broadcast([np_, H, Dh]), op=ALU.mult), r=skeys + [k_("hr"), k_("hs")], w=[k_("t1")]))
            st.append(lambda: S.op("dve", lambda e: e.tensor_tensor(out=t1v, in0=t1v, in1=gaint[0:np_].unsqueeze(1).to_broadcast([np_, H, Dh]), op=ALU.mult), r=[k_("t1"), gkey], w=[k_("t1")]))
            st.append(lambda: S.op("dve", lambda e: e.tensor_tensor(out=rv["ra"], in0=x1, in1=cb, op=ALU.mult), r=[k_("t1")] + tr, w=[k_("ra")]))
            st.append(lambda: S.op("dve", lambda e: e.tensor_tensor(out=rv["rb"], in0=x2, in1=sb_, op=ALU.mult), r=[k_("t1")] + tr, w=[k_("rb")]))
            st.append(lambda: S.op("dve", lambda e: e.tensor_tensor(out=dst[:, :, ro:ro + hf], in0=rv["ra"], in1=rv["rb"], op=ALU.subtract), r=[k_("ra"), k_("rb")], w=[dkey]))
            st.append(lambda: S.op("dve", lambda e: e.tensor_tensor(out=rv["ra2"], in0=x2, in1=cb, op=ALU.mult), r=[k_("t1")] + tr, w=[k_("ra2")]))
            st.append(lambda: S.op("dve", lambda e: e.tensor_tensor(out=rv["rb2"], in0=x1, in1=sb_, op=ALU.mult), r=[k_("t1")] + tr, w=[k_("rb2")]))
            st.append(lambda: S.op("dve", lambda e: e.tensor_tensor(out=dst[:, :, ro + hf:ro + 2 * hf], in0=rv["ra2"], in1=rv["rb2"], op=ALU.add), r=[k_("ra2"), k_("rb2")], w=[dkey]))

            def copies():
                if ro > 0:
                    S.op("pool", lambda e: e.tensor_copy(out=dst[:, :, 0:ro], in_=t1v[:, :, 0:ro]), r=[k_("t1")], w=[dkey])
                if ro + 2 * hf < Dh:
                    S.op("pool", lambda e: e.tensor_copy(out=dst[:, :, ro + 2 * hf:Dh], in_=t1v[:, :, ro + 2 * hf:Dh]), r=[k_("t1")], w=[dkey])
            st.insert(6, copies)
            return st

        def run_interleaved(chains):
            for s_ in range(max(len(c_) for c_ in chains)):
                for c_ in chains:
                    if s_ < len(c_):
                        c_[s_]()

        def head_norm_rope(src, skeys, H, Dh, gaint, gkey, ro, hf, cos_, sin_, dst, dkey, np_=128):
            n = H * Dh
            t1v = t1[0:np_, 0:n].rearrange("p (h d) -> p h d", h=H)
            t2v = t2[0:np_, 0:n].rearrange("p (h d) -> p h d", h=H)
            hs = hss[0:np_, 0:H]; hr = hrs[0:np_, 0:H]
            act(t1v, src, AF.Square, r=skeys, w=["t1"])
            S.op("dve", lambda e: e.tensor_reduce(out=hs, in_=t1v, axis=AX.X, op=ALU.add), r=["t1"], w=["hss"])
            act(hr, hs, AF.Sqrt, r=["hss"], w=["hrs"], scale=1.0 / Dh, bias=EPS)
            S.op("dve", lambda e: e.reciprocal(out=hr, in_=hr), r=["hrs"], w=["hrs"])
            S.op("dve", lambda e: e.tensor_tensor(out=t2v, in0=src, in1=hr.unsqueeze(2).to_broadcast([np_, H, Dh]), op=ALU.mult), r=skeys + ["hrs"], w=["t2"])
            S.op("dve", lambda e: e.tensor_tensor(out=t1v, in0=t2v, in1=gaint[0:np_].unsqueeze(1).to_broadcast([np_, H, Dh]), op=ALU.mult), r=["t2", gkey], w=["t1"])
            x1 = t1v[:, :, ro:ro + hf]; x2 = t1v[:, :, ro + hf:ro + 2 * hf]
            cb = cos_.unsqueeze(1).to_broadcast([np_, H, hf]); sb_ = sin_.unsqueeze(1).to_broadcast([np_, H, hf])
            rav = ra[0:np_, 0:H * hf].rearrange("p (h d) -> p h d", h=H)
            rbv = rbb[0:np_, 0:H * hf].rearrange("p (h d) -> p h d", h=H)
            tr = ["Mout", "Nout", "Cout"]
            S.op("dve", lambda e: e.tensor_tensor(out=rav, in0=x1, in1=cb, op=ALU.mult), r=["t1"] + tr, w=["ra"])
            S.op("dve", lambda e: e.tensor_tensor(out=rbv, in0=x2, in1=sb_, op=ALU.mult), r=["t1"] + tr, w=["rbb"])
            S.op("dve", lambda e: e.tensor_tensor(out=dst[:, :, ro:ro + hf], in0=rav, in1=rbv, op=ALU.subtract), r=["ra", "rbb"], w=[dkey])
            S.op("dve", lambda e: e.tensor_tensor(out=rav, in0=x2, in1=cb, op=ALU.mult), r=["t1"] + tr, w=["ra"])
            S.op("dve", lambda e: e.tensor_tensor(out=rbv, in0=x1, in1=sb_, op=ALU.mult), r=["t1"] + tr, w=["rbb"])
            S.op("dve", lambda e: e.tensor_tensor(out=dst[:, :, ro + hf:ro + 2 * hf], in0=rav, in1=rbv, op=ALU.add), r=["ra", "rbb"], w=[dkey])
            if ro > 0:
                S.op("pool", lambda e: e.tensor_copy(out=dst[:, :, 0:ro], in_=t1v[:, :, 0:ro]), r=["t1"], w=[dkey])
            if ro + 2 * hf < Dh:
                S.op("pool", lambda e: e.tensor_copy(out=dst[:, :, ro + 2 * hf:Dh], in_=t1v[:, :, ro + 2 * hf:Dh]), r=["t1"], w=[dkey])

        Msub = Mem(big, NB); Msub.lo = lo_pers; Msub.hi = lo0
        rawq = [Msub.lo_alloc([768], F32) for _ in range(2)]; rawk = [Msub.lo_alloc([768], F32), M.lo_alloc([768], F32)]
        Tq = [mk_tmps(Msub, 768, 8, 16) for _ in range(2)]; Tk = [mk_tmps(Msub, 768, 8, 16) for _ in range(2)]

        def b2_front(i):
            ts = slice(i * 128, (i + 1) * 128)
            par = i % 2
            bA, bB = nb(), nb()
            for k in range(6):
                mm(P[bA][:, :], cqT[:, k, ts], Wqb[:, k, 0:512], k == 0, k == 5, r=["Wqb", ("cqT", i // 4)], w=[PK[bA]])
            for k in range(6):
                mm(P[bB][:, 0:256], cqT[:, k, ts], Wqb[:, k, 512:768], k == 0, k == 5, r=["Wqb", ("cqT", i // 4)], w=[PK[bB]])
            act(rawq[par][:, 0:512], P[bA][:, :], AF.Copy, r=[PK[bA]], w=[("rawq", par)])
            act(rawq[par][:, 512:768], P[bB][:, 0:256], AF.Copy, r=[PK[bB]], w=[("rawq", par)])
            bA, bB = nb(), nb()
            for hh, bb in ((0, bA), (1, bB)):
                for k in range(2):
                    mm(P[bb][:, :], ckvT[:, k, ts], Wkvb[:, k, hh * 512:(hh + 1) * 512], k == 0, k == 1, r=["Wkvb", ("ckvT", i // 4)], w=[PK[bb]])
            rv = rawk[par].rearrange("p (h d) -> p h d", h=8)
            for hh, bb in ((0, bA), (1, bB)):
                pv = P[bb][:, :].rearrange("p (h d) -> p h d", h=4)
                act(rv[:, hh * 4:(hh + 1) * 4, 0:64], pv[:, :, 0:64], AF.Copy, r=[PK[bb]], w=[("rawk", par)])
                act(V[:, i, hh * 4:(hh + 1) * 4, 0:64], pv[:, :, 64:128], AF.Copy, r=[PK[bb]], w=[("V", i)])
            S.op("pool", lambda e, i=i: e.tensor_copy(out=rv[:, :, 64:96], in_=kpe[:, i, :].unsqueeze(1).to_broadcast([128, 8, 32])), r=[("kpe", i)], w=[("rawk", par)])

        def b2_chains(i):
            par = i % 2
            dq = drq[par].rearrange("p (h d) -> p h d", h=8); dk = drk[par].rearrange("p (h d) -> p h d", h=8)
            cq_ = hnr_stages(("cq", par), Tq[par], rawq[par].rearrange("p (h d) -> p h d", h=8), [("rawq", par)], 8, 96, qgt, "qgt", 64, 16, cosM[:, i, :], sinM[:, i, :], dq, ("drq", par))
            ck_ = hnr_stages(("ck", par), Tk[par], rawk[par].rearrange("p (h d) -> p h d", h=8), [("rawk", par)], 8, 96, kgt, "kgt", 64, 16, cosM[:, i, :], sinM[:, i, :], dk, ("drk", par))
            return [cq_[:6], ck_[:6]], [cq_[6:], ck_[6:]]

        def b2_out(i):
            ts = slice(i * 128, (i + 1) * 128)
            par = i % 2
            dq = drq[par].rearrange("p (h d) -> p h d", h=8); dk = drk[par].rearrange("p (h d) -> p h d", h=8)
            for (dd, dkey_, dstT, okey) in ((dq, ("drq", par), QT, "QT"), (dk, ("drk", par), KT, "KT")):
                b = nb(); pb = P[b][:, :].bitcast(BF16)
                for h in range(8):
                    tp(pb[0:96, h * 128:(h + 1) * 128], dd[:, h, :], ident, r=[dkey_, "ident"], w=[PK[b]])
                act(dstT[0:96, :, ts], pb[0:96, :].rearrange("p (h t) -> p h t", h=8), AF.Copy, r=[PK[b]], w=[(okey, i)])

        b2_front(0); b2_front(1)
        h1_, h2_ = b2_chains(0)
        run_interleaved(h1_)
        for i in range(NT):
            if i + 2 < NT:
                b2_front(i + 2)
            nxt = b2_chains(i + 1) if i + 1 < NT else ([], [])
            run_interleaved(nxt[0] + h2_)
            h2_ = nxt[1]
            if i >= 1:
                b2_out(i - 1)
        b2_out(NT - 1)
        QTk = [("QT", i) for i in range(NT)]
        dbg_out("QT", QT[0:96].rearrange("p k t -> p (k t)"), QTk)
        dbg_out("KT", KT[0:96].rearrange("p k t -> p (k t)"), [("KT", i) for i in range(NT)])
        dbg_out("V", V.rearrange("p a b c -> p (a b c)"), [("V", i) for i in range(NT)] + ["Vones"])
        if upto <= 3:
            return finish(nc, S, st)
        S.barrier()

        nbanks[0] = 6
        M.lo = lo0
        om = M.lo_alloc([NT, 512], BF16)
        PT = [M.lo_alloc([512], BF16) for _ in range(4)]
        rec4 = [M.lo_alloc([4], F32) for _ in range(2)]
        pt_i = [0]

        gchunk = [0]

        def causal_attn_multi(jobs):
            steps = []
            for ji in range(len(jobs)):
                for c in range(4):
                    for kt in range(4 * c + 4):
                        steps.append((ji, c, kt))

            def emit_qk(step):
                ji, c, kt = step
                J = jobs[ji]
                q0 = max(kt - 4 * c, 0)
                n = 512 - 128 * q0
                b = nb()
                has_extra = J["extra"] is not None
                mm(P[b][:, 0:n], J["KT"][0:J["kn"], kt * 128:(kt + 1) * 128], J["QT"](c * 512 + q0 * 128, (c + 1) * 512),
                   True, not has_extra, r=J["kk"](kt) + J["qk"](c), w=[PK[b]])
                if has_extra:
                    J["extra"](P[b][:, 0:n], kt, c * 512 + q0 * 128, (c + 1) * 512, PK[b])
                return b, n, q0

            LA = 2
            pend = [emit_qk(steps[q]) for q in range(min(LA, len(steps)))]
            for si, (ji, c, kt) in enumerate(steps):
                J = jobs[ji]
                b, n, q0 = pend.pop(0)
                if si + LA < len(steps):
                    pend.append(emit_qk(steps[si + LA]))
                if kt == 0:
                    gchunk[0] += 1
                ab = 6 + (gchunk[0] % 2)
                Oacc = P[ab][:, 0:260].rearrange("p (q d) -> p q d", q=4)
                pi = pt_i[0]; pt_i[0] = (pi + 1) % len(PT)
                pt = PT[pi]
                act(pt[:, 0:n], P[b][:, 0:n], AF.Exp, r=[PK[b]], w=[("PT", pi)], scale=J["scale"])
                if kt >= 4 * c:
                    S.op("dve", lambda e, pt=pt: e.tensor_tensor(out=pt[:, 0:128], in0=pt[:, 0:128], in1=tri, op=ALU.mult), r=[("PT", pi), "tri"], w=[("PT", pi)])
                for qi in range(q0, 4):
                    mm(Oacc[:, qi, :], pt[:, (qi - q0) * 128:(qi - q0 + 1) * 128], J["V"](kt), kt == 0 and qi == 0, kt == 4 * c + qi,
                       r=[("PT", pi)] + J["vk"](kt), w=[PK[ab]], skip_group_check=True)
                if kt == 4 * c + 3:
                    J["fin"](c, Oacc, PK[ab])

        jobs = []
        for h in range(8):
            def fin(c, Oacc, pk, h=h):
                rc = rec4[c % 2]
                S.op("dve", lambda e: e.reciprocal(out=rc, in_=Oacc[:, :, 64]), r=[pk], w=[("rec4", c % 2)])
                S.op("dve", lambda e: e.tensor_tensor(out=om[:, 4 * c:4 * c + 4, h * 64:(h + 1) * 64], in0=Oacc[:, :, 0:64],
                                                      in1=rc.unsqueeze(2).to_broadcast([128, 4, 64]), op=ALU.mult),
                     r=[pk, ("rec4", c % 2)], w=[("om", c)])
            jobs.append(dict(KT=KT[:, h, :], kn=96, QT=(lambda a, b_, h=h: QT[0:96, h, a:b_]), V=(lambda kt, h=h: V[:, kt, h, :]), scale=96 ** -0.5,
                             extra=None, fin=fin, qk=(lambda c: [("QT", 4 * c + q) for q in range(4)]), kk=(lambda kt: [("KT", kt)]),
                             vk=(lambda kt: [("V", kt), "Vones"])))
        causal_attn_multi(jobs)
        for i in range(NT):
            b = nb(); pb = P[b][:, :].bitcast(BF16)
            for j in range(4):
                tp(pb[:, j * 128:(j + 1) * 128], om[:, i, j * 128:(j + 1) * 128], ident, r=[("om", i // 4), "ident"], w=[PK[b]])
            act(omT[:, :, i * 128:(i + 1) * 128], pb[:, 0:512].rearrange("p (k t) -> p k t", k=4), AF.Copy, r=[PK[b]], w=[("omT", i)])
        dbg_out("om", om.rearrange("p a b -> p (a b)"), [("om", c) for c in range(4)])
        if upto <= 4:
            return finish(nc, S, st)
        S.barrier()

        nbanks[0] = 4
        M.lo = lo0; M.hi = hi0
        qnT = M.lo_alloc([8, S_], BF16); ksT = M.lo_alloc([2, S_], BF16); kwT = M.lo_alloc([2, S_], BF16)
        vs = M.lo_alloc([NT, 2, 65], BF16); vw = M.lo_alloc([NT, 2, 65], BF16)
        gates = M.lo_alloc([NT, 3, 8], F32)
        kcmpT = M.lo_alloc([2, 128], BF16); VCX = M.lo_alloc([2, 97], BF16)
        PT = [M.lo_alloc([512], BF16) for _ in range(4)]
        rec4 = [M.lo_alloc([4], F32) for _ in range(2)]
        t1 = M.lo_alloc([512], F32); t2 = M.lo_alloc([512], F32)
        hss = M.lo_alloc([8], F32); hrs = M.lo_alloc([8], F32)
        ra = M.lo_alloc([128], F32); rbb = M.lo_alloc([128], F32)
        drb = [M.lo_alloc([512], BF16) for _ in range(2)]
        nqt = M.lo_alloc([64], F32); nkct = M.lo_alloc([64], F32); nkst = M.lo_alloc([64], F32); nkwt = M.lo_alloc([64], F32)
        lo2 = M.lo
        kc2 = M.hi_alloc([2, S_], BF16); vc2 = M.hi_alloc([2, S_], BF16)
        hi_kv = M.hi
        hT = M.hi_alloc([8, S_], BF16)
        Wqn = M.hi_alloc([8, 512], BF16); Wkc2 = M.hi_alloc([8, 256], BF16); Wvc2 = M.hi_alloc([8, 256], BF16)
        Wkv4 = M.hi_alloc([8, 512], BF16); Wgn = M.hi_alloc([8, 24], BF16)
        ge = M.hi_alloc([24], F32)
        S.dma("sp", hT.rearrange("p k t -> p (k t)"), hTs, r=["hTs"], w=["hTall"])
        for t_, d_ in ((nqt, nqB), (nkct, nkcB), (nkst, nksB), (nkwt, nkwB)):
            S.dma("sp", t_, d_, w=["ngain"])
        load_w(Wqn, w_qn, "Wqn", 8, 512); load_w(Wkv4, w_kv4, "Wkv4", 8, 512); load_w(Wgn, w_gn, "Wgn", 8, 24)
        load_w(Wkc2, w_kc2, "Wkc2", 8, 256); load_w(Wvc2, w_vc2, "Wvc2", 8, 256)
        for g_ in range(2):
            S.dma("sp", ksT[64:96, g_, :], XEall_d, w=["ksTx"])
        S.op("pool", lambda e: e.memset(vs[:, :, :, 64:65], 1.0), w=["vsones"])
        S.op("pool", lambda e: e.memset(vw[:, :, :, 64:65], 1.0), w=["vwones"])
        S.op("pool", lambda e: e.memset(kc2[64:128, :, S_ - 1:S_], 0.0), w=["kc2pad"])
        S.op("pool", lambda e: e.memset(vc2[64:128, :, S_ - 1:S_], 0.0), w=["vc2pad"])
        MsubD = Mem(big, NB); MsubD.lo = lo_pers + 16384; MsubD.hi = lo0
        TDq = [mk_tmps(MsubD, 512, 8, 8) for _ in range(2)]; TDs = [mk_tmps(MsubD, 128, 2, 8) for _ in range(2)]; TDw = [mk_tmps(MsubD, 128, 2, 8) for _ in range(2)]
        dnq = [MsubD.lo_alloc([512], BF16) for _ in range(2)]
        dns = [MsubD.lo_alloc([128], BF16) for _ in range(2)]; dnw = [MsubD.lo_alloc([128], BF16) for _ in range(2)]

        def d_front(i):
            ts = slice(i * 128, (i + 1) * 128)
            bq = 4 + 2 * (i % 2)
            for k in range(8):
                mm(P[bq][:, :], hT[:, k, ts], Wqn[:, k, :], k == 0, k == 7, r=["hTall", "Wqn"], w=[PK[bq]])
            bk = 5 + 2 * (i % 2)
            for k in range(8):
                mm(P[bk][:, :], hT[:, k, ts], Wkv4[:, k, :], k == 0, k == 7, r=["hTall", "Wkv4"], w=[PK[bk]])
            bg = nb()
            for k in range(8):
                mm(P[bg][:, 0:24], hT[:, k, ts], Wgn[:, k, :], k == 0, k == 7, r=["hTall", "Wgn"], w=[PK[bg]])
            act(vs[:, i, :, 0:64], P[bk][:, 256:384].rearrange("p (g d) -> p g d", g=2), AF.Copy, r=[PK[bk]], w=[("vs", i)])
            act(vw[:, i, :, 0:64], P[bk][:, 384:512].rearrange("p (g d) -> p g d", g=2), AF.Copy, r=[PK[bk]], w=[("vw", i)])
            act(ge, P[bg][:, 0:24], AF.Exp, r=[PK[bg]], w=["ge"], scale=-1.0)
            S.op("dve", lambda e: e.tensor_scalar(out=ge, in0=ge, scalar1=1.0, scalar2=None, op0=ALU.add), r=["ge"], w=["ge"])
            S.op("dve", lambda e, i=i: e.reciprocal(out=gates[:, i].rearrange("p a b -> p (a b)"), in_=ge), r=["ge"], w=[("gates", i)])
            return bq, bk

        def d_chains(i, bq, bk):
            par = i % 2
            dq = dnq[par].rearrange("p (h d) -> p h d", h=8)
            ds_ = dns[par].rearrange("p (h d) -> p h d", h=2); dw_ = dnw[par].rearrange("p (h d) -> p h d", h=2)
            c1 = hnr_stages(("dq", par), TDq[par], P[bq][:, :].rearrange("p (h d) -> p h d", h=8), [PK[bq]], 8, 64, nqt, "ngain", 0, 8, cosN[:, i, :], sinN[:, i, :], dq, ("dnq", par))
            c2 = hnr_stages(("ds", par), TDs[par], P[bk][:, 0:128].rearrange("p (h d) -> p h d", h=2), [PK[bk]], 2, 64, nkst, "ngain", 0, 8, cosN[:, i, :], sinN[:, i, :], ds_, ("dns", par))
            c3 = hnr_stages(("dw", par), TDw[par], P[bk][:, 128:256].rearrange("p (h d) -> p h d", h=2), [PK[bk]], 2, 64, nkwt, "ngain", 0, 8, cosN[:, i, :], sinN[:, i, :], dw_, ("dnw", par))
            return [c1[:6], c2[:6], c3[:6]], [c1[6:], c2[6:], c3[6:]]

        def d_out(i):
            ts = slice(i * 128, (i + 1) * 128)
            par = i % 2
            b = nb(); pb = P[b][:, :].bitcast(BF16)
            for p_ in range(8):
                tp(pb[0:64, p_ * 128:(p_ + 1) * 128], dnq[par][:, p_ * 64:(p_ + 1) * 64], ident, r=[("dnq", par), "ident"], w=[PK[b]])
            act(qnT[0:64, :, ts], pb[0:64, :].rearrange("p (k t) -> p k t", k=8), AF.Copy, r=[PK[b]], w=[("qnT", i)])
            for (dd, dkey_, dstT, dk) in ((dns[par], ("dns", par), ksT, "ksT"), (dnw[par], ("dnw", par), kwT, "kwT")):
                b2 = nb(); pb = P[b2][:, :].bitcast(BF16)
                for g_ in range(2):
                    tp(pb[0:64, g_ * 128:(g_ + 1) * 128], dd[:, g_ * 64:(g_ + 1) * 64], ident, r=[dkey_, "ident"], w=[PK[b2]])
                act(dstT[0:64, :, ts], pb[0:64, 0:256].rearrange("p (g t) -> p g t", g=2), AF.Copy, r=[PK[b2]], w=[(dk, i)])

        def fm_cmp_proj(idx):
            c, rem = idx // 4, idx % 4
            (Wt, wk, dst, dk) = ((Wkc2, "Wkc2", kc2, "kc2"), (Wvc2, "Wvc2", vc2, "vc2"))[rem // 2]
            g = rem % 2
            b = nb()
            for k in range(8):
                mm(P[b][:, :], Wt[:, k, g * 128:(g + 1) * 128], hT[:, k, c * 512:(c + 1) * 512], k == 0, k == 7, r=["hTall", wk], w=[PK[b]])
            act(dst[0:64, g, c * 512:(c + 1) * 512], P[b][0:64, :], AF.Copy, r=[PK[b]], w=[dk])
            if c == 0:
                act(dst[64:128, g, 0:511], P[b][64:128, 1:512], AF.Copy, r=[PK[b]], w=[dk])
            else:
                act(dst[64:128, g, c * 512 - 1:(c + 1) * 512 - 1], P[b][64:128, :], AF.Copy, r=[PK[b]], w=[dk])

        fr_ = {0: d_front(0), 1: d_front(1)}
        h1_, h2_ = d_chains(0, *fr_[0])
        run_interleaved(h1_)
        for i in range(NT):
            if i + 2 < NT:
                fr_[i + 2] = d_front(i + 2)
            nxt = d_chains(i + 1, *fr_[i + 1]) if i + 1 < NT else ([], [])
            run_interleaved(nxt[0] + h2_)
            h2_ = nxt[1]
            fm_cmp_proj(i)
            if i >= 1:
                d_out(i - 1)
        d_out(NT - 1)
        dbg_out("qnT", qnT[0:64].rearrange("p k t -> p (k t)"), [("qnT", i) for i in range(NT)])
        dbg_out("ksT", ksT[0:64].rearrange("p k t -> p (k t)"), [("ksT", i) for i in range(NT)])
        dbg_out("gates", gates.rearrange("p a b c -> p (a b c)"), [("gates", i) for i in range(NT)])
        dbg_out("kc2", kc2.rearrange("p k t -> p (k t)"), ["kc2", "kc2pad"])
        if upto <= 5:
            return finish(nc, S, st)
        S.barrier()

        nbanks[0] = 8
        M.hi = hi_kv
        hiE = M.hi
        W1k = M.lo_alloc([16, 256], BF16); W1v = M.lo_alloc([16, 256], BF16)
        W2k = M.lo_alloc([2, 64], BF16); W2v = M.lo_alloc([2, 64], BF16)
        pkf = M.lo_alloc([16], F32); pvf = M.lo_alloc([16], F32); pkb = M.lo_alloc([16], BF16); pvb = M.lo_alloc([16], BF16)
        biask = M.lo_alloc([2], F32); biasv = M.lo_alloc([2], F32)
        hid = [M.lo_alloc([128], BF16) for _ in range(2)]
        ovl = M.lo_alloc([32], BF16)
        load_w(W1k, w1k, "W1k", 16, 256); load_w(W1v, w1v, "W1v", 16, 256)
        load_w(W2k, w2k, "W2k", 2, 64); load_w(W2v, w2v, "W2v", 2, 64)
        S.dma("sp", pkf, posk, w=["pkf"]); S.dma("sp", pvf, posv, w=["pvf"]); S.dma("sp", ovl[0:127], ovl_d, w=["ovl"])
        S.op("pool", lambda e: e.memset(VCX[0:127, :, 64:65], 1.0), w=["VCXa"])
        for g in range(2):
            S.op("pool", lambda e, g=g: e.tensor_copy(out=VCX[0:127, g, 65:97], in_=ovl[0:127]), r=["ovl"], w=["VCXb"])
        rt = [M.lo_alloc([128], BF16) for _ in range(3)]
        rt_i = [0]
        for (W1, w1key, W2, w2key, src, skey, posf_, pkey, isk) in ((W1k, "W1k", W2k, "W2k", kc2, ["kc2", "kc2pad"], pkf, "pkf", True),
                                                                    (W1v, "W1v", W2v, "W2v", vc2, ["vc2", "vc2pad"], pvf, "pvf", False)):
            srcv = src.rearrange("p g (n s) -> p g n s", s=16)
            bo = nb()
            for g in range(2):
                bh = []
                for hc in range(2):
                    b = nb()
                    while b == bo or b in bh:
                        b = nb()
                    bh.append(b)
                for lc in range(16):
                    ri = rt_i[0]; rt_i[0] = (ri + 1) % 3
                    rtv = rt[ri][:, 0:127]
                    S.op("dve", lambda e, rtv=rtv, g=g, lc=lc, srcv=srcv, posf_=posf_: e.tensor_scalar(
                        out=rtv, in0=srcv[:, g, (2 * lc) // 16:(2 * lc) // 16 + 127, (2 * lc) % 16], scalar1=posf_[:, lc:lc + 1], scalar2=None, op0=ALU.add),
                        r=skey + [pkey], w=[("rt", ri)])
                    for hc in range(2):
                        mm(P[bh[hc]][:, 0:127], W1[:, lc, hc * 128:(hc + 1) * 128], rtv, lc == 0, lc == 15, r=[w1key, ("rt", ri)], w=[PK[bh[hc]]], sig=(hc == 1 or lc == 15))
                for hc in range(2):
                    act(hid[hc][:, 0:127], P[bh[hc]][:, 0:127], AF.Silu, r=[PK[bh[hc]]], w=[("hid", hc)])
                for hc in range(2):
                    mm(P[bo][0:127, g * 64:(g + 1) * 64], hid[hc][:, 0:127], W2[:, hc, :], hc == 0, hc == 1, r=[("hid", hc), w2key], w=[PK[bo]])
            if isk:
                d = drb[1][0:127, 0:128].rearrange("p (h d) -> p h d", h=2)
                head_norm_rope(P[bo][0:127, 0:128].rearrange("p (h d) -> p h d", h=2), [PK[bo]], 2, 64, nkct, "ngain", 0, 8, cosC[0:127], sinC[0:127], d, "drb1", np_=127)
                b2 = nb(); pb = P[b2][:, :].bitcast(BF16)
                for g_ in range(2):
                    tp(pb[0:64, g_ * 128:g_ * 128 + 127], drb[1][0:127, g_ * 64:(g_ + 1) * 64], ident[0:127, 0:127], r=["drb1", "ident"], w=[PK[b2]])
                act(kcmpT[0:64, :, 0:127], pb[0:64, 0:256].rearrange("p (g t) -> p g t", g=2)[:, :, 0:127], AF.Copy, r=[PK[b2]], w=["kcmpT"])
            else:
                act(VCX[0:127, :, 0:64], P[bo][0:127, 0:128].rearrange("p (g d) -> p g d", g=2), AF.Copy, r=[PK[bo]], w=["VCXc"])
        dbg_out("kcmpT", kcmpT[0:64].rearrange("p g t -> p (g t)"), ["kcmpT"])
        dbg_out("VCX", VCX[0:127].rearrange("p a b -> p (a b)"), ["VCXa", "VCXb", "VCXc"])
        if upto <= 6:
            return finish(nc, S, st)
        S.barrier()

        M.lo = lo2; M.hi = hi0
        onsa = M.hi_alloc([NT, 512], F32)
        mbT = M.hi_alloc([2, S_], BF16)
        vmT = M.hi_alloc([S_], BF16); XE = M.hi_alloc([NT, 128], BF16)
        fb = M.hi_alloc([NT, 32], F32); vj = M.hi_alloc([NT, 32], F32)
        pc = [M.lo_alloc([4, 128], BF16) for _ in range(2)]
        rsum = M.lo_alloc([8], F32); rec8 = M.lo_alloc([8], F32); gr = M.lo_alloc([8], F32)
        tmp_i = M.lo_alloc([8, 32], F32); imp = M.lo_alloc([2, 32], F32); m8 = M.lo_alloc([2, 8], F32)
        sel = M.lo_alloc([2, 32], F32); mbf = M.lo_alloc([2, 96], BF16)
        S.op("pool", lambda e: e.memset(mbf, 0.0), w=["mbf"])
        tmpo = M.lo_alloc([8, 64], F32)
        pw = [M.lo_alloc([3, 128], BF16) for _ in range(4)]
        onb = [M.lo_alloc([512], BF16) for _ in range(2)]
        S.dma("sp", vmT[0:127], vmT_d, w=["vmT"]); S.dma("sp", XE[0:32].rearrange("p a b -> p (a b)"), XE_d, w=["XE"])
        S.dma("sp", fb.rearrange("p a b -> p (a b)"), fb_d, w=["fb"]); S.dma("sp", vj.rearrange("p a b -> p (a b)"), vj_d, w=["vj"])
        VCXk = ["VCXa", "VCXb", "VCXc"]
        for i in range(NT):
            ts = slice(i * 128, (i + 1) * 128)
            import os
            if int(os.environ.get('KDEV_F', '9')) <= 0:
                continue
            sb_ = [nb(), nb()]
            ob = [nb(), nb()]
            for p in range(8):
                j, g = p // 2, p % 2
                if os.environ.get('KDEV_G0'):
                    g = 0
                mm(P[sb_[p // 4]][0:127, (p % 4) * 128:(p % 4 + 1) * 128], kcmpT[0:64, g, 0:127], qnT[0:64, p, ts], True, True,
                   r=["kcmpT", ("qnT", i)], w=[PK[sb_[p // 4]]])
            for hf_ in range(2):
                pcv = pc[hf_]
                act(pcv[0:127], P[sb_[hf_]][0:127, :].rearrange("p (a b) -> p a b", a=4), AF.Exp, r=[PK[sb_[hf_]]], w=[("pc", hf_)], scale=0.125)
                S.op("dve", lambda e, pcv=pcv: e.tensor_tensor(out=pcv[0:127], in0=pcv[0:127], in1=vmT[0:127, ts].unsqueeze(1).to_broadcast([127, 4, 128]), op=ALU.mult),
                     r=[("pc", hf_), "vmT"], w=[("pc", hf_)])
            import os
            FL = int(os.environ.get('KDEV_F', '9'))
            if FL <= 1:
                continue
            for p in range(8):
                g = p % 2
                mm(P[ob[p // 4]][:, (p % 4) * 97:(p % 4 + 1) * 97], pc[p // 4][0:127, p % 4, :], VCX[0:127, g, :], True, True,
                   r=[("pc", p // 4)] + VCXk, w=[PK[ob[p // 4]]])
            if FL <= 2:
                continue
            OC = [P[ob[h_]][:, 0:388].rearrange("p (a b) -> p a b", a=4) for h_ in range(2)]
            for h_ in range(2):
                S.op("dve", lambda e, h_=h_: e.tensor_scalar(out=rsum[:, h_ * 4:(h_ + 1) * 4], in0=OC[h_][:, :, 64], scalar1=1e-30, scalar2=None, op0=ALU.max), r=[PK[ob[h_]]], w=["rsum"])
            S.op("dve", lambda e: e.reciprocal(out=rec8, in_=rsum), r=["rsum"], w=["rec8"])
            S.op("dve", lambda e, i=i: e.tensor_tensor(out=gr, in0=gates[:, i, 0, :], in1=rec8, op=ALU.mult), r=["rec8", ("gates", i)], w=["gr"])
            for h_ in range(2):
                S.op("dve", lambda e, h_=h_, i=i: e.tensor_tensor(out=onsa[:, i, h_ * 256:(h_ + 1) * 256].rearrange("p (a b) -> p a b", a=4), in0=OC[h_][:, :, 0:64],
                                                                   in1=gr[:, h_ * 4:(h_ + 1) * 4].unsqueeze(2).to_broadcast([128, 4, 64]), op=ALU.mult),
                     r=[PK[ob[h_]], "gr"], w=[("onsa", i)])
                S.op("dve", lambda e, h_=h_: e.tensor_tensor(out=tmp_i[:, h_ * 4:(h_ + 1) * 4, :], in0=OC[h_][:, :, 65:97],
                                                             in1=rec8[:, h_ * 4:(h_ + 1) * 4].unsqueeze(2).to_broadcast([128, 4, 32]), op=ALU.mult),
                     r=[PK[ob[h_]], "rec8"], w=["tmp_i"])
            if FL <= 3:
                continue
            S.op("dve", lambda e: e.tensor_reduce(out=imp, in_=tmp_i.rearrange("t (j g) n -> t g n j", g=2), axis=AX.X, op=ALU.add), r=["tmp_i"], w=["imp"])
            S.op("dve", lambda e, i=i: e.tensor_tensor(out=imp, in0=imp, in1=fb[:, i, :].unsqueeze(1).to_broadcast([128, 2, 32]), op=ALU.add), r=["imp", "fb"], w=["imp"])
            if FL <= 4:
                continue
            for g in range(2):
                S.op("dve", lambda e, g=g: e.max(out=m8[:, g, :], in_=imp[:, g, :]), r=["imp"], w=["m8"])
                S.op("dve", lambda e, g=g: e.tensor_scalar(out=sel[:, g, :], in0=imp[:, g, :], scalar1=m8[:, g, 7:8], scalar2=None, op0=ALU.is_ge), r=["imp", "m8"], w=["sel"])
            S.op("dve", lambda e, i=i: e.tensor_tensor(out=sel, in0=sel, in1=vj[:, i, :].unsqueeze(1).to_broadcast([128, 2, 32]), op=ALU.mult), r=["sel", "vj"], w=["sel"])
            S.op("dve", lambda e: e.tensor_scalar(out=mbf[:, :, 64:96], in0=sel, scalar1=-1.0, scalar2=30000.0, op0=ALU.add, op1=ALU.mult), r=["sel"], w=["mbf"])
            if i == 5:
                dbg_out("sel5", sel.rearrange("p a b -> p (a b)"), ["sel"])
                dbg_out("imp5", imp.rearrange("p a b -> p (a b)"), ["imp"])
            if FL <= 5:
                continue
            b = nb(); pb = P[b][:, :].bitcast(BF16)
            for g in range(2):
                tp(pb[0:96, g * 128:(g + 1) * 128], mbf[:, g, :], ident, r=["mbf", "ident"], w=[PK[b]])
            qm = qnT[64:96].rearrange("p (j g) t -> p j g t", g=2)
            for g in range(2):
                act(qm[:, :, g, ts], pb[64:96, g * 128:(g + 1) * 128].unsqueeze(1).to_broadcast([32, 4, 128]), AF.Copy, r=[PK[b]], w=[("mbq", i)])
        dbg_out("onsa_c", onsa.rearrange("p a b -> p (a b)"), [("onsa", i) for i in range(NT)])
        dbg_out("mbT", qnT[64:96, 0:2, :].rearrange("p a b -> p (a b)"), [("mbq", i) for i in range(NT)])
        if upto <= 7:
            return finish(nc, S, st)

        S.barrier()
        nbanks[0] = 6
        jobs = []
        for p in range(8):
            j, g = p // 2, p % 2

            def extra(ps_ap, kt, a, b_, pk, g=g):
                mm(ps_ap, XE[0:32, kt, :], mbT[0:32, g, a:b_], False, True, r=["XE"] + [("mbT", q) for q in range(a // 128, b_ // 128)], w=[pk])

            def fin(c, Oacc, pk, p=p):
                rc = rec4[c % 2]
                S.op("dve", lambda e: e.reciprocal(out=rc, in_=Oacc[:, :, 64]), r=[pk], w=[("rec4", c % 2)])
                S.op("dve", lambda e: e.tensor_tensor(out=rc, in0=rc, in1=gates[:, 4 * c:4 * c + 4, 1, p], op=ALU.mult), r=[("rec4", c % 2)] + [("gates", 4 * c + q) for q in range(4)], w=[("rec4", c % 2)])
                tv = tmpo[:, 0:4, :]
                S.op("dve", lambda e: e.tensor_tensor(out=tv, in0=Oacc[:, :, 0:64], in1=rc.unsqueeze(2).to_broadcast([128, 4, 64]), op=ALU.mult), r=[pk, ("rec4", c % 2)], w=["tmpo"])
                S.op("pool", lambda e: e.tensor_tensor(out=onsa[:, 4 * c:4 * c + 4, p * 64:(p + 1) * 64], in0=onsa[:, 4 * c:4 * c + 4, p * 64:(p + 1) * 64], in1=tv, op=ALU.add),
                     r=["tmpo"] + [("onsa", 4 * c + q) for q in range(4)], w=[("onsa", 4 * c + q) for q in range(4)])
            jobs.append(dict(KT=ksT[:, g, :], kn=96, QT=(lambda a, b_, p=p: qnT[0:96, p, a:b_]), V=(lambda kt, g=g: vs[:, kt, g, :]), scale=0.125,
                             extra=None, fin=fin, qk=(lambda c: [("qnT", 4 * c + q) for q in range(4)] + [("mbq", 4 * c + q) for q in range(4)]), kk=(lambda kt: [("ksT", kt), "ksTx"]),
                             vk=(lambda kt: [("vs", kt), "vsones"])))
        causal_attn_multi(jobs)
        dbg_out("onsa_cs", onsa.rearrange("p a b -> p (a b)"), [("onsa", i) for i in range(NT)])
        if upto <= 8:
            return finish(nc, S, st)

        pw_i = [0]
        nbanks[0] = 4
        bank_state[0] = 0

        def emit_ws(i, p):
            g = p % 2
            kts = [kt for kt in (i - 2, i - 1, i) if kt >= 0]
            b = nb()
            for kt in kts:
                sl = kt - (i - 2)
                mm(P[b][:, sl * 128:(sl + 1) * 128], kwT[0:64, g, kt * 128:(kt + 1) * 128], qnT[0:64, p, i * 128:(i + 1) * 128], True, True,
                   r=[("kwT", kt), ("qnT", i)], w=[PK[b]])
            return b

        wsteps = [(i, p) for i in range(NT) for p in range(8)]
        WLA = 2
        wpend = [emit_ws(*wsteps[q]) for q in range(WLA)]
        for wi_, (i, p) in enumerate(wsteps):
            ts = slice(i * 128, (i + 1) * 128)
            g = p % 2
            kts = [kt for kt in (i - 2, i - 1, i) if kt >= 0]
            b = wpend.pop(0)
            if wi_ + WLA < len(wsteps):
                wpend.append(emit_ws(*wsteps[wi_ + WLA]))
            ob = [4 + 2 * (i % 2), 5 + 2 * (i % 2)]
            s0 = kts[0] - (i - 2)
            wi = pw_i[0]; pw_i[0] = (wi + 1) % len(pw)
            pwv = pw[wi]
            act(pwv[:, s0:3, :], P[b][:, s0 * 128:384].rearrange("p (a b) -> p a b", b=128), AF.Exp, r=[PK[b]], w=[("pw", wi)], scale=0.125)
            S.op("dve", lambda e, pwv=pwv, s0=s0: e.tensor_tensor(out=pwv[:, s0:3, :], in0=pwv[:, s0:3, :], in1=winm[:, s0:3, :], op=ALU.mult), r=[("pw", wi), "winm"], w=[("pw", wi)])
            for kt in kts:
                sl = kt - (i - 2)
                mm(P[ob[p // 4]][:, (p % 4) * 65:(p % 4 + 1) * 65], pwv[:, sl, :], vw[:, kt, g, :], kt == kts[0], kt == kts[-1],
                   r=[("pw", wi), ("vw", kt), "vwones"], w=[PK[ob[p // 4]]])
            if p != 7:
                continue
            OW = [P[ob[h_]][:, 0:260].rearrange("p (a b) -> p a b", a=4) for h_ in range(2)]
            for h_ in range(2):
                S.op("dve", lambda e, h_=h_: e.reciprocal(out=rec8[:, h_ * 4:(h_ + 1) * 4], in_=OW[h_][:, :, 64]), r=[PK[ob[h_]]], w=["rec8"])
            S.op("dve", lambda e, i=i: e.tensor_tensor(out=gr, in0=gates[:, i, 2, :], in1=rec8, op=ALU.mult), r=["rec8", ("gates", i)], w=["gr"])
            for h_ in range(2):
                S.op("dve", lambda e, h_=h_: e.tensor_tensor(out=tmpo[:, h_ * 4:(h_ + 1) * 4, :], in0=OW[h_][:, :, 0:64],
                                                             in1=gr[:, h_ * 4:(h_ + 1) * 4].unsqueeze(2).to_broadcast([128, 4, 64]), op=ALU.mult), r=[PK[ob[h_]], "gr"], w=["tmpo"])
            S.op("pool", lambda e, i=i: e.tensor_tensor(out=onb[i % 2], in0=onsa[:, i, :], in1=tmpo.rearrange("p a b -> p (a b)"), op=ALU.add), r=["tmpo", ("onsa", i)], w=[("onb", i % 2)])
            if "onsa_all" in D_:
                S.dma("sp", D_["onsa_all"][:, i * 512:(i + 1) * 512], onb[i % 2], r=[("onb", i % 2)])
            b = nb(); pb = P[b][:, :].bitcast(BF16)
            for j in range(4):
                tp(pb[:, j * 128:(j + 1) * 128], onb[i % 2][:, j * 128:(j + 1) * 128], ident, r=[("onb", i % 2), "ident"], w=[PK[b]])
            act(onT[:, :, ts], pb[:, 0:512].rearrange("p (k t) -> p k t", k=4), AF.Copy, r=[PK[b]], w=[("onT", i)])
        if upto <= 9:
            return finish(nc, S, st)
        S.barrier()

        nbanks[0] = 8
        M.lo = lo0; M.hi = hi0
        hT = M.hi_alloc([8, S_], BF16)
        mergedT = M.hi_alloc([8, S_], BF16)
        hi2 = M.hi
        Wgm = M.lo_alloc([8, 512], BF16); Wgnm = M.lo_alloc([8, 512], BF16); Wom = M.lo_alloc([4, 512], BF16); Won = M.lo_alloc([4, 512], BF16)
        e3 = M.lo_alloc([512], F32); e4 = M.lo_alloc([512], F32); tA = M.lo_alloc([512], F32); tB = M.lo_alloc([512], F32)
        mgb = [M.lo_alloc([512], BF16) for _ in range(2)]
        S.dma("sp", hT.rearrange("p k t -> p (k t)"), hTs, r=["hTs"], w=["hTall"])
        e3 = [e3, M.lo_alloc([512], F32)]; e4 = [e4, M.lo_alloc([512], F32)]
        tA = [tA, M.lo_alloc([512], F32)]; tB = [tB, M.lo_alloc([512], F32)]

        def i_front(cc, i, it):
            ts = slice(i * 128, (i + 1) * 128)
            bs = [4 * (it % 2) + q for q in range(4)]
            b1, b2, b3, b4 = bs
            for k in range(8):
                mm(P[b3][:, :], hT[:, k, ts], Wgm[:, k, :], k == 0, k == 7, r=["hTall", "Wgm"], w=[PK[b3]])
            for k in range(8):
                mm(P[b4][:, :], hT[:, k, ts], Wgnm[:, k, :], k == 0, k == 7, r=["hTall", "Wgnm"], w=[PK[b4]])
            for k in range(4):
                mm(P[b1][:, :], omT[:, k, ts], Wom[:, k, :], k == 0, k == 3, r=[("omT", i), "Wom"], w=[PK[b1]])
            for k in range(4):
                mm(P[b2][:, :], onT[:, k, ts], Won[:, k, :], k == 0, k == 3, r=[("onT", i), "Won"], w=[PK[b2]])
            return bs

        def i_back(cc, i, it, bs):
            ts = slice(i * 128, (i + 1) * 128)
            b1, b2, b3, b4 = bs
            par = it % 2
            act(e3[par], P[b3][:, :], AF.Sigmoid, r=[PK[b3]], w=[("e3", par)])
            act(e4[par], P[b4][:, :], AF.Sigmoid, r=[PK[b4]], w=[("e4", par)])
            S.op("dve", lambda e: e.tensor_tensor(out=tA[par], in0=P[b1][:, :], in1=e3[par], op=ALU.mult), r=[PK[b1], ("e3", par)], w=[("tA", par)])
            S.op("dve", lambda e: e.tensor_tensor(out=tB[par], in0=P[b2][:, :], in1=e4[par], op=ALU.mult), r=[PK[b2], ("e4", par)], w=[("tB", par)])
            S.op("dve", lambda e: e.tensor_tensor(out=mgb[par], in0=tA[par], in1=tB[par], op=ALU.add), r=[("tA", par), ("tB", par)], w=[("mgb", par)])
            if "merged" in D_:
                S.dma("sp", D_["merged"][i * 128:(i + 1) * 128, cc * 512:(cc + 1) * 512], mgb[par], r=[("mgb", par)])
            pb = P[b2][:, :].bitcast(BF16)
            for j in range(4):
                tp(pb[:, j * 128:(j + 1) * 128], mgb[par][:, j * 128:(j + 1) * 128], ident, r=[("mgb", par), "ident"], w=[PK[b2]])
            act(mergedT[:, cc * 4:(cc + 1) * 4, ts], pb[:, 0:512].rearrange("p (k t) -> p k t", k=4), AF.Copy, r=[PK[b2]], w=[("mergedT", i)])

        it = 0
        for cc in range(2):
            load_w_cols(Wgm, w_gm, "Wgm", 8, cc * 512, (cc + 1) * 512); load_w_cols(Wgnm, w_gnm, "Wgnm", 8, cc * 512, (cc + 1) * 512)
            load_w_cols(Wom, wo_mla, "Wom", 4, cc * 512, (cc + 1) * 512); load_w_cols(Won, wo_nsa, "Won", 4, cc * 512, (cc + 1) * 512)
            pend_ = i_front(cc, 0, it)
            for i in range(NT):
                cur_ = pend_
                if i + 1 < NT:
                    pend_ = i_front(cc, i + 1, it + 1)
                i_back(cc, i, it, cur_)
                it += 1
        if upto <= 10:
            return finish(nc, S, st)
        S.barrier()

        M.lo = lo0
        h2T = hT
        Wout = M.lo_alloc([8, D], BF16)
        G1 = M.lo_alloc([D], F32); A2 = M.lo_alloc([D], F32); B2 = M.lo_alloc([D], F32)
        xb = [M.lo_alloc([D], F32) for _ in range(2)]
        x1t = [M.lo_alloc([D], F32) for _ in range(2)]
        junk = M.lo_alloc([D], F32); tmpA = M.lo_alloc([D], F32)
        hb = [M.lo_alloc([D], BF16) for _ in range(2)]
        ssq = M.lo_alloc([NT], F32); rs = M.lo_alloc([NT], F32)
        S.dma("sp", G1, mods[:, 2 * D:3 * D], r=["mods"], w=["G1"])
        S.dma("sp", B2, mods[:, 3 * D:4 * D], r=["mods"], w=["AB2"])
        S.dma("sp", A2, mods[:, 4 * D:5 * D], r=["mods"], w=["AB2"])
        load_w(Wout, w_out, "Wout", 8, D, ceng="pool")
        tmpA2 = [tmpA, M.lo_alloc([D], F32)]
        tmpJ = [M.lo_alloc([D], F32) for _ in range(3)]
        xb = xb + [M.lo_alloc([D], F32)]
        x1t = x1t + [M.lo_alloc([D], F32)]

        def j_front(i):
            ts = slice(i * 128, (i + 1) * 128)
            par = i % 3
            S.dma("sp", xb[par], x[ts, :], w=[("xb", par)])
            for cc in range(2):
                b = 2 * par + cc
                for k in range(8):
                    mm(P[b][:, :], mergedT[:, k, ts], Wout[:, k, cc * 512:(cc + 1) * 512], k == 0, k == 7, r=[("mergedT", i), "Wout"], w=[PK[b]])

        def j_mid(i):
            ts = slice(i * 128, (i + 1) * 128)
            par = i % 3
            for cc in range(2):
                b = 2 * par + cc
                S.op("dve", lambda e, b=b, cc=cc: e.tensor_tensor(out=tmpJ[par][:, cc * 512:(cc + 1) * 512], in0=P[b][:, :], in1=G1[:, cc * 512:(cc + 1) * 512], op=ALU.mult), r=[PK[b], "G1"], w=[("tmpJ", par)])
                S.op("pool", lambda e, cc=cc: e.tensor_tensor(out=x1t[par][:, cc * 512:(cc + 1) * 512], in0=tmpJ[par][:, cc * 512:(cc + 1) * 512], in1=xb[par][:, cc * 512:(cc + 1) * 512], op=ALU.add),
                     r=[("tmpJ", par), ("xb", par)], w=[("x1t", par)])
            S.dma("sp", x1s[ts, :], x1t[par], r=[("x1t", par)], w=[("x1s", i)])
            norm_a(i, x1t[par], ("x1t", par))

        bank_state[0] = 0

        def nbJ():
            b = 6 + (bank_state[0] % 2)
            bank_state[0] = (bank_state[0] + 1) % 2
            return b
        nb_saved = nb
        nb = nbJ
        j_front(0); j_mid(0); j_front(1); j_mid(1)
        for i in range(NT):
            if i + 2 < NT:
                j_front(i + 2)
            b_ = norm_b1(i, x1t[i % 3], ("x1t", i % 3), A2, B2, ["AB2"])
            if i + 2 < NT:
                j_mid(i + 2)
            norm_b2(i, b_, h2T, "h2T")
        nb = nb_saved
        bank_state[0] = 0
        if "x1" in D_:
            S.dma("sp", D_["x1"], x1s, r=[("x1s", i) for i in range(NT)])
        if upto <= 11:
            return finish(nc, S, st)
        S.barrier()

        M.lo = lo_pers; M.hi = hi2 + 8 * S_ * 2
        Wd = M.hi_alloc([NFC, D], BF16)
        actT = M.hi_alloc([NFC, 1024], BF16)
        G2 = M.lo_alloc([D], F32)
        Wg2 = [M.lo_alloc([8, 256], BF16) for _ in range(2)]; Wu2 = [M.lo_alloc([8, 256], BF16) for _ in range(2)]
        sg = [M.lo_alloc([512], F32) for _ in range(2)]
        xb = [M.lo_alloc([D], F32) for _ in range(2)]
        ot = [M.lo_alloc([D], F32) for _ in range(2)]
        tmpA = M.lo_alloc([D], F32)
        S.dma("sp", G2, mods[:, 5 * D:6 * D], r=["mods"], w=["G2"])
        load_w(Wd, wd, "Wd", NFC, D)
        h2k = [("h2T", i) for i in range(NT)]
        out_toks = []
        for half in range(2):
            def ld(jg_):
                wb_ = jg_ % 2
                load_w_cols(Wg2[wb_], wg, ("Wg2", wb_), 8, jg_ * 256, (jg_ + 1) * 256)
                load_w_cols(Wu2[wb_], wu, ("Wu2", wb_), 8, jg_ * 256, (jg_ + 1) * 256)
            ld(0)
            for jg in range(NFC // 2):
                wb = jg % 2
                if jg + 1 < NFC // 2:
                    ld(jg + 1)
                for jj in range(2):
                    j = jg * 2 + jj
                    for tc in range(2):
                        t0 = half * 1024 + tc * 512
                        bg, bu = nb(), nb()
                        for k in range(8):
                            mm(P[bg][:, :], Wg2[wb][:, k, jj * 128:(jj + 1) * 128], h2T[:, k, t0:t0 + 512], k == 0, k == 7, r=[("Wg2", wb)] + h2k, w=[PK[bg]])
                        for k in range(8):
                            mm(P[bu][:, :], Wu2[wb][:, k, jj * 128:(jj + 1) * 128], h2T[:, k, t0:t0 + 512], k == 0, k == 7, r=[("Wu2", wb)] + h2k, w=[PK[bu]])
                        act(sg[tc], P[bg][:, :], AF.Silu, r=[PK[bg]], w=[("sg", tc)])
                        S.op("dve", lambda e, j=j, tc=tc, bu=bu: e.tensor_tensor(out=actT[:, j, tc * 512:(tc + 1) * 512], in0=P[bu][:, :], in1=sg[tc], op=ALU.mult),
                             r=[PK[bu], ("sg", tc)], w=[("actT", tc)])
            for il in range(8):
                i = half * 8 + il
                ts = slice(i * 128, (i + 1) * 128)
                S.dma("sp", xb[i % 2], x1s[ts, :], r=[("x1s", i)], w=[("xb", i % 2)])
                for cc in range(2):
                    b = nb()
                    for j in range(NFC):
                        mm(P[b][:, :], actT[:, j, il * 128:(il + 1) * 128], Wd[:, j, cc * 512:(cc + 1) * 512], j == 0, j == NFC - 1, r=[("actT", il // 4), "Wd"], w=[PK[b]])
                    S.op("dve", lambda e, b=b, cc=cc: e.tensor_tensor(out=tmpA[:, cc * 512:(cc + 1) * 512], in0=P[b][:, :], in1=G2[:, cc * 512:(cc + 1) * 512], op=ALU.mult), r=[PK[b], "G2"], w=["tmpA"])
                    S.op("pool", lambda e, i=i, cc=cc: e.tensor_tensor(out=ot[i % 2][:, cc * 512:(cc + 1) * 512], in0=tmpA[:, cc * 512:(cc + 1) * 512], in1=xb[i % 2][:, cc * 512:(cc + 1) * 512], op=ALU.add),
                         r=["tmpA", ("xb", i % 2)], w=[("ot", i % 2)])
                S.dma("sp", out[ts, :], ot[i % 2], r=[("ot", i % 2)], w=[("out", i)])
        return finish(nc, S, st)


def finish(nc, S, st):
    for q in S.dsem:
        for i in range(len(S.dsem[q])):
            if S.dcnt[q][i]:
                S._wait("sp", (("d", q, i), S.dcnt[q][i]))
    for e2 in ("pe", "act", "dve", "pool"):
        if S.cnt[e2]:
            S._wait("sp", (e2, S.cnt[e2]))
    S.check_no_deadlock()
    st.close()
    return nc


def _consts():
    bf = ml_dtypes.bfloat16
    c = {}
    c["ident"] = np.eye(128, dtype=np.float32).astype(bf)
    a = np.arange(128)
    c["tri"] = (a[:, None] <= a[None, :]).astype(np.float32).astype(bf)
    w = np.zeros((128, 3, 128), np.float32)
    w[:, 0, :] = (a[:, None] > a[None, :])
    w[:, 1, :] = 1.0
    w[:, 2, :] = (a[:, None] <= a[None, :])
    c["winm"] = w.reshape(128, 384).astype(bf)
    inv16 = (np.float32(500000.0) ** (-np.arange(0, 32, 2, dtype=np.float32) / np.float32(32))).astype(np.float32)
    inv8 = (np.float32(500000.0) ** (-np.arange(0, 16, 2, dtype=np.float32) / np.float32(16))).astype(np.float32)
    c["inv16"] = np.tile(inv16[None], (128, 1)).astype(np.float32)
    c["inv8"] = np.tile(inv8[None], (128, 1)).astype(np.float32)
    n = np.arange(127)
    starts = n * 16
    j = np.arange(32)
    ovl = ((starts[:, None] < j[None, :] * 64 + 64) & (starts[:, None] + 32 > j[None, :] * 64))
    c["ovl"] = ovl.astype(np.float32).astype(bf)
    t = np.arange(S_)
    c["vmT"] = ((starts[:, None] + 31) <= t[None, :]).astype(np.float32).astype(bf)
    XE = np.zeros((32, NT, 128), np.float32)
    for kt in range(NT):
        XE[2 * kt, kt, 0:64] = 1.0
        XE[2 * kt + 1, kt, 64:128] = 1.0
    c["XE"] = XE.reshape(32, NT * 128).astype(bf)
    c["XEall"] = (np.arange(32)[:, None] == (np.arange(S_)[None, :] // 64)).astype(np.float32).astype(bf)
    cur = (t // 64)
    forced = (j[None, :] == 0) | (j[None, :] == cur[:, None]) | (j[None, :] == cur[:, None] - 1)
    valid = j[None, :] <= cur[:, None]
    fb = np.where(valid, np.where(forced, 1e4, 0.0), -1e30).astype(np.float32)
    c["fb"] = fb.reshape(NT, 128, 32).transpose(1, 0, 2).reshape(128, NT * 32).copy()
    c["vj"] = valid.astype(np.float32).reshape(NT, 128, 32).transpose(1, 0, 2).reshape(128, NT * 32).copy()
    return c


def _rep(v, n=128):
    return np.ascontiguousarray(np.broadcast_to(np.asarray(v, np.float32)[None, :], (n, v.shape[0])))


def prep_inputs(inp):
    f = lambda a: np.ascontiguousarray(np.asarray(a, dtype=np.float32))
    w_in = f(inp["w_in"][0])
    o = np.cumsum([0, 768, 256, 32, 512, 128, 128, 128, 128, 128, 128, 24, 1024, 1024])
    seg = lambda i: w_in[:, o[i]:o[i + 1]]
    shared = {}
    shared["ada_w"] = f(inp["ada_w"][0]); shared["adabB"] = _rep(f(inp["ada_b"][0]))
    shared["g1B"] = _rep(f(inp["norm1_gain"][0])); shared["g2B"] = _rep(f(inp["norm2_gain"][0]))
    shared["w_cq"] = f(seg(0)); shared["w_ckv"] = f(seg(1)); shared["w_kpe"] = f(seg(2))
    qn = seg(3).reshape(D, 8, 64)
    shared["w_qn"] = f(qn[:, PH, :].reshape(D, 512))
    kc = seg(4).reshape(D, 2, 64); vc = seg(5).reshape(D, 2, 64)
    shared["w_kc2"] = f(np.stack([kc[:, 0], kc[:, 0], kc[:, 1], kc[:, 1]], 1).reshape(D, 256))
    shared["w_vc2"] = f(np.stack([vc[:, 0], vc[:, 0], vc[:, 1], vc[:, 1]], 1).reshape(D, 256))
    shared["w_kv4"] = f(np.concatenate([seg(6), seg(8), seg(7), seg(9)], 1))
    gn = seg(10).reshape(D, 8, 3)
    shared["w_gn"] = f(gn[:, PH, :].transpose(0, 2, 1).reshape(D, 24))
    shared["w_gm"] = f(seg(11)); shared["w_gnm"] = f(seg(12))
    shared["qag"] = f(f(inp["mla_q_a_gain"][0]).reshape(6, 128).T); shared["kvag"] = f(f(inp["mla_kv_a_gain"][0]).reshape(2, 128).T)
    shared["w_qb"] = f(inp["mla_w_q_b"][0]); shared["w_kvb"] = f(inp["mla_w_kv_b"][0])
    shared["qgB"] = _rep(f(inp["mla_q_gain"][0])); shared["kgB"] = _rep(f(inp["mla_k_gain"][0]))
    shared["nqB"] = _rep(f(inp["nsa_q_gain"][0])); shared["nkcB"] = _rep(f(inp["nsa_kc_gain"][0]))
    shared["nksB"] = _rep(f(inp["nsa_ks_gain"][0])); shared["nkwB"] = _rep(f(inp["nsa_kw_gain"][0]))
    shared["posk"] = f(f(inp["cmp_pos_k"][0]).reshape(16, 128).T); shared["posv"] = f(f(inp["cmp_pos_v"][0]).reshape(16, 128).T)
    shared["w1k"] = f(inp["cmp_w1_k"][0]); shared["w2k"] = f(inp["cmp_w2_k"][0])
    shared["w1v"] = f(inp["cmp_w1_v"][0]); shared["w2v"] = f(inp["cmp_w2_v"][0])
    shared["wo_mla"] = f(inp["w_o_mla"][0])
    shared["wo_nsa"] = f(f(inp["w_o_nsa"][0]).reshape(8, 64, D)[PH].reshape(512, D))
    shared["w_out"] = f(inp["w_out"][0])
    shared["wg"] = f(inp["ffn_w_gate"][0]); shared["wu"] = f(inp["ffn_w_up"][0]); shared["wd"] = f(inp["ffn_w_down"][0])
    shared.update(_consts())
    maps = []
    xs = np.asarray(inp["x"], np.float32); cs = np.asarray(inp["c"], np.float32); ps = np.asarray(inp["positions"]).astype(np.int32)
    for b in range(xs.shape[0]):
        m = dict(shared)
        m["x"] = np.ascontiguousarray(xs[b])
        m["c_pk"] = np.ascontiguousarray(cs[b].reshape(8, 128).T)
        m["pos_pk"] = np.ascontiguousarray(ps[b].reshape(NT, 128).T)
        m["posC"] = np.ascontiguousarray(ps[b][31::16][:127].reshape(127, 1))
        maps.append(m)
    return maps


_NC_CACHE = {}


def kernel(**inputs):
    maps = prep_inputs(inputs)
    if "nc" not in _NC_CACHE:
        _NC_CACHE["nc"] = build()
    nc = _NC_CACHE["nc"]
    res = run_bass_kernel_spmd(nc, maps, core_ids=list(range(len(maps))))
    return np.stack([np.asarray(r["out"], dtype=np.float32) for r in res.results], 0)
```

```python
import contextlib
import numpy as np
import ml_dtypes
import concourse.bass as bass
import concourse.mybir as mybir
from concourse.bass_utils import run_bass_kernel_spmd

F32 = mybir.dt.float32
BF16 = mybir.dt.bfloat16
I32 = mybir.dt.int32
ALU = mybir.AluOpType
AF = mybir.ActivationFunctionType
AX = mybir.AxisListType

S_ = 2048
D = 1024
NT = 16
DFF = 2816
NFC = 22
EPS = 1e-6
PH = [0, 4, 1, 5, 2, 6, 3, 7]
TWO_PI = float(2 * np.pi)
PI = float(np.pi)


class Sched:
    N_DMA_SLOTS = {"sp": 24, "pool": 8, "act": 4}

    def __init__(self, nc, stack):
        self.nc = nc
        self.E = {"pe": nc.tensor, "act": nc.scalar, "dve": nc.vector, "pool": nc.gpsimd, "sp": nc.sync}
        self.sem, self.cnt = {}, {}
        for e in ("pe", "act", "dve", "pool"):
            self.sem[e] = stack.enter_context(nc.semaphore("s_" + e))
            self.cnt[e] = 0
        self.dsem, self.dcnt, self.dnext = {}, {}, {}
        for q, n in self.N_DMA_SLOTS.items():
            self.dsem[q] = [stack.enter_context(nc.semaphore(f"d_{q}{i}")) for i in range(n)]
            self.dcnt[q] = [0] * n
            self.dnext[q] = 0
        self.seen = {e: {} for e in self.E}
        self.lastw, self.readers = {}, {}
        self.n_wait = 0
        self.n_inst = 0
        self.prog = {}
        self.ninst = {}

    def check_no_deadlock(self):
        val = {}
        ptr = {e: 0 for e in self.prog}
        progress = True
        while progress:
            progress = False
            for e, lst in self.prog.items():
                while ptr[e] < len(lst):
                    it = lst[ptr[e]]
                    if it[0] == "w":
                        if val.get(it[1], 0) < it[2]:
                            break
                    else:
                        val[it[1]] = val.get(it[1], 0) + it[2]
                    ptr[e] += 1
                    progress = True
        stuck = {e: (ptr[e], len(l), l[ptr[e]]) for e, l in self.prog.items() if ptr[e] < len(l)}
        assert not stuck, f"DEADLOCK in emitted program: {stuck}"

    def _sem_of(self, src):
        return self.dsem[src[1]][src[2]] if isinstance(src, tuple) else self.sem[src]

    def _wait(self, e, tok):
        src, val = tok
        if self.seen[e].get(src, 0) >= val:
            return
        self.E[e].wait_ge(self._sem_of(src), val)
        self.seen[e][src] = val
        self.n_wait += 1
        self.prog.setdefault(e, []).append(("w", src, val))

    def _deps(self, e, r, w):
        toks = []
        for k in r:
            t = self.lastw.get(k)
            if t is not None:
                toks.append(t)
        for k in w:
            t = self.lastw.get(k)
            if t is not None:
                toks.append(t)
            for t in self.readers.get(k, ()):
                toks.append(t)
        for t in toks:
            if t[0] == e and e == "pe":
                continue
            self._wait(e, t)

    def _commit(self, tok, r, w):
        for k in r:
            lst = self.readers.setdefault(k, [])
            lst[:] = [t for t in lst if t[0] != tok[0]]
            lst.append(tok)
        for k in w:
            self.lastw[k] = tok
            self.readers[k] = []

    def op(self, e, fn, r=(), w=(), signal=True):
        self._deps(e, r, w)
        ins = fn(self.E[e])
        if signal:
            self.cnt[e] += 1
            ins.then_inc(self.sem[e], 1)
            tok = (e, self.cnt[e])
            self.prog.setdefault(e, []).append(("i", e, 1))
        else:
            tok = (e, self.cnt[e] + 1)
        self._commit(tok, r, w)
        self.n_inst += 1
        self.ninst[e] = self.ninst.get(e, 0) + 1
        return tok

    def dma(self, q, out, in_, r=(), w=(), **kw):
        slot = self.dnext[q]
        self.dnext[q] = (slot + 1) % len(self.dsem[q])
        src = ("d", q, slot)
        if self.dcnt[q][slot] > 0:
            self._wait(q, (src, self.dcnt[q][slot]))
        self._deps(q, r, w)
        ins = self.E[q].dma_start(out=out, in_=in_, **kw)
        self.dcnt[q][slot] += 16
        ins.then_inc(self.dsem[q][slot], 16)
        self.prog.setdefault(q, []).append(("i", src, 16))
        tok = (src, self.dcnt[q][slot])
        self._commit(tok, r, w)
        self.n_inst += 1
        return tok

    def barrier(self):
        for e in ("pe", "act", "dve", "pool", "sp"):
            for e2 in ("pe", "act", "dve", "pool"):
                if self.cnt[e2] and not (e2 == e == "pe"):
                    self._wait(e, (e2, self.cnt[e2]))
            for q in self.dsem:
                for i in range(len(self.dsem[q])):
                    if self.dcnt[q][i]:
                        self._wait(e, (("d", q, i), self.dcnt[q][i]))
        self.lastw.clear()
        self.readers.clear()


class Mem:
    def __init__(self, big, nbytes):
        self.big, self.lo, self.hi, self.n = big, 0, nbytes, nbytes

    def _view(self, off, shape, dt):
        nel = int(np.prod(shape))
        esz = 4 if dt in (F32, I32) else 2
        nb = nel * esz
        ap = self.big[:, off // 2:(off + nb) // 2]
        if esz == 4:
            ap = ap.bitcast(dt)
        if len(shape) == 2:
            ap = ap.rearrange("p (a b) -> p a b", a=shape[0])
        elif len(shape) == 3:
            ap = ap.rearrange("p (a b c) -> p a b c", a=shape[0], b=shape[1])
        return ap

    def lo_alloc(self, shape, dt):
        nb = int(np.prod(shape)) * (4 if dt in (F32, I32) else 2)
        nb = (nb + 63) // 64 * 64
        off = self.lo
        self.lo += nb
        assert self.lo <= self.hi, f"SBUF overflow lo={self.lo} hi={self.hi}"
        return self._view(off, shape, dt)

    def hi_alloc(self, shape, dt):
        nb = int(np.prod(shape)) * (4 if dt in (F32, I32) else 2)
        nb = (nb + 63) // 64 * 64
        self.hi -= nb
        assert self.lo <= self.hi, f"SBUF overflow lo={self.lo} hi={self.hi}"
        return self._view(self.hi, shape, dt)


def build(upto=99, dbg=()):
    nc = bass.Bass("TRN2", target_bir_lowering=False)
    I = {}

    def din(name, shape, dt=F32):
        I[name] = nc.dram_tensor(name, list(shape), dt, kind="ExternalInput").ap()
        return I[name]

    x = din("x", [S_, D]); c_pk = din("c_pk", [128, 8]); pos_pk = din("pos_pk", [128, NT], I32)
    posC = din("posC", [127, 1], I32)
    ada_w = din("ada_w", [D, 6 * D]); adabB = din("adabB", [128, 6 * D]); g1B = din("g1B", [128, D]); g2B = din("g2B", [128, D])
    w_cq = din("w_cq", [D, 768]); w_ckv = din("w_ckv", [D, 256]); w_kpe = din("w_kpe", [D, 32])
    w_qn = din("w_qn", [D, 512]); w_kc2 = din("w_kc2", [D, 256]); w_vc2 = din("w_vc2", [D, 256])
    w_kv4 = din("w_kv4", [D, 512]); w_gn = din("w_gn", [D, 24]); w_gm = din("w_gm", [D, D]); w_gnm = din("w_gnm", [D, D])
    qag = din("qag", [128, 6]); kvag = din("kvag", [128, 2])
    w_qb = din("w_qb", [768, 768]); w_kvb = din("w_kvb", [256, 1024])
    qgB = din("qgB", [128, 96]); kgB = din("kgB", [128, 96])
    nqB = din("nqB", [128, 64]); nkcB = din("nkcB", [128, 64]); nksB = din("nksB", [128, 64]); nkwB = din("nkwB", [128, 64])
    posk = din("posk", [128, 16]); w1k = din("w1k", [2048, 256]); w2k = din("w2k", [256, 64])
    posv = din("posv", [128, 16]); w1v = din("w1v", [2048, 256]); w2v = din("w2v", [256, 64])
    wo_mla = din("wo_mla", [512, D]); wo_nsa = din("wo_nsa", [512, D]); w_out = din("w_out", [D, D])
    wg = din("wg", [D, DFF]); wu = din("wu", [D, DFF]); wd = din("wd", [DFF, D])
    ident_d = din("ident", [128, 128], BF16); tri_d = din("tri", [128, 128], BF16); winm_d = din("winm", [128, 384], BF16)
    inv16_d = din("inv16", [128, 16]); inv8_d = din("inv8", [128, 8])
    XEall_d = din("XEall", [32, S_], BF16); ovl_d = din("ovl", [127, 32], BF16); vmT_d = din("vmT", [127, S_], BF16); XE_d = din("XE", [32, NT * 128], BF16)
    fb_d = din("fb", [128, NT * 32]); vj_d = din("vj", [128, NT * 32])
    out = nc.dram_tensor("out", [S_, D], F32, kind="ExternalOutput").ap()
    hTs = nc.dram_tensor("hTs", [128, 8 * S_], BF16).ap()
    mods = nc.dram_tensor("mods", [128, 6 * D], F32).ap()
    x1s = nc.dram_tensor("x1s", [S_, D], F32).ap()
    D_ = {}
    for name, shape, dt in dbg:
        D_[name] = nc.dram_tensor("dbg_" + name, list(shape), dt, kind="ExternalOutput").ap()

    st = contextlib.ExitStack()
    with st:
        S = Sched(nc, st)
        NB = 204800
        big = st.enter_context(nc.sbuf_tensor("big", [128, NB // 2], BF16))
        M = Mem(big, NB)
        P = [st.enter_context(nc.psum_tensor(f"ps{i}", [128, 512], F32)) for i in range(8)]
        PK = [f"ps{i}" for i in range(8)]
        bank_state = [0]

        nbanks = [8]

        def nb():
            b = bank_state[0] % nbanks[0]
            bank_state[0] = (b + 1) % nbanks[0]
            return b

        def mm(ps_ap, lhsT, rhs, start, stop, r, w, sig=None, **kw):
            S.op("pe", lambda e: e.matmul(ps_ap, lhsT=lhsT, rhs=rhs, start=start, stop=stop, **kw), r=r, w=w, signal=bool(stop) if sig is None else sig)

        def tp(ps_ap, in_, ident_ap, r, w):
            S.op("pe", lambda e: e.transpose(out=ps_ap, in_=in_, identity=ident_ap), r=r, w=w)

        def act(out_, in_, func, r, w, **kw):
            S.op("act", lambda e: e.activation(out=out_, in_=in_, func=func, **kw), r=r, w=w)

        def dbg_out(name, ap, r):
            if name in D_:
                S.dma("sp", D_[name], ap, r=r)

        ident = M.lo_alloc([128], BF16); tri = M.lo_alloc([128], BF16); winm = M.lo_alloc([3, 128], BF16)
        onesb = M.lo_alloc([128], BF16)
        stg = [M.lo_alloc([1024], F32) for _ in range(3)]
        stg_i = [0]
        cosM = M.lo_alloc([NT, 16], F32); sinM = M.lo_alloc([NT, 16], F32)
        cosN = M.lo_alloc([NT, 8], F32); sinN = M.lo_alloc([NT, 8], F32)
        cosC = M.lo_alloc([8], F32); sinC = M.lo_alloc([8], F32)
        lo_pers = M.lo
        omT = M.lo_alloc([4, S_], BF16); onT = M.lo_alloc([4, S_], BF16)
        S.dma("sp", ident, ident_d, w=["ident"])
        S.dma("sp", tri, tri_d, w=["tri"])
        S.dma("sp", winm.rearrange("p a b -> p (a b)"), winm_d, w=["winm"])
        S.op("pool", lambda e: e.memset(onesb, 1.0), w=["onesb"])

        def load_w(dst, W, key, KC, N, ceng="dve"):
            Wv = W.rearrange("(k p) n -> p k n", p=128)
            if N <= 1024:
                g = max(1, min(KC, 1024 // N))
                for k0 in range(0, KC, g):
                    k1 = min(KC, k0 + g)
                    si = stg_i[0]; stg_i[0] = (si + 1) % 3
                    sv = stg[si][:, 0:(k1 - k0) * N].rearrange("p (k n) -> p k n", n=N)
                    S.dma("sp", sv, Wv[:, k0:k1, :], w=[("stg", si)])
                    S.op(ceng, lambda e, sv=sv, k0=k0, k1=k1: e.tensor_copy(out=dst[:, k0:k1, :], in_=sv), r=[("stg", si)], w=[key])
            else:
                for k in range(KC):
                    for c0 in range(0, N, 1024):
                        c1 = min(N, c0 + 1024)
                        si = stg_i[0]; stg_i[0] = (si + 1) % 3
                        sv = stg[si][:, 0:c1 - c0]
                        S.dma("sp", sv, Wv[:, k, c0:c1], w=[("stg", si)])
                        S.op(ceng, lambda e, sv=sv, k=k, c0=c0, c1=c1: e.tensor_copy(out=dst[:, k, c0:c1], in_=sv), r=[("stg", si)], w=[key])

        def load_w_cols(dst, W, key, KC, c0, c1, ceng="dve"):
            Wv = W.rearrange("(k p) n -> p k n", p=128)
            N = c1 - c0
            g = max(1, min(KC, 1024 // N))
            for k0 in range(0, KC, g):
                k1 = min(KC, k0 + g)
                si = stg_i[0]; stg_i[0] = (si + 1) % 3
                sv = stg[si][:, 0:(k1 - k0) * N].rearrange("p (k n) -> p k n", n=N)
                S.dma("sp", sv, Wv[:, k0:k1, c0:c1], w=[("stg", si)])
                S.op(ceng, lambda e, sv=sv, k0=k0, k1=k1: e.tensor_copy(out=dst[:, k0:k1, :], in_=sv), r=[("stg", si)], w=[key])

        def sincos(ang, shape, cos_o, sin_o, np_, tmp_f, tmp_i, tmp_m, key):
            for (shift, dst) in ((0.0, sin_o), (PI / 2, cos_o)):
                S.op("dve", lambda e: e.tensor_scalar(out=tmp_f, in0=ang, scalar1=shift, scalar2=None, op0=ALU.add), r=[key + "ang"], w=[key + "f"])
                S.op("dve", lambda e: e.tensor_scalar(out=tmp_i, in0=tmp_f, scalar1=float(1 / TWO_PI), scalar2=None, op0=ALU.mult), r=[key + "f"], w=[key + "i"])
                S.op("dve", lambda e: e.tensor_copy(out=tmp_m, in_=tmp_i), r=[key + "i"], w=[key + "m"])
                S.op("dve", lambda e: e.scalar_tensor_tensor(out=tmp_f, in0=tmp_m, scalar=-TWO_PI, in1=tmp_f, op0=ALU.mult, op1=ALU.add), r=[key + "m", key + "f"], w=[key + "f"])
                S.op("dve", lambda e: e.tensor_scalar(out=tmp_m, in0=tmp_f, scalar1=PI, scalar2=None, op0=ALU.is_gt), r=[key + "f"], w=[key + "m"])
                S.op("dve", lambda e: e.scalar_tensor_tensor(out=tmp_f, in0=tmp_m, scalar=-TWO_PI, in1=tmp_f, op0=ALU.mult, op1=ALU.add), r=[key + "m", key + "f"], w=[key + "f"])
                S.op("dve", lambda e: e.tensor_scalar(out=tmp_m, in0=tmp_f, scalar1=-PI, scalar2=None, op0=ALU.is_lt), r=[key + "f"], w=[key + "m"])
                S.op("dve", lambda e: e.scalar_tensor_tensor(out=tmp_f, in0=tmp_m, scalar=TWO_PI, in1=tmp_f, op0=ALU.mult, op1=ALU.add), r=[key + "m", key + "f"], w=[key + "f"])
                act(dst, tmp_f, AF.Sin, r=[key + "f"], w=[key + "out"])

        lo0, hi0 = M.lo, M.hi
        if upto <= -2:
            return finish(nc, S, st)
        posi = M.lo_alloc([NT], I32); posf = M.lo_alloc([NT], F32)
        posCi = M.lo_alloc([1], I32); posCf = M.lo_alloc([1], F32)
        inv16 = M.lo_alloc([16], F32); inv8 = M.lo_alloc([8], F32)
        angM = M.lo_alloc([NT, 16], F32); tfM = M.lo_alloc([NT, 16], F32); tiM = M.lo_alloc([NT, 16], I32); tmM = M.lo_alloc([NT, 16], F32)
        S.dma("sp", posi, pos_pk, w=["posi"])
        S.dma("sp", posCi[0:127], posC, w=["posCi"])
        S.dma("sp", inv16, inv16_d, w=["inv16"])
        S.dma("sp", inv8, inv8_d, w=["inv8"])
        S.op("dve", lambda e: e.tensor_copy(out=posf, in_=posi), r=["posi"], w=["posf"])
        S.op("dve", lambda e: e.tensor_copy(out=posCf[0:127], in_=posCi[0:127]), r=["posCi"], w=["posCf"])
        S.op("dve", lambda e: e.tensor_tensor(out=angM, in0=posf.unsqueeze(2).to_broadcast([128, NT, 16]),
                                              in1=inv16.unsqueeze(1).to_broadcast([128, NT, 16]), op=ALU.mult), r=["posf", "inv16"], w=["Mang"])
        sincos(angM, None, cosM, sinM, 128, tfM, tiM, tmM, "M")
        a8 = angM.rearrange("p a b -> p (a b)")[:, 0:NT * 8].rearrange("p (a b) -> p a b", b=8)
        f8 = tfM.rearrange("p a b -> p (a b)")[:, 0:NT * 8].rearrange("p (a b) -> p a b", b=8)
        i8 = tiM.rearrange("p a b -> p (a b)")[:, 0:NT * 8].rearrange("p (a b) -> p a b", b=8)
        m8_ = tmM.rearrange("p a b -> p (a b)")[:, 0:NT * 8].rearrange("p (a b) -> p a b", b=8)
        S.op("dve", lambda e: e.tensor_tensor(out=a8, in0=posf.unsqueeze(2).to_broadcast([128, NT, 8]),
                                              in1=inv8.unsqueeze(1).to_broadcast([128, NT, 8]), op=ALU.mult), r=["posf", "inv8", "Mout", "Mf", "Mm", "Mi"], w=["Nang"])
        sincos(a8, None, cosN, sinN, 128, f8, i8, m8_, "N")
        aC = angM.rearrange("p a b -> p (a b)")[0:127, 0:8]
        fC = tfM.rearrange("p a b -> p (a b)")[0:127, 0:8]
        iC = tiM.rearrange("p a b -> p (a b)")[0:127, 0:8]
        mC = tmM.rearrange("p a b -> p (a b)")[0:127, 0:8]
        S.op("dve", lambda e: e.tensor_scalar(out=aC, in0=inv8[0:127], scalar1=posCf[0:127, 0:1], scalar2=None, op0=ALU.mult),
             r=["posCf", "inv8", "Nout", "Nf", "Nm", "Ni", "Nang"], w=["Cang"])
        sincos(aC, None, cosC[0:127], sinC[0:127], 127, fC, iC, mC, "C")
        dbg_out("cosM", cosM, ["Mout"]); dbg_out("sinM", sinM, ["Mout"])

        if upto <= -1:
            return finish(nc, S, st)
        cpk = M.lo_alloc([8], F32); sc = M.lo_alloc([8], F32)
        sch = M.lo_alloc([8], BF16); scl = M.lo_alloc([8], BF16)
        cBh = M.lo_alloc([8, 128], BF16); cBl = M.lo_alloc([8, 128], BF16)
        modB = M.lo_alloc([6 * D], F32)
        g1t = M.lo_alloc([D], F32); g2t = M.lo_alloc([D], F32)
        awb = [M.hi_alloc([8, 512], F32) for _ in range(3)]
        abb = [M.hi_alloc([512], F32) for _ in range(3)]
        awh = [M.hi_alloc([8, 512], BF16) for _ in range(3)]
        awl = [M.hi_alloc([8, 512], BF16) for _ in range(3)]
        S.dma("sp", cpk, c_pk, w=["cpk"])
        S.dma("sp", g1t, g1B, w=["g1t"]); S.dma("sp", g2t, g2B, w=["g2t"])
        act(sc, cpk, AF.Silu, r=["cpk"], w=["sc"])
        S.op("dve", lambda e: e.tensor_copy(out=sch, in_=sc), r=["sc"], w=["sch"])
        S.op("dve", lambda e: e.tensor_tensor(out=scl, in0=sc, in1=sch, op=ALU.subtract), r=["sc", "sch"], w=["scl"])
        for k in range(8):
            S.op("dve", lambda e, k=k: e.tensor_copy(out=cBh[:, k, :], in_=sch[:, k:k + 1].to_broadcast([128, 128])), r=["sch"], w=["cBh"])
            S.op("dve", lambda e, k=k: e.tensor_copy(out=cBl[:, k, :], in_=scl[:, k:k + 1].to_broadcast([128, 128])), r=["scl"], w=["cBl"])
        awv = ada_w.rearrange("(k p) n -> p k n", p=128)
        for n in range(12):
            q_ = n % 3
            S.dma("sp", awb[q_], awv[:, :, n * 512:(n + 1) * 512], w=[("awb", q_)])
            S.dma("sp", abb[q_], adabB[:, n * 512:(n + 1) * 512], w=[("abb", q_)])
            act(awh[q_], awb[q_], AF.Copy, r=[("awb", q_)], w=[("awh", q_)])
            S.op("dve", lambda e, q_=q_: e.tensor_tensor(out=awl[q_], in0=awb[q_], in1=awh[q_], op=ALU.subtract), r=[("awb", q_), ("awh", q_)], w=[("awl", q_)])
            b = nb()
            passes = [(cBh, "cBh", awh, "awh"), (cBh, "cBh", awl, "awl"), (cBl, "cBl", awh, "awh")]
            for pi_, (cb_, ck, ww, wk) in enumerate(passes):
                for k in range(8):
                    mm(P[b][:, :], cb_[:, k, :], ww[q_][:, k, :], pi_ == 0 and k == 0, pi_ == 2 and k == 7, r=[ck, (wk, q_)], w=[PK[b]])
            S.op("dve", lambda e, n=n, b=b, q_=q_: e.tensor_tensor(out=modB[:, n * 512:(n + 1) * 512], in0=P[b][:, :], in1=abb[q_], op=ALU.add),
                 r=[PK[b], ("abb", q_)], w=["modB"])
        S.op("dve", lambda e: e.scalar_tensor_tensor(out=modB[:, D:2 * D], in0=modB[:, D:2 * D], scalar=1.0, in1=g1t, op0=ALU.add, op1=ALU.mult), r=["modB", "g1t"], w=["modB"])
        S.op("dve", lambda e: e.scalar_tensor_tensor(out=modB[:, 4 * D:5 * D], in0=modB[:, 4 * D:5 * D], scalar=1.0, in1=g2t, op0=ALU.add, op1=ALU.mult), r=["modB", "g2t"], w=["modB"])
        S.dma("sp", mods, modB, r=["modB"], w=["mods"])
        dbg_out("modB", modB, ["modB"])
        B1 = modB[:, 0:D]; A1 = modB[:, D:2 * D]
        if upto <= 0:
            return finish(nc, S, st)

        M.hi = hi0
        hT = M.hi_alloc([8, S_], BF16)
        xb = [M.lo_alloc([D], F32) for _ in range(2)]
        junk = M.lo_alloc([D], F32); tmpA = M.lo_alloc([D], F32)
        hb = [M.lo_alloc([D], BF16) for _ in range(2)]
        ssq = M.lo_alloc([NT], F32); rs = M.lo_alloc([NT], F32)

        tmpA2 = [tmpA, M.lo_alloc([D], F32)]

        def norm_a(i, xt, xkey):
            act(junk, xt, AF.Square, r=[xkey], w=["junk", ("ssq", i)], accum_out=ssq[:, i:i + 1])
            act(rs[:, i:i + 1], ssq[:, i:i + 1], AF.Sqrt, r=[("ssq", i)], w=[("rs", i)], scale=1.0 / D, bias=EPS)
            S.op("dve", lambda e: e.reciprocal(out=rs[:, i:i + 1], in_=rs[:, i:i + 1]), r=[("rs", i)], w=[("rs", i)])

        def norm_b1(i, xt, xkey, A, B, Akeys):
            par = i % 2
            S.op("dve", lambda e: e.scalar_tensor_tensor(out=tmpA2[par], in0=xt, scalar=rs[:, i:i + 1], in1=A, op0=ALU.mult, op1=ALU.mult),
                 r=[xkey, ("rs", i)] + Akeys, w=[("tmpA2", par)])
            S.op("pool", lambda e: e.tensor_tensor(out=hb[par], in0=tmpA2[par], in1=B, op=ALU.add), r=[("tmpA2", par)] + Akeys, w=[("hb", par)])
            b = nb()
            pb = P[b][:, :].bitcast(BF16)
            for k in range(8):
                tp(pb[:, k * 128:(k + 1) * 128], hb[par][:, k * 128:(k + 1) * 128], ident, r=[("hb", par), "ident"], w=[PK[b]])
            return b

        def norm_b2(i, b, dstT, dkey):
            pb = P[b][:, :].bitcast(BF16)
            act(dstT[:, :, i * 128:(i + 1) * 128], pb.rearrange("p (k t) -> p k t", k=8), AF.Copy, r=[PK[b]], w=[(dkey, i)])

        xb = xb + [M.lo_alloc([D], F32), M.lo_alloc([D], F32)]

        def a_front(i):
            S.dma("sp", xb[i % 4], x[i * 128:(i + 1) * 128, :], w=[("xb", i % 4)])
            norm_a(i, xb[i % 4], ("xb", i % 4))

        a_front(0); a_front(1)
        for i in range(NT):
            b_ = norm_b1(i, xb[i % 4], ("xb", i % 4), A1, B1, ["modB"])
            if i + 2 < NT:
                a_front(i + 2)
            norm_b2(i, b_, hT, "hT")
        hTk = [("hT", i) for i in range(NT)]
        S.dma("sp", hTs, hT.rearrange("p k t -> p (k t)"), r=hTk, w=["hTs"])
        dbg_out("hT", hT.rearrange("p k t -> p (k t)"), hTk)
        if upto <= 1:
            return finish(nc, S, st)
        S.barrier()

        nbanks[0] = 6
        M.lo = lo0
        cqT = M.lo_alloc([6, S_], BF16); ckvT = M.lo_alloc([2, S_], BF16); kpe = M.lo_alloc([NT, 32], F32)
        lo1 = M.lo
        Wcq = M.lo_alloc([8, 768], BF16); Wckv = M.lo_alloc([8, 256], BF16); Wkpe = M.lo_alloc([8, 32], BF16)
        qagt = M.lo_alloc([6], F32); kvagt = M.lo_alloc([2], F32)
        sqb = [M.lo_alloc([512], BF16) for _ in range(2)]
        rb = M.lo_alloc([512], F32)
        S.dma("sp", qagt, qag, w=["qagt"]); S.dma("sp", kvagt, kvag, w=["kvagt"])
        load_w(Wcq, w_cq, "Wcq", 8, 768); load_w(Wckv, w_ckv, "Wckv", 8, 256); load_w(Wkpe, w_kpe, "Wkpe", 8, 32)

        def fm_proj_norm(dstT, dkey, Wt, wkey, nf, gaint, gkey, nfeat):
            for c in range(4):
                hk = [("hT", 4 * c + q) for q in range(4)]
                for j in range(nf + 1):
                    if j < nf:
                        b = nb()
                        for k in range(8):
                            mm(P[b][:, :], Wt[:, k, j * 128:(j + 1) * 128], hT[:, k, c * 512:(c + 1) * 512], k == 0, k == 7, r=[wkey] + hk, w=[PK[b]])
                    if j >= 1:
                        jj = j - 1
                        mm(P[6][:, :], onesb, sqb[jj % 2], jj == 0, jj == nf - 1, r=["onesb", ("sqb", jj % 2)], w=[PK[6]], sig=True)
                    if j < nf:
                        act(sqb[j % 2], P[b][:, :], AF.Square, r=[PK[b]], w=[("sqb", j % 2)])
                        act(dstT[:, j, c * 512:(c + 1) * 512], P[b][:, :], AF.Copy, r=[PK[b]], w=[(dkey, c)])
                act(rb, P[6][:, :], AF.Sqrt, r=[PK[6]], w=["rb"], scale=1.0 / nfeat, bias=EPS)
                S.op("dve", lambda e: e.reciprocal(out=rb, in_=rb), r=["rb"], w=["rb"])
                for j in range(nf):
                    S.op("dve", lambda e, j=j, c=c: e.scalar_tensor_tensor(out=dstT[:, j, c * 512:(c + 1) * 512], in0=dstT[:, j, c * 512:(c + 1) * 512],
                                                                             scalar=gaint[:, j:j + 1], in1=rb, op0=ALU.mult, op1=ALU.mult),
                         r=[(dkey, c), "rb", gkey], w=[(dkey, c)])

        fm_proj_norm(cqT, "cqT", Wcq, "Wcq", 6, qagt, "qagt", 768)
        fm_proj_norm(ckvT, "ckvT", Wckv, "Wckv", 2, kvagt, "kvagt", 256)
        for i in range(NT):
            b = nb()
            for k in range(8):
                mm(P[b][:, 0:32], hT[:, k, i * 128:(i + 1) * 128], Wkpe[:, k, :], k == 0, k == 7, r=["Wkpe", ("hT", i)], w=[PK[b]])
            S.op("dve", lambda e, i=i, b=b: e.tensor_copy(out=kpe[:, i, :], in_=P[b][:, 0:32]), r=[PK[b]], w=[("kpe", i)])
        cqk = [("cqT", c) for c in range(4)]
        dbg_out("cqT", cqT.rearrange("p k t -> p (k t)"), cqk)
        dbg_out("kpe", kpe.rearrange("p a b -> p (a b)"), [("kpe", i) for i in range(NT)])
        if upto <= 2:
            return finish(nc, S, st)
        S.barrier()

        nbanks[0] = 8
        M.lo = lo1
        M.hi = hi0
        QT = M.hi_alloc([8, S_], BF16); KT = M.hi_alloc([8, S_], BF16); V = M.hi_alloc([NT, 8, 65], BF16)
        hi1 = M.hi
        Wqb = M.lo_alloc([6, 768], BF16); Wkvb = M.lo_alloc([2, 1024], BF16)
        qgt = M.lo_alloc([96], F32); kgt = M.lo_alloc([96], F32)
        drq = [M.lo_alloc([768], BF16) for _ in range(2)]; drk = [M.lo_alloc([768], BF16) for _ in range(2)]
        S.dma("sp", qgt, qgB, w=["qgt"]); S.dma("sp", kgt, kgB, w=["kgt"])
        load_w(Wqb, w_qb, "Wqb", 6, 768); load_w(Wkvb, w_kvb, "Wkvb", 2, 1024)
        S.op("pool", lambda e: e.memset(V[:, :, :, 64:65], 1.0), w=["Vones"])

        def mk_tmps(Mx, n, H, hf):
            return dict(t1=Mx.lo_alloc([n], F32), hs=Mx.lo_alloc([H], F32), hr=Mx.lo_alloc([H], F32),
                        ra=Mx.lo_alloc([H * hf], F32), rb=Mx.lo_alloc([H * hf], F32), ra2=Mx.lo_alloc([H * hf], F32), rb2=Mx.lo_alloc([H * hf], F32))

        def hnr_stages(tag, T, src, skeys, H, Dh, gaint, gkey, ro, hf, cos_, sin_, dst, dkey, np_=128):
            n = H * Dh
            t1v = T["t1"][0:np_, 0:n].rearrange("p (h d) -> p h d", h=H)
            hs = T["hs"][0:np_, 0:H]; hr = T["hr"][0:np_, 0:H]
            x1 = t1v[:, :, ro:ro + hf]; x2 = t1v[:, :, ro + hf:ro + 2 * hf]
            cb = cos_.unsqueeze(1).to_broadcast([np_, H, hf]); sb_ = sin_.unsqueeze(1).to_broadcast([np_, H, hf])
            rv = {k: T[k][0:np_, 0:H * hf].rearrange("p (h d) -> p h d", h=H) for k in ("ra", "rb", "ra2", "rb2")}
            tr = ["Mout", "Nout", "Cout"]
            k_ = lambda nm: (tag, nm)
            st = []
            st.append(lambda: act(t1v, src, AF.Square, r=skeys, w=[k_("t1")]))
            st.append(lambda: S.op("dve", lambda e: e.tensor_reduce(out=hs, in_=t1v, axis=AX.X, op=ALU.add), r=[k_("t1")], w=[k_("hs")]))
            st.append(lambda: act(hr, hs, AF.Sqrt, r=[k_("hs")], w=[k_("hr")], scale=1.0 / Dh, bias=EPS))
            st.append(lambda: S.op("dve", lambda e: e.reciprocal(out=hr, in_=hr), r=[k_("hr")], w=[k_("hr")]))
            st.append(lambda: S.op("dve", lambda e: e.tensor_tensor(out=t1v, in0=src, in1=hr.unsqueeze(2).to_broadcast([np_, H, Dh]), op=ALU.mult), r=skeys + [k_("hr"), k_("hs")], w=[k_("t1")]))
            st.append(lambda: S.op("dve", lambda e: e.tensor_tensor(out=t1v, in0=t1v, in1=gaint[0:np_].unsqueeze(1).to_broadcast([np_, H, Dh]), op=ALU.mult), r=[k_("t1"), gkey], w=[k_("t1")]))
            st.append(lambda: S.op("dve", lambda e: e.tensor_tensor(out=rv["ra"], in0=x1, in1=cb, op=ALU.mult), r=[k_("t1")] + tr, w=[k_("ra")]))
            st.append(lambda: S.op("dve", lambda e: e.tensor_tensor(out=rv["rb"], in0=x2, in1=sb_, op=ALU.mult), r=[k_("t1")] + tr, w=[k_("rb")]))
            st.append(lambda: S.op("dve", lambda e: e.tensor_tensor(out=dst[:, :, ro:ro + hf], in0=rv["ra"], in1=rv["rb"], op=ALU.subtract), r=[k_("ra"), k_("rb")], w=[dkey]))
            st.append(lambda: S.op("dve", lambda e: e.tensor_tensor(out=rv["ra2"], in0=x2, in1=cb, op=ALU.mult), r=[k_("t1")] + tr, w=[k_("ra2")]))
            st.append(lambda: S.op("dve", lambda e: e.tensor_tensor(out=rv["rb2"], in0=x1, in1=sb_, op=ALU.mult), r=[k_("t1")] + tr, w=[k_("rb2")]))
            st.append(lambda: S.op("dve", lambda e: e.tensor_tensor(out=dst[:, :, ro + hf:ro + 2 * hf], in0=rv["ra2"], in1=rv["rb2"], op=ALU.add), r=[k_("ra2"), k_("rb2")], w=[dkey]))

            def copies():
                if ro > 0:
                    S.op("pool", lambda e: e.tensor_copy(out=dst[:, :, 0:ro], in_=t1v[:, :, 0:ro]), r=[k_("t1")], w=[dkey])
                if ro + 2 * hf < Dh:
                    S.op("pool", lambda e: e.tensor_copy(out=dst[:, :, ro + 2 * hf:Dh], in_=t1v[:, :, ro + 2 * hf:Dh]), r=[k_("t1")], w=[dkey])
            st.insert(6, copies)
            return st

        def run_interleaved(chains):
            for s_ in range(max(len(c_) for c_ in chains)):
                for c_ in chains:
                    if s_ < len(c_):
                        c_[s_]()

        def head_norm_rope(src, skeys, H, Dh, gaint, gkey, ro, hf, cos_, sin_, dst, dkey, np_=128):
            n = H * Dh
            t1v = t1[0:np_, 0:n].rearrange("p (h d) -> p h d", h=H)
            t2v = t2[0:np_, 0:n].rearrange("p (h d) -> p h d", h=H)
            hs = hss[0:np_, 0:H]; hr = hrs[0:np_, 0:H]
            act(t1v, src, AF.Square, r=skeys, w=["t1"])
            S.op("dve", lambda e: e.tensor_reduce(out=hs, in_=t1v, axis=AX.X, op=ALU.add), r=["t1"], w=["hss"])
            act(hr, hs, AF.Sqrt, r=["hss"], w=["hrs"], scale=1.0 / Dh, bias=EPS)
            S.op("dve", lambda e: e.reciprocal(out=hr, in_=hr), r=["hrs"], w=["hrs"])
            S.op("dve", lambda e: e.tensor_tensor(out=t2v, in0=src, in1=hr.unsqueeze(2).to_broadcast([np_, H, Dh]), op=ALU.mult), r=skeys + ["hrs"], w=["t2"])
            S.op("dve", lambda e: e.tensor_tensor(out=t1v, in0=t2v, in1=gaint[0:np_].unsqueeze(1).to_broadcast([np_, H, Dh]), op=ALU.mult), r=["t2", gkey], w=["t1"])
            x1 = t1v[:, :, ro:ro + hf]; x2 = t1v[:, :, ro + hf:ro + 2 * hf]
            cb = cos_.unsqueeze(1).to_broadcast([np_, H, hf]); sb_ = sin_.unsqueeze(1).to_broadcast([np_, H, hf])
            rav = ra[0:np_, 0:H * hf].rearrange("p (h d) -> p h d", h=H)
            rbv = rbb[0:np_, 0:H * hf].rearrange("p (h d) -> p h d", h=H)
            tr = ["Mout", "Nout", "Cout"]
            S.op("dve", lambda e: e.tensor_tensor(out=rav, in0=x1, in1=cb, op=ALU.mult), r=["t1"] + tr, w=["ra"])
            S.op("dve", lambda e: e.tensor_tensor(out=rbv, in0=x2, in1=sb_, op=ALU.mult), r=["t1"] + tr, w=["rbb"])
            S.op("dve", lambda e: e.tensor_tensor(out=dst[:, :, ro:ro + hf], in0=rav, in1=rbv, op=ALU.subtract), r=["ra", "rbb"], w=[dkey])
            S.op("dve", lambda e: e.tensor_tensor(out=rav, in0=x2, in1=cb, op=ALU.mult), r=["t1"] + tr, w=["ra"])
            S.op("dve", lambda e: e.tensor_tensor(out=rbv, in0=x1, in1=sb_, op=ALU.mult), r=["t1"] + tr, w=["rbb"])
            S.op("dve", lambda e: e.tensor_tensor(out=dst[:, :, ro + hf:ro + 2 * hf], in0=rav, in1=rbv, op=ALU.add), r=["ra", "rbb"], w=[dkey])
            if ro > 0:
                S.op("pool", lambda e: e.tensor_copy(out=dst[:, :, 0:ro], in_=t1v[:, :, 0:ro]), r=["t1"], w=[dkey])
            if ro + 2 * hf < Dh:
                S.op("pool", lambda e: e.tensor_copy(out=dst[:, :, ro + 2 * hf:Dh], in_=t1v[:, :, ro + 2 * hf:Dh]), r=["t1"], w=[dkey])

        Msub = Mem(big, NB); Msub.lo = lo_pers; Msub.hi = lo0
        rawq = [Msub.lo_alloc([768], F32) for _ in range(2)]; rawk = [Msub.lo_alloc([768], F32), M.lo_alloc([768], F32)]
        Tq = [mk_tmps(Msub, 768, 8, 16) for _ in range(2)]; Tk = [mk_tmps(Msub, 768, 8, 16) for _ in range(2)]

        def b2_front(i):
            ts = slice(i * 128, (i + 1) * 128)
            par = i % 2
            bA, bB = nb(), nb()
            for k in range(6):
                mm(P[bA][:, :], cqT[:, k, ts], Wqb[:, k, 0:512], k == 0, k == 5, r=["Wqb", ("cqT", i // 4)], w=[PK[bA]])
            for k in range(6):
                mm(P[bB][:, 0:256], cqT[:, k, ts], Wqb[:, k, 512:768], k == 0, k == 5, r=["Wqb", ("cqT", i // 4)], w=[PK[bB]])
            act(rawq[par][:, 0:512], P[bA][:, :], AF.Copy, r=[PK[bA]], w=[("rawq", par)])
            act(rawq[par][:, 512:768], P[bB][:, 0:256], AF.Copy, r=[PK[bB]], w=[("rawq", par)])
            bA, bB = nb(), nb()
            for hh, bb in ((0, bA), (1, bB)):
                for k in range(2):
                    mm(P[bb][:, :], ckvT[:, k, ts], Wkvb[:, k, hh * 512:(hh + 1) * 512], k == 0, k == 1, r=["Wkvb", ("ckvT", i // 4)], w=[PK[bb]])
            rv = rawk[par].rearrange("p (h d) -> p h d", h=8)
            for hh, bb in ((0, bA), (1, bB)):
                pv = P[bb][:, :].rearrange("p (h d) -> p h d", h=4)
                act(rv[:, hh * 4:(hh + 1) * 4, 0:64], pv[:, :, 0:64], AF.Copy, r=[PK[bb]], w=[("rawk", par)])
                act(V[:, i, hh * 4:(hh + 1) * 4, 0:64], pv[:, :, 64:128], AF.Copy, r=[PK[bb]], w=[("V", i)])
            S.op("pool", lambda e, i=i: e.tensor_copy(out=rv[:, :, 64:96], in_=kpe[:, i, :].unsqueeze(1).to_broadcast([128, 8, 32])), r=[("kpe", i)], w=[("rawk", par)])

        def b2_chains(i):
            par = i % 2
            dq = drq[par].rearrange("p (h d) -> p h d", h=8); dk = drk[par].rearrange("p (h d) -> p h d", h=8)
            cq_ = hnr_stages(("cq", par), Tq[par], rawq[par].rearrange("p (h d) -> p h d", h=8), [("rawq", par)], 8, 96, qgt, "qgt", 64, 16, cosM[:, i, :], sinM[:, i, :], dq, ("drq", par))
            ck_ = hnr_stages(("ck", par), Tk[par], rawk[par].rearrange("p (h d) -> p h d", h=8), [("rawk", par)], 8, 96, kgt, "kgt", 64, 16, cosM[:, i, :], sinM[:, i, :], dk, ("drk", par))
            return [cq_[:6], ck_[:6]], [cq_[6:], ck_[6:]]

        def b2_out(i):
            ts = slice(i * 128, (i + 1) * 128)
            par = i % 2
            dq = drq[par].rearrange("p (h d) -> p h d", h=8); dk = drk[par].rearrange("p (h d) -> p h d", h=8)
            for (dd, dkey_, dstT, okey) in ((dq, ("drq", par), QT, "QT"), (dk, ("drk", par), KT, "KT")):
                b = nb(); pb = P[b][:, :].bitcast(BF16)
                for h in range(8):
                    tp(pb[0:96, h * 128:(h + 1) * 128], dd[:, h, :], ident, r=[dkey_, "ident"], w=[PK[b]])
                act(dstT[0:96, :, ts], pb[0:96, :].rearrange("p (h t) -> p h t", h=8), AF.Copy, r=[PK[b]], w=[(okey, i)])

        b2_front(0); b2_front(1)
        h1_, h2_ = b2_chains(0)
        run_interleaved(h1_)
        for i in range(NT):
            if i + 2 < NT:
                b2_front(i + 2)
            nxt = b2_chains(i + 1) if i + 1 < NT else ([], [])
            run_interleaved(nxt[0] + h2_)
            h2_ = nxt[1]
            if i >= 1:
                b2_out(i - 1)
        b2_out(NT - 1)
        QTk = [("QT", i) for i in range(NT)]
        dbg_out("QT", QT[0:96].rearrange("p k t -> p (k t)"), QTk)
        dbg_out("KT", KT[0:96].rearrange("p k t -> p (k t)"), [("KT", i) for i in range(NT)])
        dbg_out("V", V.rearrange("p a b c -> p (a b c)"), [("V", i) for i in range(NT)] + ["Vones"])
        if upto <= 3:
            return finish(nc, S, st)
        S.barrier()

        nbanks[0] = 6
        M.lo = lo0
        om = M.lo_alloc([NT, 512], BF16)
        PT = [M.lo_alloc([512], BF16) for _ in range(4)]
        rec4 = [M.lo_alloc([4], F32) for _ in range(2)]
        pt_i = [0]

        gchunk = [0]

        def causal_attn_multi(jobs):
            steps = []
            for ji in range(len(jobs)):
                for c in range(4):
                    for kt in range(4 * c + 4):
                        steps.append((ji, c, kt))

            def emit_qk(step):
                ji, c, kt = step
                J = jobs[ji]
                q0 = max(kt - 4 * c, 0)
                n = 512 - 128 * q0
                b = nb()
                has_extra = J["extra"] is not None
                mm(P[b][:, 0:n], J["KT"][0:J["kn"], kt * 128:(kt + 1) * 128], J["QT"](c * 512 + q0 * 128, (c + 1) * 512),
                   True, not has_extra, r=J["kk"](kt) + J["qk"](c), w=[PK[b]])
                if has_extra:
                    J["extra"](P[b][:, 0:n], kt, c * 512 + q0 * 128, (c + 1) * 512, PK[b])
                return b, n, q0

            LA = 2
            pend = [emit_qk(steps[q]) for q in range(min(LA, len(steps)))]
            for si, (ji, c, kt) in enumerate(steps):
                J = jobs[ji]
                b, n, q0 = pend.pop(0)
                if si + LA < len(steps):
                    pend.append(emit_qk(steps[si + LA]))
                if kt == 0:
                    gchunk[0] += 1
                ab = 6 + (gchunk[0] % 2)
                Oacc = P[ab][:, 0:260].rearrange("p (q d) -> p q d", q=4)
                pi = pt_i[0]; pt_i[0] = (pi + 1) % len(PT)
                pt = PT[pi]
                act(pt[:, 0:n], P[b][:, 0:n], AF.Exp, r=[PK[b]], w=[("PT", pi)], scale=J["scale"])
                if kt >= 4 * c:
                    S.op("dve", lambda e, pt=pt: e.tensor_tensor(out=pt[:, 0:128], in0=pt[:, 0:128], in1=tri, op=ALU.mult), r=[("PT", pi), "tri"], w=[("PT", pi)])
                for qi in range(q0, 4):
                    mm(Oacc[:, qi, :], pt[:, (qi - q0) * 128:(qi - q0 + 1) * 128], J["V"](kt), kt == 0 and qi == 0, kt == 4 * c + qi,
                       r=[("PT", pi)] + J["vk"](kt), w=[PK[ab]], skip_group_check=True)
                if kt == 4 * c + 3:
                    J["fin"](c, Oacc, PK[ab])

        jobs = []
        for h in range(8):
            def fin(c, Oacc, pk, h=h):
                rc = rec4[c % 2]
                S.op("dve", lambda e: e.reciprocal(out=rc, in_=Oacc[:, :, 64]), r=[pk], w=[("rec4", c % 2)])
                S.op("dve", lambda e: e.tensor_tensor(out=om[:, 4 * c:4 * c + 4, h * 64:(h + 1) * 64], in0=Oacc[:, :, 0:64],
                                                      in1=rc.unsqueeze(2).to_broadcast([128, 4, 64]), op=ALU.mult),
                     r=[pk, ("rec4", c % 2)], w=[("om", c)])
            jobs.append(dict(KT=KT[:, h, :], kn=96, QT=(lambda a, b_, h=h: QT[0:96, h, a:b_]), V=(lambda kt, h=h: V[:, kt, h, :]), scale=96 ** -0.5,
                             extra=None, fin=fin, qk=(lambda c: [("QT", 4 * c + q) for q in range(4)]), kk=(lambda kt: [("KT", kt)]),
                             vk=(lambda kt: [("V", kt), "Vones"])))
        causal_attn_multi(jobs)
        for i in range(NT):
            b = nb(); pb = P[b][:, :].bitcast(BF16)
            for j in range(4):
                tp(pb[:, j * 128:(j + 1) * 128], om[:, i, j * 128:(j + 1) * 128], ident, r=[("om", i // 4), "ident"], w=[PK[b]])
            act(omT[:, :, i * 128:(i + 1) * 128], pb[:, 0:512].rearrange("p (k t) -> p k t", k=4), AF.Copy, r=[PK[b]], w=[("omT", i)])
        dbg_out("om", om.rearrange("p a b -> p (a b)"), [("om", c) for c in range(4)])
        if upto <= 4:
            return finish(nc, S, st)
        S.barrier()

        nbanks[0] = 4
        M.lo = lo0; M.hi = hi0
        qnT = M.lo_alloc([8, S_], BF16); ksT = M.lo_alloc([2, S_], BF16); kwT = M.lo_alloc([2, S_], BF16)
        vs = M.lo_alloc([NT, 2, 65], BF16); vw = M.lo_alloc([NT, 2, 65], BF16)
        gates = M.lo_alloc([NT, 3, 8], F32)
        kcmpT = M.lo_alloc([2, 128], BF16); VCX = M.lo_alloc([2, 97], BF16)
        PT = [M.lo_alloc([512], BF16) for _ in range(4)]
        rec4 = [M.lo_alloc([4], F32) for _ in range(2)]
        t1 = M.lo_alloc([512], F32); t2 = M.lo_alloc([512], F32)
        hss = M.lo_alloc([8], F32); hrs = M.lo_alloc([8], F32)
        ra = M.lo_alloc([128], F32); rbb = M.lo_alloc([128], F32)
        drb = [M.lo_alloc([512], BF16) for _ in range(2)]
        nqt = M.lo_alloc([64], F32); nkct = M.lo_alloc([64], F32); nkst = M.lo_alloc([64], F32); nkwt = M.lo_alloc([64], F32)
        lo2 = M.lo
        kc2 = M.hi_alloc([2, S_], BF16); vc2 = M.hi_alloc([2, S_], BF16)
        hi_kv = M.hi
        hT = M.hi_alloc([8, S_], BF16)
        Wqn = M.hi_alloc([8, 512], BF16); Wkc2 = M.hi_alloc([8, 256], BF16); Wvc2 = M.hi_alloc([8, 256], BF16)
        Wkv4 = M.hi_alloc([8, 512], BF16); Wgn = M.hi_alloc([8, 24], BF16)
        ge = M.hi_alloc([24], F32)
        S.dma("sp", hT.rearrange("p k t -> p (k t)"), hTs, r=["hTs"], w=["hTall"])
        for t_, d_ in ((nqt, nqB), (nkct, nkcB), (nkst, nksB), (nkwt, nkwB)):
            S.dma("sp", t_, d_, w=["ngain"])
        load_w(Wqn, w_qn, "Wqn", 8, 512); load_w(Wkv4, w_kv4, "Wkv4", 8, 512); load_w(Wgn, w_gn, "Wgn", 8, 24)
        load_w(Wkc2, w_kc2, "Wkc2", 8, 256); load_w(Wvc2, w_vc2, "Wvc2", 8, 256)
        for g_ in range(2):
            S.dma("sp", ksT[64:96, g_, :], XEall_d, w=["ksTx"])
        S.op("pool", lambda e: e.memset(vs[:, :, :, 64:65], 1.0), w=["vsones"])
        S.op("pool", lambda e: e.memset(vw[:, :, :, 64:65], 1.0), w=["vwones"])
        S.op("pool", lambda e: e.memset(kc2[64:128, :, S_ - 1:S_], 0.0), w=["kc2pad"])
        S.op("pool", lambda e: e.memset(vc2[64:128, :, S_ - 1:S_], 0.0), w=["vc2pad"])
        MsubD = Mem(big, NB); MsubD.lo = lo_pers + 16384; MsubD.hi = lo0
        TDq = [mk_tmps(MsubD, 512, 8, 8) for _ in range(2)]; TDs = [mk_tmps(MsubD, 128, 2, 8) for _ in range(2)]; TDw = [mk_tmps(MsubD, 128, 2, 8) for _ in range(2)]
        dnq = [MsubD.lo_alloc([512], BF16) for _ in range(2)]
        dns = [MsubD.lo_alloc([128], BF16) for _ in range(2)]; dnw = [MsubD.lo_alloc([128], BF16) for _ in range(2)]

        def d_front(i):
            ts = slice(i * 128, (i + 1) * 128)
            bq = 4 + 2 * (i % 2)
            for k in range(8):
                mm(P[bq][:, :], hT[:, k, ts], Wqn[:, k, :], k == 0, k == 7, r=["hTall", "Wqn"], w=[PK[bq]])
            bk = 5 + 2 * (i % 2)
            for k in range(8):
                mm(P[bk][:, :], hT[:, k, ts], Wkv4[:, k, :], k == 0, k == 7, r=["hTall", "Wkv4"], w=[PK[bk]])
            bg = nb()
            for k in range(8):
                mm(P[bg][:, 0:24], hT[:, k, ts], Wgn[:, k, :], k == 0, k == 7, r=["hTall", "Wgn"], w=[PK[bg]])
            act(vs[:, i, :, 0:64], P[bk][:, 256:384].rearrange("p (g d) -> p g d", g=2), AF.Copy, r=[PK[bk]], w=[("vs", i)])
            act(vw[:, i, :, 0:64], P[bk][:, 384:512].rearrange("p (g d) -> p g d", g=2), AF.Copy, r=[PK[bk]], w=[("vw", i)])
            act(ge, P[bg][:, 0:24], AF.Exp, r=[PK[bg]], w=["ge"], scale=-1.0)
            S.op("dve", lambda e: e.tensor_scalar(out=ge, in0=ge, scalar1=1.0, scalar2=None, op0=ALU.add), r=["ge"], w=["ge"])
            S.op("dve", lambda e, i=i: e.reciprocal(out=gates[:, i].rearrange("p a b -> p (a b)"), in_=ge), r=["ge"], w=[("gates", i)])
            return bq, bk

        def d_chains(i, bq, bk):
            par = i % 2
            dq = dnq[par].rearrange("p (h d) -> p h d", h=8)
            ds_ = dns[par].rearrange("p (h d) -> p h d", h=2); dw_ = dnw[par].rearrange("p (h d) -> p h d", h=2)
            c1 = hnr_stages(("dq", par), TDq[par], P[bq][:, :].rearrange("p (h d) -> p h d", h=8), [PK[bq]], 8, 64, nqt, "ngain", 0, 8, cosN[:, i, :], sinN[:, i, :], dq, ("dnq", par))
            c2 = hnr_stages(("ds", par), TDs[par], P[bk][:, 0:128].rearrange("p (h d) -> p h d", h=2), [PK[bk]], 2, 64, nkst, "ngain", 0, 8, cosN[:, i, :], sinN[:, i, :], ds_, ("dns", par))
            c3 = hnr_stages(("dw", par), TDw[par], P[bk][:, 128:256].rearrange("p (h d) -> p h d", h=2), [PK[bk]], 2, 64, nkwt, "ngain", 0, 8, cosN[:, i, :], sinN[:, i, :], dw_, ("dnw", par))
            return [c1[:6], c2[:6], c3[:6]], [c1[6:], c2[6:], c3[6:]]

        def d_out(i):
            ts = slice(i * 128, (i + 1) * 128)
            par = i % 2
            b = nb(); pb = P[b][:, :].bitcast(BF16)
            for p_ in range(8):
                tp(pb[0:64, p_ * 128:(p_ + 1) * 128], dnq[par][:, p_ * 64:(p_ + 1) * 64], ident, r=[("dnq", par), "ident"], w=[PK[b]])
            act(qnT[0:64, :, ts], pb[0:64, :].rearrange("p (k t) -> p k t", k=8), AF.Copy, r=[PK[b]], w=[("qnT", i)])
            for (dd, dkey_, dstT, dk) in ((dns[par], ("dns", par), ksT, "ksT"), (dnw[par], ("dnw", par), kwT, "kwT")):
                b2 = nb(); pb = P[b2][:, :].bitcast(BF16)
                for g_ in range(2):
                    tp(pb[0:64, g_ * 128:(g_ + 1) * 128], dd[:, g_ * 64:(g_ + 1) * 64], ident, r=[dkey_, "ident"], w=[PK[b2]])
                act(dstT[0:64, :, ts], pb[0:64, 0:256].rearrange("p (g t) -> p g t", g=2), AF.Copy, r=[PK[b2]], w=[(dk, i)])

        def fm_cmp_proj(idx):
            c, rem = idx // 4, idx % 4
            (Wt, wk, dst, dk) = ((Wkc2, "Wkc2", kc2, "kc2"), (Wvc2, "Wvc2", vc2, "vc2"))[rem // 2]
            g = rem % 2
            b = nb()
            for k in range(8):
                mm(P[b][:, :], Wt[:, k, g * 128:(g + 1) * 128], hT[:, k, c * 512:(c + 1) * 512], k == 0, k == 7, r=["hTall", wk], w=[PK[b]])
            act(dst[0:64, g, c * 512:(c + 1) * 512], P[b][0:64, :], AF.Copy, r=[PK[b]], w=[dk])
            if c == 0:
                act(dst[64:128, g, 0:511], P[b][64:128, 1:512], AF.Copy, r=[PK[b]], w=[dk])
            else:
                act(dst[64:128, g, c * 512 - 1:(c + 1) * 512 - 1], P[b][64:128, :], AF.Copy, r=[PK[b]], w=[dk])

        fr_ = {0: d_front(0), 1: d_front(1)}
        h1_, h2_ = d_chains(0, *fr_[0])
        run_interleaved(h1_)
        for i in range(NT):
            if i + 2 < NT:
                fr_[i + 2] = d_front(i + 2)
            nxt = d_chains(i + 1, *fr_[i + 1]) if i + 1 < NT else ([], [])
            run_interleaved(nxt[0] + h2_)
            h2_ = nxt[1]
            fm_cmp_proj(i)
            if i >= 1:
                d_out(i - 1)
        d_out(NT - 1)
        dbg_out("qnT", qnT[0:64].rearrange("p k t -> p (k t)"), [("qnT", i) for i in range(NT)])
        dbg_out("ksT", ksT[0:64].rearrange("p k t -> p (k t)"), [("ksT", i) for i in range(NT)])
        dbg_out("gates", gates.rearrange("p a b c -> p (a b c)"), [("gates", i) for i in range(NT)])
        dbg_out("kc2", kc2.rearrange("p k t -> p (k t)"), ["kc2", "kc2pad"])
        if upto <= 5:
            return finish(nc, S, st)
        S.barrier()

        nbanks[0] = 8
        M.hi = hi_kv
        hiE = M.hi
        W1k = M.lo_alloc([16, 256], BF16); W1v = M.lo_alloc([16, 256], BF16)
        W2k = M.lo_alloc([2, 64], BF16); W2v = M.lo_alloc([2, 64], BF16)
        pkf = M.lo_alloc([16], F32); pvf = M.lo_alloc([16], F32); pkb = M.lo_alloc([16], BF16); pvb = M.lo_alloc([16], BF16)
        biask = M.lo_alloc([2], F32); biasv = M.lo_alloc([2], F32)
        hid = [M.lo_alloc([128], BF16) for _ in range(2)]
        ovl = M.lo_alloc([32], BF16)
        load_w(W1k, w1k, "W1k", 16, 256); load_w(W1v, w1v, "W1v", 16, 256)
        load_w(W2k, w2k, "W2k", 2, 64); load_w(W2v, w2v, "W2v", 2, 64)
        S.dma("sp", pkf, posk, w=["pkf"]); S.dma("sp", pvf, posv, w=["pvf"]); S.dma("sp", ovl[0:127], ovl_d, w=["ovl"])
        S.op("pool", lambda e: e.memset(VCX[0:127, :, 64:65], 1.0), w=["VCXa"])
        for g in range(2):
            S.op("pool", lambda e, g=g: e.tensor_copy(out=VCX[0:127, g, 65:97], in_=ovl[0:127]), r=["ovl"], w=["VCXb"])
        rt = [M.lo_alloc([128], BF16) for _ in range(3)]
        rt_i = [0]
        for (W1, w1key, W2, w2key, src, skey, posf_, pkey, isk) in ((W1k, "W1k", W2k, "W2k", kc2, ["kc2", "kc2pad"], pkf, "pkf", True),
                                                                    (W1v, "W1v", W2v, "W2v", vc2, ["vc2", "vc2pad"], pvf, "pvf", False)):
            srcv = src.rearrange("p g (n s) -> p g n s", s=16)
            bo = nb()
            for g in range(2):
                bh = []
                for hc in range(2):
                    b = nb()
                    while b == bo or b in bh:
                        b = nb()
                    bh.append(b)
                for lc in range(16):
                    ri = rt_i[0]; rt_i[0] = (ri + 1) % 3
                    rtv = rt[ri][:, 0:127]
                    S.op("dve", lambda e, rtv=rtv, g=g, lc=lc, srcv=srcv, posf_=posf_: e.tensor_scalar(
                        out=rtv, in0=srcv[:, g, (2 * lc) // 16:(2 * lc) // 16 + 127, (2 * lc) % 16], scalar1=posf_[:, lc:lc + 1], scalar2=None, op0=ALU.add),
                        r=skey + [pkey], w=[("rt", ri)])
                    for hc in range(2):
                        mm(P[bh[hc]][:, 0:127], W1[:, lc, hc * 128:(hc + 1) * 128], rtv, lc == 0, lc == 15, r=[w1key, ("rt", ri)], w=[PK[bh[hc]]], sig=(hc == 1 or lc == 15))
                for hc in range(2):
                    act(hid[hc][:, 0:127], P[bh[hc]][:, 0:127], AF.Silu, r=[PK[bh[hc]]], w=[("hid", hc)])
                for hc in range(2):
                    mm(P[bo][0:127, g * 64:(g + 1) * 64], hid[hc][:, 0:127], W2[:, hc, :], hc == 0, hc == 1, r=[("hid", hc), w2key], w=[PK[bo]])
            if isk:
                d = drb[1][0:127, 0:128].rearrange("p (h d) -> p h d", h=2)
                head_norm_rope(P[bo][0:127, 0:128].rearrange("p (h d) -> p h d", h=2), [PK[bo]], 2, 64, nkct, "ngain", 0, 8, cosC[0:127], sinC[0:127], d, "drb1", np_=127)
                b2 = nb(); pb = P[b2][:, :].bitcast(BF16)
                for g_ in range(2):
                    tp(pb[0:64, g_ * 128:g_ * 128 + 127], drb[1][0:127, g_ * 64:(g_ + 1) * 64], ident[0:127, 0:127], r=["drb1", "ident"], w=[PK[b2]])
                act(kcmpT[0:64, :, 0:127], pb[0:64, 0:256].rearrange("p (g t) -> p g t", g=2)[:, :, 0:127], AF.Copy, r=[PK[b2]], w=["kcmpT"])
            else:
                act(VCX[0:127, :, 0:64], P[bo][0:127, 0:128].rearrange("p (g d) -> p g d", g=2), AF.Copy, r=[PK[bo]], w=["VCXc"])
        dbg_out("kcmpT", kcmpT[0:64].rearrange("p g t -> p (g t)"), ["kcmpT"])
        dbg_out("VCX", VCX[0:127].rearrange("p a b -> p (a b)"), ["VCXa", "VCXb", "VCXc"])
        if upto <= 6:
            return finish(nc, S, st)
        S.barrier()

        M.lo = lo2; M.hi = hi0
        onsa = M.hi_alloc([NT, 512], F32)
        mbT = M.hi_alloc([2, S_], BF16)
        vmT = M.hi_alloc([S_], BF16); XE = M.hi_alloc([NT, 128], BF16)
        fb = M.hi_alloc([NT, 32], F32); vj = M.hi_alloc([NT, 32], F32)
        pc = [M.lo_alloc([4, 128], BF16) for _ in range(2)]
        rsum = M.lo_alloc([8], F32); rec8 = M.lo_alloc([8], F32); gr = M.lo_alloc([8], F32)
        tmp_i = M.lo_alloc([8, 32], F32); imp = M.lo_alloc([2, 32], F32); m8 = M.lo_alloc([2, 8], F32)
        sel = M.lo_alloc([2, 32], F32); mbf = M.lo_alloc([2, 96], BF16)
        S.op("pool", lambda e: e.memset(mbf, 0.0), w=["mbf"])
        tmpo = M.lo_alloc([8, 64], F32)
        pw = [M.lo_alloc([3, 128], BF16) for _ in range(4)]
        onb = [M.lo_alloc([512], BF16) for _ in range(2)]
        S.dma("sp", vmT[0:127], vmT_d, w=["vmT"]); S.dma("sp", XE[0:32].rearrange("p a b -> p (a b)"), XE_d, w=["XE"])
        S.dma("sp", fb.rearrange("p a b -> p (a b)"), fb_d, w=["fb"]); S.dma("sp", vj.rearrange("p a b -> p (a b)"), vj_d, w=["vj"])
        VCXk = ["VCXa", "VCXb", "VCXc"]
        for i in range(NT):
            ts = slice(i * 128, (i + 1) * 128)
            sb_ = [nb(), nb()]
            ob = [nb(), nb()]
            for p in range(8):
                j, g = p // 2, p % 2
                mm(P[sb_[p // 4]][0:127, (p % 4) * 128:(p % 4 + 1) * 128], kcmpT[0:64, g, 0:127], qnT[0:64, p, ts], True, True,
                   r=["kcmpT", ("qnT", i)], w=[PK[sb_[p // 4]]])
            for hf_ in range(2):
                pcv = pc[hf_]
                act(pcv[0:127], P[sb_[hf_]][0:127, :].rearrange("p (a b) -> p a b", a=4), AF.Exp, r=[PK[sb_[hf_]]], w=[("pc", hf_)], scale=0.125)
                S.op("dve", lambda e, pcv=pcv: e.tensor_tensor(out=pcv[0:127], in0=pcv[0:127], in1=vmT[0:127, ts].unsqueeze(1).to_broadcast([127, 4, 128]), op=ALU.mult),
                     r=[("pc", hf_), "vmT"], w=[("pc", hf_)])
            for p in range(8):
                g = p % 2
                mm(P[ob[p // 4]][:, (p % 4) * 97:(p % 4 + 1) * 97], pc[p // 4][0:127, p % 4, :], VCX[0:127, g, :], True, True,
                   r=[("pc", p // 4)] + VCXk, w=[PK[ob[p // 4]]])
            OC = [P[ob[h_]][:, 0:388].rearrange("p (a b) -> p a b", a=4) for h_ in range(2)]
            for h_ in range(2):
                S.op("dve", lambda e, h_=h_: e.tensor_scalar(out=rsum[:, h_ * 4:(h_ + 1) * 4], in0=OC[h_][:, :, 64], scalar1=1e-30, scalar2=None, op0=ALU.max), r=[PK[ob[h_]]], w=["rsum"])
            S.op("dve", lambda e: e.reciprocal(out=rec8, in_=rsum), r=["rsum"], w=["rec8"])
            S.op("dve", lambda e, i=i: e.tensor_tensor(out=gr, in0=gates[:, i, 0, :], in1=rec8, op=ALU.mult), r=["rec8", ("gates", i)], w=["gr"])
            for h_ in range(2):
                S.op("dve", lambda e, h_=h_, i=i: e.tensor_tensor(out=onsa[:, i, h_ * 256:(h_ + 1) * 256].rearrange("p (a b) -> p a b", a=4), in0=OC[h_][:, :, 0:64],
                                                                   in1=gr[:, h_ * 4:(h_ + 1) * 4].unsqueeze(2).to_broadcast([128, 4, 64]), op=ALU.mult),
                     r=[PK[ob[h_]], "gr"], w=[("onsa", i)])
                S.op("dve", lambda e, h_=h_: e.tensor_tensor(out=tmp_i[:, h_ * 4:(h_ + 1) * 4, :], in0=OC[h_][:, :, 65:97],
                                                             in1=rec8[:, h_ * 4:(h_ + 1) * 4].unsqueeze(2).to_broadcast([128, 4, 32]), op=ALU.mult),
                     r=[PK[ob[h_]], "rec8"], w=["tmp_i"])
            S.op("dve", lambda e: e.tensor_reduce(out=imp, in_=tmp_i.rearrange("t (j g) n -> t g n j", g=2), axis=AX.X, op=ALU.add), r=["tmp_i"], w=["imp"])
            S.op("dve", lambda e, i=i: e.tensor_tensor(out=imp, in0=imp, in1=fb[:, i, :].unsqueeze(1).to_broadcast([128, 2, 32]), op=ALU.add), r=["imp", "fb"], w=["imp"])
            for g in range(2):
                S.op("dve", lambda e, g=g: e.max(out=m8[:, g, :], in_=imp[:, g, :]), r=["imp"], w=["m8"])
                S.op("dve", lambda e, g=g: e.tensor_scalar(out=sel[:, g, :], in0=imp[:, g, :], scalar1=m8[:, g, 7:8], scalar2=None, op0=ALU.is_ge), r=["imp", "m8"], w=["sel"])
            S.op("dve", lambda e, i=i: e.tensor_tensor(out=sel, in0=sel, in1=vj[:, i, :].unsqueeze(1).to_broadcast([128, 2, 32]), op=ALU.mult), r=["sel", "vj"], w=["sel"])
            S.op("dve", lambda e: e.tensor_scalar(out=mbf[:, :, 64:96], in0=sel, scalar1=-1.0, scalar2=30000.0, op0=ALU.add, op1=ALU.mult), r=["sel"], w=["mbf"])
            if i == 5:
                dbg_out("sel5", sel.rearrange("p a b -> p (a b)"), ["sel"])
                dbg_out("imp5", imp.rearrange("p a b -> p (a b)"), ["imp"])
            b = nb(); pb = P[b][:, :].bitcast(BF16)
            for g in range(2):
                tp(pb[0:96, g * 128:(g + 1) * 128], mbf[:, g, :], ident, r=["mbf", "ident"], w=[PK[b]])
            qm = qnT[64:96].rearrange("p (j g) t -> p j g t", g=2)
            for g in range(2):
                act(qm[:, :, g, ts], pb[64:96, g * 128:(g + 1) * 128].unsqueeze(1).to_broadcast([32, 4, 128]), AF.Copy, r=[PK[b]], w=[("mbq", i)])
        dbg_out("onsa_c", onsa.rearrange("p a b -> p (a b)"), [("onsa", i) for i in range(NT)])
        dbg_out("mbT", qnT[64:96, 0:2, :].rearrange("p a b -> p (a b)"), [("mbq", i) for i in range(NT)])
        if upto <= 7:
            return finish(nc, S, st)

        S.barrier()
        nbanks[0] = 6
        jobs = []
        for p in range(8):
            j, g = p // 2, p % 2

            def extra(ps_ap, kt, a, b_, pk, g=g):
                mm(ps_ap, XE[0:32, kt, :], mbT[0:32, g, a:b_], False, True, r=["XE"] + [("mbT", q) for q in range(a // 128, b_ // 128)], w=[pk])

            def fin(c, Oacc, pk, p=p):
                rc = rec4[c % 2]
                S.op("dve", lambda e: e.reciprocal(out=rc, in_=Oacc[:, :, 64]), r=[pk], w=[("rec4", c % 2)])
                S.op("dve", lambda e: e.tensor_tensor(out=rc, in0=rc, in1=gates[:, 4 * c:4 * c + 4, 1, p], op=ALU.mult), r=[("rec4", c % 2)] + [("gates", 4 * c + q) for q in range(4)], w=[("rec4", c % 2)])
                tv = tmpo[:, 0:4, :]
                S.op("dve", lambda e: e.tensor_tensor(out=tv, in0=Oacc[:, :, 0:64], in1=rc.unsqueeze(2).to_broadcast([128, 4, 64]), op=ALU.mult), r=[pk, ("rec4", c % 2)], w=["tmpo"])
                S.op("pool", lambda e: e.tensor_tensor(out=onsa[:, 4 * c:4 * c + 4, p * 64:(p + 1) * 64], in0=onsa[:, 4 * c:4 * c + 4, p * 64:(p + 1) * 64], in1=tv, op=ALU.add),
                     r=["tmpo"] + [("onsa", 4 * c + q) for q in range(4)], w=[("onsa", 4 * c + q) for q in range(4)])
            jobs.append(dict(KT=ksT[:, g, :], kn=96, QT=(lambda a, b_, p=p: qnT[0:96, p, a:b_]), V=(lambda kt, g=g: vs[:, kt, g, :]), scale=0.125,
                             extra=None, fin=fin, qk=(lambda c: [("qnT", 4 * c + q) for q in range(4)] + [("mbq", 4 * c + q) for q in range(4)]), kk=(lambda kt: [("ksT", kt), "ksTx"]),
                             vk=(lambda kt: [("vs", kt), "vsones"])))
        causal_attn_multi(jobs)
        dbg_out("onsa_cs", onsa.rearrange("p a b -> p (a b)"), [("onsa", i) for i in range(NT)])
        if upto <= 8:
            return finish(nc, S, st)

        pw_i = [0]
        nbanks[0] = 4
        bank_state[0] = 0

        def emit_ws(i, p):
            g = p % 2
            kts = [kt for kt in (i - 2, i - 1, i) if kt >= 0]
            b = nb()
            for kt in kts:
                sl = kt - (i - 2)
                mm(P[b][:, sl * 128:(sl + 1) * 128], kwT[0:64, g, kt * 128:(kt + 1) * 128], qnT[0:64, p, i * 128:(i + 1) * 128], True, True,
                   r=[("kwT", kt), ("qnT", i)], w=[PK[b]])
            return b

        wsteps = [(i, p) for i in range(NT) for p in range(8)]
        WLA = 2
        wpend = [emit_ws(*wsteps[q]) for q in range(WLA)]
        for wi_, (i, p) in enumerate(wsteps):
            ts = slice(i * 128, (i + 1) * 128)
            g = p % 2
            kts = [kt for kt in (i - 2, i - 1, i) if kt >= 0]
            b = wpend.pop(0)
            if wi_ + WLA < len(wsteps):
                wpend.append(emit_ws(*wsteps[wi_ + WLA]))
            ob = [4 + 2 * (i % 2), 5 + 2 * (i % 2)]
            s0 = kts[0] - (i - 2)
            wi = pw_i[0]; pw_i[0] = (wi + 1) % len(pw)
            pwv = pw[wi]
            act(pwv[:, s0:3, :], P[b][:, s0 * 128:384].rearrange("p (a b) -> p a b", b=128), AF.Exp, r=[PK[b]], w=[("pw", wi)], scale=0.125)
            S.op("dve", lambda e, pwv=pwv, s0=s0: e.tensor_tensor(out=pwv[:, s0:3, :], in0=pwv[:, s0:3, :], in1=winm[:, s0:3, :], op=ALU.mult), r=[("pw", wi), "winm"], w=[("pw", wi)])
            for kt in kts:
                sl = kt - (i - 2)
                mm(P[ob[p // 4]][:, (p % 4) * 65:(p % 4 + 1) * 65], pwv[:, sl, :], vw[:, kt, g, :], kt == kts[0], kt == kts[-1],
                   r=[("pw", wi), ("vw", kt), "vwones"], w=[PK[ob[p // 4]]])
            if p != 7:
                continue
            OW = [P[ob[h_]][:, 0:260].rearrange("p (a b) -> p a b", a=4) for h_ in range(2)]
            for h_ in range(2):
                S.op("dve", lambda e, h_=h_: e.reciprocal(out=rec8[:, h_ * 4:(h_ + 1) * 4], in_=OW[h_][:, :, 64]), r=[PK[ob[h_]]], w=["rec8"])
            S.op("dve", lambda e, i=i: e.tensor_tensor(out=gr, in0=gates[:, i, 2, :], in1=rec8, op=ALU.mult), r=["rec8", ("gates", i)], w=["gr"])
            for h_ in range(2):
                S.op("dve", lambda e, h_=h_: e.tensor_tensor(out=tmpo[:, h_ * 4:(h_ + 1) * 4, :], in0=OW[h_][:, :, 0:64],
                                                             in1=gr[:, h_ * 4:(h_ + 1) * 4].unsqueeze(2).to_broadcast([128, 4, 64]), op=ALU.mult), r=[PK[ob[h_]], "gr"], w=["tmpo"])
            S.op("pool", lambda e, i=i: e.tensor_tensor(out=onb[i % 2], in0=onsa[:, i, :], in1=tmpo.rearrange("p a b -> p (a b)"), op=ALU.add), r=["tmpo", ("onsa", i)], w=[("onb", i % 2)])
            if "onsa_all" in D_:
                S.dma("sp", D_["onsa_all"][:, i * 512:(i + 1) * 512], onb[i % 2], r=[("onb", i % 2)])
            b = nb(); pb = P[b][:, :].bitcast(BF16)
            for j in range(4):
                tp(pb[:, j * 128:(j + 1) * 128], onb[i % 2][:, j * 128:(j + 1) * 128], ident, r=[("onb", i % 2), "ident"], w=[PK[b]])
            act(onT[:, :, ts], pb[:, 0:512].rearrange("p (k t) -> p k t", k=4), AF.Copy, r=[PK[b]], w=[("onT", i)])
        if upto <= 9:
            return finish(nc, S, st)
        S.barrier()

        nbanks[0] = 8
        M.lo = lo0; M.hi = hi0
        hT = M.hi_alloc([8, S_], BF16)
        mergedT = M.hi_alloc([8, S_], BF16)
        hi2 = M.hi
        Wgm = M.lo_alloc([8, 512], BF16); Wgnm = M.lo_alloc([8, 512], BF16); Wom = M.lo_alloc([4, 512], BF16); Won = M.lo_alloc([4, 512], BF16)
        e3 = M.lo_alloc([512], F32); e4 = M.lo_alloc([512], F32); tA = M.lo_alloc([512], F32); tB = M.lo_alloc([512], F32)
        mgb = [M.lo_alloc([512], BF16) for _ in range(2)]
        S.dma("sp", hT.rearrange("p k t -> p (k t)"), hTs, r=["hTs"], w=["hTall"])
        e3 = [e3, M.lo_alloc([512], F32)]; e4 = [e4, M.lo_alloc([512], F32)]
        tA = [tA, M.lo_alloc([512], F32)]; tB = [tB, M.lo_alloc([512], F32)]

        def i_front(cc, i, it):
            ts = slice(i * 128, (i + 1) * 128)
            bs = [4 * (it % 2) + q for q in range(4)]
            b1, b2, b3, b4 = bs
            for k in range(8):
                mm(P[b3][:, :], hT[:, k, ts], Wgm[:, k, :], k == 0, k == 7, r=["hTall", "Wgm"], w=[PK[b3]])
            for k in range(8):
                mm(P[b4][:, :], hT[:, k, ts], Wgnm[:, k, :], k == 0, k == 7, r=["hTall", "Wgnm"], w=[PK[b4]])
            for k in range(4):
                mm(P[b1][:, :], omT[:, k, ts], Wom[:, k, :], k == 0, k == 3, r=[("omT", i), "Wom"], w=[PK[b1]])
            for k in range(4):
                mm(P[b2][:, :], onT[:, k, ts], Won[:, k, :], k == 0, k == 3, r=[("onT", i), "Won"], w=[PK[b2]])
            return bs

        def i_back(cc, i, it, bs):
            ts = slice(i * 128, (i + 1) * 128)
            b1, b2, b3, b4 = bs
            par = it % 2
            act(e3[par], P[b3][:, :], AF.Sigmoid, r=[PK[b3]], w=[("e3", par)])
            act(e4[par], P[b4][:, :], AF.Sigmoid, r=[PK[b4]], w=[("e4", par)])
            S.op("dve", lambda e: e.tensor_tensor(out=tA[par], in0=P[b1][:, :], in1=e3[par], op=ALU.mult), r=[PK[b1], ("e3", par)], w=[("tA", par)])
            S.op("dve", lambda e: e.tensor_tensor(out=tB[par], in0=P[b2][:, :], in1=e4[par], op=ALU.mult), r=[PK[b2], ("e4", par)], w=[("tB", par)])
            S.op("dve", lambda e: e.tensor_tensor(out=mgb[par], in0=tA[par], in1=tB[par], op=ALU.add), r=[("tA", par), ("tB", par)], w=[("mgb", par)])
            if "merged" in D_:
                S.dma("sp", D_["merged"][i * 128:(i + 1) * 128, cc * 512:(cc + 1) * 512], mgb[par], r=[("mgb", par)])
            pb = P[b2][:, :].bitcast(BF16)
            for j in range(4):
                tp(pb[:, j * 128:(j + 1) * 128], mgb[par][:, j * 128:(j + 1) * 128], ident, r=[("mgb", par), "ident"], w=[PK[b2]])
            act(mergedT[:, cc * 4:(cc + 1) * 4, ts], pb[:, 0:512].rearrange("p (k t) -> p k t", k=4), AF.Copy, r=[PK[b2]], w=[("mergedT", i)])

        it = 0
        for cc in range(2):
            load_w_cols(Wgm, w_gm, "Wgm", 8, cc * 512, (cc + 1) * 512); load_w_cols(Wgnm, w_gnm, "Wgnm", 8, cc * 512, (cc + 1) * 512)
            load_w_cols(Wom, wo_mla, "Wom", 4, cc * 512, (cc + 1) * 512); load_w_cols(Won, wo_nsa, "Won", 4, cc * 512, (cc + 1) * 512)
            pend_ = i_front(cc, 0, it)
            for i in range(NT):
                cur_ = pend_
                if i + 1 < NT:
                    pend_ = i_front(cc, i + 1, it + 1)
                i_back(cc, i, it, cur_)
                it += 1
        if upto <= 10:
            return finish(nc, S, st)
        S.barrier()

        M.lo = lo0
        h2T = hT
        Wout = M.lo_alloc([8, D], BF16)
        G1 = M.lo_alloc([D], F32); A2 = M.lo_alloc([D], F32); B2 = M.lo_alloc([D], F32)
        xb = [M.lo_alloc([D], F32) for _ in range(2)]
        x1t = [M.lo_alloc([D], F32) for _ in range(2)]
        junk = M.lo_alloc([D], F32); tmpA = M.lo_alloc([D], F32)
        hb = [M.lo_alloc([D], BF16) for _ in range(2)]
        ssq = M.lo_alloc([NT], F32); rs = M.lo_alloc([NT], F32)
        S.dma("sp", G1, mods[:, 2 * D:3 * D], r=["mods"], w=["G1"])
        S.dma("sp", B2, mods[:, 3 * D:4 * D], r=["mods"], w=["AB2"])
        S.dma("sp", A2, mods[:, 4 * D:5 * D], r=["mods"], w=["AB2"])
        load_w(Wout, w_out, "Wout", 8, D, ceng="pool")
        tmpA2 = [tmpA, M.lo_alloc([D], F32)]
        tmpJ = [M.lo_alloc([D], F32) for _ in range(3)]
        xb = xb + [M.lo_alloc([D], F32)]
        x1t = x1t + [M.lo_alloc([D], F32)]

        def j_front(i):
            ts = slice(i * 128, (i + 1) * 128)
            par = i % 3
            S.dma("sp", xb[par], x[ts, :], w=[("xb", par)])
            for cc in range(2):
                b = 2 * par + cc
                for k in range(8):
                    mm(P[b][:, :], mergedT[:, k, ts], Wout[:, k, cc * 512:(cc + 1) * 512], k == 0, k == 7, r=[("mergedT", i), "Wout"], w=[PK[b]])

        def j_mid(i):
            ts = slice(i * 128, (i + 1) * 128)
            par = i % 3
            for cc in range(2):
                b = 2 * par + cc
                S.op("dve", lambda e, b=b, cc=cc: e.tensor_tensor(out=tmpJ[par][:, cc * 512:(cc + 1) * 512], in0=P[b][:, :], in1=G1[:, cc * 512:(cc + 1) * 512], op=ALU.mult), r=[PK[b], "G1"], w=[("tmpJ", par)])
                S.op("pool", lambda e, cc=cc: e.tensor_tensor(out=x1t[par][:, cc * 512:(cc + 1) * 512], in0=tmpJ[par][:, cc * 512:(cc + 1) * 512], in1=xb[par][:, cc * 512:(cc + 1) * 512], op=ALU.add),
                     r=[("tmpJ", par), ("xb", par)], w=[("x1t", par)])
            S.dma("sp", x1s[ts, :], x1t[par], r=[("x1t", par)], w=[("x1s", i)])
            norm_a(i, x1t[par], ("x1t", par))

        bank_state[0] = 0

        def nbJ():
            b = 6 + (bank_state[0] % 2)
            bank_state[0] = (bank_state[0] + 1) % 2
            return b
        nb_saved = nb
        nb = nbJ
        j_front(0); j_mid(0); j_front(1); j_mid(1)
        for i in range(NT):
            if i + 2 < NT:
                j_front(i + 2)
            b_ = norm_b1(i, x1t[i % 3], ("x1t", i % 3), A2, B2, ["AB2"])
            if i + 2 < NT:
                j_mid(i + 2)
            norm_b2(i, b_, h2T, "h2T")
        nb = nb_saved
        bank_state[0] = 0
        if "x1" in D_:
            S.dma("sp", D_["x1"], x1s, r=[("x1s", i) for i in range(NT)])
        if upto <= 11:
            return finish(nc, S, st)
        S.barrier()

        M.lo = lo_pers; M.hi = hi2 + 8 * S_ * 2
        Wd = M.hi_alloc([NFC, D], BF16)
        actT = M.hi_alloc([NFC, 1024], BF16)
        G2 = M.lo_alloc([D], F32)
        Wg2 = [M.lo_alloc([8, 256], BF16) for _ in range(2)]; Wu2 = [M.lo_alloc([8, 256], BF16) for _ in range(2)]
        sg = [M.lo_alloc([512], F32) for _ in range(2)]
        xb = [M.lo_alloc([D], F32) for _ in range(2)]
        ot = [M.lo_alloc([D], F32) for _ in range(2)]
        tmpA = M.lo_alloc([D], F32)
        S.dma("sp", G2, mods[:, 5 * D:6 * D], r=["mods"], w=["G2"])
        load_w(Wd, wd, "Wd", NFC, D)
        h2k = [("h2T", i) for i in range(NT)]
        out_toks = []
        for half in range(2):
            def ld(jg_):
                wb_ = jg_ % 2
                load_w_cols(Wg2[wb_], wg, ("Wg2", wb_), 8, jg_ * 256, (jg_ + 1) * 256)
                load_w_cols(Wu2[wb_], wu, ("Wu2", wb_), 8, jg_ * 256, (jg_ + 1) * 256)
            ld(0)
            for jg in range(NFC // 2):
                wb = jg % 2
                if jg + 1 < NFC // 2:
                    ld(jg + 1)
                for jj in range(2):
                    j = jg * 2 + jj
                    for tc in range(2):
                        t0 = half * 1024 + tc * 512
                        bg, bu = nb(), nb()
                        for k in range(8):
                            mm(P[bg][:, :], Wg2[wb][:, k, jj * 128:(jj + 1) * 128], h2T[:, k, t0:t0 + 512], k == 0, k == 7, r=[("Wg2", wb)] + h2k, w=[PK[bg]])
                        for k in range(8):
                            mm(P[bu][:, :], Wu2[wb][:, k, jj * 128:(jj + 1) * 128], h2T[:, k, t0:t0 + 512], k == 0, k == 7, r=[("Wu2", wb)] + h2k, w=[PK[bu]])
                        act(sg[tc], P[bg][:, :], AF.Silu, r=[PK[bg]], w=[("sg", tc)])
                        S.op("dve", lambda e, j=j, tc=tc, bu=bu: e.tensor_tensor(out=actT[:, j, tc * 512:(tc + 1) * 512], in0=P[bu][:, :], in1=sg[tc], op=ALU.mult),
                             r=[PK[bu], ("sg", tc)], w=[("actT", tc)])
            for il in range(8):
                i = half * 8 + il
                ts = slice(i * 128, (i + 1) * 128)
                S.dma("sp", xb[i % 2], x1s[ts, :], r=[("x1s", i)], w=[("xb", i % 2)])
                for cc in range(2):
                    b = nb()
                    for j in range(NFC):
                        mm(P[b][:, :], actT[:, j, il * 128:(il + 1) * 128], Wd[:, j, cc * 512:(cc + 1) * 512], j == 0, j == NFC - 1, r=[("actT", il // 4), "Wd"], w=[PK[b]])
                    S.op("dve", lambda e, b=b, cc=cc: e.tensor_tensor(out=tmpA[:, cc * 512:(cc + 1) * 512], in0=P[b][:, :], in1=G2[:, cc * 512:(cc + 1) * 512], op=ALU.mult), r=[PK[b], "G2"], w=["tmpA"])
                    S.op("pool", lambda e, i=i, cc=cc: e.tensor_tensor(out=ot[i % 2][:, cc * 512:(cc + 1) * 512], in0=tmpA[:, cc * 512:(cc + 1) * 512], in1=xb[i % 2][:, cc * 512:(cc + 1) * 512], op=ALU.add),
                         r=["tmpA", ("xb", i % 2)], w=[("ot", i % 2)])
                S.dma("sp", out[ts, :], ot[i % 2], r=[("ot", i % 2)], w=[("out", i)])
        return finish(nc, S, st)


def finish(nc, S, st):
    for q in S.dsem:
        for i in range(len(S.dsem[q])):
            if S.dcnt[q][i]:
                S._wait("sp", (("d", q, i), S.dcnt[q][i]))
    for e2 in ("pe", "act", "dve", "pool"):
        if S.cnt[e2]:
            S._wait("sp", (e2, S.cnt[e2]))
    S.check_no_deadlock()
    st.close()
    return nc


def _consts():
    bf = ml_dtypes.bfloat16
    c = {}
    c["ident"] = np.eye(128, dtype=np.float32).astype(bf)
    a = np.arange(128)
    c["tri"] = (a[:, None] <= a[None, :]).astype(np.float32).astype(bf)
    w = np.zeros((128, 3, 128), np.float32)
    w[:, 0, :] = (a[:, None] > a[None, :])
    w[:, 1, :] = 1.0
    w[:, 2, :] = (a[:, None] <= a[None, :])
    c["winm"] = w.reshape(128, 384).astype(bf)
    inv16 = (np.float32(500000.0) ** (-np.arange(0, 32, 2, dtype=np.float32) / np.float32(32))).astype(np.float32)
    inv8 = (np.float32(500000.0) ** (-np.arange(0, 16, 2, dtype=np.float32) / np.float32(16))).astype(np.float32)
    c["inv16"] = np.tile(inv16[None], (128, 1)).astype(np.float32)
    c["inv8"] = np.tile(inv8[None], (128, 1)).astype(np.float32)
    n = np.arange(127)
    starts = n * 16
    j = np.arange(32)
    ovl = ((starts[:, None] < j[None, :] * 64 + 64) & (starts[:, None] + 32 > j[None, :] * 64))
    c["ovl"] = ovl.astype(np.float32).astype(bf)
    t = np.arange(S_)
    c["vmT"] = ((starts[:, None] + 31) <= t[None, :]).astype(np.float32).astype(bf)
    XE = np.zeros((32, NT, 128), np.float32)
    for kt in range(NT):
        XE[2 * kt, kt, 0:64] = 1.0
        XE[2 * kt + 1, kt, 64:128] = 1.0
    c["XE"] = XE.reshape(32, NT * 128).astype(bf)
    c["XEall"] = (np.arange(32)[:, None] == (np.arange(S_)[None, :] // 64)).astype(np.float32).astype(bf)
    cur = (t // 64)
    forced = (j[None, :] == 0) | (j[None, :] == cur[:, None]) | (j[None, :] == cur[:, None] - 1)
    valid = j[None, :] <= cur[:, None]
    fb = np.where(valid, np.where(forced, 1e4, 0.0), -1e30).astype(np.float32)
    c["fb"] = fb.reshape(NT, 128, 32).transpose(1, 0, 2).reshape(128, NT * 32).copy()
    c["vj"] = valid.astype(np.float32).reshape(NT, 128, 32).transpose(1, 0, 2).reshape(128, NT * 32).copy()
    return c


def _rep(v, n=128):
    return np.ascontiguousarray(np.broadcast_to(np.asarray(v, np.float32)[None, :], (n, v.shape[0])))


def prep_inputs(inp):
    f = lambda a: np.ascontiguousarray(np.asarray(a, dtype=np.float32))
    w_in = f(inp["w_in"][0])
    o = np.cumsum([0, 768, 256, 32, 512, 128, 128, 128, 128, 128, 128, 24, 1024, 1024])
    seg = lambda i: w_in[:, o[i]:o[i + 1]]
    shared = {}
    shared["ada_w"] = f(inp["ada_w"][0]); shared["adabB"] = _rep(f(inp["ada_b"][0]))
    shared["g1B"] = _rep(f(inp["norm1_gain"][0])); shared["g2B"] = _rep(f(inp["norm2_gain"][0]))
    shared["w_cq"] = f(seg(0)); shared["w_ckv"] = f(seg(1)); shared["w_kpe"] = f(seg(2))
    qn = seg(3).reshape(D, 8, 64)
    shared["w_qn"] = f(qn[:, PH, :].reshape(D, 512))
    kc = seg(4).reshape(D, 2, 64); vc = seg(5).reshape(D, 2, 64)
    shared["w_kc2"] = f(np.stack([kc[:, 0], kc[:, 0], kc[:, 1], kc[:, 1]], 1).reshape(D, 256))
    shared["w_vc2"] = f(np.stack([vc[:, 0], vc[:, 0], vc[:, 1], vc[:, 1]], 1).reshape(D, 256))
    shared["w_kv4"] = f(np.concatenate([seg(6), seg(8), seg(7), seg(9)], 1))
    gn = seg(10).reshape(D, 8, 3)
    shared["w_gn"] = f(gn[:, PH, :].transpose(0, 2, 1).reshape(D, 24))
    shared["w_gm"] = f(seg(11)); shared["w_gnm"] = f(seg(12))
    shared["qag"] = f(f(inp["mla_q_a_gain"][0]).reshape(6, 128).T); shared["kvag"] = f(f(inp["mla_kv_a_gain"][0]).reshape(2, 128).T)
    shared["w_qb"] = f(inp["mla_w_q_b"][0]); shared["w_kvb"] = f(inp["mla_w_kv_b"][0])
    shared["qgB"] = _rep(f(inp["mla_q_gain"][0])); shared["kgB"] = _rep(f(inp["mla_k_gain"][0]))
    shared["nqB"] = _rep(f(inp["nsa_q_gain"][0])); shared["nkcB"] = _rep(f(inp["nsa_kc_gain"][0]))
    shared["nksB"] = _rep(f(inp["nsa_ks_gain"][0])); shared["nkwB"] = _rep(f(inp["nsa_kw_gain"][0]))
    shared["posk"] = f(f(inp["cmp_pos_k"][0]).reshape(16, 128).T); shared["posv"] = f(f(inp["cmp_pos_v"][0]).reshape(16, 128).T)
    shared["w1k"] = f(inp["cmp_w1_k"][0]); shared["w2k"] = f(inp["cmp_w2_k"][0])
    shared["w1v"] = f(inp["cmp_w1_v"][0]); shared["w2v"] = f(inp["cmp_w2_v"][0])
    shared["wo_mla"] = f(inp["w_o_mla"][0])
    shared["wo_nsa"] = f(f(inp["w_o_nsa"][0]).reshape(8, 64, D)[PH].reshape(512, D))
    shared["w_out"] = f(inp["w_out"][0])
    shared["wg"] = f(inp["ffn_w_gate"][0]); shared["wu"] = f(inp["ffn_w_up"][0]); shared["wd"] = f(inp["ffn_w_down"][0])
    shared.update(_consts())
    maps = []
    xs = np.asarray(inp["x"], np.float32); cs = np.asarray(inp["c"], np.float32); ps = np.asarray(inp["positions"]).astype(np.int32)
    for b in range(xs.shape[0]):
        m = dict(shared)
        m["x"] = np.ascontiguousarray(xs[b])
        m["c_pk"] = np.ascontiguousarray(cs[b].reshape(8, 128).T)
        m["pos_pk"] = np.ascontiguousarray(ps[b].reshape(NT, 128).T)
        m["posC"] = np.ascontiguousarray(ps[b][31::16][:127].reshape(127, 1))
        maps.append(m)
    return maps


_NC_CACHE = {}


def kernel(**inputs):
    maps = prep_inputs(inputs)
    if "nc" not in _NC_CACHE:
        _NC_CACHE["nc"] = build()
    nc = _NC_CACHE["nc"]
    res = run_bass_kernel_spmd(nc, maps, core_ids=list(range(len(maps))))
    return np.stack([np.asarray(r["out"], dtype=np.float32) for r in res.results], 0)
```

```python
import contextlib
import numpy as np
import ml_dtypes
import concourse.bass as bass
import concourse.mybir as mybir
from concourse.bass_utils import run_bass_kernel_spmd

F32 = mybir.dt.float32
BF16 = mybir.dt.bfloat16
I32 = mybir.dt.int32
ALU = mybir.AluOpType
AF = mybir.ActivationFunctionType
AX = mybir.AxisListType

S_ = 2048
D = 1024
NT = 16
DFF = 2816
NFC = 22
EPS = 1e-6
PH = [0, 4, 1, 5, 2, 6, 3, 7]
TWO_PI = float(2 * np.pi)
PI = float(np.pi)


class Sched:
    N_DMA_SLOTS = {"sp": 24, "pool": 8, "act": 4}

    def __init__(self, nc, stack):
        self.nc = nc
        self.E = {"pe": nc.tensor, "act": nc.scalar, "dve": nc.vector, "pool": nc.gpsimd, "sp": nc.sync}
        self.sem, self.cnt = {}, {}
        for e in ("pe", "act", "dve", "pool"):
            self.sem[e] = stack.enter_context(nc.semaphore("s_" + e))
            self.cnt[e] = 0
        self.dsem, self.dcnt, self.dnext = {}, {}, {}
        for q, n in self.N_DMA_SLOTS.items():
            self.dsem[q] = [stack.enter_context(nc.semaphore(f"d_{q}{i}")) for i in range(n)]
            self.dcnt[q] = [0] * n
            self.dnext[q] = 0
        self.seen = {e: {} for e in self.E}
        self.lastw, self.readers = {}, {}
        self.n_wait = 0
        self.n_inst = 0
        self.prog = {}
        self.ninst = {}

    def check_no_deadlock(self):
        val = {}
        ptr = {e: 0 for e in self.prog}
        progress = True
        while progress:
            progress = False
            for e, lst in self.prog.items():
                while ptr[e] < len(lst):
                    it = lst[ptr[e]]
                    if it[0] == "w":
                        if val.get(it[1], 0) < it[2]:
                            break
                    else:
                        val[it[1]] = val.get(it[1], 0) + it[2]
                    ptr[e] += 1
                    progress = True
        stuck = {e: (ptr[e], len(l), l[ptr[e]]) for e, l in self.prog.items() if ptr[e] < len(l)}
        assert not stuck, f"DEADLOCK in emitted program: {stuck}"

    def _sem_of(self, src):
        return self.dsem[src[1]][src[2]] if isinstance(src, tuple) else self.sem[src]

    def _wait(self, e, tok):
        src, val = tok
        if self.seen[e].get(src, 0) >= val:
            return
        self.E[e].wait_ge(self._sem_of(src), val)
        self.seen[e][src] = val
        self.n_wait += 1
        self.prog.setdefault(e, []).append(("w", src, val))

    def _deps(self, e, r, w):
        toks = []
        for k in r:
            t = self.lastw.get(k)
            if t is not None:
                toks.append(t)
        for k in w:
            t = self.lastw.get(k)
            if t is not None:
                toks.append(t)
            for t in self.readers.get(k, ()):
                toks.append(t)
        for t in toks:
            if t[0] == e and e == "pe":
                continue
            self._wait(e, t)

    def _commit(self, tok, r, w):
        for k in r:
            lst = self.readers.setdefault(k, [])
            lst[:] = [t for t in lst if t[0] != tok[0]]
            lst.append(tok)
        for k in w:
            self.lastw[k] = tok
            self.readers[k] = []

    def op(self, e, fn, r=(), w=(), signal=True):
        self._deps(e, r, w)
        ins = fn(self.E[e])
        if signal:
            self.cnt[e] += 1
            ins.then_inc(self.sem[e], 1)
            tok = (e, self.cnt[e])
            self.prog.setdefault(e, []).append(("i", e, 1))
        else:
            tok = (e, self.cnt[e] + 1)
        self._commit(tok, r, w)
        self.n_inst += 1
        self.ninst[e] = self.ninst.get(e, 0) + 1
        return tok

    def dma(self, q, out, in_, r=(), w=(), **kw):
        slot = self.dnext[q]
        self.dnext[q] = (slot + 1) % len(self.dsem[q])
        src = ("d", q, slot)
        if self.dcnt[q][slot] > 0:
            self._wait(q, (src, self.dcnt[q][slot]))
        self._deps(q, r, w)
        ins = self.E[q].dma_start(out=out, in_=in_, **kw)
        self.dcnt[q][slot] += 16
        ins.then_inc(self.dsem[q][slot], 16)
        self.prog.setdefault(q, []).append(("i", src, 16))
        tok = (src, self.dcnt[q][slot])
        self._commit(tok, r, w)
        self.n_inst += 1
        return tok

    def barrier(self):
        for e in ("pe", "act", "dve", "pool", "sp"):
            for e2 in ("pe", "act", "dve", "pool"):
                if self.cnt[e2] and not (e2 == e == "pe"):
                    self._wait(e, (e2, self.cnt[e2]))
            for q in self.dsem:
                for i in range(len(self.dsem[q])):
                    if self.dcnt[q][i]:
                        self._wait(e, (("d", q, i), self.dcnt[q][i]))
        self.lastw.clear()
        self.readers.clear()


class Mem:
    def __init__(self, big, nbytes):
        self.big, self.lo, self.hi, self.n = big, 0, nbytes, nbytes

    def _view(self, off, shape, dt):
        nel = int(np.prod(shape))
        esz = 4 if dt in (F32, I32) else 2
        nb = nel * esz
        ap = self.big[:, off // 2:(off + nb) // 2]
        if esz == 4:
            ap = ap.bitcast(dt)
        if len(shape) == 2:
            ap = ap.rearrange("p (a b) -> p a b", a=shape[0])
        elif len(shape) == 3:
            ap = ap.rearrange("p (a b c) -> p a b c", a=shape[0], b=shape[1])
        return ap

    def lo_alloc(self, shape, dt):
        nb = int(np.prod(shape)) * (4 if dt in (F32, I32) else 2)
        nb = (nb + 63) // 64 * 64
        off = self.lo
        self.lo += nb
        assert self.lo <= self.hi, f"SBUF overflow lo={self.lo} hi={self.hi}"
        return self._view(off, shape, dt)

    def hi_alloc(self, shape, dt):
        nb = int(np.prod(shape)) * (4 if dt in (F32, I32) else 2)
        nb = (nb + 63) // 64 * 64
        self.hi -= nb
        assert self.lo <= self.hi, f"SBUF overflow lo={self.lo} hi={self.hi}"
        return self._view(self.hi, shape, dt)


def build(upto=99, dbg=()):
    nc = bass.Bass("TRN2", target_bir_lowering=False)
    I = {}

    def din(name, shape, dt=F32):
        I[name] = nc.dram_tensor(name, list(shape), dt, kind="ExternalInput").ap()
        return I[name]

    x = din("x", [S_, D]); c_pk = din("c_pk", [128, 8]); pos_pk = din("pos_pk", [128, NT], I32)
    posC = din("posC", [127, 1], I32)
    ada_w = din("ada_w", [D, 6 * D]); adabB = din("adabB", [128, 6 * D]); g1B = din("g1B", [128, D]); g2B = din("g2B", [128, D])
    w_cq = din("w_cq", [D, 768]); w_ckv = din("w_ckv", [D, 256]); w_kpe = din("w_kpe", [D, 32])
    w_qn = din("w_qn", [D, 512]); w_kc2 = din("w_kc2", [D, 256]); w_vc2 = din("w_vc2", [D, 256])
    w_kv4 = din("w_kv4", [D, 512]); w_gn = din("w_gn", [D, 24]); w_gm = din("w_gm", [D, D]); w_gnm = din("w_gnm", [D, D])
    qag = din("qag", [128, 6]); kvag = din("kvag", [128, 2])
    w_qb = din("w_qb", [768, 768]); w_kvb = din("w_kvb", [256, 1024])
    qgB = din("qgB", [128, 96]); kgB = din("kgB", [128, 96])
    nqB = din("nqB", [128, 64]); nkcB = din("nkcB", [128, 64]); nksB = din("nksB", [128, 64]); nkwB = din("nkwB", [128, 64])
    posk = din("posk", [128, 16]); w1k = din("w1k", [2048, 256]); w2k = din("w2k", [256, 64])
    posv = din("posv", [128, 16]); w1v = din("w1v", [2048, 256]); w2v = din("w2v", [256, 64])
    wo_mla = din("wo_mla", [512, D]); wo_nsa = din("wo_nsa", [512, D]); w_out = din("w_out", [D, D])
    wg = din("wg", [D, DFF]); wu = din("wu", [D, DFF]); wd = din("wd", [DFF, D])
    ident_d = din("ident", [128, 128], BF16); tri_d = din("tri", [128, 128], BF16); winm_d = din("winm", [128, 384], BF16)
    inv16_d = din("inv16", [128, 16]); inv8_d = din("inv8", [128, 8])
    XEall_d = din("XEall", [32, S_], BF16); ovl_d = din("ovl", [127, 32], BF16); vmT_d = din("vmT", [127, S_], BF16); XE_d = din("XE", [32, NT * 128], BF16)
    fb_d = din("fb", [128, NT * 32]); vj_d = din("vj", [128, NT * 32])
    out = nc.dram_tensor("out", [S_, D], F32, kind="ExternalOutput").ap()
    hTs = nc.dram_tensor("hTs", [128, 8 * S_], BF16).ap()
    mods = nc.dram_tensor("mods", [128, 6 * D], F32).ap()
    x1s = nc.dram_tensor("x1s", [S_, D], F32).ap()
    D_ = {}
    for name, shape, dt in dbg:
        D_[name] = nc.dram_tensor("dbg_" + name, list(shape), dt, kind="ExternalOutput").ap()

    st = contextlib.ExitStack()
    with st:
        S = Sched(nc, st)
        NB = 204800
        big = st.enter_context(nc.sbuf_tensor("big", [128, NB // 2], BF16))
        M = Mem(big, NB)
        P = [st.enter_context(nc.psum_tensor(f"ps{i}", [128, 512], F32)) for i in range(8)]
        PK = [f"ps{i}" for i in range(8)]
        bank_state = [0]

        nbanks = [8]

        def nb():
            b = bank_state[0] % nbanks[0]
            bank_state[0] = (b + 1) % nbanks[0]
            return b

        def mm(ps_ap, lhsT, rhs, start, stop, r, w, sig=None, **kw):
            S.op("pe", lambda e: e.matmul(ps_ap, lhsT=lhsT, rhs=rhs, start=start, stop=stop, **kw), r=r, w=w, signal=bool(stop) if sig is None else sig)

        def tp(ps_ap, in_, ident_ap, r, w):
            S.op("pe", lambda e: e.transpose(out=ps_ap, in_=in_, identity=ident_ap), r=r, w=w)

        def act(out_, in_, func, r, w, **kw):
            S.op("act", lambda e: e.activation(out=out_, in_=in_, func=func, **kw), r=r, w=w)

        def dbg_out(name, ap, r):
            if name in D_:
                S.dma("sp", D_[name], ap, r=r)

        ident = M.lo_alloc([128], BF16); tri = M.lo_alloc([128], BF16); winm = M.lo_alloc([3, 128], BF16)
        onesb = M.lo_alloc([128], BF16)
        stg = [M.lo_alloc([1024], F32) for _ in range(3)]
        stg_i = [0]
        cosM = M.lo_alloc([NT, 16], F32); sinM = M.lo_alloc([NT, 16], F32)
        cosN = M.lo_alloc([NT, 8], F32); sinN = M.lo_alloc([NT, 8], F32)
        cosC = M.lo_alloc([8], F32); sinC = M.lo_alloc([8], F32)
        lo_pers = M.lo
        omT = M.lo_alloc([4, S_], BF16); onT = M.lo_alloc([4, S_], BF16)
        S.dma("sp", ident, ident_d, w=["ident"])
        S.dma("sp", tri, tri_d, w=["tri"])
        S.dma("sp", winm.rearrange("p a b -> p (a b)"), winm_d, w=["winm"])
        S.op("pool", lambda e: e.memset(onesb, 1.0), w=["onesb"])

        def load_w(dst, W, key, KC, N, ceng="dve"):
            Wv = W.rearrange("(k p) n -> p k n", p=128)
            if N <= 1024:
                g = max(1, min(KC, 1024 // N))
                for k0 in range(0, KC, g):
                    k1 = min(KC, k0 + g)
                    si = stg_i[0]; stg_i[0] = (si + 1) % 3
                    sv = stg[si][:, 0:(k1 - k0) * N].rearrange("p (k n) -> p k n", n=N)
                    S.dma("sp", sv, Wv[:, k0:k1, :], w=[("stg", si)])
                    S.op(ceng, lambda e, sv=sv, k0=k0, k1=k1: e.tensor_copy(out=dst[:, k0:k1, :], in_=sv), r=[("stg", si)], w=[key])
            else:
                for k in range(KC):
                    for c0 in range(0, N, 1024):
                        c1 = min(N, c0 + 1024)
                        si = stg_i[0]; stg_i[0] = (si + 1) % 3
                        sv = stg[si][:, 0:c1 - c0]
                        S.dma("sp", sv, Wv[:, k, c0:c1], w=[("stg", si)])
                        S.op(ceng, lambda e, sv=sv, k=k, c0=c0, c1=c1: e.tensor_copy(out=dst[:, k, c0:c1], in_=sv), r=[("stg", si)], w=[key])

        def load_w_cols(dst, W, key, KC, c0, c1, ceng="dve"):
            Wv = W.rearrange("(k p) n -> p k n", p=128)
            N = c1 - c0
            g = max(1, min(KC, 1024 // N))
            for k0 in range(0, KC, g):
                k1 = min(KC, k0 + g)
                si = stg_i[0]; stg_i[0] = (si + 1) % 3
                sv = stg[si][:, 0:(k1 - k0) * N].rearrange("p (k n) -> p k n", n=N)
                S.dma("sp", sv, Wv[:, k0:k1, c0:c1], w=[("stg", si)])
                S.op(ceng, lambda e, sv=sv, k0=k0, k1=k1: e.tensor_copy(out=dst[:, k0:k1, :], in_=sv), r=[("stg", si)], w=[key])

        def sincos(ang, shape, cos_o, sin_o, np_, tmp_f, tmp_i, tmp_m, key):
            for (shift, dst) in ((0.0, sin_o), (PI / 2, cos_o)):
                S.op("dve", lambda e: e.tensor_scalar(out=tmp_f, in0=ang, scalar1=shift, scalar2=None, op0=ALU.add), r=[key + "ang"], w=[key + "f"])
                S.op("dve", lambda e: e.tensor_scalar(out=tmp_i, in0=tmp_f, scalar1=float(1 / TWO_PI), scalar2=None, op0=ALU.mult), r=[key + "f"], w=[key + "i"])
                S.op("dve", lambda e: e.tensor_copy(out=tmp_m, in_=tmp_i), r=[key + "i"], w=[key + "m"])
                S.op("dve", lambda e: e.scalar_tensor_tensor(out=tmp_f, in0=tmp_m, scalar=-TWO_PI, in1=tmp_f, op0=ALU.mult, op1=ALU.add), r=[key + "m", key + "f"], w=[key + "f"])
                S.op("dve", lambda e: e.tensor_scalar(out=tmp_m, in0=tmp_f, scalar1=PI, scalar2=None, op0=ALU.is_gt), r=[key + "f"], w=[key + "m"])
                S.op("dve", lambda e: e.scalar_tensor_tensor(out=tmp_f, in0=tmp_m, scalar=-TWO_PI, in1=tmp_f, op0=ALU.mult, op1=ALU.add), r=[key + "m", key + "f"], w=[key + "f"])
                S.op("dve", lambda e: e.tensor_scalar(out=tmp_m, in0=tmp_f, scalar1=-PI, scalar2=None, op0=ALU.is_lt), r=[key + "f"], w=[key + "m"])
                S.op("dve", lambda e: e.scalar_tensor_tensor(out=tmp_f, in0=tmp_m, scalar=TWO_PI, in1=tmp_f, op0=ALU.mult, op1=ALU.add), r=[key + "m", key + "f"], w=[key + "f"])
                act(dst, tmp_f, AF.Sin, r=[key + "f"], w=[key + "out"])

        lo0, hi0 = M.lo, M.hi
        if upto <= -2:
            return finish(nc, S, st)
        posi = M.lo_alloc([NT], I32); posf = M.lo_alloc([NT], F32)
        posCi = M.lo_alloc([1], I32); posCf = M.lo_alloc([1], F32)
        inv16 = M.lo_alloc([16], F32); inv8 = M.lo_alloc([8], F32)
        angM = M.lo_alloc([NT, 16], F32); tfM = M.lo_alloc([NT, 16], F32); tiM = M.lo_alloc([NT, 16], I32); tmM = M.lo_alloc([NT, 16], F32)
        S.dma("sp", posi, pos_pk, w=["posi"])
        S.dma("sp", posCi[0:127], posC, w=["posCi"])
        S.dma("sp", inv16, inv16_d, w=["inv16"])
        S.dma("sp", inv8, inv8_d, w=["inv8"])
        S.op("dve", lambda e: e.tensor_copy(out=posf, in_=posi), r=["posi"], w=["posf"])
        S.op("dve", lambda e: e.tensor_copy(out=posCf[0:127], in_=posCi[0:127]), r=["posCi"], w=["posCf"])
        S.op("dve", lambda e: e.tensor_tensor(out=angM, in0=posf.unsqueeze(2).to_broadcast([128, NT, 16]),
                                              in1=inv16.unsqueeze(1).to_broadcast([128, NT, 16]), op=ALU.mult), r=["posf", "inv16"], w=["Mang"])
        sincos(angM, None, cosM, sinM, 128, tfM, tiM, tmM, "M")
        a8 = angM.rearrange("p a b -> p (a b)")[:, 0:NT * 8].rearrange("p (a b) -> p a b", b=8)
        f8 = tfM.rearrange("p a b -> p (a b)")[:, 0:NT * 8].rearrange("p (a b) -> p a b", b=8)
        i8 = tiM.rearrange("p a b -> p (a b)")[:, 0:NT * 8].rearrange("p (a b) -> p a b", b=8)
        m8_ = tmM.rearrange("p a b -> p (a b)")[:, 0:NT * 8].rearrange("p (a b) -> p a b", b=8)
        S.op("dve", lambda e: e.tensor_tensor(out=a8, in0=posf.unsqueeze(2).to_broadcast([128, NT, 8]),
                                              in1=inv8.unsqueeze(1).to_broadcast([128, NT, 8]), op=ALU.mult), r=["posf", "inv8", "Mout", "Mf", "Mm", "Mi"], w=["Nang"])
        sincos(a8, None, cosN, sinN, 128, f8, i8, m8_, "N")
        aC = angM.rearrange("p a b -> p (a b)")[0:127, 0:8]
        fC = tfM.rearrange("p a b -> p (a b)")[0:127, 0:8]
        iC = tiM.rearrange("p a b -> p (a b)")[0:127, 0:8]
        mC = tmM.rearrange("p a b -> p (a b)")[0:127, 0:8]
        S.op("dve", lambda e: e.tensor_scalar(out=aC, in0=inv8[0:127], scalar1=posCf[0:127, 0:1], scalar2=None, op0=ALU.mult),
             r=["posCf", "inv8", "Nout", "Nf", "Nm", "Ni", "Nang"], w=["Cang"])
        sincos(aC, None, cosC[0:127], sinC[0:127], 127, fC, iC, mC, "C")
        dbg_out("cosM", cosM, ["Mout"]); dbg_out("sinM", sinM, ["Mout"])

        if upto <= -1:
            return finish(nc, S, st)
        cpk = M.lo_alloc([8], F32); sc = M.lo_alloc([8], F32)
        sch = M.lo_alloc([8], BF16); scl = M.lo_alloc([8], BF16)
        cBh = M.lo_alloc([8, 128], BF16); cBl = M.lo_alloc([8, 128], BF16)
        modB = M.lo_alloc([6 * D], F32)
        g1t = M.lo_alloc([D], F32); g2t = M.lo_alloc([D], F32)
        awb = [M.hi_alloc([8, 512], F32) for _ in range(3)]
        abb = [M.hi_alloc([512], F32) for _ in range(3)]
        awh = [M.hi_alloc([8, 512], BF16) for _ in range(3)]
        awl = [M.hi_alloc([8, 512], BF16) for _ in range(3)]
        S.dma("sp", cpk, c_pk, w=["cpk"])
        S.dma("sp", g1t, g1B, w=["g1t"]); S.dma("sp", g2t, g2B, w=["g2t"])
        act(sc, cpk, AF.Silu, r=["cpk"], w=["sc"])
        S.op("dve", lambda e: e.tensor_copy(out=sch, in_=sc), r=["sc"], w=["sch"])
        S.op("dve", lambda e: e.tensor_tensor(out=scl, in0=sc, in1=sch, op=ALU.subtract), r=["sc", "sch"], w=["scl"])
        for k in range(8):
            S.op("dve", lambda e, k=k: e.tensor_copy(out=cBh[:, k, :], in_=sch[:, k:k + 1].to_broadcast([128, 128])), r=["sch"], w=["cBh"])
            S.op("dve", lambda e, k=k: e.tensor_copy(out=cBl[:, k, :], in_=scl[:, k:k + 1].to_broadcast([128, 128])), r=["scl"], w=["cBl"])
        awv = ada_w.rearrange("(k p) n -> p k n", p=128)
        for n in range(12):
            q_ = n % 3
            S.dma("sp", awb[q_], awv[:, :, n * 512:(n + 1) * 512], w=[("awb", q_)])
            S.dma("sp", abb[q_], adabB[:, n * 512:(n + 1) * 512], w=[("abb", q_)])
            act(awh[q_], awb[q_], AF.Copy, r=[("awb", q_)], w=[("awh", q_)])
            S.op("dve", lambda e, q_=q_: e.tensor_tensor(out=awl[q_], in0=awb[q_], in1=awh[q_], op=ALU.subtract), r=[("awb", q_), ("awh", q_)], w=[("awl", q_)])
            b = nb()
            passes = [(cBh, "cBh", awh, "awh"), (cBh, "cBh", awl, "awl"), (cBl, "cBl", awh, "awh")]
            for pi_, (cb_, ck, ww, wk) in enumerate(passes):
                for k in range(8):
                    mm(P[b][:, :], cb_[:, k, :], ww[q_][:, k, :], pi_ == 0 and k == 0, pi_ == 2 and k == 7, r=[ck, (wk, q_)], w=[PK[b]])
            S.op("dve", lambda e, n=n, b=b, q_=q_: e.tensor_tensor(out=modB[:, n * 512:(n + 1) * 512], in0=P[b][:, :], in1=abb[q_], op=ALU.add),
                 r=[PK[b], ("abb", q_)], w=["modB"])
        S.op("dve", lambda e: e.scalar_tensor_tensor(out=modB[:, D:2 * D], in0=modB[:, D:2 * D], scalar=1.0, in1=g1t, op0=ALU.add, op1=ALU.mult), r=["modB", "g1t"], w=["modB"])
        S.op("dve", lambda e: e.scalar_tensor_tensor(out=modB[:, 4 * D:5 * D], in0=modB[:, 4 * D:5 * D], scalar=1.0, in1=g2t, op0=ALU.add, op1=ALU.mult), r=["modB", "g2t"], w=["modB"])
        S.dma("sp", mods, modB, r=["modB"], w=["mods"])
        dbg_out("modB", modB, ["modB"])
        B1 = modB[:, 0:D]; A1 = modB[:, D:2 * D]
        if upto <= 0:
            return finish(nc, S, st)

        M.hi = hi0
        hT = M.hi_alloc([8, S_], BF16)
        xb = [M.lo_alloc([D], F32) for _ in range(2)]
        junk = M.lo_alloc([D], F32); tmpA = M.lo_alloc([D], F32)
        hb = [M.lo_alloc([D], BF16) for _ in range(2)]
        ssq = M.lo_alloc([NT], F32); rs = M.lo_alloc([NT], F32)

        tmpA2 = [tmpA, M.lo_alloc([D], F32)]

        def norm_a(i, xt, xkey):
            act(junk, xt, AF.Square, r=[xkey], w=["junk", ("ssq", i)], accum_out=ssq[:, i:i + 1])
            act(rs[:, i:i + 1], ssq[:, i:i + 1], AF.Sqrt, r=[("ssq", i)], w=[("rs", i)], scale=1.0 / D, bias=EPS)
            S.op("dve", lambda e: e.reciprocal(out=rs[:, i:i + 1], in_=rs[:, i:i + 1]), r=[("rs", i)], w=[("rs", i)])

        def norm_b1(i, xt, xkey, A, B, Akeys):
            par = i % 2
            S.op("dve", lambda e: e.scalar_tensor_tensor(out=tmpA2[par], in0=xt, scalar=rs[:, i:i + 1], in1=A, op0=ALU.mult, op1=ALU.mult),
                 r=[xkey, ("rs", i)] + Akeys, w=[("tmpA2", par)])
            S.op("pool", lambda e: e.tensor_tensor(out=hb[par][:, 0:512], in0=tmpA2[par][:, 0:512], in1=B[:, 0:512], op=ALU.add), r=[("tmpA2", par)] + Akeys, w=[("hb", par, 0)])
            S.op("dve", lambda e: e.tensor_tensor(out=hb[par][:, 512:1024], in0=tmpA2[par][:, 512:1024], in1=B[:, 512:1024], op=ALU.add), r=[("tmpA2", par)] + Akeys, w=[("hb", par, 1)])
            b = nb()
            pb = P[b][:, :].bitcast(BF16)
            for k in range(8):
                tp(pb[:, k * 128:(k + 1) * 128], hb[par][:, k * 128:(k + 1) * 128], ident, r=[("hb", par, k // 4), "ident"], w=[PK[b]])
            return b

        def norm_b2(i, b, dstT, dkey):
            pb = P[b][:, :].bitcast(BF16)
            act(dstT[:, :, i * 128:(i + 1) * 128], pb.rearrange("p (k t) -> p k t", k=8), AF.Copy, r=[PK[b]], w=[(dkey, i)])

        xb = xb + [M.lo_alloc([D], F32), M.lo_alloc([D], F32)]

        def a_front(i):
            S.dma("sp", xb[i % 4], x[i * 128:(i + 1) * 128, :], w=[("xb", i % 4)])
            norm_a(i, xb[i % 4], ("xb", i % 4))

        a_front(0); a_front(1)
        for i in range(NT):
            b_ = norm_b1(i, xb[i % 4], ("xb", i % 4), A1, B1, ["modB"])
            if i + 2 < NT:
                a_front(i + 2)
            norm_b2(i, b_, hT, "hT")
        hTk = [("hT", i) for i in range(NT)]
        S.dma("sp", hTs, hT.rearrange("p k t -> p (k t)"), r=hTk, w=["hTs"])
        dbg_out("hT", hT.rearrange("p k t -> p (k t)"), hTk)
        if upto <= 1:
            return finish(nc, S, st)
        S.barrier()

        nbanks[0] = 6
        M.lo = lo0
        cqT = M.lo_alloc([6, S_], BF16); ckvT = M.lo_alloc([2, S_], BF16); kpe = M.lo_alloc([NT, 32], F32)
        lo1 = M.lo
        Wcq = M.lo_alloc([8, 768], BF16); Wckv = M.lo_alloc([8, 256], BF16); Wkpe = M.lo_alloc([8, 32], BF16)
        qagt = M.lo_alloc([6], F32); kvagt = M.lo_alloc([2], F32)
        sqb = [M.lo_alloc([512], BF16) for _ in range(2)]
        rb = M.lo_alloc([512], F32)
        S.dma("sp", qagt, qag, w=["qagt"]); S.dma("sp", kvagt, kvag, w=["kvagt"])
        load_w(Wcq, w_cq, "Wcq", 8, 768); load_w(Wckv, w_ckv, "Wckv", 8, 256); load_w(Wkpe, w_kpe, "Wkpe", 8, 32)

        def fm_proj_norm(dstT, dkey, Wt, wkey, nf, gaint, gkey, nfeat):
            for c in range(4):
                hk = [("hT", 4 * c + q) for q in range(4)]
                for j in range(nf + 1):
                    if j < nf:
                        b = nb()
                        for k in range(8):
                            mm(P[b][:, :], Wt[:, k, j * 128:(j + 1) * 128], hT[:, k, c * 512:(c + 1) * 512], k == 0, k == 7, r=[wkey] + hk, w=[PK[b]])
                    if j >= 1:
                        jj = j - 1
                        mm(P[6][:, :], onesb, sqb[jj % 2], jj == 0, jj == nf - 1, r=["onesb", ("sqb", jj % 2)], w=[PK[6]], sig=True)
                    if j < nf:
                        act(sqb[j % 2], P[b][:, :], AF.Square, r=[PK[b]], w=[("sqb", j % 2)])
                        act(dstT[:, j, c * 512:(c + 1) * 512], P[b][:, :], AF.Copy, r=[PK[b]], w=[(dkey, c)])
                act(rb, P[6][:, :], AF.Sqrt, r=[PK[6]], w=["rb"], scale=1.0 / nfeat, bias=EPS)
                S.op("dve", lambda e: e.reciprocal(out=rb, in_=rb), r=["rb"], w=["rb"])
                for j in range(nf):
                    S.op("dve", lambda e, j=j, c=c: e.scalar_tensor_tensor(out=dstT[:, j, c * 512:(c + 1) * 512], in0=dstT[:, j, c * 512:(c + 1) * 512],
                                                                             scalar=gaint[:, j:j + 1], in1=rb, op0=ALU.mult, op1=ALU.mult),
                         r=[(dkey, c), "rb", gkey], w=[(dkey, c)])

        fm_proj_norm(cqT, "cqT", Wcq, "Wcq", 6, qagt, "qagt", 768)
        fm_proj_norm(ckvT, "ckvT", Wckv, "Wckv", 2, kvagt, "kvagt", 256)
        for i in range(NT):
            b = nb()
            for k in range(8):
                mm(P[b][:, 0:32], hT[:, k, i * 128:(i + 1) * 128], Wkpe[:, k, :], k == 0, k == 7, r=["Wkpe", ("hT", i)], w=[PK[b]])
            S.op("dve", lambda e, i=i, b=b: e.tensor_copy(out=kpe[:, i, :], in_=P[b][:, 0:32]), r=[PK[b]], w=[("kpe", i)])
        cqk = [("cqT", c) for c in range(4)]
        dbg_out("cqT", cqT.rearrange("p k t -> p (k t)"), cqk)
        dbg_out("kpe", kpe.rearrange("p a b -> p (a b)"), [("kpe", i) for i in range(NT)])
        if upto <= 2:
            return finish(nc, S, st)
        S.barrier()

        nbanks[0] = 8
        M.lo = lo1
        M.hi = hi0
        QT = M.hi_alloc([8, S_], BF16); KT = M.hi_alloc([8, S_], BF16); V = M.hi_alloc([NT, 8, 65], BF16)
        hi1 = M.hi
        Wqb = M.lo_alloc([6, 768], BF16); Wkvb = M.lo_alloc([2, 1024], BF16)
        qgt = M.lo_alloc([96], F32); kgt = M.lo_alloc([96], F32)
        drq = [M.lo_alloc([768], BF16) for _ in range(2)]; drk = [M.lo_alloc([768], BF16) for _ in range(2)]
        S.dma("sp", qgt, qgB, w=["qgt"]); S.dma("sp", kgt, kgB, w=["kgt"])
        load_w(Wqb, w_qb, "Wqb", 6, 768); load_w(Wkvb, w_kvb, "Wkvb", 2, 1024)
        S.op("pool", lambda e: e.memset(V[:, :, :, 64:65], 1.0), w=["Vones"])

        def mk_tmps(Mx, n, H, hf):
            return dict(t1=Mx.lo_alloc([n], F32), hs=Mx.lo_alloc([H], F32), hr=Mx.lo_alloc([H], F32),
                        ra=Mx.lo_alloc([H * hf], F32), rb=Mx.lo_alloc([H * hf], F32), ra2=Mx.lo_alloc([H * hf], F32), rb2=Mx.lo_alloc([H * hf], F32))

        def hnr_stages(tag, T, src, skeys, H, Dh, gaint, gkey, ro, hf, cos_, sin_, dst, dkey, np_=128):
            n = H * Dh
            t1v = T["t1"][0:np_, 0:n].rearrange("p (h d) -> p h d", h=H)
            hs = T["hs"][0:np_, 0:H]; hr = T["hr"][0:np_, 0:H]
            x1 = t1v[:, :, ro:ro + hf]; x2 = t1v[:, :, ro + hf:ro + 2 * hf]
            cb = cos_.unsqueeze(1).to_broadcast([np_, H, hf]); sb_ = sin_.unsqueeze(1).to_broadcast([np_, H, hf])
            rv = {k: T[k][0:np_, 0:H * hf].rearrange("p (h d) -> p h d", h=H) for k in ("ra", "rb", "ra2", "rb2")}
            tr = ["Mout", "Nout", "Cout"]
            k_ = lambda nm: (tag, nm)
            st = []
            st.append(lambda: act(t1v, src, AF.Square, r=skeys, w=[k_("t1")]))
            st.append(lambda: S.op("dve", lambda e: e.tensor_reduce(out=hs, in_=t1v, axis=AX.X, op=ALU.add), r=[k_("t1")], w=[k_("hs")]))
            st.append(lambda: act(hr, hs, AF.Sqrt, r=[k_("hs")], w=[k_("hr")], scale=1.0 / Dh, bias=EPS))
            st.append(lambda: S.op("dve", lambda e: e.reciprocal(out=hr, in_=hr), r=[k_("hr")], w=[k_("hr")]))
            st.append(lambda: S.op("dve", lambda e: e.tensor_tensor(out=t1v, in0=src, in1=hr.unsqueeze(2).to_broadcast([np_, H, Dh]), op=ALU.mult), r=skeys + [k_("hr"), k_("hs")], w=[k_("t1")]))
            st.append(lambda: S.op("dve", lambda e: e.tensor_tensor(out=t1v, in0=t1v, in1=gaint[0:np_].unsqueeze(1).to_broadcast([np_, H, Dh]), op=ALU.mult), r=[k_("t1"), gkey], w=[k_("t1")]))
            st.append(lambda: S.op("dve", lambda e: e.tensor_tensor(out=rv["ra"], in0=x1, in1=cb, op=ALU.mult), r=[k_("t1")] + tr, w=[k_("ra")]))
            st.append(lambda: S.op("dve", lambda e: e.tensor_tensor(out=rv["rb"], in0=x2, in1=sb_, op=ALU.mult), r=[k_("t1")] + tr, w=[k_("rb")]))
            st.append(lambda: S.op("dve", lambda e: e.tensor_tensor(out=dst[:, :, ro:ro + hf], in0=rv["ra"], in1=rv["rb"], op=ALU.subtract), r=[k_("ra"), k_("rb")], w=[dkey]))
            st.append(lambda: S.op("dve", lambda e: e.tensor_tensor(out=rv["ra2"], in0=x2, in1=cb, op=ALU.mult), r=[k_("t1")] + tr, w=[k_("ra2")]))
            st.append(lambda: S.op("dve", lambda e: e.tensor_tensor(out=rv["rb2"], in0=x1, in1=sb_, op=ALU.mult), r=[k_("t1")] + tr, w=[k_("rb2")]))
            st.append(lambda: S.op("dve", lambda e: e.tensor_tensor(out=dst[:, :, ro + hf:ro + 2 * hf], in0=rv["ra2"], in1=rv["rb2"], op=ALU.add), r=[k_("ra2"), k_("rb2")], w=[dkey]))

            def copies():
                if ro > 0:
                    S.op("pool", lambda e: e.tensor_copy(out=dst[:, :, 0:ro], in_=t1v[:, :, 0:ro]), r=[k_("t1")], w=[dkey])
                if ro + 2 * hf < Dh:
                    S.op("pool", lambda e: e.tensor_copy(out=dst[:, :, ro + 2 * hf:Dh], in_=t1v[:, :, ro + 2 * hf:Dh]), r=[k_("t1")], w=[dkey])
            st.insert(6, copies)
            return st

        def run_interleaved(chains):
            for s_ in range(max(len(c_) for c_ in chains)):
                for c_ in chains:
                    if s_ < len(c_):
                        c_[s_]()

        def head_norm_rope(src, skeys, H, Dh, gaint, gkey, ro, hf, cos_, sin_, dst, dkey, np_=128):
            n = H * Dh
            t1v = t1[0:np_, 0:n].rearrange("p (h d) -> p h d", h=H)
            t2v = t2[0:np_, 0:n].rearrange("p (h d) -> p h d", h=H)
            hs = hss[0:np_, 0:H]; hr = hrs[0:np_, 0:H]
            act(t1v, src, AF.Square, r=skeys, w=["t1"])
            S.op("dve", lambda e: e.tensor_reduce(out=hs, in_=t1v, axis=AX.X, op=ALU.add), r=["t1"], w=["hss"])
            act(hr, hs, AF.Sqrt, r=["hss"], w=["hrs"], scale=1.0 / Dh, bias=EPS)
            S.op("dve", lambda e: e.reciprocal(out=hr, in_=hr), r=["hrs"], w=["hrs"])
            S.op("dve", lambda e: e.tensor_tensor(out=t2v, in0=src, in1=hr.unsqueeze(2).to_broadcast([np_, H, Dh]), op=ALU.mult), r=skeys + ["hrs"], w=["t2"])
            S.op("dve", lambda e: e.tensor_tensor(out=t1v, in0=t2v, in1=gaint[0:np_].unsqueeze(1).to_broadcast([np_, H, Dh]), op=ALU.mult), r=["t2", gkey], w=["t1"])
            x1 = t1v[:, :, ro:ro + hf]; x2 = t1v[:, :, ro + hf:ro + 2 * hf]
            cb = cos_.unsqueeze(1).to_broadcast([np_, H, hf]); sb_ = sin_.unsqueeze(1).to_broadcast([np_, H, hf])
            rav = ra[0:np_, 0:H * hf].rearrange("p (h d) -> p h d", h=H)
            rbv = rbb[0:np_, 0:H * hf].rearrange("p (h d) -> p h d", h=H)
            tr = ["Mout", "Nout", "Cout"]
            S.op("dve", lambda e: e.tensor_tensor(out=rav, in0=x1, in1=cb, op=ALU.mult), r=["t1"] + tr, w=["ra"])
            S.op("dve", lambda e: e.tensor_tensor(out=rbv, in0=x2, in1=sb_, op=ALU.mult), r=["t1"] + tr, w=["rbb"])
            S.op("dve", lambda e: e.tensor_tensor(out=dst[:, :, ro:ro + hf], in0=rav, in1=rbv, op=ALU.subtract), r=["ra", "rbb"], w=[dkey])
            S.op("dve", lambda e: e.tensor_tensor(out=rav, in0=x2, in1=cb, op=ALU.mult), r=["t1"] + tr, w=["ra"])
            S.op("dve", lambda e: e.tensor_tensor(out=rbv, in0=x1, in1=sb_, op=ALU.mult), r=["t1"] + tr, w=["rbb"])
            S.op("dve", lambda e: e.tensor_tensor(out=dst[:, :, ro + hf:ro + 2 * hf], in0=rav, in1=rbv, op=ALU.add), r=["ra", "rbb"], w=[dkey])
            if ro > 0:
                S.op("pool", lambda e: e.tensor_copy(out=dst[:, :, 0:ro], in_=t1v[:, :, 0:ro]), r=["t1"], w=[dkey])
            if ro + 2 * hf < Dh:
                S.op("pool", lambda e: e.tensor_copy(out=dst[:, :, ro + 2 * hf:Dh], in_=t1v[:, :, ro + 2 * hf:Dh]), r=["t1"], w=[dkey])

        Msub = Mem(big, NB); Msub.lo = lo_pers; Msub.hi = lo0
        rawq = [Msub.lo_alloc([768], F32) for _ in range(2)]; rawk = [Msub.lo_alloc([768], F32), M.lo_alloc([768], F32)]
        Tq = [mk_tmps(Msub, 768, 8, 16) for _ in range(2)]; Tk = [mk_tmps(Msub, 768, 8, 16) for _ in range(2)]

        def b2_front(i):
            ts = slice(i * 128, (i + 1) * 128)
            par = i % 2
            bA, bB = nb(), nb()
            for k in range(6):
                mm(P[bA][:, :], cqT[:, k, ts], Wqb[:, k, 0:512], k == 0, k == 5, r=["Wqb", ("cqT", i // 4)], w=[PK[bA]])
            for k in range(6):
                mm(P[bB][:, 0:256], cqT[:, k, ts], Wqb[:, k, 512:768], k == 0, k == 5, r=["Wqb", ("cqT", i // 4)], w=[PK[bB]])
            act(rawq[par][:, 0:512], P[bA][:, :], AF.Copy, r=[PK[bA]], w=[("rawq", par)])
            act(rawq[par][:, 512:768], P[bB][:, 0:256], AF.Copy, r=[PK[bB]], w=[("rawq", par)])
            bA, bB = nb(), nb()
            for hh, bb in ((0, bA), (1, bB)):
                for k in range(2):
                    mm(P[bb][:, :], ckvT[:, k, ts], Wkvb[:, k, hh * 512:(hh + 1) * 512], k == 0, k == 1, r=["Wkvb", ("ckvT", i // 4)], w=[PK[bb]])
            rv = rawk[par].rearrange("p (h d) -> p h d", h=8)
            for hh, bb in ((0, bA), (1, bB)):
                pv = P[bb][:, :].rearrange("p (h d) -> p h d", h=4)
                act(rv[:, hh * 4:(hh + 1) * 4, 0:64], pv[:, :, 0:64], AF.Copy, r=[PK[bb]], w=[("rawk", par)])
                act(V[:, i, hh * 4:(hh + 1) * 4, 0:64], pv[:, :, 64:128], AF.Copy, r=[PK[bb]], w=[("V", i)])
            S.op("pool", lambda e, i=i: e.tensor_copy(out=rv[:, :, 64:96], in_=kpe[:, i, :].unsqueeze(1).to_broadcast([128, 8, 32])), r=[("kpe", i)], w=[("rawk", par)])

        def b2_chains(i):
            par = i % 2
            dq = drq[par].rearrange("p (h d) -> p h d", h=8); dk = drk[par].rearrange("p (h d) -> p h d", h=8)
            cq_ = hnr_stages(("cq", par), Tq[par], rawq[par].rearrange("p (h d) -> p h d", h=8), [("rawq", par)], 8, 96, qgt, "qgt", 64, 16, cosM[:, i, :], sinM[:, i, :], dq, ("drq", par))
            ck_ = hnr_stages(("ck", par), Tk[par], rawk[par].rearrange("p (h d) -> p h d", h=8), [("rawk", par)], 8, 96, kgt, "kgt", 64, 16, cosM[:, i, :], sinM[:, i, :], dk, ("drk", par))
            return [cq_[:6], ck_[:6]], [cq_[6:], ck_[6:]]

        def b2_out(i):
            ts = slice(i * 128, (i + 1) * 128)
            par = i % 2
            dq = drq[par].rearrange("p (h d) -> p h d", h=8); dk = drk[par].rearrange("p (h d) -> p h d", h=8)
            for (dd, dkey_, dstT, okey) in ((dq, ("drq", par), QT, "QT"), (dk, ("drk", par), KT, "KT")):
                b = nb(); pb = P[b][:, :].bitcast(BF16)
                for h in range(8):
                    tp(pb[0:96, h * 128:(h + 1) * 128], dd[:, h, :], ident, r=[dkey_, "ident"], w=[PK[b]])
                act(dstT[0:96, :, ts], pb[0:96, :].rearrange("p (h t) -> p h t", h=8), AF.Copy, r=[PK[b]], w=[(okey, i)])

        b2_front(0); b2_front(1)
        h1_, h2_ = b2_chains(0)
        run_interleaved(h1_)
        for i in range(NT):
            if i + 2 < NT:
                b2_front(i + 2)
            nxt = b2_chains(i + 1) if i + 1 < NT else ([], [])
            run_interleaved(nxt[0] + h2_)
            h2_ = nxt[1]
            if i >= 1:
                b2_out(i - 1)
        b2_out(NT - 1)
        QTk = [("QT", i) for i in range(NT)]
        dbg_out("QT", QT[0:96].rearrange("p k t -> p (k t)"), QTk)
        dbg_out("KT", KT[0:96].rearrange("p k t -> p (k t)"), [("KT", i) for i in range(NT)])
        dbg_out("V", V.rearrange("p a b c -> p (a b c)"), [("V", i) for i in range(NT)] + ["Vones"])
        if upto <= 3:
            return finish(nc, S, st)
        S.barrier()

        nbanks[0] = 6
        M.lo = lo0
        om = M.lo_alloc([NT, 512], BF16)
        PT = [M.lo_alloc([512], BF16) for _ in range(4)]
        rec4 = [M.lo_alloc([4], F32) for _ in range(2)]
        pt_i = [0]

        gchunk = [0]

        def causal_attn_multi(jobs):
            steps = []
            for ji in range(len(jobs)):
                for c in range(4):
                    for kt in range(4 * c + 4):
                        steps.append((ji, c, kt))

            def emit_qk(step):
                ji, c, kt = step
                J = jobs[ji]
                q0 = max(kt - 4 * c, 0)
                n = 512 - 128 * q0
                b = nb()
                has_extra = J["extra"] is not None
                mm(P[b][:, 0:n], J["KT"][0:J["kn"], kt * 128:(kt + 1) * 128], J["QT"](c * 512 + q0 * 128, (c + 1) * 512),
                   True, not has_extra, r=J["kk"](kt) + J["qk"](c), w=[PK[b]])
                if has_extra:
                    J["extra"](P[b][:, 0:n], kt, c * 512 + q0 * 128, (c + 1) * 512, PK[b])
                return b, n, q0

            LA = 2
            pend = [emit_qk(steps[q]) for q in range(min(LA, len(steps)))]
            for si, (ji, c, kt) in enumerate(steps):
                J = jobs[ji]
                b, n, q0 = pend.pop(0)
                if si + LA < len(steps):
                    pend.append(emit_qk(steps[si + LA]))
                if kt == 0:
                    gchunk[0] += 1
                ab = 6 + (gchunk[0] % 2)
                Oacc = P[ab][:, 0:260].rearrange("p (q d) -> p q d", q=4)
                pi = pt_i[0]; pt_i[0] = (pi + 1) % len(PT)
                pt = PT[pi]
                act(pt[:, 0:n], P[b][:, 0:n], AF.Exp, r=[PK[b]], w=[("PT", pi)], scale=J["scale"])
                if kt >= 4 * c:
                    S.op("dve", lambda e, pt=pt: e.tensor_tensor(out=pt[:, 0:128], in0=pt[:, 0:128], in1=tri, op=ALU.mult), r=[("PT", pi), "tri"], w=[("PT", pi)])
                for qi in range(q0, 4):
                    mm(Oacc[:, qi, :], pt[:, (qi - q0) * 128:(qi - q0 + 1) * 128], J["V"](kt), kt == 0 and qi == 0, kt == 4 * c + qi,
                       r=[("PT", pi)] + J["vk"](kt), w=[PK[ab]], skip_group_check=True)
                if kt == 4 * c + 3:
                    J["fin"](c, Oacc, PK[ab])

        jobs = []
        for h in range(8):
            def fin(c, Oacc, pk, h=h):
                rc = rec4[c % 2]
                S.op("dve", lambda e: e.reciprocal(out=rc, in_=Oacc[:, :, 64]), r=[pk], w=[("rec4", c % 2)])
                S.op("dve", lambda e: e.tensor_tensor(out=om[:, 4 * c:4 * c + 4, h * 64:(h + 1) * 64], in0=Oacc[:, :, 0:64],
                                                      in1=rc.unsqueeze(2).to_broadcast([128, 4, 64]), op=ALU.mult),
                     r=[pk, ("rec4", c % 2)], w=[("om", c)])
            jobs.append(dict(KT=KT[:, h, :], kn=96, QT=(lambda a, b_, h=h: QT[0:96, h, a:b_]), V=(lambda kt, h=h: V[:, kt, h, :]), scale=96 ** -0.5,
                             extra=None, fin=fin, qk=(lambda c: [("QT", 4 * c + q) for q in range(4)]), kk=(lambda kt: [("KT", kt)]),
                             vk=(lambda kt: [("V", kt), "Vones"])))
        causal_attn_multi(jobs)
        for i in range(NT):
            b = nb(); pb = P[b][:, :].bitcast(BF16)
            for j in range(4):
                tp(pb[:, j * 128:(j + 1) * 128], om[:, i, j * 128:(j + 1) * 128], ident, r=[("om", i // 4), "ident"], w=[PK[b]])
            act(omT[:, :, i * 128:(i + 1) * 128], pb[:, 0:512].rearrange("p (k t) -> p k t", k=4), AF.Copy, r=[PK[b]], w=[("omT", i)])
        dbg_out("om", om.rearrange("p a b -> p (a b)"), [("om", c) for c in range(4)])
        if upto <= 4:
            return finish(nc, S, st)
        S.barrier()

        nbanks[0] = 4
        M.lo = lo0; M.hi = hi0
        qnT = M.lo_alloc([8, S_], BF16); ksT = M.lo_alloc([2, S_], BF16); kwT = M.lo_alloc([2, S_], BF16)
        vs = M.lo_alloc([NT, 2, 65], BF16); vw = M.lo_alloc([NT, 2, 65], BF16)
        gates = M.lo_alloc([NT, 3, 8], F32)
        kcmpT = M.lo_alloc([2, 128], BF16); VCX = M.lo_alloc([2, 97], BF16)
        PT = [M.lo_alloc([512], BF16) for _ in range(4)]
        rec4 = [M.lo_alloc([4], F32) for _ in range(2)]
        t1 = M.lo_alloc([512], F32); t2 = M.lo_alloc([512], F32)
        hss = M.lo_alloc([8], F32); hrs = M.lo_alloc([8], F32)
        ra = M.lo_alloc([128], F32); rbb = M.lo_alloc([128], F32)
        drb = [M.lo_alloc([512], BF16) for _ in range(2)]
        nqt = M.lo_alloc([64], F32); nkct = M.lo_alloc([64], F32); nkst = M.lo_alloc([64], F32); nkwt = M.lo_alloc([64], F32)
        lo2 = M.lo
        kc2 = M.hi_alloc([2, S_], BF16); vc2 = M.hi_alloc([2, S_], BF16)
        hi_kv = M.hi
        hT = M.hi_alloc([8, S_], BF16)
        Wqn = M.hi_alloc([8, 512], BF16); Wkc2 = M.hi_alloc([8, 256], BF16); Wvc2 = M.hi_alloc([8, 256], BF16)
        Wkv4 = M.hi_alloc([8, 512], BF16); Wgn = M.hi_alloc([8, 24], BF16)
        ge = M.hi_alloc([24], F32)
        S.dma("sp", hT.rearrange("p k t -> p (k t)"), hTs, r=["hTs"], w=["hTall"])
        for t_, d_ in ((nqt, nqB), (nkct, nkcB), (nkst, nksB), (nkwt, nkwB)):
            S.dma("sp", t_, d_, w=["ngain"])
        load_w(Wqn, w_qn, "Wqn", 8, 512); load_w(Wkv4, w_kv4, "Wkv4", 8, 512); load_w(Wgn, w_gn, "Wgn", 8, 24)
        load_w(Wkc2, w_kc2, "Wkc2", 8, 256); load_w(Wvc2, w_vc2, "Wvc2", 8, 256)
        for g_ in range(2):
            S.dma("sp", ksT[64:96, g_, :], XEall_d, w=["ksTx"])
        S.op("pool", lambda e: e.memset(vs[:, :, :, 64:65], 1.0), w=["vsones"])
        S.op("pool", lambda e: e.memset(vw[:, :, :, 64:65], 1.0), w=["vwones"])
        S.op("pool", lambda e: e.memset(kc2[64:128, :, S_ - 1:S_], 0.0), w=["kc2pad"])
        S.op("pool", lambda e: e.memset(vc2[64:128, :, S_ - 1:S_], 0.0), w=["vc2pad"])
        MsubD = Mem(big, NB); MsubD.lo = lo_pers + 16384; MsubD.hi = lo0
        TDq = [mk_tmps(MsubD, 512, 8, 8) for _ in range(2)]; TDs = [mk_tmps(MsubD, 128, 2, 8) for _ in range(2)]; TDw = [mk_tmps(MsubD, 128, 2, 8) for _ in range(2)]
        dnq = [MsubD.lo_alloc([512], BF16) for _ in range(2)]
        dns = [MsubD.lo_alloc([128], BF16) for _ in range(2)]; dnw = [MsubD.lo_alloc([128], BF16) for _ in range(2)]

        def d_front(i):
            ts = slice(i * 128, (i + 1) * 128)
            bq = 4 + 2 * (i % 2)
            for k in range(8):
                mm(P[bq][:, :], hT[:, k, ts], Wqn[:, k, :], k == 0, k == 7, r=["hTall", "Wqn"], w=[PK[bq]])
            bk = 5 + 2 * (i % 2)
            for k in range(8):
                mm(P[bk][:, :], hT[:, k, ts], Wkv4[:, k, :], k == 0, k == 7, r=["hTall", "Wkv4"], w=[PK[bk]])
            bg = nb()
            for k in range(8):
                mm(P[bg][:, 0:24], hT[:, k, ts], Wgn[:, k, :], k == 0, k == 7, r=["hTall", "Wgn"], w=[PK[bg]])
            act(vs[:, i, :, 0:64], P[bk][:, 256:384].rearrange("p (g d) -> p g d", g=2), AF.Copy, r=[PK[bk]], w=[("vs", i)])
            act(vw[:, i, :, 0:64], P[bk][:, 384:512].rearrange("p (g d) -> p g d", g=2), AF.Copy, r=[PK[bk]], w=[("vw", i)])
            act(ge, P[bg][:, 0:24], AF.Exp, r=[PK[bg]], w=["ge"], scale=-1.0)
            S.op("dve", lambda e: e.tensor_scalar(out=ge, in0=ge, scalar1=1.0, scalar2=None, op0=ALU.add), r=["ge"], w=["ge"])
            S.op("dve", lambda e, i=i: e.reciprocal(out=gates[:, i].rearrange("p a b -> p (a b)"), in_=ge), r=["ge"], w=[("gates", i)])
            return bq, bk

        def d_chains(i, bq, bk):
            par = i % 2
            dq = dnq[par].rearrange("p (h d) -> p h d", h=8)
            ds_ = dns[par].rearrange("p (h d) -> p h d", h=2); dw_ = dnw[par].rearrange("p (h d) -> p h d", h=2)
            c1 = hnr_stages(("dq", par), TDq[par], P[bq][:, :].rearrange("p (h d) -> p h d", h=8), [PK[bq]], 8, 64, nqt, "ngain", 0, 8, cosN[:, i, :], sinN[:, i, :], dq, ("dnq", par))
            c2 = hnr_stages(("ds", par), TDs[par], P[bk][:, 0:128].rearrange("p (h d) -> p h d", h=2), [PK[bk]], 2, 64, nkst, "ngain", 0, 8, cosN[:, i, :], sinN[:, i, :], ds_, ("dns", par))
            c3 = hnr_stages(("dw", par), TDw[par], P[bk][:, 128:256].rearrange("p (h d) -> p h d", h=2), [PK[bk]], 2, 64, nkwt, "ngain", 0, 8, cosN[:, i, :], sinN[:, i, :], dw_, ("dnw", par))
            return [c1[:6], c2[:6], c3[:6]], [c1[6:], c2[6:], c3[6:]]

        def d_out(i):
            ts = slice(i * 128, (i + 1) * 128)
            par = i % 2
            b = nb(); pb = P[b][:, :].bitcast(BF16)
            for p_ in range(8):
                tp(pb[0:64, p_ * 128:(p_ + 1) * 128], dnq[par][:, p_ * 64:(p_ + 1) * 64], ident, r=[("dnq", par), "ident"], w=[PK[b]])
            act(qnT[0:64, :, ts], pb[0:64, :].rearrange("p (k t) -> p k t", k=8), AF.Copy, r=[PK[b]], w=[("qnT", i)])
            for (dd, dkey_, dstT, dk) in ((dns[par], ("dns", par), ksT, "ksT"), (dnw[par], ("dnw", par), kwT, "kwT")):
                b2 = nb(); pb = P[b2][:, :].bitcast(BF16)
                for g_ in range(2):
                    tp(pb[0:64, g_ * 128:(g_ + 1) * 128], dd[:, g_ * 64:(g_ + 1) * 64], ident, r=[dkey_, "ident"], w=[PK[b2]])
                act(dstT[0:64, :, ts], pb[0:64, 0:256].rearrange("p (g t) -> p g t", g=2), AF.Copy, r=[PK[b2]], w=[(dk, i)])

        def fm_cmp_proj(idx):
            c, rem = idx // 4, idx % 4
            (Wt, wk, dst, dk) = ((Wkc2, "Wkc2", kc2, "kc2"), (Wvc2, "Wvc2", vc2, "vc2"))[rem // 2]
            g = rem % 2
            b = nb()
            for k in range(8):
                mm(P[b][:, :], Wt[:, k, g * 128:(g + 1) * 128], hT[:, k, c * 512:(c + 1) * 512], k == 0, k == 7, r=["hTall", wk], w=[PK[b]])
            act(dst[0:64, g, c * 512:(c + 1) * 512], P[b][0:64, :], AF.Copy, r=[PK[b]], w=[dk])
            if c == 0:
                act(dst[64:128, g, 0:511], P[b][64:128, 1:512], AF.Copy, r=[PK[b]], w=[dk])
            else:
                act(dst[64:128, g, c * 512 - 1:(c + 1) * 512 - 1], P[b][64:128, :], AF.Copy, r=[PK[b]], w=[dk])

        fr_ = {0: d_front(0), 1: d_front(1)}
        h1_, h2_ = d_chains(0, *fr_[0])
        run_interleaved(h1_)
        for i in range(NT):
            if i + 2 < NT:
                fr_[i + 2] = d_front(i + 2)
            nxt = d_chains(i + 1, *fr_[i + 1]) if i + 1 < NT else ([], [])
            run_interleaved(nxt[0] + h2_)
            h2_ = nxt[1]
            fm_cmp_proj(i)
            if i >= 1:
                d_out(i - 1)
        d_out(NT - 1)
        dbg_out("qnT", qnT[0:64].rearrange("p k t -> p (k t)"), [("qnT", i) for i in range(NT)])
        dbg_out("ksT", ksT[0:64].rearrange("p k t -> p (k t)"), [("ksT", i) for i in range(NT)])
        dbg_out("gates", gates.rearrange("p a b c -> p (a b c)"), [("gates", i) for i in range(NT)])
        dbg_out("kc2", kc2.rearrange("p k t -> p (k t)"), ["kc2", "kc2pad"])
        if upto <= 5:
            return finish(nc, S, st)
        S.barrier()

        nbanks[0] = 8
        M.hi = hi_kv
        hiE = M.hi
        W1k = M.lo_alloc([16, 256], BF16); W1v = M.lo_alloc([16, 256], BF16)
        W2k = M.lo_alloc([2, 64], BF16); W2v = M.lo_alloc([2, 64], BF16)
        pkf = M.lo_alloc([16], F32); pvf = M.lo_alloc([16], F32); pkb = M.lo_alloc([16], BF16); pvb = M.lo_alloc([16], BF16)
        biask = M.lo_alloc([2], F32); biasv = M.lo_alloc([2], F32)
        hid = [M.lo_alloc([128], BF16) for _ in range(2)]
        ovl = M.lo_alloc([32], BF16)
        load_w(W1k, w1k, "W1k", 16, 256); load_w(W1v, w1v, "W1v", 16, 256)
        load_w(W2k, w2k, "W2k", 2, 64); load_w(W2v, w2v, "W2v", 2, 64)
        S.dma("sp", pkf, posk, w=["pkf"]); S.dma("sp", pvf, posv, w=["pvf"]); S.dma("sp", ovl[0:127], ovl_d, w=["ovl"])
        S.op("pool", lambda e: e.memset(VCX[0:127, :, 64:65], 1.0), w=["VCXa"])
        for g in range(2):
            S.op("pool", lambda e, g=g: e.tensor_copy(out=VCX[0:127, g, 65:97], in_=ovl[0:127]), r=["ovl"], w=["VCXb"])
        rt = [M.lo_alloc([128], BF16) for _ in range(3)]
        rt_i = [0]
        for (W1, w1key, W2, w2key, src, skey, posf_, pkey, isk) in ((W1k, "W1k", W2k, "W2k", kc2, ["kc2", "kc2pad"], pkf, "pkf", True),
                                                                    (W1v, "W1v", W2v, "W2v", vc2, ["vc2", "vc2pad"], pvf, "pvf", False)):
            srcv = src.rearrange("p g (n s) -> p g n s", s=16)
            bo = nb()
            for g in range(2):
                bh = []
                for hc in range(2):
                    b = nb()
                    while b == bo or b in bh:
                        b = nb()
                    bh.append(b)
                for lc in range(16):
                    ri = rt_i[0]; rt_i[0] = (ri + 1) % 3
                    rtv = rt[ri][:, 0:127]
                    S.op("dve", lambda e, rtv=rtv, g=g, lc=lc, srcv=srcv, posf_=posf_: e.tensor_scalar(
                        out=rtv, in0=srcv[:, g, (2 * lc) // 16:(2 * lc) // 16 + 127, (2 * lc) % 16], scalar1=posf_[:, lc:lc + 1], scalar2=None, op0=ALU.add),
                        r=skey + [pkey], w=[("rt", ri)])
                    for hc in range(2):
                        mm(P[bh[hc]][:, 0:127], W1[:, lc, hc * 128:(hc + 1) * 128], rtv, lc == 0, lc == 15, r=[w1key, ("rt", ri)], w=[PK[bh[hc]]], sig=(hc == 1 or lc == 15))
                for hc in range(2):
                    act(hid[hc][:, 0:127], P[bh[hc]][:, 0:127], AF.Silu, r=[PK[bh[hc]]], w=[("hid", hc)])
                for hc in range(2):
                    mm(P[bo][0:127, g * 64:(g + 1) * 64], hid[hc][:, 0:127], W2[:, hc, :], hc == 0, hc == 1, r=[("hid", hc), w2key], w=[PK[bo]])
            if isk:
                d = drb[1][0:127, 0:128].rearrange("p (h d) -> p h d", h=2)
                head_norm_rope(P[bo][0:127, 0:128].rearrange("p (h d) -> p h d", h=2), [PK[bo]], 2, 64, nkct, "ngain", 0, 8, cosC[0:127], sinC[0:127], d, "drb1", np_=127)
                b2 = nb(); pb = P[b2][:, :].bitcast(BF16)
                for g_ in range(2):
                    tp(pb[0:64, g_ * 128:g_ * 128 + 127], drb[1][0:127, g_ * 64:(g_ + 1) * 64], ident[0:127, 0:127], r=["drb1", "ident"], w=[PK[b2]])
                act(kcmpT[0:64, :, 0:127], pb[0:64, 0:256].rearrange("p (g t) -> p g t", g=2)[:, :, 0:127], AF.Copy, r=[PK[b2]], w=["kcmpT"])
            else:
                act(VCX[0:127, :, 0:64], P[bo][0:127, 0:128].rearrange("p (g d) -> p g d", g=2), AF.Copy, r=[PK[bo]], w=["VCXc"])
        dbg_out("kcmpT", kcmpT[0:64].rearrange("p g t -> p (g t)"), ["kcmpT"])
        dbg_out("VCX", VCX[0:127].rearrange("p a b -> p (a b)"), ["VCXa", "VCXb", "VCXc"])
        if upto <= 6:
            return finish(nc, S, st)
        S.barrier()

        M.lo = lo2; M.hi = hi0
        onsa = M.hi_alloc([NT, 512], F32)
        mbT = M.hi_alloc([2, S_], BF16)
        vmT = M.hi_alloc([S_], BF16); XE = M.hi_alloc([NT, 128], BF16)
        fb = M.hi_alloc([NT, 32], F32); vj = M.hi_alloc([NT, 32], F32)
        pc = [M.lo_alloc([4, 128], BF16) for _ in range(2)]
        rsum = M.lo_alloc([8], F32); rec8 = M.lo_alloc([8], F32); gr = M.lo_alloc([8], F32)
        tmp_i = M.lo_alloc([8, 32], F32); imp = M.lo_alloc([2, 32], F32); m8 = M.lo_alloc([2, 8], F32)
        sel = M.lo_alloc([2, 32], F32); mbf = M.lo_alloc([2, 96], BF16)
        S.op("pool", lambda e: e.memset(mbf, 0.0), w=["mbf"])
        tmpo = M.lo_alloc([8, 64], F32)
        pw = [M.lo_alloc([3, 128], BF16) for _ in range(4)]
        onb = [M.lo_alloc([512], BF16) for _ in range(2)]
        S.dma("sp", vmT[0:127], vmT_d, w=["vmT"]); S.dma("sp", XE[0:32].rearrange("p a b -> p (a b)"), XE_d, w=["XE"])
        S.dma("sp", fb.rearrange("p a b -> p (a b)"), fb_d, w=["fb"]); S.dma("sp", vj.rearrange("p a b -> p (a b)"), vj_d, w=["vj"])
        VCXk = ["VCXa", "VCXb", "VCXc"]
        for i in range(NT):
            ts = slice(i * 128, (i + 1) * 128)
            sb_ = [nb(), nb()]
            ob = [nb(), nb()]
            for p in range(8):
                j, g = p // 2, p % 2
                mm(P[sb_[p // 4]][0:127, (p % 4) * 128:(p % 4 + 1) * 128], kcmpT[0:64, g, 0:127], qnT[0:64, p, ts], True, True,
                   r=["kcmpT", ("qnT", i)], w=[PK[sb_[p // 4]]])
            for hf_ in range(2):
                pcv = pc[hf_]
                act(pcv[0:127], P[sb_[hf_]][0:127, :].rearrange("p (a b) -> p a b", a=4), AF.Exp, r=[PK[sb_[hf_]]], w=[("pc", hf_)], scale=0.125)
                S.op("dve", lambda e, pcv=pcv: e.tensor_tensor(out=pcv[0:127], in0=pcv[0:127], in1=vmT[0:127, ts].unsqueeze(1).to_broadcast([127, 4, 128]), op=ALU.mult),
                     r=[("pc", hf_), "vmT"], w=[("pc", hf_)])
            for p in range(8):
                g = p % 2
                mm(P[ob[p // 4]][:, (p % 4) * 97:(p % 4 + 1) * 97], pc[p // 4][0:127, p % 4, :], VCX[0:127, g, :], True, True,
                   r=[("pc", p // 4)] + VCXk, w=[PK[ob[p // 4]]])
            OC = [P[ob[h_]][:, 0:388].rearrange("p (a b) -> p a b", a=4) for h_ in range(2)]
            for h_ in range(2):
                S.op("dve", lambda e, h_=h_: e.tensor_scalar(out=rsum[:, h_ * 4:(h_ + 1) * 4], in0=OC[h_][:, :, 64], scalar1=1e-30, scalar2=None, op0=ALU.max), r=[PK[ob[h_]]], w=["rsum"])
            S.op("dve", lambda e: e.reciprocal(out=rec8, in_=rsum), r=["rsum"], w=["rec8"])
            S.op("dve", lambda e, i=i: e.tensor_tensor(out=gr, in0=gates[:, i, 0, :], in1=rec8, op=ALU.mult), r=["rec8", ("gates", i)], w=["gr"])
            for h_ in range(2):
                S.op("dve", lambda e, h_=h_, i=i: e.tensor_tensor(out=onsa[:, i, h_ * 256:(h_ + 1) * 256].rearrange("p (a b) -> p a b", a=4), in0=OC[h_][:, :, 0:64],
                                                                   in1=gr[:, h_ * 4:(h_ + 1) * 4].unsqueeze(2).to_broadcast([128, 4, 64]), op=ALU.mult),
                     r=[PK[ob[h_]], "gr"], w=[("onsa", i)])
                S.op("dve", lambda e, h_=h_: e.tensor_tensor(out=tmp_i[:, h_ * 4:(h_ + 1) * 4, :], in0=OC[h_][:, :, 65:97],
                                                             in1=rec8[:, h_ * 4:(h_ + 1) * 4].unsqueeze(2).to_broadcast([128, 4, 32]), op=ALU.mult),
                     r=[PK[ob[h_]], "rec8"], w=["tmp_i"])
            S.op("dve", lambda e: e.tensor_reduce(out=imp, in_=tmp_i.rearrange("t (j g) n -> t g n j", g=2), axis=AX.X, op=ALU.add), r=["tmp_i"], w=["imp"])
            S.op("dve", lambda e, i=i: e.tensor_tensor(out=imp, in0=imp, in1=fb[:, i, :].unsqueeze(1).to_broadcast([128, 2, 32]), op=ALU.add), r=["imp", "fb"], w=["imp"])
            for g in range(2):
                S.op("dve", lambda e, g=g: e.max(out=m8[:, g, :], in_=imp[:, g, :]), r=["imp"], w=["m8"])
                S.op("dve", lambda e, g=g: e.tensor_scalar(out=sel[:, g, :], in0=imp[:, g, :], scalar1=m8[:, g, 7:8], scalar2=None, op0=ALU.is_ge), r=["imp", "m8"], w=["sel"])
            S.op("dve", lambda e, i=i: e.tensor_tensor(out=sel, in0=sel, in1=vj[:, i, :].unsqueeze(1).to_broadcast([128, 2, 32]), op=ALU.mult), r=["sel", "vj"], w=["sel"])
            S.op("dve", lambda e: e.tensor_scalar(out=mbf[:, :, 64:96], in0=sel, scalar1=-1.0, scalar2=30000.0, op0=ALU.add, op1=ALU.mult), r=["sel"], w=["mbf"])
            if i == 5:
                dbg_out("sel5", sel.rearrange("p a b -> p (a b)"), ["sel"])
                dbg_out("imp5", imp.rearrange("p a b -> p (a b)"), ["imp"])
            b = nb(); pb = P[b][:, :].bitcast(BF16)
            for g in range(2):
                tp(pb[0:96, g * 128:(g + 1) * 128], mbf[:, g, :], ident, r=["mbf", "ident"], w=[PK[b]])
            qm = qnT[64:96].rearrange("p (j g) t -> p j g t", g=2)
            for g in range(2):
                act(qm[:, :, g, ts], pb[64:96, g * 128:(g + 1) * 128].unsqueeze(1).to_broadcast([32, 4, 128]), AF.Copy, r=[PK[b]], w=[("mbq", i)])
        dbg_out("onsa_c", onsa.rearrange("p a b -> p (a b)"), [("onsa", i) for i in range(NT)])
        dbg_out("mbT", qnT[64:96, 0:2, :].rearrange("p a b -> p (a b)"), [("mbq", i) for i in range(NT)])
        if upto <= 7:
            return finish(nc, S, st)

        S.barrier()
        nbanks[0] = 6
        jobs = []
        for p in range(8):
            j, g = p // 2, p % 2

            def extra(ps_ap, kt, a, b_, pk, g=g):
                mm(ps_ap, XE[0:32, kt, :], mbT[0:32, g, a:b_], False, True, r=["XE"] + [("mbT", q) for q in range(a // 128, b_ // 128)], w=[pk])

            def fin(c, Oacc, pk, p=p):
                rc = rec4[c % 2]
                S.op("dve", lambda e: e.reciprocal(out=rc, in_=Oacc[:, :, 64]), r=[pk], w=[("rec4", c % 2)])
                S.op("dve", lambda e: e.tensor_tensor(out=rc, in0=rc, in1=gates[:, 4 * c:4 * c + 4, 1, p], op=ALU.mult), r=[("rec4", c % 2)] + [("gates", 4 * c + q) for q in range(4)], w=[("rec4", c % 2)])
                tv = tmpo[:, 0:4, :]
                S.op("dve", lambda e: e.tensor_tensor(out=tv, in0=Oacc[:, :, 0:64], in1=rc.unsqueeze(2).to_broadcast([128, 4, 64]), op=ALU.mult), r=[pk, ("rec4", c % 2)], w=["tmpo"])
                S.op("pool", lambda e: e.tensor_tensor(out=onsa[:, 4 * c:4 * c + 4, p * 64:(p + 1) * 64], in0=onsa[:, 4 * c:4 * c + 4, p * 64:(p + 1) * 64], in1=tv, op=ALU.add),
                     r=["tmpo"] + [("onsa", 4 * c + q) for q in range(4)], w=[("onsa", 4 * c + q) for q in range(4)])
            jobs.append(dict(KT=ksT[:, g, :], kn=96, QT=(lambda a, b_, p=p: qnT[0:96, p, a:b_]), V=(lambda kt, g=g: vs[:, kt, g, :]), scale=0.125,
                             extra=None, fin=fin, qk=(lambda c: [("qnT", 4 * c + q) for q in range(4)] + [("mbq", 4 * c + q) for q in range(4)]), kk=(lambda kt: [("ksT", kt), "ksTx"]),
                             vk=(lambda kt: [("vs", kt), "vsones"])))
        causal_attn_multi(jobs)
        dbg_out("onsa_cs", onsa.rearrange("p a b -> p (a b)"), [("onsa", i) for i in range(NT)])
        if upto <= 8:
            return finish(nc, S, st)

        pw_i = [0]
        nbanks[0] = 4
        bank_state[0] = 0

        def emit_ws(i, p):
            g = p % 2
            kts = [kt for kt in (i - 2, i - 1, i) if kt >= 0]
            b = nb()
            for kt in kts:
                sl = kt - (i - 2)
                mm(P[b][:, sl * 128:(sl + 1) * 128], kwT[0:64, g, kt * 128:(kt + 1) * 128], qnT[0:64, p, i * 128:(i + 1) * 128], True, True,
                   r=[("kwT", kt), ("qnT", i)], w=[PK[b]])
            return b

        wsteps = [(i, p) for i in range(NT) for p in range(8)]
        WLA = 2
        wpend = [emit_ws(*wsteps[q]) for q in range(WLA)]
        for wi_, (i, p) in enumerate(wsteps):
            ts = slice(i * 128, (i + 1) * 128)
            g = p % 2
            kts = [kt for kt in (i - 2, i - 1, i) if kt >= 0]
            b = wpend.pop(0)
            if wi_ + WLA < len(wsteps):
                wpend.append(emit_ws(*wsteps[wi_ + WLA]))
            ob = [4 + 2 * (i % 2), 5 + 2 * (i % 2)]
            s0 = kts[0] - (i - 2)
            wi = pw_i[0]; pw_i[0] = (wi + 1) % len(pw)
            pwv = pw[wi]
            act(pwv[:, s0:3, :], P[b][:, s0 * 128:384].rearrange("p (a b) -> p a b", b=128), AF.Exp, r=[PK[b]], w=[("pw", wi)], scale=0.125)
            S.op("dve", lambda e, pwv=pwv, s0=s0: e.tensor_tensor(out=pwv[:, s0:3, :], in0=pwv[:, s0:3, :], in1=winm[:, s0:3, :], op=ALU.mult), r=[("pw", wi), "winm"], w=[("pw", wi)])
            for kt in kts:
                sl = kt - (i - 2)
                mm(P[ob[p // 4]][:, (p % 4) * 65:(p % 4 + 1) * 65], pwv[:, sl, :], vw[:, kt, g, :], kt == kts[0], kt == kts[-1],
                   r=[("pw", wi), ("vw", kt), "vwones"], w=[PK[ob[p // 4]]])
            if p != 7:
                continue
            OW = [P[ob[h_]][:, 0:260].rearrange("p (a b) -> p a b", a=4) for h_ in range(2)]
            for h_ in range(2):
                S.op("dve", lambda e, h_=h_: e.reciprocal(out=rec8[:, h_ * 4:(h_ + 1) * 4], in_=OW[h_][:, :, 64]), r=[PK[ob[h_]]], w=["rec8"])
            S.op("dve", lambda e, i=i: e.tensor_tensor(out=gr, in0=gates[:, i, 2, :], in1=rec8, op=ALU.mult), r=["rec8", ("gates", i)], w=["gr"])
            for h_ in range(2):
                S.op("dve", lambda e, h_=h_: e.tensor_tensor(out=tmpo[:, h_ * 4:(h_ + 1) * 4, :], in0=OW[h_][:, :, 0:64],
                                                             in1=gr[:, h_ * 4:(h_ + 1) * 4].unsqueeze(2).to_broadcast([128, 4, 64]), op=ALU.mult), r=[PK[ob[h_]], "gr"], w=["tmpo"])
            S.op("pool", lambda e, i=i: e.tensor_tensor(out=onb[i % 2], in0=onsa[:, i, :], in1=tmpo.rearrange("p a b -> p (a b)"), op=ALU.add), r=["tmpo", ("onsa", i)], w=[("onb", i % 2)])
            if "onsa_all" in D_:
                S.dma("sp", D_["onsa_all"][:, i * 512:(i + 1) * 512], onb[i % 2], r=[("onb", i % 2)])
            b = nb(); pb = P[b][:, :].bitcast(BF16)
            for j in range(4):
                tp(pb[:, j * 128:(j + 1) * 128], onb[i % 2][:, j * 128:(j + 1) * 128], ident, r=[("onb", i % 2), "ident"], w=[PK[b]])
            act(onT[:, :, ts], pb[:, 0:512].rearrange("p (k t) -> p k t", k=4), AF.Copy, r=[PK[b]], w=[("onT", i)])
        if upto <= 9:
            return finish(nc, S, st)
        S.barrier()

        nbanks[0] = 8
        M.lo = lo0; M.hi = hi0
        hT = M.hi_alloc([8, S_], BF16)
        mergedT = M.hi_alloc([8, S_], BF16)
        hi2 = M.hi
        Wgm = M.lo_alloc([8, 512], BF16); Wgnm = M.lo_alloc([8, 512], BF16); Wom = M.lo_alloc([4, 512], BF16); Won = M.lo_alloc([4, 512], BF16)
        e3 = M.lo_alloc([512], F32); e4 = M.lo_alloc([512], F32); tA = M.lo_alloc([512], F32); tB = M.lo_alloc([512], F32)
        mgb = [M.lo_alloc([512], BF16) for _ in range(2)]
        S.dma("sp", hT.rearrange("p k t -> p (k t)"), hTs, r=["hTs"], w=["hTall"])
        e3 = [e3, M.lo_alloc([512], F32)]; e4 = [e4, M.lo_alloc([512], F32)]
        tA = [tA, M.lo_alloc([512], F32)]; tB = [tB, M.lo_alloc([512], F32)]

        def i_front(cc, i, it):
            ts = slice(i * 128, (i + 1) * 128)
            bs = [4 * (it % 2) + q for q in range(4)]
            b1, b2, b3, b4 = bs
            for k in range(8):
                mm(P[b3][:, :], hT[:, k, ts], Wgm[:, k, :], k == 0, k == 7, r=["hTall", "Wgm"], w=[PK[b3]])
            for k in range(8):
                mm(P[b4][:, :], hT[:, k, ts], Wgnm[:, k, :], k == 0, k == 7, r=["hTall", "Wgnm"], w=[PK[b4]])
            for k in range(4):
                mm(P[b1][:, :], omT[:, k, ts], Wom[:, k, :], k == 0, k == 3, r=[("omT", i), "Wom"], w=[PK[b1]])
            for k in range(4):
                mm(P[b2][:, :], onT[:, k, ts], Won[:, k, :], k == 0, k == 3, r=[("onT", i), "Won"], w=[PK[b2]])
            return bs

        def i_back(cc, i, it, bs):
            ts = slice(i * 128, (i + 1) * 128)
            b1, b2, b3, b4 = bs
            par = it % 2
            act(e3[par], P[b3][:, :], AF.Sigmoid, r=[PK[b3]], w=[("e3", par)])
            act(e4[par], P[b4][:, :], AF.Sigmoid, r=[PK[b4]], w=[("e4", par)])
            S.op("dve", lambda e: e.tensor_tensor(out=tA[par], in0=P[b1][:, :], in1=e3[par], op=ALU.mult), r=[PK[b1], ("e3", par)], w=[("tA", par)])
            S.op("dve", lambda e: e.tensor_tensor(out=tB[par], in0=P[b2][:, :], in1=e4[par], op=ALU.mult), r=[PK[b2], ("e4", par)], w=[("tB", par)])
            S.op("dve", lambda e: e.tensor_tensor(out=mgb[par], in0=tA[par], in1=tB[par], op=ALU.add), r=[("tA", par), ("tB", par)], w=[("mgb", par)])
            if "merged" in D_:
                S.dma("sp", D_["merged"][i * 128:(i + 1) * 128, cc * 512:(cc + 1) * 512], mgb[par], r=[("mgb", par)])
            pb = P[b2][:, :].bitcast(BF16)
            for j in range(4):
                tp(pb[:, j * 128:(j + 1) * 128], mgb[par][:, j * 128:(j + 1) * 128], ident, r=[("mgb", par), "ident"], w=[PK[b2]])
            act(mergedT[:, cc * 4:(cc + 1) * 4, ts], pb[:, 0:512].rearrange("p (k t) -> p k t", k=4), AF.Copy, r=[PK[b2]], w=[("mergedT", i)])

        it = 0
        for cc in range(2):
            load_w_cols(Wgm, w_gm, "Wgm", 8, cc * 512, (cc + 1) * 512); load_w_cols(Wgnm, w_gnm, "Wgnm", 8, cc * 512, (cc + 1) * 512)
            load_w_cols(Wom, wo_mla, "Wom", 4, cc * 512, (cc + 1) * 512); load_w_cols(Won, wo_nsa, "Won", 4, cc * 512, (cc + 1) * 512)
            pend_ = i_front(cc, 0, it)
            for i in range(NT):
                cur_ = pend_
                if i + 1 < NT:
                    pend_ = i_front(cc, i + 1, it + 1)
                i_back(cc, i, it, cur_)
                it += 1
        if upto <= 10:
            return finish(nc, S, st)
        S.barrier()

        M.lo = lo0
        h2T = hT
        Wout = M.lo_alloc([8, D], BF16)
        G1 = M.lo_alloc([D], F32); A2 = M.lo_alloc([D], F32); B2 = M.lo_alloc([D], F32)
        xb = [M.lo_alloc([D], F32) for _ in range(2)]
        x1t = [M.lo_alloc([D], F32) for _ in range(2)]
        junk = M.lo_alloc([D], F32); tmpA = M.lo_alloc([D], F32)
        hb = [M.lo_alloc([D], BF16) for _ in range(2)]
        ssq = M.lo_alloc([NT], F32); rs = M.lo_alloc([NT], F32)
        S.dma("sp", G1, mods[:, 2 * D:3 * D], r=["mods"], w=["G1"])
        S.dma("sp", B2, mods[:, 3 * D:4 * D], r=["mods"], w=["AB2"])
        S.dma("sp", A2, mods[:, 4 * D:5 * D], r=["mods"], w=["AB2"])
        load_w(Wout, w_out, "Wout", 8, D, ceng="pool")
        tmpA2 = [tmpA, M.lo_alloc([D], F32)]
        tmpJ = [M.lo_alloc([D], F32) for _ in range(3)]
        xb = xb + [M.lo_alloc([D], F32)]
        x1t = x1t + [M.lo_alloc([D], F32)]

        def j_front(i):
            ts = slice(i * 128, (i + 1) * 128)
            par = i % 3
            S.dma("sp", xb[par], x[ts, :], w=[("xb", par)])
            for cc in range(2):
                b = 2 * par + cc
                for k in range(8):
                    mm(P[b][:, :], mergedT[:, k, ts], Wout[:, k, cc * 512:(cc + 1) * 512], k == 0, k == 7, r=[("mergedT", i), "Wout"], w=[PK[b]])

        def j_mid(i):
            ts = slice(i * 128, (i + 1) * 128)
            par = i % 3
            for cc in range(2):
                b = 2 * par + cc
                S.op("dve", lambda e, b=b, cc=cc: e.tensor_tensor(out=tmpJ[par][:, cc * 512:(cc + 1) * 512], in0=P[b][:, :], in1=G1[:, cc * 512:(cc + 1) * 512], op=ALU.mult), r=[PK[b], "G1"], w=[("tmpJ", par)])
                S.op("pool", lambda e, cc=cc: e.tensor_tensor(out=x1t[par][:, cc * 512:(cc + 1) * 512], in0=tmpJ[par][:, cc * 512:(cc + 1) * 512], in1=xb[par][:, cc * 512:(cc + 1) * 512], op=ALU.add),
                     r=[("tmpJ", par), ("xb", par)], w=[("x1t", par)])
            S.dma("sp", x1s[ts, :], x1t[par], r=[("x1t", par)], w=[("x1s", i)])
            norm_a(i, x1t[par], ("x1t", par))

        bank_state[0] = 0

        def nbJ():
            b = 6 + (bank_state[0] % 2)
            bank_state[0] = (bank_state[0] + 1) % 2
            return b
        nb_saved = nb
        nb = nbJ
        j_front(0); j_mid(0); j_front(1); j_mid(1)
        for i in range(NT):
            if i + 2 < NT:
                j_front(i + 2)
            b_ = norm_b1(i, x1t[i % 3], ("x1t", i % 3), A2, B2, ["AB2"])
            if i + 2 < NT:
                j_mid(i + 2)
            norm_b2(i, b_, h2T, "h2T")
        nb = nb_saved
        bank_state[0] = 0
        if "x1" in D_:
            S.dma("sp", D_["x1"], x1s, r=[("x1s", i) for i in range(NT)])
        if upto <= 11:
            return finish(nc, S, st)
        S.barrier()

        M.lo = lo_pers; M.hi = hi2 + 8 * S_ * 2
        Wd = M.hi_alloc([NFC, D], BF16)
        actT = M.hi_alloc([NFC, 1024], BF16)
        G2 = M.lo_alloc([D], F32)
        Wg2 = [M.lo_alloc([8, 256], BF16) for _ in range(2)]; Wu2 = [M.lo_alloc([8, 256], BF16) for _ in range(2)]
        sg = [M.lo_alloc([512], F32) for _ in range(2)]
        xb = [M.lo_alloc([D], F32) for _ in range(2)]
        ot = [M.lo_alloc([D], F32) for _ in range(2)]
        tmpA = M.lo_alloc([D], F32)
        S.dma("sp", G2, mods[:, 5 * D:6 * D], r=["mods"], w=["G2"])
        load_w(Wd, wd, "Wd", NFC, D)
        h2k = [("h2T", i) for i in range(NT)]
        out_toks = []
        for half in range(2):
            def ld(jg_):
                wb_ = jg_ % 2
                load_w_cols(Wg2[wb_], wg, ("Wg2", wb_), 8, jg_ * 256, (jg_ + 1) * 256)
                load_w_cols(Wu2[wb_], wu, ("Wu2", wb_), 8, jg_ * 256, (jg_ + 1) * 256)
            ld(0)
            for jg in range(NFC // 2):
                wb = jg % 2
                if jg + 1 < NFC // 2:
                    ld(jg + 1)
                for jj in range(2):
                    j = jg * 2 + jj
                    for tc in range(2):
                        t0 = half * 1024 + tc * 512
                        bg, bu = nb(), nb()
                        for k in range(8):
                            mm(P[bg][:, :], Wg2[wb][:, k, jj * 128:(jj + 1) * 128], h2T[:, k, t0:t0 + 512], k == 0, k == 7, r=[("Wg2", wb)] + h2k, w=[PK[bg]])
                        for k in range(8):
                            mm(P[bu][:, :], Wu2[wb][:, k, jj * 128:(jj + 1) * 128], h2T[:, k, t0:t0 + 512], k == 0, k == 7, r=[("Wu2", wb)] + h2k, w=[PK[bu]])
                        act(sg[tc], P[bg][:, :], AF.Silu, r=[PK[bg]], w=[("sg", tc)])
                        S.op("dve", lambda e, j=j, tc=tc, bu=bu: e.tensor_tensor(out=actT[:, j, tc * 512:(tc + 1) * 512], in0=P[bu][:, :], in1=sg[tc], op=ALU.mult),
                             r=[PK[bu], ("sg", tc)], w=[("actT", tc)])
            for il in range(8):
                i = half * 8 + il
                ts = slice(i * 128, (i + 1) * 128)
                S.dma("sp", xb[i % 2], x1s[ts, :], r=[("x1s", i)], w=[("xb", i % 2)])
                for cc in range(2):
                    b = nb()
                    for j in range(NFC):
                        mm(P[b][:, :], actT[:, j, il * 128:(il + 1) * 128], Wd[:, j, cc * 512:(cc + 1) * 512], j == 0, j == NFC - 1, r=[("actT", il // 4), "Wd"], w=[PK[b]])
                    S.op("dve", lambda e, b=b, cc=cc: e.tensor_tensor(out=tmpA[:, cc * 512:(cc + 1) * 512], in0=P[b][:, :], in1=G2[:, cc * 512:(cc + 1) * 512], op=ALU.mult), r=[PK[b], "G2"], w=["tmpA"])
                    S.op("pool", lambda e, i=i, cc=cc: e.tensor_tensor(out=ot[i % 2][:, cc * 512:(cc + 1) * 512], in0=tmpA[:, cc * 512:(cc + 1) * 512], in1=xb[i % 2][:, cc * 512:(cc + 1) * 512], op=ALU.add),
                         r=["tmpA", ("xb", i % 2)], w=[("ot", i % 2)])
                S.dma("sp", out[ts, :], ot[i % 2], r=[("ot", i % 2)], w=[("out", i)])
        return finish(nc, S, st)


def finish(nc, S, st):
    for q in S.dsem:
        for i in range(len(S.dsem[q])):
            if S.dcnt[q][i]:
                S._wait("sp", (("d", q, i), S.dcnt[q][i]))
    for e2 in ("pe", "act", "dve", "pool"):
        if S.cnt[e2]:
            S._wait("sp", (e2, S.cnt[e2]))
    S.check_no_deadlock()
    st.close()
    return nc


def _consts():
    bf = ml_dtypes.bfloat16
    c = {}
    c["ident"] = np.eye(128, dtype=np.float32).astype(bf)
    a = np.arange(128)
    c["tri"] = (a[:, None] <= a[None, :]).astype(np.float32).astype(bf)
    w = np.zeros((128, 3, 128), np.float32)
    w[:, 0, :] = (a[:, None] > a[None, :])
    w[:, 1, :] = 1.0
    w[:, 2, :] = (a[:, None] <= a[None, :])
    c["winm"] = w.reshape(128, 384).astype(bf)
    inv16 = (np.float32(500000.0) ** (-np.arange(0, 32, 2, dtype=np.float32) / np.float32(32))).astype(np.float32)
    inv8 = (np.float32(500000.0) ** (-np.arange(0, 16, 2, dtype=np.float32) / np.float32(16))).astype(np.float32)
    c["inv16"] = np.tile(inv16[None], (128, 1)).astype(np.float32)
    c["inv8"] = np.tile(inv8[None], (128, 1)).astype(np.float32)
    n = np.arange(127)
    starts = n * 16
    j = np.arange(32)
    ovl = ((starts[:, None] < j[None, :] * 64 + 64) & (starts[:, None] + 32 > j[None, :] * 64))
    c["ovl"] = ovl.astype(np.float32).astype(bf)
    t = np.arange(S_)
    c["vmT"] = ((starts[:, None] + 31) <= t[None, :]).astype(np.float32).astype(bf)
    XE = np.zeros((32, NT, 128), np.float32)
    for kt in range(NT):
        XE[2 * kt, kt, 0:64] = 1.0
        XE[2 * kt + 1, kt, 64:128] = 1.0
    c["XE"] = XE.reshape(32, NT * 128).astype(bf)
    c["XEall"] = (np.arange(32)[:, None] == (np.arange(S_)[None, :] // 64)).astype(np.float32).astype(bf)
    cur = (t // 64)
    forced = (j[None, :] == 0) | (j[None, :] == cur[:, None]) | (j[None, :] == cur[:, None] - 1)
    valid = j[None, :] <= cur[:, None]
    fb = np.where(valid, np.where(forced, 1e4, 0.0), -1e30).astype(np.float32)
    c["fb"] = fb.reshape(NT, 128, 32).transpose(1, 0, 2).reshape(128, NT * 32).copy()
    c["vj"] = valid.astype(np.float32).reshape(NT, 128, 32).transpose(1, 0, 2).reshape(128, NT * 32).copy()
    return c


def _rep(v, n=128):
    return np.ascontiguousarray(np.broadcast_to(np.asarray(v, np.float32)[None, :], (n, v.shape[0])))


def prep_inputs(inp):
    f = lambda a: np.ascontiguousarray(np.asarray(a, dtype=np.float32))
    w_in = f(inp["w_in"][0])
    o = np.cumsum([0, 768, 256, 32, 512, 128, 128, 128, 128, 128, 128, 24, 1024, 1024])
    seg = lambda i: w_in[:, o[i]:o[i + 1]]
    shared = {}
    shared["ada_w"] = f(inp["ada_w"][0]); shared["adabB"] = _rep(f(inp["ada_b"][0]))
    shared["g1B"] = _rep(f(inp["norm1_gain"][0])); shared["g2B"] = _rep(f(inp["norm2_gain"][0]))
    shared["w_cq"] = f(seg(0)); shared["w_ckv"] = f(seg(1)); shared["w_kpe"] = f(seg(2))
    qn = seg(3).reshape(D, 8, 64)
    shared["w_qn"] = f(qn[:, PH, :].reshape(D, 512))
    kc = seg(4).reshape(D, 2, 64); vc = seg(5).reshape(D, 2, 64)
    shared["w_kc2"] = f(np.stack([kc[:, 0], kc[:, 0], kc[:, 1], kc[:, 1]], 1).reshape(D, 256))
    shared["w_vc2"] = f(np.stack([vc[:, 0], vc[:, 0], vc[:, 1], vc[:, 1]], 1).reshape(D, 256))
    shared["w_kv4"] = f(np.concatenate([seg(6), seg(8), seg(7), seg(9)], 1))
    gn = seg(10).reshape(D, 8, 3)
    shared["w_gn"] = f(gn[:, PH, :].transpose(0, 2, 1).reshape(D, 24))
    shared["w_gm"] = f(seg(11)); shared["w_gnm"] = f(seg(12))
    shared["qag"] = f(f(inp["mla_q_a_gain"][0]).reshape(6, 128).T); shared["kvag"] = f(f(inp["mla_kv_a_gain"][0]).reshape(2, 128).T)
    shared["w_qb"] = f(inp["mla_w_q_b"][0]); shared["w_kvb"] = f(inp["mla_w_kv_b"][0])
    shared["qgB"] = _rep(f(inp["mla_q_gain"][0])); shared["kgB"] = _rep(f(inp["mla_k_gain"][0]))
    shared["nqB"] = _rep(f(inp["nsa_q_gain"][0])); shared["nkcB"] = _rep(f(inp["nsa_kc_gain"][0]))
    shared["nksB"] = _rep(f(inp["nsa_ks_gain"][0])); shared["nkwB"] = _rep(f(inp["nsa_kw_gain"][0]))
    shared["posk"] = f(f(inp["cmp_pos_k"][0]).reshape(16, 128).T); shared["posv"] = f(f(inp["cmp_pos_v"][0]).reshape(16, 128).T)
    shared["w1k"] = f(inp["cmp_w1_k"][0]); shared["w2k"] = f(inp["cmp_w2_k"][0])
    shared["w1v"] = f(inp["cmp_w1_v"][0]); shared["w2v"] = f(inp["cmp_w2_v"][0])
    shared["wo_mla"] = f(inp["w_o_mla"][0])
    shared["wo_nsa"] = f(f(inp["w_o_nsa"][0]).reshape(8, 64, D)[PH].reshape(512, D))
    shared["w_out"] = f(inp["w_out"][0])
    shared["wg"] = f(inp["ffn_w_gate"][0]); shared["wu"] = f(inp["ffn_w_up"][0]); shared["wd"] = f(inp["ffn_w_down"][0])
    shared.update(_consts())
    maps = []
    xs = np.asarray(inp["x"], np.float32); cs = np.asarray(inp["c"], np.float32); ps = np.asarray(inp["positions"]).astype(np.int32)
    for b in range(xs.shape[0]):
        m = dict(shared)
        m["x"] = np.ascontiguousarray(xs[b])
        m["c_pk"] = np.ascontiguousarray(cs[b].reshape(8, 128).T)
        m["pos_pk"] = np.ascontiguousarray(ps[b].reshape(NT, 128).T)
        m["posC"] = np.ascontiguousarray(ps[b][31::16][:127].reshape(127, 1))
        maps.append(m)
    return maps


_NC_CACHE = {}


def kernel(**inputs):
    maps = prep_inputs(inputs)
    if "nc" not in _NC_CACHE:
        _NC_CACHE["nc"] = build()
    nc = _NC_CACHE["nc"]
    res = run_bass_kernel_spmd(nc, maps, core_ids=list(range(len(maps))))
    return np.stack([np.asarray(r["out"], dtype=np.float32) for r in res.results], 0)
```
